# Optimizing a Trainium2 kernel written in Bass

```python
import math
import jax
import jax.numpy as jnp
from jax import lax
import numpy as np

D_MODEL = 1024
BATCH = 8
SEQ = 2048
DEPTH = 2

F32 = jnp.float32

GDN_HEADS = 6
GDN_DK = 64
GDN_DV = 64
GDN_CONV = 5
GDN_CHUNK = 64
GDN_QK_W = GDN_HEADS * GDN_DK
GDN_W = GDN_HEADS * GDN_DV
GDN_QKV_W = 2 * GDN_QK_W + GDN_W

MLA_HEADS = 6
MLA_NOPE = 64
MLA_ROPE = 32
MLA_V = 64
MLA_Q_RANK = 192
MLA_KV_RANK = 128
MLA_QBLOCK = 128
ROPE_THETA = 10000.0
MLA_W = MLA_HEADS * MLA_V

RWKV_HEADS = 4
RWKV_HD = 64
RWKV_DECAY_LORA = 64
RWKV_ICLR_LORA = 64
RWKV_W = RWKV_HEADS * RWKV_HD
RWKV_SHIFT_W = 3 * RWKV_W + RWKV_DECAY_LORA + RWKV_ICLR_LORA
RWKV_LN_EPS = 64e-5

D_MIX = GDN_W + MLA_W + RWKV_W
NORM_EPS = 1e-6

IN_LAYOUT = (
    ('gate', D_MIX),
    ('gdn_q', GDN_QK_W), ('gdn_k', GDN_QK_W), ('gdn_v', GDN_W),
    ('gdn_a', 2 * GDN_HEADS), ('gdn_b', 2 * GDN_HEADS),
    ('mla_cq', MLA_Q_RANK), ('mla_ckv', MLA_KV_RANK), ('mla_kr', MLA_ROPE),
    ('rw_r', RWKV_W), ('rw_k', RWKV_W), ('rw_v', RWKV_W),
    ('rw_wd', RWKV_DECAY_LORA), ('rw_ad', RWKV_ICLR_LORA),
)
N_IN = sum(width for _, width in IN_LAYOUT)

kernel_name = 'bidir_hybrid_gdn_mla_rwkv7_block'


def split_columns(p):
    cols = {}
    off = 0
    for name, width in IN_LAYOUT:
        cols[name] = p[..., off:off + width]
        off += width
    return cols


def rms_norm(x, g, eps=NORM_EPS):
    xf = x.astype(F32)
    y = xf * lax.rsqrt(jnp.mean(xf * xf, axis=-1, keepdims=True) + eps)
    return (y * g.astype(F32)).astype(x.dtype)


def l2_normalize(x, eps=1e-6):
    xf = x.astype(F32)
    return xf * lax.rsqrt(jnp.sum(xf * xf, axis=-1, keepdims=True) + eps)


def centred_depthwise_conv(x, w):
    pad = w.shape[0] // 2
    return lax.conv_general_dilated(
        x, w[:, None, :].astype(x.dtype), window_strides=(1,), padding=[(pad, pad)],
        dimension_numbers=('NWC', 'WIO', 'NWC'), feature_group_count=x.shape[-1])


def centred_token_shift(x, mu):
    zeros = jnp.zeros_like(x[:, :1])
    x_prev = jnp.concatenate([zeros, x[:, :-1]], axis=1)
    x_next = jnp.concatenate([x[:, 1:], zeros], axis=1)
    return x + mu[0] * (x_prev - x) + mu[1] * (x_next - x)


def bidir_stack(t_fwd, t_bwd):
    return jnp.concatenate([t_fwd, jnp.flip(t_bwd, axis=1)], axis=0)


def bidir_merge(y):
    b = y.shape[0] // 2
    return y[:b] + jnp.flip(y[b:], axis=1)


def gated_delta_chunked(q, k, v, g, beta):
    n, l, h, dk = q.shape
    dv = v.shape[-1]
    c = GDN_CHUNK
    nc = l // c

    def chunks(t):
        return jnp.moveaxis(t.reshape((n, nc, c, h) + t.shape[3:]), 3, 1)

    q, k, v, g, beta = chunks(q), chunks(k), chunks(v), chunks(g), chunks(beta)
    cum = jnp.cumsum(g, axis=-1)
    diff = cum[..., :, None] - cum[..., None, :]
    idx = jnp.arange(c)
    incl = idx[:, None] >= idx[None, :]
    strict = idx[:, None] > idx[None, :]
    d_incl = jnp.exp(jnp.where(incl, diff, -jnp.inf))
    d_strict = jnp.where(strict, d_incl, 0.0)
    a_mat = beta[..., :, None] * jnp.einsum('nhcid,nhcjd->nhcij', k, k) * d_strict
    gam = jnp.exp(cum)
    rhs = jnp.concatenate([beta[..., None] * v, (beta * gam)[..., None] * k], axis=-1)
    sol = lax.linalg.triangular_solve(a_mat, rhs, left_side=True, lower=True, unit_diagonal=True)
    u0, w = sol[..., :dv], sol[..., dv:]
    p_intra = jnp.einsum('nhcid,nhcjd->nhcij', q, k) * d_incl
    q_g = q * gam[..., None]
    k_d = k * jnp.exp(cum[..., -1:] - cum)[..., None]
    g_end = jnp.exp(cum[..., -1])
    xs = tuple(jnp.moveaxis(t, 2, 0) for t in (w, u0, p_intra, q_g, k_d, g_end))

    def step(s, inp):
        w_c, u0_c, p_c, qg_c, kd_c, ge_c = inp
        u = u0_c - jnp.einsum('nhcd,nhde->nhce', w_c, s)
        o = jnp.einsum('nhcd,nhde->nhce', qg_c, s) + jnp.einsum('nhij,nhje->nhie', p_c, u)
        s = ge_c[..., None, None] * s + jnp.einsum('nhcd,nhce->nhde', kd_c, u)
        return s, o

    s0 = jnp.zeros((n, h, dk, dv), F32)
    _, o = lax.scan(step, s0, xs)
    return jnp.transpose(o, (1, 0, 3, 2, 4)).reshape(n, l, h, dv)


def gdn_branch(cols, conv_w, a_log, dt_bias, norm_g):
    b, l = cols['gdn_q'].shape[:2]
    qkv = jnp.concatenate([cols['gdn_q'], cols['gdn_k'], cols['gdn_v']], axis=-1)
    qkv = jax.nn.silu(centred_depthwise_conv(qkv, conv_w))
    q, k, v = jnp.split(qkv, [GDN_QK_W, 2 * GDN_QK_W], axis=-1)
    q = l2_normalize(q.reshape(b, l, GDN_HEADS, GDN_DK)) * (GDN_DK ** -0.5)
    k = l2_normalize(k.reshape(b, l, GDN_HEADS, GDN_DK))
    v = v.reshape(b, l, GDN_HEADS, GDN_DV).astype(F32)
    a = cols['gdn_a'].astype(F32).reshape(b, l, 2, GDN_HEADS)
    bt = cols['gdn_b'].astype(F32).reshape(b, l, 2, GDN_HEADS)
    g = -jnp.exp(a_log.astype(F32)) * jax.nn.softplus(a + dt_bias.astype(F32))
    beta = jax.nn.sigmoid(bt)
    o = gated_delta_chunked(
        bidir_stack(q, q), bidir_stack(k, k), bidir_stack(v, v),
        bidir_stack(g[:, :, 0], g[:, :, 1]), bidir_stack(beta[:, :, 0], beta[:, :, 1]))
    o = rms_norm(bidir_merge(o), norm_g)
    return o.reshape(b, l, GDN_W)


def apply_rope(x, cos, sin):
    xf = x.astype(F32)
    x1, x2 = jnp.split(xf, 2, axis=-1)
    return jnp.concatenate([x1 * cos - x2 * sin, x2 * cos + x1 * sin], axis=-1)


def mla_branch(cols, positions, q_norm_g, w_uq, kv_norm_g, w_ukv):
    b, l = cols['mla_cq'].shape[:2]
    q = (rms_norm(cols['mla_cq'], q_norm_g) @ w_uq).reshape(b, l, MLA_HEADS, MLA_NOPE + MLA_ROPE)
    kv = (rms_norm(cols['mla_ckv'], kv_norm_g) @ w_ukv).reshape(b, l, MLA_HEADS, MLA_NOPE + MLA_V)
    q_nope, q_rot = q[..., :MLA_NOPE], q[..., MLA_NOPE:]
    k_nope, v = kv[..., :MLA_NOPE], kv[..., MLA_NOPE:]
    half = MLA_ROPE // 2
    inv_freq = ROPE_THETA ** (-jnp.arange(half, dtype=F32) / half)
    ang = positions.astype(F32)[..., None] * inv_freq
    cos, sin = jnp.cos(ang), jnp.sin(ang)
    q_rot = apply_rope(q_rot, cos[:, :, None], sin[:, :, None])
    k_rot = apply_rope(cols['mla_kr'], cos, sin)
    scale = (MLA_NOPE + MLA_ROPE) ** -0.5
    q = jnp.concatenate([q_nope.astype(F32), q_rot], axis=-1) * scale
    k = jnp.concatenate([k_nope.astype(F32),
                         jnp.broadcast_to(k_rot[:, :, None], (b, l, MLA_HEADS, MLA_ROPE))], axis=-1)
    v = v.astype(F32)
    nb = l // MLA_QBLOCK
    q_blocks = jnp.moveaxis(q.reshape(b, nb, MLA_QBLOCK, MLA_HEADS, MLA_NOPE + MLA_ROPE), 1, 0)

    def attend(q_blk):
        s = jnp.einsum('bqhd,bkhd->bhqk', q_blk, k)
        p = jax.nn.softmax(s, axis=-1)
        return jnp.einsum('bhqk,bkhd->bqhd', p, v)

    o = lax.map(attend, q_blocks)
    return jnp.moveaxis(o, 0, 1).reshape(b, l, MLA_W)


def to_heads(t):
    return t.reshape(t.shape[:-1] + (RWKV_HEADS, RWKV_HD))


def rwkv7_branch(cols, mu, w0, w2, a0, a2, k_k, k_a, r_k, ln_g, ln_b):
    b, l = cols['rw_r'].shape[:2]
    feats = jnp.concatenate([cols['rw_r'], cols['rw_k'], cols['rw_v'], cols['rw_wd'], cols['rw_ad']],
                            axis=-1).astype(F32)
    feats = centred_token_shift(feats, mu.astype(F32))
    r, k, v, wd, ad = jnp.split(feats, [RWKV_W, 2 * RWKV_W, 3 * RWKV_W, 3 * RWKV_W + RWKV_DECAY_LORA], axis=-1)
    w_log = -jax.nn.softplus(-(w0.astype(F32)[:, None, None]
                               + jnp.einsum('bld,edc->eblc', jnp.tanh(wd), w2.astype(F32)))) - 0.5
    decay = jnp.exp(-jnp.exp(w_log))
    a = jax.nn.sigmoid(a0.astype(F32)[:, None, None] + jnp.einsum('bld,edc->eblc', ad, a2.astype(F32)))
    kk = l2_normalize(to_heads(k * k_k.astype(F32)))
    k_mod = k[None] * (1.0 + (a - 1.0) * k_a.astype(F32))
    r_h, v_h = to_heads(r), to_heads(v)
    a_h = to_heads(a)
    xs = (
        bidir_stack(r_h, r_h),
        bidir_stack(to_heads(decay[0]), to_heads(decay[1])),
        bidir_stack(to_heads(k_mod[0]), to_heads(k_mod[1])),
        bidir_stack(v_h, v_h),
        bidir_stack(kk, kk),
        bidir_stack(a_h[0] * kk, a_h[1] * kk),
    )
    xs = tuple(jnp.moveaxis(t, 1, 0) for t in xs)

    def step(s, inp):
        r_t, w_t, k_t, v_t, kk_t, b_t = inp
        sa = jnp.einsum('nhij,nhj->nhi', s, -kk_t)
        s = s * w_t[:, :, None, :] + sa[..., None] * b_t[:, :, None, :] + v_t[..., None] * k_t[:, :, None, :]
        y = jnp.einsum('nhij,nhj->nhi', s, r_t)
        return s, y

    s0 = jnp.zeros((2 * b, RWKV_HEADS, RWKV_HD, RWKV_HD), F32)
    _, y = lax.scan(step, s0, xs)
    y = bidir_merge(jnp.moveaxis(y, 0, 1))
    mean = jnp.mean(y, axis=-1, keepdims=True)
    var = jnp.mean(jnp.square(y - mean), axis=-1, keepdims=True)
    y = ((y - mean) * lax.rsqrt(var + RWKV_LN_EPS)).reshape(b, l, RWKV_W) * ln_g.astype(F32) + ln_b.astype(F32)
    k_bonus = to_heads(0.5 * (k_mod[0] + k_mod[1]))
    bonus = jnp.sum(r_h * k_bonus * r_k.astype(F32), axis=-1, keepdims=True) * v_h
    return y + bonus.reshape(b, l, RWKV_W)


def setup_inputs(seed: int = 0) -> dict:
    key = jax.random.key(seed)
    ks = jax.random.split(key, 24)

    def nrm(k, shape, scale):
        return jax.random.normal(k, shape, F32) * scale

    x = nrm(ks[0], (BATCH, SEQ, D_MODEL), 1.0)
    positions = (jnp.arange(SEQ, dtype=jnp.int32)[None, :]
                 + jax.random.randint(ks[1], (BATCH, 1), 0, 512, dtype=jnp.int32))
    norm_g = 1.0 + nrm(ks[2], (DEPTH, D_MODEL), 0.02)
    w_in = nrm(ks[3], (DEPTH, D_MODEL, N_IN), D_MODEL ** -0.5)
    gdn_conv = nrm(ks[4], (DEPTH, GDN_CONV, GDN_QKV_W), GDN_CONV ** -0.5)
    gdn_a_log = jnp.log(jax.random.uniform(ks[5], (DEPTH, 2, GDN_HEADS), F32, 1.0, 16.0))
    dt = jnp.exp(jax.random.uniform(ks[6], (DEPTH, 2, GDN_HEADS), F32, math.log(1e-3), math.log(1e-1)))
    gdn_dt_bias = dt + jnp.log(-jnp.expm1(-dt))
    gdn_norm_g = 1.0 + nrm(ks[7], (DEPTH, GDN_DV), 0.02)
    mla_q_norm_g = 1.0 + nrm(ks[8], (DEPTH, MLA_Q_RANK), 0.02)
    mla_w_uq = nrm(ks[9], (DEPTH, MLA_Q_RANK, MLA_HEADS * (MLA_NOPE + MLA_ROPE)), MLA_Q_RANK ** -0.5)
    mla_kv_norm_g = 1.0 + nrm(ks[10], (DEPTH, MLA_KV_RANK), 0.02)
    mla_w_ukv = nrm(ks[11], (DEPTH, MLA_KV_RANK, MLA_HEADS * (MLA_NOPE + MLA_V)), MLA_KV_RANK ** -0.5)
    rwkv_mu = jax.random.uniform(ks[12], (DEPTH, 2, RWKV_SHIFT_W), F32, 0.0, 0.5)
    rwkv_w0 = jax.random.uniform(ks[13], (DEPTH, 2, RWKV_W), F32, -6.0, -1.0)
    rwkv_w2 = nrm(ks[14], (DEPTH, 2, RWKV_DECAY_LORA, RWKV_W), 0.1)
    rwkv_a0 = nrm(ks[15], (DEPTH, 2, RWKV_W), 0.1)
    rwkv_a2 = nrm(ks[16], (DEPTH, 2, RWKV_ICLR_LORA, RWKV_W), 0.1)
    rwkv_k_k = 0.85 + nrm(ks[17], (DEPTH, RWKV_W), 0.02)
    rwkv_k_a = 1.0 + nrm(ks[18], (DEPTH, RWKV_W), 0.02)
    rwkv_r_k = nrm(ks[19], (DEPTH, RWKV_HEADS, RWKV_HD), 0.1)
    rwkv_ln_g = 1.0 + nrm(ks[20], (DEPTH, RWKV_W), 0.02)
    rwkv_ln_b = nrm(ks[21], (DEPTH, RWKV_W), 0.02)
    w_out = nrm(ks[22], (DEPTH, D_MIX, D_MODEL), D_MIX ** -0.5)
    final_norm_g = 1.0 + nrm(ks[23], (D_MODEL,), 0.02)
    return {
        'x': x, 'positions': positions, 'norm_g': norm_g, 'w_in': w_in,
        'gdn_conv': gdn_conv, 'gdn_a_log': gdn_a_log, 'gdn_dt_bias': gdn_dt_bias, 'gdn_norm_g': gdn_norm_g,
        'mla_q_norm_g': mla_q_norm_g, 'mla_w_uq': mla_w_uq, 'mla_kv_norm_g': mla_kv_norm_g, 'mla_w_ukv': mla_w_ukv,
        'rwkv_mu': rwkv_mu, 'rwkv_w0': rwkv_w0, 'rwkv_w2': rwkv_w2, 'rwkv_a0': rwkv_a0, 'rwkv_a2': rwkv_a2,
        'rwkv_k_k': rwkv_k_k, 'rwkv_k_a': rwkv_k_a, 'rwkv_r_k': rwkv_r_k,
        'rwkv_ln_g': rwkv_ln_g, 'rwkv_ln_b': rwkv_ln_b,
        'w_out': w_out, 'final_norm_g': final_norm_g,
    }


def reference(x, positions, norm_g, w_in, gdn_conv, gdn_a_log, gdn_dt_bias, gdn_norm_g,
              mla_q_norm_g, mla_w_uq, mla_kv_norm_g, mla_w_ukv,
              rwkv_mu, rwkv_w0, rwkv_w2, rwkv_a0, rwkv_a2, rwkv_k_k, rwkv_k_a, rwkv_r_k,
              rwkv_ln_g, rwkv_ln_b, w_out, final_norm_g):
    for layer in range(DEPTH):
        h = rms_norm(x, norm_g[layer])
        cols = split_columns(h @ w_in[layer])
        y_gdn = gdn_branch(cols, gdn_conv[layer], gdn_a_log[layer], gdn_dt_bias[layer], gdn_norm_g[layer])
        y_mla = mla_branch(cols, positions, mla_q_norm_g[layer], mla_w_uq[layer],
                           mla_kv_norm_g[layer], mla_w_ukv[layer])
        y_rwkv = rwkv7_branch(cols, rwkv_mu[layer], rwkv_w0[layer], rwkv_w2[layer], rwkv_a0[layer],
                              rwkv_a2[layer], rwkv_k_k[layer], rwkv_k_a[layer], rwkv_r_k[layer],
                              rwkv_ln_g[layer], rwkv_ln_b[layer])
        mix = jnp.concatenate([y_gdn.astype(F32), y_mla.astype(F32), y_rwkv.astype(F32)], axis=-1)
        mix = mix * jax.nn.silu(cols['gate'].astype(F32))
        x = x + mix.astype(x.dtype) @ w_out[layer]
    return rms_norm(x, final_norm_g)
```

```python
import numpy as np
import concourse.bass as bass
import concourse.mybir as mybir
from concourse.bass_utils import run_bass_kernel_spmd
from contextlib import ExitStack

F32 = mybir.dt.float32
BF16 = mybir.dt.bfloat16
I32 = mybir.dt.int32
AF = mybir.ActivationFunctionType
ALU = mybir.AluOpType
AX = mybir.AxisListType
DTSIZE = {F32: 4, BF16: 2, I32: 4}


class _Op:
    __slots__ = ("eng", "emit", "deps", "idx", "needed", "sigval", "dsem", "dval", "isdma")

    def __init__(self, eng, emit, isdma=False):
        self.eng = eng
        self.emit = emit
        self.deps = []
        self.idx = -1
        self.needed = False
        self.sigval = 0
        self.dsem = None
        self.dval = 0
        self.isdma = isdma


class _Blk:
    __slots__ = ("w", "r")

    def __init__(self):
        self.w = None
        self.r = {}


class Prog:
    ENGS = ("pe", "dve", "act", "pool", "sp")
    NDMA = 48
    NHW = 32

    def __init__(self, nc, stack):
        self.nc = nc
        self.stack = stack
        self.ops = {e: [] for e in self.ENGS}
        self.track = {}
        self.seen = {e: {} for e in self.ENGS}
        self.seen_dma = {e: set() for e in self.ENGS}
        self.dma_last = [None] * self.NDMA
        self.dma_uses = [0] * self.NDMA
        self.dma_rr = 0
        self.dma_rr_sw = 0
        self.ndma_ops = 0
        self.untracked = set()
        self.out_dmas = []
        self.dma_pending = []
        self.last_compute = {}

    def sb(self, name, shape, dtype=F32, blk=None):
        self.uid = getattr(self, "uid", 0) + 1
        name = "s%d_%s" % (self.uid, name)
        t = self.stack.enter_context(self.nc.sbuf_tensor(name, list(shape), dtype))
        self._register(name, shape, dtype, blk)
        return t

    def ps(self, name, shape=(128, 512), dtype=F32, blk=None):
        self.uid = getattr(self, "uid", 0) + 1
        name = "p%d_%s" % (self.uid, name)
        t = self.stack.enter_context(self.nc.psum_tensor(name, list(shape), dtype))
        self._register(name, shape, dtype, blk)
        return t

    def _register(self, name, shape, dtype, blk):
        row = int(np.prod(shape[1:])) * DTSIZE[dtype]
        bb = row if blk is None else blk * DTSIZE[dtype]
        nb = (row + bb - 1) // bb
        self.track[name] = (bb, row, [_Blk() for _ in range(nb)])

    def dram_track(self, name, total_bytes, blk_bytes):
        nb = (total_bytes + blk_bytes - 1) // blk_bytes
        self.track[name] = (blk_bytes, -1, [_Blk() for _ in range(nb)])

    def _blocks(self, ap):
        name = ap.tensor.name
        if name not in self.track:
            return ()
        bb, row, blks = self.track[name]
        if len(blks) == 1:
            return blks
        ds = DTSIZE[ap.dtype]
        pat = ap.ap
        if row < 0:
            lo = hi = ap.offset
            for step, cnt in pat:
                ext = step * (cnt - 1)
                if ext < 0:
                    lo += ext
                else:
                    hi += ext
            return blks[(lo * ds) // bb:(hi * ds) // bb + 1]
        rowel = row // ds
        foff = ap.offset % rowel
        lo = hi = foff
        for step, cnt in pat[1:]:
            ext = step * (cnt - 1)
            if ext < 0:
                lo += ext
            else:
                hi += ext
        b0 = (lo * ds) // bb
        b1 = (hi * ds) // bb
        return blks[b0:b1 + 1]

    def _dep(self, x, y):
        if y is None or y is x:
            return
        e = x.eng
        if y.isdma:
            if id(y) in self.seen_dma[e]:
                return
            self.seen_dma[e].add(id(y))
            x.deps.append(y)
            return
        if y.eng == "pe" and e == "pe":
            return
        if y.idx <= self.seen[e].get(y.eng, -1):
            return
        self.seen[e][y.eng] = y.idx
        y.needed = True
        x.deps.append(y)

    def add(self, eng, emit, reads=(), writes=(), isdma=False):
        x = _Op(eng, emit, isdma)
        x.idx = len(self.ops[eng])
        rb = []
        for ap in reads:
            if ap is None or isinstance(ap, (int, float)):
                continue
            rb.extend(self._blocks(ap))
        wb = []
        for ap in writes:
            wb.extend(self._blocks(ap))
        for ap in reads:
            if ap is None or isinstance(ap, (int, float)) or not ap.tensor.name.startswith("p"):
                continue
            for b in self._blocks(ap):
                for key, y in b.r.items():
                    if key != eng:
                        self._dep(x, y)
        for b in rb:
            self._dep(x, b.w)
        for b in wb:
            self._dep(x, b.w)
            for y in b.r.values():
                self._dep(x, y)
        if isdma:
            if eng == "pool":
                s = self.NHW + self.dma_rr_sw
                self.dma_rr_sw = (self.dma_rr_sw + 1) % (self.NDMA - self.NHW)
            else:
                s = self.dma_rr
                self.dma_rr = (self.dma_rr + 1) % self.NHW
            self._dep(x, self.dma_last[s])
            self.dma_last[s] = x
            self.dma_uses[s] += 1
            x.dsem = s
            x.dval = 16 * self.dma_uses[s]
            self.ndma_ops += 1
        key = id(x) if isdma else eng
        for b in rb:
            b.r[key] = x
        for b in wb:
            b.w = x
            b.r = {}
        self.ops[eng].append(x)
        if isdma:
            self.dma_pending.append(x)
        else:
            self.last_compute[eng] = x
        return x

    def barrier(self):
        lasts = dict(self.last_compute)
        pend = list(self.dma_pending)
        self.dma_pending = []
        for e in self.ENGS:
            b = _Op(e, None)
            b.idx = len(self.ops[e])
            for e2, y in lasts.items():
                if e2 == e and e == "pe":
                    continue
                self._dep(b, y)
            for y in pend:
                self._dep(b, y)
            self.ops[e].append(b)

    def mm(self, out, lhsT, rhs, start=True, stop=True):
        return self.add("pe", lambda e: e.matmul(out, lhsT, rhs, start=start, stop=stop),
                        reads=(lhsT, rhs), writes=(out,))

    def tr(self, out, in_, ident):
        return self.add("pe", lambda e: e.transpose(out, in_, ident), reads=(in_, ident), writes=(out,))

    def tt(self, out, in0, in1, op, eng="dve"):
        return self.add(eng, lambda e: e.tensor_tensor(out, in0, in1, op), reads=(in0, in1), writes=(out,))

    def ts(self, out, in0, s1, op0, s2=None, op1=None, eng="dve", accum_out=None):
        kw = {}
        if eng == "pool" and op1 is None:
            if op0 == ALU.mult:
                s2, op1 = 0.0, ALU.add
            elif op0 == ALU.add:
                s2, op1 = 1.0, ALU.mult
        if op1 is not None:
            kw["op1"] = op1
        if accum_out is not None:
            kw["accum_out"] = accum_out
        w = (out,) if accum_out is None else (out, accum_out)
        return self.add(eng, lambda e: e.tensor_scalar(out, in0, s1, s2, op0, **kw),
                        reads=(in0, s1, s2), writes=w)

    def stt(self, out, in0, scalar, in1, op0, op1, accum_out=None):
        kw = {}
        if accum_out is not None:
            kw["accum_out"] = accum_out
        w = (out,) if accum_out is None else (out, accum_out)
        return self.add("dve", lambda e: e.scalar_tensor_tensor(out, in0, scalar, in1, op0, op1, **kw),
                        reads=(in0, scalar, in1), writes=w)

    def cp(self, out, in_, eng="dve"):
        if eng == "act":
            return self.add("act", lambda e: e.copy(out, in_), reads=(in_,), writes=(out,))
        return self.add(eng, lambda e: e.tensor_copy(out, in_), reads=(in_,), writes=(out,))

    def act(self, out, in_, func, bias=0.0, scale=1.0, accum_out=None):
        kw = {}
        if accum_out is not None:
            kw["accum_out"] = accum_out
        w = (out,) if accum_out is None else (out, accum_out)
        return self.add("act", lambda e: e.activation(out, in_, func, bias=bias, scale=scale, **kw),
                        reads=(in_, bias, scale), writes=w)

    def red(self, out, in_, op, axis=AX.X, eng="dve"):
        return self.add(eng, lambda e: e.tensor_reduce(out, in_, axis, op), reads=(in_,), writes=(out,))

    def recip(self, out, in_):
        return self.add("dve", lambda e: e.reciprocal(out, in_), reads=(in_,), writes=(out,))

    def rpow(self, out, in_, power, scale=1.0, bias=0.0):
        self.act(out, in_, AF.Ln, bias=bias, scale=scale)
        return self.act(out, out, AF.Exp, scale=power)

    def memset(self, ap, val, eng="dve"):
        return self.add(eng, lambda e: e.memset(ap, val), writes=(ap,))

    def scan(self, out, d0, d1, init, op0, op1):
        return self.add("dve", lambda e: e.tensor_tensor_scan(out, d0, d1, init, op0, op1),
                        reads=(d0, d1, init), writes=(out,))

    def dma(self, out, in_, eng="sp", is_output=False):
        x = self.add(eng, lambda e: e.dma_start(out=out, in_=in_), reads=(in_,), writes=(out,), isdma=True)
        if is_output:
            self.out_dmas.append(x)
        return x

    def finish(self):
        nc = self.nc
        fin = _Op("sp", None)
        fin.idx = len(self.ops["sp"])
        for y in self.out_dmas:
            self._dep(fin, y)
        self.ops["sp"].append(fin)
        sems = {}
        for e in ("pe", "dve", "act", "pool"):
            sems[e] = self.stack.enter_context(nc.semaphore("s_" + e))
        dsems = [self.stack.enter_context(nc.semaphore("d%d" % i)) for i in range(self.NDMA)]
        for e in ("pe", "dve", "act", "pool"):
            c = 0
            for x in self.ops[e]:
                if x.isdma:
                    continue
                if x.needed:
                    c += 1
                    x.sigval = c
            self.stats_sig = getattr(self, "stats_sig", {})
            self.stats_sig[e] = c
        ops = self.ops

        def replay(e, engobj):
            for x in ops[e]:
                for y in x.deps:
                    if y.isdma:
                        engobj.wait_ge(dsems[y.dsem], y.dval)
                    else:
                        engobj.wait_ge(sems[y.eng], y.sigval)
                if x.emit is None:
                    continue
                ins = x.emit(engobj)
                if x.isdma:
                    ins.then_inc(dsems[x.dsem], 16)
                elif x.needed:
                    ins.then_inc(sems[e], 1)

        with nc.Block() as block:
            @block.tensor
            def _(eng):
                replay("pe", eng)

            @block.vector
            def _(eng):
                replay("dve", eng)

            @block.scalar
            def _(eng):
                replay("act", eng)

            @block.gpsimd
            def _(eng):
                replay("pool", eng)

            @block.sync
            def _(eng):
                replay("sp", eng)


L = 2048
D = 1024
NT = L // 128
DEPTH = 2
N_IN = 3448
EPS = 1e-6

OFF = dict(gate=0, gdn_q=1024, gdn_k=1408, gdn_v=1792, gdn_a=2176, gdn_b=2188, mla_cq=2200, mla_ckv=2392,
           mla_kr=2520, rw_r=2552, rw_k=2808, rw_v=3064, rw_wd=3320, rw_ad=3384)


def chunk_table():
    ch = []
    for h in range(3):
        ch.append(("gq%d" % h, [(0, OFF["gdn_q"] + h * 128, 128)]))
        ch.append(("gk%d" % h, [(0, OFF["gdn_k"] + h * 128, 128)]))
        ch.append(("gv%d" % h, [(0, OFF["gdn_v"] + h * 128, 128)]))
    ch.append(("gab", [(0, OFF["gdn_a"], 6), (32, OFF["gdn_a"] + 6, 6), (64, OFF["gdn_b"], 6), (96, OFF["gdn_b"] + 6, 6)]))
    ch.append(("cq0", [(0, OFF["mla_cq"], 128)]))
    ch.append(("cq1", [(0, OFF["mla_cq"] + 128, 64)]))
    ch.append(("ckv", [(0, OFF["mla_ckv"], 128)]))
    ch.append(("kr", [(0, OFF["mla_kr"], 32)]))
    for i in range(2):
        ch.append(("rr%d" % i, [(0, OFF["rw_r"] + i * 128, 128)]))
        ch.append(("rk%d" % i, [(0, OFF["rw_k"] + i * 128, 128)]))
        ch.append(("rv%d" % i, [(0, OFF["rw_v"] + i * 128, 128)]))
    ch.append(("rwd", [(0, OFF["rw_wd"], 64)]))
    ch.append(("rad", [(0, OFF["rw_ad"], 64)]))
    for i in range(8):
        ch.append(("g%d" % i, [(0, OFF["gate"] + i * 128, 128)]))
    return ch


CHUNKS = chunk_table()
CH_IDX = {n: i for i, (n, _) in enumerate(CHUNKS)}
NCH = len(CHUNKS)


def host_win(w_in):
    out = np.zeros((DEPTH, NCH, 128, 8, 128), np.float32)
    for ci, (_, parts) in enumerate(CHUNKS):
        for dst, src, w in parts:
            blk = w_in[:, :, src:src + w].reshape(DEPTH, 8, 128, w)
            out[:, ci, :, :, dst:dst + w] = blk.transpose(0, 2, 1, 3)
    return out


class K:
    pass


def build(depth=DEPTH, mixers=("gdn", "mla", "rwkv"), dbg=False):
    nc = bass.Bass("TRN2", target_bir_lowering=False)
    k = K()
    k.nc = nc
    k.dbg = dbg
    k.dbg_outs = []
    k.cut = 99

    def din(name, shape, dt=F32):
        return nc.dram_tensor(name, list(shape), dt, kind="ExternalInput").ap()

    k.x_d = din("x", [L, D])
    k.win_d = din("win", [DEPTH, NCH, 128, 8, 128])
    k.normg_d = din("normg", [DEPTH, 128, 8])
    k.wout_d = din("wout", [DEPTH, 128, 8, 1024])
    k.fing_d = din("fing", [1, D])
    k.ident_d = din("ident", [128, 128])
    k.out_d = nc.dram_tensor("out", [L, D], F32, kind="ExternalOutput").ap()
    mla_decl(k)
    gdn_decl(k)
    rwkv_decl(k)

    with ExitStack() as st:
        P = Prog(nc, st)
        k.P = P
        k.xscr = nc.dram_tensor("xscr", [L, D], F32, kind="Internal").ap()
        P.dram_track("xscr", L * D * 4, 128 * D * 4)
        k.hT = P.sb("hT", [128, 8, L], BF16, blk=512)
        k.ident = P.sb("ident", [128, 128], F32)
        k.identb = P.sb("identb", [128, 128], BF16)
        k.normg = P.sb("normg", [128, DEPTH, 8], F32)
        k.wst = [P.sb("wst%d" % i, [128, 8, 128], F32) for i in range(2)]
        k.wbf = [P.sb("wbf%d" % i, [128, 8, 128], BF16) for i in range(2)]
        k.wrr = 0
        k.PAW = [P.ps("paw%d" % i, [128, 1024], F32, blk=512) for i in range(2)]
        k.PA = [k.PAW[i // 2][:, (i % 2) * 512:(i % 2 + 1) * 512] for i in range(4)]
        k.PB = [P.ps("pb%d" % i, [128, 512], F32) for i in range(4)]

        P.dma(k.ident[:], k.ident_d[:])
        P.cp(k.identb[:], k.ident[:])
        for l in range(DEPTH):
            P.dma(k.normg[:, l, :], k.normg_d[l])

        for l in range(depth):
            phase_a(k, l)
            with scope(k):
                k.mix_r = P.sb("mix_r", [128, 2, L], BF16, blk=512)
                if "rwkv" in mixers:
                    rwkv_phase(k, l)
                else:
                    P.memset(k.mix_r[:].rearrange("p a b -> p (a b)"), 1.0)
                with scope(k):
                    k.mix_m = P.sb("mix_m", [128, 3, L], BF16, blk=512)
                    if "mla" in mixers:
                        mla_phase(k, l)
                    else:
                        P.memset(k.mix_m[:].rearrange("p a b -> p (a b)"), 1.0)
                    with scope(k):
                        k.mix_g = P.sb("mix_g", [128, 3, L], BF16, blk=512)
                        if "gdn" in mixers:
                            gdn_phase(k, l)
                        else:
                            P.memset(k.mix_g[:].rearrange("p a b -> p (a b)"), 1.0)
                        if k.dbg:
                            for nm, t_, n_ in (("g", k.mix_g, 3), ("m", k.mix_m, 3), ("r", k.mix_r, 2)):
                                dump(k, "mix_%s%d" % (nm, l), t_[:].rearrange("p a b -> p (a b)"), [128, n_ * L])
                        phase_z(k, l, last=(l == depth - 1))
        P.finish()
        print("ops:", {e: len(v) for e, v in P.ops.items()}, "sig:", P.stats_sig, "dma:", P.ndma_ops)
    return k


def mixc(k, c):
    if c < 3:
        return k.mix_g[:, c, :]
    if c < 6:
        return k.mix_m[:, c - 3, :]
    return k.mix_r[:, c - 6, :]


def dump(k, name, ap, shape=None):
    if not k.dbg:
        return
    P = k.P
    shape = list(ap.shape) if shape is None else shape
    d = k.nc.dram_tensor("dbg_" + name, shape, ap.dtype, kind="ExternalOutput").ap()
    P.dma(d[:] if len(shape) == 2 else d, ap, is_output=True)
    k.dbg_outs.append("dbg_" + name)


def scope(k):
    class _S:
        def __enter__(s):
            s.old = k.P.stack
            s.st = ExitStack()
            s.st.__enter__()
            k.P.stack = s.st
            return s

        def __exit__(s, *a):
            k.P.barrier()
            k.P.stack = s.old
            s.st.__exit__(*a)
            return False
    return _S()


def phase_a(k, l):
    P = k.P
    with scope(k):
        ssq = P.sb("a_ssq", [128, NT])
        rs = P.sb("a_rs", [128, NT])
        rstd = P.sb("a_rstd", [128, NT])
        junk = [P.sb("a_junk%d" % i, [128, D], BF16) for i in range(2)]
        xs = [P.sb("a_xs%d" % i, [128, D], BF16) for i in range(2)]
        xin = [P.sb("a_xin%d" % i, [128, D], F32) for i in range(3)]
        src = k.x_d if l == 0 else k.xscr
        for tt in range(NT):
            b = tt % 2
            xt_ = xin[tt % 3]
            P.dma(xt_[:], src[tt * 128:(tt + 1) * 128, :])
            P.act(junk[b][:], xt_[:], AF.Square, accum_out=ssq[:, tt:tt + 1])
            P.act(rs[:, tt:tt + 1], ssq[:, tt:tt + 1], AF.Sqrt, bias=EPS, scale=1.0 / D)
            P.recip(rstd[:, tt:tt + 1], rs[:, tt:tt + 1])
            P.ts(xs[b][:], xt_[:], rstd[:, tt:tt + 1], ALU.mult)
            pt = k.PB[b][:].bitcast(BF16)
            for dc in range(8):
                P.tr(pt[:, dc * 128:(dc + 1) * 128], xs[b][:, dc * 128:(dc + 1) * 128], k.identb[:])
            P.cp(k.hT[:, :, tt * 128:(tt + 1) * 128], pt[:].rearrange("p (a b) -> p a b", a=8),
                 eng=("act" if tt % 2 == 0 else "dve"))


def proj(k, l, name, alt=False):
    P = k.P
    BK = k.PB if alt else k.PA
    ci = CH_IDX[name]
    b = k.wrr
    k.wrr ^= 1
    P.dma(k.wst[b][:], k.win_d[l, ci], eng="sp")
    gb = k.normg[:, l, :].unsqueeze(2).broadcast_to([128, 8, 128])
    P.tt(k.wbf[b][:], k.wst[b][:], gb, ALU.mult, eng="pool")
    for tb in range(4):
        for dc in range(8):
            P.mm(BK[tb][:, :], k.wbf[b][:, dc, :], k.hT[:, dc, tb * 512:(tb + 1) * 512], start=(dc == 0), stop=(dc == 7))
    return BK


ZQ = "act"


def phase_z(k, l, last):
    P = k.P
    with scope(k):
        wst = P.sb("z_wst", [128, 8, 512], F32)
        wob = P.sb("z_wob", [128, 8, 1024], BF16, blk=512)
        sg = [P.sb("z_sg%d" % i, [128, L], BF16, blk=512) for i in range(2)]
        for nb in range(2):
            P.dma(wst[:], k.wout_d[l, :, :, nb * 512:(nb + 1) * 512])
            P.cp(wob[:, :, nb * 512:(nb + 1) * 512], wst[:], eng="act")
        for gc in range(8):
            pa = proj(k, l, "g%d" % gc, alt=(gc % 2 == 1))
            s = sg[gc % 2]
            for tb in range(4):
                P.act(s[:, tb * 512:(tb + 1) * 512], pa[tb][:, :], AF.Silu)
                mc = mixc(k, gc)[:, tb * 512:(tb + 1) * 512]
                P.tt(mc, mc, s[:, tb * 512:(tb + 1) * 512], ALU.mult)
        xin = [P.sb("z_xin%d" % i, [128, D], F32) for i in range(3)]
        src = k.x_d if l == 0 else k.xscr
        if last:
            ssq = P.sb("f_ssq", [128, NT])
            rs = P.sb("f_rs", [128, NT])
            rstd = P.sb("f_rstd", [128, NT])
            junk = [P.sb("f_junk%d" % i, [128, D], BF16) for i in range(2)]
            gf = P.sb("f_g", [128, D])
            ot = [P.sb("f_o%d" % i, [128, D]) for i in range(2)]
            P.dma(gf[:], k.fing_d[0:1, :].partition_broadcast(128))
        for tt in range(NT):
            xt_ = xin[tt % 3]
            P.dma(xt_[:], src[tt * 128:(tt + 1) * 128, :])
            for nb in range(2):
                ps = k.PB[(tt * 2 + nb) % 4]
                for kc in range(8):
                    P.mm(ps[:, :], mixc(k, kc)[:, tt * 128:(tt + 1) * 128], wob[:, kc, nb * 512:(nb + 1) * 512],
                         start=(kc == 0), stop=(kc == 7))
                xs = xt_[:, nb * 512:(nb + 1) * 512]
                P.tt(xs, xs, ps[:, :], ALU.add)
            if not last:
                P.dma(k.xscr[tt * 128:(tt + 1) * 128, :], xt_[:], eng=ZQ)
            else:
                b = tt % 2
                P.act(junk[b][:], xt_[:], AF.Square, accum_out=ssq[:, tt:tt + 1])
                P.act(rs[:, tt:tt + 1], ssq[:, tt:tt + 1], AF.Sqrt, bias=EPS, scale=1.0 / D)
                P.recip(rstd[:, tt:tt + 1], rs[:, tt:tt + 1])
                P.stt(ot[b][:], xt_[:], rstd[:, tt:tt + 1], gf[:], ALU.mult, ALU.mult)
                P.dma(k.out_d[tt * 128:(tt + 1) * 128, :], ot[b][:], eng=ZQ, is_output=True)


def host_inputs(inp, b):
    m = {}
    m["x"] = np.ascontiguousarray(inp["x"][b])
    m["win"] = host_win(inp["w_in"])
    m["normg"] = np.ascontiguousarray(inp["norm_g"].reshape(DEPTH, 8, 128).transpose(0, 2, 1))
    m["wout"] = np.ascontiguousarray(inp["w_out"].reshape(DEPTH, 8, 128, 1024).transpose(0, 2, 1, 3))
    m["fing"] = np.ascontiguousarray(inp["final_norm_g"].reshape(1, D))
    m["ident"] = np.eye(128, dtype=np.float32)
    host_mla(inp, b, m)
    host_gdn(inp, b, m)
    host_rwkv(inp, b, m)
    return m


TWO_PI = 2.0 * np.pi


def host_mla(inp, b, m):
    half = 16
    inv_freq = (10000.0 ** (-np.arange(half, dtype=np.float32) / half)).astype(np.float32)
    invf = np.zeros((32, 1), np.float32)
    invf[:, 0] = np.tile(inv_freq, 2) / np.float32(TWO_PI)
    m["invf"] = invf
    rm = np.zeros((32, 32), np.float32)
    for i in range(16):
        rm[i, i + 16] = -1.0
        rm[i + 16, i] = 1.0
    m["rmT"] = np.ascontiguousarray(rm.T)
    m["pos"] = np.ascontiguousarray(inp["positions"][b].reshape(1, L).astype(np.int32))
    wuq = inp["mla_w_uq"]
    o = np.zeros((DEPTH, 128, 2, 6, 128), np.float32)
    for h in range(6):
        nope = wuq[:, :, h * 96:h * 96 + 64]
        rope = wuq[:, :, h * 96 + 64:h * 96 + 96]
        o[:, :, 0, h, 64:128] = nope[:, 0:128]
        o[:, 0:64, 1, h, 64:128] = nope[:, 128:192]
        o[:, :, 0, h, 0:32] = rope[:, 0:128]
        o[:, 0:64, 1, h, 0:32] = rope[:, 128:192]
    m["wuq"] = o
    gq = np.zeros((DEPTH, 128, 2), np.float32)
    gq[:, :, 0] = inp["mla_q_norm_g"][:, 0:128]
    gq[:, 0:64, 1] = inp["mla_q_norm_g"][:, 128:192]
    m["gq"] = gq
    wukv = inp["mla_w_ukv"]
    wk = np.zeros((DEPTH, 128, 6, 128), np.float32)
    wv = np.zeros((DEPTH, 128, 6, 64), np.float32)
    for h in range(6):
        wk[:, :, h, 64:128] = wukv[:, :, h * 128:h * 128 + 64]
        wv[:, :, h, :] = wukv[:, :, h * 128 + 64:h * 128 + 128]
    m["wuk"] = wk
    m["wuv"] = wv
    m["gkv"] = np.ascontiguousarray(inp["mla_kv_norm_g"].reshape(DEPTH, 128, 1))


def mla_decl(k):
    nc = k.nc

    def din(name, shape, dt=F32):
        return nc.dram_tensor(name, list(shape), dt, kind="ExternalInput").ap()
    k.invf_d = din("invf", [32, 1])
    k.rmT_d = din("rmT", [32, 32])
    k.pos_d = din("pos", [1, L], I32)
    k.wuq_d = din("wuq", [DEPTH, 128, 2, 6, 128])
    k.gq_d = din("gq", [DEPTH, 128, 2])
    k.wuk_d = din("wuk", [DEPTH, 128, 6, 128])
    k.wuv_d = din("wuv", [DEPTH, 128, 6, 64])
    k.gkv_d = din("gkv", [DEPTH, 128, 1])


def latent_norm(k, l, names, nfeat, outs, ones):
    P = k.P
    sq = [P.sb("ln_sq%d" % i, [128, 512]) for i in range(2)]
    rq = [P.sb("ln_rq%d" % i, [128, 512]) for i in range(2)]
    n = len(names)
    for i, nm in enumerate(names):
        pa = proj(k, l, nm)
        for tb in range(4):
            s = sq[tb % 2]
            sk = ""
            if "a" not in sk:
                P.act(s[:], pa[tb][:, :], AF.Square)
            if "c" not in sk:
                P.cp(outs[i][:, tb * 512:(tb + 1) * 512], pa[tb][:, :], eng="dve")
            if "m" not in sk:
                P.mm(k.PB[tb][:, :], ones[:], s[:], start=(i == 0), stop=(i == n - 1))
    c2 = 9
    if c2 < 1:
        return
    for tb in range(4):
        r = rq[tb % 2]
        P.rpow(r[:], k.PB[tb][:, :], -0.5, scale=1.0 / nfeat, bias=EPS)
        for i in range(n):
            o = outs[i][:, tb * 512:(tb + 1) * 512]
            P.tt(o, o, r[:], ALU.mult)


def mla_phase(k, l):
    P = k.P
    SC = 96.0 ** -0.5
    with scope(k):
        cqn0 = P.sb("m_cqn0", [128, L], BF16, blk=512)
        cqn1 = P.sb("m_cqn1", [128, L], BF16, blk=512)
        ckvn = P.sb("m_ckvn", [128, L], BF16, blk=512)
        krope = P.sb("m_krope", [32, L], BF16, blk=512)
        cos2 = P.sb("m_cos2", [32, L], BF16, blk=512)
        sin2 = P.sb("m_sin2", [32, L], BF16, blk=512)
        wq = P.sb("m_wq", [128, 2, 6, 128], BF16)
        wk = P.sb("m_wk", [128, 6, 128], BF16)
        wv = P.sb("m_wv", [128, 6, 64], BF16)
        ones = P.sb("m_ones", [128, 128], F32)
        onesk = P.sb("m_onesk", [128, 128], F32)
        rmT = P.sb("m_rmT", [32, 32], F32)
        P.memset(ones[:], 1.0)
        P.memset(onesk[:], 1.0)
        P.memset(onesk[32:64, :], 0.0)
        P.dma(rmT[:], k.rmT_d[:])
        with scope(k):
            st = P.sb("m_st", [128, 2, 6, 128], F32)
            g = P.sb("m_g", [128, 4], F32)
            P.dma(st[:], k.wuq_d[l])
            P.dma(g[:, 0:2], k.gq_d[l])
            P.dma(g[:, 2:3], k.gkv_d[l])
            for kc in range(2):
                P.ts(wq[:, kc].rearrange("p a b -> p (a b)"), st[:, kc].rearrange("p a b -> p (a b)"),
                     g[:, kc:kc + 1], ALU.mult)
            st2 = P.sb("m_st2", [128, 6, 128], F32)
            P.dma(st2[:], k.wuk_d[l])
            P.ts(wk[:].rearrange("p a b -> p (a b)"), st2[:].rearrange("p a b -> p (a b)"), g[:, 2:3], ALU.mult)
            st3 = P.sb("m_st3", [128, 6, 64], F32)
            P.dma(st3[:], k.wuv_d[l])
            P.ts(wv[:].rearrange("p a b -> p (a b)"), st3[:].rearrange("p a b -> p (a b)"), g[:, 2:3], ALU.mult)
        if k.cut < 1:
            return
        with scope(k):
            latent_norm(k, l, ["cq0", "cq1"], 192, [cqn0, cqn1], ones)
            latent_norm(k, l, ["ckv"], 128, [ckvn], ones)
        if k.cut < 2:
            return
        with scope(k):
            invf = P.sb("m_invf", [32, 1], F32)
            P.dma(invf[:], k.invf_d[:])
            pa = proj(k, l, "kr")
            for tb in range(4):
                sl = slice(tb * 512, (tb + 1) * 512)
                b_ = tb % 2
                posi = P.sb("m_posi%d" % tb, [32, 512], I32)
                y = P.sb("m_y%d" % tb, [32, 512], F32)
                yi = P.sb("m_yi%d" % tb, [32, 512], I32)
                fr = P.sb("m_fr%d" % tb, [32, 512], F32)
                kr = P.sb("m_kr%d" % tb, [32, 512], F32)
                t1 = P.sb("m_t1%d" % tb, [32, 512], F32)
                t2 = P.sb("m_t2%d" % tb, [32, 512], F32)
                P.dma(posi[:], k.pos_d[0:1, sl].partition_broadcast(32))
                P.cp(y[:], posi[:])
                P.ts(y[:], y[:], invf[:, 0:1], ALU.mult)
                for off, dst in ((0.0, sin2), (0.25, cos2)):
                    if off != 0.0:
                        P.ts(y[:], y[:], off, ALU.add)
                    P.cp(yi[:], y[:])
                    P.cp(fr[:], yi[:])
                    P.tt(fr[:], y[:], fr[:], ALU.subtract)
                    P.act(dst[:, sl], fr[:], AF.Sin, scale=TWO_PI * (1.0 - 1e-6))
                P.cp(kr[:], pa[tb][0:32, :], eng="act")
                P.mm(k.PB[3][0:32, :], rmT[:], kr[:])
                P.tt(t1[:], kr[:], cos2[:, sl], ALU.mult)
                P.tt(t2[:], k.PB[3][0:32, :], sin2[:, sl], ALU.mult)
                P.tt(krope[:, sl], t1[:], t2[:], ALU.add)
        if k.cut < 3:
            return
        kT = [P.sb("m_kT%d" % i, [128, L], BF16, blk=512) for i in range(2)]
        qT = [P.sb("m_qT%d" % i, [128, L], BF16, blk=512) for i in range(2)]
        Vh = [P.sb("m_V%d" % i, [128, NT, 96], BF16) for i in range(2)]
        pT = [P.sb("m_pT%d" % i, [128, 1024], BF16, blk=512) for i in range(2)]
        sq = [P.sb("m_sq%d" % i, [128, 512], F32) for i in range(2)]
        qr = [P.sb("m_qr%d" % i, [32, 512], F32) for i in range(2)]
        t1 = P.sb("m_t1b", [32, 512], F32)
        t2 = P.sb("m_t2b", [32, 512], F32)
        mrow = P.sb("m_mrow", [64, 512], F32)
        km4 = P.sb("m_km4", [128, 4], F32)
        kmax2 = P.sb("m_kmax2", [128, 1], F32)
        rden = [P.sb("m_rden%d" % i, [64, 512], F32) for i in range(2)]
        for i in range(2):
            P.memset(kT[i][32:64, :], 0.0)
            P.memset(kT[i][32:33, :], 1.0)
            P.memset(qT[i][32:64, :], 0.0)
            P.memset(Vh[i][:, :, 64:96], 1.0)
        kmx = [P.sb("m_kmx%d" % i, [128, 1], F32) for i in range(2)]

        def prep(h):
            kt_, qt_, vh_ = kT[h % 2], qT[h % 2], Vh[h % 2]
            kmax2_ = kmx[h % 2]
            P.cp(kt_[0:32, :], krope[:], eng="pool")
            for tb in range(4):
                sl = slice(tb * 512, (tb + 1) * 512)
                s = sq[tb % 2]
                P.mm(k.PB[2][:, :], wk[:, h, :], ckvn[:, sl])
                yield
                P.cp(kt_[64:128, sl], k.PB[2][64:128, :], eng="dve")
                yield
                P.tt(s[:], kt_[:, sl], kt_[:, sl], ALU.mult, eng="pool")
                yield
                yield
                P.mm(k.PB[3][:, :], onesk[:], s[:])
                yield
                P.red(km4[:, tb:tb + 1], k.PB[3][:, :], ALU.max)
                yield
            P.red(kmax2_[:], km4[:], ALU.max)
            for half in range(2):
                for j in range(8):
                    tt = half * 8 + j
                    P.mm(k.PB[2][:, j * 64:(j + 1) * 64], ckvn[:, tt * 128:(tt + 1) * 128], wv[:, h, :])
                yield
                P.cp(vh_[:, half * 8:(half + 1) * 8, 0:64], k.PB[2][:, :].rearrange("p (a b) -> p a b", a=8), eng="dve")
                yield
            for tb in range(4):
                sl = slice(tb * 512, (tb + 1) * 512)
                s = sq[tb % 2]
                q_ = qr[tb % 2]
                P.mm(k.PB[2][:, :], wq[:, 0, h, :], cqn0[:, sl], start=True, stop=False)
                P.mm(k.PB[2][:, :], wq[:, 1, h, :], cqn1[:, sl], start=False, stop=True)
                yield
                P.cp(qt_[64:128, sl], k.PB[2][64:128, :], eng="dve")
                P.cp(q_[:], k.PB[2][0:32, :], eng="dve")
                yield
                P.act(s[:], k.PB[2][:, :], AF.Square)
                yield
                P.mm(k.PB[3][:, :], ones[:], s[:])
                yield
                P.act(mrow[32:33, :], k.PB[3][32:33, :], AF.Sqrt, scale=kmax2_[32:33, 0:1])
                yield
                P.ts(qt_[32:33, sl], mrow[32:33, :], -1.0, ALU.mult)
                P.mm(k.PB[2][0:32, :], rmT[:], q_[:])
                P.tt(t1[:], q_[:], cos2[:, sl], ALU.mult, eng="pool")
                yield
                P.tt(t2[:], k.PB[2][0:32, :], sin2[:, sl], ALU.mult)
                yield
                P.tt(qt_[0:32, sl], t1[:], t2[:], ALU.add)
                yield

        def attn(h):
            kt_, qt_, vh_ = kT[h % 2], qT[h % 2], Vh[h % 2]
            pti = 0
            for qb in range(4):
                qs = slice(qb * 512, (qb + 1) * 512)
                O = k.PB[qb % 2]

                def s_pair(m_):
                    for j in range(2):
                        kt = 2 * m_ + j
                        P.mm(k.PAW[m_ % 2][:, j * 512:(j + 1) * 512], kt_[:, kt * 128:(kt + 1) * 128], qt_[:, qs])
                s_pair(0)
                s_pair(1)
                for m_ in range(NT // 2):
                    p_ = pT[pti % 2]
                    pti += 1
                    P.act(p_[:], k.PAW[m_ % 2][:, :], AF.Exp, scale=SC)
                    if m_ + 2 < NT // 2:
                        s_pair(m_ + 2)
                    for j in range(2):
                        kt = 2 * m_ + j
                        P.mm(O[0:96, :], vh_[:, kt, :], p_[:, j * 512:(j + 1) * 512], start=(kt == 0), stop=(kt == NT - 1))
                    yield
                rd = rden[qb % 2]
                P.rpow(rd[0:32, :], O[64:96, :], -1.0)
                P.rpow(rd[32:64, :], O[64:96, :], -1.0)
                ob = (h % 2) * 64
                P.tt(mixc(k, 3 + h // 2)[ob:ob + 64, qs], O[0:64, :], rd[:], ALU.mult)
                yield

        for _ in prep(0):
            pass
        mode = "il"
        for h in range(6):
            gens = [attn(h)]
            if h + 1 < 6:
                if mode == "il":
                    gens.append(prep(h + 1))
                elif mode == "seq":
                    run_interleaved(gens)
                    gens = [prep(h + 1)]
                elif mode == "noprep":
                    pass
            if mode == "noprep" and h > 0:
                gens = [attn(0)]
            run_interleaved(gens)


NCK = L // 64
NEG = -30000.0


def host_gdn(inp, b, m):
    cw = inp["gdn_conv"]
    o = np.zeros((DEPTH, 128, 9, 5), np.float32)
    for part in range(3):
        for p in range(3):
            o[:, :, part * 3 + p, :] = cw[:, :, part * 384 + p * 128: part * 384 + (p + 1) * 128].transpose(0, 2, 1)
    m["gconv"] = o
    gb = np.zeros((DEPTH, 128, 2), np.float32)
    for d in range(2):
        gb[:, d * 32:d * 32 + 6, 0] = inp["gdn_dt_bias"][:, d, :]
        gb[:, d * 32:d * 32 + 6, 1] = inp["gdn_a_log"][:, d, :]
    m["ggb"] = gb
    m["gng"] = np.ascontiguousarray(np.tile(inp["gdn_norm_g"], (1, 2)).reshape(DEPTH, 128, 1))
    sel = np.zeros((64, 6, 128), np.float32)
    for d in range(2):
        for p in range(3):
            sel[d * 32 + 2 * p, d * 3 + p, 0:64] = 1.0
            sel[d * 32 + 2 * p + 1, d * 3 + p, 64:128] = 1.0
    m["gsel"] = sel
    j = np.arange(64)[:, None]
    i = np.arange(64)[None, :]
    nm = np.zeros((128, 2, 64), np.float32)
    nm[:, 0, :] = np.tile(np.where(i > j, 0.0, NEG), (2, 1))
    nm[:, 1, :] = np.tile(np.where(i < j, 0.0, NEG), (2, 1))
    m["gnegm"] = nm
    m["gid2"] = np.ascontiguousarray(np.tile(np.eye(64, dtype=np.float32), (2, 1)))


def gdn_decl(k):
    nc = k.nc

    def din(name, shape, dt=F32):
        return nc.dram_tensor(name, list(shape), dt, kind="ExternalInput").ap()
    k.gconv_d = din("gconv", [DEPTH, 128, 9, 5])
    k.ggb_d = din("ggb", [DEPTH, 128, 2])
    k.gng_d = din("gng", [DEPTH, 128, 1])
    k.gsel_d = din("gsel", [64, 6, 128])
    k.gnegm_d = din("gnegm", [128, 2, 64])
    k.gid2_d = din("gid2", [128, 64])


def bc3(ap2, n):
    return ap2.unsqueeze(2).broadcast_to([ap2.shape[0], ap2.shape[1], n])


def bcm(ap2, n):
    return ap2.unsqueeze(1).broadcast_to([ap2.shape[0], n, ap2.shape[1]])


HS = (slice(0, 64), slice(64, 128))


def v3(ps, n=8):
    return ps[:, 0:n * 64].rearrange("p (a b) -> p a b", a=n)


def mm2(P, ps, c, lhsT, rhs, **kw):
    for hs in HS:
        P.mm(ps[hs, c * 64:(c + 1) * 64], lhsT[hs], rhs[hs], **kw)


def tr2(P, ps, c, in_, ident):
    for hs in HS:
        P.mm(ps[hs, c * 64:(c + 1) * 64], in_[hs], ident[hs, hs])


def neumann2(k, Nn, Rm, tmp, bank, id2, n8=8):
    P = k.P
    idb = bcm(id2[:, :], n8)
    tA, tB, tC, tD = tmp
    pa_, pb_, pc_ = bank
    for c in range(n8):
        tr2(P, pa_, c, Nn[:, c, :], k.identb)
    P.cp(tA[:], v3(pa_, n8), eng="act")
    P.tt(Rm[:], Nn[:], idb, ALU.add)
    yield
    cur, curT = Nn, tA
    targets = [(tB, tC), (tD, tA)]
    for lvl in range(1, 7):
        nxt, nxtT = targets[(lvl - 1) % 2]
        for c in range(n8):
            if lvl <= 5:
                mm2(P, pb_, c, cur[:, c, :], curT[:, c, :])
                if lvl < 5:
                    mm2(P, pa_, c, curT[:, c, :], cur[:, c, :])
            if lvl >= 2:
                mm2(P, pc_, c, curT[:, c, :], Rm[:, c, :])
        if lvl <= 5:
            P.cp(nxtT[:], v3(pb_, n8), eng="act")
            if lvl < 5:
                P.cp(nxt[:], v3(pa_, n8), eng="act")
        if lvl >= 2:
            P.tt(Rm[:], Rm[:], v3(pc_, n8), ALU.add)
        yield
        cur, curT = nxt, nxtT


def run_interleaved(gens):
    gens = list(gens)
    while gens:
        for g in list(gens):
            try:
                next(g)
            except StopIteration:
                gens.remove(g)


def run_pipelined(chains, depth=2):
    active = []
    nxt = [0] * len(chains)

    def start(ci):
        if nxt[ci] < len(chains[ci]):
            active.append((ci, chains[ci][nxt[ci]](nxt[ci] % depth)))
            nxt[ci] += 1
    for ci in range(len(chains)):
        for _ in range(depth):
            start(ci)
    while active:
        for item in list(active):
            try:
                next(item[1])
            except StopIteration:
                active.remove(item)
                start(item[0])


def gdn_phase(k, l):
    P = k.P
    with scope(k):
        GC = P.sb("g_GC", [64, L], F32, blk=512)
        GP = [P.sb("g_GP%d" % p, [128, NCK, 4], F32) for p in range(3)]
        NBP = [P.sb("g_NBP%d" % p, [128, NCK, 2], F32) for p in range(3)]
        sel = P.sb("g_sel", [64, 6, 128], F32)
        negm = P.sb("g_negm", [128, 2, 64], F32)
        id2 = P.sb("g_id2", [128, 64], F32)
        cw = P.sb("g_cw", [128, 9, 5], F32)
        ng = P.sb("g_ng", [128, 1], F32)
        bones = P.sb("g_bones", [128, 128], F32)
        P.dma(sel[:], k.gsel_d[:])
        P.dma(negm[:], k.gnegm_d[:])
        P.dma(id2[:], k.gid2_d[:])
        P.dma(cw[:], k.gconv_d[l])
        P.dma(ng[:], k.gng_d[l])
        P.memset(bones[:], 0.0)
        P.memset(bones[0:64, 0:64], 1.0)
        P.memset(bones[64:128, 64:128], 1.0)
        with scope(k):
            GT = P.sb("g_GT", [128, L], F32, blk=512)
            m0 = P.sb("g_m0", [64, L], F32)
            gb = P.sb("g_gb", [128, 2], F32)
            negA = P.sb("g_negA", [128, 1], F32)
            P.dma(gb[:], k.ggb_d[l])
            P.act(negA[:], gb[:, 1:2], AF.Exp)
            P.ts(negA[:], negA[:], -1.0, ALU.mult)
            P.memset(m0[:], 1.0)
            P.memset(m0[:, 0:L:64], 0.0)
            pa = proj(k, l, "gab")
            for tb in range(4):
                sl = slice(tb * 512, (tb + 1) * 512)
                P.act(GT[0:64, sl], pa[tb][0:64, :], AF.Exp, bias=gb[0:64, 0:1])
                P.act(GT[64:128, sl], pa[tb][64:128, :], AF.Sigmoid)
            P.act(GT[0:64, :], GT[0:64, :], AF.Ln, bias=1.0)
            P.ts(GT[0:64, :], GT[0:64, :], negA[0:64, 0:1], ALU.mult)
            P.scan(GC[:, :], m0[:, :], GT[0:64, :], 0.0, ALU.mult, ALU.add)
            gc3 = GC[32:64, :].rearrange("p (a b) -> p a b", b=64)
            P.tt(m0[32:64, :].rearrange("p (a b) -> p a b", b=64), bc3(GC[32:64, 63:L:64], 64), gc3, ALU.subtract)
            P.tt(GC[32:64, :], m0[32:64, :], GT[32:64, :], ALU.add)
            for grp in range(4):
                g8 = slice(grp * 8, (grp + 1) * 8)
                for c in range(8):
                    ck = grp * 8 + c
                    cs = slice(c * 64, (c + 1) * 64)
                    for hs in HS:
                        P.mm(k.PB[0][hs, cs], GC[:, ck * 64:(ck + 1) * 64], k.ident[0:64, 0:64])
                        P.mm(k.PB[1][hs, cs], GT[64:128, ck * 64:(ck + 1) * 64], k.ident[64:128, 64:128])
                n_ = 0
                for p in range(3):
                    for hf, hs in enumerate(HS):
                        h = 2 * p + hf
                        for q, ps in ((0, k.PB[0]), (1, k.PB[1])):
                            src = v3(ps)[hs, :, h:h + 33:32]
                            P.cp(GP[p][hs, g8, 2 * q:2 * q + 2], src, eng=("act" if q else "dve"))
            for p in range(3):
                P.ts(NBP[p][:], GP[p][:, :, 2:4], -1.0, ALU.mult)
        for p in range(3):
            with scope(k):
                Q = P.sb("g_Q", [128, L], BF16, blk=512)
                K_ = P.sb("g_K", [128, L], BF16, blk=512)
                Kt = P.sb("g_Kt", [128, NCK, 64], BF16, blk=512)
                Vt = P.sb("g_Vt", [128, NCK, 64], BF16, blk=512)
                O = P.sb("g_O", [128, L], F32, blk=512)
                P.memset(O[:], 0.0, eng="pool")
                with scope(k):
                    xp = P.sb("g_xp", [128, L + 4], F32)
                    Vf = P.sb("g_Vf", [128, L], BF16, blk=512)
                    cv = P.sb("g_cv", [128, L], F32, blk=512)
                    sq = P.sb("g_sq", [128, 512], F32)
                    rn = P.sb("g_rn", [128, 512], F32)
                    P.memset(xp[:, 0:2], 0.0)
                    P.memset(xp[:, L + 2:L + 4], 0.0)
                    for part, nm, dst in ((0, "gq", Q), (1, "gk", K_), (2, "gv", Vf)):
                        pa = proj(k, l, "%s%d" % (nm, p))
                        for tb in range(4):
                            P.cp(xp[:, 2 + tb * 512:2 + (tb + 1) * 512], pa[tb][:, :], eng=("act" if tb % 2 else "dve"))
                        wi = part * 3 + p
                        P.ts(cv[:], xp[:, 0:L], cw[:, wi, 0:1], ALU.mult)
                        for j in range(1, 5):
                            P.stt(cv[:], xp[:, j:j + L], cw[:, wi, j:j + 1], cv[:], ALU.mult, ALU.add)
                        if part == 2:
                            P.act(dst[:], cv[:], AF.Silu)
                        else:
                            P.act(cv[:], cv[:], AF.Silu)
                        if part < 2:
                            for tb in range(4):
                                sl = slice(tb * 512, (tb + 1) * 512)
                                P.act(sq[:], cv[:, sl], AF.Square)
                                P.mm(k.PB[2][:, :], bones[:], sq[:])
                                if part == 0:
                                    P.rpow(rn[:], k.PB[2][:, :], -0.5, scale=64.0, bias=64e-6)
                                else:
                                    P.rpow(rn[:], k.PB[2][:, :], -0.5, scale=1.0, bias=1e-6)
                                P.tt(dst[:, sl], cv[:, sl], rn[:], ALU.mult)
                    for src, dstt in ((K_, Kt), (Vf, Vt)):
                        for grp in range(4):
                            ps = k.PB[grp % 2]
                            for c in range(8):
                                ck = grp * 8 + c
                                tr2(P, ps, c, src[:, ck * 64:(ck + 1) * 64], k.identb)
                            P.cp(dstt[:, grp * 8:(grp + 1) * 8, :], v3(ps), eng=("act" if grp % 2 else "dve"))
                with scope(k):
                    T = dict(GC=GC, GP=GP[p], NBP=NBP[p], sel=sel, negm=negm, id2=id2, Q=Q, K=K_, Kt=Kt, Vt=Vt, O=O)
                    run_pipelined([gdn_chain(k, p, d, T) for d in range(2)], depth=1)
                with scope(k):
                    sq = P.sb("g_osq", [128, 512], F32)
                    rn = P.sb("g_orn", [128, 512], F32)
                    for tb in range(4):
                        sl = slice(tb * 512, (tb + 1) * 512)
                        P.act(sq[:], O[:, sl], AF.Square)
                        P.mm(k.PB[2][:, :], bones[:], sq[:])
                        P.rpow(rn[:], k.PB[2][:, :], -0.5, scale=1.0 / 64, bias=EPS)
                        P.tt(rn[:], O[:, sl], rn[:], ALU.mult)
                        P.ts(mixc(k, p)[:, sl], rn[:], ng[:, 0:1], ALU.mult)


def gdn_chain(k, p, d, T):
    P = k.P
    GC, GP, NBP, sel, negm, id2, Q, K_, Kt, Vt, O = (T[n] for n in ("GC", "GP", "NBP", "sel", "negm", "id2", "Q", "K", "Kt", "Vt", "O"))
    B = k.PA if d == 0 else k.PB
    tag = "g%d_" % d
    names = ("CB", "EI", "QG", "Rm", "U0", "WT", "BW", "KD", "GK", "AcT", "Sg", "Ug", "nA", "nB", "nC", "nD", "Nb", "PTb", "Sgb")
    f32n = ("CB", "EI", "AcT", "Sg")
    NSET = 1
    GG = [{n: P.sb(tag + "%d" % s_ + n, [128, 8, 64], F32 if n in f32n else BF16) for n in names} for s_ in range(NSET)]
    for s_ in range(NSET):
        GG[s_]["gend"] = P.sb(tag + "gend%d" % s_, [128, 8], F32)
        GG[s_]["kds"] = P.sb(tag + "kds%d" % s_, [128, 8], F32)
    gam = P.sb(tag + "gam", [128, NCK], F32)
    shared = {"scan": 0}
    Scar = P.sb(tag + "Scar", [128, 64], F32)
    e_ = 63 if d == 0 else 0
    idb = bcm(id2[:, :], 8)

    def f2(t):
        return t[:].rearrange("p a b -> p (a b)")
    P.memset(Scar[:], 0.0)
    P.act(gam[:], GP[:, :, d], AF.Exp)
    gorder = list(range(4)) if d == 0 else list(range(3, -1, -1))

    def group(gi, grp, G):
        Nb, PTb, Sgb, gend, kds = G["Nb"], G["PTb"], G["Sgb"], G["gend"], G["kds"]
        sl = slice(grp * 512, (grp + 1) * 512)
        g8 = slice(grp * 8, (grp + 1) * 8)
        cj = GP[:, g8, d]
        nb = NBP[:, g8, d]
        CB, EI, QG, Rm, U0, WT, BW, KD, GK, AcT, Sg, Ug = (G[n] for n in names[:12])
        P.mm(B[0][:, :], sel[:, d * 3 + p, :], GC[:, sl])
        P.cp(f2(CB), B[0][:, :], eng="act")
        P.act(f2(EI), f2(CB), AF.Exp)
        P.cp(gend[:], EI[:, :, e_], eng="pool")
        P.tt(f2(QG), f2(EI), Q[:, sl], ALU.mult)
        P.tt(kds[:], CB[:, :, e_], cj, ALU.subtract)
        P.act(kds[:], kds[:], AF.Exp)
        P.tt(GK[:], Kt[:, g8, :], bc3(gam[:, g8], 64), ALU.mult, eng="pool")
        P.tt(KD[:], Kt[:, g8, :], bc3(kds[:], 64), ALU.mult, eng="pool")
        P.tt(CB[:], CB[:], bc3(cj, 64), ALU.subtract)
        P.tt(CB[:], CB[:], bcm(negm[:, d, :], 8), ALU.add)
        P.act(f2(CB), f2(CB), AF.Exp)
        yield
        for c in range(8):
            cs = slice((grp * 8 + c) * 64, (grp * 8 + c + 1) * 64)
            mm2(P, B[0], c, K_[:, cs], Q[:, cs])
            mm2(P, B[1], c, K_[:, cs], K_[:, cs])
        P.tt(EI[:], CB[:], idb, ALU.add)
        P.tt(PTb[:], EI[:], v3(B[0]), ALU.mult)
        P.tt(CB[:], CB[:], v3(B[1]), ALU.mult)
        P.tt(Nb[:], CB[:], bc3(nb, 64), ALU.mult)
        yield
        for _ in neumann2(k, Nb, Rm, (G["nA"], G["nB"], G["nC"], G["nD"]), (B[0], B[1], B[2]), id2):
            yield
        for c in range(8):
            mm2(P, B[0], c, Rm[:, c, :], Vt[:, grp * 8 + c, :])
            mm2(P, B[1], c, GK[:, c, :], Rm[:, c, :])
            mm2(P, B[2], c, Rm[:, c, :], GK[:, c, :])
        P.tt(U0[:], v3(B[0]), bc3(nb, 64), ALU.mult)
        P.ts(f2(U0), f2(U0), -1.0, ALU.mult, eng="pool")
        P.cp(WT[:], v3(B[1]), eng="act")
        P.tt(BW[:], v3(B[2]), bc3(nb, 64), ALU.mult)
        yield
        for c in range(8):
            mm2(P, B[3], c, BW[:, c, :], KD[:, c, :])
        P.tt(AcT[:], idb, bc3(gend[:], 64), ALU.mult, eng="pool")
        P.tt(AcT[:], AcT[:], v3(B[3]), ALU.add)
        yield
        while shared["scan"] != gi:
            yield
        corder = range(8) if d == 0 else range(7, -1, -1)
        prev = Scar[:]
        for n, c in enumerate(corder):
            P.cp(Sg[:, c, :], prev, eng="pool") if n == 0 else None
            ps = B[2 + n % 2]
            mm2(P, ps, 0, AcT[:, c, :], Sg[:, c, :], start=True, stop=False)
            mm2(P, ps, 0, KD[:, c, :], U0[:, c, :], start=False, stop=True)
            last = (n == 7)
            dst = Scar[:] if last else Sg[:, corder[n + 1], :]
            P.cp(dst, ps[:, 0:64], eng="act")
            yield
        shared["scan"] = gi + 1
        P.cp(Sgb[:], Sg[:], eng="pool")
        for c in range(8):
            mm2(P, B[0], c, WT[:, c, :], Sgb[:, c, :])
        P.tt(CB[:], v3(B[0]), bc3(nb, 64), ALU.mult)
        P.tt(Ug[:], CB[:], U0[:], ALU.add)
        yield
        for c in range(8):
            mm2(P, B[1], c, Sgb[:, c, :], QG[:, c, :], start=True, stop=False)
            mm2(P, B[1], c, Ug[:, c, :], PTb[:, c, :], start=False, stop=True)
        P.tt(O[:, sl], O[:, sl], B[1][:, :], ALU.add)
        yield
    return [(lambda slot, gi=gi, grp=grp: group(gi, grp, GG[slot])) for gi, grp in enumerate(gorder)]


RW_EPS = 64e-5
DEC = float(np.exp(-0.5))
GS = 8
NG = NCK // GS


def host_rwkv(inp, b, m):
    mu = inp["rwkv_mu"]
    o = np.zeros((DEPTH, 128, 8, 2), np.float32)
    for part in range(3):
        for p in range(2):
            o[:, :, part * 2 + p, :] = mu[:, :, part * 256 + p * 128: part * 256 + (p + 1) * 128].transpose(0, 2, 1)
    o[:, 0:64, 6, :] = mu[:, :, 768:832].transpose(0, 2, 1)
    o[:, 0:64, 7, :] = mu[:, :, 832:896].transpose(0, 2, 1)
    m["rmu"] = o

    def pp(a):
        if a.ndim == 2:
            return np.ascontiguousarray(a.reshape(DEPTH, 2, 128).transpose(0, 2, 1))
        return np.ascontiguousarray(a.reshape(DEPTH, 2, 2, 128).transpose(0, 3, 1, 2))
    pv = np.zeros((DEPTH, 128, 7, 2), np.float32)
    pv[:, :, 0:2, :] = pp(inp["rwkv_w0"])
    pv[:, :, 2:4, :] = pp(inp["rwkv_a0"])
    pv[:, :, 4, :] = pp(inp["rwkv_k_k"])
    pv[:, :, 5, :] = pp(inp["rwkv_k_a"])
    pv[:, :, 6, :] = pp(inp["rwkv_r_k"].reshape(DEPTH, 256))
    m["rpv"] = pv
    ln = np.zeros((DEPTH, 128, 2, 2), np.float32)
    ln[:, :, 0, :] = pp(inp["rwkv_ln_g"])
    ln[:, :, 1, :] = pp(inp["rwkv_ln_b"])
    m["rln"] = ln
    m["rw2"] = np.ascontiguousarray(inp["rwkv_w2"].transpose(0, 2, 1, 3))
    m["ra2"] = np.ascontiguousarray(inp["rwkv_a2"].transpose(0, 2, 1, 3))
    s_ = np.arange(64)[:, None]
    t_ = np.arange(64)[None, :]
    msk = np.zeros((128, 2, 4, 64), np.float32)
    msk[:, 0, 0, :] = np.tile((t_ > s_), (2, 1))
    msk[:, 0, 1, :] = np.tile((t_ >= s_), (2, 1))
    msk[:, 1, 0, :] = np.tile((t_ < s_), (2, 1))
    msk[:, 1, 1, :] = np.tile((t_ <= s_), (2, 1))
    msk[:, :, 2:4, :] = -msk[:, :, 0:2, :]
    m["rmsk"] = msk


def rwkv_decl(k):
    nc = k.nc

    def din(name, shape, dt=F32):
        return nc.dram_tensor(name, list(shape), dt, kind="ExternalInput").ap()
    k.rmu_d = din("rmu", [DEPTH, 128, 8, 2])
    k.rpv_d = din("rpv", [DEPTH, 128, 7, 2])
    k.rln_d = din("rln", [DEPTH, 128, 2, 2])
    k.rw2_d = din("rw2", [DEPTH, 64, 2, 256])
    k.ra2_d = din("ra2", [DEPTH, 64, 2, 256])
    k.rmsk_d = din("rmsk", [128, 2, 4, 64])


def rwkv_phase(k, l):
    P = k.P
    with scope(k):
        mu = P.sb("r_mu", [128, 8, 3], F32)
        pv = P.sb("r_pv", [128, 7, 2], F32)
        omka = P.sb("r_omka", [128, 2], F32)
        hrk = P.sb("r_hrk", [128, 2], F32)
        ln = P.sb("r_ln", [128, 2, 2], F32)
        w2 = P.sb("r_w2", [64, 2, 256], BF16)
        a2 = P.sb("r_a2", [64, 2, 256], BF16)
        msk = P.sb("r_msk", [128, 2, 4, 64], F32)
        id2 = P.sb("r_id2", [128, 64], F32)
        bones = P.sb("r_bones", [128, 128], F32)
        m0 = P.sb("r_m0", [128, GS * 64], F32)
        twd = P.sb("r_twd", [64, L], BF16, blk=512)
        adx = P.sb("r_adx", [64, L], BF16, blk=512)
        sh32 = P.sb("r_sh32", [128, L], F32, blk=512)
        xp = P.sb("r_xp", [128, L + 2], F32)
        P.dma(mu[:, :, 0:2], k.rmu_d[l])
        P.dma(pv[:], k.rpv_d[l])
        P.dma(ln[:], k.rln_d[l])
        with scope(k):
            w2f = P.sb("r_w2f", [64, 2, 256], F32)
            a2f = P.sb("r_a2f", [64, 2, 256], F32)
            P.dma(w2f[:], k.rw2_d[l])
            P.dma(a2f[:], k.ra2_d[l])
            P.cp(w2[:], w2f[:], eng="act")
            P.cp(a2[:], a2f[:], eng="act")
        P.dma(msk[:], k.rmsk_d[:])
        P.dma(id2[:], k.gid2_d[:])
        P.memset(bones[:], 0.0)
        P.memset(bones[0:64, 0:64], 1.0)
        P.memset(bones[64:128, 64:128], 1.0)
        P.memset(m0[:], 1.0)
        P.memset(m0[:, 0:GS * 64:64], 0.0)
        P.memset(xp[:, 0:1], 0.0)
        P.memset(xp[:, L + 1:L + 2], 0.0)
        P.tt(mu[:, :, 2], mu[:, :, 0], mu[:, :, 1], ALU.add)
        P.ts(mu[:, :, 2], mu[:, :, 2], -1.0, ALU.mult, 1.0, ALU.add)
        P.ts(omka[:], pv[:, 5, :], -1.0, ALU.mult, 1.0, ALU.add)
        P.ts(hrk[:], pv[:, 6, :], 0.5, ALU.mult)

        def shifted(name, ci, dst, np_=128, fn=None):
            pa = proj(k, l, name, alt=(ci % 2 == 1))
            for tb in range(4):
                P.cp(xp[0:np_, 1 + tb * 512:1 + (tb + 1) * 512], pa[tb][0:np_, :], eng=("act" if tb % 2 else "dve"))
            t_ = sh32[0:np_, :]
            P.ts(t_, xp[0:np_, 1:L + 1], mu[0:np_, ci, 2:3], ALU.mult)
            P.stt(t_, xp[0:np_, 0:L], mu[0:np_, ci, 0:1], t_, ALU.mult, ALU.add)
            if fn is None:
                P.stt(dst[:], xp[0:np_, 2:L + 2], mu[0:np_, ci, 1:2], t_, ALU.mult, ALU.add)
            else:
                P.stt(t_, xp[0:np_, 2:L + 2], mu[0:np_, ci, 1:2], t_, ALU.mult, ALU.add)
                P.act(dst[:], t_, fn)

        shifted("rwd", 6, twd, 64, AF.Tanh)
        shifted("rad", 7, adx, 64)
        R_ = P.sb("r_R", [128, L], BF16, blk=512)
        KX = P.sb("r_KX", [128, L], BF16, blk=512)
        V_ = P.sb("r_V", [128, L], BF16, blk=512)
        KK = P.sb("r_KK", [128, L], BF16, blk=512)
        Vt = P.sb("r_Vt", [128, NCK, 64], BF16, blk=512)
        KS = P.sb("r_KS", [128, L], F32, blk=512)
        Y = xp[:, 1:L + 1]
        sq = P.sb("r_sq", [128, 512], F32)
        rn = P.sb("r_rn", [128, 512], F32)
        CH = [rwkv_tiles(k, e) for e in range(2)]
        for p in range(2):
            shifted("rr%d" % p, 0 + p, R_)
            shifted("rk%d" % p, 2 + p, KX)
            shifted("rv%d" % p, 4 + p, V_)
            P.ts(sh32[:], KX[:], pv[:, 4, p:p + 1], ALU.mult)
            for tb in range(4):
                sl = slice(tb * 512, (tb + 1) * 512)
                P.act(sq[:], sh32[:, sl], AF.Square)
                P.mm(k.PB[2][:, :], bones[:], sq[:])
                P.rpow(rn[:], k.PB[2][:, :], -0.5, scale=1.0, bias=1e-6)
                P.tt(KK[:, sl], sh32[:, sl], rn[:], ALU.mult)
            for grp in range(4):
                ps = k.PB[grp % 2]
                for c in range(8):
                    ck = grp * 8 + c
                    tr2(P, ps, c, V_[:, ck * 64:(ck + 1) * 64], k.identb)
                P.cp(Vt[:, grp * 8:(grp + 1) * 8, :], v3(ps), eng=("act" if grp % 2 else "dve"))
            P.memset(xp[:, 1:L + 1], 0.0, eng="pool")
            P.memset(KS[:], 0.0, eng="pool")
            T = dict(pv=pv, omka=omka, w2=w2, a2=a2, msk=msk, id2=id2, m0=m0, twd=twd, adx=adx,
                     R=R_, KX=KX, KK=KK, Vt=Vt, KS=KS, Y=Y)
            run_interleaved([rwkv_chain(k, p, e, T, CH[e]) for e in range(2)])
            for tb in range(4):
                sl = slice(tb * 512, (tb + 1) * 512)
                P.mm(k.PB[2][:, :], bones[:], Y[:, sl])
                P.stt(Y[:, sl], k.PB[2][:, :], -1.0 / 64, Y[:, sl], ALU.mult, ALU.add)
                P.act(sq[:], Y[:, sl], AF.Square)
                P.mm(k.PB[2][:, :], bones[:], sq[:])
                P.rpow(rn[:], k.PB[2][:, :], -0.5, scale=1.0 / 64, bias=RW_EPS)
                P.tt(Y[:, sl], Y[:, sl], rn[:], ALU.mult)
                P.ts(Y[:, sl], Y[:, sl], ln[:, 0, p:p + 1], ALU.mult, ln[:, 1, p:p + 1], ALU.add)
                P.tt(sq[:], R_[:, sl], KS[:, sl], ALU.mult)
                P.ts(sq[:], sq[:], hrk[:, p:p + 1], ALU.mult)
                P.mm(k.PB[3][:, :], bones[:], sq[:])
                P.tt(rn[:], k.PB[3][:, :], V_[:, sl], ALU.mult)
                P.tt(mixc(k, 6 + p)[:, sl], Y[:, sl], rn[:], ALU.add)


RW_F32 = ("lw", "a", "km", "b", "cl", "e1", "e2", "dend", "AcT", "Tg")
RW_BF16 = ("kap", "rt", "kt_", "bt_", "ke", "be", "kapT", "keT", "nbeT", "N", "Akv", "Brk", "nBrb", "Rm",
           "nA", "nB", "nC", "nD", "X0", "P0", "WkT", "Wk", "Tgb", "Pg")


def rwkv_tiles(k, e):
    P = k.P
    G = {n: P.sb("r%d_%s" % (e, n), [128, GS, 64], F32) for n in RW_F32}
    for n in RW_BF16:
        G[n] = P.sb("r%d_%s" % (e, n), [128, GS, 64], BF16)
    G["gC"] = P.sb("r%d_gC" % e, [128, GS], F32)
    G["Tcar"] = P.sb("r%d_Tcar" % e, [128, 64], F32)
    return G


def rwkv_chain(k, p, e, T, G):
    P = k.P
    pv, omka, w2, a2, msk, id2, m0, twd, adx, R_, KX, KK, Vt, KS, Y = (T[n] for n in (
        "pv", "omka", "w2", "a2", "msk", "id2", "m0", "twd", "adx", "R", "KX", "KK", "Vt", "KS", "Y"))
    B = k.PA if e == 0 else k.PB
    W = GS * 64
    e_ = 63 if e == 0 else 0
    idb = bcm(id2[:, :], GS)
    gC, Tcar = G["gC"], G["Tcar"]

    def f2(t):
        return t[:].rearrange("p a b -> p (a b)")

    def w3(ps):
        return v3(ps, GS)
    P.memset(Tcar[:], 0.0)
    gorder = range(NG) if e == 0 else range(NG - 1, -1, -1)
    pc = slice(p * 128, (p + 1) * 128)
    for grp in gorder:
        sl = slice(grp * W, (grp + 1) * W)
        c0 = grp * GS
        P.mm(B[0][:, 0:W], w2[:, e, pc], twd[:, sl])
        P.mm(B[1][:, 0:W], a2[:, e, pc], adx[:, sl])
        P.act(f2(G["lw"]), B[0][:, 0:W], AF.Sigmoid, bias=pv[:, 0 + e, p:p + 1])
        P.ts(f2(G["lw"]), f2(G["lw"]), -DEC, ALU.mult, eng="pool")
        P.act(f2(G["a"]), B[1][:, 0:W], AF.Sigmoid, bias=pv[:, 2 + e, p:p + 1])
        P.ts(f2(G["km"]), f2(G["a"]), pv[:, 5, p:p + 1], ALU.mult, omka[:, p:p + 1], ALU.add)
        P.tt(f2(G["km"]), f2(G["km"]), KX[:, sl], ALU.mult)
        P.tt(f2(G["b"]), f2(G["a"]), KK[:, sl], ALU.mult, eng="pool")
        P.tt(KS[:, sl], KS[:, sl], f2(G["km"]), ALU.add, eng="pool")
        P.scan(f2(G["cl"]), m0[:], f2(G["lw"]), 0.0, ALU.mult, ALU.add)
        if e == 1:
            P.tt(G["e1"][:], bc3(G["cl"][:, :, 63], 64), G["cl"][:], ALU.subtract)
            P.tt(G["cl"][:], G["e1"][:], G["lw"][:], ALU.add)
        yield
        P.act(G["e1"][:], G["cl"][:], AF.Exp)
        P.act(G["e2"][:], G["cl"][:], AF.Exp, scale=-1.0)
        P.tt(f2(G["rt"]), f2(G["e1"]), R_[:, sl], ALU.mult)
        P.tt(G["kt_"][:], G["e2"][:], G["km"][:], ALU.mult, eng="pool")
        P.tt(G["bt_"][:], G["e2"][:], G["b"][:], ALU.mult)
        P.tt(G["dend"][:], G["cl"][:], G["lw"][:], ALU.subtract, eng="pool")
        P.act(G["dend"][:], G["dend"][:], AF.Exp)
        P.tt(f2(G["kap"]), f2(G["dend"]), KK[:, sl], ALU.mult)
        P.cp(gC[:], G["e1"][:, :, e_], eng="pool")
        P.tt(G["dend"][:], bc3(G["cl"][:, :, e_], 64), G["cl"][:], ALU.subtract, eng="pool")
        P.act(G["dend"][:], G["dend"][:], AF.Exp)
        P.tt(G["ke"][:], G["dend"][:], G["km"][:], ALU.mult)
        P.tt(G["be"][:], G["dend"][:], G["b"][:], ALU.mult, eng="pool")
        yield
        for src, dst, sc in ((G["kap"], G["kapT"], 1.0), (G["ke"], G["keT"], 1.0), (G["be"], G["nbeT"], -1.0)):
            ps = B[0] if sc == 1.0 and src is G["kap"] else (B[1] if sc == 1.0 else B[2])
            for c in range(GS):
                tr2(P, ps, c, src[:, c, :], k.identb)
            if sc == 1.0:
                P.cp(dst[:], w3(ps), eng="act")
            else:
                P.ts(dst[:], w3(ps), -1.0, ALU.mult)
        yield
        for c in range(GS):
            mm2(P, B[0], c, G["bt_"][:, c, :], G["kap"][:, c, :])
            mm2(P, B[1], c, G["kt_"][:, c, :], G["kap"][:, c, :])
            mm2(P, B[2], c, G["kt_"][:, c, :], G["rt"][:, c, :])
            mm2(P, B[3], c, G["bt_"][:, c, :], G["rt"][:, c, :])
        ms = bcm(msk[:, e, 0, :], GS)
        mi = bcm(msk[:, e, 1, :], GS)
        nms = bcm(msk[:, e, 2, :], GS)
        nmi = bcm(msk[:, e, 3, :], GS)
        P.tt(G["N"][:], w3(B[0]), nms, ALU.mult)
        P.tt(G["Akv"][:], w3(B[1]), ms, ALU.mult)
        P.tt(G["Brk"][:], w3(B[2]), mi, ALU.mult)
        P.tt(G["nBrb"][:], w3(B[3]), nmi, ALU.mult)
        yield
        for _ in neumann2(k, G["N"], G["Rm"], (G["nA"], G["nB"], G["nC"], G["nD"]), (B[0], B[1], B[2]), id2, GS):
            yield
        for c in range(GS):
            mm2(P, B[0], c, G["Akv"][:, c, :], Vt[:, c0 + c, :])
        P.cp(G["X0"][:], w3(B[0]), eng="act")
        yield
        for c in range(GS):
            mm2(P, B[0], c, G["Rm"][:, c, :], G["X0"][:, c, :])
            mm2(P, B[1], c, G["kapT"][:, c, :], G["Rm"][:, c, :])
            mm2(P, B[2], c, G["Rm"][:, c, :], G["kapT"][:, c, :])
        P.cp(G["P0"][:], w3(B[0]), eng="act")
        P.cp(G["WkT"][:], w3(B[1]), eng="dve")
        P.cp(G["Wk"][:], w3(B[2]), eng="act")
        yield
        for c in range(GS):
            mm2(P, B[3], c, G["Wk"][:, c, :], G["nbeT"][:, c, :])
        P.tt(G["AcT"][:], idb, bc3(gC[:], 64), ALU.mult, eng="pool")
        P.tt(G["AcT"][:], G["AcT"][:], w3(B[3]), ALU.add)
        yield
        corder = list(range(GS)) if e == 0 else list(range(GS - 1, -1, -1))
        Tg = G["Tg"]
        for n, c in enumerate(corder):
            if n == 0:
                P.cp(Tg[:, c, :], Tcar[:], eng="pool")
            ps = B[2 + n % 2]
            mm2(P, ps, 0, G["AcT"][:, c, :], Tg[:, c, :], start=True, stop=False)
            mm2(P, ps, 0, G["keT"][:, c, :], Vt[:, c0 + c, :], start=False, stop=False)
            mm2(P, ps, 0, G["nbeT"][:, c, :], G["P0"][:, c, :], start=False, stop=True)
            dst = Tcar[:] if n == GS - 1 else Tg[:, corder[n + 1], :]
            P.cp(dst, ps[:, 0:64], eng="act")
            yield
        P.cp(G["Tgb"][:], Tg[:], eng="pool")
        for c in range(GS):
            mm2(P, B[0], c, G["WkT"][:, c, :], G["Tgb"][:, c, :])
        P.tt(G["Pg"][:], w3(B[0]), G["P0"][:], ALU.add)
        yield
        for c in range(GS):
            mm2(P, B[1], c, Vt[:, c0 + c, :], G["Brk"][:, c, :], start=True, stop=False)
            mm2(P, B[1], c, G["Tgb"][:, c, :], G["rt"][:, c, :], start=False, stop=False)
            mm2(P, B[1], c, G["Pg"][:, c, :], G["nBrb"][:, c, :], start=False, stop=True)
        P.tt(Y[:, sl], Y[:, sl], B[1][:, 0:W], ALU.add)
        yield


_CACHE = {}


def kernel(**inputs):
    inp = {k_: np.asarray(v) for k_, v in inputs.items()}
    if "k" not in _CACHE:
        _CACHE["k"] = build()
    k = _CACHE["k"]
    B = inp["x"].shape[0]
    base = host_inputs(inp, 0)
    in_maps = []
    for b in range(B):
        m = dict(base)
        m["x"] = np.ascontiguousarray(inp["x"][b], dtype=np.float32)
        m["pos"] = np.ascontiguousarray(inp["positions"][b].reshape(1, L).astype(np.int32))
        in_maps.append(m)
    res = run_bass_kernel_spmd(k.nc, in_maps, core_ids=list(range(B)))
    return np.stack([np.asarray(r["out"], dtype=np.float32) for r in res.results], axis=0)
```

```python
import numpy as np
import concourse.bass as bass
import concourse.mybir as mybir
from concourse.bass_utils import run_bass_kernel_spmd
from contextlib import ExitStack

F32 = mybir.dt.float32
BF16 = mybir.dt.bfloat16
I32 = mybir.dt.int32
AF = mybir.ActivationFunctionType
ALU = mybir.AluOpType
AX = mybir.AxisListType
DTSIZE = {F32: 4, BF16: 2, I32: 4}


class _Op:
    __slots__ = ("eng", "emit", "deps", "idx", "needed", "sigval", "dsem", "dval", "isdma")

    def __init__(self, eng, emit, isdma=False):
        self.eng = eng
        self.emit = emit
        self.deps = []
        self.idx = -1
        self.needed = False
        self.sigval = 0
        self.dsem = None
        self.dval = 0
        self.isdma = isdma


class _Blk:
    __slots__ = ("w", "r")

    def __init__(self):
        self.w = None
        self.r = {}


class Prog:
    ENGS = ("pe", "dve", "act", "pool", "sp")
    NDMA = 48
    NHW = 32

    def __init__(self, nc, stack):
        self.nc = nc
        self.stack = stack
        self.ops = {e: [] for e in self.ENGS}
        self.track = {}
        self.seen = {e: {} for e in self.ENGS}
        self.seen_dma = {e: set() for e in self.ENGS}
        self.dma_last = [None] * self.NDMA
        self.dma_uses = [0] * self.NDMA
        self.dma_rr = 0
        self.dma_rr_sw = 0
        self.ndma_ops = 0
        self.untracked = set()
        self.out_dmas = []
        self.dma_pending = []
        self.last_compute = {}

    def sb(self, name, shape, dtype=F32, blk=None):
        self.uid = getattr(self, "uid", 0) + 1
        name = "s%d_%s" % (self.uid, name)
        t = self.stack.enter_context(self.nc.sbuf_tensor(name, list(shape), dtype))
        self._register(name, shape, dtype, blk)
        return t

    def ps(self, name, shape=(128, 512), dtype=F32, blk=None):
        self.uid = getattr(self, "uid", 0) + 1
        name = "p%d_%s" % (self.uid, name)
        t = self.stack.enter_context(self.nc.psum_tensor(name, list(shape), dtype))
        self._register(name, shape, dtype, blk)
        return t

    def _register(self, name, shape, dtype, blk):
        row = int(np.prod(shape[1:])) * DTSIZE[dtype]
        bb = row if blk is None else blk * DTSIZE[dtype]
        nb = (row + bb - 1) // bb
        self.track[name] = (bb, row, [_Blk() for _ in range(nb)])

    def dram_track(self, name, total_bytes, blk_bytes):
        nb = (total_bytes + blk_bytes - 1) // blk_bytes
        self.track[name] = (blk_bytes, -1, [_Blk() for _ in range(nb)])

    def _blocks(self, ap):
        name = ap.tensor.name
        if name not in self.track:
            return ()
        bb, row, blks = self.track[name]
        if len(blks) == 1:
            return blks
        ds = DTSIZE[ap.dtype]
        pat = ap.ap
        if row < 0:
            lo = hi = ap.offset
            for step, cnt in pat:
                ext = step * (cnt - 1)
                if ext < 0:
                    lo += ext
                else:
                    hi += ext
            return blks[(lo * ds) // bb:(hi * ds) // bb + 1]
        rowel = row // ds
        foff = ap.offset % rowel
        lo = hi = foff
        for step, cnt in pat[1:]:
            ext = step * (cnt - 1)
            if ext < 0:
                lo += ext
            else:
                hi += ext
        b0 = (lo * ds) // bb
        b1 = (hi * ds) // bb
        return blks[b0:b1 + 1]

    def _dep(self, x, y):
        if y is None or y is x:
            return
        e = x.eng
        if y.isdma:
            if id(y) in self.seen_dma[e]:
                return
            self.seen_dma[e].add(id(y))
            x.deps.append(y)
            return
        if y.eng == "pe" and e == "pe":
            return
        if y.idx <= self.seen[e].get(y.eng, -1):
            return
        self.seen[e][y.eng] = y.idx
        y.needed = True
        x.deps.append(y)

    def add(self, eng, emit, reads=(), writes=(), isdma=False):
        x = _Op(eng, emit, isdma)
        x.idx = len(self.ops[eng])
        rb = []
        for ap in reads:
            if ap is None or isinstance(ap, (int, float)):
                continue
            rb.extend(self._blocks(ap))
        wb = []
        for ap in writes:
            wb.extend(self._blocks(ap))
        for ap in reads:
            if ap is None or isinstance(ap, (int, float)) or not ap.tensor.name.startswith("p"):
                continue
            for b in self._blocks(ap):
                for key, y in b.r.items():
                    if key != eng:
                        self._dep(x, y)
        for b in rb:
            self._dep(x, b.w)
        for b in wb:
            self._dep(x, b.w)
            for y in b.r.values():
                self._dep(x, y)
        if isdma:
            if eng == "pool":
                s = self.NHW + self.dma_rr_sw
                self.dma_rr_sw = (self.dma_rr_sw + 1) % (self.NDMA - self.NHW)
            else:
                s = self.dma_rr
                self.dma_rr = (self.dma_rr + 1) % self.NHW
            self._dep(x, self.dma_last[s])
            self.dma_last[s] = x
            self.dma_uses[s] += 1
            x.dsem = s
            x.dval = 16 * self.dma_uses[s]
            self.ndma_ops += 1
        key = id(x) if isdma else eng
        for b in rb:
            b.r[key] = x
        for b in wb:
            b.w = x
            b.r = {}
        self.ops[eng].append(x)
        if isdma:
            self.dma_pending.append(x)
        else:
            self.last_compute[eng] = x
        return x

    def barrier(self):
        lasts = dict(self.last_compute)
        pend = list(self.dma_pending)
        self.dma_pending = []
        for e in self.ENGS:
            b = _Op(e, None)
            b.idx = len(self.ops[e])
            for e2, y in lasts.items():
                if e2 == e and e == "pe":
                    continue
                self._dep(b, y)
            for y in pend:
                self._dep(b, y)
            self.ops[e].append(b)

    def mm(self, out, lhsT, rhs, start=True, stop=True):
        return self.add("pe", lambda e: e.matmul(out, lhsT, rhs, start=start, stop=stop),
                        reads=(lhsT, rhs), writes=(out,))

    def tr(self, out, in_, ident):
        return self.add("pe", lambda e: e.transpose(out, in_, ident), reads=(in_, ident), writes=(out,))

    def tt(self, out, in0, in1, op, eng="dve"):
        return self.add(eng, lambda e: e.tensor_tensor(out, in0, in1, op), reads=(in0, in1), writes=(out,))

    def ts(self, out, in0, s1, op0, s2=None, op1=None, eng="dve", accum_out=None):
        kw = {}
        if eng == "pool" and op1 is None:
            if op0 == ALU.mult:
                s2, op1 = 0.0, ALU.add
            elif op0 == ALU.add:
                s2, op1 = 1.0, ALU.mult
        if op1 is not None:
            kw["op1"] = op1
        if accum_out is not None:
            kw["accum_out"] = accum_out
        w = (out,) if accum_out is None else (out, accum_out)
        return self.add(eng, lambda e: e.tensor_scalar(out, in0, s1, s2, op0, **kw),
                        reads=(in0, s1, s2), writes=w)

    def stt(self, out, in0, scalar, in1, op0, op1, accum_out=None):
        kw = {}
        if accum_out is not None:
            kw["accum_out"] = accum_out
        w = (out,) if accum_out is None else (out, accum_out)
        return self.add("dve", lambda e: e.scalar_tensor_tensor(out, in0, scalar, in1, op0, op1, **kw),
                        reads=(in0, scalar, in1), writes=w)

    def cp(self, out, in_, eng="dve"):
        if eng == "act":
            return self.add("act", lambda e: e.copy(out, in_), reads=(in_,), writes=(out,))
        return self.add(eng, lambda e: e.tensor_copy(out, in_), reads=(in_,), writes=(out,))

    def act(self, out, in_, func, bias=0.0, scale=1.0, accum_out=None):
        kw = {}
        if accum_out is not None:
            kw["accum_out"] = accum_out
        w = (out,) if accum_out is None else (out, accum_out)
        return self.add("act", lambda e: e.activation(out, in_, func, bias=bias, scale=scale, **kw),
                        reads=(in_, bias, scale), writes=w)

    def red(self, out, in_, op, axis=AX.X, eng="dve"):
        return self.add(eng, lambda e: e.tensor_reduce(out, in_, axis, op), reads=(in_,), writes=(out,))

    def recip(self, out, in_):
        return self.add("dve", lambda e: e.reciprocal(out, in_), reads=(in_,), writes=(out,))

    def rpow(self, out, in_, power, scale=1.0, bias=0.0):
        self.act(out, in_, AF.Ln, bias=bias, scale=scale)
        return self.act(out, out, AF.Exp, scale=power)

    def memset(self, ap, val, eng="dve"):
        return self.add(eng, lambda e: e.memset(ap, val), writes=(ap,))

    def scan(self, out, d0, d1, init, op0, op1):
        return self.add("dve", lambda e: e.tensor_tensor_scan(out, d0, d1, init, op0, op1),
                        reads=(d0, d1, init), writes=(out,))

    def dma(self, out, in_, eng="sp", is_output=False):
        x = self.add(eng, lambda e: e.dma_start(out=out, in_=in_), reads=(in_,), writes=(out,), isdma=True)
        if is_output:
            self.out_dmas.append(x)
        return x

    def finish(self):
        nc = self.nc
        fin = _Op("sp", None)
        fin.idx = len(self.ops["sp"])
        for y in self.out_dmas:
            self._dep(fin, y)
        self.ops["sp"].append(fin)
        sems = {}
        for e in ("pe", "dve", "act", "pool"):
            sems[e] = self.stack.enter_context(nc.semaphore("s_" + e))
        dsems = [self.stack.enter_context(nc.semaphore("d%d" % i)) for i in range(self.NDMA)]
        for e in ("pe", "dve", "act", "pool"):
            c = 0
            for x in self.ops[e]:
                if x.isdma:
                    continue
                if x.needed:
                    c += 1
                    x.sigval = c
            self.stats_sig = getattr(self, "stats_sig", {})
            self.stats_sig[e] = c
        ops = self.ops

        def replay(e, engobj):
            for x in ops[e]:
                for y in x.deps:
                    if y.isdma:
                        engobj.wait_ge(dsems[y.dsem], y.dval)
                    else:
                        engobj.wait_ge(sems[y.eng], y.sigval)
                if x.emit is None:
                    continue
                ins = x.emit(engobj)
                if x.isdma:
                    ins.then_inc(dsems[x.dsem], 16)
                elif x.needed:
                    ins.then_inc(sems[e], 1)

        with nc.Block() as block:
            @block.tensor
            def _(eng):
                replay("pe", eng)

            @block.vector
            def _(eng):
                replay("dve", eng)

            @block.scalar
            def _(eng):
                replay("act", eng)

            @block.gpsimd
            def _(eng):
                replay("pool", eng)

            @block.sync
            def _(eng):
                replay("sp", eng)


L = 2048
D = 1024
NT = L // 128
DEPTH = 2
N_IN = 3448
EPS = 1e-6

OFF = dict(gate=0, gdn_q=1024, gdn_k=1408, gdn_v=1792, gdn_a=2176, gdn_b=2188, mla_cq=2200, mla_ckv=2392,
           mla_kr=2520, rw_r=2552, rw_k=2808, rw_v=3064, rw_wd=3320, rw_ad=3384)


def chunk_table():
    ch = []
    for h in range(3):
        ch.append(("gq%d" % h, [(0, OFF["gdn_q"] + h * 128, 128)]))
        ch.append(("gk%d" % h, [(0, OFF["gdn_k"] + h * 128, 128)]))
        ch.append(("gv%d" % h, [(0, OFF["gdn_v"] + h * 128, 128)]))
    ch.append(("gab", [(0, OFF["gdn_a"], 6), (32, OFF["gdn_a"] + 6, 6), (64, OFF["gdn_b"], 6), (96, OFF["gdn_b"] + 6, 6)]))
    ch.append(("cq0", [(0, OFF["mla_cq"], 128)]))
    ch.append(("cq1", [(0, OFF["mla_cq"] + 128, 64)]))
    ch.append(("ckv", [(0, OFF["mla_ckv"], 128)]))
    ch.append(("kr", [(0, OFF["mla_kr"], 32)]))
    for i in range(2):
        ch.append(("rr%d" % i, [(0, OFF["rw_r"] + i * 128, 128)]))
        ch.append(("rk%d" % i, [(0, OFF["rw_k"] + i * 128, 128)]))
        ch.append(("rv%d" % i, [(0, OFF["rw_v"] + i * 128, 128)]))
    ch.append(("rwd", [(0, OFF["rw_wd"], 64)]))
    ch.append(("rad", [(0, OFF["rw_ad"], 64)]))
    for i in range(8):
        ch.append(("g%d" % i, [(0, OFF["gate"] + i * 128, 128)]))
    return ch


CHUNKS = chunk_table()
CH_IDX = {n: i for i, (n, _) in enumerate(CHUNKS)}
NCH = len(CHUNKS)


def host_win(w_in):
    out = np.zeros((DEPTH, NCH, 128, 8, 128), np.float32)
    for ci, (_, parts) in enumerate(CHUNKS):
        for dst, src, w in parts:
            blk = w_in[:, :, src:src + w].reshape(DEPTH, 8, 128, w)
            out[:, ci, :, :, dst:dst + w] = blk.transpose(0, 2, 1, 3)
    return out


class K:
    pass


def build(depth=DEPTH, mixers=("gdn", "mla", "rwkv"), dbg=False):
    nc = bass.Bass("TRN2", target_bir_lowering=False)
    k = K()
    k.nc = nc
    k.dbg = dbg
    k.dbg_outs = []
    k.cut = 99

    def din(name, shape, dt=F32):
        return nc.dram_tensor(name, list(shape), dt, kind="ExternalInput").ap()

    k.x_d = din("x", [L, D])
    k.win_d = din("win", [DEPTH, NCH, 128, 8, 128])
    k.normg_d = din("normg", [DEPTH, 128, 8])
    k.wout_d = din("wout", [DEPTH, 128, 8, 1024])
    k.fing_d = din("fing", [1, D])
    k.ident_d = din("ident", [128, 128])
    k.out_d = nc.dram_tensor("out", [L, D], F32, kind="ExternalOutput").ap()
    mla_decl(k)
    gdn_decl(k)
    rwkv_decl(k)

    with ExitStack() as st:
        P = Prog(nc, st)
        k.P = P
        k.xscr = nc.dram_tensor("xscr", [L, D], F32, kind="Internal").ap()
        P.dram_track("xscr", L * D * 4, 128 * D * 4)
        k.hT = P.sb("hT", [128, 8, L], BF16, blk=512)
        k.ident = P.sb("ident", [128, 128], F32)
        k.identb = P.sb("identb", [128, 128], BF16)
        k.normg = P.sb("normg", [128, DEPTH, 8], F32)
        k.wst = [P.sb("wst%d" % i, [128, 8, 128], F32) for i in range(2)]
        k.wbf = [P.sb("wbf%d" % i, [128, 8, 128], BF16) for i in range(2)]
        k.wrr = 0
        k.PAW = [P.ps("paw%d" % i, [128, 1024], F32, blk=512) for i in range(2)]
        k.PA = [k.PAW[i // 2][:, (i % 2) * 512:(i % 2 + 1) * 512] for i in range(4)]
        k.PB = [P.ps("pb%d" % i, [128, 512], F32) for i in range(4)]

        P.dma(k.ident[:], k.ident_d[:])
        P.cp(k.identb[:], k.ident[:])
        for l in range(DEPTH):
            P.dma(k.normg[:, l, :], k.normg_d[l])

        for l in range(depth):
            phase_a(k, l)
            with scope(k):
                k.mix_r = P.sb("mix_r", [128, 2, L], BF16, blk=512)
                if "rwkv" in mixers:
                    rwkv_phase(k, l)
                else:
                    P.memset(k.mix_r[:].rearrange("p a b -> p (a b)"), 1.0)
                with scope(k):
                    k.mix_m = P.sb("mix_m", [128, 3, L], BF16, blk=512)
                    if "mla" in mixers:
                        mla_phase(k, l)
                    else:
                        P.memset(k.mix_m[:].rearrange("p a b -> p (a b)"), 1.0)
                    with scope(k):
                        k.mix_g = P.sb("mix_g", [128, 3, L], BF16, blk=512)
                        if "gdn" in mixers:
                            gdn_phase(k, l)
                        else:
                            P.memset(k.mix_g[:].rearrange("p a b -> p (a b)"), 1.0)
                        if k.dbg:
                            for nm, t_, n_ in (("g", k.mix_g, 3), ("m", k.mix_m, 3), ("r", k.mix_r, 2)):
                                dump(k, "mix_%s%d" % (nm, l), t_[:].rearrange("p a b -> p (a b)"), [128, n_ * L])
                        phase_z(k, l, last=(l == depth - 1))
        P.finish()
        print("ops:", {e: len(v) for e, v in P.ops.items()}, "sig:", P.stats_sig, "dma:", P.ndma_ops)
    return k


def mixc(k, c):
    if c < 3:
        return k.mix_g[:, c, :]
    if c < 6:
        return k.mix_m[:, c - 3, :]
    return k.mix_r[:, c - 6, :]


def dump(k, name, ap, shape=None):
    if not k.dbg:
        return
    P = k.P
    shape = list(ap.shape) if shape is None else shape
    d = k.nc.dram_tensor("dbg_" + name, shape, ap.dtype, kind="ExternalOutput").ap()
    P.dma(d[:] if len(shape) == 2 else d, ap, is_output=True)
    k.dbg_outs.append("dbg_" + name)


def scope(k):
    class _S:
        def __enter__(s):
            s.old = k.P.stack
            s.st = ExitStack()
            s.st.__enter__()
            k.P.stack = s.st
            return s

        def __exit__(s, *a):
            k.P.barrier()
            k.P.stack = s.old
            s.st.__exit__(*a)
            return False
    return _S()


def phase_a(k, l):
    P = k.P
    with scope(k):
        ssq = P.sb("a_ssq", [128, NT])
        rs = P.sb("a_rs", [128, NT])
        rstd = P.sb("a_rstd", [128, NT])
        junk = [P.sb("a_junk%d" % i, [128, D], BF16) for i in range(2)]
        xs = [P.sb("a_xs%d" % i, [128, D], BF16) for i in range(2)]
        xin = [P.sb("a_xin%d" % i, [128, D], F32) for i in range(3)]
        src = k.x_d if l == 0 else k.xscr

        def stage1(tt):
            b = tt % 2
            xt_ = xin[tt % 3]
            P.dma(xt_[:], src[tt * 128:(tt + 1) * 128, :])
            P.act(junk[b][:], xt_[:], AF.Square, accum_out=ssq[:, tt:tt + 1])
            P.act(rs[:, tt:tt + 1], ssq[:, tt:tt + 1], AF.Sqrt, bias=EPS, scale=1.0 / D)
            P.recip(rstd[:, tt:tt + 1], rs[:, tt:tt + 1])
            P.ts(xs[b][:], xt_[:], rstd[:, tt:tt + 1], ALU.mult)

        def stage2(tt):
            b = tt % 2
            pt = k.PB[b][:].bitcast(BF16)
            for dc in range(8):
                P.tr(pt[:, dc * 128:(dc + 1) * 128], xs[b][:, dc * 128:(dc + 1) * 128], k.identb[:])
            P.cp(k.hT[:, :, tt * 128:(tt + 1) * 128], pt[:].rearrange("p (a b) -> p a b", a=8),
                 eng=("act" if tt % 2 == 0 else "dve"))
        stage1(0)
        for tt in range(NT):
            if tt + 1 < NT:
                stage1(tt + 1)
            stage2(tt)


def proj(k, l, name, alt=False):
    P = k.P
    BK = k.PB if alt else k.PA
    ci = CH_IDX[name]
    b = k.wrr
    k.wrr ^= 1
    P.dma(k.wst[b][:], k.win_d[l, ci], eng="sp")
    gb = k.normg[:, l, :].unsqueeze(2).broadcast_to([128, 8, 128])
    P.tt(k.wbf[b][:], k.wst[b][:], gb, ALU.mult, eng="pool")
    for tb in range(4):
        for dc in range(8):
            P.mm(BK[tb][:, :], k.wbf[b][:, dc, :], k.hT[:, dc, tb * 512:(tb + 1) * 512], start=(dc == 0), stop=(dc == 7))
    return BK


ZQ = "act"


def phase_z(k, l, last):
    P = k.P
    with scope(k):
        wst = P.sb("z_wst", [128, 8, 512], F32)
        wob = P.sb("z_wob", [128, 8, 1024], BF16, blk=512)
        sg = [P.sb("z_sg%d" % i, [128, L], BF16, blk=512) for i in range(2)]
        for nb in range(2):
            P.dma(wst[:], k.wout_d[l, :, :, nb * 512:(nb + 1) * 512])
            P.cp(wob[:, :, nb * 512:(nb + 1) * 512], wst[:], eng="act")
        for gc in range(8):
            pa = proj(k, l, "g%d" % gc, alt=(gc % 2 == 1))
            s = sg[gc % 2]
            for tb in range(4):
                P.act(s[:, tb * 512:(tb + 1) * 512], pa[tb][:, :], AF.Silu)
                mc = mixc(k, gc)[:, tb * 512:(tb + 1) * 512]
                P.tt(mc, mc, s[:, tb * 512:(tb + 1) * 512], ALU.mult)
        xin = [P.sb("z_xin%d" % i, [128, D], F32) for i in range(3)]
        src = k.x_d if l == 0 else k.xscr
        if last:
            ssq = P.sb("f_ssq", [128, NT])
            rs = P.sb("f_rs", [128, NT])
            rstd = P.sb("f_rstd", [128, NT])
            junk = [P.sb("f_junk%d" % i, [128, D], BF16) for i in range(2)]
            gf = P.sb("f_g", [128, D])
            ot = [P.sb("f_o%d" % i, [128, D]) for i in range(2)]
            P.dma(gf[:], k.fing_d[0:1, :].partition_broadcast(128))
        def za(tt):
            xt_ = xin[tt % 3]
            P.dma(xt_[:], src[tt * 128:(tt + 1) * 128, :])
            for nb in range(2):
                ps = k.PB[(tt * 2 + nb) % 4]
                for kc in range(8):
                    P.mm(ps[:, :], mixc(k, kc)[:, tt * 128:(tt + 1) * 128], wob[:, kc, nb * 512:(nb + 1) * 512],
                         start=(kc == 0), stop=(kc == 7))
                xs = xt_[:, nb * 512:(nb + 1) * 512]
                P.tt(xs, xs, ps[:, :], ALU.add)
            if not last:
                P.dma(k.xscr[tt * 128:(tt + 1) * 128, :], xt_[:], eng=ZQ)
            else:
                b = tt % 2
                P.act(junk[b][:], xt_[:], AF.Square, accum_out=ssq[:, tt:tt + 1])
                P.act(rs[:, tt:tt + 1], ssq[:, tt:tt + 1], AF.Sqrt, bias=EPS, scale=1.0 / D)

        def zb(tt):
            if last:
                xt_ = xin[tt % 3]
                b = tt % 2
                P.recip(rstd[:, tt:tt + 1], rs[:, tt:tt + 1])
                P.stt(ot[b][:], xt_[:], rstd[:, tt:tt + 1], gf[:], ALU.mult, ALU.mult)
                P.dma(k.out_d[tt * 128:(tt + 1) * 128, :], ot[b][:], eng=ZQ, is_output=True)
        za(0)
        for tt in range(NT):
            if tt + 1 < NT:
                za(tt + 1)
            zb(tt)


def host_inputs(inp, b):
    m = {}
    m["x"] = np.ascontiguousarray(inp["x"][b])
    m["win"] = host_win(inp["w_in"])
    m["normg"] = np.ascontiguousarray(inp["norm_g"].reshape(DEPTH, 8, 128).transpose(0, 2, 1))
    m["wout"] = np.ascontiguousarray(inp["w_out"].reshape(DEPTH, 8, 128, 1024).transpose(0, 2, 1, 3))
    m["fing"] = np.ascontiguousarray(inp["final_norm_g"].reshape(1, D))
    m["ident"] = np.eye(128, dtype=np.float32)
    host_mla(inp, b, m)
    host_gdn(inp, b, m)
    host_rwkv(inp, b, m)
    return m


TWO_PI = 2.0 * np.pi


def host_mla(inp, b, m):
    half = 16
    inv_freq = (10000.0 ** (-np.arange(half, dtype=np.float32) / half)).astype(np.float32)
    invf = np.zeros((32, 1), np.float32)
    invf[:, 0] = np.tile(inv_freq, 2) / np.float32(TWO_PI)
    m["invf"] = invf
    rm = np.zeros((32, 32), np.float32)
    for i in range(16):
        rm[i, i + 16] = -1.0
        rm[i + 16, i] = 1.0
    m["rmT"] = np.ascontiguousarray(rm.T)
    m["pos"] = np.ascontiguousarray(inp["positions"][b].reshape(1, L).astype(np.int32))
    wuq = inp["mla_w_uq"]
    o = np.zeros((DEPTH, 128, 2, 6, 128), np.float32)
    for h in range(6):
        nope = wuq[:, :, h * 96:h * 96 + 64]
        rope = wuq[:, :, h * 96 + 64:h * 96 + 96]
        o[:, :, 0, h, 64:128] = nope[:, 0:128]
        o[:, 0:64, 1, h, 64:128] = nope[:, 128:192]
        o[:, :, 0, h, 0:32] = rope[:, 0:128]
        o[:, 0:64, 1, h, 0:32] = rope[:, 128:192]
    m["wuq"] = o
    gq = np.zeros((DEPTH, 128, 2), np.float32)
    gq[:, :, 0] = inp["mla_q_norm_g"][:, 0:128]
    gq[:, 0:64, 1] = inp["mla_q_norm_g"][:, 128:192]
    m["gq"] = gq
    wukv = inp["mla_w_ukv"]
    wk = np.zeros((DEPTH, 128, 6, 128), np.float32)
    wv = np.zeros((DEPTH, 128, 6, 64), np.float32)
    for h in range(6):
        wk[:, :, h, 64:128] = wukv[:, :, h * 128:h * 128 + 64]
        wv[:, :, h, :] = wukv[:, :, h * 128 + 64:h * 128 + 128]
    m["wuk"] = wk
    m["wuv"] = wv
    m["gkv"] = np.ascontiguousarray(inp["mla_kv_norm_g"].reshape(DEPTH, 128, 1))


def mla_decl(k):
    nc = k.nc

    def din(name, shape, dt=F32):
        return nc.dram_tensor(name, list(shape), dt, kind="ExternalInput").ap()
    k.invf_d = din("invf", [32, 1])
    k.rmT_d = din("rmT", [32, 32])
    k.pos_d = din("pos", [1, L], I32)
    k.wuq_d = din("wuq", [DEPTH, 128, 2, 6, 128])
    k.gq_d = din("gq", [DEPTH, 128, 2])
    k.wuk_d = din("wuk", [DEPTH, 128, 6, 128])
    k.wuv_d = din("wuv", [DEPTH, 128, 6, 64])
    k.gkv_d = din("gkv", [DEPTH, 128, 1])


def latent_norm(k, l, names, nfeat, outs, ones):
    P = k.P
    sq = [P.sb("ln_sq%d" % i, [128, 512]) for i in range(2)]
    rq = [P.sb("ln_rq%d" % i, [128, 512]) for i in range(2)]
    n = len(names)
    for i, nm in enumerate(names):
        pa = proj(k, l, nm)
        for tb in range(4):
            s = sq[tb % 2]
            sk = ""
            if "a" not in sk:
                P.act(s[:], pa[tb][:, :], AF.Square)
            if "c" not in sk:
                P.cp(outs[i][:, tb * 512:(tb + 1) * 512], pa[tb][:, :], eng="dve")
            if "m" not in sk:
                P.mm(k.PB[tb][:, :], ones[:], s[:], start=(i == 0), stop=(i == n - 1))
    c2 = 9
    if c2 < 1:
        return
    for tb in range(4):
        r = rq[tb % 2]
        P.rpow(r[:], k.PB[tb][:, :], -0.5, scale=1.0 / nfeat, bias=EPS)
        for i in range(n):
            o = outs[i][:, tb * 512:(tb + 1) * 512]
            P.tt(o, o, r[:], ALU.mult)


def mla_phase(k, l):
    P = k.P
    SC = 96.0 ** -0.5
    with scope(k):
        cqn0 = P.sb("m_cqn0", [128, L], BF16, blk=512)
        cqn1 = P.sb("m_cqn1", [128, L], BF16, blk=512)
        ckvn = P.sb("m_ckvn", [128, L], BF16, blk=512)
        krope = P.sb("m_krope", [32, L], BF16, blk=512)
        cos2 = P.sb("m_cos2", [32, L], BF16, blk=512)
        sin2 = P.sb("m_sin2", [32, L], BF16, blk=512)
        wq = P.sb("m_wq", [128, 2, 6, 128], BF16)
        wk = P.sb("m_wk", [128, 6, 128], BF16)
        wv = P.sb("m_wv", [128, 6, 64], BF16)
        ones = P.sb("m_ones", [128, 128], F32)
        onesk = P.sb("m_onesk", [128, 128], F32)
        rmT = P.sb("m_rmT", [32, 32], F32)
        P.memset(ones[:], 1.0)
        P.memset(onesk[:], 1.0)
        P.memset(onesk[32:64, :], 0.0)
        P.dma(rmT[:], k.rmT_d[:])
        with scope(k):
            st = P.sb("m_st", [128, 2, 6, 128], F32)
            g = P.sb("m_g", [128, 4], F32)
            P.dma(st[:], k.wuq_d[l])
            P.dma(g[:, 0:2], k.gq_d[l])
            P.dma(g[:, 2:3], k.gkv_d[l])
            for kc in range(2):
                P.ts(wq[:, kc].rearrange("p a b -> p (a b)"), st[:, kc].rearrange("p a b -> p (a b)"),
                     g[:, kc:kc + 1], ALU.mult)
            st2 = P.sb("m_st2", [128, 6, 128], F32)
            P.dma(st2[:], k.wuk_d[l])
            P.ts(wk[:].rearrange("p a b -> p (a b)"), st2[:].rearrange("p a b -> p (a b)"), g[:, 2:3], ALU.mult)
            st3 = P.sb("m_st3", [128, 6, 64], F32)
            P.dma(st3[:], k.wuv_d[l])
            P.ts(wv[:].rearrange("p a b -> p (a b)"), st3[:].rearrange("p a b -> p (a b)"), g[:, 2:3], ALU.mult)
        if k.cut < 1:
            return
        with scope(k):
            latent_norm(k, l, ["cq0", "cq1"], 192, [cqn0, cqn1], ones)
            latent_norm(k, l, ["ckv"], 128, [ckvn], ones)
        if k.cut < 2:
            return
        with scope(k):
            invf = P.sb("m_invf", [32, 1], F32)
            P.dma(invf[:], k.invf_d[:])
            pa = proj(k, l, "kr")
            for tb in range(4):
                sl = slice(tb * 512, (tb + 1) * 512)
                b_ = tb % 2
                posi = P.sb("m_posi%d" % tb, [32, 512], I32)
                y = P.sb("m_y%d" % tb, [32, 512], F32)
                yi = P.sb("m_yi%d" % tb, [32, 512], I32)
                fr = P.sb("m_fr%d" % tb, [32, 512], F32)
                kr = P.sb("m_kr%d" % tb, [32, 512], F32)
                t1 = P.sb("m_t1%d" % tb, [32, 512], F32)
                t2 = P.sb("m_t2%d" % tb, [32, 512], F32)
                P.dma(posi[:], k.pos_d[0:1, sl].partition_broadcast(32))
                P.cp(y[:], posi[:])
                P.ts(y[:], y[:], invf[:, 0:1], ALU.mult)
                for off, dst in ((0.0, sin2), (0.25, cos2)):
                    if off != 0.0:
                        P.ts(y[:], y[:], off, ALU.add)
                    P.cp(yi[:], y[:])
                    P.cp(fr[:], yi[:])
                    P.tt(fr[:], y[:], fr[:], ALU.subtract)
                    P.act(dst[:, sl], fr[:], AF.Sin, scale=TWO_PI * (1.0 - 1e-6))
                P.cp(kr[:], pa[tb][0:32, :], eng="act")
                P.mm(k.PB[3][0:32, :], rmT[:], kr[:])
                P.tt(t1[:], kr[:], cos2[:, sl], ALU.mult)
                P.tt(t2[:], k.PB[3][0:32, :], sin2[:, sl], ALU.mult)
                P.tt(krope[:, sl], t1[:], t2[:], ALU.add)
        if k.cut < 3:
            return
        kT = [P.sb("m_kT%d" % i, [128, L], BF16, blk=512) for i in range(2)]
        qT = [P.sb("m_qT%d" % i, [128, L], BF16, blk=512) for i in range(2)]
        Vh = [P.sb("m_V%d" % i, [128, NT, 96], BF16) for i in range(2)]
        pT = [P.sb("m_pT%d" % i, [128, 1024], BF16, blk=512) for i in range(2)]
        sq = [P.sb("m_sq%d" % i, [128, 512], F32) for i in range(2)]
        qr = [P.sb("m_qr%d" % i, [32, 512], F32) for i in range(2)]
        t1 = P.sb("m_t1b", [32, 512], F32)
        t2 = P.sb("m_t2b", [32, 512], F32)
        mrow = P.sb("m_mrow", [64, 512], F32)
        km4 = P.sb("m_km4", [128, 4], F32)
        kmax2 = P.sb("m_kmax2", [128, 1], F32)
        rden = [P.sb("m_rden%d" % i, [64, 512], F32) for i in range(2)]
        for i in range(2):
            P.memset(kT[i][32:64, :], 0.0)
            P.memset(kT[i][32:33, :], 1.0)
            P.memset(qT[i][32:64, :], 0.0)
            P.memset(Vh[i][:, :, 64:96], 1.0)
        kmx = [P.sb("m_kmx%d" % i, [128, 1], F32) for i in range(2)]

        def prep(h):
            kt_, qt_, vh_ = kT[h % 2], qT[h % 2], Vh[h % 2]
            kmax2_ = kmx[h % 2]
            P.cp(kt_[0:32, :], krope[:], eng="pool")
            for tb in range(4):
                sl = slice(tb * 512, (tb + 1) * 512)
                s = sq[tb % 2]
                P.mm(k.PB[2][:, :], wk[:, h, :], ckvn[:, sl])
                yield
                P.cp(kt_[64:128, sl], k.PB[2][64:128, :], eng="dve")
                yield
                P.tt(s[:], kt_[:, sl], kt_[:, sl], ALU.mult, eng="pool")
                yield
                yield
                P.mm(k.PB[3][:, :], onesk[:], s[:])
                yield
                P.red(km4[:, tb:tb + 1], k.PB[3][:, :], ALU.max)
                yield
            P.red(kmax2_[:], km4[:], ALU.max)
            for half in range(2):
                for j in range(8):
                    tt = half * 8 + j
                    P.mm(k.PB[2][:, j * 64:(j + 1) * 64], ckvn[:, tt * 128:(tt + 1) * 128], wv[:, h, :])
                yield
                P.cp(vh_[:, half * 8:(half + 1) * 8, 0:64], k.PB[2][:, :].rearrange("p (a b) -> p a b", a=8), eng="dve")
                yield
            for tb in range(4):
                sl = slice(tb * 512, (tb + 1) * 512)
                s = sq[tb % 2]
                q_ = qr[tb % 2]
                P.mm(k.PB[2][:, :], wq[:, 0, h, :], cqn0[:, sl], start=True, stop=False)
                P.mm(k.PB[2][:, :], wq[:, 1, h, :], cqn1[:, sl], start=False, stop=True)
                yield
                P.cp(qt_[64:128, sl], k.PB[2][64:128, :], eng="dve")
                P.cp(q_[:], k.PB[2][0:32, :], eng="dve")
                yield
                P.act(s[:], k.PB[2][:, :], AF.Square)
                yield
                P.mm(k.PB[3][:, :], ones[:], s[:])
                yield
                P.act(mrow[32:33, :], k.PB[3][32:33, :], AF.Sqrt, scale=kmax2_[32:33, 0:1])
                yield
                P.ts(qt_[32:33, sl], mrow[32:33, :], -1.0, ALU.mult)
                P.mm(k.PB[2][0:32, :], rmT[:], q_[:])
                P.tt(t1[:], q_[:], cos2[:, sl], ALU.mult, eng="pool")
                yield
                P.tt(t2[:], k.PB[2][0:32, :], sin2[:, sl], ALU.mult)
                yield
                P.tt(qt_[0:32, sl], t1[:], t2[:], ALU.add)
                yield

        def attn(h):
            kt_, qt_, vh_ = kT[h % 2], qT[h % 2], Vh[h % 2]
            pti = 0
            for qb in range(4):
                qs = slice(qb * 512, (qb + 1) * 512)
                O = k.PB[qb % 2]

                def s_pair(m_):
                    for j in range(2):
                        kt = 2 * m_ + j
                        P.mm(k.PAW[m_ % 2][:, j * 512:(j + 1) * 512], kt_[:, kt * 128:(kt + 1) * 128], qt_[:, qs])
                s_pair(0)
                s_pair(1)
                for m_ in range(NT // 2):
                    p_ = pT[pti % 2]
                    pti += 1
                    P.act(p_[:], k.PAW[m_ % 2][:, :], AF.Exp, scale=SC)
                    if m_ + 2 < NT // 2:
                        s_pair(m_ + 2)
                    for j in range(2):
                        kt = 2 * m_ + j
                        P.mm(O[0:96, :], vh_[:, kt, :], p_[:, j * 512:(j + 1) * 512], start=(kt == 0), stop=(kt == NT - 1))
                    yield
                rd = rden[qb % 2]
                P.rpow(rd[0:32, :], O[64:96, :], -1.0)
                P.rpow(rd[32:64, :], O[64:96, :], -1.0)
                ob = (h % 2) * 64
                P.tt(mixc(k, 3 + h // 2)[ob:ob + 64, qs], O[0:64, :], rd[:], ALU.mult)
                yield

        for _ in prep(0):
            pass
        mode = "il"
        for h in range(6):
            gens = [attn(h)]
            if h + 1 < 6:
                if mode == "il":
                    gens.append(prep(h + 1))
                elif mode == "seq":
                    run_interleaved(gens)
                    gens = [prep(h + 1)]
                elif mode == "noprep":
                    pass
            if mode == "noprep" and h > 0:
                gens = [attn(0)]
            run_interleaved(gens)


NCK = L // 64
NEG = -30000.0


def host_gdn(inp, b, m):
    cw = inp["gdn_conv"]
    o = np.zeros((DEPTH, 128, 9, 5), np.float32)
    for part in range(3):
        for p in range(3):
            o[:, :, part * 3 + p, :] = cw[:, :, part * 384 + p * 128: part * 384 + (p + 1) * 128].transpose(0, 2, 1)
    m["gconv"] = o
    gb = np.zeros((DEPTH, 128, 2), np.float32)
    for d in range(2):
        gb[:, d * 32:d * 32 + 6, 0] = inp["gdn_dt_bias"][:, d, :]
        gb[:, d * 32:d * 32 + 6, 1] = inp["gdn_a_log"][:, d, :]
    m["ggb"] = gb
    m["gng"] = np.ascontiguousarray(np.tile(inp["gdn_norm_g"], (1, 2)).reshape(DEPTH, 128, 1))
    sel = np.zeros((64, 6, 128), np.float32)
    for d in range(2):
        for p in range(3):
            sel[d * 32 + 2 * p, d * 3 + p, 0:64] = 1.0
            sel[d * 32 + 2 * p + 1, d * 3 + p, 64:128] = 1.0
    m["gsel"] = sel
    j = np.arange(64)[:, None]
    i = np.arange(64)[None, :]
    nm = np.zeros((128, 2, 64), np.float32)
    nm[:, 0, :] = np.tile(np.where(i > j, 0.0, NEG), (2, 1))
    nm[:, 1, :] = np.tile(np.where(i < j, 0.0, NEG), (2, 1))
    m["gnegm"] = nm
    m["gid2"] = np.ascontiguousarray(np.tile(np.eye(64, dtype=np.float32), (2, 1)))


def gdn_decl(k):
    nc = k.nc

    def din(name, shape, dt=F32):
        return nc.dram_tensor(name, list(shape), dt, kind="ExternalInput").ap()
    k.gconv_d = din("gconv", [DEPTH, 128, 9, 5])
    k.ggb_d = din("ggb", [DEPTH, 128, 2])
    k.gng_d = din("gng", [DEPTH, 128, 1])
    k.gsel_d = din("gsel", [64, 6, 128])
    k.gnegm_d = din("gnegm", [128, 2, 64])
    k.gid2_d = din("gid2", [128, 64])


def bc3(ap2, n):
    return ap2.unsqueeze(2).broadcast_to([ap2.shape[0], ap2.shape[1], n])


def bcm(ap2, n):
    return ap2.unsqueeze(1).broadcast_to([ap2.shape[0], n, ap2.shape[1]])


HS = (slice(0, 64), slice(64, 128))


def v3(ps, n=8):
    return ps[:, 0:n * 64].rearrange("p (a b) -> p a b", a=n)


def mm2(P, ps, c, lhsT, rhs, **kw):
    for hs in HS:
        P.mm(ps[hs, c * 64:(c + 1) * 64], lhsT[hs], rhs[hs], **kw)


def tr2(P, ps, c, in_, ident):
    for hs in HS:
        P.mm(ps[hs, c * 64:(c + 1) * 64], in_[hs], ident[hs, hs])


def neumann2(k, Nn, Rm, tmp, bank, id2, n8=8):
    P = k.P
    idb = bcm(id2[:, :], n8)
    tA, tB, tC, tD = tmp
    pa_, pb_, pc_ = bank
    for c in range(n8):
        tr2(P, pa_, c, Nn[:, c, :], k.identb)
    P.cp(tA[:], v3(pa_, n8), eng="act")
    P.tt(Rm[:], Nn[:], idb, ALU.add)
    yield
    cur, curT = Nn, tA
    targets = [(tB, tC), (tD, tA)]
    for lvl in range(1, 7):
        nxt, nxtT = targets[(lvl - 1) % 2]
        for c in range(n8):
            if lvl <= 5:
                mm2(P, pb_, c, cur[:, c, :], curT[:, c, :])
                if lvl < 5:
                    mm2(P, pa_, c, curT[:, c, :], cur[:, c, :])
            if lvl >= 2:
                mm2(P, pc_, c, curT[:, c, :], Rm[:, c, :])
        if lvl <= 5:
            P.cp(nxtT[:], v3(pb_, n8), eng="act")
            if lvl < 5:
                P.cp(nxt[:], v3(pa_, n8), eng="act")
        if lvl >= 2:
            P.tt(Rm[:], Rm[:], v3(pc_, n8), ALU.add)
        yield
        cur, curT = nxt, nxtT


def run_interleaved(gens):
    gens = list(gens)
    while gens:
        for g in list(gens):
            try:
                next(g)
            except StopIteration:
                gens.remove(g)


def run_pipelined(chains, depth=2):
    active = []
    nxt = [0] * len(chains)

    def start(ci):
        if nxt[ci] < len(chains[ci]):
            active.append((ci, chains[ci][nxt[ci]](nxt[ci] % depth)))
            nxt[ci] += 1
    for ci in range(len(chains)):
        for _ in range(depth):
            start(ci)
    while active:
        for item in list(active):
            try:
                next(item[1])
            except StopIteration:
                active.remove(item)
                start(item[0])


def gdn_phase(k, l):
    P = k.P
    with scope(k):
        GC = P.sb("g_GC", [64, L], F32, blk=512)
        GP = [P.sb("g_GP%d" % p, [128, NCK, 4], F32) for p in range(3)]
        NBP = [P.sb("g_NBP%d" % p, [128, NCK, 2], F32) for p in range(3)]
        sel = P.sb("g_sel", [64, 6, 128], F32)
        negm = P.sb("g_negm", [128, 2, 64], F32)
        id2 = P.sb("g_id2", [128, 64], F32)
        cw = P.sb("g_cw", [128, 9, 5], F32)
        ng = P.sb("g_ng", [128, 1], F32)
        bones = P.sb("g_bones", [128, 128], F32)
        P.dma(sel[:], k.gsel_d[:])
        P.dma(negm[:], k.gnegm_d[:])
        P.dma(id2[:], k.gid2_d[:])
        P.dma(cw[:], k.gconv_d[l])
        P.dma(ng[:], k.gng_d[l])
        P.memset(bones[:], 0.0)
        P.memset(bones[0:64, 0:64], 1.0)
        P.memset(bones[64:128, 64:128], 1.0)
        with scope(k):
            GT = P.sb("g_GT", [128, L], F32, blk=512)
            m0 = P.sb("g_m0", [64, L], F32)
            gb = P.sb("g_gb", [128, 2], F32)
            negA = P.sb("g_negA", [128, 1], F32)
            P.dma(gb[:], k.ggb_d[l])
            P.act(negA[:], gb[:, 1:2], AF.Exp)
            P.ts(negA[:], negA[:], -1.0, ALU.mult)
            P.memset(m0[:], 1.0)
            P.memset(m0[:, 0:L:64], 0.0)
            pa = proj(k, l, "gab")
            for tb in range(4):
                sl = slice(tb * 512, (tb + 1) * 512)
                P.act(GT[0:64, sl], pa[tb][0:64, :], AF.Exp, bias=gb[0:64, 0:1])
                P.act(GT[64:128, sl], pa[tb][64:128, :], AF.Sigmoid)
            P.act(GT[0:64, :], GT[0:64, :], AF.Ln, bias=1.0)
            P.ts(GT[0:64, :], GT[0:64, :], negA[0:64, 0:1], ALU.mult)
            P.scan(GC[:, :], m0[:, :], GT[0:64, :], 0.0, ALU.mult, ALU.add)
            gc3 = GC[32:64, :].rearrange("p (a b) -> p a b", b=64)
            P.tt(m0[32:64, :].rearrange("p (a b) -> p a b", b=64), bc3(GC[32:64, 63:L:64], 64), gc3, ALU.subtract)
            P.tt(GC[32:64, :], m0[32:64, :], GT[32:64, :], ALU.add)
            for grp in range(4):
                g8 = slice(grp * 8, (grp + 1) * 8)
                for c in range(8):
                    ck = grp * 8 + c
                    cs = slice(c * 64, (c + 1) * 64)
                    for hs in HS:
                        P.mm(k.PB[0][hs, cs], GC[:, ck * 64:(ck + 1) * 64], k.ident[0:64, 0:64])
                        P.mm(k.PB[1][hs, cs], GT[64:128, ck * 64:(ck + 1) * 64], k.ident[64:128, 64:128])
                n_ = 0
                for p in range(3):
                    for hf, hs in enumerate(HS):
                        h = 2 * p + hf
                        for q, ps in ((0, k.PB[0]), (1, k.PB[1])):
                            src = v3(ps)[hs, :, h:h + 33:32]
                            P.cp(GP[p][hs, g8, 2 * q:2 * q + 2], src, eng=("act" if q else "dve"))
            for p in range(3):
                P.ts(NBP[p][:], GP[p][:, :, 2:4], -1.0, ALU.mult)
        for p in range(3):
            with scope(k):
                Q = P.sb("g_Q", [128, L], BF16, blk=512)
                K_ = P.sb("g_K", [128, L], BF16, blk=512)
                Kt = P.sb("g_Kt", [128, NCK, 64], BF16, blk=512)
                Vt = P.sb("g_Vt", [128, NCK, 64], BF16, blk=512)
                O = P.sb("g_O", [128, L], F32, blk=512)
                P.memset(O[:], 0.0, eng="pool")
                with scope(k):
                    xp = P.sb("g_xp", [128, L + 4], F32)
                    Vf = P.sb("g_Vf", [128, L], BF16, blk=512)
                    cv = P.sb("g_cv", [128, L], F32, blk=512)
                    sq = P.sb("g_sq", [128, 512], F32)
                    rn = P.sb("g_rn", [128, 512], F32)
                    P.memset(xp[:, 0:2], 0.0)
                    P.memset(xp[:, L + 2:L + 4], 0.0)
                    for part, nm, dst in ((0, "gq", Q), (1, "gk", K_), (2, "gv", Vf)):
                        pa = proj(k, l, "%s%d" % (nm, p))
                        for tb in range(4):
                            P.cp(xp[:, 2 + tb * 512:2 + (tb + 1) * 512], pa[tb][:, :], eng=("act" if tb % 2 else "dve"))
                        wi = part * 3 + p
                        P.ts(cv[:], xp[:, 0:L], cw[:, wi, 0:1], ALU.mult)
                        for j in range(1, 5):
                            P.stt(cv[:], xp[:, j:j + L], cw[:, wi, j:j + 1], cv[:], ALU.mult, ALU.add)
                        if part == 2:
                            P.act(dst[:], cv[:], AF.Silu)
                        else:
                            P.act(cv[:], cv[:], AF.Silu)
                        if part < 2:
                            for tb in range(4):
                                sl = slice(tb * 512, (tb + 1) * 512)
                                P.act(sq[:], cv[:, sl], AF.Square)
                                P.mm(k.PB[2][:, :], bones[:], sq[:])
                                if part == 0:
                                    P.rpow(rn[:], k.PB[2][:, :], -0.5, scale=64.0, bias=64e-6)
                                else:
                                    P.rpow(rn[:], k.PB[2][:, :], -0.5, scale=1.0, bias=1e-6)
                                P.tt(dst[:, sl], cv[:, sl], rn[:], ALU.mult)
                    for src, dstt in ((K_, Kt), (Vf, Vt)):
                        for grp in range(4):
                            ps = k.PB[grp % 2]
                            for c in range(8):
                                ck = grp * 8 + c
                                tr2(P, ps, c, src[:, ck * 64:(ck + 1) * 64], k.identb)
                            P.cp(dstt[:, grp * 8:(grp + 1) * 8, :], v3(ps), eng=("act" if grp % 2 else "dve"))
                with scope(k):
                    T = dict(GC=GC, GP=GP[p], NBP=NBP[p], sel=sel, negm=negm, id2=id2, Q=Q, K=K_, Kt=Kt, Vt=Vt, O=O)
                    run_pipelined([gdn_chain(k, p, d, T) for d in range(2)], depth=1)
                with scope(k):
                    sq = P.sb("g_osq", [128, 512], F32)
                    rn = P.sb("g_orn", [128, 512], F32)
                    for tb in range(4):
                        sl = slice(tb * 512, (tb + 1) * 512)
                        P.act(sq[:], O[:, sl], AF.Square)
                        P.mm(k.PB[2][:, :], bones[:], sq[:])
                        P.rpow(rn[:], k.PB[2][:, :], -0.5, scale=1.0 / 64, bias=EPS)
                        P.tt(rn[:], O[:, sl], rn[:], ALU.mult)
                        P.ts(mixc(k, p)[:, sl], rn[:], ng[:, 0:1], ALU.mult)


def gdn_chain(k, p, d, T):
    P = k.P
    GC, GP, NBP, sel, negm, id2, Q, K_, Kt, Vt, O = (T[n] for n in ("GC", "GP", "NBP", "sel", "negm", "id2", "Q", "K", "Kt", "Vt", "O"))
    B = k.PA if d == 0 else k.PB
    tag = "g%d_" % d
    names = ("CB", "EI", "QG", "Rm", "U0", "WT", "BW", "KD", "GK", "AcT", "Sg", "Ug", "nA", "nB", "nC", "nD", "Nb", "PTb", "Sgb")
    f32n = ("CB", "EI", "AcT", "Sg")
    NSET = 1
    GG = [{n: P.sb(tag + "%d" % s_ + n, [128, 8, 64], F32 if n in f32n else BF16) for n in names} for s_ in range(NSET)]
    for s_ in range(NSET):
        GG[s_]["gend"] = P.sb(tag + "gend%d" % s_, [128, 8], F32)
        GG[s_]["kds"] = P.sb(tag + "kds%d" % s_, [128, 8], F32)
    gam = P.sb(tag + "gam", [128, NCK], F32)
    shared = {"scan": 0}
    Scar = P.sb(tag + "Scar", [128, 64], F32)
    e_ = 63 if d == 0 else 0
    idb = bcm(id2[:, :], 8)

    def f2(t):
        return t[:].rearrange("p a b -> p (a b)")
    P.memset(Scar[:], 0.0)
    P.act(gam[:], GP[:, :, d], AF.Exp)
    gorder = list(range(4)) if d == 0 else list(range(3, -1, -1))

    def group(gi, grp, G):
        Nb, PTb, Sgb, gend, kds = G["Nb"], G["PTb"], G["Sgb"], G["gend"], G["kds"]
        sl = slice(grp * 512, (grp + 1) * 512)
        g8 = slice(grp * 8, (grp + 1) * 8)
        cj = GP[:, g8, d]
        nb = NBP[:, g8, d]
        CB, EI, QG, Rm, U0, WT, BW, KD, GK, AcT, Sg, Ug = (G[n] for n in names[:12])
        P.mm(B[0][:, :], sel[:, d * 3 + p, :], GC[:, sl])
        P.cp(f2(CB), B[0][:, :], eng="act")
        P.act(f2(EI), f2(CB), AF.Exp)
        P.cp(gend[:], EI[:, :, e_], eng="pool")
        P.tt(f2(QG), f2(EI), Q[:, sl], ALU.mult)
        P.tt(kds[:], CB[:, :, e_], cj, ALU.subtract)
        P.act(kds[:], kds[:], AF.Exp)
        P.tt(GK[:], Kt[:, g8, :], bc3(gam[:, g8], 64), ALU.mult, eng="pool")
        P.tt(KD[:], Kt[:, g8, :], bc3(kds[:], 64), ALU.mult, eng="pool")
        P.tt(CB[:], CB[:], bc3(cj, 64), ALU.subtract)
        P.tt(CB[:], CB[:], bcm(negm[:, d, :], 8), ALU.add)
        P.act(f2(CB), f2(CB), AF.Exp)
        yield
        for c in range(8):
            cs = slice((grp * 8 + c) * 64, (grp * 8 + c + 1) * 64)
            mm2(P, B[0], c, K_[:, cs], Q[:, cs])
            mm2(P, B[1], c, K_[:, cs], K_[:, cs])
        P.tt(EI[:], CB[:], idb, ALU.add)
        P.tt(PTb[:], EI[:], v3(B[0]), ALU.mult)
        P.tt(CB[:], CB[:], v3(B[1]), ALU.mult)
        P.tt(Nb[:], CB[:], bc3(nb, 64), ALU.mult)
        yield
        for _ in neumann2(k, Nb, Rm, (G["nA"], G["nB"], G["nC"], G["nD"]), (B[0], B[1], B[2]), id2):
            yield
        for c in range(8):
            mm2(P, B[0], c, Rm[:, c, :], Vt[:, grp * 8 + c, :])
            mm2(P, B[1], c, GK[:, c, :], Rm[:, c, :])
            mm2(P, B[2], c, Rm[:, c, :], GK[:, c, :])
        P.tt(U0[:], v3(B[0]), bc3(nb, 64), ALU.mult)
        P.ts(f2(U0), f2(U0), -1.0, ALU.mult, eng="pool")
        P.cp(WT[:], v3(B[1]), eng="act")
        P.tt(BW[:], v3(B[2]), bc3(nb, 64), ALU.mult)
        yield
        for c in range(8):
            mm2(P, B[3], c, BW[:, c, :], KD[:, c, :])
        P.tt(AcT[:], idb, bc3(gend[:], 64), ALU.mult, eng="pool")
        P.tt(AcT[:], AcT[:], v3(B[3]), ALU.add)
        yield
        while shared["scan"] != gi:
            yield
        corder = range(8) if d == 0 else range(7, -1, -1)
        prev = Scar[:]
        for n, c in enumerate(corder):
            P.cp(Sg[:, c, :], prev, eng="pool") if n == 0 else None
            ps = B[2 + n % 2]
            mm2(P, ps, 0, AcT[:, c, :], Sg[:, c, :], start=True, stop=False)
            mm2(P, ps, 0, KD[:, c, :], U0[:, c, :], start=False, stop=True)
            last = (n == 7)
            dst = Scar[:] if last else Sg[:, corder[n + 1], :]
            P.cp(dst, ps[:, 0:64], eng="act")
            yield
        shared["scan"] = gi + 1
        P.cp(Sgb[:], Sg[:], eng="pool")
        for c in range(8):
            mm2(P, B[0], c, WT[:, c, :], Sgb[:, c, :])
        P.tt(CB[:], v3(B[0]), bc3(nb, 64), ALU.mult)
        P.tt(Ug[:], CB[:], U0[:], ALU.add)
        yield
        for c in range(8):
            mm2(P, B[1], c, Sgb[:, c, :], QG[:, c, :], start=True, stop=False)
            mm2(P, B[1], c, Ug[:, c, :], PTb[:, c, :], start=False, stop=True)
        P.tt(O[:, sl], O[:, sl], B[1][:, :], ALU.add)
        yield
    return [(lambda slot, gi=gi, grp=grp: group(gi, grp, GG[slot])) for gi, grp in enumerate(gorder)]


RW_EPS = 64e-5
DEC = float(np.exp(-0.5))
GS = 8
NG = NCK // GS


def host_rwkv(inp, b, m):
    mu = inp["rwkv_mu"]
    o = np.zeros((DEPTH, 128, 8, 2), np.float32)
    for part in range(3):
        for p in range(2):
            o[:, :, part * 2 + p, :] = mu[:, :, part * 256 + p * 128: part * 256 + (p + 1) * 128].transpose(0, 2, 1)
    o[:, 0:64, 6, :] = mu[:, :, 768:832].transpose(0, 2, 1)
    o[:, 0:64, 7, :] = mu[:, :, 832:896].transpose(0, 2, 1)
    m["rmu"] = o

    def pp(a):
        if a.ndim == 2:
            return np.ascontiguousarray(a.reshape(DEPTH, 2, 128).transpose(0, 2, 1))
        return np.ascontiguousarray(a.reshape(DEPTH, 2, 2, 128).transpose(0, 3, 1, 2))
    pv = np.zeros((DEPTH, 128, 7, 2), np.float32)
    pv[:, :, 0:2, :] = pp(inp["rwkv_w0"])
    pv[:, :, 2:4, :] = pp(inp["rwkv_a0"])
    pv[:, :, 4, :] = pp(inp["rwkv_k_k"])
    pv[:, :, 5, :] = pp(inp["rwkv_k_a"])
    pv[:, :, 6, :] = pp(inp["rwkv_r_k"].reshape(DEPTH, 256))
    m["rpv"] = pv
    ln = np.zeros((DEPTH, 128, 2, 2), np.float32)
    ln[:, :, 0, :] = pp(inp["rwkv_ln_g"])
    ln[:, :, 1, :] = pp(inp["rwkv_ln_b"])
    m["rln"] = ln
    m["rw2"] = np.ascontiguousarray(inp["rwkv_w2"].transpose(0, 2, 1, 3))
    m["ra2"] = np.ascontiguousarray(inp["rwkv_a2"].transpose(0, 2, 1, 3))
    s_ = np.arange(64)[:, None]
    t_ = np.arange(64)[None, :]
    msk = np.zeros((128, 2, 4, 64), np.float32)
    msk[:, 0, 0, :] = np.tile((t_ > s_), (2, 1))
    msk[:, 0, 1, :] = np.tile((t_ >= s_), (2, 1))
    msk[:, 1, 0, :] = np.tile((t_ < s_), (2, 1))
    msk[:, 1, 1, :] = np.tile((t_ <= s_), (2, 1))
    msk[:, :, 2:4, :] = -msk[:, :, 0:2, :]
    m["rmsk"] = msk


def rwkv_decl(k):
    nc = k.nc

    def din(name, shape, dt=F32):
        return nc.dram_tensor(name, list(shape), dt, kind="ExternalInput").ap()
    k.rmu_d = din("rmu", [DEPTH, 128, 8, 2])
    k.rpv_d = din("rpv", [DEPTH, 128, 7, 2])
    k.rln_d = din("rln", [DEPTH, 128, 2, 2])
    k.rw2_d = din("rw2", [DEPTH, 64, 2, 256])
    k.ra2_d = din("ra2", [DEPTH, 64, 2, 256])
    k.rmsk_d = din("rmsk", [128, 2, 4, 64])


def rwkv_phase(k, l):
    P = k.P
    with scope(k):
        mu = P.sb("r_mu", [128, 8, 3], F32)
        pv = P.sb("r_pv", [128, 7, 2], F32)
        omka = P.sb("r_omka", [128, 2], F32)
        hrk = P.sb("r_hrk", [128, 2], F32)
        ln = P.sb("r_ln", [128, 2, 2], F32)
        w2 = P.sb("r_w2", [64, 2, 256], BF16)
        a2 = P.sb("r_a2", [64, 2, 256], BF16)
        msk = P.sb("r_msk", [128, 2, 4, 64], F32)
        id2 = P.sb("r_id2", [128, 64], F32)
        bones = P.sb("r_bones", [128, 128], F32)
        m0 = P.sb("r_m0", [128, GS * 64], F32)
        twd = P.sb("r_twd", [64, L], BF16, blk=512)
        adx = P.sb("r_adx", [64, L], BF16, blk=512)
        sh32 = P.sb("r_sh32", [128, L], F32, blk=512)
        xp = P.sb("r_xp", [128, L + 2], F32)
        P.dma(mu[:, :, 0:2], k.rmu_d[l])
        P.dma(pv[:], k.rpv_d[l])
        P.dma(ln[:], k.rln_d[l])
        with scope(k):
            w2f = P.sb("r_w2f", [64, 2, 256], F32)
            a2f = P.sb("r_a2f", [64, 2, 256], F32)
            P.dma(w2f[:], k.rw2_d[l])
            P.dma(a2f[:], k.ra2_d[l])
            P.cp(w2[:], w2f[:], eng="act")
            P.cp(a2[:], a2f[:], eng="act")
        P.dma(msk[:], k.rmsk_d[:])
        P.dma(id2[:], k.gid2_d[:])
        P.memset(bones[:], 0.0)
        P.memset(bones[0:64, 0:64], 1.0)
        P.memset(bones[64:128, 64:128], 1.0)
        P.memset(m0[:], 1.0)
        P.memset(m0[:, 0:GS * 64:64], 0.0)
        P.memset(xp[:, 0:1], 0.0)
        P.memset(xp[:, L + 1:L + 2], 0.0)
        P.tt(mu[:, :, 2], mu[:, :, 0], mu[:, :, 1], ALU.add)
        P.ts(mu[:, :, 2], mu[:, :, 2], -1.0, ALU.mult, 1.0, ALU.add)
        P.ts(omka[:], pv[:, 5, :], -1.0, ALU.mult, 1.0, ALU.add)
        P.ts(hrk[:], pv[:, 6, :], 0.5, ALU.mult)

        def shifted(name, ci, dst, np_=128, fn=None):
            pa = proj(k, l, name, alt=(ci % 2 == 1))
            for tb in range(4):
                P.cp(xp[0:np_, 1 + tb * 512:1 + (tb + 1) * 512], pa[tb][0:np_, :], eng=("act" if tb % 2 else "dve"))
            t_ = sh32[0:np_, :]
            P.ts(t_, xp[0:np_, 1:L + 1], mu[0:np_, ci, 2:3], ALU.mult)
            P.stt(t_, xp[0:np_, 0:L], mu[0:np_, ci, 0:1], t_, ALU.mult, ALU.add)
            if fn is None:
                P.stt(dst[:], xp[0:np_, 2:L + 2], mu[0:np_, ci, 1:2], t_, ALU.mult, ALU.add)
            else:
                P.stt(t_, xp[0:np_, 2:L + 2], mu[0:np_, ci, 1:2], t_, ALU.mult, ALU.add)
                P.act(dst[:], t_, fn)

        shifted("rwd", 6, twd, 64, AF.Tanh)
        shifted("rad", 7, adx, 64)
        R_ = P.sb("r_R", [128, L], BF16, blk=512)
        KX = P.sb("r_KX", [128, L], BF16, blk=512)
        V_ = P.sb("r_V", [128, L], BF16, blk=512)
        KK = P.sb("r_KK", [128, L], BF16, blk=512)
        Vt = P.sb("r_Vt", [128, NCK, 64], BF16, blk=512)
        KS = P.sb("r_KS", [128, L], F32, blk=512)
        Y = xp[:, 1:L + 1]
        sq = P.sb("r_sq", [128, 512], F32)
        rn = P.sb("r_rn", [128, 512], F32)
        CH = [rwkv_tiles(k, e) for e in range(2)]
        for p in range(2):
            shifted("rr%d" % p, 0 + p, R_)
            shifted("rk%d" % p, 2 + p, KX)
            shifted("rv%d" % p, 4 + p, V_)
            P.ts(sh32[:], KX[:], pv[:, 4, p:p + 1], ALU.mult)
            for tb in range(4):
                sl = slice(tb * 512, (tb + 1) * 512)
                P.act(sq[:], sh32[:, sl], AF.Square)
                P.mm(k.PB[2][:, :], bones[:], sq[:])
                P.rpow(rn[:], k.PB[2][:, :], -0.5, scale=1.0, bias=1e-6)
                P.tt(KK[:, sl], sh32[:, sl], rn[:], ALU.mult)
            for grp in range(4):
                ps = k.PB[grp % 2]
                for c in range(8):
                    ck = grp * 8 + c
                    tr2(P, ps, c, V_[:, ck * 64:(ck + 1) * 64], k.identb)
                P.cp(Vt[:, grp * 8:(grp + 1) * 8, :], v3(ps), eng=("act" if grp % 2 else "dve"))
            P.memset(xp[:, 1:L + 1], 0.0, eng="pool")
            P.memset(KS[:], 0.0, eng="pool")
            T = dict(pv=pv, omka=omka, w2=w2, a2=a2, msk=msk, id2=id2, m0=m0, twd=twd, adx=adx,
                     R=R_, KX=KX, KK=KK, Vt=Vt, KS=KS, Y=Y)
            run_interleaved([rwkv_chain(k, p, e, T, CH[e]) for e in range(2)])
            for tb in range(4):
                sl = slice(tb * 512, (tb + 1) * 512)
                P.mm(k.PB[2][:, :], bones[:], Y[:, sl])
                P.stt(Y[:, sl], k.PB[2][:, :], -1.0 / 64, Y[:, sl], ALU.mult, ALU.add)
                P.act(sq[:], Y[:, sl], AF.Square)
                P.mm(k.PB[2][:, :], bones[:], sq[:])
                P.rpow(rn[:], k.PB[2][:, :], -0.5, scale=1.0 / 64, bias=RW_EPS)
                P.tt(Y[:, sl], Y[:, sl], rn[:], ALU.mult)
                P.ts(Y[:, sl], Y[:, sl], ln[:, 0, p:p + 1], ALU.mult, ln[:, 1, p:p + 1], ALU.add)
                P.tt(sq[:], R_[:, sl], KS[:, sl], ALU.mult)
                P.ts(sq[:], sq[:], hrk[:, p:p + 1], ALU.mult)
                P.mm(k.PB[3][:, :], bones[:], sq[:])
                P.tt(rn[:], k.PB[3][:, :], V_[:, sl], ALU.mult)
                P.tt(mixc(k, 6 + p)[:, sl], Y[:, sl], rn[:], ALU.add)


RW_F32 = ("lw", "a", "km", "b", "cl", "e1", "e2", "dend", "AcT", "Tg")
RW_BF16 = ("kap", "rt", "kt_", "bt_", "ke", "be", "kapT", "keT", "nbeT", "N", "Akv", "Brk", "nBrb", "Rm",
           "nA", "nB", "nC", "nD", "X0", "P0", "WkT", "Wk", "Tgb", "Pg")


def rwkv_tiles(k, e):
    P = k.P
    G = {n: P.sb("r%d_%s" % (e, n), [128, GS, 64], F32) for n in RW_F32}
    for n in RW_BF16:
        G[n] = P.sb("r%d_%s" % (e, n), [128, GS, 64], BF16)
    G["gC"] = P.sb("r%d_gC" % e, [128, GS], F32)
    G["Tcar"] = P.sb("r%d_Tcar" % e, [128, 64], F32)
    return G


def rwkv_chain(k, p, e, T, G):
    P = k.P
    pv, omka, w2, a2, msk, id2, m0, twd, adx, R_, KX, KK, Vt, KS, Y = (T[n] for n in (
        "pv", "omka", "w2", "a2", "msk", "id2", "m0", "twd", "adx", "R", "KX", "KK", "Vt", "KS", "Y"))
    B = k.PA if e == 0 else k.PB
    W = GS * 64
    e_ = 63 if e == 0 else 0
    idb = bcm(id2[:, :], GS)
    gC, Tcar = G["gC"], G["Tcar"]

    def f2(t):
        return t[:].rearrange("p a b -> p (a b)")

    def w3(ps):
        return v3(ps, GS)
    P.memset(Tcar[:], 0.0)
    gorder = range(NG) if e == 0 else range(NG - 1, -1, -1)
    pc = slice(p * 128, (p + 1) * 128)
    for grp in gorder:
        sl = slice(grp * W, (grp + 1) * W)
        c0 = grp * GS
        P.mm(B[0][:, 0:W], w2[:, e, pc], twd[:, sl])
        P.mm(B[1][:, 0:W], a2[:, e, pc], adx[:, sl])
        P.act(f2(G["lw"]), B[0][:, 0:W], AF.Sigmoid, bias=pv[:, 0 + e, p:p + 1])
        P.ts(f2(G["lw"]), f2(G["lw"]), -DEC, ALU.mult, eng="pool")
        P.act(f2(G["a"]), B[1][:, 0:W], AF.Sigmoid, bias=pv[:, 2 + e, p:p + 1])
        P.ts(f2(G["km"]), f2(G["a"]), pv[:, 5, p:p + 1], ALU.mult, omka[:, p:p + 1], ALU.add)
        P.tt(f2(G["km"]), f2(G["km"]), KX[:, sl], ALU.mult)
        P.tt(f2(G["b"]), f2(G["a"]), KK[:, sl], ALU.mult, eng="pool")
        P.tt(KS[:, sl], KS[:, sl], f2(G["km"]), ALU.add, eng="pool")
        P.scan(f2(G["cl"]), m0[:], f2(G["lw"]), 0.0, ALU.mult, ALU.add)
        if e == 1:
            P.tt(G["e1"][:], bc3(G["cl"][:, :, 63], 64), G["cl"][:], ALU.subtract)
            P.tt(G["cl"][:], G["e1"][:], G["lw"][:], ALU.add)
        yield
        P.act(G["e1"][:], G["cl"][:], AF.Exp)
        P.act(G["e2"][:], G["cl"][:], AF.Exp, scale=-1.0)
        P.tt(f2(G["rt"]), f2(G["e1"]), R_[:, sl], ALU.mult)
        P.tt(G["kt_"][:], G["e2"][:], G["km"][:], ALU.mult, eng="pool")
        P.tt(G["bt_"][:], G["e2"][:], G["b"][:], ALU.mult)
        P.tt(G["dend"][:], G["cl"][:], G["lw"][:], ALU.subtract, eng="pool")
        P.act(G["dend"][:], G["dend"][:], AF.Exp)
        P.tt(f2(G["kap"]), f2(G["dend"]), KK[:, sl], ALU.mult)
        P.cp(gC[:], G["e1"][:, :, e_], eng="pool")
        P.tt(G["dend"][:], bc3(G["cl"][:, :, e_], 64), G["cl"][:], ALU.subtract, eng="pool")
        P.act(G["dend"][:], G["dend"][:], AF.Exp)
        P.tt(G["ke"][:], G["dend"][:], G["km"][:], ALU.mult)
        P.tt(G["be"][:], G["dend"][:], G["b"][:], ALU.mult, eng="pool")
        yield
        for src, dst, sc in ((G["kap"], G["kapT"], 1.0), (G["ke"], G["keT"], 1.0), (G["be"], G["nbeT"], -1.0)):
            ps = B[0] if sc == 1.0 and src is G["kap"] else (B[1] if sc == 1.0 else B[2])
            for c in range(GS):
                tr2(P, ps, c, src[:, c, :], k.identb)
            if sc == 1.0:
                P.cp(dst[:], w3(ps), eng="act")
            else:
                P.ts(dst[:], w3(ps), -1.0, ALU.mult)
        yield
        for c in range(GS):
            mm2(P, B[0], c, G["bt_"][:, c, :], G["kap"][:, c, :])
            mm2(P, B[1], c, G["kt_"][:, c, :], G["kap"][:, c, :])
            mm2(P, B[2], c, G["kt_"][:, c, :], G["rt"][:, c, :])
            mm2(P, B[3], c, G["bt_"][:, c, :], G["rt"][:, c, :])
        ms = bcm(msk[:, e, 0, :], GS)
        mi = bcm(msk[:, e, 1, :], GS)
        nms = bcm(msk[:, e, 2, :], GS)
        nmi = bcm(msk[:, e, 3, :], GS)
        P.tt(G["N"][:], w3(B[0]), nms, ALU.mult)
        P.tt(G["Akv"][:], w3(B[1]), ms, ALU.mult)
        P.tt(G["Brk"][:], w3(B[2]), mi, ALU.mult)
        P.tt(G["nBrb"][:], w3(B[3]), nmi, ALU.mult)
        yield
        for _ in neumann2(k, G["N"], G["Rm"], (G["nA"], G["nB"], G["nC"], G["nD"]), (B[0], B[1], B[2]), id2, GS):
            yield
        for c in range(GS):
            mm2(P, B[0], c, G["Akv"][:, c, :], Vt[:, c0 + c, :])
        P.cp(G["X0"][:], w3(B[0]), eng="act")
        yield
        for c in range(GS):
            mm2(P, B[0], c, G["Rm"][:, c, :], G["X0"][:, c, :])
            mm2(P, B[1], c, G["kapT"][:, c, :], G["Rm"][:, c, :])
            mm2(P, B[2], c, G["Rm"][:, c, :], G["kapT"][:, c, :])
        P.cp(G["P0"][:], w3(B[0]), eng="act")
        P.cp(G["WkT"][:], w3(B[1]), eng="dve")
        P.cp(G["Wk"][:], w3(B[2]), eng="act")
        yield
        for c in range(GS):
            mm2(P, B[3], c, G["Wk"][:, c, :], G["nbeT"][:, c, :])
        P.tt(G["AcT"][:], idb, bc3(gC[:], 64), ALU.mult, eng="pool")
        P.tt(G["AcT"][:], G["AcT"][:], w3(B[3]), ALU.add)
        yield
        corder = list(range(GS)) if e == 0 else list(range(GS - 1, -1, -1))
        Tg = G["Tg"]
        for n, c in enumerate(corder):
            if n == 0:
                P.cp(Tg[:, c, :], Tcar[:], eng="pool")
            ps = B[2 + n % 2]
            mm2(P, ps, 0, G["AcT"][:, c, :], Tg[:, c, :], start=True, stop=False)
            mm2(P, ps, 0, G["keT"][:, c, :], Vt[:, c0 + c, :], start=False, stop=False)
            mm2(P, ps, 0, G["nbeT"][:, c, :], G["P0"][:, c, :], start=False, stop=True)
            dst = Tcar[:] if n == GS - 1 else Tg[:, corder[n + 1], :]
            P.cp(dst, ps[:, 0:64], eng="act")
            yield
        P.cp(G["Tgb"][:], Tg[:], eng="pool")
        for c in range(GS):
            mm2(P, B[0], c, G["WkT"][:, c, :], G["Tgb"][:, c, :])
        P.tt(G["Pg"][:], w3(B[0]), G["P0"][:], ALU.add)
        yield
        for c in range(GS):
            mm2(P, B[1], c, Vt[:, c0 + c, :], G["Brk"][:, c, :], start=True, stop=False)
            mm2(P, B[1], c, G["Tgb"][:, c, :], G["rt"][:, c, :], start=False, stop=False)
            mm2(P, B[1], c, G["Pg"][:, c, :], G["nBrb"][:, c, :], start=False, stop=True)
        P.tt(Y[:, sl], Y[:, sl], B[1][:, 0:W], ALU.add)
        yield


_CACHE = {}


def kernel(**inputs):
    inp = {k_: np.asarray(v) for k_, v in inputs.items()}
    if "k" not in _CACHE:
        _CACHE["k"] = build()
    k = _CACHE["k"]
    B = inp["x"].shape[0]
    base = host_inputs(inp, 0)
    in_maps = []
    for b in range(B):
        m = dict(base)
        m["x"] = np.ascontiguousarray(inp["x"][b], dtype=np.float32)
        m["pos"] = np.ascontiguousarray(inp["positions"][b].reshape(1, L).astype(np.int32))
        in_maps.append(m)
    res = run_bass_kernel_spmd(k.nc, in_maps, core_ids=list(range(B)))
    return np.stack([np.asarray(r["out"], dtype=np.float32) for r in res.results], axis=0)
```

```python
import numpy as np
import concourse.bass as bass
import concourse.mybir as mybir
from concourse.bass_utils import run_bass_kernel_spmd
from contextlib import ExitStack

F32 = mybir.dt.float32
BF16 = mybir.dt.bfloat16
I32 = mybir.dt.int32
AF = mybir.ActivationFunctionType
ALU = mybir.AluOpType
AX = mybir.AxisListType
DTSIZE = {F32: 4, BF16: 2, I32: 4}


class _Op:
    __slots__ = ("eng", "emit", "deps", "idx", "needed", "sigval", "dsem", "dval", "isdma")

    def __init__(self, eng, emit, isdma=False):
        self.eng = eng
        self.emit = emit
        self.deps = []
        self.idx = -1
        self.needed = False
        self.sigval = 0
        self.dsem = None
        self.dval = 0
        self.isdma = isdma


class _Blk:
    __slots__ = ("w", "r")

    def __init__(self):
        self.w = None
        self.r = {}


class Prog:
    ENGS = ("pe", "dve", "act", "pool", "sp")
    NDMA = 48
    NHW = 32

    def __init__(self, nc, stack):
        self.nc = nc
        self.stack = stack
        self.ops = {e: [] for e in self.ENGS}
        self.track = {}
        self.seen = {e: {} for e in self.ENGS}
        self.seen_dma = {e: set() for e in self.ENGS}
        self.dma_last = [None] * self.NDMA
        self.dma_uses = [0] * self.NDMA
        self.dma_rr = 0
        self.dma_rr_sw = 0
        self.ndma_ops = 0
        self.untracked = set()
        self.out_dmas = []
        self.dma_pending = []
        self.last_compute = {}

    def sb(self, name, shape, dtype=F32, blk=None):
        self.uid = getattr(self, "uid", 0) + 1
        name = "s%d_%s" % (self.uid, name)
        t = self.stack.enter_context(self.nc.sbuf_tensor(name, list(shape), dtype))
        self._register(name, shape, dtype, blk)
        return t

    def ps(self, name, shape=(128, 512), dtype=F32, blk=None):
        self.uid = getattr(self, "uid", 0) + 1
        name = "p%d_%s" % (self.uid, name)
        t = self.stack.enter_context(self.nc.psum_tensor(name, list(shape), dtype))
        self._register(name, shape, dtype, blk)
        return t

    def _register(self, name, shape, dtype, blk):
        row = int(np.prod(shape[1:])) * DTSIZE[dtype]
        bb = row if blk is None else blk * DTSIZE[dtype]
        nb = (row + bb - 1) // bb
        self.track[name] = (bb, row, [_Blk() for _ in range(nb)])

    def dram_track(self, name, total_bytes, blk_bytes):
        nb = (total_bytes + blk_bytes - 1) // blk_bytes
        self.track[name] = (blk_bytes, -1, [_Blk() for _ in range(nb)])

    def _blocks(self, ap):
        name = ap.tensor.name
        if name not in self.track:
            return ()
        bb, row, blks = self.track[name]
        if len(blks) == 1:
            return blks
        ds = DTSIZE[ap.dtype]
        pat = ap.ap
        if row < 0:
            lo = hi = ap.offset
            for step, cnt in pat:
                ext = step * (cnt - 1)
                if ext < 0:
                    lo += ext
                else:
                    hi += ext
            return blks[(lo * ds) // bb:(hi * ds) // bb + 1]
        rowel = row // ds
        foff = ap.offset % rowel
        lo = hi = foff
        for step, cnt in pat[1:]:
            ext = step * (cnt - 1)
            if ext < 0:
                lo += ext
            else:
                hi += ext
        b0 = (lo * ds) // bb
        b1 = (hi * ds) // bb
        return blks[b0:b1 + 1]

    def _dep(self, x, y):
        if y is None or y is x:
            return
        e = x.eng
        if y.isdma:
            if id(y) in self.seen_dma[e]:
                return
            self.seen_dma[e].add(id(y))
            x.deps.append(y)
            return
        if y.eng == "pe" and e == "pe":
            return
        if y.idx <= self.seen[e].get(y.eng, -1):
            return
        self.seen[e][y.eng] = y.idx
        y.needed = True
        x.deps.append(y)

    def add(self, eng, emit, reads=(), writes=(), isdma=False):
        x = _Op(eng, emit, isdma)
        x.idx = len(self.ops[eng])
        rb = []
        for ap in reads:
            if ap is None or isinstance(ap, (int, float)):
                continue
            rb.extend(self._blocks(ap))
        wb = []
        for ap in writes:
            wb.extend(self._blocks(ap))
        for ap in reads:
            if ap is None or isinstance(ap, (int, float)) or not ap.tensor.name.startswith("p"):
                continue
            for b in self._blocks(ap):
                for key, y in b.r.items():
                    if key != eng:
                        self._dep(x, y)
        for b in rb:
            self._dep(x, b.w)
        for b in wb:
            self._dep(x, b.w)
            for y in b.r.values():
                self._dep(x, y)
        if isdma:
            if eng == "pool":
                s = self.NHW + self.dma_rr_sw
                self.dma_rr_sw = (self.dma_rr_sw + 1) % (self.NDMA - self.NHW)
            else:
                s = self.dma_rr
                self.dma_rr = (self.dma_rr + 1) % self.NHW
            self._dep(x, self.dma_last[s])
            self.dma_last[s] = x
            self.dma_uses[s] += 1
            x.dsem = s
            x.dval = 16 * self.dma_uses[s]
            self.ndma_ops += 1
        key = id(x) if isdma else eng
        for b in rb:
            b.r[key] = x
        for b in wb:
            b.w = x
            b.r = {}
        self.ops[eng].append(x)
        if isdma:
            self.dma_pending.append(x)
        else:
            self.last_compute[eng] = x
        return x

    def barrier(self):
        lasts = dict(self.last_compute)
        pend = list(self.dma_pending)
        self.dma_pending = []
        for e in self.ENGS:
            b = _Op(e, None)
            b.idx = len(self.ops[e])
            for e2, y in lasts.items():
                if e2 == e and e == "pe":
                    continue
                self._dep(b, y)
            for y in pend:
                self._dep(b, y)
            self.ops[e].append(b)

    def mm(self, out, lhsT, rhs, start=True, stop=True):
        return self.add("pe", lambda e: e.matmul(out, lhsT, rhs, start=start, stop=stop),
                        reads=(lhsT, rhs), writes=(out,))

    def tr(self, out, in_, ident):
        return self.add("pe", lambda e: e.transpose(out, in_, ident), reads=(in_, ident), writes=(out,))

    def tt(self, out, in0, in1, op, eng="dve"):
        return self.add(eng, lambda e: e.tensor_tensor(out, in0, in1, op), reads=(in0, in1), writes=(out,))

    def ts(self, out, in0, s1, op0, s2=None, op1=None, eng="dve", accum_out=None):
        kw = {}
        if eng == "pool" and op1 is None:
            if op0 == ALU.mult:
                s2, op1 = 0.0, ALU.add
            elif op0 == ALU.add:
                s2, op1 = 1.0, ALU.mult
        if op1 is not None:
            kw["op1"] = op1
        if accum_out is not None:
            kw["accum_out"] = accum_out
        w = (out,) if accum_out is None else (out, accum_out)
        return self.add(eng, lambda e: e.tensor_scalar(out, in0, s1, s2, op0, **kw),
                        reads=(in0, s1, s2), writes=w)

    def stt(self, out, in0, scalar, in1, op0, op1, accum_out=None):
        kw = {}
        if accum_out is not None:
            kw["accum_out"] = accum_out
        w = (out,) if accum_out is None else (out, accum_out)
        return self.add("dve", lambda e: e.scalar_tensor_tensor(out, in0, scalar, in1, op0, op1, **kw),
                        reads=(in0, scalar, in1), writes=w)

    def cp(self, out, in_, eng="dve"):
        if eng == "act":
            return self.add("act", lambda e: e.copy(out, in_), reads=(in_,), writes=(out,))
        return self.add(eng, lambda e: e.tensor_copy(out, in_), reads=(in_,), writes=(out,))

    def act(self, out, in_, func, bias=0.0, scale=1.0, accum_out=None):
        kw = {}
        if accum_out is not None:
            kw["accum_out"] = accum_out
        w = (out,) if accum_out is None else (out, accum_out)
        return self.add("act", lambda e: e.activation(out, in_, func, bias=bias, scale=scale, **kw),
                        reads=(in_, bias, scale), writes=w)

    def red(self, out, in_, op, axis=AX.X, eng="dve"):
        return self.add(eng, lambda e: e.tensor_reduce(out, in_, axis, op), reads=(in_,), writes=(out,))

    def recip(self, out, in_):
        return self.add("dve", lambda e: e.reciprocal(out, in_), reads=(in_,), writes=(out,))

    def rpow(self, out, in_, power, scale=1.0, bias=0.0):
        self.act(out, in_, AF.Ln, bias=bias, scale=scale)
        return self.act(out, out, AF.Exp, scale=power)

    def memset(self, ap, val, eng="dve"):
        return self.add(eng, lambda e: e.memset(ap, val), writes=(ap,))

    def scan(self, out, d0, d1, init, op0, op1):
        return self.add("dve", lambda e: e.tensor_tensor_scan(out, d0, d1, init, op0, op1),
                        reads=(d0, d1, init), writes=(out,))

    def dma(self, out, in_, eng="sp", is_output=False):
        x = self.add(eng, lambda e: e.dma_start(out=out, in_=in_), reads=(in_,), writes=(out,), isdma=True)
        if is_output:
            self.out_dmas.append(x)
        return x

    def finish(self):
        nc = self.nc
        fin = _Op("sp", None)
        fin.idx = len(self.ops["sp"])
        for y in self.out_dmas:
            self._dep(fin, y)
        self.ops["sp"].append(fin)
        sems = {}
        for e in ("pe", "dve", "act", "pool"):
            sems[e] = self.stack.enter_context(nc.semaphore("s_" + e))
        dsems = [self.stack.enter_context(nc.semaphore("d%d" % i)) for i in range(self.NDMA)]
        for e in ("pe", "dve", "act", "pool"):
            c = 0
            for x in self.ops[e]:
                if x.isdma:
                    continue
                if x.needed:
                    c += 1
                    x.sigval = c
            self.stats_sig = getattr(self, "stats_sig", {})
            self.stats_sig[e] = c
        ops = self.ops

        def replay(e, engobj):
            for x in ops[e]:
                for y in x.deps:
                    if y.isdma:
                        engobj.wait_ge(dsems[y.dsem], y.dval)
                    else:
                        engobj.wait_ge(sems[y.eng], y.sigval)
                if x.emit is None:
                    continue
                ins = x.emit(engobj)
                if x.isdma:
                    ins.then_inc(dsems[x.dsem], 16)
                elif x.needed:
                    ins.then_inc(sems[e], 1)

        with nc.Block() as block:
            @block.tensor
            def _(eng):
                replay("pe", eng)

            @block.vector
            def _(eng):
                replay("dve", eng)

            @block.scalar
            def _(eng):
                replay("act", eng)

            @block.gpsimd
            def _(eng):
                replay("pool", eng)

            @block.sync
            def _(eng):
                replay("sp", eng)


L = 2048
D = 1024
NT = L // 128
DEPTH = 2
N_IN = 3448
EPS = 1e-6

OFF = dict(gate=0, gdn_q=1024, gdn_k=1408, gdn_v=1792, gdn_a=2176, gdn_b=2188, mla_cq=2200, mla_ckv=2392,
           mla_kr=2520, rw_r=2552, rw_k=2808, rw_v=3064, rw_wd=3320, rw_ad=3384)


def chunk_table():
    ch = []
    for h in range(3):
        ch.append(("gq%d" % h, [(0, OFF["gdn_q"] + h * 128, 128)]))
        ch.append(("gk%d" % h, [(0, OFF["gdn_k"] + h * 128, 128)]))
        ch.append(("gv%d" % h, [(0, OFF["gdn_v"] + h * 128, 128)]))
    ch.append(("gab", [(0, OFF["gdn_a"], 6), (32, OFF["gdn_a"] + 6, 6), (64, OFF["gdn_b"], 6), (96, OFF["gdn_b"] + 6, 6)]))
    ch.append(("cq0", [(0, OFF["mla_cq"], 128)]))
    ch.append(("cq1", [(0, OFF["mla_cq"] + 128, 64)]))
    ch.append(("ckv", [(0, OFF["mla_ckv"], 128)]))
    ch.append(("kr", [(0, OFF["mla_kr"], 32)]))
    for i in range(2):
        ch.append(("rr%d" % i, [(0, OFF["rw_r"] + i * 128, 128)]))
        ch.append(("rk%d" % i, [(0, OFF["rw_k"] + i * 128, 128)]))
        ch.append(("rv%d" % i, [(0, OFF["rw_v"] + i * 128, 128)]))
    ch.append(("rwd", [(0, OFF["rw_wd"], 64)]))
    ch.append(("rad", [(0, OFF["rw_ad"], 64)]))
    for i in range(8):
        ch.append(("g%d" % i, [(0, OFF["gate"] + i * 128, 128)]))
    return ch


CHUNKS = chunk_table()
CH_IDX = {n: i for i, (n, _) in enumerate(CHUNKS)}
NCH = len(CHUNKS)


def host_win(w_in):
    out = np.zeros((DEPTH, NCH, 128, 8, 128), np.float32)
    for ci, (_, parts) in enumerate(CHUNKS):
        for dst, src, w in parts:
            blk = w_in[:, :, src:src + w].reshape(DEPTH, 8, 128, w)
            out[:, ci, :, :, dst:dst + w] = blk.transpose(0, 2, 1, 3)
    return out


class K:
    pass


def build(depth=DEPTH, mixers=("gdn", "mla", "rwkv"), dbg=False):
    nc = bass.Bass("TRN2", target_bir_lowering=False)
    k = K()
    k.nc = nc
    k.dbg = dbg
    k.dbg_outs = []
    k.cut = 99

    def din(name, shape, dt=F32):
        return nc.dram_tensor(name, list(shape), dt, kind="ExternalInput").ap()

    k.x_d = din("x", [L, D])
    k.win_d = din("win", [DEPTH, NCH, 128, 8, 128])
    k.normg_d = din("normg", [DEPTH, 128, 8])
    k.wout_d = din("wout", [DEPTH, 128, 8, 1024])
    k.fing_d = din("fing", [1, D])
    k.ident_d = din("ident", [128, 128])
    k.out_d = nc.dram_tensor("out", [L, D], F32, kind="ExternalOutput").ap()
    mla_decl(k)
    gdn_decl(k)
    rwkv_decl(k)

    with ExitStack() as st:
        P = Prog(nc, st)
        k.P = P
        k.xscr = nc.dram_tensor("xscr", [L, D], F32, kind="Internal").ap()
        P.dram_track("xscr", L * D * 4, 128 * D * 4)
        k.hT = P.sb("hT", [128, 8, L], BF16, blk=512)
        k.ident = P.sb("ident", [128, 128], F32)
        k.identb = P.sb("identb", [128, 128], BF16)
        k.normg = P.sb("normg", [128, DEPTH, 8], F32)
        k.wst = [P.sb("wst%d" % i, [128, 8, 128], F32) for i in range(2)]
        k.wbf = [P.sb("wbf%d" % i, [128, 8, 128], BF16) for i in range(2)]
        k.wrr = 0
        k.PAW = [P.ps("paw%d" % i, [128, 1024], F32, blk=512) for i in range(2)]
        k.PA = [k.PAW[i // 2][:, (i % 2) * 512:(i % 2 + 1) * 512] for i in range(4)]
        k.PB = [P.ps("pb%d" % i, [128, 512], F32) for i in range(4)]

        P.dma(k.ident[:], k.ident_d[:])
        P.cp(k.identb[:], k.ident[:])
        for l in range(DEPTH):
            P.dma(k.normg[:, l, :], k.normg_d[l])

        for l in range(depth):
            phase_a(k, l)
            with scope(k):
                k.mix_r = P.sb("mix_r", [128, 2, L], BF16, blk=512)
                if "rwkv" in mixers:
                    rwkv_phase(k, l)
                else:
                    P.memset(k.mix_r[:].rearrange("p a b -> p (a b)"), 1.0)
                with scope(k):
                    k.mix_m = P.sb("mix_m", [128, 3, L], BF16, blk=512)
                    if "mla" in mixers:
                        mla_phase(k, l)
                    else:
                        P.memset(k.mix_m[:].rearrange("p a b -> p (a b)"), 1.0)
                    with scope(k):
                        k.mix_g = P.sb("mix_g", [128, 3, L], BF16, blk=512)
                        if "gdn" in mixers:
                            gdn_phase(k, l)
                        else:
                            P.memset(k.mix_g[:].rearrange("p a b -> p (a b)"), 1.0)
                        if k.dbg:
                            for nm, t_, n_ in (("g", k.mix_g, 3), ("m", k.mix_m, 3), ("r", k.mix_r, 2)):
                                dump(k, "mix_%s%d" % (nm, l), t_[:].rearrange("p a b -> p (a b)"), [128, n_ * L])
                        phase_z(k, l, last=(l == depth - 1))
        P.finish()
        print("ops:", {e: len(v) for e, v in P.ops.items()}, "sig:", P.stats_sig, "dma:", P.ndma_ops)
    return k


def mixc(k, c):
    if c < 3:
        return k.mix_g[:, c, :]
    if c < 6:
        return k.mix_m[:, c - 3, :]
    return k.mix_r[:, c - 6, :]


def dump(k, name, ap, shape=None):
    if not k.dbg:
        return
    P = k.P
    shape = list(ap.shape) if shape is None else shape
    d = k.nc.dram_tensor("dbg_" + name, shape, ap.dtype, kind="ExternalOutput").ap()
    P.dma(d[:] if len(shape) == 2 else d, ap, is_output=True)
    k.dbg_outs.append("dbg_" + name)


def scope(k):
    class _S:
        def __enter__(s):
            s.old = k.P.stack
            s.st = ExitStack()
            s.st.__enter__()
            k.P.stack = s.st
            return s

        def __exit__(s, *a):
            k.P.barrier()
            k.P.stack = s.old
            s.st.__exit__(*a)
            return False
    return _S()


def phase_a(k, l):
    P = k.P
    with scope(k):
        ssq = P.sb("a_ssq", [128, NT])
        rs = P.sb("a_rs", [128, NT])
        rstd = P.sb("a_rstd", [128, NT])
        junk = [P.sb("a_junk%d" % i, [128, D], BF16) for i in range(2)]
        xs = [P.sb("a_xs%d" % i, [128, D], BF16) for i in range(2)]
        xin = [P.sb("a_xin%d" % i, [128, D], F32) for i in range(3)]
        src = k.x_d if l == 0 else k.xscr

        def stage1(tt):
            b = tt % 2
            xt_ = xin[tt % 3]
            P.dma(xt_[:], src[tt * 128:(tt + 1) * 128, :])
            P.act(junk[b][:], xt_[:], AF.Square, accum_out=ssq[:, tt:tt + 1])
            P.act(rs[:, tt:tt + 1], ssq[:, tt:tt + 1], AF.Sqrt, bias=EPS, scale=1.0 / D)
            P.recip(rstd[:, tt:tt + 1], rs[:, tt:tt + 1])
            P.ts(xs[b][:], xt_[:], rstd[:, tt:tt + 1], ALU.mult)

        def stage2(tt):
            b = tt % 2
            pt = k.PB[b][:].bitcast(BF16)
            for dc in range(8):
                P.tr(pt[:, dc * 128:(dc + 1) * 128], xs[b][:, dc * 128:(dc + 1) * 128], k.identb[:])
            P.cp(k.hT[:, :, tt * 128:(tt + 1) * 128], pt[:].rearrange("p (a b) -> p a b", a=8),
                 eng=("act" if tt % 2 == 0 else "dve"))
        stage1(0)
        for tt in range(NT):
            if tt + 1 < NT:
                stage1(tt + 1)
            stage2(tt)


def proj(k, l, name, alt=False):
    P = k.P
    BK = k.PB if alt else k.PA
    ci = CH_IDX[name]
    b = k.wrr
    k.wrr ^= 1
    P.dma(k.wst[b][:], k.win_d[l, ci], eng="sp")
    gb = k.normg[:, l, :].unsqueeze(2).broadcast_to([128, 8, 128])
    P.tt(k.wbf[b][:], k.wst[b][:], gb, ALU.mult, eng="pool")
    for tb in range(4):
        for dc in range(8):
            P.mm(BK[tb][:, :], k.wbf[b][:, dc, :], k.hT[:, dc, tb * 512:(tb + 1) * 512], start=(dc == 0), stop=(dc == 7))
    return BK


ZQ = "act"


def phase_z(k, l, last):
    P = k.P
    with scope(k):
        wst = P.sb("z_wst", [128, 8, 512], F32)
        wob = P.sb("z_wob", [128, 8, 1024], BF16, blk=512)
        sg = [P.sb("z_sg%d" % i, [128, L], BF16, blk=512) for i in range(2)]
        for nb in range(2):
            P.dma(wst[:], k.wout_d[l, :, :, nb * 512:(nb + 1) * 512])
            P.cp(wob[:, :, nb * 512:(nb + 1) * 512], wst[:], eng="act")
        for gc in range(8):
            pa = proj(k, l, "g%d" % gc, alt=(gc % 2 == 1))
            s = sg[gc % 2]
            for tb in range(4):
                P.act(s[:, tb * 512:(tb + 1) * 512], pa[tb][:, :], AF.Silu)
                mc = mixc(k, gc)[:, tb * 512:(tb + 1) * 512]
                P.tt(mc, mc, s[:, tb * 512:(tb + 1) * 512], ALU.mult)
        xin = [P.sb("z_xin%d" % i, [128, D], F32) for i in range(3)]
        src = k.x_d if l == 0 else k.xscr
        if last:
            ssq = P.sb("f_ssq", [128, NT])
            rs = P.sb("f_rs", [128, NT])
            rstd = P.sb("f_rstd", [128, NT])
            junk = [P.sb("f_junk%d" % i, [128, D], BF16) for i in range(2)]
            gf = P.sb("f_g", [128, D])
            ot = [P.sb("f_o%d" % i, [128, D]) for i in range(2)]
            P.dma(gf[:], k.fing_d[0:1, :].partition_broadcast(128))
        def za(tt):
            xt_ = xin[tt % 3]
            P.dma(xt_[:], src[tt * 128:(tt + 1) * 128, :])
            for nb in range(2):
                ps = k.PB[(tt * 2 + nb) % 4]
                for kc in range(8):
                    P.mm(ps[:, :], mixc(k, kc)[:, tt * 128:(tt + 1) * 128], wob[:, kc, nb * 512:(nb + 1) * 512],
                         start=(kc == 0), stop=(kc == 7))
                xs = xt_[:, nb * 512:(nb + 1) * 512]
                P.tt(xs, xs, ps[:, :], ALU.add)
            if not last:
                P.dma(k.xscr[tt * 128:(tt + 1) * 128, :], xt_[:], eng=ZQ)
            else:
                b = tt % 2
                P.act(junk[b][:], xt_[:], AF.Square, accum_out=ssq[:, tt:tt + 1])
                P.act(rs[:, tt:tt + 1], ssq[:, tt:tt + 1], AF.Sqrt, bias=EPS, scale=1.0 / D)

        def zb(tt):
            if last:
                xt_ = xin[tt % 3]
                b = tt % 2
                P.recip(rstd[:, tt:tt + 1], rs[:, tt:tt + 1])
                P.stt(ot[b][:], xt_[:], rstd[:, tt:tt + 1], gf[:], ALU.mult, ALU.mult)
                P.dma(k.out_d[tt * 128:(tt + 1) * 128, :], ot[b][:], eng=ZQ, is_output=True)
        za(0)
        for tt in range(NT):
            if tt + 1 < NT:
                za(tt + 1)
            zb(tt)


def host_inputs(inp, b):
    m = {}
    m["x"] = np.ascontiguousarray(inp["x"][b])
    m["win"] = host_win(inp["w_in"])
    m["normg"] = np.ascontiguousarray(inp["norm_g"].reshape(DEPTH, 8, 128).transpose(0, 2, 1))
    m["wout"] = np.ascontiguousarray(inp["w_out"].reshape(DEPTH, 8, 128, 1024).transpose(0, 2, 1, 3))
    m["fing"] = np.ascontiguousarray(inp["final_norm_g"].reshape(1, D))
    m["ident"] = np.eye(128, dtype=np.float32)
    host_mla(inp, b, m)
    host_gdn(inp, b, m)
    host_rwkv(inp, b, m)
    return m


TWO_PI = 2.0 * np.pi


def host_mla(inp, b, m):
    half = 16
    inv_freq = (10000.0 ** (-np.arange(half, dtype=np.float32) / half)).astype(np.float32)
    invf = np.zeros((32, 1), np.float32)
    invf[:, 0] = np.tile(inv_freq, 2) / np.float32(TWO_PI)
    m["invf"] = invf
    rm = np.zeros((32, 32), np.float32)
    for i in range(16):
        rm[i, i + 16] = -1.0
        rm[i + 16, i] = 1.0
    m["rmT"] = np.ascontiguousarray(rm.T)
    m["pos"] = np.ascontiguousarray(inp["positions"][b].reshape(1, L).astype(np.int32))
    wuq = inp["mla_w_uq"]
    o = np.zeros((DEPTH, 128, 2, 6, 128), np.float32)
    for h in range(6):
        nope = wuq[:, :, h * 96:h * 96 + 64]
        rope = wuq[:, :, h * 96 + 64:h * 96 + 96]
        o[:, :, 0, h, 64:128] = nope[:, 0:128]
        o[:, 0:64, 1, h, 64:128] = nope[:, 128:192]
        o[:, :, 0, h, 0:32] = rope[:, 0:128]
        o[:, 0:64, 1, h, 0:32] = rope[:, 128:192]
    m["wuq"] = o
    gq = np.zeros((DEPTH, 128, 2), np.float32)
    gq[:, :, 0] = inp["mla_q_norm_g"][:, 0:128]
    gq[:, 0:64, 1] = inp["mla_q_norm_g"][:, 128:192]
    m["gq"] = gq
    wukv = inp["mla_w_ukv"]
    wk = np.zeros((DEPTH, 128, 6, 128), np.float32)
    wv = np.zeros((DEPTH, 128, 6, 64), np.float32)
    for h in range(6):
        wk[:, :, h, 64:128] = wukv[:, :, h * 128:h * 128 + 64]
        wv[:, :, h, :] = wukv[:, :, h * 128 + 64:h * 128 + 128]
    m["wuk"] = wk
    m["wuv"] = wv
    m["gkv"] = np.ascontiguousarray(inp["mla_kv_norm_g"].reshape(DEPTH, 128, 1))


def mla_decl(k):
    nc = k.nc

    def din(name, shape, dt=F32):
        return nc.dram_tensor(name, list(shape), dt, kind="ExternalInput").ap()
    k.invf_d = din("invf", [32, 1])
    k.rmT_d = din("rmT", [32, 32])
    k.pos_d = din("pos", [1, L], I32)
    k.wuq_d = din("wuq", [DEPTH, 128, 2, 6, 128])
    k.gq_d = din("gq", [DEPTH, 128, 2])
    k.wuk_d = din("wuk", [DEPTH, 128, 6, 128])
    k.wuv_d = din("wuv", [DEPTH, 128, 6, 64])
    k.gkv_d = din("gkv", [DEPTH, 128, 1])


def latent_norm(k, l, names, nfeat, outs, ones):
    P = k.P
    sq = [P.sb("ln_sq%d" % i, [128, 512]) for i in range(2)]
    rq = [P.sb("ln_rq%d" % i, [128, 512]) for i in range(2)]
    n = len(names)
    for i, nm in enumerate(names):
        pa = proj(k, l, nm)
        for tb in range(4):
            s = sq[tb % 2]
            sk = ""
            if "a" not in sk:
                P.act(s[:], pa[tb][:, :], AF.Square)
            if "c" not in sk:
                P.cp(outs[i][:, tb * 512:(tb + 1) * 512], pa[tb][:, :], eng="dve")
            if "m" not in sk:
                P.mm(k.PB[tb][:, :], ones[:], s[:], start=(i == 0), stop=(i == n - 1))
    c2 = 9
    if c2 < 1:
        return
    for tb in range(4):
        r = rq[tb % 2]
        P.rpow(r[:], k.PB[tb][:, :], -0.5, scale=1.0 / nfeat, bias=EPS)
        for i in range(n):
            o = outs[i][:, tb * 512:(tb + 1) * 512]
            P.tt(o, o, r[:], ALU.mult)


def mla_phase(k, l):
    P = k.P
    SC = 96.0 ** -0.5
    with scope(k):
        cqn0 = P.sb("m_cqn0", [128, L], BF16, blk=512)
        cqn1 = P.sb("m_cqn1", [128, L], BF16, blk=512)
        ckvn = P.sb("m_ckvn", [128, L], BF16, blk=512)
        krope = P.sb("m_krope", [32, L], BF16, blk=512)
        cos2 = P.sb("m_cos2", [32, L], BF16, blk=512)
        sin2 = P.sb("m_sin2", [32, L], BF16, blk=512)
        wq = P.sb("m_wq", [128, 2, 6, 128], BF16)
        wk = P.sb("m_wk", [128, 6, 128], BF16)
        wv = P.sb("m_wv", [128, 6, 64], BF16)
        ones = P.sb("m_ones", [128, 128], F32)
        onesk = P.sb("m_onesk", [128, 128], F32)
        rmT = P.sb("m_rmT", [32, 32], F32)
        P.memset(ones[:], 1.0)
        P.memset(onesk[:], 1.0)
        P.memset(onesk[32:64, :], 0.0)
        P.dma(rmT[:], k.rmT_d[:])
        with scope(k):
            st = P.sb("m_st", [128, 2, 6, 128], F32)
            g = P.sb("m_g", [128, 4], F32)
            P.dma(st[:], k.wuq_d[l])
            P.dma(g[:, 0:2], k.gq_d[l])
            P.dma(g[:, 2:3], k.gkv_d[l])
            for kc in range(2):
                P.ts(wq[:, kc].rearrange("p a b -> p (a b)"), st[:, kc].rearrange("p a b -> p (a b)"),
                     g[:, kc:kc + 1], ALU.mult)
            st2 = P.sb("m_st2", [128, 6, 128], F32)
            P.dma(st2[:], k.wuk_d[l])
            P.ts(wk[:].rearrange("p a b -> p (a b)"), st2[:].rearrange("p a b -> p (a b)"), g[:, 2:3], ALU.mult)
            st3 = P.sb("m_st3", [128, 6, 64], F32)
            P.dma(st3[:], k.wuv_d[l])
            P.ts(wv[:].rearrange("p a b -> p (a b)"), st3[:].rearrange("p a b -> p (a b)"), g[:, 2:3], ALU.mult)
        if k.cut < 1:
            return
        with scope(k):
            latent_norm(k, l, ["cq0", "cq1"], 192, [cqn0, cqn1], ones)
            latent_norm(k, l, ["ckv"], 128, [ckvn], ones)
        if k.cut < 2:
            return
        with scope(k):
            invf = P.sb("m_invf", [32, 1], F32)
            P.dma(invf[:], k.invf_d[:])
            pa = proj(k, l, "kr")
            for tb in range(4):
                sl = slice(tb * 512, (tb + 1) * 512)
                b_ = tb % 2
                posi = P.sb("m_posi%d" % tb, [32, 512], I32)
                y = P.sb("m_y%d" % tb, [32, 512], F32)
                yi = P.sb("m_yi%d" % tb, [32, 512], I32)
                fr = P.sb("m_fr%d" % tb, [32, 512], F32)
                kr = P.sb("m_kr%d" % tb, [32, 512], F32)
                t1 = P.sb("m_t1%d" % tb, [32, 512], F32)
                t2 = P.sb("m_t2%d" % tb, [32, 512], F32)
                P.dma(posi[:], k.pos_d[0:1, sl].partition_broadcast(32))
                P.cp(y[:], posi[:])
                P.ts(y[:], y[:], invf[:, 0:1], ALU.mult)
                for off, dst in ((0.0, sin2), (0.25, cos2)):
                    if off != 0.0:
                        P.ts(y[:], y[:], off, ALU.add)
                    P.cp(yi[:], y[:])
                    P.cp(fr[:], yi[:])
                    P.tt(fr[:], y[:], fr[:], ALU.subtract)
                    P.act(dst[:, sl], fr[:], AF.Sin, scale=TWO_PI * (1.0 - 1e-6))
                P.cp(kr[:], pa[tb][0:32, :], eng="act")
                P.mm(k.PB[3][0:32, :], rmT[:], kr[:])
                P.tt(t1[:], kr[:], cos2[:, sl], ALU.mult)
                P.tt(t2[:], k.PB[3][0:32, :], sin2[:, sl], ALU.mult)
                P.tt(krope[:, sl], t1[:], t2[:], ALU.add)
        if k.cut < 3:
            return
        kT = [P.sb("m_kT%d" % i, [128, L], BF16, blk=512) for i in range(2)]
        qT = [P.sb("m_qT%d" % i, [128, L], BF16, blk=512) for i in range(2)]
        Vh = [P.sb("m_V%d" % i, [128, NT, 96], BF16) for i in range(2)]
        pT = [P.sb("m_pT%d" % i, [128, 1024], BF16, blk=512) for i in range(2)]
        sq = [P.sb("m_sq%d" % i, [128, 512], F32) for i in range(2)]
        qr = [P.sb("m_qr%d" % i, [32, 512], F32) for i in range(2)]
        t1 = P.sb("m_t1b", [32, 512], F32)
        t2 = P.sb("m_t2b", [32, 512], F32)
        mrow = P.sb("m_mrow", [64, 512], F32)
        km4 = P.sb("m_km4", [128, 4], F32)
        kmax2 = P.sb("m_kmax2", [128, 1], F32)
        rden = [P.sb("m_rden%d" % i, [64, 512], F32) for i in range(2)]
        for i in range(2):
            P.memset(kT[i][32:64, :], 0.0)
            P.memset(kT[i][32:33, :], 1.0)
            P.memset(qT[i][32:64, :], 0.0)
            P.memset(Vh[i][:, :, 64:96], 1.0)
        kmx = [P.sb("m_kmx%d" % i, [128, 1], F32) for i in range(2)]

        def prep(h):
            kt_, qt_, vh_ = kT[h % 2], qT[h % 2], Vh[h % 2]
            kmax2_ = kmx[h % 2]
            P.cp(kt_[0:32, :], krope[:], eng="pool")
            for tb in range(4):
                sl = slice(tb * 512, (tb + 1) * 512)
                s = sq[tb % 2]
                P.mm(k.PB[2][:, :], wk[:, h, :], ckvn[:, sl])
                yield
                P.cp(kt_[64:128, sl], k.PB[2][64:128, :], eng="dve")
                yield
                P.tt(s[:], kt_[:, sl], kt_[:, sl], ALU.mult, eng="pool")
                yield
                yield
                P.mm(k.PB[3][:, :], onesk[:], s[:])
                yield
                P.red(km4[:, tb:tb + 1], k.PB[3][:, :], ALU.max)
                yield
            P.red(kmax2_[:], km4[:], ALU.max)
            for half in range(2):
                for j in range(8):
                    tt = half * 8 + j
                    P.mm(k.PB[2][:, j * 64:(j + 1) * 64], ckvn[:, tt * 128:(tt + 1) * 128], wv[:, h, :])
                yield
                P.cp(vh_[:, half * 8:(half + 1) * 8, 0:64], k.PB[2][:, :].rearrange("p (a b) -> p a b", a=8), eng="dve")
                yield
            for tb in range(4):
                sl = slice(tb * 512, (tb + 1) * 512)
                s = sq[tb % 2]
                q_ = qr[tb % 2]
                P.mm(k.PB[2][:, :], wq[:, 0, h, :], cqn0[:, sl], start=True, stop=False)
                P.mm(k.PB[2][:, :], wq[:, 1, h, :], cqn1[:, sl], start=False, stop=True)
                yield
                P.cp(qt_[64:128, sl], k.PB[2][64:128, :], eng="dve")
                P.cp(q_[:], k.PB[2][0:32, :], eng="dve")
                yield
                P.act(s[:], k.PB[2][:, :], AF.Square)
                yield
                P.mm(k.PB[3][:, :], ones[:], s[:])
                yield
                P.act(mrow[32:33, :], k.PB[3][32:33, :], AF.Sqrt, scale=kmax2_[32:33, 0:1])
                yield
                P.ts(qt_[32:33, sl], mrow[32:33, :], -1.0, ALU.mult)
                P.mm(k.PB[2][0:32, :], rmT[:], q_[:])
                P.tt(t1[:], q_[:], cos2[:, sl], ALU.mult, eng="pool")
                yield
                P.tt(t2[:], k.PB[2][0:32, :], sin2[:, sl], ALU.mult)
                yield
                P.tt(qt_[0:32, sl], t1[:], t2[:], ALU.add)
                yield

        def attn(h):
            kt_, qt_, vh_ = kT[h % 2], qT[h % 2], Vh[h % 2]
            pti = 0
            for qb in range(4):
                qs = slice(qb * 512, (qb + 1) * 512)
                O = k.PB[qb % 2]

                def s_pair(m_):
                    for j in range(2):
                        kt = 2 * m_ + j
                        P.mm(k.PAW[m_ % 2][:, j * 512:(j + 1) * 512], kt_[:, kt * 128:(kt + 1) * 128], qt_[:, qs])
                s_pair(0)
                s_pair(1)
                for m_ in range(NT // 2):
                    p_ = pT[pti % 2]
                    pti += 1
                    P.act(p_[:], k.PAW[m_ % 2][:, :], AF.Exp, scale=SC)
                    if m_ + 2 < NT // 2:
                        s_pair(m_ + 2)
                    for j in range(2):
                        kt = 2 * m_ + j
                        P.mm(O[0:96, :], vh_[:, kt, :], p_[:, j * 512:(j + 1) * 512], start=(kt == 0), stop=(kt == NT - 1))
                    yield
                rd = rden[qb % 2]
                P.rpow(rd[0:32, :], O[64:96, :], -1.0)
                P.rpow(rd[32:64, :], O[64:96, :], -1.0)
                ob = (h % 2) * 64
                P.tt(mixc(k, 3 + h // 2)[ob:ob + 64, qs], O[0:64, :], rd[:], ALU.mult)
                yield

        for _ in prep(0):
            pass
        mode = "il"
        for h in range(6):
            gens = [attn(h)]
            if h + 1 < 6:
                if mode == "il":
                    gens.append(prep(h + 1))
                elif mode == "seq":
                    run_interleaved(gens)
                    gens = [prep(h + 1)]
                elif mode == "noprep":
                    pass
            if mode == "noprep" and h > 0:
                gens = [attn(0)]
            run_interleaved(gens)


NCK = L // 64
NEG = -30000.0


def host_gdn(inp, b, m):
    cw = inp["gdn_conv"]
    o = np.zeros((DEPTH, 128, 9, 5), np.float32)
    for part in range(3):
        for p in range(3):
            o[:, :, part * 3 + p, :] = cw[:, :, part * 384 + p * 128: part * 384 + (p + 1) * 128].transpose(0, 2, 1)
    m["gconv"] = o
    gb = np.zeros((DEPTH, 128, 2), np.float32)
    for d in range(2):
        gb[:, d * 32:d * 32 + 6, 0] = inp["gdn_dt_bias"][:, d, :]
        gb[:, d * 32:d * 32 + 6, 1] = inp["gdn_a_log"][:, d, :]
    m["ggb"] = gb
    m["gng"] = np.ascontiguousarray(np.tile(inp["gdn_norm_g"], (1, 2)).reshape(DEPTH, 128, 1))
    sel = np.zeros((64, 6, 128), np.float32)
    for d in range(2):
        for p in range(3):
            sel[d * 32 + 2 * p, d * 3 + p, 0:64] = 1.0
            sel[d * 32 + 2 * p + 1, d * 3 + p, 64:128] = 1.0
    m["gsel"] = sel
    j = np.arange(64)[:, None]
    i = np.arange(64)[None, :]
    nm = np.zeros((128, 2, 64), np.float32)
    nm[:, 0, :] = np.tile(np.where(i > j, 0.0, NEG), (2, 1))
    nm[:, 1, :] = np.tile(np.where(i < j, 0.0, NEG), (2, 1))
    m["gnegm"] = nm
    m["gid2"] = np.ascontiguousarray(np.tile(np.eye(64, dtype=np.float32), (2, 1)))


def gdn_decl(k):
    nc = k.nc

    def din(name, shape, dt=F32):
        return nc.dram_tensor(name, list(shape), dt, kind="ExternalInput").ap()
    k.gconv_d = din("gconv", [DEPTH, 128, 9, 5])
    k.ggb_d = din("ggb", [DEPTH, 128, 2])
    k.gng_d = din("gng", [DEPTH, 128, 1])
    k.gsel_d = din("gsel", [64, 6, 128])
    k.gnegm_d = din("gnegm", [128, 2, 64])
    k.gid2_d = din("gid2", [128, 64])


def bc3(ap2, n):
    return ap2.unsqueeze(2).broadcast_to([ap2.shape[0], ap2.shape[1], n])


def bcm(ap2, n):
    return ap2.unsqueeze(1).broadcast_to([ap2.shape[0], n, ap2.shape[1]])


HS = (slice(0, 64), slice(64, 128))


def v3(ps, n=8):
    return ps[:, 0:n * 64].rearrange("p (a b) -> p a b", a=n)


def mm2(P, ps, c, lhsT, rhs, **kw):
    for hs in HS:
        P.mm(ps[hs, c * 64:(c + 1) * 64], lhsT[hs], rhs[hs], **kw)


def tr2(P, ps, c, in_, ident):
    for hs in HS:
        P.mm(ps[hs, c * 64:(c + 1) * 64], in_[hs], ident[hs, hs])


def neumann2(k, Nn, Rm, tmp, bank, id2, n8=8):
    P = k.P
    idb = bcm(id2[:, :], n8)
    tA, tB, tC, tD = tmp
    pa_, pb_, pc_ = bank
    for c in range(n8):
        tr2(P, pa_, c, Nn[:, c, :], k.identb)
    P.cp(tA[:], v3(pa_, n8), eng="act")
    P.tt(Rm[:], Nn[:], idb, ALU.add)
    yield
    cur, curT = Nn, tA
    targets = [(tB, tC), (tD, tA)]
    for lvl in range(1, 7):
        nxt, nxtT = targets[(lvl - 1) % 2]
        for c in range(n8):
            if lvl <= 5:
                mm2(P, pb_, c, cur[:, c, :], curT[:, c, :])
                if lvl < 5:
                    mm2(P, pa_, c, curT[:, c, :], cur[:, c, :])
            if lvl >= 2:
                mm2(P, pc_, c, curT[:, c, :], Rm[:, c, :])
        if lvl <= 5:
            P.cp(nxtT[:], v3(pb_, n8), eng="act")
            if lvl < 5:
                P.cp(nxt[:], v3(pa_, n8), eng="act")
        if lvl >= 2:
            P.tt(Rm[:], Rm[:], v3(pc_, n8), ALU.add)
        yield
        cur, curT = nxt, nxtT


def run_interleaved(gens):
    gens = list(gens)
    while gens:
        for g in list(gens):
            try:
                next(g)
            except StopIteration:
                gens.remove(g)


def norm_pipe(k, n, src_fn, sqb, rnb, banks, bones, power, scale, bias, post_fn):
    P = k.P

    def pre(i):
        P.act(sqb[i % 2][:], src_fn(i), AF.Square)
        P.mm(banks[i % 2][:, :], bones[:], sqb[i % 2][:])

    def post(i):
        P.rpow(rnb[i % 2][:], banks[i % 2][:, :], power, scale=scale, bias=bias)
        post_fn(i, rnb[i % 2])
    pre(0)
    for i in range(n):
        if i + 1 < n:
            pre(i + 1)
        post(i)


def run_pipelined(chains, depth=2):
    active = []
    nxt = [0] * len(chains)

    def start(ci):
        if nxt[ci] < len(chains[ci]):
            active.append((ci, chains[ci][nxt[ci]](nxt[ci] % depth)))
            nxt[ci] += 1
    for ci in range(len(chains)):
        for _ in range(depth):
            start(ci)
    while active:
        for item in list(active):
            try:
                next(item[1])
            except StopIteration:
                active.remove(item)
                start(item[0])


def gdn_phase(k, l):
    P = k.P
    with scope(k):
        GC = P.sb("g_GC", [64, L], F32, blk=512)
        GP = [P.sb("g_GP%d" % p, [128, NCK, 4], F32) for p in range(3)]
        NBP = [P.sb("g_NBP%d" % p, [128, NCK, 2], F32) for p in range(3)]
        sel = P.sb("g_sel", [64, 6, 128], F32)
        negm = P.sb("g_negm", [128, 2, 64], F32)
        id2 = P.sb("g_id2", [128, 64], F32)
        cw = P.sb("g_cw", [128, 9, 5], F32)
        ng = P.sb("g_ng", [128, 1], F32)
        bones = P.sb("g_bones", [128, 128], F32)
        P.dma(sel[:], k.gsel_d[:])
        P.dma(negm[:], k.gnegm_d[:])
        P.dma(id2[:], k.gid2_d[:])
        P.dma(cw[:], k.gconv_d[l])
        P.dma(ng[:], k.gng_d[l])
        P.memset(bones[:], 0.0)
        P.memset(bones[0:64, 0:64], 1.0)
        P.memset(bones[64:128, 64:128], 1.0)
        with scope(k):
            GT = P.sb("g_GT", [128, L], F32, blk=512)
            m0 = P.sb("g_m0", [64, L], F32)
            gb = P.sb("g_gb", [128, 2], F32)
            negA = P.sb("g_negA", [128, 1], F32)
            P.dma(gb[:], k.ggb_d[l])
            P.act(negA[:], gb[:, 1:2], AF.Exp)
            P.ts(negA[:], negA[:], -1.0, ALU.mult)
            P.memset(m0[:], 1.0)
            P.memset(m0[:, 0:L:64], 0.0)
            pa = proj(k, l, "gab")
            for tb in range(4):
                sl = slice(tb * 512, (tb + 1) * 512)
                P.act(GT[0:64, sl], pa[tb][0:64, :], AF.Exp, bias=gb[0:64, 0:1])
                P.act(GT[64:128, sl], pa[tb][64:128, :], AF.Sigmoid)
            P.act(GT[0:64, :], GT[0:64, :], AF.Ln, bias=1.0)
            P.ts(GT[0:64, :], GT[0:64, :], negA[0:64, 0:1], ALU.mult)
            P.scan(GC[:, :], m0[:, :], GT[0:64, :], 0.0, ALU.mult, ALU.add)
            gc3 = GC[32:64, :].rearrange("p (a b) -> p a b", b=64)
            P.tt(m0[32:64, :].rearrange("p (a b) -> p a b", b=64), bc3(GC[32:64, 63:L:64], 64), gc3, ALU.subtract)
            P.tt(GC[32:64, :], m0[32:64, :], GT[32:64, :], ALU.add)
            for grp in range(4):
                g8 = slice(grp * 8, (grp + 1) * 8)
                for c in range(8):
                    ck = grp * 8 + c
                    cs = slice(c * 64, (c + 1) * 64)
                    for hs in HS:
                        P.mm(k.PB[0][hs, cs], GC[:, ck * 64:(ck + 1) * 64], k.ident[0:64, 0:64])
                        P.mm(k.PB[1][hs, cs], GT[64:128, ck * 64:(ck + 1) * 64], k.ident[64:128, 64:128])
                n_ = 0
                for p in range(3):
                    for hf, hs in enumerate(HS):
                        h = 2 * p + hf
                        for q, ps in ((0, k.PB[0]), (1, k.PB[1])):
                            src = v3(ps)[hs, :, h:h + 33:32]
                            P.cp(GP[p][hs, g8, 2 * q:2 * q + 2], src, eng=("act" if q else "dve"))
            for p in range(3):
                P.ts(NBP[p][:], GP[p][:, :, 2:4], -1.0, ALU.mult)
        for p in range(3):
            with scope(k):
                Q = P.sb("g_Q", [128, L], BF16, blk=512)
                K_ = P.sb("g_K", [128, L], BF16, blk=512)
                Kt = P.sb("g_Kt", [128, NCK, 64], BF16, blk=512)
                Vt = P.sb("g_Vt", [128, NCK, 64], BF16, blk=512)
                O = P.sb("g_O", [128, L], F32, blk=512)
                P.memset(O[:], 0.0, eng="pool")
                with scope(k):
                    xp = P.sb("g_xp", [128, L + 4], F32)
                    Vf = P.sb("g_Vf", [128, L], BF16, blk=512)
                    cv = P.sb("g_cv", [128, L], F32, blk=512)
                    sq = P.sb("g_sq", [128, 512], F32)
                    rn = P.sb("g_rn", [128, 512], F32)
                    sq2 = P.sb("g_sq2", [128, 512], F32)
                    rn2 = P.sb("g_rn2", [128, 512], F32)
                    P.memset(xp[:, 0:2], 0.0)
                    P.memset(xp[:, L + 2:L + 4], 0.0)
                    for part, nm, dst in ((0, "gq", Q), (1, "gk", K_), (2, "gv", Vf)):
                        pa = proj(k, l, "%s%d" % (nm, p))
                        for tb in range(4):
                            P.cp(xp[:, 2 + tb * 512:2 + (tb + 1) * 512], pa[tb][:, :], eng=("act" if tb % 2 else "dve"))
                        wi = part * 3 + p
                        P.ts(cv[:], xp[:, 0:L], cw[:, wi, 0:1], ALU.mult)
                        for j in range(1, 5):
                            P.stt(cv[:], xp[:, j:j + L], cw[:, wi, j:j + 1], cv[:], ALU.mult, ALU.add)
                        if part == 2:
                            P.act(dst[:], cv[:], AF.Silu)
                        else:
                            P.act(cv[:], cv[:], AF.Silu)
                        if part < 2:
                            def fin(tb, r, dst=dst):
                                P.tt(dst[:, tb * 512:(tb + 1) * 512], cv[:, tb * 512:(tb + 1) * 512], r[:], ALU.mult)
                            norm_pipe(k, 4, lambda tb: cv[:, tb * 512:(tb + 1) * 512], (sq, sq2), (rn, rn2),
                                      (k.PB[2], k.PB[3]), bones, -0.5, 64.0 if part == 0 else 1.0,
                                      64e-6 if part == 0 else 1e-6, fin)
                    for src, dstt in ((K_, Kt), (Vf, Vt)):
                        for grp in range(4):
                            ps = k.PB[grp % 2]
                            for c in range(8):
                                ck = grp * 8 + c
                                tr2(P, ps, c, src[:, ck * 64:(ck + 1) * 64], k.identb)
                            P.cp(dstt[:, grp * 8:(grp + 1) * 8, :], v3(ps), eng=("act" if grp % 2 else "dve"))
                with scope(k):
                    T = dict(GC=GC, GP=GP[p], NBP=NBP[p], sel=sel, negm=negm, id2=id2, Q=Q, K=K_, Kt=Kt, Vt=Vt, O=O)
                    run_pipelined([gdn_chain(k, p, d, T) for d in range(2)], depth=1)
                with scope(k):
                    sqo = [P.sb("g_osq%d" % i, [128, 512], F32) for i in range(2)]
                    rno = [P.sb("g_orn%d" % i, [128, 512], F32) for i in range(2)]

                    def fin_o(tb, r):
                        sl = slice(tb * 512, (tb + 1) * 512)
                        P.tt(r[:], O[:, sl], r[:], ALU.mult)
                        P.ts(mixc(k, p)[:, sl], r[:], ng[:, 0:1], ALU.mult)
                    norm_pipe(k, 4, lambda tb: O[:, tb * 512:(tb + 1) * 512], sqo, rno, (k.PB[2], k.PB[3]), bones,
                              -0.5, 1.0 / 64, EPS, fin_o)


def gdn_chain(k, p, d, T):
    P = k.P
    GC, GP, NBP, sel, negm, id2, Q, K_, Kt, Vt, O = (T[n] for n in ("GC", "GP", "NBP", "sel", "negm", "id2", "Q", "K", "Kt", "Vt", "O"))
    B = k.PA if d == 0 else k.PB
    tag = "g%d_" % d
    names = ("CB", "EI", "QG", "Rm", "U0", "WT", "BW", "KD", "GK", "AcT", "Sg", "Ug", "nA", "nB", "nC", "nD", "Nb", "PTb", "Sgb")
    f32n = ("CB", "EI", "AcT", "Sg")
    NSET = 1
    GG = [{n: P.sb(tag + "%d" % s_ + n, [128, 8, 64], F32 if n in f32n else BF16) for n in names} for s_ in range(NSET)]
    for s_ in range(NSET):
        GG[s_]["gend"] = P.sb(tag + "gend%d" % s_, [128, 8], F32)
        GG[s_]["kds"] = P.sb(tag + "kds%d" % s_, [128, 8], F32)
    gam = P.sb(tag + "gam", [128, NCK], F32)
    shared = {"scan": 0}
    Scar = P.sb(tag + "Scar", [128, 64], F32)
    e_ = 63 if d == 0 else 0
    idb = bcm(id2[:, :], 8)

    def f2(t):
        return t[:].rearrange("p a b -> p (a b)")
    P.memset(Scar[:], 0.0)
    P.act(gam[:], GP[:, :, d], AF.Exp)
    gorder = list(range(4)) if d == 0 else list(range(3, -1, -1))

    def group(gi, grp, G):
        Nb, PTb, Sgb, gend, kds = G["Nb"], G["PTb"], G["Sgb"], G["gend"], G["kds"]
        sl = slice(grp * 512, (grp + 1) * 512)
        g8 = slice(grp * 8, (grp + 1) * 8)
        cj = GP[:, g8, d]
        nb = NBP[:, g8, d]
        CB, EI, QG, Rm, U0, WT, BW, KD, GK, AcT, Sg, Ug = (G[n] for n in names[:12])
        P.mm(B[0][:, :], sel[:, d * 3 + p, :], GC[:, sl])
        P.cp(f2(CB), B[0][:, :], eng="act")
        P.act(f2(EI), f2(CB), AF.Exp)
        P.cp(gend[:], EI[:, :, e_], eng="pool")
        P.tt(f2(QG), f2(EI), Q[:, sl], ALU.mult)
        P.tt(kds[:], CB[:, :, e_], cj, ALU.subtract)
        P.act(kds[:], kds[:], AF.Exp)
        P.tt(GK[:], Kt[:, g8, :], bc3(gam[:, g8], 64), ALU.mult, eng="pool")
        P.tt(KD[:], Kt[:, g8, :], bc3(kds[:], 64), ALU.mult, eng="pool")
        P.tt(CB[:], CB[:], bc3(cj, 64), ALU.subtract)
        P.tt(CB[:], CB[:], bcm(negm[:, d, :], 8), ALU.add)
        P.act(f2(CB), f2(CB), AF.Exp)
        yield
        for c in range(8):
            cs = slice((grp * 8 + c) * 64, (grp * 8 + c + 1) * 64)
            mm2(P, B[0], c, K_[:, cs], Q[:, cs])
            mm2(P, B[1], c, K_[:, cs], K_[:, cs])
        P.tt(EI[:], CB[:], idb, ALU.add)
        P.tt(PTb[:], EI[:], v3(B[0]), ALU.mult)
        P.tt(CB[:], CB[:], v3(B[1]), ALU.mult)
        P.tt(Nb[:], CB[:], bc3(nb, 64), ALU.mult)
        yield
        for _ in neumann2(k, Nb, Rm, (G["nA"], G["nB"], G["nC"], G["nD"]), (B[0], B[1], B[2]), id2):
            yield
        for c in range(8):
            mm2(P, B[0], c, Rm[:, c, :], Vt[:, grp * 8 + c, :])
            mm2(P, B[1], c, GK[:, c, :], Rm[:, c, :])
            mm2(P, B[2], c, Rm[:, c, :], GK[:, c, :])
        P.tt(U0[:], v3(B[0]), bc3(nb, 64), ALU.mult)
        P.ts(f2(U0), f2(U0), -1.0, ALU.mult, eng="pool")
        P.cp(WT[:], v3(B[1]), eng="act")
        P.tt(BW[:], v3(B[2]), bc3(nb, 64), ALU.mult)
        yield
        for c in range(8):
            mm2(P, B[3], c, BW[:, c, :], KD[:, c, :])
        P.tt(AcT[:], idb, bc3(gend[:], 64), ALU.mult, eng="pool")
        P.tt(AcT[:], AcT[:], v3(B[3]), ALU.add)
        yield
        while shared["scan"] != gi:
            yield
        corder = range(8) if d == 0 else range(7, -1, -1)
        prev = Scar[:]
        for n, c in enumerate(corder):
            P.cp(Sg[:, c, :], prev, eng="pool") if n == 0 else None
            ps = B[2 + n % 2]
            mm2(P, ps, 0, AcT[:, c, :], Sg[:, c, :], start=True, stop=False)
            mm2(P, ps, 0, KD[:, c, :], U0[:, c, :], start=False, stop=True)
            last = (n == 7)
            dst = Scar[:] if last else Sg[:, corder[n + 1], :]
            P.cp(dst, ps[:, 0:64], eng="act")
            yield
        shared["scan"] = gi + 1
        P.cp(Sgb[:], Sg[:], eng="pool")
        for c in range(8):
            mm2(P, B[0], c, WT[:, c, :], Sgb[:, c, :])
        P.tt(CB[:], v3(B[0]), bc3(nb, 64), ALU.mult)
        P.tt(Ug[:], CB[:], U0[:], ALU.add)
        yield
        for c in range(8):
            mm2(P, B[1], c, Sgb[:, c, :], QG[:, c, :], start=True, stop=False)
            mm2(P, B[1], c, Ug[:, c, :], PTb[:, c, :], start=False, stop=True)
        P.tt(O[:, sl], O[:, sl], B[1][:, :], ALU.add)
        yield
    return [(lambda slot, gi=gi, grp=grp: group(gi, grp, GG[slot])) for gi, grp in enumerate(gorder)]


RW_EPS = 64e-5
DEC = float(np.exp(-0.5))
GS = 8
NG = NCK // GS


def host_rwkv(inp, b, m):
    mu = inp["rwkv_mu"]
    o = np.zeros((DEPTH, 128, 8, 2), np.float32)
    for part in range(3):
        for p in range(2):
            o[:, :, part * 2 + p, :] = mu[:, :, part * 256 + p * 128: part * 256 + (p + 1) * 128].transpose(0, 2, 1)
    o[:, 0:64, 6, :] = mu[:, :, 768:832].transpose(0, 2, 1)
    o[:, 0:64, 7, :] = mu[:, :, 832:896].transpose(0, 2, 1)
    m["rmu"] = o

    def pp(a):
        if a.ndim == 2:
            return np.ascontiguousarray(a.reshape(DEPTH, 2, 128).transpose(0, 2, 1))
        return np.ascontiguousarray(a.reshape(DEPTH, 2, 2, 128).transpose(0, 3, 1, 2))
    pv = np.zeros((DEPTH, 128, 7, 2), np.float32)
    pv[:, :, 0:2, :] = pp(inp["rwkv_w0"])
    pv[:, :, 2:4, :] = pp(inp["rwkv_a0"])
    pv[:, :, 4, :] = pp(inp["rwkv_k_k"])
    pv[:, :, 5, :] = pp(inp["rwkv_k_a"])
    pv[:, :, 6, :] = pp(inp["rwkv_r_k"].reshape(DEPTH, 256))
    m["rpv"] = pv
    ln = np.zeros((DEPTH, 128, 2, 2), np.float32)
    ln[:, :, 0, :] = pp(inp["rwkv_ln_g"])
    ln[:, :, 1, :] = pp(inp["rwkv_ln_b"])
    m["rln"] = ln
    m["rw2"] = np.ascontiguousarray(inp["rwkv_w2"].transpose(0, 2, 1, 3))
    m["ra2"] = np.ascontiguousarray(inp["rwkv_a2"].transpose(0, 2, 1, 3))
    s_ = np.arange(64)[:, None]
    t_ = np.arange(64)[None, :]
    msk = np.zeros((128, 2, 4, 64), np.float32)
    msk[:, 0, 0, :] = np.tile((t_ > s_), (2, 1))
    msk[:, 0, 1, :] = np.tile((t_ >= s_), (2, 1))
    msk[:, 1, 0, :] = np.tile((t_ < s_), (2, 1))
    msk[:, 1, 1, :] = np.tile((t_ <= s_), (2, 1))
    msk[:, :, 2:4, :] = -msk[:, :, 0:2, :]
    m["rmsk"] = msk


def rwkv_decl(k):
    nc = k.nc

    def din(name, shape, dt=F32):
        return nc.dram_tensor(name, list(shape), dt, kind="ExternalInput").ap()
    k.rmu_d = din("rmu", [DEPTH, 128, 8, 2])
    k.rpv_d = din("rpv", [DEPTH, 128, 7, 2])
    k.rln_d = din("rln", [DEPTH, 128, 2, 2])
    k.rw2_d = din("rw2", [DEPTH, 64, 2, 256])
    k.ra2_d = din("ra2", [DEPTH, 64, 2, 256])
    k.rmsk_d = din("rmsk", [128, 2, 4, 64])


def rwkv_phase(k, l):
    P = k.P
    with scope(k):
        mu = P.sb("r_mu", [128, 8, 3], F32)
        pv = P.sb("r_pv", [128, 7, 2], F32)
        omka = P.sb("r_omka", [128, 2], F32)
        hrk = P.sb("r_hrk", [128, 2], F32)
        ln = P.sb("r_ln", [128, 2, 2], F32)
        w2 = P.sb("r_w2", [64, 2, 256], BF16)
        a2 = P.sb("r_a2", [64, 2, 256], BF16)
        msk = P.sb("r_msk", [128, 2, 4, 64], F32)
        id2 = P.sb("r_id2", [128, 64], F32)
        bones = P.sb("r_bones", [128, 128], F32)
        m0 = P.sb("r_m0", [128, GS * 64], F32)
        twd = P.sb("r_twd", [64, L], BF16, blk=512)
        adx = P.sb("r_adx", [64, L], BF16, blk=512)
        sh32 = P.sb("r_sh32", [128, L], F32, blk=512)
        xp = P.sb("r_xp", [128, L + 2], F32)
        P.dma(mu[:, :, 0:2], k.rmu_d[l])
        P.dma(pv[:], k.rpv_d[l])
        P.dma(ln[:], k.rln_d[l])
        with scope(k):
            w2f = P.sb("r_w2f", [64, 2, 256], F32)
            a2f = P.sb("r_a2f", [64, 2, 256], F32)
            P.dma(w2f[:], k.rw2_d[l])
            P.dma(a2f[:], k.ra2_d[l])
            P.cp(w2[:], w2f[:], eng="act")
            P.cp(a2[:], a2f[:], eng="act")
        P.dma(msk[:], k.rmsk_d[:])
        P.dma(id2[:], k.gid2_d[:])
        P.memset(bones[:], 0.0)
        P.memset(bones[0:64, 0:64], 1.0)
        P.memset(bones[64:128, 64:128], 1.0)
        P.memset(m0[:], 1.0)
        P.memset(m0[:, 0:GS * 64:64], 0.0)
        P.memset(xp[:, 0:1], 0.0)
        P.memset(xp[:, L + 1:L + 2], 0.0)
        P.tt(mu[:, :, 2], mu[:, :, 0], mu[:, :, 1], ALU.add)
        P.ts(mu[:, :, 2], mu[:, :, 2], -1.0, ALU.mult, 1.0, ALU.add)
        P.ts(omka[:], pv[:, 5, :], -1.0, ALU.mult, 1.0, ALU.add)
        P.ts(hrk[:], pv[:, 6, :], 0.5, ALU.mult)

        def shifted(name, ci, dst, np_=128, fn=None):
            pa = proj(k, l, name, alt=(ci % 2 == 1))
            for tb in range(4):
                P.cp(xp[0:np_, 1 + tb * 512:1 + (tb + 1) * 512], pa[tb][0:np_, :], eng=("act" if tb % 2 else "dve"))
            t_ = sh32[0:np_, :]
            P.ts(t_, xp[0:np_, 1:L + 1], mu[0:np_, ci, 2:3], ALU.mult)
            P.stt(t_, xp[0:np_, 0:L], mu[0:np_, ci, 0:1], t_, ALU.mult, ALU.add)
            if fn is None:
                P.stt(dst[:], xp[0:np_, 2:L + 2], mu[0:np_, ci, 1:2], t_, ALU.mult, ALU.add)
            else:
                P.stt(t_, xp[0:np_, 2:L + 2], mu[0:np_, ci, 1:2], t_, ALU.mult, ALU.add)
                P.act(dst[:], t_, fn)

        shifted("rwd", 6, twd, 64, AF.Tanh)
        shifted("rad", 7, adx, 64)
        R_ = P.sb("r_R", [128, L], BF16, blk=512)
        KX = P.sb("r_KX", [128, L], BF16, blk=512)
        V_ = P.sb("r_V", [128, L], BF16, blk=512)
        KK = P.sb("r_KK", [128, L], BF16, blk=512)
        Vt = P.sb("r_Vt", [128, NCK, 64], BF16, blk=512)
        KS = P.sb("r_KS", [128, L], F32, blk=512)
        Y = xp[:, 1:L + 1]
        sq = P.sb("r_sq", [128, 512], F32)
        rn = P.sb("r_rn", [128, 512], F32)
        CH = [rwkv_tiles(k, e) for e in range(2)]

        class _V:
            def __init__(s_, t):
                s_.t = t

            def __getitem__(s_, key):
                return s_.t[:].rearrange("p a b -> p (a b)")[key]
        sq2, rn2 = _V(CH[0]["lw"]), _V(CH[0]["a"])
        for p in range(2):
            shifted("rr%d" % p, 0 + p, R_)
            shifted("rk%d" % p, 2 + p, KX)
            shifted("rv%d" % p, 4 + p, V_)
            P.ts(sh32[:], KX[:], pv[:, 4, p:p + 1], ALU.mult)

            def fin_k(tb, r):
                P.tt(KK[:, tb * 512:(tb + 1) * 512], sh32[:, tb * 512:(tb + 1) * 512], r[:], ALU.mult)
            norm_pipe(k, 4, lambda tb: sh32[:, tb * 512:(tb + 1) * 512], (sq, sq2), (rn, rn2), (k.PB[2], k.PB[3]),
                      bones, -0.5, 1.0, 1e-6, fin_k)
            for grp in range(4):
                ps = k.PB[grp % 2]
                for c in range(8):
                    ck = grp * 8 + c
                    tr2(P, ps, c, V_[:, ck * 64:(ck + 1) * 64], k.identb)
                P.cp(Vt[:, grp * 8:(grp + 1) * 8, :], v3(ps), eng=("act" if grp % 2 else "dve"))
            P.memset(xp[:, 1:L + 1], 0.0, eng="pool")
            P.memset(KS[:], 0.0, eng="pool")
            T = dict(pv=pv, omka=omka, w2=w2, a2=a2, msk=msk, id2=id2, m0=m0, twd=twd, adx=adx,
                     R=R_, KX=KX, KK=KK, Vt=Vt, KS=KS, Y=Y)
            run_interleaved([rwkv_chain(k, p, e, T, CH[e]) for e in range(2)])
            for tb in range(4):
                sl = slice(tb * 512, (tb + 1) * 512)
                P.mm(k.PB[tb % 2][:, :], bones[:], Y[:, sl])
                P.stt(Y[:, sl], k.PB[tb % 2][:, :], -1.0 / 64, Y[:, sl], ALU.mult, ALU.add)

            def fin_y(tb, r):
                sl = slice(tb * 512, (tb + 1) * 512)
                P.tt(Y[:, sl], Y[:, sl], r[:], ALU.mult)
                P.ts(Y[:, sl], Y[:, sl], ln[:, 0, p:p + 1], ALU.mult, ln[:, 1, p:p + 1], ALU.add)
            norm_pipe(k, 4, lambda tb: Y[:, tb * 512:(tb + 1) * 512], (sq, sq2), (rn, rn2), (k.PB[2], k.PB[3]),
                      bones, -0.5, 1.0 / 64, RW_EPS, fin_y)
            for tb in range(4):
                sl = slice(tb * 512, (tb + 1) * 512)
                s_ = (sq, sq2)[tb % 2]
                r_ = (rn, rn2)[tb % 2]
                P.tt(s_[:], R_[:, sl], KS[:, sl], ALU.mult)
                P.ts(s_[:], s_[:], hrk[:, p:p + 1], ALU.mult, eng="pool")
                P.mm(k.PB[tb % 2][:, :], bones[:], s_[:])
                P.tt(r_[:], k.PB[tb % 2][:, :], V_[:, sl], ALU.mult)
                P.tt(mixc(k, 6 + p)[:, sl], Y[:, sl], r_[:], ALU.add)


RW_F32 = ("lw", "a", "km", "b", "cl", "e1", "e2", "dend", "AcT", "Tg")
RW_BF16 = ("kap", "rt", "kt_", "bt_", "ke", "be", "kapT", "keT", "nbeT", "N", "Akv", "Brk", "nBrb", "Rm",
           "nA", "nB", "nC", "nD", "X0", "P0", "WkT", "Wk", "Tgb", "Pg")


def rwkv_tiles(k, e):
    P = k.P
    G = {n: P.sb("r%d_%s" % (e, n), [128, GS, 64], F32) for n in RW_F32}
    for n in RW_BF16:
        G[n] = P.sb("r%d_%s" % (e, n), [128, GS, 64], BF16)
    G["gC"] = P.sb("r%d_gC" % e, [128, GS], F32)
    G["Tcar"] = P.sb("r%d_Tcar" % e, [128, 64], F32)
    return G


def rwkv_chain(k, p, e, T, G):
    P = k.P
    pv, omka, w2, a2, msk, id2, m0, twd, adx, R_, KX, KK, Vt, KS, Y = (T[n] for n in (
        "pv", "omka", "w2", "a2", "msk", "id2", "m0", "twd", "adx", "R", "KX", "KK", "Vt", "KS", "Y"))
    B = k.PA if e == 0 else k.PB
    W = GS * 64
    e_ = 63 if e == 0 else 0
    idb = bcm(id2[:, :], GS)
    gC, Tcar = G["gC"], G["Tcar"]

    def f2(t):
        return t[:].rearrange("p a b -> p (a b)")

    def w3(ps):
        return v3(ps, GS)
    P.memset(Tcar[:], 0.0)
    gorder = range(NG) if e == 0 else range(NG - 1, -1, -1)
    pc = slice(p * 128, (p + 1) * 128)
    for grp in gorder:
        sl = slice(grp * W, (grp + 1) * W)
        c0 = grp * GS
        P.mm(B[0][:, 0:W], w2[:, e, pc], twd[:, sl])
        P.mm(B[1][:, 0:W], a2[:, e, pc], adx[:, sl])
        P.act(f2(G["lw"]), B[0][:, 0:W], AF.Sigmoid, bias=pv[:, 0 + e, p:p + 1])
        P.ts(f2(G["lw"]), f2(G["lw"]), -DEC, ALU.mult, eng="pool")
        P.act(f2(G["a"]), B[1][:, 0:W], AF.Sigmoid, bias=pv[:, 2 + e, p:p + 1])
        P.ts(f2(G["km"]), f2(G["a"]), pv[:, 5, p:p + 1], ALU.mult, omka[:, p:p + 1], ALU.add)
        P.tt(f2(G["km"]), f2(G["km"]), KX[:, sl], ALU.mult)
        P.tt(f2(G["b"]), f2(G["a"]), KK[:, sl], ALU.mult, eng="pool")
        P.tt(KS[:, sl], KS[:, sl], f2(G["km"]), ALU.add, eng="pool")
        P.scan(f2(G["cl"]), m0[:], f2(G["lw"]), 0.0, ALU.mult, ALU.add)
        if e == 1:
            P.tt(G["e1"][:], bc3(G["cl"][:, :, 63], 64), G["cl"][:], ALU.subtract)
            P.tt(G["cl"][:], G["e1"][:], G["lw"][:], ALU.add)
        yield
        P.act(G["e1"][:], G["cl"][:], AF.Exp)
        P.act(G["e2"][:], G["cl"][:], AF.Exp, scale=-1.0)
        P.tt(f2(G["rt"]), f2(G["e1"]), R_[:, sl], ALU.mult)
        P.tt(G["kt_"][:], G["e2"][:], G["km"][:], ALU.mult, eng="pool")
        P.tt(G["bt_"][:], G["e2"][:], G["b"][:], ALU.mult)
        P.tt(G["dend"][:], G["cl"][:], G["lw"][:], ALU.subtract, eng="pool")
        P.act(G["dend"][:], G["dend"][:], AF.Exp)
        P.tt(f2(G["kap"]), f2(G["dend"]), KK[:, sl], ALU.mult)
        P.cp(gC[:], G["e1"][:, :, e_], eng="pool")
        P.tt(G["dend"][:], bc3(G["cl"][:, :, e_], 64), G["cl"][:], ALU.subtract, eng="pool")
        P.act(G["dend"][:], G["dend"][:], AF.Exp)
        P.tt(G["ke"][:], G["dend"][:], G["km"][:], ALU.mult)
        P.tt(G["be"][:], G["dend"][:], G["b"][:], ALU.mult, eng="pool")
        yield
        for src, dst, sc in ((G["kap"], G["kapT"], 1.0), (G["ke"], G["keT"], 1.0), (G["be"], G["nbeT"], -1.0)):
            ps = B[0] if sc == 1.0 and src is G["kap"] else (B[1] if sc == 1.0 else B[2])
            for c in range(GS):
                tr2(P, ps, c, src[:, c, :], k.identb)
            if sc == 1.0:
                P.cp(dst[:], w3(ps), eng="act")
            else:
                P.ts(dst[:], w3(ps), -1.0, ALU.mult)
        yield
        for c in range(GS):
            mm2(P, B[0], c, G["bt_"][:, c, :], G["kap"][:, c, :])
            mm2(P, B[1], c, G["kt_"][:, c, :], G["kap"][:, c, :])
            mm2(P, B[2], c, G["kt_"][:, c, :], G["rt"][:, c, :])
            mm2(P, B[3], c, G["bt_"][:, c, :], G["rt"][:, c, :])
        ms = bcm(msk[:, e, 0, :], GS)
        mi = bcm(msk[:, e, 1, :], GS)
        nms = bcm(msk[:, e, 2, :], GS)
        nmi = bcm(msk[:, e, 3, :], GS)
        P.tt(G["N"][:], w3(B[0]), nms, ALU.mult)
        P.tt(G["Akv"][:], w3(B[1]), ms, ALU.mult)
        P.tt(G["Brk"][:], w3(B[2]), mi, ALU.mult)
        P.tt(G["nBrb"][:], w3(B[3]), nmi, ALU.mult)
        yield
        for _ in neumann2(k, G["N"], G["Rm"], (G["nA"], G["nB"], G["nC"], G["nD"]), (B[0], B[1], B[2]), id2, GS):
            yield
        for c in range(GS):
            mm2(P, B[0], c, G["Akv"][:, c, :], Vt[:, c0 + c, :])
        P.cp(G["X0"][:], w3(B[0]), eng="act")
        yield
        for c in range(GS):
            mm2(P, B[0], c, G["Rm"][:, c, :], G["X0"][:, c, :])
            mm2(P, B[1], c, G["kapT"][:, c, :], G["Rm"][:, c, :])
            mm2(P, B[2], c, G["Rm"][:, c, :], G["kapT"][:, c, :])
        P.cp(G["P0"][:], w3(B[0]), eng="act")
        P.cp(G["WkT"][:], w3(B[1]), eng="dve")
        P.cp(G["Wk"][:], w3(B[2]), eng="act")
        yield
        for c in range(GS):
            mm2(P, B[3], c, G["Wk"][:, c, :], G["nbeT"][:, c, :])
        P.tt(G["AcT"][:], idb, bc3(gC[:], 64), ALU.mult, eng="pool")
        P.tt(G["AcT"][:], G["AcT"][:], w3(B[3]), ALU.add)
        yield
        corder = list(range(GS)) if e == 0 else list(range(GS - 1, -1, -1))
        Tg = G["Tg"]
        for n, c in enumerate(corder):
            if n == 0:
                P.cp(Tg[:, c, :], Tcar[:], eng="pool")
            ps = B[2 + n % 2]
            mm2(P, ps, 0, G["AcT"][:, c, :], Tg[:, c, :], start=True, stop=False)
            mm2(P, ps, 0, G["keT"][:, c, :], Vt[:, c0 + c, :], start=False, stop=False)
            mm2(P, ps, 0, G["nbeT"][:, c, :], G["P0"][:, c, :], start=False, stop=True)
            dst = Tcar[:] if n == GS - 1 else Tg[:, corder[n + 1], :]
            P.cp(dst, ps[:, 0:64], eng="act")
            yield
        P.cp(G["Tgb"][:], Tg[:], eng="pool")
        for c in range(GS):
            mm2(P, B[0], c, G["WkT"][:, c, :], G["Tgb"][:, c, :])
        P.tt(G["Pg"][:], w3(B[0]), G["P0"][:], ALU.add)
        yield
        for c in range(GS):
            mm2(P, B[1], c, Vt[:, c0 + c, :], G["Brk"][:, c, :], start=True, stop=False)
            mm2(P, B[1], c, G["Tgb"][:, c, :], G["rt"][:, c, :], start=False, stop=False)
            mm2(P, B[1], c, G["Pg"][:, c, :], G["nBrb"][:, c, :], start=False, stop=True)
        P.tt(Y[:, sl], Y[:, sl], B[1][:, 0:W], ALU.add)
        yield


_CACHE = {}


def kernel(**inputs):
    inp = {k_: np.asarray(v) for k_, v in inputs.items()}
    if "k" not in _CACHE:
        _CACHE["k"] = build()
    k = _CACHE["k"]
    B = inp["x"].shape[0]
    base = host_inputs(inp, 0)
    in_maps = []
    for b in range(B):
        m = dict(base)
        m["x"] = np.ascontiguousarray(inp["x"][b], dtype=np.float32)
        m["pos"] = np.ascontiguousarray(inp["positions"][b].reshape(1, L).astype(np.int32))
        in_maps.append(m)
    res = run_bass_kernel_spmd(k.nc, in_maps, core_ids=list(range(B)))
    return np.stack([np.asarray(r["out"], dtype=np.float32) for r in res.results], axis=0)
```

```python
import numpy as np
import concourse.bass as bass
import concourse.mybir as mybir
from concourse.bass_utils import run_bass_kernel_spmd
from contextlib import ExitStack

F32 = mybir.dt.float32
BF16 = mybir.dt.bfloat16
I32 = mybir.dt.int32
AF = mybir.ActivationFunctionType
ALU = mybir.AluOpType
AX = mybir.AxisListType
DTSIZE = {F32: 4, BF16: 2, I32: 4}


class _Op:
    __slots__ = ("eng", "emit", "deps", "idx", "needed", "sigval", "dsem", "dval", "isdma")

    def __init__(self, eng, emit, isdma=False):
        self.eng = eng
        self.emit = emit
        self.deps = []
        self.idx = -1
        self.needed = False
        self.sigval = 0
        self.dsem = None
        self.dval = 0
        self.isdma = isdma


class _Blk:
    __slots__ = ("w", "r")

    def __init__(self):
        self.w = None
        self.r = {}


class Prog:
    ENGS = ("pe", "dve", "act", "pool", "sp")
    NDMA = 48
    NHW = 32

    def __init__(self, nc, stack):
        self.nc = nc
        self.stack = stack
        self.ops = {e: [] for e in self.ENGS}
        self.track = {}
        self.seen = {e: {} for e in self.ENGS}
        self.seen_dma = {e: set() for e in self.ENGS}
        self.dma_last = [None] * self.NDMA
        self.dma_uses = [0] * self.NDMA
        self.dma_rr = 0
        self.dma_rr_sw = 0
        self.ndma_ops = 0
        self.untracked = set()
        self.out_dmas = []
        self.dma_pending = []
        self.last_compute = {}

    def sb(self, name, shape, dtype=F32, blk=None):
        self.uid = getattr(self, "uid", 0) + 1
        name = "s%d_%s" % (self.uid, name)
        t = self.stack.enter_context(self.nc.sbuf_tensor(name, list(shape), dtype))
        self._register(name, shape, dtype, blk)
        return t

    def ps(self, name, shape=(128, 512), dtype=F32, blk=None):
        self.uid = getattr(self, "uid", 0) + 1
        name = "p%d_%s" % (self.uid, name)
        t = self.stack.enter_context(self.nc.psum_tensor(name, list(shape), dtype))
        self._register(name, shape, dtype, blk)
        return t

    def _register(self, name, shape, dtype, blk):
        row = int(np.prod(shape[1:])) * DTSIZE[dtype]
        bb = row if blk is None else blk * DTSIZE[dtype]
        nb = (row + bb - 1) // bb
        self.track[name] = (bb, row, [_Blk() for _ in range(nb)])

    def dram_track(self, name, total_bytes, blk_bytes):
        nb = (total_bytes + blk_bytes - 1) // blk_bytes
        self.track[name] = (blk_bytes, -1, [_Blk() for _ in range(nb)])

    def _blocks(self, ap):
        name = ap.tensor.name
        if name not in self.track:
            return ()
        bb, row, blks = self.track[name]
        if len(blks) == 1:
            return blks
        ds = DTSIZE[ap.dtype]
        pat = ap.ap
        if row < 0:
            lo = hi = ap.offset
            for step, cnt in pat:
                ext = step * (cnt - 1)
                if ext < 0:
                    lo += ext
                else:
                    hi += ext
            return blks[(lo * ds) // bb:(hi * ds) // bb + 1]
        rowel = row // ds
        foff = ap.offset % rowel
        lo = hi = foff
        for step, cnt in pat[1:]:
            ext = step * (cnt - 1)
            if ext < 0:
                lo += ext
            else:
                hi += ext
        b0 = (lo * ds) // bb
        b1 = (hi * ds) // bb
        return blks[b0:b1 + 1]

    def _dep(self, x, y):
        if y is None or y is x:
            return
        e = x.eng
        if y.isdma:
            if id(y) in self.seen_dma[e]:
                return
            self.seen_dma[e].add(id(y))
            x.deps.append(y)
            return
        if y.eng == "pe" and e == "pe":
            return
        if y.idx <= self.seen[e].get(y.eng, -1):
            return
        self.seen[e][y.eng] = y.idx
        y.needed = True
        x.deps.append(y)

    def add(self, eng, emit, reads=(), writes=(), isdma=False):
        x = _Op(eng, emit, isdma)
        x.idx = len(self.ops[eng])
        rb = []
        for ap in reads:
            if ap is None or isinstance(ap, (int, float)):
                continue
            rb.extend(self._blocks(ap))
        wb = []
        for ap in writes:
            wb.extend(self._blocks(ap))
        for ap in reads:
            if ap is None or isinstance(ap, (int, float)) or not ap.tensor.name.startswith("p"):
                continue
            for b in self._blocks(ap):
                for key, y in b.r.items():
                    if key != eng:
                        self._dep(x, y)
        for b in rb:
            self._dep(x, b.w)
        for b in wb:
            self._dep(x, b.w)
            for y in b.r.values():
                self._dep(x, y)
        if isdma:
            if eng == "pool":
                s = self.NHW + self.dma_rr_sw
                self.dma_rr_sw = (self.dma_rr_sw + 1) % (self.NDMA - self.NHW)
            else:
                s = self.dma_rr
                self.dma_rr = (self.dma_rr + 1) % self.NHW
            self._dep(x, self.dma_last[s])
            self.dma_last[s] = x
            self.dma_uses[s] += 1
            x.dsem = s
            x.dval = 16 * self.dma_uses[s]
            self.ndma_ops += 1
        key = id(x) if isdma else eng
        for b in rb:
            b.r[key] = x
        for b in wb:
            b.w = x
            b.r = {}
        self.ops[eng].append(x)
        if isdma:
            self.dma_pending.append(x)
        else:
            self.last_compute[eng] = x
        return x

    def barrier(self):
        lasts = dict(self.last_compute)
        pend = list(self.dma_pending)
        self.dma_pending = []
        for e in self.ENGS:
            b = _Op(e, None)
            b.idx = len(self.ops[e])
            for e2, y in lasts.items():
                if e2 == e and e == "pe":
                    continue
                self._dep(b, y)
            for y in pend:
                self._dep(b, y)
            self.ops[e].append(b)

    def mm(self, out, lhsT, rhs, start=True, stop=True):
        return self.add("pe", lambda e: e.matmul(out, lhsT, rhs, start=start, stop=stop),
                        reads=(lhsT, rhs), writes=(out,))

    def tr(self, out, in_, ident):
        return self.add("pe", lambda e: e.transpose(out, in_, ident), reads=(in_, ident), writes=(out,))

    def tt(self, out, in0, in1, op, eng="dve"):
        return self.add(eng, lambda e: e.tensor_tensor(out, in0, in1, op), reads=(in0, in1), writes=(out,))

    def ts(self, out, in0, s1, op0, s2=None, op1=None, eng="dve", accum_out=None):
        kw = {}
        if eng == "pool" and op1 is None:
            if op0 == ALU.mult:
                s2, op1 = 0.0, ALU.add
            elif op0 == ALU.add:
                s2, op1 = 1.0, ALU.mult
        if op1 is not None:
            kw["op1"] = op1
        if accum_out is not None:
            kw["accum_out"] = accum_out
        w = (out,) if accum_out is None else (out, accum_out)
        return self.add(eng, lambda e: e.tensor_scalar(out, in0, s1, s2, op0, **kw),
                        reads=(in0, s1, s2), writes=w)

    def stt(self, out, in0, scalar, in1, op0, op1, accum_out=None):
        kw = {}
        if accum_out is not None:
            kw["accum_out"] = accum_out
        w = (out,) if accum_out is None else (out, accum_out)
        return self.add("dve", lambda e: e.scalar_tensor_tensor(out, in0, scalar, in1, op0, op1, **kw),
                        reads=(in0, scalar, in1), writes=w)

    def cp(self, out, in_, eng="dve"):
        if eng == "act":
            return self.add("act", lambda e: e.copy(out, in_), reads=(in_,), writes=(out,))
        return self.add(eng, lambda e: e.tensor_copy(out, in_), reads=(in_,), writes=(out,))

    def act(self, out, in_, func, bias=0.0, scale=1.0, accum_out=None):
        kw = {}
        if accum_out is not None:
            kw["accum_out"] = accum_out
        w = (out,) if accum_out is None else (out, accum_out)
        return self.add("act", lambda e: e.activation(out, in_, func, bias=bias, scale=scale, **kw),
                        reads=(in_, bias, scale), writes=w)

    def red(self, out, in_, op, axis=AX.X, eng="dve"):
        return self.add(eng, lambda e: e.tensor_reduce(out, in_, axis, op), reads=(in_,), writes=(out,))

    def recip(self, out, in_):
        return self.add("dve", lambda e: e.reciprocal(out, in_), reads=(in_,), writes=(out,))

    def rpow(self, out, in_, power, scale=1.0, bias=0.0):
        self.act(out, in_, AF.Ln, bias=bias, scale=scale)
        return self.act(out, out, AF.Exp, scale=power)

    def memset(self, ap, val, eng="dve"):
        return self.add(eng, lambda e: e.memset(ap, val), writes=(ap,))

    def scan(self, out, d0, d1, init, op0, op1):
        return self.add("dve", lambda e: e.tensor_tensor_scan(out, d0, d1, init, op0, op1),
                        reads=(d0, d1, init), writes=(out,))

    def dma(self, out, in_, eng="sp", is_output=False):
        x = self.add(eng, lambda e: e.dma_start(out=out, in_=in_), reads=(in_,), writes=(out,), isdma=True)
        if is_output:
            self.out_dmas.append(x)
        return x

    def finish(self):
        nc = self.nc
        fin = _Op("sp", None)
        fin.idx = len(self.ops["sp"])
        for y in self.out_dmas:
            self._dep(fin, y)
        self.ops["sp"].append(fin)
        sems = {}
        for e in ("pe", "dve", "act", "pool"):
            sems[e] = self.stack.enter_context(nc.semaphore("s_" + e))
        dsems = [self.stack.enter_context(nc.semaphore("d%d" % i)) for i in range(self.NDMA)]
        for e in ("pe", "dve", "act", "pool"):
            c = 0
            for x in self.ops[e]:
                if x.isdma:
                    continue
                if x.needed:
                    c += 1
                    x.sigval = c
            self.stats_sig = getattr(self, "stats_sig", {})
            self.stats_sig[e] = c
        ops = self.ops

        def replay(e, engobj):
            for x in ops[e]:
                for y in x.deps:
                    if y.isdma:
                        engobj.wait_ge(dsems[y.dsem], y.dval)
                    else:
                        engobj.wait_ge(sems[y.eng], y.sigval)
                if x.emit is None:
                    continue
                ins = x.emit(engobj)
                if x.isdma:
                    ins.then_inc(dsems[x.dsem], 16)
                elif x.needed:
                    ins.then_inc(sems[e], 1)

        with nc.Block() as block:
            @block.tensor
            def _(eng):
                replay("pe", eng)

            @block.vector
            def _(eng):
                replay("dve", eng)

            @block.scalar
            def _(eng):
                replay("act", eng)

            @block.gpsimd
            def _(eng):
                replay("pool", eng)

            @block.sync
            def _(eng):
                replay("sp", eng)


L = 2048
D = 1024
NT = L // 128
DEPTH = 2
N_IN = 3448
EPS = 1e-6

OFF = dict(gate=0, gdn_q=1024, gdn_k=1408, gdn_v=1792, gdn_a=2176, gdn_b=2188, mla_cq=2200, mla_ckv=2392,
           mla_kr=2520, rw_r=2552, rw_k=2808, rw_v=3064, rw_wd=3320, rw_ad=3384)


def chunk_table():
    ch = []
    for h in range(3):
        ch.append(("gq%d" % h, [(0, OFF["gdn_q"] + h * 128, 128)]))
        ch.append(("gk%d" % h, [(0, OFF["gdn_k"] + h * 128, 128)]))
        ch.append(("gv%d" % h, [(0, OFF["gdn_v"] + h * 128, 128)]))
    ch.append(("gab", [(0, OFF["gdn_a"], 6), (32, OFF["gdn_a"] + 6, 6), (64, OFF["gdn_b"], 6), (96, OFF["gdn_b"] + 6, 6)]))
    ch.append(("cq0", [(0, OFF["mla_cq"], 128)]))
    ch.append(("cq1", [(0, OFF["mla_cq"] + 128, 64)]))
    ch.append(("ckv", [(0, OFF["mla_ckv"], 128)]))
    ch.append(("kr", [(0, OFF["mla_kr"], 32)]))
    for i in range(2):
        ch.append(("rr%d" % i, [(0, OFF["rw_r"] + i * 128, 128)]))
        ch.append(("rk%d" % i, [(0, OFF["rw_k"] + i * 128, 128)]))
        ch.append(("rv%d" % i, [(0, OFF["rw_v"] + i * 128, 128)]))
    ch.append(("rwd", [(0, OFF["rw_wd"], 64)]))
    ch.append(("rad", [(0, OFF["rw_ad"], 64)]))
    for i in range(8):
        ch.append(("g%d" % i, [(0, OFF["gate"] + i * 128, 128)]))
    return ch


CHUNKS = chunk_table()
CH_IDX = {n: i for i, (n, _) in enumerate(CHUNKS)}
NCH = len(CHUNKS)


def host_win(w_in):
    out = np.zeros((DEPTH, NCH, 128, 8, 128), np.float32)
    for ci, (_, parts) in enumerate(CHUNKS):
        for dst, src, w in parts:
            blk = w_in[:, :, src:src + w].reshape(DEPTH, 8, 128, w)
            out[:, ci, :, :, dst:dst + w] = blk.transpose(0, 2, 1, 3)
    return out


class K:
    pass


def build(depth=DEPTH, mixers=("gdn", "mla", "rwkv"), dbg=False):
    nc = bass.Bass("TRN2", target_bir_lowering=False)
    k = K()
    k.nc = nc
    k.dbg = dbg
    k.dbg_outs = []
    k.cut = 99

    def din(name, shape, dt=F32):
        return nc.dram_tensor(name, list(shape), dt, kind="ExternalInput").ap()

    k.x_d = din("x", [L, D])
    k.win_d = din("win", [DEPTH, NCH, 128, 8, 128])
    k.normg_d = din("normg", [DEPTH, 128, 8])
    k.wout_d = din("wout", [DEPTH, 128, 8, 1024])
    k.fing_d = din("fing", [1, D])
    k.ident_d = din("ident", [128, 128])
    k.out_d = nc.dram_tensor("out", [L, D], F32, kind="ExternalOutput").ap()
    mla_decl(k)
    gdn_decl(k)
    rwkv_decl(k)

    with ExitStack() as st:
        P = Prog(nc, st)
        k.P = P
        k.xscr = nc.dram_tensor("xscr", [L, D], F32, kind="Internal").ap()
        P.dram_track("xscr", L * D * 4, 128 * D * 4)
        k.hT = P.sb("hT", [128, 8, L], BF16, blk=512)
        k.ident = P.sb("ident", [128, 128], F32)
        k.identb = P.sb("identb", [128, 128], BF16)
        k.normg = P.sb("normg", [128, DEPTH, 8], F32)
        k.wst = [P.sb("wst%d" % i, [128, 8, 128], F32) for i in range(2)]
        k.wbf = [P.sb("wbf%d" % i, [128, 8, 128], BF16) for i in range(2)]
        k.wrr = 0
        k.PAW = [P.ps("paw%d" % i, [128, 1024], F32, blk=512) for i in range(2)]
        k.PA = [k.PAW[i // 2][:, (i % 2) * 512:(i % 2 + 1) * 512] for i in range(4)]
        k.PB = [P.ps("pb%d" % i, [128, 512], F32) for i in range(4)]

        P.dma(k.ident[:], k.ident_d[:])
        P.cp(k.identb[:], k.ident[:])
        for l in range(DEPTH):
            P.dma(k.normg[:, l, :], k.normg_d[l])

        for l in range(depth):
            phase_a(k, l)
            with scope(k):
                k.mix_r = P.sb("mix_r", [128, 2, L], BF16, blk=512)
                if "rwkv" in mixers:
                    rwkv_phase(k, l)
                else:
                    P.memset(k.mix_r[:].rearrange("p a b -> p (a b)"), 1.0)
                with scope(k):
                    k.mix_m = P.sb("mix_m", [128, 3, L], BF16, blk=512)
                    if "mla" in mixers:
                        mla_phase(k, l)
                    else:
                        P.memset(k.mix_m[:].rearrange("p a b -> p (a b)"), 1.0)
                    with scope(k):
                        k.mix_g = P.sb("mix_g", [128, 3, L], BF16, blk=512)
                        if "gdn" in mixers:
                            gdn_phase(k, l)
                        else:
                            P.memset(k.mix_g[:].rearrange("p a b -> p (a b)"), 1.0)
                        if k.dbg:
                            for nm, t_, n_ in (("g", k.mix_g, 3), ("m", k.mix_m, 3), ("r", k.mix_r, 2)):
                                dump(k, "mix_%s%d" % (nm, l), t_[:].rearrange("p a b -> p (a b)"), [128, n_ * L])
                        phase_z(k, l, last=(l == depth - 1))
        P.finish()
        print("ops:", {e: len(v) for e, v in P.ops.items()}, "sig:", P.stats_sig, "dma:", P.ndma_ops)
    return k


def mixc(k, c):
    if c < 3:
        return k.mix_g[:, c, :]
    if c < 6:
        return k.mix_m[:, c - 3, :]
    return k.mix_r[:, c - 6, :]


def dump(k, name, ap, shape=None):
    if not k.dbg:
        return
    P = k.P
    shape = list(ap.shape) if shape is None else shape
    d = k.nc.dram_tensor("dbg_" + name, shape, ap.dtype, kind="ExternalOutput").ap()
    P.dma(d[:] if len(shape) == 2 else d, ap, is_output=True)
    k.dbg_outs.append("dbg_" + name)


def scope(k):
    class _S:
        def __enter__(s):
            s.old = k.P.stack
            s.st = ExitStack()
            s.st.__enter__()
            k.P.stack = s.st
            return s

        def __exit__(s, *a):
            k.P.barrier()
            k.P.stack = s.old
            s.st.__exit__(*a)
            return False
    return _S()


def phase_a(k, l):
    P = k.P
    with scope(k):
        ssq = P.sb("a_ssq", [128, NT])
        rs = P.sb("a_rs", [128, NT])
        rstd = P.sb("a_rstd", [128, NT])
        junk = [P.sb("a_junk%d" % i, [128, D], BF16) for i in range(2)]
        xs = [P.sb("a_xs%d" % i, [128, D], BF16) for i in range(2)]
        xin = [P.sb("a_xin%d" % i, [128, D], F32) for i in range(3)]
        src = k.x_d if l == 0 else k.xscr

        def stage1(tt):
            b = tt % 2
            xt_ = xin[tt % 3]
            P.dma(xt_[:], src[tt * 128:(tt + 1) * 128, :])
            P.act(junk[b][:], xt_[:], AF.Square, accum_out=ssq[:, tt:tt + 1])
            P.act(rs[:, tt:tt + 1], ssq[:, tt:tt + 1], AF.Sqrt, bias=EPS, scale=1.0 / D)
            P.recip(rstd[:, tt:tt + 1], rs[:, tt:tt + 1])
            P.ts(xs[b][:], xt_[:], rstd[:, tt:tt + 1], ALU.mult)

        def stage2(tt):
            b = tt % 2
            pt = k.PB[b][:].bitcast(BF16)
            for dc in range(8):
                P.tr(pt[:, dc * 128:(dc + 1) * 128], xs[b][:, dc * 128:(dc + 1) * 128], k.identb[:])
            P.cp(k.hT[:, :, tt * 128:(tt + 1) * 128], pt[:].rearrange("p (a b) -> p a b", a=8),
                 eng=("act" if tt % 2 == 0 else "dve"))
        stage1(0)
        for tt in range(NT):
            if tt + 1 < NT:
                stage1(tt + 1)
            stage2(tt)


def proj(k, l, name, alt=False):
    P = k.P
    BK = k.PB if alt else k.PA
    ci = CH_IDX[name]
    b = k.wrr
    k.wrr ^= 1
    P.dma(k.wst[b][:], k.win_d[l, ci], eng="sp")
    gb = k.normg[:, l, :].unsqueeze(2).broadcast_to([128, 8, 128])
    P.tt(k.wbf[b][:], k.wst[b][:], gb, ALU.mult, eng="pool")
    for tb in range(4):
        for dc in range(8):
            P.mm(BK[tb][:, :], k.wbf[b][:, dc, :], k.hT[:, dc, tb * 512:(tb + 1) * 512], start=(dc == 0), stop=(dc == 7))
    return BK


ZQ = "act"


def phase_z(k, l, last):
    P = k.P
    with scope(k):
        wst = P.sb("z_wst", [128, 8, 512], F32)
        wob = P.sb("z_wob", [128, 8, 1024], BF16, blk=512)
        sg = [P.sb("z_sg%d" % i, [128, L], BF16, blk=512) for i in range(2)]
        for nb in range(2):
            P.dma(wst[:], k.wout_d[l, :, :, nb * 512:(nb + 1) * 512])
            P.cp(wob[:, :, nb * 512:(nb + 1) * 512], wst[:], eng="act")
        for gc in range(8):
            pa = proj(k, l, "g%d" % gc, alt=(gc % 2 == 1))
            s = sg[gc % 2]
            for tb in range(4):
                P.act(s[:, tb * 512:(tb + 1) * 512], pa[tb][:, :], AF.Silu)
                mc = mixc(k, gc)[:, tb * 512:(tb + 1) * 512]
                P.tt(mc, mc, s[:, tb * 512:(tb + 1) * 512], ALU.mult)
        xin = [P.sb("z_xin%d" % i, [128, D], F32) for i in range(3)]
        src = k.x_d if l == 0 else k.xscr
        if last:
            ssq = P.sb("f_ssq", [128, NT])
            rs = P.sb("f_rs", [128, NT])
            rstd = P.sb("f_rstd", [128, NT])
            junk = [P.sb("f_junk%d" % i, [128, D], BF16) for i in range(2)]
            gf = P.sb("f_g", [128, D])
            ot = [P.sb("f_o%d" % i, [128, D]) for i in range(2)]
            P.dma(gf[:], k.fing_d[0:1, :].partition_broadcast(128))
        def za(tt):
            xt_ = xin[tt % 3]
            P.dma(xt_[:], src[tt * 128:(tt + 1) * 128, :])
            for nb in range(2):
                ps = k.PB[(tt * 2 + nb) % 4]
                for kc in range(8):
                    P.mm(ps[:, :], mixc(k, kc)[:, tt * 128:(tt + 1) * 128], wob[:, kc, nb * 512:(nb + 1) * 512],
                         start=(kc == 0), stop=(kc == 7))
                xs = xt_[:, nb * 512:(nb + 1) * 512]
                P.tt(xs, xs, ps[:, :], ALU.add)
            if not last:
                P.dma(k.xscr[tt * 128:(tt + 1) * 128, :], xt_[:], eng=ZQ)
            else:
                b = tt % 2
                P.act(junk[b][:], xt_[:], AF.Square, accum_out=ssq[:, tt:tt + 1])
                P.act(rs[:, tt:tt + 1], ssq[:, tt:tt + 1], AF.Sqrt, bias=EPS, scale=1.0 / D)

        def zb(tt):
            if last:
                xt_ = xin[tt % 3]
                b = tt % 2
                P.recip(rstd[:, tt:tt + 1], rs[:, tt:tt + 1])
                P.stt(ot[b][:], xt_[:], rstd[:, tt:tt + 1], gf[:], ALU.mult, ALU.mult)
                P.dma(k.out_d[tt * 128:(tt + 1) * 128, :], ot[b][:], eng=ZQ, is_output=True)
        za(0)
        for tt in range(NT):
            if tt + 1 < NT:
                za(tt + 1)
            zb(tt)


def host_inputs(inp, b):
    m = {}
    m["x"] = np.ascontiguousarray(inp["x"][b])
    m["win"] = host_win(inp["w_in"])
    m["normg"] = np.ascontiguousarray(inp["norm_g"].reshape(DEPTH, 8, 128).transpose(0, 2, 1))
    m["wout"] = np.ascontiguousarray(inp["w_out"].reshape(DEPTH, 8, 128, 1024).transpose(0, 2, 1, 3))
    m["fing"] = np.ascontiguousarray(inp["final_norm_g"].reshape(1, D))
    m["ident"] = np.eye(128, dtype=np.float32)
    host_mla(inp, b, m)
    host_gdn(inp, b, m)
    host_rwkv(inp, b, m)
    return m


TWO_PI = 2.0 * np.pi


def host_mla(inp, b, m):
    half = 16
    inv_freq = (10000.0 ** (-np.arange(half, dtype=np.float32) / half)).astype(np.float32)
    invf = np.zeros((32, 1), np.float32)
    invf[:, 0] = np.tile(inv_freq, 2) / np.float32(TWO_PI)
    m["invf"] = invf
    rm = np.zeros((32, 32), np.float32)
    for i in range(16):
        rm[i, i + 16] = -1.0
        rm[i + 16, i] = 1.0
    m["rmT"] = np.ascontiguousarray(rm.T)
    m["pos"] = np.ascontiguousarray(inp["positions"][b].reshape(1, L).astype(np.int32))
    wuq = inp["mla_w_uq"]
    o = np.zeros((DEPTH, 128, 2, 6, 128), np.float32)
    for h in range(6):
        nope = wuq[:, :, h * 96:h * 96 + 64]
        rope = wuq[:, :, h * 96 + 64:h * 96 + 96]
        o[:, :, 0, h, 64:128] = nope[:, 0:128]
        o[:, 0:64, 1, h, 64:128] = nope[:, 128:192]
        o[:, :, 0, h, 0:32] = rope[:, 0:128]
        o[:, 0:64, 1, h, 0:32] = rope[:, 128:192]
    m["wuq"] = o
    gq = np.zeros((DEPTH, 128, 2), np.float32)
    gq[:, :, 0] = inp["mla_q_norm_g"][:, 0:128]
    gq[:, 0:64, 1] = inp["mla_q_norm_g"][:, 128:192]
    m["gq"] = gq
    wukv = inp["mla_w_ukv"]
    wk = np.zeros((DEPTH, 128, 6, 128), np.float32)
    wv = np.zeros((DEPTH, 128, 6, 64), np.float32)
    for h in range(6):
        wk[:, :, h, 64:128] = wukv[:, :, h * 128:h * 128 + 64]
        wv[:, :, h, :] = wukv[:, :, h * 128 + 64:h * 128 + 128]
    m["wuk"] = wk
    m["wuv"] = wv
    m["gkv"] = np.ascontiguousarray(inp["mla_kv_norm_g"].reshape(DEPTH, 128, 1))


def mla_decl(k):
    nc = k.nc

    def din(name, shape, dt=F32):
        return nc.dram_tensor(name, list(shape), dt, kind="ExternalInput").ap()
    k.invf_d = din("invf", [32, 1])
    k.rmT_d = din("rmT", [32, 32])
    k.pos_d = din("pos", [1, L], I32)
    k.wuq_d = din("wuq", [DEPTH, 128, 2, 6, 128])
    k.gq_d = din("gq", [DEPTH, 128, 2])
    k.wuk_d = din("wuk", [DEPTH, 128, 6, 128])
    k.wuv_d = din("wuv", [DEPTH, 128, 6, 64])
    k.gkv_d = din("gkv", [DEPTH, 128, 1])


def latent_norm(k, l, names, nfeat, outs, ones):
    P = k.P
    sq = [P.sb("ln_sq%d" % i, [128, 512]) for i in range(2)]
    rq = [P.sb("ln_rq%d" % i, [128, 512]) for i in range(2)]
    n = len(names)
    for i, nm in enumerate(names):
        pa = proj(k, l, nm)
        for tb in range(4):
            s = sq[tb % 2]
            sk = ""
            if "a" not in sk:
                P.act(s[:], pa[tb][:, :], AF.Square)
            if "c" not in sk:
                P.cp(outs[i][:, tb * 512:(tb + 1) * 512], pa[tb][:, :], eng="dve")
            if "m" not in sk:
                P.mm(k.PB[tb][:, :], ones[:], s[:], start=(i == 0), stop=(i == n - 1))
    c2 = 9
    if c2 < 1:
        return
    for tb in range(4):
        r = rq[tb % 2]
        P.rpow(r[:], k.PB[tb][:, :], -0.5, scale=1.0 / nfeat, bias=EPS)
        for i in range(n):
            o = outs[i][:, tb * 512:(tb + 1) * 512]
            P.tt(o, o, r[:], ALU.mult)


def mla_phase(k, l):
    P = k.P
    SC = 96.0 ** -0.5
    with scope(k):
        cqn0 = P.sb("m_cqn0", [128, L], BF16, blk=512)
        cqn1 = P.sb("m_cqn1", [128, L], BF16, blk=512)
        ckvn = P.sb("m_ckvn", [128, L], BF16, blk=512)
        krope = P.sb("m_krope", [32, L], BF16, blk=512)
        cos2 = P.sb("m_cos2", [32, L], BF16, blk=512)
        sin2 = P.sb("m_sin2", [32, L], BF16, blk=512)
        wq = P.sb("m_wq", [128, 2, 6, 128], BF16)
        wk = P.sb("m_wk", [128, 6, 128], BF16)
        wv = P.sb("m_wv", [128, 6, 64], BF16)
        ones = P.sb("m_ones", [128, 128], F32)
        onesk = P.sb("m_onesk", [128, 128], F32)
        rmT = P.sb("m_rmT", [32, 32], F32)
        P.memset(ones[:], 1.0)
        P.memset(onesk[:], 1.0)
        P.memset(onesk[32:64, :], 0.0)
        P.dma(rmT[:], k.rmT_d[:])
        with scope(k):
            st = P.sb("m_st", [128, 2, 6, 128], F32)
            g = P.sb("m_g", [128, 4], F32)
            P.dma(st[:], k.wuq_d[l])
            P.dma(g[:, 0:2], k.gq_d[l])
            P.dma(g[:, 2:3], k.gkv_d[l])
            for kc in range(2):
                P.ts(wq[:, kc].rearrange("p a b -> p (a b)"), st[:, kc].rearrange("p a b -> p (a b)"),
                     g[:, kc:kc + 1], ALU.mult)
            st2 = P.sb("m_st2", [128, 6, 128], F32)
            P.dma(st2[:], k.wuk_d[l])
            P.ts(wk[:].rearrange("p a b -> p (a b)"), st2[:].rearrange("p a b -> p (a b)"), g[:, 2:3], ALU.mult)
            st3 = P.sb("m_st3", [128, 6, 64], F32)
            P.dma(st3[:], k.wuv_d[l])
            P.ts(wv[:].rearrange("p a b -> p (a b)"), st3[:].rearrange("p a b -> p (a b)"), g[:, 2:3], ALU.mult)
        if k.cut < 1:
            return
        with scope(k):
            latent_norm(k, l, ["cq0", "cq1"], 192, [cqn0, cqn1], ones)
            latent_norm(k, l, ["ckv"], 128, [ckvn], ones)
        if k.cut < 2:
            return
        with scope(k):
            invf = P.sb("m_invf", [32, 1], F32)
            P.dma(invf[:], k.invf_d[:])
            pa = proj(k, l, "kr")

            def rope_blk(tb):
                sl = slice(tb * 512, (tb + 1) * 512)
                posi = P.sb("m_posi%d" % tb, [32, 512], I32)
                y = P.sb("m_y%d" % tb, [32, 512], F32)
                yi = P.sb("m_yi%d" % tb, [32, 512], I32)
                fr = P.sb("m_fr%d" % tb, [32, 512], F32)
                kr = P.sb("m_kr%d" % tb, [32, 512], F32)
                t1 = P.sb("m_t1%d" % tb, [32, 512], F32)
                t2 = P.sb("m_t2%d" % tb, [32, 512], F32)
                P.dma(posi[:], k.pos_d[0:1, sl].partition_broadcast(32))
                P.cp(kr[:], pa[tb][0:32, :], eng="act")
                yield
                P.cp(y[:], posi[:])
                P.mm(k.PB[tb][0:32, :], rmT[:], kr[:])
                yield
                P.ts(y[:], y[:], invf[:, 0:1], ALU.mult)
                yield
                for off, dst in ((0.0, sin2), (0.25, cos2)):
                    if off != 0.0:
                        P.ts(y[:], y[:], off, ALU.add)
                        yield
                    P.cp(yi[:], y[:])
                    yield
                    P.cp(fr[:], yi[:])
                    yield
                    P.tt(fr[:], y[:], fr[:], ALU.subtract)
                    yield
                    P.act(dst[:, sl], fr[:], AF.Sin, scale=TWO_PI * (1.0 - 1e-6))
                    yield
                P.tt(t1[:], kr[:], cos2[:, sl], ALU.mult)
                P.tt(t2[:], k.PB[tb][0:32, :], sin2[:, sl], ALU.mult)
                yield
                P.tt(krope[:, sl], t1[:], t2[:], ALU.add)
                yield
            run_interleaved([rope_blk(tb) for tb in range(4)])
        if k.cut < 3:
            return
        kT = [P.sb("m_kT%d" % i, [128, L], BF16, blk=512) for i in range(2)]
        qT = [P.sb("m_qT%d" % i, [128, L], BF16, blk=512) for i in range(2)]
        Vh = [P.sb("m_V%d" % i, [128, NT, 96], BF16) for i in range(2)]
        pT = [P.sb("m_pT%d" % i, [128, 1024], BF16, blk=512) for i in range(2)]
        sq = [P.sb("m_sq%d" % i, [128, 512], F32) for i in range(2)]
        qr = [P.sb("m_qr%d" % i, [32, 512], F32) for i in range(2)]
        t1 = P.sb("m_t1b", [32, 512], F32)
        t2 = P.sb("m_t2b", [32, 512], F32)
        mrow = P.sb("m_mrow", [64, 512], F32)
        km4 = P.sb("m_km4", [128, 4], F32)
        kmax2 = P.sb("m_kmax2", [128, 1], F32)
        rden = [P.sb("m_rden%d" % i, [64, 512], F32) for i in range(2)]
        for i in range(2):
            P.memset(kT[i][32:64, :], 0.0)
            P.memset(kT[i][32:33, :], 1.0)
            P.memset(qT[i][32:64, :], 0.0)
            P.memset(Vh[i][:, :, 64:96], 1.0)
        kmx = [P.sb("m_kmx%d" % i, [128, 1], F32) for i in range(2)]

        def prep(h):
            kt_, qt_, vh_ = kT[h % 2], qT[h % 2], Vh[h % 2]
            kmax2_ = kmx[h % 2]
            P.cp(kt_[0:32, :], krope[:], eng="pool")
            for tb in range(4):
                sl = slice(tb * 512, (tb + 1) * 512)
                s = sq[tb % 2]
                P.mm(k.PB[2][:, :], wk[:, h, :], ckvn[:, sl])
                yield
                P.cp(kt_[64:128, sl], k.PB[2][64:128, :], eng="dve")
                yield
                P.tt(s[:], kt_[:, sl], kt_[:, sl], ALU.mult, eng="pool")
                yield
                yield
                P.mm(k.PB[3][:, :], onesk[:], s[:])
                yield
                P.red(km4[:, tb:tb + 1], k.PB[3][:, :], ALU.max)
                yield
            P.red(kmax2_[:], km4[:], ALU.max)
            for half in range(2):
                for j in range(8):
                    tt = half * 8 + j
                    P.mm(k.PB[2][:, j * 64:(j + 1) * 64], ckvn[:, tt * 128:(tt + 1) * 128], wv[:, h, :])
                yield
                P.cp(vh_[:, half * 8:(half + 1) * 8, 0:64], k.PB[2][:, :].rearrange("p (a b) -> p a b", a=8), eng="dve")
                yield
            for tb in range(4):
                sl = slice(tb * 512, (tb + 1) * 512)
                s = sq[tb % 2]
                q_ = qr[tb % 2]
                P.mm(k.PB[2][:, :], wq[:, 0, h, :], cqn0[:, sl], start=True, stop=False)
                P.mm(k.PB[2][:, :], wq[:, 1, h, :], cqn1[:, sl], start=False, stop=True)
                yield
                P.cp(qt_[64:128, sl], k.PB[2][64:128, :], eng="dve")
                P.cp(q_[:], k.PB[2][0:32, :], eng="dve")
                yield
                P.act(s[:], k.PB[2][:, :], AF.Square)
                yield
                P.mm(k.PB[3][:, :], ones[:], s[:])
                yield
                P.act(mrow[32:33, :], k.PB[3][32:33, :], AF.Sqrt, scale=kmax2_[32:33, 0:1])
                yield
                P.ts(qt_[32:33, sl], mrow[32:33, :], -1.0, ALU.mult)
                P.mm(k.PB[2][0:32, :], rmT[:], q_[:])
                P.tt(t1[:], q_[:], cos2[:, sl], ALU.mult, eng="pool")
                yield
                P.tt(t2[:], k.PB[2][0:32, :], sin2[:, sl], ALU.mult)
                yield
                P.tt(qt_[0:32, sl], t1[:], t2[:], ALU.add)
                yield

        def attn(h):
            kt_, qt_, vh_ = kT[h % 2], qT[h % 2], Vh[h % 2]
            pti = 0
            for qb in range(4):
                qs = slice(qb * 512, (qb + 1) * 512)
                O = k.PB[qb % 2]

                def s_pair(m_):
                    for j in range(2):
                        kt = 2 * m_ + j
                        P.mm(k.PAW[m_ % 2][:, j * 512:(j + 1) * 512], kt_[:, kt * 128:(kt + 1) * 128], qt_[:, qs])
                s_pair(0)
                s_pair(1)
                for m_ in range(NT // 2):
                    p_ = pT[pti % 2]
                    pti += 1
                    P.act(p_[:], k.PAW[m_ % 2][:, :], AF.Exp, scale=SC)
                    if m_ + 2 < NT // 2:
                        s_pair(m_ + 2)
                    for j in range(2):
                        kt = 2 * m_ + j
                        P.mm(O[0:96, :], vh_[:, kt, :], p_[:, j * 512:(j + 1) * 512], start=(kt == 0), stop=(kt == NT - 1))
                    yield
                rd = rden[qb % 2]
                P.rpow(rd[0:32, :], O[64:96, :], -1.0)
                P.rpow(rd[32:64, :], O[64:96, :], -1.0)
                ob = (h % 2) * 64
                P.tt(mixc(k, 3 + h // 2)[ob:ob + 64, qs], O[0:64, :], rd[:], ALU.mult)
                yield

        for _ in prep(0):
            pass
        mode = "il"
        for h in range(6):
            gens = [attn(h)]
            if h + 1 < 6:
                if mode == "il":
                    gens.append(prep(h + 1))
                elif mode == "seq":
                    run_interleaved(gens)
                    gens = [prep(h + 1)]
                elif mode == "noprep":
                    pass
            if mode == "noprep" and h > 0:
                gens = [attn(0)]
            run_interleaved(gens)


NCK = L // 64
NEG = -30000.0


def host_gdn(inp, b, m):
    cw = inp["gdn_conv"]
    o = np.zeros((DEPTH, 128, 9, 5), np.float32)
    for part in range(3):
        for p in range(3):
            o[:, :, part * 3 + p, :] = cw[:, :, part * 384 + p * 128: part * 384 + (p + 1) * 128].transpose(0, 2, 1)
    m["gconv"] = o
    gb = np.zeros((DEPTH, 128, 2), np.float32)
    for d in range(2):
        gb[:, d * 32:d * 32 + 6, 0] = inp["gdn_dt_bias"][:, d, :]
        gb[:, d * 32:d * 32 + 6, 1] = inp["gdn_a_log"][:, d, :]
    m["ggb"] = gb
    m["gng"] = np.ascontiguousarray(np.tile(inp["gdn_norm_g"], (1, 2)).reshape(DEPTH, 128, 1))
    sel = np.zeros((64, 6, 128), np.float32)
    for d in range(2):
        for p in range(3):
            sel[d * 32 + 2 * p, d * 3 + p, 0:64] = 1.0
            sel[d * 32 + 2 * p + 1, d * 3 + p, 64:128] = 1.0
    m["gsel"] = sel
    j = np.arange(64)[:, None]
    i = np.arange(64)[None, :]
    nm = np.zeros((128, 2, 64), np.float32)
    nm[:, 0, :] = np.tile(np.where(i > j, 0.0, NEG), (2, 1))
    nm[:, 1, :] = np.tile(np.where(i < j, 0.0, NEG), (2, 1))
    m["gnegm"] = nm
    m["gid2"] = np.ascontiguousarray(np.tile(np.eye(64, dtype=np.float32), (2, 1)))


def gdn_decl(k):
    nc = k.nc

    def din(name, shape, dt=F32):
        return nc.dram_tensor(name, list(shape), dt, kind="ExternalInput").ap()
    k.gconv_d = din("gconv", [DEPTH, 128, 9, 5])
    k.ggb_d = din("ggb", [DEPTH, 128, 2])
    k.gng_d = din("gng", [DEPTH, 128, 1])
    k.gsel_d = din("gsel", [64, 6, 128])
    k.gnegm_d = din("gnegm", [128, 2, 64])
    k.gid2_d = din("gid2", [128, 64])


def bc3(ap2, n):
    return ap2.unsqueeze(2).broadcast_to([ap2.shape[0], ap2.shape[1], n])


def bcm(ap2, n):
    return ap2.unsqueeze(1).broadcast_to([ap2.shape[0], n, ap2.shape[1]])


HS = (slice(0, 64), slice(64, 128))


def v3(ps, n=8):
    return ps[:, 0:n * 64].rearrange("p (a b) -> p a b", a=n)


def mm2(P, ps, c, lhsT, rhs, **kw):
    for hs in HS:
        P.mm(ps[hs, c * 64:(c + 1) * 64], lhsT[hs], rhs[hs], **kw)


def tr2(P, ps, c, in_, ident):
    for hs in HS:
        P.mm(ps[hs, c * 64:(c + 1) * 64], in_[hs], ident[hs, hs])


def neumann2(k, Nn, Rm, tmp, bank, id2, n8=8):
    P = k.P
    idb = bcm(id2[:, :], n8)
    tA, tB, tC, tD = tmp
    pa_, pb_, pc_ = bank
    for c in range(n8):
        tr2(P, pa_, c, Nn[:, c, :], k.identb)
    P.cp(tA[:], v3(pa_, n8), eng="act")
    P.tt(Rm[:], Nn[:], idb, ALU.add)
    yield
    cur, curT = Nn, tA
    targets = [(tB, tC), (tD, tA)]
    for lvl in range(1, 7):
        nxt, nxtT = targets[(lvl - 1) % 2]
        for c in range(n8):
            if lvl <= 5:
                mm2(P, pb_, c, cur[:, c, :], curT[:, c, :])
                if lvl < 5:
                    mm2(P, pa_, c, curT[:, c, :], cur[:, c, :])
            if lvl >= 2:
                mm2(P, pc_, c, curT[:, c, :], Rm[:, c, :])
        if lvl <= 5:
            P.cp(nxtT[:], v3(pb_, n8), eng="act")
            if lvl < 5:
                P.cp(nxt[:], v3(pa_, n8), eng="act")
        if lvl >= 2:
            P.tt(Rm[:], Rm[:], v3(pc_, n8), ALU.add)
        yield
        cur, curT = nxt, nxtT


def run_interleaved(gens):
    gens = list(gens)
    while gens:
        for g in list(gens):
            try:
                next(g)
            except StopIteration:
                gens.remove(g)


def norm_pipe(k, n, src_fn, sqb, rnb, banks, bones, power, scale, bias, post_fn):
    P = k.P

    def pre(i):
        P.act(sqb[i % 2][:], src_fn(i), AF.Square)
        P.mm(banks[i % 2][:, :], bones[:], sqb[i % 2][:])

    def post(i):
        P.rpow(rnb[i % 2][:], banks[i % 2][:, :], power, scale=scale, bias=bias)
        post_fn(i, rnb[i % 2])
    pre(0)
    for i in range(n):
        if i + 1 < n:
            pre(i + 1)
        post(i)


def run_pipelined(chains, depth=2):
    active = []
    nxt = [0] * len(chains)

    def start(ci):
        if nxt[ci] < len(chains[ci]):
            active.append((ci, chains[ci][nxt[ci]](nxt[ci] % depth)))
            nxt[ci] += 1
    for ci in range(len(chains)):
        for _ in range(depth):
            start(ci)
    while active:
        for item in list(active):
            try:
                next(item[1])
            except StopIteration:
                active.remove(item)
                start(item[0])


def gdn_phase(k, l):
    P = k.P
    with scope(k):
        GC = P.sb("g_GC", [64, L], F32, blk=512)
        GP = [P.sb("g_GP%d" % p, [128, NCK, 4], F32) for p in range(3)]
        NBP = [P.sb("g_NBP%d" % p, [128, NCK, 2], F32) for p in range(3)]
        sel = P.sb("g_sel", [64, 6, 128], F32)
        negm = P.sb("g_negm", [128, 2, 64], F32)
        id2 = P.sb("g_id2", [128, 64], F32)
        cw = P.sb("g_cw", [128, 9, 5], F32)
        ng = P.sb("g_ng", [128, 1], F32)
        bones = P.sb("g_bones", [128, 128], F32)
        P.dma(sel[:], k.gsel_d[:])
        P.dma(negm[:], k.gnegm_d[:])
        P.dma(id2[:], k.gid2_d[:])
        P.dma(cw[:], k.gconv_d[l])
        P.dma(ng[:], k.gng_d[l])
        P.memset(bones[:], 0.0)
        P.memset(bones[0:64, 0:64], 1.0)
        P.memset(bones[64:128, 64:128], 1.0)
        with scope(k):
            GT = P.sb("g_GT", [128, L], F32, blk=512)
            m0 = P.sb("g_m0", [64, L], F32)
            gb = P.sb("g_gb", [128, 2], F32)
            negA = P.sb("g_negA", [128, 1], F32)
            P.dma(gb[:], k.ggb_d[l])
            P.act(negA[:], gb[:, 1:2], AF.Exp)
            P.ts(negA[:], negA[:], -1.0, ALU.mult)
            P.memset(m0[:], 1.0)
            P.memset(m0[:, 0:L:64], 0.0)
            pa = proj(k, l, "gab")
            for tb in range(4):
                sl = slice(tb * 512, (tb + 1) * 512)
                P.act(GT[0:64, sl], pa[tb][0:64, :], AF.Exp, bias=gb[0:64, 0:1])
                P.act(GT[64:128, sl], pa[tb][64:128, :], AF.Sigmoid)
            P.act(GT[0:64, :], GT[0:64, :], AF.Ln, bias=1.0)
            P.ts(GT[0:64, :], GT[0:64, :], negA[0:64, 0:1], ALU.mult)
            P.scan(GC[:, :], m0[:, :], GT[0:64, :], 0.0, ALU.mult, ALU.add)
            gc3 = GC[32:64, :].rearrange("p (a b) -> p a b", b=64)
            P.tt(m0[32:64, :].rearrange("p (a b) -> p a b", b=64), bc3(GC[32:64, 63:L:64], 64), gc3, ALU.subtract)
            P.tt(GC[32:64, :], m0[32:64, :], GT[32:64, :], ALU.add)
            for grp in range(4):
                g8 = slice(grp * 8, (grp + 1) * 8)
                for c in range(8):
                    ck = grp * 8 + c
                    cs = slice(c * 64, (c + 1) * 64)
                    for hs in HS:
                        P.mm(k.PB[2 * (grp % 2)][hs, cs], GC[:, ck * 64:(ck + 1) * 64], k.ident[0:64, 0:64])
                        P.mm(k.PB[2 * (grp % 2) + 1][hs, cs], GT[64:128, ck * 64:(ck + 1) * 64], k.ident[64:128, 64:128])
                n_ = 0
                for p in range(3):
                    for hf, hs in enumerate(HS):
                        h = 2 * p + hf
                        for q, ps in ((0, k.PB[2 * (grp % 2)]), (1, k.PB[2 * (grp % 2) + 1])):
                            src = v3(ps)[hs, :, h:h + 33:32]
                            P.cp(GP[p][hs, g8, 2 * q:2 * q + 2], src, eng=("act" if q else "dve"))
            for p in range(3):
                P.ts(NBP[p][:], GP[p][:, :, 2:4], -1.0, ALU.mult)
        for p in range(3):
            with scope(k):
                Q = P.sb("g_Q", [128, L], BF16, blk=512)
                K_ = P.sb("g_K", [128, L], BF16, blk=512)
                Kt = P.sb("g_Kt", [128, NCK, 64], BF16, blk=512)
                Vt = P.sb("g_Vt", [128, NCK, 64], BF16, blk=512)
                O = P.sb("g_O", [128, L], F32, blk=512)
                P.memset(O[:], 0.0, eng="pool")
                with scope(k):
                    xp = P.sb("g_xp", [128, L + 4], F32)
                    Vf = P.sb("g_Vf", [128, L], BF16, blk=512)
                    cv = P.sb("g_cv", [128, L], F32, blk=512)
                    sq = P.sb("g_sq", [128, 512], F32)
                    rn = P.sb("g_rn", [128, 512], F32)
                    sq2 = P.sb("g_sq2", [128, 512], F32)
                    rn2 = P.sb("g_rn2", [128, 512], F32)
                    P.memset(xp[:, 0:2], 0.0)
                    P.memset(xp[:, L + 2:L + 4], 0.0)
                    for part, nm, dst in ((0, "gq", Q), (1, "gk", K_), (2, "gv", Vf)):
                        pa = proj(k, l, "%s%d" % (nm, p))
                        for tb in range(4):
                            P.cp(xp[:, 2 + tb * 512:2 + (tb + 1) * 512], pa[tb][:, :], eng=("act" if tb % 2 else "dve"))
                        wi = part * 3 + p
                        P.ts(cv[:], xp[:, 0:L], cw[:, wi, 0:1], ALU.mult)
                        for j in range(1, 5):
                            P.stt(cv[:], xp[:, j:j + L], cw[:, wi, j:j + 1], cv[:], ALU.mult, ALU.add)
                        if part == 2:
                            P.act(dst[:], cv[:], AF.Silu)
                        else:
                            P.act(cv[:], cv[:], AF.Silu)
                        if part < 2:
                            def fin(tb, r, dst=dst):
                                P.tt(dst[:, tb * 512:(tb + 1) * 512], cv[:, tb * 512:(tb + 1) * 512], r[:], ALU.mult)
                            norm_pipe(k, 4, lambda tb: cv[:, tb * 512:(tb + 1) * 512], (sq, sq2), (rn, rn2),
                                      (k.PB[2], k.PB[3]), bones, -0.5, 64.0 if part == 0 else 1.0,
                                      64e-6 if part == 0 else 1e-6, fin)
                    for src, dstt in ((K_, Kt), (Vf, Vt)):
                        for grp in range(4):
                            ps = k.PB[grp % 2]
                            for c in range(8):
                                ck = grp * 8 + c
                                tr2(P, ps, c, src[:, ck * 64:(ck + 1) * 64], k.identb)
                            P.cp(dstt[:, grp * 8:(grp + 1) * 8, :], v3(ps), eng=("act" if grp % 2 else "dve"))
                with scope(k):
                    T = dict(GC=GC, GP=GP[p], NBP=NBP[p], sel=sel, negm=negm, id2=id2, Q=Q, K=K_, Kt=Kt, Vt=Vt, O=O)
                    run_pipelined([gdn_chain(k, p, d, T) for d in range(2)], depth=1)
                with scope(k):
                    sqo = [P.sb("g_osq%d" % i, [128, 512], F32) for i in range(2)]
                    rno = [P.sb("g_orn%d" % i, [128, 512], F32) for i in range(2)]

                    def fin_o(tb, r):
                        sl = slice(tb * 512, (tb + 1) * 512)
                        P.tt(r[:], O[:, sl], r[:], ALU.mult)
                        P.ts(mixc(k, p)[:, sl], r[:], ng[:, 0:1], ALU.mult)
                    norm_pipe(k, 4, lambda tb: O[:, tb * 512:(tb + 1) * 512], sqo, rno, (k.PB[2], k.PB[3]), bones,
                              -0.5, 1.0 / 64, EPS, fin_o)


def gdn_chain(k, p, d, T):
    P = k.P
    GC, GP, NBP, sel, negm, id2, Q, K_, Kt, Vt, O = (T[n] for n in ("GC", "GP", "NBP", "sel", "negm", "id2", "Q", "K", "Kt", "Vt", "O"))
    B = k.PA if d == 0 else k.PB
    tag = "g%d_" % d
    names = ("CB", "EI", "QG", "Rm", "U0", "WT", "BW", "KD", "GK", "AcT", "Sg", "Ug", "nA", "nB", "nC", "nD", "Nb", "PTb", "Sgb")
    f32n = ("CB", "EI", "AcT", "Sg")
    NSET = 1
    GG = [{n: P.sb(tag + "%d" % s_ + n, [128, 8, 64], F32 if n in f32n else BF16) for n in names} for s_ in range(NSET)]
    for s_ in range(NSET):
        GG[s_]["gend"] = P.sb(tag + "gend%d" % s_, [128, 8], F32)
        GG[s_]["kds"] = P.sb(tag + "kds%d" % s_, [128, 8], F32)
    gam = P.sb(tag + "gam", [128, NCK], F32)
    shared = {"scan": 0}
    Scar = P.sb(tag + "Scar", [128, 64], F32)
    e_ = 63 if d == 0 else 0
    idb = bcm(id2[:, :], 8)

    def f2(t):
        return t[:].rearrange("p a b -> p (a b)")
    P.memset(Scar[:], 0.0)
    P.act(gam[:], GP[:, :, d], AF.Exp)
    gorder = list(range(4)) if d == 0 else list(range(3, -1, -1))

    def group(gi, grp, G):
        Nb, PTb, Sgb, gend, kds = G["Nb"], G["PTb"], G["Sgb"], G["gend"], G["kds"]
        sl = slice(grp * 512, (grp + 1) * 512)
        g8 = slice(grp * 8, (grp + 1) * 8)
        cj = GP[:, g8, d]
        nb = NBP[:, g8, d]
        CB, EI, QG, Rm, U0, WT, BW, KD, GK, AcT, Sg, Ug = (G[n] for n in names[:12])
        P.mm(B[0][:, :], sel[:, d * 3 + p, :], GC[:, sl])
        P.cp(f2(CB), B[0][:, :], eng="act")
        P.act(f2(EI), f2(CB), AF.Exp)
        P.cp(gend[:], EI[:, :, e_], eng="pool")
        P.tt(f2(QG), f2(EI), Q[:, sl], ALU.mult)
        P.tt(kds[:], CB[:, :, e_], cj, ALU.subtract)
        P.act(kds[:], kds[:], AF.Exp)
        P.tt(GK[:], Kt[:, g8, :], bc3(gam[:, g8], 64), ALU.mult, eng="pool")
        P.tt(KD[:], Kt[:, g8, :], bc3(kds[:], 64), ALU.mult, eng="pool")
        P.tt(CB[:], CB[:], bc3(cj, 64), ALU.subtract)
        P.tt(CB[:], CB[:], bcm(negm[:, d, :], 8), ALU.add)
        P.act(f2(CB), f2(CB), AF.Exp)
        yield
        for c in range(8):
            cs = slice((grp * 8 + c) * 64, (grp * 8 + c + 1) * 64)
            mm2(P, B[0], c, K_[:, cs], Q[:, cs])
            mm2(P, B[1], c, K_[:, cs], K_[:, cs])
        P.tt(EI[:], CB[:], idb, ALU.add)
        P.tt(PTb[:], EI[:], v3(B[0]), ALU.mult)
        P.tt(CB[:], CB[:], v3(B[1]), ALU.mult)
        P.tt(Nb[:], CB[:], bc3(nb, 64), ALU.mult)
        yield
        for _ in neumann2(k, Nb, Rm, (G["nA"], G["nB"], G["nC"], G["nD"]), (B[0], B[1], B[2]), id2):
            yield
        for c in range(8):
            mm2(P, B[0], c, Rm[:, c, :], Vt[:, grp * 8 + c, :])
            mm2(P, B[1], c, GK[:, c, :], Rm[:, c, :])
            mm2(P, B[2], c, Rm[:, c, :], GK[:, c, :])
        P.tt(U0[:], v3(B[0]), bc3(nb, 64), ALU.mult)
        P.ts(f2(U0), f2(U0), -1.0, ALU.mult, eng="pool")
        P.cp(WT[:], v3(B[1]), eng="act")
        P.tt(BW[:], v3(B[2]), bc3(nb, 64), ALU.mult)
        yield
        for c in range(8):
            mm2(P, B[3], c, BW[:, c, :], KD[:, c, :])
        P.tt(AcT[:], idb, bc3(gend[:], 64), ALU.mult, eng="pool")
        P.tt(AcT[:], AcT[:], v3(B[3]), ALU.add)
        yield
        while shared["scan"] != gi:
            yield
        corder = range(8) if d == 0 else range(7, -1, -1)
        prev = Scar[:]
        for n, c in enumerate(corder):
            P.cp(Sg[:, c, :], prev, eng="pool") if n == 0 else None
            ps = B[2 + n % 2]
            mm2(P, ps, 0, AcT[:, c, :], Sg[:, c, :], start=True, stop=False)
            mm2(P, ps, 0, KD[:, c, :], U0[:, c, :], start=False, stop=True)
            last = (n == 7)
            dst = Scar[:] if last else Sg[:, corder[n + 1], :]
            P.cp(dst, ps[:, 0:64], eng="act")
            yield
        shared["scan"] = gi + 1
        P.cp(Sgb[:], Sg[:], eng="pool")
        for c in range(8):
            mm2(P, B[0], c, WT[:, c, :], Sgb[:, c, :])
        P.tt(CB[:], v3(B[0]), bc3(nb, 64), ALU.mult)
        P.tt(Ug[:], CB[:], U0[:], ALU.add)
        yield
        for c in range(8):
            mm2(P, B[1], c, Sgb[:, c, :], QG[:, c, :], start=True, stop=False)
            mm2(P, B[1], c, Ug[:, c, :], PTb[:, c, :], start=False, stop=True)
        P.tt(O[:, sl], O[:, sl], B[1][:, :], ALU.add)
        yield
    return [(lambda slot, gi=gi, grp=grp: group(gi, grp, GG[slot])) for gi, grp in enumerate(gorder)]


RW_EPS = 64e-5
DEC = float(np.exp(-0.5))
GS = 8
NG = NCK // GS


def host_rwkv(inp, b, m):
    mu = inp["rwkv_mu"]
    o = np.zeros((DEPTH, 128, 8, 2), np.float32)
    for part in range(3):
        for p in range(2):
            o[:, :, part * 2 + p, :] = mu[:, :, part * 256 + p * 128: part * 256 + (p + 1) * 128].transpose(0, 2, 1)
    o[:, 0:64, 6, :] = mu[:, :, 768:832].transpose(0, 2, 1)
    o[:, 0:64, 7, :] = mu[:, :, 832:896].transpose(0, 2, 1)
    m["rmu"] = o

    def pp(a):
        if a.ndim == 2:
            return np.ascontiguousarray(a.reshape(DEPTH, 2, 128).transpose(0, 2, 1))
        return np.ascontiguousarray(a.reshape(DEPTH, 2, 2, 128).transpose(0, 3, 1, 2))
    pv = np.zeros((DEPTH, 128, 7, 2), np.float32)
    pv[:, :, 0:2, :] = pp(inp["rwkv_w0"])
    pv[:, :, 2:4, :] = pp(inp["rwkv_a0"])
    pv[:, :, 4, :] = pp(inp["rwkv_k_k"])
    pv[:, :, 5, :] = pp(inp["rwkv_k_a"])
    pv[:, :, 6, :] = pp(inp["rwkv_r_k"].reshape(DEPTH, 256))
    m["rpv"] = pv
    ln = np.zeros((DEPTH, 128, 2, 2), np.float32)
    ln[:, :, 0, :] = pp(inp["rwkv_ln_g"])
    ln[:, :, 1, :] = pp(inp["rwkv_ln_b"])
    m["rln"] = ln
    m["rw2"] = np.ascontiguousarray(inp["rwkv_w2"].transpose(0, 2, 1, 3))
    m["ra2"] = np.ascontiguousarray(inp["rwkv_a2"].transpose(0, 2, 1, 3))
    s_ = np.arange(64)[:, None]
    t_ = np.arange(64)[None, :]
    msk = np.zeros((128, 2, 4, 64), np.float32)
    msk[:, 0, 0, :] = np.tile((t_ > s_), (2, 1))
    msk[:, 0, 1, :] = np.tile((t_ >= s_), (2, 1))
    msk[:, 1, 0, :] = np.tile((t_ < s_), (2, 1))
    msk[:, 1, 1, :] = np.tile((t_ <= s_), (2, 1))
    msk[:, :, 2:4, :] = -msk[:, :, 0:2, :]
    m["rmsk"] = msk


def rwkv_decl(k):
    nc = k.nc

    def din(name, shape, dt=F32):
        return nc.dram_tensor(name, list(shape), dt, kind="ExternalInput").ap()
    k.rmu_d = din("rmu", [DEPTH, 128, 8, 2])
    k.rpv_d = din("rpv", [DEPTH, 128, 7, 2])
    k.rln_d = din("rln", [DEPTH, 128, 2, 2])
    k.rw2_d = din("rw2", [DEPTH, 64, 2, 256])
    k.ra2_d = din("ra2", [DEPTH, 64, 2, 256])
    k.rmsk_d = din("rmsk", [128, 2, 4, 64])


def rwkv_phase(k, l):
    P = k.P
    with scope(k):
        mu = P.sb("r_mu", [128, 8, 3], F32)
        pv = P.sb("r_pv", [128, 7, 2], F32)
        omka = P.sb("r_omka", [128, 2], F32)
        hrk = P.sb("r_hrk", [128, 2], F32)
        ln = P.sb("r_ln", [128, 2, 2], F32)
        w2 = P.sb("r_w2", [64, 2, 256], BF16)
        a2 = P.sb("r_a2", [64, 2, 256], BF16)
        msk = P.sb("r_msk", [128, 2, 4, 64], F32)
        id2 = P.sb("r_id2", [128, 64], F32)
        bones = P.sb("r_bones", [128, 128], F32)
        m0 = P.sb("r_m0", [128, GS * 64], F32)
        twd = P.sb("r_twd", [64, L], BF16, blk=512)
        adx = P.sb("r_adx", [64, L], BF16, blk=512)
        sh32 = P.sb("r_sh32", [128, L], F32, blk=512)
        xp = P.sb("r_xp", [128, L + 2], F32)
        P.dma(mu[:, :, 0:2], k.rmu_d[l])
        P.dma(pv[:], k.rpv_d[l])
        P.dma(ln[:], k.rln_d[l])
        with scope(k):
            w2f = P.sb("r_w2f", [64, 2, 256], F32)
            a2f = P.sb("r_a2f", [64, 2, 256], F32)
            P.dma(w2f[:], k.rw2_d[l])
            P.dma(a2f[:], k.ra2_d[l])
            P.cp(w2[:], w2f[:], eng="act")
            P.cp(a2[:], a2f[:], eng="act")
        P.dma(msk[:], k.rmsk_d[:])
        P.dma(id2[:], k.gid2_d[:])
        P.memset(bones[:], 0.0)
        P.memset(bones[0:64, 0:64], 1.0)
        P.memset(bones[64:128, 64:128], 1.0)
        P.memset(m0[:], 1.0)
        P.memset(m0[:, 0:GS * 64:64], 0.0)
        P.memset(xp[:, 0:1], 0.0)
        P.memset(xp[:, L + 1:L + 2], 0.0)
        P.tt(mu[:, :, 2], mu[:, :, 0], mu[:, :, 1], ALU.add)
        P.ts(mu[:, :, 2], mu[:, :, 2], -1.0, ALU.mult, 1.0, ALU.add)
        P.ts(omka[:], pv[:, 5, :], -1.0, ALU.mult, 1.0, ALU.add)
        P.ts(hrk[:], pv[:, 6, :], 0.5, ALU.mult)

        def shifted(name, ci, dst, np_=128, fn=None):
            pa = proj(k, l, name, alt=(ci % 2 == 1))
            for tb in range(4):
                P.cp(xp[0:np_, 1 + tb * 512:1 + (tb + 1) * 512], pa[tb][0:np_, :], eng=("act" if tb % 2 else "dve"))
            t_ = sh32[0:np_, :]
            P.ts(t_, xp[0:np_, 1:L + 1], mu[0:np_, ci, 2:3], ALU.mult)
            P.stt(t_, xp[0:np_, 0:L], mu[0:np_, ci, 0:1], t_, ALU.mult, ALU.add)
            if fn is None:
                P.stt(dst[:], xp[0:np_, 2:L + 2], mu[0:np_, ci, 1:2], t_, ALU.mult, ALU.add)
            else:
                P.stt(t_, xp[0:np_, 2:L + 2], mu[0:np_, ci, 1:2], t_, ALU.mult, ALU.add)
                P.act(dst[:], t_, fn)

        shifted("rwd", 6, twd, 64, AF.Tanh)
        shifted("rad", 7, adx, 64)
        R_ = P.sb("r_R", [128, L], BF16, blk=512)
        KX = P.sb("r_KX", [128, L], BF16, blk=512)
        V_ = P.sb("r_V", [128, L], BF16, blk=512)
        KK = P.sb("r_KK", [128, L], BF16, blk=512)
        Vt = P.sb("r_Vt", [128, NCK, 64], BF16, blk=512)
        KS = P.sb("r_KS", [128, L], F32, blk=512)
        Y = xp[:, 1:L + 1]
        sq = P.sb("r_sq", [128, 512], F32)
        rn = P.sb("r_rn", [128, 512], F32)
        CH = [rwkv_tiles(k, e) for e in range(2)]

        class _V:
            def __init__(s_, t):
                s_.t = t

            def __getitem__(s_, key):
                return s_.t[:].rearrange("p a b -> p (a b)")[key]
        sq2, rn2 = _V(CH[0]["lw"]), _V(CH[0]["a"])
        for p in range(2):
            shifted("rr%d" % p, 0 + p, R_)
            shifted("rk%d" % p, 2 + p, KX)
            shifted("rv%d" % p, 4 + p, V_)
            P.ts(sh32[:], KX[:], pv[:, 4, p:p + 1], ALU.mult)

            def fin_k(tb, r):
                P.tt(KK[:, tb * 512:(tb + 1) * 512], sh32[:, tb * 512:(tb + 1) * 512], r[:], ALU.mult)
            norm_pipe(k, 4, lambda tb: sh32[:, tb * 512:(tb + 1) * 512], (sq, sq2), (rn, rn2), (k.PB[2], k.PB[3]),
                      bones, -0.5, 1.0, 1e-6, fin_k)
            for grp in range(4):
                ps = k.PB[grp % 2]
                for c in range(8):
                    ck = grp * 8 + c
                    tr2(P, ps, c, V_[:, ck * 64:(ck + 1) * 64], k.identb)
                P.cp(Vt[:, grp * 8:(grp + 1) * 8, :], v3(ps), eng=("act" if grp % 2 else "dve"))
            P.memset(xp[:, 1:L + 1], 0.0, eng="pool")
            P.memset(KS[:], 0.0, eng="pool")
            T = dict(pv=pv, omka=omka, w2=w2, a2=a2, msk=msk, id2=id2, m0=m0, twd=twd, adx=adx,
                     R=R_, KX=KX, KK=KK, Vt=Vt, KS=KS, Y=Y)
            run_interleaved([rwkv_chain(k, p, e, T, CH[e]) for e in range(2)])
            for tb in range(4):
                sl = slice(tb * 512, (tb + 1) * 512)
                P.mm(k.PB[tb % 2][:, :], bones[:], Y[:, sl])
                P.stt(Y[:, sl], k.PB[tb % 2][:, :], -1.0 / 64, Y[:, sl], ALU.mult, ALU.add)

            def fin_y(tb, r):
                sl = slice(tb * 512, (tb + 1) * 512)
                P.tt(Y[:, sl], Y[:, sl], r[:], ALU.mult)
                P.ts(Y[:, sl], Y[:, sl], ln[:, 0, p:p + 1], ALU.mult, ln[:, 1, p:p + 1], ALU.add)
            norm_pipe(k, 4, lambda tb: Y[:, tb * 512:(tb + 1) * 512], (sq, sq2), (rn, rn2), (k.PB[2], k.PB[3]),
                      bones, -0.5, 1.0 / 64, RW_EPS, fin_y)
            for tb in range(4):
                sl = slice(tb * 512, (tb + 1) * 512)
                s_ = (sq, sq2)[tb % 2]
                r_ = (rn, rn2)[tb % 2]
                P.tt(s_[:], R_[:, sl], KS[:, sl], ALU.mult)
                P.ts(s_[:], s_[:], hrk[:, p:p + 1], ALU.mult, eng="pool")
                P.mm(k.PB[tb % 2][:, :], bones[:], s_[:])
                P.tt(r_[:], k.PB[tb % 2][:, :], V_[:, sl], ALU.mult)
                P.tt(mixc(k, 6 + p)[:, sl], Y[:, sl], r_[:], ALU.add)


RW_F32 = ("lw", "a", "km", "b", "cl", "e1", "e2", "dend", "AcT", "Tg")
RW_BF16 = ("kap", "rt", "kt_", "bt_", "ke", "be", "kapT", "keT", "nbeT", "N", "Akv", "Brk", "nBrb", "Rm",
           "nA", "nB", "nC", "nD", "X0", "P0", "WkT", "Wk", "Tgb", "Pg")


def rwkv_tiles(k, e):
    P = k.P
    G = {n: P.sb("r%d_%s" % (e, n), [128, GS, 64], F32) for n in RW_F32}
    for n in RW_BF16:
        G[n] = P.sb("r%d_%s" % (e, n), [128, GS, 64], BF16)
    G["gC"] = P.sb("r%d_gC" % e, [128, GS], F32)
    G["Tcar"] = P.sb("r%d_Tcar" % e, [128, 64], F32)
    return G


def rwkv_chain(k, p, e, T, G):
    P = k.P
    pv, omka, w2, a2, msk, id2, m0, twd, adx, R_, KX, KK, Vt, KS, Y = (T[n] for n in (
        "pv", "omka", "w2", "a2", "msk", "id2", "m0", "twd", "adx", "R", "KX", "KK", "Vt", "KS", "Y"))
    B = k.PA if e == 0 else k.PB
    W = GS * 64
    e_ = 63 if e == 0 else 0
    idb = bcm(id2[:, :], GS)
    gC, Tcar = G["gC"], G["Tcar"]

    def f2(t):
        return t[:].rearrange("p a b -> p (a b)")

    def w3(ps):
        return v3(ps, GS)
    P.memset(Tcar[:], 0.0)
    gorder = range(NG) if e == 0 else range(NG - 1, -1, -1)
    pc = slice(p * 128, (p + 1) * 128)
    for grp in gorder:
        sl = slice(grp * W, (grp + 1) * W)
        c0 = grp * GS
        P.mm(B[0][:, 0:W], w2[:, e, pc], twd[:, sl])
        P.mm(B[1][:, 0:W], a2[:, e, pc], adx[:, sl])
        P.act(f2(G["lw"]), B[0][:, 0:W], AF.Sigmoid, bias=pv[:, 0 + e, p:p + 1])
        P.ts(f2(G["lw"]), f2(G["lw"]), -DEC, ALU.mult, eng="pool")
        P.act(f2(G["a"]), B[1][:, 0:W], AF.Sigmoid, bias=pv[:, 2 + e, p:p + 1])
        P.ts(f2(G["km"]), f2(G["a"]), pv[:, 5, p:p + 1], ALU.mult, omka[:, p:p + 1], ALU.add)
        P.tt(f2(G["km"]), f2(G["km"]), KX[:, sl], ALU.mult)
        P.tt(f2(G["b"]), f2(G["a"]), KK[:, sl], ALU.mult, eng="pool")
        P.tt(KS[:, sl], KS[:, sl], f2(G["km"]), ALU.add, eng="pool")
        P.scan(f2(G["cl"]), m0[:], f2(G["lw"]), 0.0, ALU.mult, ALU.add)
        if e == 1:
            P.tt(G["e1"][:], bc3(G["cl"][:, :, 63], 64), G["cl"][:], ALU.subtract)
            P.tt(G["cl"][:], G["e1"][:], G["lw"][:], ALU.add)
        yield
        P.act(G["e1"][:], G["cl"][:], AF.Exp)
        P.act(G["e2"][:], G["cl"][:], AF.Exp, scale=-1.0)
        P.tt(f2(G["rt"]), f2(G["e1"]), R_[:, sl], ALU.mult)
        P.tt(G["kt_"][:], G["e2"][:], G["km"][:], ALU.mult, eng="pool")
        P.tt(G["bt_"][:], G["e2"][:], G["b"][:], ALU.mult)
        P.tt(G["dend"][:], G["cl"][:], G["lw"][:], ALU.subtract, eng="pool")
        P.act(G["dend"][:], G["dend"][:], AF.Exp)
        P.tt(f2(G["kap"]), f2(G["dend"]), KK[:, sl], ALU.mult)
        P.cp(gC[:], G["e1"][:, :, e_], eng="pool")
        P.tt(G["dend"][:], bc3(G["cl"][:, :, e_], 64), G["cl"][:], ALU.subtract, eng="pool")
        P.act(G["dend"][:], G["dend"][:], AF.Exp)
        P.tt(G["ke"][:], G["dend"][:], G["km"][:], ALU.mult)
        P.tt(G["be"][:], G["dend"][:], G["b"][:], ALU.mult, eng="pool")
        yield
        for src, dst, sc in ((G["kap"], G["kapT"], 1.0), (G["ke"], G["keT"], 1.0), (G["be"], G["nbeT"], -1.0)):
            ps = B[0] if sc == 1.0 and src is G["kap"] else (B[1] if sc == 1.0 else B[2])
            for c in range(GS):
                tr2(P, ps, c, src[:, c, :], k.identb)
            if sc == 1.0:
                P.cp(dst[:], w3(ps), eng="act")
            else:
                P.ts(dst[:], w3(ps), -1.0, ALU.mult)
        yield
        for c in range(GS):
            mm2(P, B[0], c, G["bt_"][:, c, :], G["kap"][:, c, :])
            mm2(P, B[1], c, G["kt_"][:, c, :], G["kap"][:, c, :])
            mm2(P, B[2], c, G["kt_"][:, c, :], G["rt"][:, c, :])
            mm2(P, B[3], c, G["bt_"][:, c, :], G["rt"][:, c, :])
        ms = bcm(msk[:, e, 0, :], GS)
        mi = bcm(msk[:, e, 1, :], GS)
        nms = bcm(msk[:, e, 2, :], GS)
        nmi = bcm(msk[:, e, 3, :], GS)
        P.tt(G["N"][:], w3(B[0]), nms, ALU.mult)
        P.tt(G["Akv"][:], w3(B[1]), ms, ALU.mult)
        P.tt(G["Brk"][:], w3(B[2]), mi, ALU.mult)
        P.tt(G["nBrb"][:], w3(B[3]), nmi, ALU.mult)
        yield
        for _ in neumann2(k, G["N"], G["Rm"], (G["nA"], G["nB"], G["nC"], G["nD"]), (B[0], B[1], B[2]), id2, GS):
            yield
        for c in range(GS):
            mm2(P, B[0], c, G["Akv"][:, c, :], Vt[:, c0 + c, :])
        P.cp(G["X0"][:], w3(B[0]), eng="act")
        yield
        for c in range(GS):
            mm2(P, B[0], c, G["Rm"][:, c, :], G["X0"][:, c, :])
            mm2(P, B[1], c, G["kapT"][:, c, :], G["Rm"][:, c, :])
            mm2(P, B[2], c, G["Rm"][:, c, :], G["kapT"][:, c, :])
        P.cp(G["P0"][:], w3(B[0]), eng="act")
        P.cp(G["WkT"][:], w3(B[1]), eng="dve")
        P.cp(G["Wk"][:], w3(B[2]), eng="act")
        yield
        for c in range(GS):
            mm2(P, B[3], c, G["Wk"][:, c, :], G["nbeT"][:, c, :])
        P.tt(G["AcT"][:], idb, bc3(gC[:], 64), ALU.mult, eng="pool")
        P.tt(G["AcT"][:], G["AcT"][:], w3(B[3]), ALU.add)
        yield
        corder = list(range(GS)) if e == 0 else list(range(GS - 1, -1, -1))
        Tg = G["Tg"]
        for n, c in enumerate(corder):
            if n == 0:
                P.cp(Tg[:, c, :], Tcar[:], eng="pool")
            ps = B[2 + n % 2]
            mm2(P, ps, 0, G["AcT"][:, c, :], Tg[:, c, :], start=True, stop=False)
            mm2(P, ps, 0, G["keT"][:, c, :], Vt[:, c0 + c, :], start=False, stop=False)
            mm2(P, ps, 0, G["nbeT"][:, c, :], G["P0"][:, c, :], start=False, stop=True)
            dst = Tcar[:] if n == GS - 1 else Tg[:, corder[n + 1], :]
            P.cp(dst, ps[:, 0:64], eng="act")
            yield
        P.cp(G["Tgb"][:], Tg[:], eng="pool")
        for c in range(GS):
            mm2(P, B[0], c, G["WkT"][:, c, :], G["Tgb"][:, c, :])
        P.tt(G["Pg"][:], w3(B[0]), G["P0"][:], ALU.add)
        yield
        for c in range(GS):
            mm2(P, B[1], c, Vt[:, c0 + c, :], G["Brk"][:, c, :], start=True, stop=False)
            mm2(P, B[1], c, G["Tgb"][:, c, :], G["rt"][:, c, :], start=False, stop=False)
            mm2(P, B[1], c, G["Pg"][:, c, :], G["nBrb"][:, c, :], start=False, stop=True)
        P.tt(Y[:, sl], Y[:, sl], B[1][:, 0:W], ALU.add)
        yield


_CACHE = {}


def kernel(**inputs):
    inp = {k_: np.asarray(v) for k_, v in inputs.items()}
    if "k" not in _CACHE:
        _CACHE["k"] = build()
    k = _CACHE["k"]
    B = inp["x"].shape[0]
    base = host_inputs(inp, 0)
    in_maps = []
    for b in range(B):
        m = dict(base)
        m["x"] = np.ascontiguousarray(inp["x"][b], dtype=np.float32)
        m["pos"] = np.ascontiguousarray(inp["positions"][b].reshape(1, L).astype(np.int32))
        in_maps.append(m)
    res = run_bass_kernel_spmd(k.nc, in_maps, core_ids=list(range(B)))
    return np.stack([np.asarray(r["out"], dtype=np.float32) for r in res.results], axis=0)
```

```python
import numpy as np
import concourse.bass as bass
import concourse.mybir as mybir
from concourse.bass_utils import run_bass_kernel_spmd
from contextlib import ExitStack

F32 = mybir.dt.float32
BF16 = mybir.dt.bfloat16
I32 = mybir.dt.int32
AF = mybir.ActivationFunctionType
ALU = mybir.AluOpType
AX = mybir.AxisListType
DTSIZE = {F32: 4, BF16: 2, I32: 4}


class _Op:
    __slots__ = ("eng", "emit", "deps", "idx", "needed", "sigval", "dsem", "dval", "isdma")

    def __init__(self, eng, emit, isdma=False):
        self.eng = eng
        self.emit = emit
        self.deps = []
        self.idx = -1
        self.needed = False
        self.sigval = 0
        self.dsem = None
        self.dval = 0
        self.isdma = isdma


class _Blk:
    __slots__ = ("w", "r")

    def __init__(self):
        self.w = None
        self.r = {}


class Prog:
    ENGS = ("pe", "dve", "act", "pool", "sp")
    NDMA = 48
    NHW = 32

    def __init__(self, nc, stack):
        self.nc = nc
        self.stack = stack
        self.ops = {e: [] for e in self.ENGS}
        self.track = {}
        self.seen = {e: {} for e in self.ENGS}
        self.seen_dma = {e: set() for e in self.ENGS}
        self.dma_last = [None] * self.NDMA
        self.dma_uses = [0] * self.NDMA
        self.dma_rr = 0
        self.dma_rr_sw = 0
        self.ndma_ops = 0
        self.untracked = set()
        self.out_dmas = []
        self.dma_pending = []
        self.last_compute = {}

    def sb(self, name, shape, dtype=F32, blk=None):
        self.uid = getattr(self, "uid", 0) + 1
        name = "s%d_%s" % (self.uid, name)
        t = self.stack.enter_context(self.nc.sbuf_tensor(name, list(shape), dtype))
        self._register(name, shape, dtype, blk)
        return t

    def ps(self, name, shape=(128, 512), dtype=F32, blk=None):
        self.uid = getattr(self, "uid", 0) + 1
        name = "p%d_%s" % (self.uid, name)
        t = self.stack.enter_context(self.nc.psum_tensor(name, list(shape), dtype))
        self._register(name, shape, dtype, blk)
        return t

    def _register(self, name, shape, dtype, blk):
        row = int(np.prod(shape[1:])) * DTSIZE[dtype]
        bb = row if blk is None else blk * DTSIZE[dtype]
        nb = (row + bb - 1) // bb
        self.track[name] = (bb, row, [_Blk() for _ in range(nb)])

    def dram_track(self, name, total_bytes, blk_bytes):
        nb = (total_bytes + blk_bytes - 1) // blk_bytes
        self.track[name] = (blk_bytes, -1, [_Blk() for _ in range(nb)])

    def _blocks(self, ap):
        name = ap.tensor.name
        if name not in self.track:
            return ()
        bb, row, blks = self.track[name]
        if len(blks) == 1:
            return blks
        ds = DTSIZE[ap.dtype]
        pat = ap.ap
        if row < 0:
            lo = hi = ap.offset
            for step, cnt in pat:
                ext = step * (cnt - 1)
                if ext < 0:
                    lo += ext
                else:
                    hi += ext
            return blks[(lo * ds) // bb:(hi * ds) // bb + 1]
        rowel = row // ds
        foff = ap.offset % rowel
        lo = hi = foff
        for step, cnt in pat[1:]:
            ext = step * (cnt - 1)
            if ext < 0:
                lo += ext
            else:
                hi += ext
        b0 = (lo * ds) // bb
        b1 = (hi * ds) // bb
        return blks[b0:b1 + 1]

    def _dep(self, x, y):
        if y is None or y is x:
            return
        e = x.eng
        if y.isdma:
            if id(y) in self.seen_dma[e]:
                return
            self.seen_dma[e].add(id(y))
            x.deps.append(y)
            return
        if y.eng == "pe" and e == "pe":
            return
        if y.idx <= self.seen[e].get(y.eng, -1):
            return
        self.seen[e][y.eng] = y.idx
        y.needed = True
        x.deps.append(y)

    def add(self, eng, emit, reads=(), writes=(), isdma=False):
        x = _Op(eng, emit, isdma)
        x.idx = len(self.ops[eng])
        rb = []
        for ap in reads:
            if ap is None or isinstance(ap, (int, float)):
                continue
            rb.extend(self._blocks(ap))
        wb = []
        for ap in writes:
            wb.extend(self._blocks(ap))
        for ap in reads:
            if ap is None or isinstance(ap, (int, float)) or not ap.tensor.name.startswith("p"):
                continue
            for b in self._blocks(ap):
                for key, y in b.r.items():
                    if key != eng:
                        self._dep(x, y)
        for b in rb:
            self._dep(x, b.w)
        for b in wb:
            self._dep(x, b.w)
            for y in b.r.values():
                self._dep(x, y)
        if isdma:
            if eng == "pool":
                s = self.NHW + self.dma_rr_sw
                self.dma_rr_sw = (self.dma_rr_sw + 1) % (self.NDMA - self.NHW)
            else:
                s = self.dma_rr
                self.dma_rr = (self.dma_rr + 1) % self.NHW
            self._dep(x, self.dma_last[s])
            self.dma_last[s] = x
            self.dma_uses[s] += 1
            x.dsem = s
            x.dval = 16 * self.dma_uses[s]
            self.ndma_ops += 1
        key = id(x) if isdma else eng
        for b in rb:
            b.r[key] = x
        for b in wb:
            b.w = x
            b.r = {}
        self.ops[eng].append(x)
        if isdma:
            self.dma_pending.append(x)
        else:
            self.last_compute[eng] = x
        return x

    def barrier(self):
        lasts = dict(self.last_compute)
        pend = list(self.dma_pending)
        self.dma_pending = []
        for e in self.ENGS:
            b = _Op(e, None)
            b.idx = len(self.ops[e])
            for e2, y in lasts.items():
                if e2 == e and e == "pe":
                    continue
                self._dep(b, y)
            for y in pend:
                self._dep(b, y)
            self.ops[e].append(b)

    def mm(self, out, lhsT, rhs, start=True, stop=True):
        return self.add("pe", lambda e: e.matmul(out, lhsT, rhs, start=start, stop=stop),
                        reads=(lhsT, rhs), writes=(out,))

    def tr(self, out, in_, ident):
        return self.add("pe", lambda e: e.transpose(out, in_, ident), reads=(in_, ident), writes=(out,))

    def tt(self, out, in0, in1, op, eng="dve"):
        return self.add(eng, lambda e: e.tensor_tensor(out, in0, in1, op), reads=(in0, in1), writes=(out,))

    def ts(self, out, in0, s1, op0, s2=None, op1=None, eng="dve", accum_out=None):
        kw = {}
        if eng == "pool" and op1 is None:
            if op0 == ALU.mult:
                s2, op1 = 0.0, ALU.add
            elif op0 == ALU.add:
                s2, op1 = 1.0, ALU.mult
        if op1 is not None:
            kw["op1"] = op1
        if accum_out is not None:
            kw["accum_out"] = accum_out
        w = (out,) if accum_out is None else (out, accum_out)
        return self.add(eng, lambda e: e.tensor_scalar(out, in0, s1, s2, op0, **kw),
                        reads=(in0, s1, s2), writes=w)

    def stt(self, out, in0, scalar, in1, op0, op1, accum_out=None):
        kw = {}
        if accum_out is not None:
            kw["accum_out"] = accum_out
        w = (out,) if accum_out is None else (out, accum_out)
        return self.add("dve", lambda e: e.scalar_tensor_tensor(out, in0, scalar, in1, op0, op1, **kw),
                        reads=(in0, scalar, in1), writes=w)

    def cp(self, out, in_, eng="dve"):
        if eng == "act":
            return self.add("act", lambda e: e.copy(out, in_), reads=(in_,), writes=(out,))
        return self.add(eng, lambda e: e.tensor_copy(out, in_), reads=(in_,), writes=(out,))

    def act(self, out, in_, func, bias=0.0, scale=1.0, accum_out=None):
        kw = {}
        if accum_out is not None:
            kw["accum_out"] = accum_out
        w = (out,) if accum_out is None else (out, accum_out)
        return self.add("act", lambda e: e.activation(out, in_, func, bias=bias, scale=scale, **kw),
                        reads=(in_, bias, scale), writes=w)

    def red(self, out, in_, op, axis=AX.X, eng="dve"):
        return self.add(eng, lambda e: e.tensor_reduce(out, in_, axis, op), reads=(in_,), writes=(out,))

    def recip(self, out, in_):
        return self.add("dve", lambda e: e.reciprocal(out, in_), reads=(in_,), writes=(out,))

    def rpow(self, out, in_, power, scale=1.0, bias=0.0):
        self.act(out, in_, AF.Ln, bias=bias, scale=scale)
        return self.act(out, out, AF.Exp, scale=power)

    def memset(self, ap, val, eng="dve"):
        return self.add(eng, lambda e: e.memset(ap, val), writes=(ap,))

    def scan(self, out, d0, d1, init, op0, op1):
        return self.add("dve", lambda e: e.tensor_tensor_scan(out, d0, d1, init, op0, op1),
                        reads=(d0, d1, init), writes=(out,))

    def dma(self, out, in_, eng="sp", is_output=False):
        x = self.add(eng, lambda e: e.dma_start(out=out, in_=in_), reads=(in_,), writes=(out,), isdma=True)
        if is_output:
            self.out_dmas.append(x)
        return x

    def finish(self):
        nc = self.nc
        fin = _Op("sp", None)
        fin.idx = len(self.ops["sp"])
        for y in self.out_dmas:
            self._dep(fin, y)
        self.ops["sp"].append(fin)
        sems = {}
        for e in ("pe", "dve", "act", "pool"):
            sems[e] = self.stack.enter_context(nc.semaphore("s_" + e))
        dsems = [self.stack.enter_context(nc.semaphore("d%d" % i)) for i in range(self.NDMA)]
        for e in ("pe", "dve", "act", "pool"):
            c = 0
            for x in self.ops[e]:
                if x.isdma:
                    continue
                if x.needed:
                    c += 1
                    x.sigval = c
            self.stats_sig = getattr(self, "stats_sig", {})
            self.stats_sig[e] = c
        ops = self.ops

        def replay(e, engobj):
            for x in ops[e]:
                for y in x.deps:
                    if y.isdma:
                        engobj.wait_ge(dsems[y.dsem], y.dval)
                    else:
                        engobj.wait_ge(sems[y.eng], y.sigval)
                if x.emit is None:
                    continue
                ins = x.emit(engobj)
                if x.isdma:
                    ins.then_inc(dsems[x.dsem], 16)
                elif x.needed:
                    ins.then_inc(sems[e], 1)

        with nc.Block() as block:
            @block.tensor
            def _(eng):
                replay("pe", eng)

            @block.vector
            def _(eng):
                replay("dve", eng)

            @block.scalar
            def _(eng):
                replay("act", eng)

            @block.gpsimd
            def _(eng):
                replay("pool", eng)

            @block.sync
            def _(eng):
                replay("sp", eng)


L = 2048
D = 1024
NT = L // 128
DEPTH = 2
N_IN = 3448
EPS = 1e-6

OFF = dict(gate=0, gdn_q=1024, gdn_k=1408, gdn_v=1792, gdn_a=2176, gdn_b=2188, mla_cq=2200, mla_ckv=2392,
           mla_kr=2520, rw_r=2552, rw_k=2808, rw_v=3064, rw_wd=3320, rw_ad=3384)


def chunk_table():
    ch = []
    for h in range(3):
        ch.append(("gq%d" % h, [(0, OFF["gdn_q"] + h * 128, 128)]))
        ch.append(("gk%d" % h, [(0, OFF["gdn_k"] + h * 128, 128)]))
        ch.append(("gv%d" % h, [(0, OFF["gdn_v"] + h * 128, 128)]))
    ch.append(("gab", [(0, OFF["gdn_a"], 6), (32, OFF["gdn_a"] + 6, 6), (64, OFF["gdn_b"], 6), (96, OFF["gdn_b"] + 6, 6)]))
    ch.append(("cq0", [(0, OFF["mla_cq"], 128)]))
    ch.append(("cq1", [(0, OFF["mla_cq"] + 128, 64)]))
    ch.append(("ckv", [(0, OFF["mla_ckv"], 128)]))
    ch.append(("kr", [(0, OFF["mla_kr"], 32)]))
    for i in range(2):
        ch.append(("rr%d" % i, [(0, OFF["rw_r"] + i * 128, 128)]))
        ch.append(("rk%d" % i, [(0, OFF["rw_k"] + i * 128, 128)]))
        ch.append(("rv%d" % i, [(0, OFF["rw_v"] + i * 128, 128)]))
    ch.append(("rwd", [(0, OFF["rw_wd"], 64)]))
    ch.append(("rad", [(0, OFF["rw_ad"], 64)]))
    for i in range(8):
        ch.append(("g%d" % i, [(0, OFF["gate"] + i * 128, 128)]))
    return ch


CHUNKS = chunk_table()
CH_IDX = {n: i for i, (n, _) in enumerate(CHUNKS)}
NCH = len(CHUNKS)


def host_win(w_in):
    out = np.zeros((DEPTH, NCH, 128, 8, 128), np.float32)
    for ci, (_, parts) in enumerate(CHUNKS):
        for dst, src, w in parts:
            blk = w_in[:, :, src:src + w].reshape(DEPTH, 8, 128, w)
            out[:, ci, :, :, dst:dst + w] = blk.transpose(0, 2, 1, 3)
    return out


class K:
    pass


def build(depth=DEPTH, mixers=("gdn", "mla", "rwkv"), dbg=False):
    nc = bass.Bass("TRN2", target_bir_lowering=False)
    k = K()
    k.nc = nc
    k.dbg = dbg
    k.dbg_outs = []
    k.cut = 99

    def din(name, shape, dt=F32):
        return nc.dram_tensor(name, list(shape), dt, kind="ExternalInput").ap()

    k.x_d = din("x", [L, D])
    k.win_d = din("win", [DEPTH, NCH, 128, 8, 128])
    k.normg_d = din("normg", [DEPTH, 128, 8])
    k.wout_d = din("wout", [DEPTH, 128, 8, 1024])
    k.fing_d = din("fing", [1, D])
    k.ident_d = din("ident", [128, 128])
    k.out_d = nc.dram_tensor("out", [L, D], F32, kind="ExternalOutput").ap()
    mla_decl(k)
    gdn_decl(k)
    rwkv_decl(k)

    with ExitStack() as st:
        P = Prog(nc, st)
        k.P = P
        k.xscr = nc.dram_tensor("xscr", [L, D], F32, kind="Internal").ap()
        P.dram_track("xscr", L * D * 4, 128 * D * 4)
        k.hT = P.sb("hT", [128, 8, L], BF16, blk=512)
        k.ident = P.sb("ident", [128, 128], F32)
        k.identb = P.sb("identb", [128, 128], BF16)
        k.normg = P.sb("normg", [128, DEPTH, 8], F32)
        k.wst = [P.sb("wst%d" % i, [128, 8, 128], F32) for i in range(2)]
        k.wbf = [P.sb("wbf%d" % i, [128, 8, 128], BF16) for i in range(2)]
        k.wrr = 0
        k.PAW = [P.ps("paw%d" % i, [128, 1024], F32, blk=512) for i in range(2)]
        k.PA = [k.PAW[i // 2][:, (i % 2) * 512:(i % 2 + 1) * 512] for i in range(4)]
        k.PB = [P.ps("pb%d" % i, [128, 512], F32) for i in range(4)]

        P.dma(k.ident[:], k.ident_d[:])
        P.cp(k.identb[:], k.ident[:])
        for l in range(DEPTH):
            P.dma(k.normg[:, l, :], k.normg_d[l])

        for l in range(depth):
            phase_a(k, l)
            with scope(k):
                k.mix_r = P.sb("mix_r", [128, 2, L], BF16, blk=512)
                if "rwkv" in mixers:
                    rwkv_phase(k, l)
                else:
                    P.memset(k.mix_r[:].rearrange("p a b -> p (a b)"), 1.0)
                with scope(k):
                    k.mix_m = P.sb("mix_m", [128, 3, L], BF16, blk=512)
                    if "mla" in mixers:
                        mla_phase(k, l)
                    else:
                        P.memset(k.mix_m[:].rearrange("p a b -> p (a b)"), 1.0)
                    with scope(k):
                        k.mix_g = P.sb("mix_g", [128, 3, L], BF16, blk=512)
                        if "gdn" in mixers:
                            gdn_phase(k, l)
                        else:
                            P.memset(k.mix_g[:].rearrange("p a b -> p (a b)"), 1.0)
                        if k.dbg:
                            for nm, t_, n_ in (("g", k.mix_g, 3), ("m", k.mix_m, 3), ("r", k.mix_r, 2)):
                                dump(k, "mix_%s%d" % (nm, l), t_[:].rearrange("p a b -> p (a b)"), [128, n_ * L])
                        phase_z(k, l, last=(l == depth - 1))
        P.finish()
        print("ops:", {e: len(v) for e, v in P.ops.items()}, "sig:", P.stats_sig, "dma:", P.ndma_ops)
    return k


def mixc(k, c):
    if c < 3:
        return k.mix_g[:, c, :]
    if c < 6:
        return k.mix_m[:, c - 3, :]
    return k.mix_r[:, c - 6, :]


def dump(k, name, ap, shape=None):
    if not k.dbg:
        return
    P = k.P
    shape = list(ap.shape) if shape is None else shape
    d = k.nc.dram_tensor("dbg_" + name, shape, ap.dtype, kind="ExternalOutput").ap()
    P.dma(d[:] if len(shape) == 2 else d, ap, is_output=True)
    k.dbg_outs.append("dbg_" + name)


def scope(k):
    class _S:
        def __enter__(s):
            s.old = k.P.stack
            s.st = ExitStack()
            s.st.__enter__()
            k.P.stack = s.st
            return s

        def __exit__(s, *a):
            k.P.barrier()
            k.P.stack = s.old
            s.st.__exit__(*a)
            return False
    return _S()


def phase_a(k, l):
    P = k.P
    with scope(k):
        ssq = P.sb("a_ssq", [128, NT])
        rs = P.sb("a_rs", [128, NT])
        rstd = P.sb("a_rstd", [128, NT])
        junk = [P.sb("a_junk%d" % i, [128, D], BF16) for i in range(2)]
        xs = [P.sb("a_xs%d" % i, [128, D], BF16) for i in range(2)]
        xin = [P.sb("a_xin%d" % i, [128, D], F32) for i in range(3)]
        src = k.x_d if l == 0 else k.xscr

        def stage1(tt):
            b = tt % 2
            xt_ = xin[tt % 3]
            P.dma(xt_[:], src[tt * 128:(tt + 1) * 128, :])
            P.act(junk[b][:], xt_[:], AF.Square, accum_out=ssq[:, tt:tt + 1])
            P.act(rs[:, tt:tt + 1], ssq[:, tt:tt + 1], AF.Sqrt, bias=EPS, scale=1.0 / D)
            P.recip(rstd[:, tt:tt + 1], rs[:, tt:tt + 1])
            P.ts(xs[b][:], xt_[:], rstd[:, tt:tt + 1], ALU.mult)

        def stage2(tt):
            b = tt % 2
            pt = k.PB[b][:].bitcast(BF16)
            for dc in range(8):
                P.tr(pt[:, dc * 128:(dc + 1) * 128], xs[b][:, dc * 128:(dc + 1) * 128], k.identb[:])
            P.cp(k.hT[:, :, tt * 128:(tt + 1) * 128], pt[:].rearrange("p (a b) -> p a b", a=8),
                 eng=("act" if tt % 2 == 0 else "dve"))
        stage1(0)
        for tt in range(NT):
            if tt + 1 < NT:
                stage1(tt + 1)
            stage2(tt)


def proj(k, l, name, alt=False):
    P = k.P
    BK = k.PB if alt else k.PA
    ci = CH_IDX[name]
    b = k.wrr
    k.wrr ^= 1
    P.dma(k.wst[b][:], k.win_d[l, ci], eng="sp")
    gb = k.normg[:, l, :].unsqueeze(2).broadcast_to([128, 8, 128])
    P.tt(k.wbf[b][:], k.wst[b][:], gb, ALU.mult, eng="pool")
    for tb in range(4):
        for dc in range(8):
            P.mm(BK[tb][:, :], k.wbf[b][:, dc, :], k.hT[:, dc, tb * 512:(tb + 1) * 512], start=(dc == 0), stop=(dc == 7))
    return BK


ZQ = "act"


def phase_z(k, l, last):
    P = k.P
    with scope(k):
        wst = P.sb("z_wst", [128, 8, 512], F32)
        wob = P.sb("z_wob", [128, 8, 1024], BF16, blk=512)
        sg = [P.sb("z_sg%d" % i, [128, L], BF16, blk=512) for i in range(2)]
        for nb in range(2):
            P.dma(wst[:], k.wout_d[l, :, :, nb * 512:(nb + 1) * 512])
            P.cp(wob[:, :, nb * 512:(nb + 1) * 512], wst[:], eng="act")
        for gc in range(8):
            pa = proj(k, l, "g%d" % gc, alt=(gc % 2 == 1))
            s = sg[gc % 2]
            for tb in range(4):
                P.act(s[:, tb * 512:(tb + 1) * 512], pa[tb][:, :], AF.Silu)
                mc = mixc(k, gc)[:, tb * 512:(tb + 1) * 512]
                P.tt(mc, mc, s[:, tb * 512:(tb + 1) * 512], ALU.mult)
        xin = [P.sb("z_xin%d" % i, [128, D], F32) for i in range(3)]
        src = k.x_d if l == 0 else k.xscr
        if last:
            ssq = P.sb("f_ssq", [128, NT])
            rs = P.sb("f_rs", [128, NT])
            rstd = P.sb("f_rstd", [128, NT])
            junk = [P.sb("f_junk%d" % i, [128, D], BF16) for i in range(2)]
            gf = P.sb("f_g", [128, D])
            ot = [P.sb("f_o%d" % i, [128, D]) for i in range(2)]
            P.dma(gf[:], k.fing_d[0:1, :].partition_broadcast(128))
        def za(tt):
            xt_ = xin[tt % 3]
            P.dma(xt_[:], src[tt * 128:(tt + 1) * 128, :])
            for nb in range(2):
                ps = k.PB[(tt * 2 + nb) % 4]
                for kc in range(8):
                    P.mm(ps[:, :], mixc(k, kc)[:, tt * 128:(tt + 1) * 128], wob[:, kc, nb * 512:(nb + 1) * 512],
                         start=(kc == 0), stop=(kc == 7))
                xs = xt_[:, nb * 512:(nb + 1) * 512]
                P.tt(xs, xs, ps[:, :], ALU.add)
            if not last:
                P.dma(k.xscr[tt * 128:(tt + 1) * 128, :], xt_[:], eng=ZQ)
            else:
                b = tt % 2
                P.act(junk[b][:], xt_[:], AF.Square, accum_out=ssq[:, tt:tt + 1])
                P.act(rs[:, tt:tt + 1], ssq[:, tt:tt + 1], AF.Sqrt, bias=EPS, scale=1.0 / D)

        def zb(tt):
            if last:
                xt_ = xin[tt % 3]
                b = tt % 2
                P.recip(rstd[:, tt:tt + 1], rs[:, tt:tt + 1])
                P.stt(ot[b][:], xt_[:], rstd[:, tt:tt + 1], gf[:], ALU.mult, ALU.mult)
                P.dma(k.out_d[tt * 128:(tt + 1) * 128, :], ot[b][:], eng=ZQ, is_output=True)
        za(0)
        for tt in range(NT):
            if tt + 1 < NT:
                za(tt + 1)
            zb(tt)


def host_inputs(inp, b):
    m = {}
    m["x"] = np.ascontiguousarray(inp["x"][b])
    m["win"] = host_win(inp["w_in"])
    m["normg"] = np.ascontiguousarray(inp["norm_g"].reshape(DEPTH, 8, 128).transpose(0, 2, 1))
    m["wout"] = np.ascontiguousarray(inp["w_out"].reshape(DEPTH, 8, 128, 1024).transpose(0, 2, 1, 3))
    m["fing"] = np.ascontiguousarray(inp["final_norm_g"].reshape(1, D))
    m["ident"] = np.eye(128, dtype=np.float32)
    host_mla(inp, b, m)
    host_gdn(inp, b, m)
    host_rwkv(inp, b, m)
    return m


TWO_PI = 2.0 * np.pi


def host_mla(inp, b, m):
    half = 16
    inv_freq = (10000.0 ** (-np.arange(half, dtype=np.float32) / half)).astype(np.float32)
    invf = np.zeros((32, 1), np.float32)
    invf[:, 0] = np.tile(inv_freq, 2) / np.float32(TWO_PI)
    m["invf"] = invf
    rm = np.zeros((32, 32), np.float32)
    for i in range(16):
        rm[i, i + 16] = -1.0
        rm[i + 16, i] = 1.0
    m["rmT"] = np.ascontiguousarray(rm.T)
    m["pos"] = np.ascontiguousarray(inp["positions"][b].reshape(1, L).astype(np.int32))
    wuq = inp["mla_w_uq"]
    o = np.zeros((DEPTH, 128, 2, 6, 128), np.float32)
    for h in range(6):
        nope = wuq[:, :, h * 96:h * 96 + 64]
        rope = wuq[:, :, h * 96 + 64:h * 96 + 96]
        o[:, :, 0, h, 64:128] = nope[:, 0:128]
        o[:, 0:64, 1, h, 64:128] = nope[:, 128:192]
        o[:, :, 0, h, 0:32] = rope[:, 0:128]
        o[:, 0:64, 1, h, 0:32] = rope[:, 128:192]
    m["wuq"] = o
    gq = np.zeros((DEPTH, 128, 2), np.float32)
    gq[:, :, 0] = inp["mla_q_norm_g"][:, 0:128]
    gq[:, 0:64, 1] = inp["mla_q_norm_g"][:, 128:192]
    m["gq"] = gq
    wukv = inp["mla_w_ukv"]
    wk = np.zeros((DEPTH, 128, 6, 128), np.float32)
    wv = np.zeros((DEPTH, 128, 6, 64), np.float32)
    for h in range(6):
        wk[:, :, h, 64:128] = wukv[:, :, h * 128:h * 128 + 64]
        wv[:, :, h, :] = wukv[:, :, h * 128 + 64:h * 128 + 128]
    m["wuk"] = wk
    m["wuv"] = wv
    m["gkv"] = np.ascontiguousarray(inp["mla_kv_norm_g"].reshape(DEPTH, 128, 1))


def mla_decl(k):
    nc = k.nc

    def din(name, shape, dt=F32):
        return nc.dram_tensor(name, list(shape), dt, kind="ExternalInput").ap()
    k.invf_d = din("invf", [32, 1])
    k.rmT_d = din("rmT", [32, 32])
    k.pos_d = din("pos", [1, L], I32)
    k.wuq_d = din("wuq", [DEPTH, 128, 2, 6, 128])
    k.gq_d = din("gq", [DEPTH, 128, 2])
    k.wuk_d = din("wuk", [DEPTH, 128, 6, 128])
    k.wuv_d = din("wuv", [DEPTH, 128, 6, 64])
    k.gkv_d = din("gkv", [DEPTH, 128, 1])


def latent_norm(k, l, names, nfeat, outs, ones):
    P = k.P
    sq = [P.sb("ln_sq%d" % i, [128, 512]) for i in range(2)]
    rq = [P.sb("ln_rq%d" % i, [128, 512]) for i in range(2)]
    n = len(names)
    for i, nm in enumerate(names):
        pa = proj(k, l, nm)
        for tb in range(4):
            s = sq[tb % 2]
            sk = ""
            if "a" not in sk:
                P.act(s[:], pa[tb][:, :], AF.Square)
            if "c" not in sk:
                P.cp(outs[i][:, tb * 512:(tb + 1) * 512], pa[tb][:, :], eng="dve")
            if "m" not in sk:
                P.mm(k.PB[tb][:, :], ones[:], s[:], start=(i == 0), stop=(i == n - 1))
    c2 = 9
    if c2 < 1:
        return
    for tb in range(4):
        r = rq[tb % 2]
        P.rpow(r[:], k.PB[tb][:, :], -0.5, scale=1.0 / nfeat, bias=EPS)
        for i in range(n):
            o = outs[i][:, tb * 512:(tb + 1) * 512]
            P.tt(o, o, r[:], ALU.mult)


def mla_phase(k, l):
    P = k.P
    SC = 96.0 ** -0.5
    with scope(k):
        cqn0 = P.sb("m_cqn0", [128, L], BF16, blk=512)
        cqn1 = P.sb("m_cqn1", [128, L], BF16, blk=512)
        ckvn = P.sb("m_ckvn", [128, L], BF16, blk=512)
        krope = P.sb("m_krope", [32, L], BF16, blk=512)
        cos2 = P.sb("m_cos2", [32, L], BF16, blk=512)
        sin2 = P.sb("m_sin2", [32, L], BF16, blk=512)
        wq = P.sb("m_wq", [128, 2, 6, 128], BF16)
        wk = P.sb("m_wk", [128, 6, 128], BF16)
        wv = P.sb("m_wv", [128, 6, 64], BF16)
        ones = P.sb("m_ones", [128, 128], F32)
        onesk = P.sb("m_onesk", [128, 128], F32)
        rmT = P.sb("m_rmT", [32, 32], F32)
        P.memset(ones[:], 1.0)
        P.memset(onesk[:], 1.0)
        P.memset(onesk[32:64, :], 0.0)
        P.dma(rmT[:], k.rmT_d[:])
        with scope(k):
            st = P.sb("m_st", [128, 2, 6, 128], F32)
            g = P.sb("m_g", [128, 4], F32)
            P.dma(st[:], k.wuq_d[l])
            P.dma(g[:, 0:2], k.gq_d[l])
            P.dma(g[:, 2:3], k.gkv_d[l])
            for kc in range(2):
                P.ts(wq[:, kc].rearrange("p a b -> p (a b)"), st[:, kc].rearrange("p a b -> p (a b)"),
                     g[:, kc:kc + 1], ALU.mult)
            st2 = P.sb("m_st2", [128, 6, 128], F32)
            P.dma(st2[:], k.wuk_d[l])
            P.ts(wk[:].rearrange("p a b -> p (a b)"), st2[:].rearrange("p a b -> p (a b)"), g[:, 2:3], ALU.mult)
            st3 = P.sb("m_st3", [128, 6, 64], F32)
            P.dma(st3[:], k.wuv_d[l])
            P.ts(wv[:].rearrange("p a b -> p (a b)"), st3[:].rearrange("p a b -> p (a b)"), g[:, 2:3], ALU.mult)
        if k.cut < 1:
            return
        with scope(k):
            latent_norm(k, l, ["cq0", "cq1"], 192, [cqn0, cqn1], ones)
            latent_norm(k, l, ["ckv"], 128, [ckvn], ones)
        if k.cut < 2:
            return
        with scope(k):
            invf = P.sb("m_invf", [32, 1], F32)
            P.dma(invf[:], k.invf_d[:])
            pa = proj(k, l, "kr")

            def rope_blk(tb):
                sl = slice(tb * 512, (tb + 1) * 512)
                posi = P.sb("m_posi%d" % tb, [32, 512], I32)
                y = P.sb("m_y%d" % tb, [32, 512], F32)
                yi = P.sb("m_yi%d" % tb, [32, 512], I32)
                fr = P.sb("m_fr%d" % tb, [32, 512], F32)
                kr = P.sb("m_kr%d" % tb, [32, 512], F32)
                t1 = P.sb("m_t1%d" % tb, [32, 512], F32)
                t2 = P.sb("m_t2%d" % tb, [32, 512], F32)
                P.dma(posi[:], k.pos_d[0:1, sl].partition_broadcast(32))
                P.cp(kr[:], pa[tb][0:32, :], eng="act")
                yield
                P.cp(y[:], posi[:])
                P.mm(k.PB[tb][0:32, :], rmT[:], kr[:])
                yield
                P.ts(y[:], y[:], invf[:, 0:1], ALU.mult)
                yield
                for off, dst in ((0.0, sin2), (0.25, cos2)):
                    if off != 0.0:
                        P.ts(y[:], y[:], off, ALU.add)
                        yield
                    P.cp(yi[:], y[:])
                    yield
                    P.cp(fr[:], yi[:])
                    yield
                    P.tt(fr[:], y[:], fr[:], ALU.subtract)
                    yield
                    P.act(dst[:, sl], fr[:], AF.Sin, scale=TWO_PI * (1.0 - 1e-6))
                    yield
                P.tt(t1[:], kr[:], cos2[:, sl], ALU.mult)
                P.tt(t2[:], k.PB[tb][0:32, :], sin2[:, sl], ALU.mult)
                yield
                P.tt(krope[:, sl], t1[:], t2[:], ALU.add)
                yield
            run_interleaved([rope_blk(tb) for tb in range(4)])
        if k.cut < 3:
            return
        kT = [P.sb("m_kT%d" % i, [128, L], BF16, blk=512) for i in range(2)]
        qT = [P.sb("m_qT%d" % i, [128, L], BF16, blk=512) for i in range(2)]
        Vh = [P.sb("m_V%d" % i, [128, NT, 96], BF16) for i in range(2)]
        pT = [P.sb("m_pT%d" % i, [128, 1024], BF16, blk=512) for i in range(2)]
        sq = [P.sb("m_sq%d" % i, [128, 512], F32) for i in range(2)]
        qr = [P.sb("m_qr%d" % i, [32, 512], F32) for i in range(2)]
        t1 = P.sb("m_t1b", [32, 512], F32)
        t2 = P.sb("m_t2b", [32, 512], F32)
        mrow = P.sb("m_mrow", [64, 512], F32)
        km4 = P.sb("m_km4", [128, 4], F32)
        kmax2 = P.sb("m_kmax2", [128, 1], F32)
        rden = [P.sb("m_rden%d" % i, [64, 512], F32) for i in range(2)]
        for i in range(2):
            P.memset(kT[i][32:64, :], 0.0)
            P.memset(kT[i][32:33, :], 1.0)
            P.memset(qT[i][32:64, :], 0.0)
            P.memset(Vh[i][:, :, 64:96], 1.0)
        kmx = [P.sb("m_kmx%d" % i, [128, 1], F32) for i in range(2)]

        def prep(h):
            kt_, qt_, vh_ = kT[h % 2], qT[h % 2], Vh[h % 2]
            kmax2_ = kmx[h % 2]
            P.cp(kt_[0:32, :], krope[:], eng="pool")
            for tb in range(4):
                sl = slice(tb * 512, (tb + 1) * 512)
                s = sq[tb % 2]
                P.mm(k.PB[2][:, :], wk[:, h, :], ckvn[:, sl])
                yield
                P.cp(kt_[64:128, sl], k.PB[2][64:128, :], eng="dve")
                yield
                P.tt(s[:], kt_[:, sl], kt_[:, sl], ALU.mult, eng="pool")
                yield
                yield
                P.mm(k.PB[3][:, :], onesk[:], s[:])
                yield
                P.red(km4[:, tb:tb + 1], k.PB[3][:, :], ALU.max)
                yield
            P.red(kmax2_[:], km4[:], ALU.max)
            for half in range(2):
                for j in range(8):
                    tt = half * 8 + j
                    P.mm(k.PB[2][:, j * 64:(j + 1) * 64], ckvn[:, tt * 128:(tt + 1) * 128], wv[:, h, :])
                yield
                P.cp(vh_[:, half * 8:(half + 1) * 8, 0:64], k.PB[2][:, :].rearrange("p (a b) -> p a b", a=8), eng="dve")
                yield
            for tb in range(4):
                sl = slice(tb * 512, (tb + 1) * 512)
                s = sq[tb % 2]
                q_ = qr[tb % 2]
                P.mm(k.PB[2][:, :], wq[:, 0, h, :], cqn0[:, sl], start=True, stop=False)
                P.mm(k.PB[2][:, :], wq[:, 1, h, :], cqn1[:, sl], start=False, stop=True)
                yield
                P.cp(qt_[64:128, sl], k.PB[2][64:128, :], eng="dve")
                P.cp(q_[:], k.PB[2][0:32, :], eng="dve")
                yield
                P.act(s[:], k.PB[2][:, :], AF.Square)
                yield
                P.mm(k.PB[3][:, :], ones[:], s[:])
                yield
                P.act(mrow[32:33, :], k.PB[3][32:33, :], AF.Sqrt, scale=kmax2_[32:33, 0:1])
                yield
                P.ts(qt_[32:33, sl], mrow[32:33, :], -1.0, ALU.mult)
                P.mm(k.PB[2][0:32, :], rmT[:], q_[:])
                P.tt(t1[:], q_[:], cos2[:, sl], ALU.mult, eng="pool")
                yield
                P.tt(t2[:], k.PB[2][0:32, :], sin2[:, sl], ALU.mult)
                yield
                P.tt(qt_[0:32, sl], t1[:], t2[:], ALU.add)
                yield

        def attn(h):
            kt_, qt_, vh_ = kT[h % 2], qT[h % 2], Vh[h % 2]
            pti = 0
            for qb in range(4):
                qs = slice(qb * 512, (qb + 1) * 512)
                O = k.PB[qb % 2]

                def s_pair(m_):
                    for j in range(2):
                        kt = 2 * m_ + j
                        P.mm(k.PAW[m_ % 2][:, j * 512:(j + 1) * 512], kt_[:, kt * 128:(kt + 1) * 128], qt_[:, qs])
                s_pair(0)
                s_pair(1)
                for m_ in range(NT // 2):
                    p_ = pT[pti % 2]
                    pti += 1
                    P.act(p_[:], k.PAW[m_ % 2][:, :], AF.Exp, scale=SC)
                    if m_ + 2 < NT // 2:
                        s_pair(m_ + 2)
                    for j in range(2):
                        kt = 2 * m_ + j
                        P.mm(O[0:96, :], vh_[:, kt, :], p_[:, j * 512:(j + 1) * 512], start=(kt == 0), stop=(kt == NT - 1))
                    yield
                rd = rden[qb % 2]
                P.rpow(rd[0:32, :], O[64:96, :], -1.0)
                P.rpow(rd[32:64, :], O[64:96, :], -1.0)
                ob = (h % 2) * 64
                P.tt(mixc(k, 3 + h // 2)[ob:ob + 64, qs], O[0:64, :], rd[:], ALU.mult)
                yield

        for _ in prep(0):
            pass
        mode = "il"
        for h in range(6):
            gens = [attn(h)]
            if h + 1 < 6:
                if mode == "il":
                    gens.append(prep(h + 1))
                elif mode == "seq":
                    run_interleaved(gens)
                    gens = [prep(h + 1)]
                elif mode == "noprep":
                    pass
            if mode == "noprep" and h > 0:
                gens = [attn(0)]
            run_interleaved(gens)


NCK = L // 64
NEG = -30000.0


def host_gdn(inp, b, m):
    cw = inp["gdn_conv"]
    o = np.zeros((DEPTH, 128, 9, 5), np.float32)
    for part in range(3):
        for p in range(3):
            o[:, :, part * 3 + p, :] = cw[:, :, part * 384 + p * 128: part * 384 + (p + 1) * 128].transpose(0, 2, 1)
    m["gconv"] = o
    gb = np.zeros((DEPTH, 128, 2), np.float32)
    for d in range(2):
        gb[:, d * 32:d * 32 + 6, 0] = inp["gdn_dt_bias"][:, d, :]
        gb[:, d * 32:d * 32 + 6, 1] = inp["gdn_a_log"][:, d, :]
    m["ggb"] = gb
    m["gng"] = np.ascontiguousarray(np.tile(inp["gdn_norm_g"], (1, 2)).reshape(DEPTH, 128, 1))
    sel = np.zeros((64, 6, 128), np.float32)
    for d in range(2):
        for p in range(3):
            sel[d * 32 + 2 * p, d * 3 + p, 0:64] = 1.0
            sel[d * 32 + 2 * p + 1, d * 3 + p, 64:128] = 1.0
    m["gsel"] = sel
    j = np.arange(64)[:, None]
    i = np.arange(64)[None, :]
    nm = np.zeros((128, 2, 64), np.float32)
    nm[:, 0, :] = np.tile(np.where(i > j, 0.0, NEG), (2, 1))
    nm[:, 1, :] = np.tile(np.where(i < j, 0.0, NEG), (2, 1))
    m["gnegm"] = nm
    m["gid2"] = np.ascontiguousarray(np.tile(np.eye(64, dtype=np.float32), (2, 1)))


def gdn_decl(k):
    nc = k.nc

    def din(name, shape, dt=F32):
        return nc.dram_tensor(name, list(shape), dt, kind="ExternalInput").ap()
    k.gconv_d = din("gconv", [DEPTH, 128, 9, 5])
    k.ggb_d = din("ggb", [DEPTH, 128, 2])
    k.gng_d = din("gng", [DEPTH, 128, 1])
    k.gsel_d = din("gsel", [64, 6, 128])
    k.gnegm_d = din("gnegm", [128, 2, 64])
    k.gid2_d = din("gid2", [128, 64])


def bc3(ap2, n):
    return ap2.unsqueeze(2).broadcast_to([ap2.shape[0], ap2.shape[1], n])


def bcm(ap2, n):
    return ap2.unsqueeze(1).broadcast_to([ap2.shape[0], n, ap2.shape[1]])


HS = (slice(0, 64), slice(64, 128))


def v3(ps, n=8):
    return ps[:, 0:n * 64].rearrange("p (a b) -> p a b", a=n)


def mm2(P, ps, c, lhsT, rhs, **kw):
    for hs in HS:
        P.mm(ps[hs, c * 64:(c + 1) * 64], lhsT[hs], rhs[hs], **kw)


def tr2(P, ps, c, in_, ident):
    for hs in HS:
        P.mm(ps[hs, c * 64:(c + 1) * 64], in_[hs], ident[hs, hs])


def neumann2(k, Nn, Rm, tmp, bank, id2, n8=8):
    P = k.P
    idb = bcm(id2[:, :], n8)
    tA, tB, tC, tD = tmp
    pa_, pb_, pc_ = bank
    for c in range(n8):
        tr2(P, pa_, c, Nn[:, c, :], k.identb)
    P.cp(tA[:], v3(pa_, n8), eng="act")
    P.tt(Rm[:], Nn[:], idb, ALU.add)
    yield
    cur, curT = Nn, tA
    targets = [(tB, tC), (tD, tA)]
    for lvl in range(1, 7):
        nxt, nxtT = targets[(lvl - 1) % 2]
        for c in range(n8):
            if lvl <= 5:
                mm2(P, pb_, c, cur[:, c, :], curT[:, c, :])
                if lvl < 5:
                    mm2(P, pa_, c, curT[:, c, :], cur[:, c, :])
            if lvl >= 2:
                mm2(P, pc_, c, curT[:, c, :], Rm[:, c, :])
        if lvl <= 5:
            P.cp(nxtT[:], v3(pb_, n8), eng="act")
            if lvl < 5:
                P.cp(nxt[:], v3(pa_, n8), eng="act")
        if lvl >= 2:
            P.tt(Rm[:], Rm[:], v3(pc_, n8), ALU.add)
        yield
        cur, curT = nxt, nxtT


def run_interleaved(gens):
    gens = list(gens)
    while gens:
        for g in list(gens):
            try:
                next(g)
            except StopIteration:
                gens.remove(g)


def norm_pipe(k, n, src_fn, sqb, rnb, banks, bones, power, scale, bias, post_fn):
    P = k.P

    def pre(i):
        P.act(sqb[i % 2][:], src_fn(i), AF.Square)
        P.mm(banks[i % 2][:, :], bones[:], sqb[i % 2][:])

    def post(i):
        P.rpow(rnb[i % 2][:], banks[i % 2][:, :], power, scale=scale, bias=bias)
        post_fn(i, rnb[i % 2])
    pre(0)
    for i in range(n):
        if i + 1 < n:
            pre(i + 1)
        post(i)


def run_pipelined(chains, depth=2):
    active = []
    nxt = [0] * len(chains)

    def start(ci):
        if nxt[ci] < len(chains[ci]):
            active.append((ci, chains[ci][nxt[ci]](nxt[ci] % depth)))
            nxt[ci] += 1
    for ci in range(len(chains)):
        for _ in range(depth):
            start(ci)
    while active:
        for item in list(active):
            try:
                next(item[1])
            except StopIteration:
                active.remove(item)
                start(item[0])


def gdn_phase(k, l):
    P = k.P
    with scope(k):
        GC = P.sb("g_GC", [64, L], F32, blk=512)
        GP = [P.sb("g_GP%d" % p, [128, NCK, 4], F32) for p in range(3)]
        NBP = [P.sb("g_NBP%d" % p, [128, NCK, 2], F32) for p in range(3)]
        sel = P.sb("g_sel", [64, 6, 128], F32)
        negm = P.sb("g_negm", [128, 2, 64], F32)
        id2 = P.sb("g_id2", [128, 64], F32)
        cw = P.sb("g_cw", [128, 9, 5], F32)
        ng = P.sb("g_ng", [128, 1], F32)
        bones = P.sb("g_bones", [128, 128], F32)
        P.dma(sel[:], k.gsel_d[:])
        P.dma(negm[:], k.gnegm_d[:])
        P.dma(id2[:], k.gid2_d[:])
        P.dma(cw[:], k.gconv_d[l])
        P.dma(ng[:], k.gng_d[l])
        P.memset(bones[:], 0.0)
        P.memset(bones[0:64, 0:64], 1.0)
        P.memset(bones[64:128, 64:128], 1.0)
        with scope(k):
            GT = P.sb("g_GT", [128, L], F32, blk=512)
            m0 = P.sb("g_m0", [64, L], F32)
            gb = P.sb("g_gb", [128, 2], F32)
            negA = P.sb("g_negA", [128, 1], F32)
            P.dma(gb[:], k.ggb_d[l])
            P.act(negA[:], gb[:, 1:2], AF.Exp)
            P.ts(negA[:], negA[:], -1.0, ALU.mult)
            P.memset(m0[:], 1.0)
            P.memset(m0[:, 0:L:64], 0.0)
            pa = proj(k, l, "gab")
            for tb in range(4):
                sl = slice(tb * 512, (tb + 1) * 512)
                P.act(GT[0:64, sl], pa[tb][0:64, :], AF.Exp, bias=gb[0:64, 0:1])
                P.act(GT[64:128, sl], pa[tb][64:128, :], AF.Sigmoid)
            P.act(GT[0:64, :], GT[0:64, :], AF.Ln, bias=1.0)
            P.ts(GT[0:64, :], GT[0:64, :], negA[0:64, 0:1], ALU.mult)
            P.scan(GC[:, :], m0[:, :], GT[0:64, :], 0.0, ALU.mult, ALU.add)
            gc3 = GC[32:64, :].rearrange("p (a b) -> p a b", b=64)
            P.tt(m0[32:64, :].rearrange("p (a b) -> p a b", b=64), bc3(GC[32:64, 63:L:64], 64), gc3, ALU.subtract)
            P.tt(GC[32:64, :], m0[32:64, :], GT[32:64, :], ALU.add)
            for grp in range(4):
                g8 = slice(grp * 8, (grp + 1) * 8)
                for c in range(8):
                    ck = grp * 8 + c
                    cs = slice(c * 64, (c + 1) * 64)
                    for hs in HS:
                        P.mm(k.PB[2 * (grp % 2)][hs, cs], GC[:, ck * 64:(ck + 1) * 64], k.ident[0:64, 0:64])
                        P.mm(k.PB[2 * (grp % 2) + 1][hs, cs], GT[64:128, ck * 64:(ck + 1) * 64], k.ident[64:128, 64:128])
                n_ = 0
                for p in range(3):
                    for hf, hs in enumerate(HS):
                        h = 2 * p + hf
                        for q, ps in ((0, k.PB[2 * (grp % 2)]), (1, k.PB[2 * (grp % 2) + 1])):
                            src = v3(ps)[hs, :, h:h + 33:32]
                            P.cp(GP[p][hs, g8, 2 * q:2 * q + 2], src, eng=("act" if q else "dve"))
            for p in range(3):
                P.ts(NBP[p][:], GP[p][:, :, 2:4], -1.0, ALU.mult)
        for p in range(3):
            with scope(k):
                Q = P.sb("g_Q", [128, L], BF16, blk=512)
                K_ = P.sb("g_K", [128, L], BF16, blk=512)
                Kt = P.sb("g_Kt", [128, NCK, 64], BF16, blk=512)
                Vt = P.sb("g_Vt", [128, NCK, 64], BF16, blk=512)
                O = P.sb("g_O", [128, L], F32, blk=512)
                P.memset(O[:], 0.0, eng="pool")
                with scope(k):
                    xp = P.sb("g_xp", [128, L + 4], F32)
                    Vf = P.sb("g_Vf", [128, L], BF16, blk=512)
                    cv = P.sb("g_cv", [128, L], F32, blk=512)
                    sq = P.sb("g_sq", [128, 512], F32)
                    rn = P.sb("g_rn", [128, 512], F32)
                    sq2 = P.sb("g_sq2", [128, 512], F32)
                    rn2 = P.sb("g_rn2", [128, 512], F32)
                    P.memset(xp[:, 0:2], 0.0)
                    P.memset(xp[:, L + 2:L + 4], 0.0)
                    for part, nm, dst in ((0, "gq", Q), (1, "gk", K_), (2, "gv", Vf)):
                        pa = proj(k, l, "%s%d" % (nm, p))
                        for tb in range(4):
                            P.cp(xp[:, 2 + tb * 512:2 + (tb + 1) * 512], pa[tb][:, :], eng=("act" if tb % 2 else "dve"))
                        wi = part * 3 + p
                        P.ts(cv[:], xp[:, 0:L], cw[:, wi, 0:1], ALU.mult)
                        for j in range(1, 5):
                            P.stt(cv[:], xp[:, j:j + L], cw[:, wi, j:j + 1], cv[:], ALU.mult, ALU.add)
                        if part == 2:
                            P.act(dst[:], cv[:], AF.Silu)
                        else:
                            P.act(cv[:], cv[:], AF.Silu)
                        if part < 2:
                            def fin(tb, r, dst=dst):
                                P.tt(dst[:, tb * 512:(tb + 1) * 512], cv[:, tb * 512:(tb + 1) * 512], r[:], ALU.mult)
                            norm_pipe(k, 4, lambda tb: cv[:, tb * 512:(tb + 1) * 512], (sq, sq2), (rn, rn2),
                                      (k.PB[2], k.PB[3]), bones, -0.5, 64.0 if part == 0 else 1.0,
                                      64e-6 if part == 0 else 1e-6, fin)
                    for src, dstt in ((K_, Kt), (Vf, Vt)):
                        for grp in range(4):
                            ps = k.PB[grp % 2]
                            for c in range(8):
                                ck = grp * 8 + c
                                tr2(P, ps, c, src[:, ck * 64:(ck + 1) * 64], k.identb)
                            P.cp(dstt[:, grp * 8:(grp + 1) * 8, :], v3(ps), eng=("act" if grp % 2 else "dve"))
                with scope(k):
                    T = dict(GC=GC, GP=GP[p], NBP=NBP[p], sel=sel, negm=negm, id2=id2, Q=Q, K=K_, Kt=Kt, Vt=Vt, O=O)
                    run_pipelined([gdn_chain(k, p, d, T) for d in range(2)], depth=1)
                with scope(k):
                    sqo = [P.sb("g_osq%d" % i, [128, 512], F32) for i in range(2)]
                    rno = [P.sb("g_orn%d" % i, [128, 512], F32) for i in range(2)]

                    def fin_o(tb, r):
                        sl = slice(tb * 512, (tb + 1) * 512)
                        P.tt(r[:], O[:, sl], r[:], ALU.mult)
                        P.ts(mixc(k, p)[:, sl], r[:], ng[:, 0:1], ALU.mult)
                    norm_pipe(k, 4, lambda tb: O[:, tb * 512:(tb + 1) * 512], sqo, rno, (k.PB[2], k.PB[3]), bones,
                              -0.5, 1.0 / 64, EPS, fin_o)


def gdn_chain(k, p, d, T):
    P = k.P
    GC, GP, NBP, sel, negm, id2, Q, K_, Kt, Vt, O = (T[n] for n in ("GC", "GP", "NBP", "sel", "negm", "id2", "Q", "K", "Kt", "Vt", "O"))
    B = k.PA if d == 0 else k.PB
    tag = "g%d_" % d
    names = ("CB", "EI", "QG", "Rm", "U0", "WT", "BW", "KD", "GK", "AcT", "Sg", "Ug", "nA", "nB", "nC", "nD", "Nb", "PTb", "Sgb")
    f32n = ("CB", "EI", "AcT", "Sg")
    NSET = 1
    GG = [{n: P.sb(tag + "%d" % s_ + n, [128, 8, 64], F32 if n in f32n else BF16) for n in names} for s_ in range(NSET)]
    for s_ in range(NSET):
        GG[s_]["gend"] = P.sb(tag + "gend%d" % s_, [128, 8], F32)
        GG[s_]["kds"] = P.sb(tag + "kds%d" % s_, [128, 8], F32)
    gam = P.sb(tag + "gam", [128, NCK], F32)
    shared = {"scan": 0}
    Scar = P.sb(tag + "Scar", [128, 64], F32)
    e_ = 63 if d == 0 else 0
    idb = bcm(id2[:, :], 8)

    def f2(t):
        return t[:].rearrange("p a b -> p (a b)")
    P.memset(Scar[:], 0.0)
    P.act(gam[:], GP[:, :, d], AF.Exp)
    gorder = list(range(4)) if d == 0 else list(range(3, -1, -1))

    def group(gi, grp, G):
        Nb, PTb, Sgb, gend, kds = G["Nb"], G["PTb"], G["Sgb"], G["gend"], G["kds"]
        sl = slice(grp * 512, (grp + 1) * 512)
        g8 = slice(grp * 8, (grp + 1) * 8)
        cj = GP[:, g8, d]
        nb = NBP[:, g8, d]
        CB, EI, QG, Rm, U0, WT, BW, KD, GK, AcT, Sg, Ug = (G[n] for n in names[:12])
        P.mm(B[0][:, :], sel[:, d * 3 + p, :], GC[:, sl])
        P.cp(f2(CB), B[0][:, :], eng="act")
        P.act(f2(EI), f2(CB), AF.Exp)
        P.cp(gend[:], EI[:, :, e_], eng="pool")
        P.tt(f2(QG), f2(EI), Q[:, sl], ALU.mult)
        P.tt(kds[:], CB[:, :, e_], cj, ALU.subtract)
        P.act(kds[:], kds[:], AF.Exp)
        P.tt(GK[:], Kt[:, g8, :], bc3(gam[:, g8], 64), ALU.mult, eng="pool")
        P.tt(KD[:], Kt[:, g8, :], bc3(kds[:], 64), ALU.mult, eng="pool")
        P.tt(CB[:], CB[:], bc3(cj, 64), ALU.subtract)
        P.tt(CB[:], CB[:], bcm(negm[:, d, :], 8), ALU.add)
        P.act(f2(CB), f2(CB), AF.Exp)
        yield
        for c in range(8):
            cs = slice((grp * 8 + c) * 64, (grp * 8 + c + 1) * 64)
            mm2(P, B[0], c, K_[:, cs], Q[:, cs])
            mm2(P, B[1], c, K_[:, cs], K_[:, cs])
        P.tt(EI[:], CB[:], idb, ALU.add)
        P.tt(PTb[:], EI[:], v3(B[0]), ALU.mult)
        P.tt(CB[:], CB[:], v3(B[1]), ALU.mult)
        P.tt(Nb[:], CB[:], bc3(nb, 64), ALU.mult)
        yield
        for _ in neumann2(k, Nb, Rm, (G["nA"], G["nB"], G["nC"], G["nD"]), (B[0], B[1], B[2]), id2):
            yield
        for c in range(8):
            mm2(P, B[0], c, Rm[:, c, :], Vt[:, grp * 8 + c, :])
            mm2(P, B[1], c, GK[:, c, :], Rm[:, c, :])
            mm2(P, B[2], c, Rm[:, c, :], GK[:, c, :])
        P.tt(U0[:], v3(B[0]), bc3(GP[:, g8, 2 + d], 64), ALU.mult)
        P.cp(WT[:], v3(B[1]), eng="act")
        P.tt(BW[:], v3(B[2]), bc3(nb, 64), ALU.mult)
        yield
        for c in range(8):
            mm2(P, B[3], c, BW[:, c, :], KD[:, c, :])
        P.tt(AcT[:], idb, bc3(gend[:], 64), ALU.mult, eng="pool")
        P.tt(AcT[:], AcT[:], v3(B[3]), ALU.add)
        yield
        while shared["scan"] != gi:
            yield
        corder = range(8) if d == 0 else range(7, -1, -1)
        prev = Scar[:]
        for n, c in enumerate(corder):
            P.cp(Sg[:, c, :], prev, eng="pool") if n == 0 else None
            ps = B[2 + n % 2]
            mm2(P, ps, 0, AcT[:, c, :], Sg[:, c, :], start=True, stop=False)
            mm2(P, ps, 0, KD[:, c, :], U0[:, c, :], start=False, stop=True)
            last = (n == 7)
            dst = Scar[:] if last else Sg[:, corder[n + 1], :]
            P.cp(dst, ps[:, 0:64], eng="act")
            yield
        shared["scan"] = gi + 1
        P.cp(Sgb[:], Sg[:], eng="act")
        for c in range(8):
            mm2(P, B[0], c, WT[:, c, :], Sgb[:, c, :])
        P.tt(CB[:], v3(B[0]), bc3(nb, 64), ALU.mult)
        P.tt(Ug[:], CB[:], U0[:], ALU.add)
        yield
        for c in range(8):
            mm2(P, B[1], c, Sgb[:, c, :], QG[:, c, :], start=True, stop=False)
            mm2(P, B[1], c, Ug[:, c, :], PTb[:, c, :], start=False, stop=True)
        P.tt(O[:, sl], O[:, sl], B[1][:, :], ALU.add)
        yield
    return [(lambda slot, gi=gi, grp=grp: group(gi, grp, GG[slot])) for gi, grp in enumerate(gorder)]


RW_EPS = 64e-5
DEC = float(np.exp(-0.5))
GS = 8
NG = NCK // GS


def host_rwkv(inp, b, m):
    mu = inp["rwkv_mu"]
    o = np.zeros((DEPTH, 128, 8, 2), np.float32)
    for part in range(3):
        for p in range(2):
            o[:, :, part * 2 + p, :] = mu[:, :, part * 256 + p * 128: part * 256 + (p + 1) * 128].transpose(0, 2, 1)
    o[:, 0:64, 6, :] = mu[:, :, 768:832].transpose(0, 2, 1)
    o[:, 0:64, 7, :] = mu[:, :, 832:896].transpose(0, 2, 1)
    m["rmu"] = o

    def pp(a):
        if a.ndim == 2:
            return np.ascontiguousarray(a.reshape(DEPTH, 2, 128).transpose(0, 2, 1))
        return np.ascontiguousarray(a.reshape(DEPTH, 2, 2, 128).transpose(0, 3, 1, 2))
    pv = np.zeros((DEPTH, 128, 7, 2), np.float32)
    pv[:, :, 0:2, :] = pp(inp["rwkv_w0"])
    pv[:, :, 2:4, :] = pp(inp["rwkv_a0"])
    pv[:, :, 4, :] = pp(inp["rwkv_k_k"])
    pv[:, :, 5, :] = pp(inp["rwkv_k_a"])
    pv[:, :, 6, :] = pp(inp["rwkv_r_k"].reshape(DEPTH, 256))
    m["rpv"] = pv
    ln = np.zeros((DEPTH, 128, 2, 2), np.float32)
    ln[:, :, 0, :] = pp(inp["rwkv_ln_g"])
    ln[:, :, 1, :] = pp(inp["rwkv_ln_b"])
    m["rln"] = ln
    m["rw2"] = np.ascontiguousarray(inp["rwkv_w2"].transpose(0, 2, 1, 3))
    m["ra2"] = np.ascontiguousarray(inp["rwkv_a2"].transpose(0, 2, 1, 3))
    s_ = np.arange(64)[:, None]
    t_ = np.arange(64)[None, :]
    msk = np.zeros((128, 2, 4, 64), np.float32)
    msk[:, 0, 0, :] = np.tile((t_ > s_), (2, 1))
    msk[:, 0, 1, :] = np.tile((t_ >= s_), (2, 1))
    msk[:, 1, 0, :] = np.tile((t_ < s_), (2, 1))
    msk[:, 1, 1, :] = np.tile((t_ <= s_), (2, 1))
    msk[:, :, 2:4, :] = -msk[:, :, 0:2, :]
    m["rmsk"] = msk


def rwkv_decl(k):
    nc = k.nc

    def din(name, shape, dt=F32):
        return nc.dram_tensor(name, list(shape), dt, kind="ExternalInput").ap()
    k.rmu_d = din("rmu", [DEPTH, 128, 8, 2])
    k.rpv_d = din("rpv", [DEPTH, 128, 7, 2])
    k.rln_d = din("rln", [DEPTH, 128, 2, 2])
    k.rw2_d = din("rw2", [DEPTH, 64, 2, 256])
    k.ra2_d = din("ra2", [DEPTH, 64, 2, 256])
    k.rmsk_d = din("rmsk", [128, 2, 4, 64])


def rwkv_phase(k, l):
    P = k.P
    with scope(k):
        mu = P.sb("r_mu", [128, 8, 3], F32)
        pv = P.sb("r_pv", [128, 7, 2], F32)
        omka = P.sb("r_omka", [128, 2], F32)
        hrk = P.sb("r_hrk", [128, 2], F32)
        ln = P.sb("r_ln", [128, 2, 2], F32)
        w2 = P.sb("r_w2", [64, 2, 256], BF16)
        a2 = P.sb("r_a2", [64, 2, 256], BF16)
        msk = P.sb("r_msk", [128, 2, 4, 64], F32)
        id2 = P.sb("r_id2", [128, 64], F32)
        bones = P.sb("r_bones", [128, 128], F32)
        m0 = P.sb("r_m0", [128, GS * 64], F32)
        twd = P.sb("r_twd", [64, L], BF16, blk=512)
        adx = P.sb("r_adx", [64, L], BF16, blk=512)
        sh32 = P.sb("r_sh32", [128, L], F32, blk=512)
        xp = P.sb("r_xp", [128, L + 2], F32)
        P.dma(mu[:, :, 0:2], k.rmu_d[l])
        P.dma(pv[:], k.rpv_d[l])
        P.dma(ln[:], k.rln_d[l])
        with scope(k):
            w2f = P.sb("r_w2f", [64, 2, 256], F32)
            a2f = P.sb("r_a2f", [64, 2, 256], F32)
            P.dma(w2f[:], k.rw2_d[l])
            P.dma(a2f[:], k.ra2_d[l])
            P.cp(w2[:], w2f[:], eng="act")
            P.cp(a2[:], a2f[:], eng="act")
        P.dma(msk[:], k.rmsk_d[:])
        P.dma(id2[:], k.gid2_d[:])
        P.memset(bones[:], 0.0)
        P.memset(bones[0:64, 0:64], 1.0)
        P.memset(bones[64:128, 64:128], 1.0)
        P.memset(m0[:], 1.0)
        P.memset(m0[:, 0:GS * 64:64], 0.0)
        P.memset(xp[:, 0:1], 0.0)
        P.memset(xp[:, L + 1:L + 2], 0.0)
        P.tt(mu[:, :, 2], mu[:, :, 0], mu[:, :, 1], ALU.add)
        P.ts(mu[:, :, 2], mu[:, :, 2], -1.0, ALU.mult, 1.0, ALU.add)
        P.ts(omka[:], pv[:, 5, :], -1.0, ALU.mult, 1.0, ALU.add)
        P.ts(hrk[:], pv[:, 6, :], 0.5, ALU.mult)

        def shifted(name, ci, dst, np_=128, fn=None):
            pa = proj(k, l, name, alt=(ci % 2 == 1))
            for tb in range(4):
                P.cp(xp[0:np_, 1 + tb * 512:1 + (tb + 1) * 512], pa[tb][0:np_, :], eng=("act" if tb % 2 else "dve"))
            t_ = sh32[0:np_, :]
            P.ts(t_, xp[0:np_, 1:L + 1], mu[0:np_, ci, 2:3], ALU.mult)
            P.stt(t_, xp[0:np_, 0:L], mu[0:np_, ci, 0:1], t_, ALU.mult, ALU.add)
            if fn is None:
                P.stt(dst[:], xp[0:np_, 2:L + 2], mu[0:np_, ci, 1:2], t_, ALU.mult, ALU.add)
            else:
                P.stt(t_, xp[0:np_, 2:L + 2], mu[0:np_, ci, 1:2], t_, ALU.mult, ALU.add)
                P.act(dst[:], t_, fn)

        shifted("rwd", 6, twd, 64, AF.Tanh)
        shifted("rad", 7, adx, 64)
        R_ = P.sb("r_R", [128, L], BF16, blk=512)
        KX = P.sb("r_KX", [128, L], BF16, blk=512)
        V_ = P.sb("r_V", [128, L], BF16, blk=512)
        KK = P.sb("r_KK", [128, L], BF16, blk=512)
        Vt = P.sb("r_Vt", [128, NCK, 64], BF16, blk=512)
        KS = P.sb("r_KS", [128, L], F32, blk=512)
        Y = xp[:, 1:L + 1]
        sq = P.sb("r_sq", [128, 512], F32)
        rn = P.sb("r_rn", [128, 512], F32)
        CH = [rwkv_tiles(k, e) for e in range(2)]

        class _V:
            def __init__(s_, t):
                s_.t = t

            def __getitem__(s_, key):
                return s_.t[:].rearrange("p a b -> p (a b)")[key]
        sq2, rn2 = _V(CH[0]["lw"]), _V(CH[0]["a"])
        for p in range(2):
            shifted("rr%d" % p, 0 + p, R_)
            shifted("rk%d" % p, 2 + p, KX)
            shifted("rv%d" % p, 4 + p, V_)
            P.ts(sh32[:], KX[:], pv[:, 4, p:p + 1], ALU.mult)

            def fin_k(tb, r):
                P.tt(KK[:, tb * 512:(tb + 1) * 512], sh32[:, tb * 512:(tb + 1) * 512], r[:], ALU.mult)
            norm_pipe(k, 4, lambda tb: sh32[:, tb * 512:(tb + 1) * 512], (sq, sq2), (rn, rn2), (k.PB[2], k.PB[3]),
                      bones, -0.5, 1.0, 1e-6, fin_k)
            for grp in range(4):
                ps = k.PB[grp % 2]
                for c in range(8):
                    ck = grp * 8 + c
                    tr2(P, ps, c, V_[:, ck * 64:(ck + 1) * 64], k.identb)
                P.cp(Vt[:, grp * 8:(grp + 1) * 8, :], v3(ps), eng=("act" if grp % 2 else "dve"))
            P.memset(xp[:, 1:L + 1], 0.0, eng="pool")
            P.memset(KS[:], 0.0, eng="pool")
            T = dict(pv=pv, omka=omka, w2=w2, a2=a2, msk=msk, id2=id2, m0=m0, twd=twd, adx=adx,
                     R=R_, KX=KX, KK=KK, Vt=Vt, KS=KS, Y=Y)
            run_interleaved([rwkv_chain(k, p, e, T, CH[e]) for e in range(2)])
            for tb in range(4):
                sl = slice(tb * 512, (tb + 1) * 512)
                P.mm(k.PB[tb % 2][:, :], bones[:], Y[:, sl])
                P.stt(Y[:, sl], k.PB[tb % 2][:, :], -1.0 / 64, Y[:, sl], ALU.mult, ALU.add)

            def fin_y(tb, r):
                sl = slice(tb * 512, (tb + 1) * 512)
                P.tt(Y[:, sl], Y[:, sl], r[:], ALU.mult)
                P.ts(Y[:, sl], Y[:, sl], ln[:, 0, p:p + 1], ALU.mult, ln[:, 1, p:p + 1], ALU.add)
            norm_pipe(k, 4, lambda tb: Y[:, tb * 512:(tb + 1) * 512], (sq, sq2), (rn, rn2), (k.PB[2], k.PB[3]),
                      bones, -0.5, 1.0 / 64, RW_EPS, fin_y)
            for tb in range(4):
                sl = slice(tb * 512, (tb + 1) * 512)
                s_ = (sq, sq2)[tb % 2]
                r_ = (rn, rn2)[tb % 2]
                P.tt(s_[:], R_[:, sl], KS[:, sl], ALU.mult)
                P.ts(s_[:], s_[:], hrk[:, p:p + 1], ALU.mult, eng="pool")
                P.mm(k.PB[tb % 2][:, :], bones[:], s_[:])
                P.tt(r_[:], k.PB[tb % 2][:, :], V_[:, sl], ALU.mult)
                P.tt(mixc(k, 6 + p)[:, sl], Y[:, sl], r_[:], ALU.add)


RW_F32 = ("lw", "a", "km", "b", "cl", "e1", "e2", "dend", "AcT", "Tg")
RW_BF16 = ("kap", "rt", "kt_", "bt_", "ke", "be", "kapT", "keT", "nbeT", "N", "Akv", "Brk", "nBrb", "Rm",
           "nA", "nB", "nC", "nD", "X0", "P0", "WkT", "Wk", "Tgb", "Pg")


def rwkv_tiles(k, e):
    P = k.P
    G = {n: P.sb("r%d_%s" % (e, n), [128, GS, 64], F32) for n in RW_F32}
    for n in RW_BF16:
        G[n] = P.sb("r%d_%s" % (e, n), [128, GS, 64], BF16)
    G["gC"] = P.sb("r%d_gC" % e, [128, GS], F32)
    G["Tcar"] = P.sb("r%d_Tcar" % e, [128, 64], F32)
    return G


def rwkv_chain(k, p, e, T, G):
    P = k.P
    pv, omka, w2, a2, msk, id2, m0, twd, adx, R_, KX, KK, Vt, KS, Y = (T[n] for n in (
        "pv", "omka", "w2", "a2", "msk", "id2", "m0", "twd", "adx", "R", "KX", "KK", "Vt", "KS", "Y"))
    B = k.PA if e == 0 else k.PB
    W = GS * 64
    e_ = 63 if e == 0 else 0
    idb = bcm(id2[:, :], GS)
    gC, Tcar = G["gC"], G["Tcar"]

    def f2(t):
        return t[:].rearrange("p a b -> p (a b)")

    def w3(ps):
        return v3(ps, GS)
    P.memset(Tcar[:], 0.0)
    gorder = range(NG) if e == 0 else range(NG - 1, -1, -1)
    pc = slice(p * 128, (p + 1) * 128)
    for grp in gorder:
        sl = slice(grp * W, (grp + 1) * W)
        c0 = grp * GS
        P.mm(B[0][:, 0:W], w2[:, e, pc], twd[:, sl])
        P.mm(B[1][:, 0:W], a2[:, e, pc], adx[:, sl])
        P.act(f2(G["lw"]), B[0][:, 0:W], AF.Sigmoid, bias=pv[:, 0 + e, p:p + 1])
        P.act(f2(G["a"]), B[1][:, 0:W], AF.Sigmoid, bias=pv[:, 2 + e, p:p + 1])
        P.ts(f2(G["km"]), f2(G["a"]), pv[:, 5, p:p + 1], ALU.mult, omka[:, p:p + 1], ALU.add)
        P.tt(f2(G["km"]), f2(G["km"]), KX[:, sl], ALU.mult)
        P.tt(f2(G["b"]), f2(G["a"]), KK[:, sl], ALU.mult, eng="pool")
        P.tt(KS[:, sl], KS[:, sl], f2(G["km"]), ALU.add, eng="pool")
        P.scan(f2(G["cl"]), m0[:], f2(G["lw"]), 0.0, ALU.mult, ALU.add)
        if e == 1:
            P.tt(G["e1"][:], bc3(G["cl"][:, :, 63], 64), G["cl"][:], ALU.subtract)
            P.tt(G["cl"][:], G["e1"][:], G["lw"][:], ALU.add)
        yield
        P.act(G["e1"][:], G["cl"][:], AF.Exp, scale=-DEC)
        P.act(G["e2"][:], G["cl"][:], AF.Exp, scale=DEC)
        P.tt(f2(G["rt"]), f2(G["e1"]), R_[:, sl], ALU.mult)
        P.tt(G["kt_"][:], G["e2"][:], G["km"][:], ALU.mult, eng="pool")
        P.tt(G["bt_"][:], G["e2"][:], G["b"][:], ALU.mult)
        P.tt(G["dend"][:], G["cl"][:], G["lw"][:], ALU.subtract, eng="pool")
        P.act(G["dend"][:], G["dend"][:], AF.Exp, scale=-DEC)
        P.tt(f2(G["kap"]), f2(G["dend"]), KK[:, sl], ALU.mult)
        P.cp(gC[:], G["e1"][:, :, e_], eng="pool")
        P.tt(G["dend"][:], bc3(G["cl"][:, :, e_], 64), G["cl"][:], ALU.subtract, eng="pool")
        P.act(G["dend"][:], G["dend"][:], AF.Exp, scale=-DEC)
        P.tt(G["ke"][:], G["dend"][:], G["km"][:], ALU.mult)
        P.tt(G["be"][:], G["dend"][:], G["b"][:], ALU.mult, eng="pool")
        yield
        for src, dst, sc in ((G["kap"], G["kapT"], 1.0), (G["ke"], G["keT"], 1.0), (G["be"], G["nbeT"], -1.0)):
            ps = B[0] if sc == 1.0 and src is G["kap"] else (B[1] if sc == 1.0 else B[2])
            for c in range(GS):
                tr2(P, ps, c, src[:, c, :], k.identb)
            if sc == 1.0:
                P.cp(dst[:], w3(ps), eng="act")
            else:
                P.ts(dst[:], w3(ps), -1.0, ALU.mult)
        yield
        for c in range(GS):
            mm2(P, B[0], c, G["bt_"][:, c, :], G["kap"][:, c, :])
            mm2(P, B[1], c, G["kt_"][:, c, :], G["kap"][:, c, :])
            mm2(P, B[2], c, G["kt_"][:, c, :], G["rt"][:, c, :])
            mm2(P, B[3], c, G["bt_"][:, c, :], G["rt"][:, c, :])
        ms = bcm(msk[:, e, 0, :], GS)
        mi = bcm(msk[:, e, 1, :], GS)
        nms = bcm(msk[:, e, 2, :], GS)
        nmi = bcm(msk[:, e, 3, :], GS)
        P.tt(G["N"][:], w3(B[0]), nms, ALU.mult)
        P.tt(G["Akv"][:], w3(B[1]), ms, ALU.mult)
        P.tt(G["Brk"][:], w3(B[2]), mi, ALU.mult)
        P.tt(G["nBrb"][:], w3(B[3]), nmi, ALU.mult)
        yield
        for _ in neumann2(k, G["N"], G["Rm"], (G["nA"], G["nB"], G["nC"], G["nD"]), (B[0], B[1], B[2]), id2, GS):
            yield
        for c in range(GS):
            mm2(P, B[0], c, G["Akv"][:, c, :], Vt[:, c0 + c, :])
        P.cp(G["X0"][:], w3(B[0]), eng="act")
        yield
        for c in range(GS):
            mm2(P, B[0], c, G["Rm"][:, c, :], G["X0"][:, c, :])
            mm2(P, B[1], c, G["kapT"][:, c, :], G["Rm"][:, c, :])
            mm2(P, B[2], c, G["Rm"][:, c, :], G["kapT"][:, c, :])
        P.cp(G["P0"][:], w3(B[0]), eng="act")
        P.cp(G["WkT"][:], w3(B[1]), eng="dve")
        P.cp(G["Wk"][:], w3(B[2]), eng="act")
        yield
        for c in range(GS):
            mm2(P, B[3], c, G["Wk"][:, c, :], G["nbeT"][:, c, :])
        P.tt(G["AcT"][:], idb, bc3(gC[:], 64), ALU.mult, eng="pool")
        P.tt(G["AcT"][:], G["AcT"][:], w3(B[3]), ALU.add)
        yield
        corder = list(range(GS)) if e == 0 else list(range(GS - 1, -1, -1))
        Tg = G["Tg"]
        for n, c in enumerate(corder):
            if n == 0:
                P.cp(Tg[:, c, :], Tcar[:], eng="pool")
            ps = B[2 + n % 2]
            mm2(P, ps, 0, G["AcT"][:, c, :], Tg[:, c, :], start=True, stop=False)
            mm2(P, ps, 0, G["keT"][:, c, :], Vt[:, c0 + c, :], start=False, stop=False)
            mm2(P, ps, 0, G["nbeT"][:, c, :], G["P0"][:, c, :], start=False, stop=True)
            dst = Tcar[:] if n == GS - 1 else Tg[:, corder[n + 1], :]
            P.cp(dst, ps[:, 0:64], eng="act")
            yield
        P.cp(G["Tgb"][:], Tg[:], eng="act")
        for c in range(GS):
            mm2(P, B[0], c, G["WkT"][:, c, :], G["Tgb"][:, c, :])
        P.tt(G["Pg"][:], w3(B[0]), G["P0"][:], ALU.add)
        yield
        for c in range(GS):
            mm2(P, B[1], c, Vt[:, c0 + c, :], G["Brk"][:, c, :], start=True, stop=False)
            mm2(P, B[1], c, G["Tgb"][:, c, :], G["rt"][:, c, :], start=False, stop=False)
            mm2(P, B[1], c, G["Pg"][:, c, :], G["nBrb"][:, c, :], start=False, stop=True)
        P.tt(Y[:, sl], Y[:, sl], B[1][:, 0:W], ALU.add)
        yield


_CACHE = {}


def kernel(**inputs):
    inp = {k_: np.asarray(v) for k_, v in inputs.items()}
    if "k" not in _CACHE:
        _CACHE["k"] = build()
    k = _CACHE["k"]
    B = inp["x"].shape[0]
    base = host_inputs(inp, 0)
    in_maps = []
    for b in range(B):
        m = dict(base)
        m["x"] = np.ascontiguousarray(inp["x"][b], dtype=np.float32)
        m["pos"] = np.ascontiguousarray(inp["positions"][b].reshape(1, L).astype(np.int32))
        in_maps.append(m)
    res = run_bass_kernel_spmd(k.nc, in_maps, core_ids=list(range(B)))
    return np.stack([np.asarray(r["out"], dtype=np.float32) for r in res.results], axis=0)
```

```python
import numpy as np
import concourse.bass as bass
import concourse.mybir as mybir
from concourse.bass_utils import run_bass_kernel_spmd
from contextlib import ExitStack

F32 = mybir.dt.float32
BF16 = mybir.dt.bfloat16
I32 = mybir.dt.int32
AF = mybir.ActivationFunctionType
ALU = mybir.AluOpType
AX = mybir.AxisListType
DTSIZE = {F32: 4, BF16: 2, I32: 4}


class _Op:
    __slots__ = ("eng", "emit", "deps", "idx", "needed", "sigval", "dsem", "dval", "isdma")

    def __init__(self, eng, emit, isdma=False):
        self.eng = eng
        self.emit = emit
        self.deps = []
        self.idx = -1
        self.needed = False
        self.sigval = 0
        self.dsem = None
        self.dval = 0
        self.isdma = isdma


class _Blk:
    __slots__ = ("w", "r")

    def __init__(self):
        self.w = None
        self.r = {}


class Prog:
    ENGS = ("pe", "dve", "act", "pool", "sp")
    NDMA = 48
    NHW = 32

    def __init__(self, nc, stack):
        self.nc = nc
        self.stack = stack
        self.ops = {e: [] for e in self.ENGS}
        self.track = {}
        self.seen = {e: {} for e in self.ENGS}
        self.seen_dma = {e: set() for e in self.ENGS}
        self.dma_last = [None] * self.NDMA
        self.dma_uses = [0] * self.NDMA
        self.dma_rr = 0
        self.dma_rr_sw = 0
        self.ndma_ops = 0
        self.untracked = set()
        self.out_dmas = []
        self.dma_pending = []
        self.last_compute = {}

    def sb(self, name, shape, dtype=F32, blk=None):
        self.uid = getattr(self, "uid", 0) + 1
        name = "s%d_%s" % (self.uid, name)
        t = self.stack.enter_context(self.nc.sbuf_tensor(name, list(shape), dtype))
        self._register(name, shape, dtype, blk)
        return t

    def ps(self, name, shape=(128, 512), dtype=F32, blk=None):
        self.uid = getattr(self, "uid", 0) + 1
        name = "p%d_%s" % (self.uid, name)
        t = self.stack.enter_context(self.nc.psum_tensor(name, list(shape), dtype))
        self._register(name, shape, dtype, blk)
        return t

    def _register(self, name, shape, dtype, blk):
        row = int(np.prod(shape[1:])) * DTSIZE[dtype]
        bb = row if blk is None else blk * DTSIZE[dtype]
        nb = (row + bb - 1) // bb
        self.track[name] = (bb, row, [_Blk() for _ in range(nb)])

    def dram_track(self, name, total_bytes, blk_bytes):
        nb = (total_bytes + blk_bytes - 1) // blk_bytes
        self.track[name] = (blk_bytes, -1, [_Blk() for _ in range(nb)])

    def _blocks(self, ap):
        name = ap.tensor.name
        if name not in self.track:
            return ()
        bb, row, blks = self.track[name]
        if len(blks) == 1:
            return blks
        ds = DTSIZE[ap.dtype]
        pat = ap.ap
        if row < 0:
            lo = hi = ap.offset
            for step, cnt in pat:
                ext = step * (cnt - 1)
                if ext < 0:
                    lo += ext
                else:
                    hi += ext
            return blks[(lo * ds) // bb:(hi * ds) // bb + 1]
        rowel = row // ds
        foff = ap.offset % rowel
        lo = hi = foff
        for step, cnt in pat[1:]:
            ext = step * (cnt - 1)
            if ext < 0:
                lo += ext
            else:
                hi += ext
        b0 = (lo * ds) // bb
        b1 = (hi * ds) // bb
        return blks[b0:b1 + 1]

    def _dep(self, x, y):
        if y is None or y is x:
            return
        e = x.eng
        if y.isdma:
            if id(y) in self.seen_dma[e]:
                return
            self.seen_dma[e].add(id(y))
            x.deps.append(y)
            return
        if y.eng == "pe" and e == "pe":
            return
        if y.idx <= self.seen[e].get(y.eng, -1):
            return
        self.seen[e][y.eng] = y.idx
        y.needed = True
        x.deps.append(y)

    def add(self, eng, emit, reads=(), writes=(), isdma=False):
        x = _Op(eng, emit, isdma)
        x.idx = len(self.ops[eng])
        rb = []
        for ap in reads:
            if ap is None or isinstance(ap, (int, float)):
                continue
            rb.extend(self._blocks(ap))
        wb = []
        for ap in writes:
            wb.extend(self._blocks(ap))
        for ap in reads:
            if ap is None or isinstance(ap, (int, float)) or not ap.tensor.name.startswith("p"):
                continue
            for b in self._blocks(ap):
                for key, y in b.r.items():
                    if key != eng:
                        self._dep(x, y)
        for b in rb:
            self._dep(x, b.w)
        for b in wb:
            self._dep(x, b.w)
            for y in b.r.values():
                self._dep(x, y)
        if isdma:
            if eng == "pool":
                s = self.NHW + self.dma_rr_sw
                self.dma_rr_sw = (self.dma_rr_sw + 1) % (self.NDMA - self.NHW)
            else:
                s = self.dma_rr
                self.dma_rr = (self.dma_rr + 1) % self.NHW
            self._dep(x, self.dma_last[s])
            self.dma_last[s] = x
            self.dma_uses[s] += 1
            x.dsem = s
            x.dval = 16 * self.dma_uses[s]
            self.ndma_ops += 1
        key = id(x) if isdma else eng
        for b in rb:
            b.r[key] = x
        for b in wb:
            b.w = x
            b.r = {}
        self.ops[eng].append(x)
        if isdma:
            self.dma_pending.append(x)
        else:
            self.last_compute[eng] = x
        return x

    def barrier(self):
        lasts = dict(self.last_compute)
        pend = list(self.dma_pending)
        self.dma_pending = []
        for e in self.ENGS:
            b = _Op(e, None)
            b.idx = len(self.ops[e])
            for e2, y in lasts.items():
                if e2 == e and e == "pe":
                    continue
                self._dep(b, y)
            for y in pend:
                self._dep(b, y)
            self.ops[e].append(b)

    def mm(self, out, lhsT, rhs, start=True, stop=True):
        return self.add("pe", lambda e: e.matmul(out, lhsT, rhs, start=start, stop=stop),
                        reads=(lhsT, rhs), writes=(out,))

    def tr(self, out, in_, ident):
        return self.add("pe", lambda e: e.transpose(out, in_, ident), reads=(in_, ident), writes=(out,))

    def tt(self, out, in0, in1, op, eng="dve"):
        return self.add(eng, lambda e: e.tensor_tensor(out, in0, in1, op), reads=(in0, in1), writes=(out,))

    def ts(self, out, in0, s1, op0, s2=None, op1=None, eng="dve", accum_out=None):
        kw = {}
        if eng == "pool" and op1 is None:
            if op0 == ALU.mult:
                s2, op1 = 0.0, ALU.add
            elif op0 == ALU.add:
                s2, op1 = 1.0, ALU.mult
        if op1 is not None:
            kw["op1"] = op1
        if accum_out is not None:
            kw["accum_out"] = accum_out
        w = (out,) if accum_out is None else (out, accum_out)
        return self.add(eng, lambda e: e.tensor_scalar(out, in0, s1, s2, op0, **kw),
                        reads=(in0, s1, s2), writes=w)

    def stt(self, out, in0, scalar, in1, op0, op1, accum_out=None):
        kw = {}
        if accum_out is not None:
            kw["accum_out"] = accum_out
        w = (out,) if accum_out is None else (out, accum_out)
        return self.add("dve", lambda e: e.scalar_tensor_tensor(out, in0, scalar, in1, op0, op1, **kw),
                        reads=(in0, scalar, in1), writes=w)

    def cp(self, out, in_, eng="dve"):
        if eng == "act":
            return self.add("act", lambda e: e.copy(out, in_), reads=(in_,), writes=(out,))
        return self.add(eng, lambda e: e.tensor_copy(out, in_), reads=(in_,), writes=(out,))

    def act(self, out, in_, func, bias=0.0, scale=1.0, accum_out=None):
        kw = {}
        if accum_out is not None:
            kw["accum_out"] = accum_out
        w = (out,) if accum_out is None else (out, accum_out)
        return self.add("act", lambda e: e.activation(out, in_, func, bias=bias, scale=scale, **kw),
                        reads=(in_, bias, scale), writes=w)

    def red(self, out, in_, op, axis=AX.X, eng="dve"):
        return self.add(eng, lambda e: e.tensor_reduce(out, in_, axis, op), reads=(in_,), writes=(out,))

    def recip(self, out, in_):
        return self.add("dve", lambda e: e.reciprocal(out, in_), reads=(in_,), writes=(out,))

    def rpow(self, out, in_, power, scale=1.0, bias=0.0):
        self.act(out, in_, AF.Ln, bias=bias, scale=scale)
        return self.act(out, out, AF.Exp, scale=power)

    def memset(self, ap, val, eng="dve"):
        return self.add(eng, lambda e: e.memset(ap, val), writes=(ap,))

    def scan(self, out, d0, d1, init, op0, op1):
        return self.add("dve", lambda e: e.tensor_tensor_scan(out, d0, d1, init, op0, op1),
                        reads=(d0, d1, init), writes=(out,))

    def dma(self, out, in_, eng="sp", is_output=False):
        x = self.add(eng, lambda e: e.dma_start(out=out, in_=in_), reads=(in_,), writes=(out,), isdma=True)
        if is_output:
            self.out_dmas.append(x)
        return x

    def finish(self):
        nc = self.nc
        fin = _Op("sp", None)
        fin.idx = len(self.ops["sp"])
        for y in self.out_dmas:
            self._dep(fin, y)
        self.ops["sp"].append(fin)
        sems = {}
        for e in ("pe", "dve", "act", "pool"):
            sems[e] = self.stack.enter_context(nc.semaphore("s_" + e))
        dsems = [self.stack.enter_context(nc.semaphore("d%d" % i)) for i in range(self.NDMA)]
        for e in ("pe", "dve", "act", "pool"):
            c = 0
            for x in self.ops[e]:
                if x.isdma:
                    continue
                if x.needed:
                    c += 1
                    x.sigval = c
            self.stats_sig = getattr(self, "stats_sig", {})
            self.stats_sig[e] = c
        ops = self.ops

        def replay(e, engobj):
            for x in ops[e]:
                for y in x.deps:
                    if y.isdma:
                        engobj.wait_ge(dsems[y.dsem], y.dval)
                    else:
                        engobj.wait_ge(sems[y.eng], y.sigval)
                if x.emit is None:
                    continue
                ins = x.emit(engobj)
                if x.isdma:
                    ins.then_inc(dsems[x.dsem], 16)
                elif x.needed:
                    ins.then_inc(sems[e], 1)

        with nc.Block() as block:
            @block.tensor
            def _(eng):
                replay("pe", eng)

            @block.vector
            def _(eng):
                replay("dve", eng)

            @block.scalar
            def _(eng):
                replay("act", eng)

            @block.gpsimd
            def _(eng):
                replay("pool", eng)

            @block.sync
            def _(eng):
                replay("sp", eng)


L = 2048
D = 1024
NT = L // 128
DEPTH = 2
N_IN = 3448
EPS = 1e-6

OFF = dict(gate=0, gdn_q=1024, gdn_k=1408, gdn_v=1792, gdn_a=2176, gdn_b=2188, mla_cq=2200, mla_ckv=2392,
           mla_kr=2520, rw_r=2552, rw_k=2808, rw_v=3064, rw_wd=3320, rw_ad=3384)


def chunk_table():
    ch = []
    for h in range(3):
        ch.append(("gq%d" % h, [(0, OFF["gdn_q"] + h * 128, 128)]))
        ch.append(("gk%d" % h, [(0, OFF["gdn_k"] + h * 128, 128)]))
        ch.append(("gv%d" % h, [(0, OFF["gdn_v"] + h * 128, 128)]))
    ch.append(("gab", [(0, OFF["gdn_a"], 6), (32, OFF["gdn_a"] + 6, 6), (64, OFF["gdn_b"], 6), (96, OFF["gdn_b"] + 6, 6)]))
    ch.append(("cq0", [(0, OFF["mla_cq"], 128)]))
    ch.append(("cq1", [(0, OFF["mla_cq"] + 128, 64)]))
    ch.append(("ckv", [(0, OFF["mla_ckv"], 128)]))
    ch.append(("kr", [(0, OFF["mla_kr"], 32)]))
    for i in range(2):
        ch.append(("rr%d" % i, [(0, OFF["rw_r"] + i * 128, 128)]))
        ch.append(("rk%d" % i, [(0, OFF["rw_k"] + i * 128, 128)]))
        ch.append(("rv%d" % i, [(0, OFF["rw_v"] + i * 128, 128)]))
    ch.append(("rwd", [(0, OFF["rw_wd"], 64)]))
    ch.append(("rad", [(0, OFF["rw_ad"], 64)]))
    for i in range(8):
        ch.append(("g%d" % i, [(0, OFF["gate"] + i * 128, 128)]))
    return ch


CHUNKS = chunk_table()
CH_IDX = {n: i for i, (n, _) in enumerate(CHUNKS)}
NCH = len(CHUNKS)


def host_win(w_in):
    out = np.zeros((DEPTH, NCH, 128, 8, 128), np.float32)
    for ci, (_, parts) in enumerate(CHUNKS):
        for dst, src, w in parts:
            blk = w_in[:, :, src:src + w].reshape(DEPTH, 8, 128, w)
            out[:, ci, :, :, dst:dst + w] = blk.transpose(0, 2, 1, 3)
    return out


class K:
    pass


def build(depth=DEPTH, mixers=("gdn", "mla", "rwkv"), dbg=False):
    nc = bass.Bass("TRN2", target_bir_lowering=False)
    k = K()
    k.nc = nc
    k.dbg = dbg
    k.dbg_outs = []
    k.cut = 99

    def din(name, shape, dt=F32):
        return nc.dram_tensor(name, list(shape), dt, kind="ExternalInput").ap()

    k.x_d = din("x", [L, D])
    k.win_d = din("win", [DEPTH, NCH, 128, 8, 128])
    k.normg_d = din("normg", [DEPTH, 128, 8])
    k.wout_d = din("wout", [DEPTH, 128, 8, 1024])
    k.fing_d = din("fing", [1, D])
    k.ident_d = din("ident", [128, 128])
    k.out_d = nc.dram_tensor("out", [L, D], F32, kind="ExternalOutput").ap()
    mla_decl(k)
    gdn_decl(k)
    rwkv_decl(k)

    with ExitStack() as st:
        P = Prog(nc, st)
        k.P = P
        k.xscr = nc.dram_tensor("xscr", [L, D], F32, kind="Internal").ap()
        P.dram_track("xscr", L * D * 4, 128 * D * 4)
        k.hT = P.sb("hT", [128, 8, L], BF16, blk=512)
        k.ident = P.sb("ident", [128, 128], F32)
        k.identb = P.sb("identb", [128, 128], BF16)
        k.normg = P.sb("normg", [128, DEPTH, 8], F32)
        k.wst = [P.sb("wst%d" % i, [128, 8, 128], F32) for i in range(2)]
        k.wbf = [P.sb("wbf%d" % i, [128, 8, 128], BF16) for i in range(2)]
        k.wrr = 0
        k.PAW = [P.ps("paw%d" % i, [128, 1024], F32, blk=512) for i in range(2)]
        k.PA = [k.PAW[i // 2][:, (i % 2) * 512:(i % 2 + 1) * 512] for i in range(4)]
        k.PB = [P.ps("pb%d" % i, [128, 512], F32) for i in range(4)]

        P.dma(k.ident[:], k.ident_d[:])
        P.cp(k.identb[:], k.ident[:])
        for l in range(DEPTH):
            P.dma(k.normg[:, l, :], k.normg_d[l])

        for l in range(depth):
            phase_a(k, l)
            with scope(k):
                k.mix_r = P.sb("mix_r", [128, 2, L], BF16, blk=512)
                if "rwkv" in mixers:
                    rwkv_phase(k, l)
                else:
                    P.memset(k.mix_r[:].rearrange("p a b -> p (a b)"), 1.0)
                with scope(k):
                    k.mix_m = P.sb("mix_m", [128, 3, L], BF16, blk=512)
                    if "mla" in mixers:
                        mla_phase(k, l)
                    else:
                        P.memset(k.mix_m[:].rearrange("p a b -> p (a b)"), 1.0)
                    with scope(k):
                        k.mix_g = P.sb("mix_g", [128, 3, L], BF16, blk=512)
                        if "gdn" in mixers:
                            gdn_phase(k, l)
                        else:
                            P.memset(k.mix_g[:].rearrange("p a b -> p (a b)"), 1.0)
                        if k.dbg:
                            for nm, t_, n_ in (("g", k.mix_g, 3), ("m", k.mix_m, 3), ("r", k.mix_r, 2)):
                                dump(k, "mix_%s%d" % (nm, l), t_[:].rearrange("p a b -> p (a b)"), [128, n_ * L])
                        phase_z(k, l, last=(l == depth - 1))
        P.finish()
        print("ops:", {e: len(v) for e, v in P.ops.items()}, "sig:", P.stats_sig, "dma:", P.ndma_ops)
    return k


def mixc(k, c):
    if c < 3:
        return k.mix_g[:, c, :]
    if c < 6:
        return k.mix_m[:, c - 3, :]
    return k.mix_r[:, c - 6, :]


def dump(k, name, ap, shape=None):
    if not k.dbg:
        return
    P = k.P
    shape = list(ap.shape) if shape is None else shape
    d = k.nc.dram_tensor("dbg_" + name, shape, ap.dtype, kind="ExternalOutput").ap()
    P.dma(d[:] if len(shape) == 2 else d, ap, is_output=True)
    k.dbg_outs.append("dbg_" + name)


def scope(k):
    class _S:
        def __enter__(s):
            s.old = k.P.stack
            s.st = ExitStack()
            s.st.__enter__()
            k.P.stack = s.st
            return s

        def __exit__(s, *a):
            k.P.barrier()
            k.P.stack = s.old
            s.st.__exit__(*a)
            return False
    return _S()


def phase_a(k, l):
    P = k.P
    with scope(k):
        ssq = P.sb("a_ssq", [128, NT])
        rs = P.sb("a_rs", [128, NT])
        rstd = P.sb("a_rstd", [128, NT])
        junk = [P.sb("a_junk%d" % i, [128, D], BF16) for i in range(2)]
        xs = [P.sb("a_xs%d" % i, [128, D], BF16) for i in range(2)]
        xin = [P.sb("a_xin%d" % i, [128, D], F32) for i in range(3)]
        src = k.x_d if l == 0 else k.xscr

        def stage1(tt):
            b = tt % 2
            xt_ = xin[tt % 3]
            P.dma(xt_[:], src[tt * 128:(tt + 1) * 128, :])
            P.act(junk[b][:], xt_[:], AF.Square, accum_out=ssq[:, tt:tt + 1])
            P.act(rs[:, tt:tt + 1], ssq[:, tt:tt + 1], AF.Sqrt, bias=EPS, scale=1.0 / D)
            P.recip(rstd[:, tt:tt + 1], rs[:, tt:tt + 1])
            P.ts(xs[b][:], xt_[:], rstd[:, tt:tt + 1], ALU.mult)

        def stage2(tt):
            b = tt % 2
            pt = k.PB[b][:].bitcast(BF16)
            for dc in range(8):
                P.tr(pt[:, dc * 128:(dc + 1) * 128], xs[b][:, dc * 128:(dc + 1) * 128], k.identb[:])
            P.cp(k.hT[:, :, tt * 128:(tt + 1) * 128], pt[:].rearrange("p (a b) -> p a b", a=8),
                 eng=("act" if tt % 2 == 0 else "dve"))
        stage1(0)
        for tt in range(NT):
            if tt + 1 < NT:
                stage1(tt + 1)
            stage2(tt)


def proj(k, l, name, alt=False):
    P = k.P
    BK = k.PB if alt else k.PA
    ci = CH_IDX[name]
    b = k.wrr
    k.wrr ^= 1
    P.dma(k.wst[b][:], k.win_d[l, ci], eng="sp")
    gb = k.normg[:, l, :].unsqueeze(2).broadcast_to([128, 8, 128])
    P.tt(k.wbf[b][:], k.wst[b][:], gb, ALU.mult, eng="pool")
    for tb in range(4):
        for dc in range(8):
            P.mm(BK[tb][:, :], k.wbf[b][:, dc, :], k.hT[:, dc, tb * 512:(tb + 1) * 512], start=(dc == 0), stop=(dc == 7))
    return BK


ZQ = "act"


def phase_z(k, l, last):
    P = k.P
    with scope(k):
        wst = P.sb("z_wst", [128, 8, 512], F32)
        wob = P.sb("z_wob", [128, 8, 1024], BF16, blk=512)
        sg = [P.sb("z_sg%d" % i, [128, L], BF16, blk=512) for i in range(2)]
        for nb in range(2):
            P.dma(wst[:], k.wout_d[l, :, :, nb * 512:(nb + 1) * 512])
            P.cp(wob[:, :, nb * 512:(nb + 1) * 512], wst[:], eng="act")
        for gc in range(8):
            pa = proj(k, l, "g%d" % gc, alt=(gc % 2 == 1))
            s = sg[gc % 2]
            for tb in range(4):
                P.act(s[:, tb * 512:(tb + 1) * 512], pa[tb][:, :], AF.Silu)
                mc = mixc(k, gc)[:, tb * 512:(tb + 1) * 512]
                P.tt(mc, mc, s[:, tb * 512:(tb + 1) * 512], ALU.mult)
        xin = [P.sb("z_xin%d" % i, [128, D], F32) for i in range(3)]
        src = k.x_d if l == 0 else k.xscr
        if last:
            ssq = P.sb("f_ssq", [128, NT])
            rs = P.sb("f_rs", [128, NT])
            rstd = P.sb("f_rstd", [128, NT])
            junk = [P.sb("f_junk%d" % i, [128, D], BF16) for i in range(2)]
            gf = P.sb("f_g", [128, D])
            ot = [P.sb("f_o%d" % i, [128, D]) for i in range(2)]
            P.dma(gf[:], k.fing_d[0:1, :].partition_broadcast(128))
        def za(tt):
            xt_ = xin[tt % 3]
            P.dma(xt_[:], src[tt * 128:(tt + 1) * 128, :])
            for nb in range(2):
                ps = k.PB[(tt * 2 + nb) % 4]
                for kc in range(8):
                    P.mm(ps[:, :], mixc(k, kc)[:, tt * 128:(tt + 1) * 128], wob[:, kc, nb * 512:(nb + 1) * 512],
                         start=(kc == 0), stop=(kc == 7))
                xs = xt_[:, nb * 512:(nb + 1) * 512]
                P.tt(xs, xs, ps[:, :], ALU.add)
            if not last:
                P.dma(k.xscr[tt * 128:(tt + 1) * 128, :], xt_[:], eng=ZQ)
            else:
                b = tt % 2
                P.act(junk[b][:], xt_[:], AF.Square, accum_out=ssq[:, tt:tt + 1])
                P.act(rs[:, tt:tt + 1], ssq[:, tt:tt + 1], AF.Sqrt, bias=EPS, scale=1.0 / D)

        def zb(tt):
            if last:
                xt_ = xin[tt % 3]
                b = tt % 2
                P.recip(rstd[:, tt:tt + 1], rs[:, tt:tt + 1])
                P.stt(ot[b][:], xt_[:], rstd[:, tt:tt + 1], gf[:], ALU.mult, ALU.mult)
                P.dma(k.out_d[tt * 128:(tt + 1) * 128, :], ot[b][:], eng=ZQ, is_output=True)
        za(0)
        for tt in range(NT):
            if tt + 1 < NT:
                za(tt + 1)
            zb(tt)


def host_inputs(inp, b):
    m = {}
    m["x"] = np.ascontiguousarray(inp["x"][b])
    m["win"] = host_win(inp["w_in"])
    m["normg"] = np.ascontiguousarray(inp["norm_g"].reshape(DEPTH, 8, 128).transpose(0, 2, 1))
    m["wout"] = np.ascontiguousarray(inp["w_out"].reshape(DEPTH, 8, 128, 1024).transpose(0, 2, 1, 3))
    m["fing"] = np.ascontiguousarray(inp["final_norm_g"].reshape(1, D))
    m["ident"] = np.eye(128, dtype=np.float32)
    host_mla(inp, b, m)
    host_gdn(inp, b, m)
    host_rwkv(inp, b, m)
    return m


TWO_PI = 2.0 * np.pi


def host_mla(inp, b, m):
    half = 16
    inv_freq = (10000.0 ** (-np.arange(half, dtype=np.float32) / half)).astype(np.float32)
    invf = np.zeros((32, 1), np.float32)
    invf[:, 0] = np.tile(inv_freq, 2) / np.float32(TWO_PI)
    m["invf"] = invf
    rm = np.zeros((32, 32), np.float32)
    for i in range(16):
        rm[i, i + 16] = -1.0
        rm[i + 16, i] = 1.0
    m["rmT"] = np.ascontiguousarray(rm.T)
    m["pos"] = np.ascontiguousarray(inp["positions"][b].reshape(1, L).astype(np.int32))
    wuq = inp["mla_w_uq"]
    o = np.zeros((DEPTH, 128, 2, 6, 128), np.float32)
    for h in range(6):
        nope = wuq[:, :, h * 96:h * 96 + 64]
        rope = wuq[:, :, h * 96 + 64:h * 96 + 96]
        o[:, :, 0, h, 64:128] = nope[:, 0:128]
        o[:, 0:64, 1, h, 64:128] = nope[:, 128:192]
        o[:, :, 0, h, 0:32] = rope[:, 0:128]
        o[:, 0:64, 1, h, 0:32] = rope[:, 128:192]
    m["wuq"] = o
    gq = np.zeros((DEPTH, 128, 2), np.float32)
    gq[:, :, 0] = inp["mla_q_norm_g"][:, 0:128]
    gq[:, 0:64, 1] = inp["mla_q_norm_g"][:, 128:192]
    m["gq"] = gq
    wukv = inp["mla_w_ukv"]
    wk = np.zeros((DEPTH, 128, 6, 128), np.float32)
    wv = np.zeros((DEPTH, 128, 6, 64), np.float32)
    for h in range(6):
        wk[:, :, h, 64:128] = wukv[:, :, h * 128:h * 128 + 64]
        wv[:, :, h, :] = wukv[:, :, h * 128 + 64:h * 128 + 128]
    m["wuk"] = wk
    m["wuv"] = wv
    m["gkv"] = np.ascontiguousarray(inp["mla_kv_norm_g"].reshape(DEPTH, 128, 1))


def mla_decl(k):
    nc = k.nc

    def din(name, shape, dt=F32):
        return nc.dram_tensor(name, list(shape), dt, kind="ExternalInput").ap()
    k.invf_d = din("invf", [32, 1])
    k.rmT_d = din("rmT", [32, 32])
    k.pos_d = din("pos", [1, L], I32)
    k.wuq_d = din("wuq", [DEPTH, 128, 2, 6, 128])
    k.gq_d = din("gq", [DEPTH, 128, 2])
    k.wuk_d = din("wuk", [DEPTH, 128, 6, 128])
    k.wuv_d = din("wuv", [DEPTH, 128, 6, 64])
    k.gkv_d = din("gkv", [DEPTH, 128, 1])


def latent_norm(k, l, names, nfeat, outs, ones):
    P = k.P
    sq = [P.sb("ln_sq%d" % i, [128, 512]) for i in range(2)]
    rq = [P.sb("ln_rq%d" % i, [128, 512]) for i in range(2)]
    n = len(names)
    for i, nm in enumerate(names):
        pa = proj(k, l, nm)
        for tb in range(4):
            s = sq[tb % 2]
            sk = ""
            if "a" not in sk:
                P.act(s[:], pa[tb][:, :], AF.Square)
            if "c" not in sk:
                P.cp(outs[i][:, tb * 512:(tb + 1) * 512], pa[tb][:, :], eng="dve")
            if "m" not in sk:
                P.mm(k.PB[tb][:, :], ones[:], s[:], start=(i == 0), stop=(i == n - 1))
    c2 = 9
    if c2 < 1:
        return
    for tb in range(4):
        r = rq[tb % 2]
        P.rpow(r[:], k.PB[tb][:, :], -0.5, scale=1.0 / nfeat, bias=EPS)
        for i in range(n):
            o = outs[i][:, tb * 512:(tb + 1) * 512]
            P.tt(o, o, r[:], ALU.mult)


def mla_phase(k, l):
    P = k.P
    SC = 96.0 ** -0.5
    with scope(k):
        cqn0 = P.sb("m_cqn0", [128, L], BF16, blk=512)
        cqn1 = P.sb("m_cqn1", [128, L], BF16, blk=512)
        ckvn = P.sb("m_ckvn", [128, L], BF16, blk=512)
        krope = P.sb("m_krope", [32, L], BF16, blk=512)
        cos2 = P.sb("m_cos2", [32, L], BF16, blk=512)
        sin2 = P.sb("m_sin2", [32, L], BF16, blk=512)
        wq = P.sb("m_wq", [128, 2, 6, 128], BF16)
        wk = P.sb("m_wk", [128, 6, 128], BF16)
        wv = P.sb("m_wv", [128, 6, 64], BF16)
        ones = P.sb("m_ones", [128, 128], F32)
        onesk = P.sb("m_onesk", [128, 128], F32)
        rmT = P.sb("m_rmT", [32, 32], F32)
        P.memset(ones[:], 1.0)
        P.memset(onesk[:], 1.0)
        P.memset(onesk[32:64, :], 0.0)
        P.dma(rmT[:], k.rmT_d[:])
        with scope(k):
            st = P.sb("m_st", [128, 2, 6, 128], F32)
            g = P.sb("m_g", [128, 4], F32)
            P.dma(st[:], k.wuq_d[l])
            P.dma(g[:, 0:2], k.gq_d[l])
            P.dma(g[:, 2:3], k.gkv_d[l])
            for kc in range(2):
                P.ts(wq[:, kc].rearrange("p a b -> p (a b)"), st[:, kc].rearrange("p a b -> p (a b)"),
                     g[:, kc:kc + 1], ALU.mult)
            st2 = P.sb("m_st2", [128, 6, 128], F32)
            P.dma(st2[:], k.wuk_d[l])
            P.ts(wk[:].rearrange("p a b -> p (a b)"), st2[:].rearrange("p a b -> p (a b)"), g[:, 2:3], ALU.mult)
            st3 = P.sb("m_st3", [128, 6, 64], F32)
            P.dma(st3[:], k.wuv_d[l])
            P.ts(wv[:].rearrange("p a b -> p (a b)"), st3[:].rearrange("p a b -> p (a b)"), g[:, 2:3], ALU.mult)
        if k.cut < 1:
            return
        with scope(k):
            latent_norm(k, l, ["cq0", "cq1"], 192, [cqn0, cqn1], ones)
            latent_norm(k, l, ["ckv"], 128, [ckvn], ones)
        if k.cut < 2:
            return
        with scope(k):
            invf = P.sb("m_invf", [32, 1], F32)
            P.dma(invf[:], k.invf_d[:])
            pa = proj(k, l, "kr")

            def rope_blk(tb):
                sl = slice(tb * 512, (tb + 1) * 512)
                posi = P.sb("m_posi%d" % tb, [32, 512], I32)
                y = P.sb("m_y%d" % tb, [32, 512], F32)
                yi = P.sb("m_yi%d" % tb, [32, 512], I32)
                fr = P.sb("m_fr%d" % tb, [32, 512], F32)
                kr = P.sb("m_kr%d" % tb, [32, 512], F32)
                t1 = P.sb("m_t1%d" % tb, [32, 512], F32)
                t2 = P.sb("m_t2%d" % tb, [32, 512], F32)
                P.dma(posi[:], k.pos_d[0:1, sl].partition_broadcast(32))
                P.cp(kr[:], pa[tb][0:32, :], eng="act")
                yield
                P.cp(y[:], posi[:])
                P.mm(k.PB[tb][0:32, :], rmT[:], kr[:])
                yield
                P.ts(y[:], y[:], invf[:, 0:1], ALU.mult)
                yield
                for off, dst in ((0.0, sin2), (0.25, cos2)):
                    if off != 0.0:
                        P.ts(y[:], y[:], off, ALU.add)
                        yield
                    P.cp(yi[:], y[:])
                    yield
                    P.cp(fr[:], yi[:])
                    yield
                    P.tt(fr[:], y[:], fr[:], ALU.subtract)
                    yield
                    P.act(dst[:, sl], fr[:], AF.Sin, scale=TWO_PI * (1.0 - 1e-6))
                    yield
                P.tt(t1[:], kr[:], cos2[:, sl], ALU.mult)
                P.tt(t2[:], k.PB[tb][0:32, :], sin2[:, sl], ALU.mult)
                yield
                P.tt(krope[:, sl], t1[:], t2[:], ALU.add)
                yield
            run_interleaved([rope_blk(tb) for tb in range(4)])
        if k.cut < 3:
            return
        kT = [P.sb("m_kT%d" % i, [128, L], BF16, blk=512) for i in range(2)]
        qT = [P.sb("m_qT%d" % i, [128, L], BF16, blk=512) for i in range(2)]
        Vh = [P.sb("m_V%d" % i, [128, NT, 96], BF16) for i in range(2)]
        pT = [P.sb("m_pT%d" % i, [128, 1024], BF16, blk=512) for i in range(2)]
        sq = [P.sb("m_sq%d" % i, [128, 512], F32) for i in range(2)]
        qr = [P.sb("m_qr%d" % i, [32, 512], F32) for i in range(2)]
        t1 = P.sb("m_t1b", [32, 512], F32)
        t2 = P.sb("m_t2b", [32, 512], F32)
        mrow = P.sb("m_mrow", [64, 512], F32)
        km4 = P.sb("m_km4", [128, 4], F32)
        kmax2 = P.sb("m_kmax2", [128, 1], F32)
        rden = [P.sb("m_rden%d" % i, [64, 512], F32) for i in range(2)]
        for i in range(2):
            P.memset(kT[i][32:64, :], 0.0)
            P.memset(kT[i][32:33, :], 1.0)
            P.memset(qT[i][32:64, :], 0.0)
            P.memset(Vh[i][:, :, 64:96], 1.0)
        kmx = [P.sb("m_kmx%d" % i, [128, 1], F32) for i in range(2)]

        def prep(h):
            kt_, qt_, vh_ = kT[h % 2], qT[h % 2], Vh[h % 2]
            kmax2_ = kmx[h % 2]
            P.cp(kt_[0:32, :], krope[:], eng="pool")
            for tb in range(4):
                sl = slice(tb * 512, (tb + 1) * 512)
                s = sq[tb % 2]
                P.mm(k.PB[2][:, :], wk[:, h, :], ckvn[:, sl])
                yield
                P.cp(kt_[64:128, sl], k.PB[2][64:128, :], eng="dve")
                yield
                P.tt(s[:], kt_[:, sl], kt_[:, sl], ALU.mult, eng="pool")
                yield
                yield
                P.mm(k.PB[3][:, :], onesk[:], s[:])
                yield
                P.red(km4[:, tb:tb + 1], k.PB[3][:, :], ALU.max)
                yield
            P.red(kmax2_[:], km4[:], ALU.max)
            for half in range(2):
                for j in range(8):
                    tt = half * 8 + j
                    P.mm(k.PB[2][:, j * 64:(j + 1) * 64], ckvn[:, tt * 128:(tt + 1) * 128], wv[:, h, :])
                yield
                P.cp(vh_[:, half * 8:(half + 1) * 8, 0:64], k.PB[2][:, :].rearrange("p (a b) -> p a b", a=8), eng="dve")
                yield
            for tb in range(4):
                sl = slice(tb * 512, (tb + 1) * 512)
                s = sq[tb % 2]
                q_ = qr[tb % 2]
                P.mm(k.PB[2][:, :], wq[:, 0, h, :], cqn0[:, sl], start=True, stop=False)
                P.mm(k.PB[2][:, :], wq[:, 1, h, :], cqn1[:, sl], start=False, stop=True)
                yield
                P.cp(qt_[64:128, sl], k.PB[2][64:128, :], eng="dve")
                P.cp(q_[:], k.PB[2][0:32, :], eng="dve")
                yield
                P.act(s[:], k.PB[2][:, :], AF.Square)
                yield
                P.mm(k.PB[3][:, :], ones[:], s[:])
                yield
                P.act(mrow[32:33, :], k.PB[3][32:33, :], AF.Sqrt, scale=kmax2_[32:33, 0:1])
                yield
                P.ts(qt_[32:33, sl], mrow[32:33, :], -1.0, ALU.mult)
                P.mm(k.PB[2][0:32, :], rmT[:], q_[:])
                P.tt(t1[:], q_[:], cos2[:, sl], ALU.mult, eng="pool")
                yield
                P.tt(t2[:], k.PB[2][0:32, :], sin2[:, sl], ALU.mult)
                yield
                P.tt(qt_[0:32, sl], t1[:], t2[:], ALU.add)
                yield

        def attn(h):
            kt_, qt_, vh_ = kT[h % 2], qT[h % 2], Vh[h % 2]
            pti = 0
            for qb in range(4):
                qs = slice(qb * 512, (qb + 1) * 512)
                O = k.PB[qb % 2]

                def s_pair(m_):
                    for j in range(2):
                        kt = 2 * m_ + j
                        P.mm(k.PAW[m_ % 2][:, j * 512:(j + 1) * 512], kt_[:, kt * 128:(kt + 1) * 128], qt_[:, qs])
                s_pair(0)
                s_pair(1)
                for m_ in range(NT // 2):
                    p_ = pT[pti % 2]
                    pti += 1
                    P.act(p_[:], k.PAW[m_ % 2][:, :], AF.Exp, scale=SC)
                    if m_ + 2 < NT // 2:
                        s_pair(m_ + 2)
                    for j in range(2):
                        kt = 2 * m_ + j
                        P.mm(O[0:96, :], vh_[:, kt, :], p_[:, j * 512:(j + 1) * 512], start=(kt == 0), stop=(kt == NT - 1))
                    yield
                rd = rden[qb % 2]
                P.rpow(rd[0:32, :], O[64:96, :], -1.0)
                P.rpow(rd[32:64, :], O[64:96, :], -1.0)
                ob = (h % 2) * 64
                P.tt(mixc(k, 3 + h // 2)[ob:ob + 64, qs], O[0:64, :], rd[:], ALU.mult)
                yield

        for _ in prep(0):
            pass
        mode = "il"
        for h in range(6):
            gens = [attn(h)]
            if h + 1 < 6:
                if mode == "il":
                    gens.append(prep(h + 1))
                elif mode == "seq":
                    run_interleaved(gens)
                    gens = [prep(h + 1)]
                elif mode == "noprep":
                    pass
            if mode == "noprep" and h > 0:
                gens = [attn(0)]
            run_interleaved(gens)


NCK = L // 64
NEG = -30000.0


def host_gdn(inp, b, m):
    cw = inp["gdn_conv"]
    o = np.zeros((DEPTH, 128, 9, 5), np.float32)
    for part in range(3):
        for p in range(3):
            o[:, :, part * 3 + p, :] = cw[:, :, part * 384 + p * 128: part * 384 + (p + 1) * 128].transpose(0, 2, 1)
    m["gconv"] = o
    gb = np.zeros((DEPTH, 128, 2), np.float32)
    for d in range(2):
        gb[:, d * 32:d * 32 + 6, 0] = inp["gdn_dt_bias"][:, d, :]
        gb[:, d * 32:d * 32 + 6, 1] = inp["gdn_a_log"][:, d, :]
    m["ggb"] = gb
    m["gng"] = np.ascontiguousarray(np.tile(inp["gdn_norm_g"], (1, 2)).reshape(DEPTH, 128, 1))
    sel = np.zeros((64, 6, 128), np.float32)
    for d in range(2):
        for p in range(3):
            sel[d * 32 + 2 * p, d * 3 + p, 0:64] = 1.0
            sel[d * 32 + 2 * p + 1, d * 3 + p, 64:128] = 1.0
    m["gsel"] = sel
    j = np.arange(64)[:, None]
    i = np.arange(64)[None, :]
    nm = np.zeros((128, 2, 64), np.float32)
    nm[:, 0, :] = np.tile(np.where(i > j, 0.0, NEG), (2, 1))
    nm[:, 1, :] = np.tile(np.where(i < j, 0.0, NEG), (2, 1))
    m["gnegm"] = nm
    m["gid2"] = np.ascontiguousarray(np.tile(np.eye(64, dtype=np.float32), (2, 1)))


def gdn_decl(k):
    nc = k.nc

    def din(name, shape, dt=F32):
        return nc.dram_tensor(name, list(shape), dt, kind="ExternalInput").ap()
    k.gconv_d = din("gconv", [DEPTH, 128, 9, 5])
    k.ggb_d = din("ggb", [DEPTH, 128, 2])
    k.gng_d = din("gng", [DEPTH, 128, 1])
    k.gsel_d = din("gsel", [64, 6, 128])
    k.gnegm_d = din("gnegm", [128, 2, 64])
    k.gid2_d = din("gid2", [128, 64])


def bc3(ap2, n):
    return ap2.unsqueeze(2).broadcast_to([ap2.shape[0], ap2.shape[1], n])


def bcm(ap2, n):
    return ap2.unsqueeze(1).broadcast_to([ap2.shape[0], n, ap2.shape[1]])


HS = (slice(0, 64), slice(64, 128))


def v3(ps, n=8):
    return ps[:, 0:n * 64].rearrange("p (a b) -> p a b", a=n)


def mm2(P, ps, c, lhsT, rhs, **kw):
    for hs in HS:
        P.mm(ps[hs, c * 64:(c + 1) * 64], lhsT[hs], rhs[hs], **kw)


def tr2(P, ps, c, in_, ident):
    for hs in HS:
        P.mm(ps[hs, c * 64:(c + 1) * 64], in_[hs], ident[hs, hs])


def neumann2(k, Nn, Rm, tmp, bank, id2, n8=8):
    P = k.P
    idb = bcm(id2[:, :], n8)
    tA, tB, tC, tD = tmp
    pa_, pb_, pc_ = bank
    for c in range(n8):
        tr2(P, pa_, c, Nn[:, c, :], k.identb)
    P.cp(tA[:], v3(pa_, n8), eng="act")
    P.tt(Rm[:], Nn[:], idb, ALU.add)
    yield
    cur, curT = Nn, tA
    targets = [(tB, tC), (tD, tA)]
    for lvl in range(1, 7):
        nxt, nxtT = targets[(lvl - 1) % 2]
        for c in range(n8):
            if lvl <= 5:
                mm2(P, pb_, c, cur[:, c, :], curT[:, c, :])
                if lvl < 5:
                    mm2(P, pa_, c, curT[:, c, :], cur[:, c, :])
            if lvl >= 2:
                mm2(P, pc_, c, curT[:, c, :], Rm[:, c, :])
        if lvl <= 5:
            P.cp(nxtT[:], v3(pb_, n8), eng="act")
            if lvl < 5:
                P.cp(nxt[:], v3(pa_, n8), eng="act")
        if lvl >= 2:
            P.tt(Rm[:], Rm[:], v3(pc_, n8), ALU.add)
        yield
        cur, curT = nxt, nxtT


def run_interleaved(gens):
    gens = list(gens)
    while gens:
        for g in list(gens):
            try:
                next(g)
            except StopIteration:
                gens.remove(g)


def norm_pipe(k, n, src_fn, sqb, rnb, banks, bones, power, scale, bias, post_fn):
    P = k.P

    def pre(i):
        P.act(sqb[i % 2][:], src_fn(i), AF.Square)
        P.mm(banks[i % 2][:, :], bones[:], sqb[i % 2][:])

    def post(i):
        P.rpow(rnb[i % 2][:], banks[i % 2][:, :], power, scale=scale, bias=bias)
        post_fn(i, rnb[i % 2])
    pre(0)
    for i in range(n):
        if i + 1 < n:
            pre(i + 1)
        post(i)


def run_pipelined(chains, depth=2):
    active = []
    nxt = [0] * len(chains)

    def start(ci):
        if nxt[ci] < len(chains[ci]):
            active.append((ci, chains[ci][nxt[ci]](nxt[ci] % depth)))
            nxt[ci] += 1
    for ci in range(len(chains)):
        for _ in range(depth):
            start(ci)
    while active:
        for item in list(active):
            try:
                next(item[1])
            except StopIteration:
                active.remove(item)
                start(item[0])


def gdn_phase(k, l):
    P = k.P
    with scope(k):
        GC = P.sb("g_GC", [64, L], F32, blk=512)
        GP = [P.sb("g_GP%d" % p, [128, NCK, 4], F32) for p in range(3)]
        NBP = [P.sb("g_NBP%d" % p, [128, NCK, 2], F32) for p in range(3)]
        sel = P.sb("g_sel", [64, 6, 128], F32)
        negm = P.sb("g_negm", [128, 2, 64], F32)
        id2 = P.sb("g_id2", [128, 64], F32)
        cw = P.sb("g_cw", [128, 9, 5], F32)
        ng = P.sb("g_ng", [128, 1], F32)
        bones = P.sb("g_bones", [128, 128], F32)
        P.dma(sel[:], k.gsel_d[:])
        P.dma(negm[:], k.gnegm_d[:])
        P.dma(id2[:], k.gid2_d[:])
        P.dma(cw[:], k.gconv_d[l])
        P.dma(ng[:], k.gng_d[l])
        P.memset(bones[:], 0.0)
        P.memset(bones[0:64, 0:64], 1.0)
        P.memset(bones[64:128, 64:128], 1.0)
        with scope(k):
            GT = P.sb("g_GT", [128, L], F32, blk=512)
            m0 = P.sb("g_m0", [64, L], F32)
            gb = P.sb("g_gb", [128, 2], F32)
            negA = P.sb("g_negA", [128, 1], F32)
            P.dma(gb[:], k.ggb_d[l])
            P.act(negA[:], gb[:, 1:2], AF.Exp)
            P.ts(negA[:], negA[:], -1.0, ALU.mult)
            P.memset(m0[:], 1.0)
            P.memset(m0[:, 0:L:64], 0.0)
            pa = proj(k, l, "gab")
            for tb in range(4):
                sl = slice(tb * 512, (tb + 1) * 512)
                P.act(GT[0:64, sl], pa[tb][0:64, :], AF.Exp, bias=gb[0:64, 0:1])
                P.act(GT[64:128, sl], pa[tb][64:128, :], AF.Sigmoid)
            P.act(GT[0:64, :], GT[0:64, :], AF.Ln, bias=1.0)
            P.ts(GT[0:64, :], GT[0:64, :], negA[0:64, 0:1], ALU.mult)
            P.scan(GC[:, :], m0[:, :], GT[0:64, :], 0.0, ALU.mult, ALU.add)
            gc3 = GC[32:64, :].rearrange("p (a b) -> p a b", b=64)
            P.tt(m0[32:64, :].rearrange("p (a b) -> p a b", b=64), bc3(GC[32:64, 63:L:64], 64), gc3, ALU.subtract)
            P.tt(GC[32:64, :], m0[32:64, :], GT[32:64, :], ALU.add)
            for grp in range(4):
                g8 = slice(grp * 8, (grp + 1) * 8)
                for c in range(8):
                    ck = grp * 8 + c
                    cs = slice(c * 64, (c + 1) * 64)
                    for hs in HS:
                        P.mm(k.PB[2 * (grp % 2)][hs, cs], GC[:, ck * 64:(ck + 1) * 64], k.ident[0:64, 0:64])
                        P.mm(k.PB[2 * (grp % 2) + 1][hs, cs], GT[64:128, ck * 64:(ck + 1) * 64], k.ident[64:128, 64:128])
                n_ = 0
                for p in range(3):
                    for hf, hs in enumerate(HS):
                        h = 2 * p + hf
                        for q, ps in ((0, k.PB[2 * (grp % 2)]), (1, k.PB[2 * (grp % 2) + 1])):
                            src = v3(ps)[hs, :, h:h + 33:32]
                            P.cp(GP[p][hs, g8, 2 * q:2 * q + 2], src, eng=("act" if q else "dve"))
            for p in range(3):
                P.ts(NBP[p][:], GP[p][:, :, 2:4], -1.0, ALU.mult)
        for p in range(3):
            with scope(k):
                Q = P.sb("g_Q", [128, L], BF16, blk=512)
                K_ = P.sb("g_K", [128, L], BF16, blk=512)
                Kt = P.sb("g_Kt", [128, NCK, 64], BF16, blk=512)
                Vt = P.sb("g_Vt", [128, NCK, 64], BF16, blk=512)
                O = P.sb("g_O", [128, L], F32, blk=512)
                P.memset(O[:], 0.0, eng="pool")
                with scope(k):
                    xp = P.sb("g_xp", [128, L + 4], BF16)
                    Dg = P.sb("g_Dg", [128, 5, 128], BF16)
                    Vf = P.sb("g_Vf", [128, L], BF16, blk=512)
                    cv = P.sb("g_cv", [128, L], F32, blk=512)
                    sq = P.sb("g_sq", [128, 512], F32)
                    rn = P.sb("g_rn", [128, 512], F32)
                    sq2 = P.sb("g_sq2", [128, 512], F32)
                    rn2 = P.sb("g_rn2", [128, 512], F32)
                    P.memset(xp[:, 0:2], 0.0)
                    P.memset(xp[:, L + 2:L + 4], 0.0)
                    for part, nm, dst in ((0, "gq", Q), (1, "gk", K_), (2, "gv", Vf)):
                        pa = proj(k, l, "%s%d" % (nm, p))
                        for tb in range(4):
                            P.cp(xp[:, 2 + tb * 512:2 + (tb + 1) * 512], pa[tb][:, :], eng=("act" if tb % 2 else "dve"))
                        wi = part * 3 + p
                        for j in range(5):
                            P.ts(Dg[:, j, :], k.identb[:], cw[:, wi, j:j + 1], ALU.mult, eng=("pool" if j % 2 else "dve"))
                        for tb in range(4):
                            for j in range(5):
                                P.mm(k.PB[tb][:, :], Dg[:, j, :], xp[:, j + tb * 512:j + (tb + 1) * 512],
                                     start=(j == 0), stop=(j == 4))
                        for tb in range(4):
                            sl = slice(tb * 512, (tb + 1) * 512)
                            P.act((dst if part == 2 else cv)[:, sl], k.PB[tb][:, :], AF.Silu)
                        if part < 2:
                            def fin(tb, r, dst=dst):
                                P.tt(dst[:, tb * 512:(tb + 1) * 512], cv[:, tb * 512:(tb + 1) * 512], r[:], ALU.mult)
                            norm_pipe(k, 4, lambda tb: cv[:, tb * 512:(tb + 1) * 512], (sq, sq2), (rn, rn2),
                                      (k.PA[2], k.PA[3]), bones, -0.5, 64.0 if part == 0 else 1.0,
                                      64e-6 if part == 0 else 1e-6, fin)
                    for src, dstt in ((K_, Kt), (Vf, Vt)):
                        for grp in range(4):
                            ps = k.PB[grp % 2]
                            for c in range(8):
                                ck = grp * 8 + c
                                tr2(P, ps, c, src[:, ck * 64:(ck + 1) * 64], k.identb)
                            P.cp(dstt[:, grp * 8:(grp + 1) * 8, :], v3(ps), eng=("act" if grp % 2 else "dve"))
                with scope(k):
                    T = dict(GC=GC, GP=GP[p], NBP=NBP[p], sel=sel, negm=negm, id2=id2, Q=Q, K=K_, Kt=Kt, Vt=Vt, O=O)
                    run_pipelined([gdn_chain(k, p, d, T) for d in range(2)], depth=1)
                with scope(k):
                    sqo = [P.sb("g_osq%d" % i, [128, 512], F32) for i in range(2)]
                    rno = [P.sb("g_orn%d" % i, [128, 512], F32) for i in range(2)]

                    def fin_o(tb, r):
                        sl = slice(tb * 512, (tb + 1) * 512)
                        P.tt(r[:], O[:, sl], r[:], ALU.mult)
                        P.ts(mixc(k, p)[:, sl], r[:], ng[:, 0:1], ALU.mult)
                    norm_pipe(k, 4, lambda tb: O[:, tb * 512:(tb + 1) * 512], sqo, rno, (k.PB[2], k.PB[3]), bones,
                              -0.5, 1.0 / 64, EPS, fin_o)


def gdn_chain(k, p, d, T):
    P = k.P
    GC, GP, NBP, sel, negm, id2, Q, K_, Kt, Vt, O = (T[n] for n in ("GC", "GP", "NBP", "sel", "negm", "id2", "Q", "K", "Kt", "Vt", "O"))
    B = k.PA if d == 0 else k.PB
    tag = "g%d_" % d
    names = ("CB", "EI", "QG", "Rm", "U0", "WT", "BW", "KD", "GK", "AcT", "Sg", "Ug", "nA", "nB", "nC", "nD", "Nb", "PTb", "Sgb")
    f32n = ("CB", "EI", "AcT", "Sg")
    NSET = 1
    GG = [{n: P.sb(tag + "%d" % s_ + n, [128, 8, 64], F32 if n in f32n else BF16) for n in names} for s_ in range(NSET)]
    for s_ in range(NSET):
        GG[s_]["gend"] = P.sb(tag + "gend%d" % s_, [128, 8], F32)
        GG[s_]["kds"] = P.sb(tag + "kds%d" % s_, [128, 8], F32)
    gam = P.sb(tag + "gam", [128, NCK], F32)
    shared = {"scan": 0}
    Scar = P.sb(tag + "Scar", [128, 64], F32)
    e_ = 63 if d == 0 else 0
    idb = bcm(id2[:, :], 8)

    def f2(t):
        return t[:].rearrange("p a b -> p (a b)")
    P.memset(Scar[:], 0.0)
    P.act(gam[:], GP[:, :, d], AF.Exp)
    gorder = list(range(4)) if d == 0 else list(range(3, -1, -1))

    def group(gi, grp, G):
        Nb, PTb, Sgb, gend, kds = G["Nb"], G["PTb"], G["Sgb"], G["gend"], G["kds"]
        sl = slice(grp * 512, (grp + 1) * 512)
        g8 = slice(grp * 8, (grp + 1) * 8)
        cj = GP[:, g8, d]
        nb = NBP[:, g8, d]
        CB, EI, QG, Rm, U0, WT, BW, KD, GK, AcT, Sg, Ug = (G[n] for n in names[:12])
        P.mm(B[0][:, :], sel[:, d * 3 + p, :], GC[:, sl])
        P.cp(f2(CB), B[0][:, :], eng="act")
        P.act(f2(EI), f2(CB), AF.Exp)
        P.cp(gend[:], EI[:, :, e_], eng="pool")
        P.tt(f2(QG), f2(EI), Q[:, sl], ALU.mult)
        P.tt(kds[:], CB[:, :, e_], cj, ALU.subtract)
        P.act(kds[:], kds[:], AF.Exp)
        P.tt(GK[:], Kt[:, g8, :], bc3(gam[:, g8], 64), ALU.mult, eng="pool")
        P.tt(KD[:], Kt[:, g8, :], bc3(kds[:], 64), ALU.mult, eng="pool")
        P.tt(CB[:], CB[:], bc3(cj, 64), ALU.subtract)
        P.tt(CB[:], CB[:], bcm(negm[:, d, :], 8), ALU.add)
        P.act(f2(CB), f2(CB), AF.Exp)
        yield
        for c in range(8):
            cs = slice((grp * 8 + c) * 64, (grp * 8 + c + 1) * 64)
            mm2(P, B[0], c, K_[:, cs], Q[:, cs])
            mm2(P, B[1], c, K_[:, cs], K_[:, cs])
        P.tt(EI[:], CB[:], idb, ALU.add)
        P.tt(PTb[:], EI[:], v3(B[0]), ALU.mult)
        P.tt(CB[:], CB[:], v3(B[1]), ALU.mult)
        P.tt(Nb[:], CB[:], bc3(nb, 64), ALU.mult)
        yield
        for _ in neumann2(k, Nb, Rm, (G["nA"], G["nB"], G["nC"], G["nD"]), (B[0], B[1], B[2]), id2):
            yield
        for c in range(8):
            mm2(P, B[0], c, Rm[:, c, :], Vt[:, grp * 8 + c, :])
            mm2(P, B[1], c, GK[:, c, :], Rm[:, c, :])
            mm2(P, B[2], c, Rm[:, c, :], GK[:, c, :])
        P.tt(U0[:], v3(B[0]), bc3(GP[:, g8, 2 + d], 64), ALU.mult)
        P.cp(WT[:], v3(B[1]), eng="act")
        P.tt(BW[:], v3(B[2]), bc3(nb, 64), ALU.mult)
        yield
        for c in range(8):
            mm2(P, B[3], c, BW[:, c, :], KD[:, c, :])
        P.tt(AcT[:], idb, bc3(gend[:], 64), ALU.mult, eng="pool")
        P.tt(AcT[:], AcT[:], v3(B[3]), ALU.add)
        yield
        while shared["scan"] != gi:
            yield
        corder = range(8) if d == 0 else range(7, -1, -1)
        prev = Scar[:]
        for n, c in enumerate(corder):
            P.cp(Sg[:, c, :], prev, eng="pool") if n == 0 else None
            ps = B[2 + n % 2]
            mm2(P, ps, 0, AcT[:, c, :], Sg[:, c, :], start=True, stop=False)
            mm2(P, ps, 0, KD[:, c, :], U0[:, c, :], start=False, stop=True)
            last = (n == 7)
            dst = Scar[:] if last else Sg[:, corder[n + 1], :]
            P.cp(dst, ps[:, 0:64], eng="act")
            yield
        shared["scan"] = gi + 1
        P.cp(Sgb[:], Sg[:], eng="act")
        for c in range(8):
            mm2(P, B[0], c, WT[:, c, :], Sgb[:, c, :])
        P.tt(CB[:], v3(B[0]), bc3(nb, 64), ALU.mult)
        P.tt(Ug[:], CB[:], U0[:], ALU.add)
        yield
        for c in range(8):
            mm2(P, B[1], c, Sgb[:, c, :], QG[:, c, :], start=True, stop=False)
            mm2(P, B[1], c, Ug[:, c, :], PTb[:, c, :], start=False, stop=True)
        P.tt(O[:, sl], O[:, sl], B[1][:, :], ALU.add)
        yield
    return [(lambda slot, gi=gi, grp=grp: group(gi, grp, GG[slot])) for gi, grp in enumerate(gorder)]


RW_EPS = 64e-5
DEC = float(np.exp(-0.5))
GS = 8
NG = NCK // GS


def host_rwkv(inp, b, m):
    mu = inp["rwkv_mu"]
    o = np.zeros((DEPTH, 128, 8, 2), np.float32)
    for part in range(3):
        for p in range(2):
            o[:, :, part * 2 + p, :] = mu[:, :, part * 256 + p * 128: part * 256 + (p + 1) * 128].transpose(0, 2, 1)
    o[:, 0:64, 6, :] = mu[:, :, 768:832].transpose(0, 2, 1)
    o[:, 0:64, 7, :] = mu[:, :, 832:896].transpose(0, 2, 1)
    m["rmu"] = o

    def pp(a):
        if a.ndim == 2:
            return np.ascontiguousarray(a.reshape(DEPTH, 2, 128).transpose(0, 2, 1))
        return np.ascontiguousarray(a.reshape(DEPTH, 2, 2, 128).transpose(0, 3, 1, 2))
    pv = np.zeros((DEPTH, 128, 7, 2), np.float32)
    pv[:, :, 0:2, :] = pp(inp["rwkv_w0"])
    pv[:, :, 2:4, :] = pp(inp["rwkv_a0"])
    pv[:, :, 4, :] = pp(inp["rwkv_k_k"])
    pv[:, :, 5, :] = pp(inp["rwkv_k_a"])
    pv[:, :, 6, :] = pp(inp["rwkv_r_k"].reshape(DEPTH, 256))
    m["rpv"] = pv
    ln = np.zeros((DEPTH, 128, 2, 2), np.float32)
    ln[:, :, 0, :] = pp(inp["rwkv_ln_g"])
    ln[:, :, 1, :] = pp(inp["rwkv_ln_b"])
    m["rln"] = ln
    m["rw2"] = np.ascontiguousarray(inp["rwkv_w2"].transpose(0, 2, 1, 3))
    m["ra2"] = np.ascontiguousarray(inp["rwkv_a2"].transpose(0, 2, 1, 3))
    s_ = np.arange(64)[:, None]
    t_ = np.arange(64)[None, :]
    msk = np.zeros((128, 2, 4, 64), np.float32)
    msk[:, 0, 0, :] = np.tile((t_ > s_), (2, 1))
    msk[:, 0, 1, :] = np.tile((t_ >= s_), (2, 1))
    msk[:, 1, 0, :] = np.tile((t_ < s_), (2, 1))
    msk[:, 1, 1, :] = np.tile((t_ <= s_), (2, 1))
    msk[:, :, 2:4, :] = -msk[:, :, 0:2, :]
    m["rmsk"] = msk


def rwkv_decl(k):
    nc = k.nc

    def din(name, shape, dt=F32):
        return nc.dram_tensor(name, list(shape), dt, kind="ExternalInput").ap()
    k.rmu_d = din("rmu", [DEPTH, 128, 8, 2])
    k.rpv_d = din("rpv", [DEPTH, 128, 7, 2])
    k.rln_d = din("rln", [DEPTH, 128, 2, 2])
    k.rw2_d = din("rw2", [DEPTH, 64, 2, 256])
    k.ra2_d = din("ra2", [DEPTH, 64, 2, 256])
    k.rmsk_d = din("rmsk", [128, 2, 4, 64])


def rwkv_phase(k, l):
    P = k.P
    with scope(k):
        mu = P.sb("r_mu", [128, 8, 3], F32)
        pv = P.sb("r_pv", [128, 7, 2], F32)
        omka = P.sb("r_omka", [128, 2], F32)
        hrk = P.sb("r_hrk", [128, 2], F32)
        ln = P.sb("r_ln", [128, 2, 2], F32)
        w2 = P.sb("r_w2", [64, 2, 256], BF16)
        a2 = P.sb("r_a2", [64, 2, 256], BF16)
        msk = P.sb("r_msk", [128, 2, 4, 64], F32)
        id2 = P.sb("r_id2", [128, 64], F32)
        bones = P.sb("r_bones", [128, 128], F32)
        m0 = P.sb("r_m0", [128, GS * 64], F32)
        twd = P.sb("r_twd", [64, L], BF16, blk=512)
        adx = P.sb("r_adx", [64, L], BF16, blk=512)
        sh32 = P.sb("r_sh32", [128, L], F32, blk=512)
        xp = P.sb("r_xp", [128, L + 2], F32)
        P.dma(mu[:, :, 0:2], k.rmu_d[l])
        P.dma(pv[:], k.rpv_d[l])
        P.dma(ln[:], k.rln_d[l])
        with scope(k):
            w2f = P.sb("r_w2f", [64, 2, 256], F32)
            a2f = P.sb("r_a2f", [64, 2, 256], F32)
            P.dma(w2f[:], k.rw2_d[l])
            P.dma(a2f[:], k.ra2_d[l])
            P.cp(w2[:], w2f[:], eng="act")
            P.cp(a2[:], a2f[:], eng="act")
        P.dma(msk[:], k.rmsk_d[:])
        P.dma(id2[:], k.gid2_d[:])
        P.memset(bones[:], 0.0)
        P.memset(bones[0:64, 0:64], 1.0)
        P.memset(bones[64:128, 64:128], 1.0)
        P.memset(m0[:], 1.0)
        P.memset(m0[:, 0:GS * 64:64], 0.0)
        P.memset(xp[:, 0:1], 0.0)
        P.memset(xp[:, L + 1:L + 2], 0.0)
        P.tt(mu[:, :, 2], mu[:, :, 0], mu[:, :, 1], ALU.add)
        P.ts(mu[:, :, 2], mu[:, :, 2], -1.0, ALU.mult, 1.0, ALU.add)
        P.ts(omka[:], pv[:, 5, :], -1.0, ALU.mult, 1.0, ALU.add)
        P.ts(hrk[:], pv[:, 6, :], 0.5, ALU.mult)

        def shifted(name, ci, dst, np_=128, fn=None):
            pa = proj(k, l, name, alt=(ci % 2 == 1))
            for tb in range(4):
                P.cp(xp[0:np_, 1 + tb * 512:1 + (tb + 1) * 512], pa[tb][0:np_, :], eng=("act" if tb % 2 else "dve"))
            t_ = sh32[0:np_, :]
            P.ts(t_, xp[0:np_, 1:L + 1], mu[0:np_, ci, 2:3], ALU.mult)
            P.stt(t_, xp[0:np_, 0:L], mu[0:np_, ci, 0:1], t_, ALU.mult, ALU.add)
            if fn is None:
                P.stt(dst[:], xp[0:np_, 2:L + 2], mu[0:np_, ci, 1:2], t_, ALU.mult, ALU.add)
            else:
                P.stt(t_, xp[0:np_, 2:L + 2], mu[0:np_, ci, 1:2], t_, ALU.mult, ALU.add)
                P.act(dst[:], t_, fn)

        shifted("rwd", 6, twd, 64, AF.Tanh)
        shifted("rad", 7, adx, 64)
        R_ = P.sb("r_R", [128, L], BF16, blk=512)
        KX = P.sb("r_KX", [128, L], BF16, blk=512)
        V_ = P.sb("r_V", [128, L], BF16, blk=512)
        KK = P.sb("r_KK", [128, L], BF16, blk=512)
        Vt = P.sb("r_Vt", [128, NCK, 64], BF16, blk=512)
        KS = P.sb("r_KS", [128, L], F32, blk=512)
        Y = xp[:, 1:L + 1]
        sq = P.sb("r_sq", [128, 512], F32)
        rn = P.sb("r_rn", [128, 512], F32)
        CH = [rwkv_tiles(k, e) for e in range(2)]

        class _V:
            def __init__(s_, t):
                s_.t = t

            def __getitem__(s_, key):
                return s_.t[:].rearrange("p a b -> p (a b)")[key]
        sq2, rn2 = _V(CH[0]["lw"]), _V(CH[0]["a"])
        for p in range(2):
            shifted("rr%d" % p, 0 + p, R_)
            shifted("rk%d" % p, 2 + p, KX)
            shifted("rv%d" % p, 4 + p, V_)
            P.ts(sh32[:], KX[:], pv[:, 4, p:p + 1], ALU.mult)

            def fin_k(tb, r):
                P.tt(KK[:, tb * 512:(tb + 1) * 512], sh32[:, tb * 512:(tb + 1) * 512], r[:], ALU.mult)
            norm_pipe(k, 4, lambda tb: sh32[:, tb * 512:(tb + 1) * 512], (sq, sq2), (rn, rn2), (k.PB[2], k.PB[3]),
                      bones, -0.5, 1.0, 1e-6, fin_k)
            for grp in range(4):
                ps = k.PB[grp % 2]
                for c in range(8):
                    ck = grp * 8 + c
                    tr2(P, ps, c, V_[:, ck * 64:(ck + 1) * 64], k.identb)
                P.cp(Vt[:, grp * 8:(grp + 1) * 8, :], v3(ps), eng=("act" if grp % 2 else "dve"))
            P.memset(xp[:, 1:L + 1], 0.0, eng="pool")
            P.memset(KS[:], 0.0, eng="pool")
            T = dict(pv=pv, omka=omka, w2=w2, a2=a2, msk=msk, id2=id2, m0=m0, twd=twd, adx=adx,
                     R=R_, KX=KX, KK=KK, Vt=Vt, KS=KS, Y=Y)
            run_interleaved([rwkv_chain(k, p, e, T, CH[e]) for e in range(2)])
            for tb in range(4):
                sl = slice(tb * 512, (tb + 1) * 512)
                P.mm(k.PB[tb % 2][:, :], bones[:], Y[:, sl])
                P.stt(Y[:, sl], k.PB[tb % 2][:, :], -1.0 / 64, Y[:, sl], ALU.mult, ALU.add)

            def fin_y(tb, r):
                sl = slice(tb * 512, (tb + 1) * 512)
                P.tt(Y[:, sl], Y[:, sl], r[:], ALU.mult)
                P.ts(Y[:, sl], Y[:, sl], ln[:, 0, p:p + 1], ALU.mult, ln[:, 1, p:p + 1], ALU.add)
            norm_pipe(k, 4, lambda tb: Y[:, tb * 512:(tb + 1) * 512], (sq, sq2), (rn, rn2), (k.PB[2], k.PB[3]),
                      bones, -0.5, 1.0 / 64, RW_EPS, fin_y)
            for tb in range(4):
                sl = slice(tb * 512, (tb + 1) * 512)
                s_ = (sq, sq2)[tb % 2]
                r_ = (rn, rn2)[tb % 2]
                P.tt(s_[:], R_[:, sl], KS[:, sl], ALU.mult)
                P.ts(s_[:], s_[:], hrk[:, p:p + 1], ALU.mult, eng="pool")
                P.mm(k.PB[tb % 2][:, :], bones[:], s_[:])
                P.tt(r_[:], k.PB[tb % 2][:, :], V_[:, sl], ALU.mult)
                P.tt(mixc(k, 6 + p)[:, sl], Y[:, sl], r_[:], ALU.add)


RW_F32 = ("lw", "a", "km", "b", "cl", "e1", "e2", "dend", "AcT", "Tg")
RW_BF16 = ("kap", "rt", "kt_", "bt_", "ke", "be", "kapT", "keT", "nbeT", "N", "Akv", "Brk", "nBrb", "Rm",
           "nA", "nB", "nC", "nD", "X0", "P0", "WkT", "Wk", "Tgb", "Pg")


def rwkv_tiles(k, e):
    P = k.P
    G = {n: P.sb("r%d_%s" % (e, n), [128, GS, 64], F32) for n in RW_F32}
    for n in RW_BF16:
        G[n] = P.sb("r%d_%s" % (e, n), [128, GS, 64], BF16)
    G["gC"] = P.sb("r%d_gC" % e, [128, GS], F32)
    G["Tcar"] = P.sb("r%d_Tcar" % e, [128, 64], F32)
    return G


def rwkv_chain(k, p, e, T, G):
    P = k.P
    pv, omka, w2, a2, msk, id2, m0, twd, adx, R_, KX, KK, Vt, KS, Y = (T[n] for n in (
        "pv", "omka", "w2", "a2", "msk", "id2", "m0", "twd", "adx", "R", "KX", "KK", "Vt", "KS", "Y"))
    B = k.PA if e == 0 else k.PB
    W = GS * 64
    e_ = 63 if e == 0 else 0
    idb = bcm(id2[:, :], GS)
    gC, Tcar = G["gC"], G["Tcar"]

    def f2(t):
        return t[:].rearrange("p a b -> p (a b)")

    def w3(ps):
        return v3(ps, GS)
    P.memset(Tcar[:], 0.0)
    gorder = range(NG) if e == 0 else range(NG - 1, -1, -1)
    pc = slice(p * 128, (p + 1) * 128)
    for grp in gorder:
        sl = slice(grp * W, (grp + 1) * W)
        c0 = grp * GS
        P.mm(B[0][:, 0:W], w2[:, e, pc], twd[:, sl])
        P.mm(B[1][:, 0:W], a2[:, e, pc], adx[:, sl])
        P.act(f2(G["lw"]), B[0][:, 0:W], AF.Sigmoid, bias=pv[:, 0 + e, p:p + 1])
        P.act(f2(G["a"]), B[1][:, 0:W], AF.Sigmoid, bias=pv[:, 2 + e, p:p + 1])
        P.ts(f2(G["km"]), f2(G["a"]), pv[:, 5, p:p + 1], ALU.mult, omka[:, p:p + 1], ALU.add)
        P.tt(f2(G["km"]), f2(G["km"]), KX[:, sl], ALU.mult)
        P.tt(f2(G["b"]), f2(G["a"]), KK[:, sl], ALU.mult, eng="pool")
        P.tt(KS[:, sl], KS[:, sl], f2(G["km"]), ALU.add, eng="pool")
        P.scan(f2(G["cl"]), m0[:], f2(G["lw"]), 0.0, ALU.mult, ALU.add)
        if e == 1:
            P.tt(G["e1"][:], bc3(G["cl"][:, :, 63], 64), G["cl"][:], ALU.subtract)
            P.tt(G["cl"][:], G["e1"][:], G["lw"][:], ALU.add)
        yield
        P.act(G["e1"][:], G["cl"][:], AF.Exp, scale=-DEC)
        P.act(G["e2"][:], G["cl"][:], AF.Exp, scale=DEC)
        P.tt(f2(G["rt"]), f2(G["e1"]), R_[:, sl], ALU.mult)
        P.tt(G["kt_"][:], G["e2"][:], G["km"][:], ALU.mult, eng="pool")
        P.tt(G["bt_"][:], G["e2"][:], G["b"][:], ALU.mult)
        P.tt(G["dend"][:], G["cl"][:], G["lw"][:], ALU.subtract, eng="pool")
        P.act(G["dend"][:], G["dend"][:], AF.Exp, scale=-DEC)
        P.tt(f2(G["kap"]), f2(G["dend"]), KK[:, sl], ALU.mult)
        P.cp(gC[:], G["e1"][:, :, e_], eng="pool")
        P.tt(G["dend"][:], bc3(G["cl"][:, :, e_], 64), G["cl"][:], ALU.subtract, eng="pool")
        P.act(G["dend"][:], G["dend"][:], AF.Exp, scale=-DEC)
        P.tt(G["ke"][:], G["dend"][:], G["km"][:], ALU.mult)
        P.tt(G["be"][:], G["dend"][:], G["b"][:], ALU.mult, eng="pool")
        yield
        for src, dst, sc in ((G["kap"], G["kapT"], 1.0), (G["ke"], G["keT"], 1.0), (G["be"], G["nbeT"], -1.0)):
            ps = B[0] if sc == 1.0 and src is G["kap"] else (B[1] if sc == 1.0 else B[2])
            for c in range(GS):
                tr2(P, ps, c, src[:, c, :], k.identb)
            if sc == 1.0:
                P.cp(dst[:], w3(ps), eng="act")
            else:
                P.ts(dst[:], w3(ps), -1.0, ALU.mult)
        yield
        for c in range(GS):
            mm2(P, B[0], c, G["bt_"][:, c, :], G["kap"][:, c, :])
            mm2(P, B[1], c, G["kt_"][:, c, :], G["kap"][:, c, :])
            mm2(P, B[2], c, G["kt_"][:, c, :], G["rt"][:, c, :])
            mm2(P, B[3], c, G["bt_"][:, c, :], G["rt"][:, c, :])
        ms = bcm(msk[:, e, 0, :], GS)
        mi = bcm(msk[:, e, 1, :], GS)
        nms = bcm(msk[:, e, 2, :], GS)
        nmi = bcm(msk[:, e, 3, :], GS)
        P.tt(G["N"][:], w3(B[0]), nms, ALU.mult)
        P.tt(G["Akv"][:], w3(B[1]), ms, ALU.mult)
        P.tt(G["Brk"][:], w3(B[2]), mi, ALU.mult)
        P.tt(G["nBrb"][:], w3(B[3]), nmi, ALU.mult)
        yield
        for _ in neumann2(k, G["N"], G["Rm"], (G["nA"], G["nB"], G["nC"], G["nD"]), (B[0], B[1], B[2]), id2, GS):
            yield
        for c in range(GS):
            mm2(P, B[0], c, G["Akv"][:, c, :], Vt[:, c0 + c, :])
        P.cp(G["X0"][:], w3(B[0]), eng="act")
        yield
        for c in range(GS):
            mm2(P, B[0], c, G["Rm"][:, c, :], G["X0"][:, c, :])
            mm2(P, B[1], c, G["kapT"][:, c, :], G["Rm"][:, c, :])
            mm2(P, B[2], c, G["Rm"][:, c, :], G["kapT"][:, c, :])
        P.cp(G["P0"][:], w3(B[0]), eng="act")
        P.cp(G["WkT"][:], w3(B[1]), eng="dve")
        P.cp(G["Wk"][:], w3(B[2]), eng="act")
        yield
        for c in range(GS):
            mm2(P, B[3], c, G["Wk"][:, c, :], G["nbeT"][:, c, :])
        P.tt(G["AcT"][:], idb, bc3(gC[:], 64), ALU.mult, eng="pool")
        P.tt(G["AcT"][:], G["AcT"][:], w3(B[3]), ALU.add)
        yield
        corder = list(range(GS)) if e == 0 else list(range(GS - 1, -1, -1))
        Tg = G["Tg"]
        for n, c in enumerate(corder):
            if n == 0:
                P.cp(Tg[:, c, :], Tcar[:], eng="pool")
            ps = B[2 + n % 2]
            mm2(P, ps, 0, G["AcT"][:, c, :], Tg[:, c, :], start=True, stop=False)
            mm2(P, ps, 0, G["keT"][:, c, :], Vt[:, c0 + c, :], start=False, stop=False)
            mm2(P, ps, 0, G["nbeT"][:, c, :], G["P0"][:, c, :], start=False, stop=True)
            dst = Tcar[:] if n == GS - 1 else Tg[:, corder[n + 1], :]
            P.cp(dst, ps[:, 0:64], eng="act")
            yield
        P.cp(G["Tgb"][:], Tg[:], eng="act")
        for c in range(GS):
            mm2(P, B[0], c, G["WkT"][:, c, :], G["Tgb"][:, c, :])
        P.tt(G["Pg"][:], w3(B[0]), G["P0"][:], ALU.add)
        yield
        for c in range(GS):
            mm2(P, B[1], c, Vt[:, c0 + c, :], G["Brk"][:, c, :], start=True, stop=False)
            mm2(P, B[1], c, G["Tgb"][:, c, :], G["rt"][:, c, :], start=False, stop=False)
            mm2(P, B[1], c, G["Pg"][:, c, :], G["nBrb"][:, c, :], start=False, stop=True)
        P.tt(Y[:, sl], Y[:, sl], B[1][:, 0:W], ALU.add)
        yield


_CACHE = {}


def kernel(**inputs):
    inp = {k_: np.asarray(v) for k_, v in inputs.items()}
    if "k" not in _CACHE:
        _CACHE["k"] = build()
    k = _CACHE["k"]
    B = inp["x"].shape[0]
    base = host_inputs(inp, 0)
    in_maps = []
    for b in range(B):
        m = dict(base)
        m["x"] = np.ascontiguousarray(inp["x"][b], dtype=np.float32)
        m["pos"] = np.ascontiguousarray(inp["positions"][b].reshape(1, L).astype(np.int32))
        in_maps.append(m)
    res = run_bass_kernel_spmd(k.nc, in_maps, core_ids=list(range(B)))
    return np.stack([np.asarray(r["out"], dtype=np.float32) for r in res.results], axis=0)
```

```python
import numpy as np
import concourse.bass as bass
import concourse.mybir as mybir
from concourse.bass_utils import run_bass_kernel_spmd
from contextlib import ExitStack

F32 = mybir.dt.float32
BF16 = mybir.dt.bfloat16
I32 = mybir.dt.int32
AF = mybir.ActivationFunctionType
ALU = mybir.AluOpType
AX = mybir.AxisListType
DTSIZE = {F32: 4, BF16: 2, I32: 4}


class _Op:
    __slots__ = ("eng", "emit", "deps", "idx", "needed", "sigval", "dsem", "dval", "isdma")

    def __init__(self, eng, emit, isdma=False):
        self.eng = eng
        self.emit = emit
        self.deps = []
        self.idx = -1
        self.needed = False
        self.sigval = 0
        self.dsem = None
        self.dval = 0
        self.isdma = isdma


class _Blk:
    __slots__ = ("w", "r")

    def __init__(self):
        self.w = None
        self.r = {}


class Prog:
    ENGS = ("pe", "dve", "act", "pool", "sp")
    NDMA = 48
    NHW = 32

    def __init__(self, nc, stack):
        self.nc = nc
        self.stack = stack
        self.ops = {e: [] for e in self.ENGS}
        self.track = {}
        self.seen = {e: {} for e in self.ENGS}
        self.seen_dma = {e: set() for e in self.ENGS}
        self.dma_last = [None] * self.NDMA
        self.dma_uses = [0] * self.NDMA
        self.dma_rr = 0
        self.dma_rr_sw = 0
        self.ndma_ops = 0
        self.untracked = set()
        self.out_dmas = []
        self.dma_pending = []
        self.last_compute = {}

    def sb(self, name, shape, dtype=F32, blk=None):
        self.uid = getattr(self, "uid", 0) + 1
        name = "s%d_%s" % (self.uid, name)
        t = self.stack.enter_context(self.nc.sbuf_tensor(name, list(shape), dtype))
        self._register(name, shape, dtype, blk)
        return t

    def ps(self, name, shape=(128, 512), dtype=F32, blk=None):
        self.uid = getattr(self, "uid", 0) + 1
        name = "p%d_%s" % (self.uid, name)
        t = self.stack.enter_context(self.nc.psum_tensor(name, list(shape), dtype))
        self._register(name, shape, dtype, blk)
        return t

    def _register(self, name, shape, dtype, blk):
        row = int(np.prod(shape[1:])) * DTSIZE[dtype]
        bb = row if blk is None else blk * DTSIZE[dtype]
        nb = (row + bb - 1) // bb
        self.track[name] = (bb, row, [_Blk() for _ in range(nb)])

    def dram_track(self, name, total_bytes, blk_bytes):
        nb = (total_bytes + blk_bytes - 1) // blk_bytes
        self.track[name] = (blk_bytes, -1, [_Blk() for _ in range(nb)])

    def _blocks(self, ap):
        name = ap.tensor.name
        if name not in self.track:
            return ()
        bb, row, blks = self.track[name]
        if len(blks) == 1:
            return blks
        ds = DTSIZE[ap.dtype]
        pat = ap.ap
        if row < 0:
            lo = hi = ap.offset
            for step, cnt in pat:
                ext = step * (cnt - 1)
                if ext < 0:
                    lo += ext
                else:
                    hi += ext
            return blks[(lo * ds) // bb:(hi * ds) // bb + 1]
        rowel = row // ds
        foff = ap.offset % rowel
        lo = hi = foff
        for step, cnt in pat[1:]:
            ext = step * (cnt - 1)
            if ext < 0:
                lo += ext
            else:
                hi += ext
        b0 = (lo * ds) // bb
        b1 = (hi * ds) // bb
        return blks[b0:b1 + 1]

    def _dep(self, x, y):
        if y is None or y is x:
            return
        e = x.eng
        if y.isdma:
            if id(y) in self.seen_dma[e]:
                return
            self.seen_dma[e].add(id(y))
            x.deps.append(y)
            return
        if y.eng == "pe" and e == "pe":
            return
        if y.idx <= self.seen[e].get(y.eng, -1):
            return
        self.seen[e][y.eng] = y.idx
        y.needed = True
        x.deps.append(y)

    def add(self, eng, emit, reads=(), writes=(), isdma=False):
        x = _Op(eng, emit, isdma)
        x.idx = len(self.ops[eng])
        rb = []
        for ap in reads:
            if ap is None or isinstance(ap, (int, float)):
                continue
            rb.extend(self._blocks(ap))
        wb = []
        for ap in writes:
            wb.extend(self._blocks(ap))
        for ap in reads:
            if ap is None or isinstance(ap, (int, float)) or not ap.tensor.name.startswith("p"):
                continue
            for b in self._blocks(ap):
                for key, y in b.r.items():
                    if key != eng:
                        self._dep(x, y)
        for b in rb:
            self._dep(x, b.w)
        for b in wb:
            self._dep(x, b.w)
            for y in b.r.values():
                self._dep(x, y)
        if isdma:
            if eng == "pool":
                s = self.NHW + self.dma_rr_sw
                self.dma_rr_sw = (self.dma_rr_sw + 1) % (self.NDMA - self.NHW)
            else:
                s = self.dma_rr
                self.dma_rr = (self.dma_rr + 1) % self.NHW
            self._dep(x, self.dma_last[s])
            self.dma_last[s] = x
            self.dma_uses[s] += 1
            x.dsem = s
            x.dval = 16 * self.dma_uses[s]
            self.ndma_ops += 1
        key = id(x) if isdma else eng
        for b in rb:
            b.r[key] = x
        for b in wb:
            b.w = x
            b.r = {}
        self.ops[eng].append(x)
        if isdma:
            self.dma_pending.append(x)
        else:
            self.last_compute[eng] = x
        return x

    def barrier(self):
        lasts = dict(self.last_compute)
        pend = list(self.dma_pending)
        self.dma_pending = []
        for e in self.ENGS:
            b = _Op(e, None)
            b.idx = len(self.ops[e])
            for e2, y in lasts.items():
                if e2 == e and e == "pe":
                    continue
                self._dep(b, y)
            for y in pend:
                self._dep(b, y)
            self.ops[e].append(b)

    def mm(self, out, lhsT, rhs, start=True, stop=True):
        return self.add("pe", lambda e: e.matmul(out, lhsT, rhs, start=start, stop=stop),
                        reads=(lhsT, rhs), writes=(out,))

    def tr(self, out, in_, ident):
        return self.add("pe", lambda e: e.transpose(out, in_, ident), reads=(in_, ident), writes=(out,))

    def tt(self, out, in0, in1, op, eng="dve"):
        return self.add(eng, lambda e: e.tensor_tensor(out, in0, in1, op), reads=(in0, in1), writes=(out,))

    def ts(self, out, in0, s1, op0, s2=None, op1=None, eng="dve", accum_out=None):
        kw = {}
        if eng == "pool" and op1 is None:
            if op0 == ALU.mult:
                s2, op1 = 0.0, ALU.add
            elif op0 == ALU.add:
                s2, op1 = 1.0, ALU.mult
        if op1 is not None:
            kw["op1"] = op1
        if accum_out is not None:
            kw["accum_out"] = accum_out
        w = (out,) if accum_out is None else (out, accum_out)
        return self.add(eng, lambda e: e.tensor_scalar(out, in0, s1, s2, op0, **kw),
                        reads=(in0, s1, s2), writes=w)

    def stt(self, out, in0, scalar, in1, op0, op1, accum_out=None):
        kw = {}
        if accum_out is not None:
            kw["accum_out"] = accum_out
        w = (out,) if accum_out is None else (out, accum_out)
        return self.add("dve", lambda e: e.scalar_tensor_tensor(out, in0, scalar, in1, op0, op1, **kw),
                        reads=(in0, scalar, in1), writes=w)

    def cp(self, out, in_, eng="dve"):
        if eng == "act":
            return self.add("act", lambda e: e.copy(out, in_), reads=(in_,), writes=(out,))
        return self.add(eng, lambda e: e.tensor_copy(out, in_), reads=(in_,), writes=(out,))

    def act(self, out, in_, func, bias=0.0, scale=1.0, accum_out=None):
        kw = {}
        if accum_out is not None:
            kw["accum_out"] = accum_out
        w = (out,) if accum_out is None else (out, accum_out)
        return self.add("act", lambda e: e.activation(out, in_, func, bias=bias, scale=scale, **kw),
                        reads=(in_, bias, scale), writes=w)

    def red(self, out, in_, op, axis=AX.X, eng="dve"):
        return self.add(eng, lambda e: e.tensor_reduce(out, in_, axis, op), reads=(in_,), writes=(out,))

    def recip(self, out, in_):
        return self.add("dve", lambda e: e.reciprocal(out, in_), reads=(in_,), writes=(out,))

    def rpow(self, out, in_, power, scale=1.0, bias=0.0):
        self.act(out, in_, AF.Ln, bias=bias, scale=scale)
        return self.act(out, out, AF.Exp, scale=power)

    def memset(self, ap, val, eng="dve"):
        return self.add(eng, lambda e: e.memset(ap, val), writes=(ap,))

    def scan(self, out, d0, d1, init, op0, op1):
        return self.add("dve", lambda e: e.tensor_tensor_scan(out, d0, d1, init, op0, op1),
                        reads=(d0, d1, init), writes=(out,))

    def dma(self, out, in_, eng="sp", is_output=False):
        x = self.add(eng, lambda e: e.dma_start(out=out, in_=in_), reads=(in_,), writes=(out,), isdma=True)
        if is_output:
            self.out_dmas.append(x)
        return x

    def finish(self):
        nc = self.nc
        fin = _Op("sp", None)
        fin.idx = len(self.ops["sp"])
        for y in self.out_dmas:
            self._dep(fin, y)
        self.ops["sp"].append(fin)
        sems = {}
        for e in ("pe", "dve", "act", "pool"):
            sems[e] = self.stack.enter_context(nc.semaphore("s_" + e))
        dsems = [self.stack.enter_context(nc.semaphore("d%d" % i)) for i in range(self.NDMA)]
        for e in ("pe", "dve", "act", "pool"):
            c = 0
            for x in self.ops[e]:
                if x.isdma:
                    continue
                if x.needed:
                    c += 1
                    x.sigval = c
            self.stats_sig = getattr(self, "stats_sig", {})
            self.stats_sig[e] = c
        ops = self.ops

        def replay(e, engobj):
            for x in ops[e]:
                for y in x.deps:
                    if y.isdma:
                        engobj.wait_ge(dsems[y.dsem], y.dval)
                    else:
                        engobj.wait_ge(sems[y.eng], y.sigval)
                if x.emit is None:
                    continue
                ins = x.emit(engobj)
                if x.isdma:
                    ins.then_inc(dsems[x.dsem], 16)
                elif x.needed:
                    ins.then_inc(sems[e], 1)

        with nc.Block() as block:
            @block.tensor
            def _(eng):
                replay("pe", eng)

            @block.vector
            def _(eng):
                replay("dve", eng)

            @block.scalar
            def _(eng):
                replay("act", eng)

            @block.gpsimd
            def _(eng):
                replay("pool", eng)

            @block.sync
            def _(eng):
                replay("sp", eng)


L = 2048
D = 1024
NT = L // 128
DEPTH = 2
N_IN = 3448
EPS = 1e-6

OFF = dict(gate=0, gdn_q=1024, gdn_k=1408, gdn_v=1792, gdn_a=2176, gdn_b=2188, mla_cq=2200, mla_ckv=2392,
           mla_kr=2520, rw_r=2552, rw_k=2808, rw_v=3064, rw_wd=3320, rw_ad=3384)


def chunk_table():
    ch = []
    for h in range(3):
        ch.append(("gq%d" % h, [(0, OFF["gdn_q"] + h * 128, 128)]))
        ch.append(("gk%d" % h, [(0, OFF["gdn_k"] + h * 128, 128)]))
        ch.append(("gv%d" % h, [(0, OFF["gdn_v"] + h * 128, 128)]))
    ch.append(("gab", [(0, OFF["gdn_a"], 6), (32, OFF["gdn_a"] + 6, 6), (64, OFF["gdn_b"], 6), (96, OFF["gdn_b"] + 6, 6)]))
    ch.append(("cq0", [(0, OFF["mla_cq"], 128)]))
    ch.append(("cq1", [(0, OFF["mla_cq"] + 128, 64)]))
    ch.append(("ckv", [(0, OFF["mla_ckv"], 128)]))
    ch.append(("kr", [(0, OFF["mla_kr"], 32)]))
    for i in range(2):
        ch.append(("rr%d" % i, [(0, OFF["rw_r"] + i * 128, 128)]))
        ch.append(("rk%d" % i, [(0, OFF["rw_k"] + i * 128, 128)]))
        ch.append(("rv%d" % i, [(0, OFF["rw_v"] + i * 128, 128)]))
    ch.append(("rwd", [(0, OFF["rw_wd"], 64)]))
    ch.append(("rad", [(0, OFF["rw_ad"], 64)]))
    for i in range(8):
        ch.append(("g%d" % i, [(0, OFF["gate"] + i * 128, 128)]))
    return ch


CHUNKS = chunk_table()
CH_IDX = {n: i for i, (n, _) in enumerate(CHUNKS)}
NCH = len(CHUNKS)


def host_win(w_in):
    out = np.zeros((DEPTH, NCH, 128, 8, 128), np.float32)
    for ci, (_, parts) in enumerate(CHUNKS):
        for dst, src, w in parts:
            blk = w_in[:, :, src:src + w].reshape(DEPTH, 8, 128, w)
            out[:, ci, :, :, dst:dst + w] = blk.transpose(0, 2, 1, 3)
    return out


class K:
    pass


def build(depth=DEPTH, mixers=("gdn", "mla", "rwkv"), dbg=False):
    nc = bass.Bass("TRN2", target_bir_lowering=False)
    k = K()
    k.nc = nc
    k.dbg = dbg
    k.dbg_outs = []
    k.cut = 99

    def din(name, shape, dt=F32):
        return nc.dram_tensor(name, list(shape), dt, kind="ExternalInput").ap()

    k.x_d = din("x", [L, D])
    k.win_d = din("win", [DEPTH, NCH, 128, 8, 128])
    k.normg_d = din("normg", [DEPTH, 128, 8])
    k.wout_d = din("wout", [DEPTH, 128, 8, 1024])
    k.fing_d = din("fing", [1, D])
    k.ident_d = din("ident", [128, 128])
    k.out_d = nc.dram_tensor("out", [L, D], F32, kind="ExternalOutput").ap()
    mla_decl(k)
    gdn_decl(k)
    rwkv_decl(k)

    with ExitStack() as st:
        P = Prog(nc, st)
        k.P = P
        k.xscr = nc.dram_tensor("xscr", [L, D], F32, kind="Internal").ap()
        P.dram_track("xscr", L * D * 4, 128 * D * 4)
        k.hT = P.sb("hT", [128, 8, L], BF16, blk=512)
        k.ident = P.sb("ident", [128, 128], F32)
        k.identb = P.sb("identb", [128, 128], BF16)
        k.normg = P.sb("normg", [128, DEPTH, 8], F32)
        k.wst = [P.sb("wst%d" % i, [128, 8, 128], F32) for i in range(2)]
        k.wbf = [P.sb("wbf%d" % i, [128, 8, 128], BF16) for i in range(2)]
        k.wrr = 0
        k.PAW = [P.ps("paw%d" % i, [128, 1024], F32, blk=512) for i in range(2)]
        k.PA = [k.PAW[i // 2][:, (i % 2) * 512:(i % 2 + 1) * 512] for i in range(4)]
        k.PB = [P.ps("pb%d" % i, [128, 512], F32) for i in range(4)]

        P.dma(k.ident[:], k.ident_d[:])
        P.cp(k.identb[:], k.ident[:])
        for l in range(DEPTH):
            P.dma(k.normg[:, l, :], k.normg_d[l])

        for l in range(depth):
            phase_a(k, l)
            with scope(k):
                k.mix_r = P.sb("mix_r", [128, 2, L], BF16, blk=512)
                if "rwkv" in mixers:
                    rwkv_phase(k, l)
                else:
                    P.memset(k.mix_r[:].rearrange("p a b -> p (a b)"), 1.0)
                with scope(k):
                    k.mix_m = P.sb("mix_m", [128, 3, L], BF16, blk=512)
                    if "mla" in mixers:
                        mla_phase(k, l)
                    else:
                        P.memset(k.mix_m[:].rearrange("p a b -> p (a b)"), 1.0)
                    with scope(k):
                        k.mix_g = P.sb("mix_g", [128, 3, L], BF16, blk=512)
                        if "gdn" in mixers:
                            gdn_phase(k, l)
                        else:
                            P.memset(k.mix_g[:].rearrange("p a b -> p (a b)"), 1.0)
                        if k.dbg:
                            for nm, t_, n_ in (("g", k.mix_g, 3), ("m", k.mix_m, 3), ("r", k.mix_r, 2)):
                                dump(k, "mix_%s%d" % (nm, l), t_[:].rearrange("p a b -> p (a b)"), [128, n_ * L])
                        phase_z(k, l, last=(l == depth - 1))
        P.finish()
        print("ops:", {e: len(v) for e, v in P.ops.items()}, "sig:", P.stats_sig, "dma:", P.ndma_ops)
    return k


def mixc(k, c):
    if c < 3:
        return k.mix_g[:, c, :]
    if c < 6:
        return k.mix_m[:, c - 3, :]
    return k.mix_r[:, c - 6, :]


def dump(k, name, ap, shape=None):
    if not k.dbg:
        return
    P = k.P
    shape = list(ap.shape) if shape is None else shape
    d = k.nc.dram_tensor("dbg_" + name, shape, ap.dtype, kind="ExternalOutput").ap()
    P.dma(d[:] if len(shape) == 2 else d, ap, is_output=True)
    k.dbg_outs.append("dbg_" + name)


def scope(k):
    class _S:
        def __enter__(s):
            s.old = k.P.stack
            s.st = ExitStack()
            s.st.__enter__()
            k.P.stack = s.st
            return s

        def __exit__(s, *a):
            k.P.barrier()
            k.P.stack = s.old
            s.st.__exit__(*a)
            return False
    return _S()


def phase_a(k, l):
    P = k.P
    with scope(k):
        ssq = P.sb("a_ssq", [128, NT])
        rs = P.sb("a_rs", [128, NT])
        rstd = P.sb("a_rstd", [128, NT])
        junk = [P.sb("a_junk%d" % i, [128, D], BF16) for i in range(2)]
        xs = [P.sb("a_xs%d" % i, [128, D], BF16) for i in range(2)]
        xin = [P.sb("a_xin%d" % i, [128, D], F32) for i in range(3)]
        src = k.x_d if l == 0 else k.xscr

        def stage1(tt):
            b = tt % 2
            xt_ = xin[tt % 3]
            P.dma(xt_[:], src[tt * 128:(tt + 1) * 128, :])
            P.act(junk[b][:], xt_[:], AF.Square, accum_out=ssq[:, tt:tt + 1])
            P.act(rs[:, tt:tt + 1], ssq[:, tt:tt + 1], AF.Sqrt, bias=EPS, scale=1.0 / D)
            P.recip(rstd[:, tt:tt + 1], rs[:, tt:tt + 1])
            P.ts(xs[b][:], xt_[:], rstd[:, tt:tt + 1], ALU.mult)

        def stage2(tt):
            b = tt % 2
            pt = k.PB[b][:].bitcast(BF16)
            for dc in range(8):
                P.tr(pt[:, dc * 128:(dc + 1) * 128], xs[b][:, dc * 128:(dc + 1) * 128], k.identb[:])
            P.cp(k.hT[:, :, tt * 128:(tt + 1) * 128], pt[:].rearrange("p (a b) -> p a b", a=8),
                 eng=("act" if tt % 2 == 0 else "dve"))
        stage1(0)
        for tt in range(NT):
            if tt + 1 < NT:
                stage1(tt + 1)
            stage2(tt)


def proj(k, l, name, alt=False):
    P = k.P
    BK = k.PB if alt else k.PA
    ci = CH_IDX[name]
    b = k.wrr
    k.wrr ^= 1
    P.dma(k.wst[b][:], k.win_d[l, ci], eng="sp")
    gb = k.normg[:, l, :].unsqueeze(2).broadcast_to([128, 8, 128])
    P.tt(k.wbf[b][:], k.wst[b][:], gb, ALU.mult, eng="pool")
    for tb in range(4):
        for dc in range(8):
            P.mm(BK[tb][:, :], k.wbf[b][:, dc, :], k.hT[:, dc, tb * 512:(tb + 1) * 512], start=(dc == 0), stop=(dc == 7))
    return BK


ZQ = "act"


def phase_z(k, l, last):
    P = k.P
    with scope(k):
        wst = P.sb("z_wst", [128, 8, 512], F32)
        wob = P.sb("z_wob", [128, 8, 1024], BF16, blk=512)
        sg = [P.sb("z_sg%d" % i, [128, L], BF16, blk=512) for i in range(2)]
        for nb in range(2):
            P.dma(wst[:], k.wout_d[l, :, :, nb * 512:(nb + 1) * 512])
            P.cp(wob[:, :, nb * 512:(nb + 1) * 512], wst[:], eng="act")
        for gc in range(8):
            pa = proj(k, l, "g%d" % gc, alt=(gc % 2 == 1))
            s = sg[gc % 2]
            for tb in range(4):
                P.act(s[:, tb * 512:(tb + 1) * 512], pa[tb][:, :], AF.Silu)
                mc = mixc(k, gc)[:, tb * 512:(tb + 1) * 512]
                P.tt(mc, mc, s[:, tb * 512:(tb + 1) * 512], ALU.mult)
        xin = [P.sb("z_xin%d" % i, [128, D], F32) for i in range(3)]
        src = k.x_d if l == 0 else k.xscr
        if last:
            ssq = P.sb("f_ssq", [128, NT])
            rs = P.sb("f_rs", [128, NT])
            rstd = P.sb("f_rstd", [128, NT])
            junk = [P.sb("f_junk%d" % i, [128, D], BF16) for i in range(2)]
            gf = P.sb("f_g", [128, D])
            ot = [P.sb("f_o%d" % i, [128, D]) for i in range(2)]
            P.dma(gf[:], k.fing_d[0:1, :].partition_broadcast(128))
        def za(tt):
            xt_ = xin[tt % 3]
            P.dma(xt_[:], src[tt * 128:(tt + 1) * 128, :])
            for nb in range(2):
                ps = k.PB[(tt * 2 + nb) % 4]
                for kc in range(8):
                    P.mm(ps[:, :], mixc(k, kc)[:, tt * 128:(tt + 1) * 128], wob[:, kc, nb * 512:(nb + 1) * 512],
                         start=(kc == 0), stop=(kc == 7))
                xs = xt_[:, nb * 512:(nb + 1) * 512]
                P.tt(xs, xs, ps[:, :], ALU.add)
            if not last:
                P.dma(k.xscr[tt * 128:(tt + 1) * 128, :], xt_[:], eng=ZQ)
            else:
                b = tt % 2
                P.act(junk[b][:], xt_[:], AF.Square, accum_out=ssq[:, tt:tt + 1])
                P.act(rs[:, tt:tt + 1], ssq[:, tt:tt + 1], AF.Sqrt, bias=EPS, scale=1.0 / D)

        def zb(tt):
            if last:
                xt_ = xin[tt % 3]
                b = tt % 2
                P.recip(rstd[:, tt:tt + 1], rs[:, tt:tt + 1])
                P.stt(ot[b][:], xt_[:], rstd[:, tt:tt + 1], gf[:], ALU.mult, ALU.mult)
                P.dma(k.out_d[tt * 128:(tt + 1) * 128, :], ot[b][:], eng=ZQ, is_output=True)
        za(0)
        for tt in range(NT):
            if tt + 1 < NT:
                za(tt + 1)
            zb(tt)


def host_inputs(inp, b):
    m = {}
    m["x"] = np.ascontiguousarray(inp["x"][b])
    m["win"] = host_win(inp["w_in"])
    m["normg"] = np.ascontiguousarray(inp["norm_g"].reshape(DEPTH, 8, 128).transpose(0, 2, 1))
    m["wout"] = np.ascontiguousarray(inp["w_out"].reshape(DEPTH, 8, 128, 1024).transpose(0, 2, 1, 3))
    m["fing"] = np.ascontiguousarray(inp["final_norm_g"].reshape(1, D))
    m["ident"] = np.eye(128, dtype=np.float32)
    host_mla(inp, b, m)
    host_gdn(inp, b, m)
    host_rwkv(inp, b, m)
    return m


TWO_PI = 2.0 * np.pi


def host_mla(inp, b, m):
    half = 16
    inv_freq = (10000.0 ** (-np.arange(half, dtype=np.float32) / half)).astype(np.float32)
    invf = np.zeros((32, 1), np.float32)
    invf[:, 0] = np.tile(inv_freq, 2) / np.float32(TWO_PI)
    m["invf"] = invf
    rm = np.zeros((32, 32), np.float32)
    for i in range(16):
        rm[i, i + 16] = -1.0
        rm[i + 16, i] = 1.0
    m["rmT"] = np.ascontiguousarray(rm.T)
    m["pos"] = np.ascontiguousarray(inp["positions"][b].reshape(1, L).astype(np.int32))
    wuq = inp["mla_w_uq"]
    o = np.zeros((DEPTH, 128, 2, 6, 128), np.float32)
    for h in range(6):
        nope = wuq[:, :, h * 96:h * 96 + 64]
        rope = wuq[:, :, h * 96 + 64:h * 96 + 96]
        o[:, :, 0, h, 64:128] = nope[:, 0:128]
        o[:, 0:64, 1, h, 64:128] = nope[:, 128:192]
        o[:, :, 0, h, 0:32] = rope[:, 0:128]
        o[:, 0:64, 1, h, 0:32] = rope[:, 128:192]
    m["wuq"] = o
    gq = np.zeros((DEPTH, 128, 2), np.float32)
    gq[:, :, 0] = inp["mla_q_norm_g"][:, 0:128]
    gq[:, 0:64, 1] = inp["mla_q_norm_g"][:, 128:192]
    m["gq"] = gq
    wukv = inp["mla_w_ukv"]
    wk = np.zeros((DEPTH, 128, 6, 128), np.float32)
    wv = np.zeros((DEPTH, 128, 6, 64), np.float32)
    for h in range(6):
        wk[:, :, h, 64:128] = wukv[:, :, h * 128:h * 128 + 64]
        wv[:, :, h, :] = wukv[:, :, h * 128 + 64:h * 128 + 128]
    m["wuk"] = wk
    m["wuv"] = wv
    m["gkv"] = np.ascontiguousarray(inp["mla_kv_norm_g"].reshape(DEPTH, 128, 1))


def mla_decl(k):
    nc = k.nc

    def din(name, shape, dt=F32):
        return nc.dram_tensor(name, list(shape), dt, kind="ExternalInput").ap()
    k.invf_d = din("invf", [32, 1])
    k.rmT_d = din("rmT", [32, 32])
    k.pos_d = din("pos", [1, L], I32)
    k.wuq_d = din("wuq", [DEPTH, 128, 2, 6, 128])
    k.gq_d = din("gq", [DEPTH, 128, 2])
    k.wuk_d = din("wuk", [DEPTH, 128, 6, 128])
    k.wuv_d = din("wuv", [DEPTH, 128, 6, 64])
    k.gkv_d = din("gkv", [DEPTH, 128, 1])


def latent_norm(k, l, names, nfeat, outs, ones):
    P = k.P
    sq = [P.sb("ln_sq%d" % i, [128, 512]) for i in range(2)]
    rq = [P.sb("ln_rq%d" % i, [128, 512]) for i in range(2)]
    n = len(names)
    for i, nm in enumerate(names):
        pa = proj(k, l, nm)
        for tb in range(4):
            s = sq[tb % 2]
            sk = ""
            if "a" not in sk:
                P.act(s[:], pa[tb][:, :], AF.Square)
            if "c" not in sk:
                P.cp(outs[i][:, tb * 512:(tb + 1) * 512], pa[tb][:, :], eng="dve")
            if "m" not in sk:
                P.mm(k.PB[tb][:, :], ones[:], s[:], start=(i == 0), stop=(i == n - 1))
    c2 = 9
    if c2 < 1:
        return
    for tb in range(4):
        r = rq[tb % 2]
        P.rpow(r[:], k.PB[tb][:, :], -0.5, scale=1.0 / nfeat, bias=EPS)
        for i in range(n):
            o = outs[i][:, tb * 512:(tb + 1) * 512]
            P.tt(o, o, r[:], ALU.mult)


def mla_phase(k, l):
    P = k.P
    SC = 96.0 ** -0.5
    with scope(k):
        cqn0 = P.sb("m_cqn0", [128, L], BF16, blk=512)
        cqn1 = P.sb("m_cqn1", [128, L], BF16, blk=512)
        ckvn = P.sb("m_ckvn", [128, L], BF16, blk=512)
        krope = P.sb("m_krope", [32, L], BF16, blk=512)
        cos2 = P.sb("m_cos2", [32, L], BF16, blk=512)
        sin2 = P.sb("m_sin2", [32, L], BF16, blk=512)
        wq = P.sb("m_wq", [128, 2, 6, 128], BF16)
        wk = P.sb("m_wk", [128, 6, 128], BF16)
        wv = P.sb("m_wv", [128, 6, 64], BF16)
        ones = P.sb("m_ones", [128, 128], F32)
        onesk = P.sb("m_onesk", [128, 128], F32)
        rmT = P.sb("m_rmT", [32, 32], F32)
        P.memset(ones[:], 1.0)
        P.memset(onesk[:], 1.0)
        P.memset(onesk[32:64, :], 0.0)
        P.dma(rmT[:], k.rmT_d[:])
        with scope(k):
            st = P.sb("m_st", [128, 2, 6, 128], F32)
            g = P.sb("m_g", [128, 4], F32)
            P.dma(st[:], k.wuq_d[l])
            P.dma(g[:, 0:2], k.gq_d[l])
            P.dma(g[:, 2:3], k.gkv_d[l])
            for kc in range(2):
                P.ts(wq[:, kc].rearrange("p a b -> p (a b)"), st[:, kc].rearrange("p a b -> p (a b)"),
                     g[:, kc:kc + 1], ALU.mult)
            st2 = P.sb("m_st2", [128, 6, 128], F32)
            P.dma(st2[:], k.wuk_d[l])
            P.ts(wk[:].rearrange("p a b -> p (a b)"), st2[:].rearrange("p a b -> p (a b)"), g[:, 2:3], ALU.mult)
            st3 = P.sb("m_st3", [128, 6, 64], F32)
            P.dma(st3[:], k.wuv_d[l])
            P.ts(wv[:].rearrange("p a b -> p (a b)"), st3[:].rearrange("p a b -> p (a b)"), g[:, 2:3], ALU.mult)
        if k.cut < 1:
            return
        with scope(k):
            latent_norm(k, l, ["cq0", "cq1"], 192, [cqn0, cqn1], ones)
            latent_norm(k, l, ["ckv"], 128, [ckvn], ones)
        if k.cut < 2:
            return
        with scope(k):
            invf = P.sb("m_invf", [32, 1], F32)
            P.dma(invf[:], k.invf_d[:])
            pa = proj(k, l, "kr")

            def rope_blk(tb):
                sl = slice(tb * 512, (tb + 1) * 512)
                posi = P.sb("m_posi%d" % tb, [32, 512], I32)
                y = P.sb("m_y%d" % tb, [32, 512], F32)
                yi = P.sb("m_yi%d" % tb, [32, 512], I32)
                fr = P.sb("m_fr%d" % tb, [32, 512], F32)
                kr = P.sb("m_kr%d" % tb, [32, 512], F32)
                t1 = P.sb("m_t1%d" % tb, [32, 512], F32)
                t2 = P.sb("m_t2%d" % tb, [32, 512], F32)
                P.dma(posi[:], k.pos_d[0:1, sl].partition_broadcast(32))
                P.cp(kr[:], pa[tb][0:32, :], eng="act")
                yield
                P.cp(y[:], posi[:])
                P.mm(k.PB[tb][0:32, :], rmT[:], kr[:])
                yield
                P.ts(y[:], y[:], invf[:, 0:1], ALU.mult)
                yield
                for off, dst in ((0.0, sin2), (0.25, cos2)):
                    if off != 0.0:
                        P.ts(y[:], y[:], off, ALU.add)
                        yield
                    P.cp(yi[:], y[:])
                    yield
                    P.cp(fr[:], yi[:])
                    yield
                    P.tt(fr[:], y[:], fr[:], ALU.subtract)
                    yield
                    P.act(dst[:, sl], fr[:], AF.Sin, scale=TWO_PI * (1.0 - 1e-6))
                    yield
                P.tt(t1[:], kr[:], cos2[:, sl], ALU.mult)
                P.tt(t2[:], k.PB[tb][0:32, :], sin2[:, sl], ALU.mult)
                yield
                P.tt(krope[:, sl], t1[:], t2[:], ALU.add)
                yield
            run_interleaved([rope_blk(tb) for tb in range(4)])
        if k.cut < 3:
            return
        kT = [P.sb("m_kT%d" % i, [128, L], BF16, blk=512) for i in range(2)]
        qT = [P.sb("m_qT%d" % i, [128, L], BF16, blk=512) for i in range(2)]
        Vh = [P.sb("m_V%d" % i, [128, NT, 96], BF16) for i in range(2)]
        pT = [P.sb("m_pT%d" % i, [128, 1024], BF16, blk=512) for i in range(2)]
        sq = [P.sb("m_sq%d" % i, [128, 512], F32) for i in range(2)]
        qr = [P.sb("m_qr%d" % i, [32, 512], F32) for i in range(2)]
        t1 = P.sb("m_t1b", [32, 512], F32)
        t2 = P.sb("m_t2b", [32, 512], F32)
        mrow = P.sb("m_mrow", [64, 512], F32)
        km4 = P.sb("m_km4", [128, 4], F32)
        kmax2 = P.sb("m_kmax2", [128, 1], F32)
        rden = [P.sb("m_rden%d" % i, [64, 512], F32) for i in range(2)]
        for i in range(2):
            P.memset(kT[i][32:64, :], 0.0)
            P.memset(kT[i][32:33, :], 1.0)
            P.memset(qT[i][32:64, :], 0.0)
            P.memset(Vh[i][:, :, 64:96], 1.0)
        kmx = [P.sb("m_kmx%d" % i, [128, 1], F32) for i in range(2)]

        def prep(h):
            kt_, qt_, vh_ = kT[h % 2], qT[h % 2], Vh[h % 2]
            kmax2_ = kmx[h % 2]
            P.cp(kt_[0:32, :], krope[:], eng="pool")
            for tb in range(4):
                sl = slice(tb * 512, (tb + 1) * 512)
                s = sq[tb % 2]
                P.mm(k.PB[2][:, :], wk[:, h, :], ckvn[:, sl])
                yield
                P.cp(kt_[64:128, sl], k.PB[2][64:128, :], eng="dve")
                yield
                P.tt(s[:], kt_[:, sl], kt_[:, sl], ALU.mult, eng="pool")
                yield
                yield
                P.mm(k.PB[3][:, :], onesk[:], s[:])
                yield
                P.red(km4[:, tb:tb + 1], k.PB[3][:, :], ALU.max)
                yield
            P.red(kmax2_[:], km4[:], ALU.max)
            for half in range(2):
                for j in range(8):
                    tt = half * 8 + j
                    P.mm(k.PB[2][:, j * 64:(j + 1) * 64], ckvn[:, tt * 128:(tt + 1) * 128], wv[:, h, :])
                yield
                P.cp(vh_[:, half * 8:(half + 1) * 8, 0:64], k.PB[2][:, :].rearrange("p (a b) -> p a b", a=8), eng="dve")
                yield
            for tb in range(4):
                sl = slice(tb * 512, (tb + 1) * 512)
                s = sq[tb % 2]
                q_ = qr[tb % 2]
                P.mm(k.PB[2][:, :], wq[:, 0, h, :], cqn0[:, sl], start=True, stop=False)
                P.mm(k.PB[2][:, :], wq[:, 1, h, :], cqn1[:, sl], start=False, stop=True)
                yield
                P.cp(qt_[64:128, sl], k.PB[2][64:128, :], eng="dve")
                P.cp(q_[:], k.PB[2][0:32, :], eng="dve")
                yield
                P.act(s[:], k.PB[2][:, :], AF.Square)
                yield
                P.mm(k.PB[3][:, :], ones[:], s[:])
                yield
                P.act(mrow[32:33, :], k.PB[3][32:33, :], AF.Sqrt, scale=kmax2_[32:33, 0:1])
                yield
                P.ts(qt_[32:33, sl], mrow[32:33, :], -1.0, ALU.mult)
                P.mm(k.PB[2][0:32, :], rmT[:], q_[:])
                P.tt(t1[:], q_[:], cos2[:, sl], ALU.mult, eng="pool")
                yield
                P.tt(t2[:], k.PB[2][0:32, :], sin2[:, sl], ALU.mult)
                yield
                P.tt(qt_[0:32, sl], t1[:], t2[:], ALU.add)
                yield

        def attn(h):
            kt_, qt_, vh_ = kT[h % 2], qT[h % 2], Vh[h % 2]
            pti = 0
            for qb in range(4):
                qs = slice(qb * 512, (qb + 1) * 512)
                O = k.PB[qb % 2]

                def s_pair(m_):
                    for j in range(2):
                        kt = 2 * m_ + j
                        P.mm(k.PAW[m_ % 2][:, j * 512:(j + 1) * 512], kt_[:, kt * 128:(kt + 1) * 128], qt_[:, qs])
                s_pair(0)
                s_pair(1)
                for m_ in range(NT // 2):
                    p_ = pT[pti % 2]
                    pti += 1
                    P.act(p_[:], k.PAW[m_ % 2][:, :], AF.Exp, scale=SC)
                    if m_ + 2 < NT // 2:
                        s_pair(m_ + 2)
                    for j in range(2):
                        kt = 2 * m_ + j
                        P.mm(O[0:96, :], vh_[:, kt, :], p_[:, j * 512:(j + 1) * 512], start=(kt == 0), stop=(kt == NT - 1))
                    yield
                rd = rden[qb % 2]
                P.rpow(rd[0:32, :], O[64:96, :], -1.0)
                P.rpow(rd[32:64, :], O[64:96, :], -1.0)
                ob = (h % 2) * 64
                P.tt(mixc(k, 3 + h // 2)[ob:ob + 64, qs], O[0:64, :], rd[:], ALU.mult)
                yield

        for _ in prep(0):
            pass
        mode = "il"
        for h in range(6):
            gens = [attn(h)]
            if h + 1 < 6:
                if mode == "il":
                    gens.append(prep(h + 1))
                elif mode == "seq":
                    run_interleaved(gens)
                    gens = [prep(h + 1)]
                elif mode == "noprep":
                    pass
            if mode == "noprep" and h > 0:
                gens = [attn(0)]
            run_interleaved(gens)


NCK = L // 64
NEG = -30000.0


def host_gdn(inp, b, m):
    cw = inp["gdn_conv"]
    o = np.zeros((DEPTH, 128, 9, 5), np.float32)
    for part in range(3):
        for p in range(3):
            o[:, :, part * 3 + p, :] = cw[:, :, part * 384 + p * 128: part * 384 + (p + 1) * 128].transpose(0, 2, 1)
    m["gconv"] = o
    gb = np.zeros((DEPTH, 128, 2), np.float32)
    for d in range(2):
        gb[:, d * 32:d * 32 + 6, 0] = inp["gdn_dt_bias"][:, d, :]
        gb[:, d * 32:d * 32 + 6, 1] = inp["gdn_a_log"][:, d, :]
    m["ggb"] = gb
    m["gng"] = np.ascontiguousarray(np.tile(inp["gdn_norm_g"], (1, 2)).reshape(DEPTH, 128, 1))
    sel = np.zeros((64, 6, 128), np.float32)
    for d in range(2):
        for p in range(3):
            sel[d * 32 + 2 * p, d * 3 + p, 0:64] = 1.0
            sel[d * 32 + 2 * p + 1, d * 3 + p, 64:128] = 1.0
    m["gsel"] = sel
    j = np.arange(64)[:, None]
    i = np.arange(64)[None, :]
    nm = np.zeros((128, 2, 64), np.float32)
    nm[:, 0, :] = np.tile(np.where(i > j, 0.0, NEG), (2, 1))
    nm[:, 1, :] = np.tile(np.where(i < j, 0.0, NEG), (2, 1))
    m["gnegm"] = nm
    m["gid2"] = np.ascontiguousarray(np.tile(np.eye(64, dtype=np.float32), (2, 1)))


def gdn_decl(k):
    nc = k.nc

    def din(name, shape, dt=F32):
        return nc.dram_tensor(name, list(shape), dt, kind="ExternalInput").ap()
    k.gconv_d = din("gconv", [DEPTH, 128, 9, 5])
    k.ggb_d = din("ggb", [DEPTH, 128, 2])
    k.gng_d = din("gng", [DEPTH, 128, 1])
    k.gsel_d = din("gsel", [64, 6, 128])
    k.gnegm_d = din("gnegm", [128, 2, 64])
    k.gid2_d = din("gid2", [128, 64])


def bc3(ap2, n):
    return ap2.unsqueeze(2).broadcast_to([ap2.shape[0], ap2.shape[1], n])


def bcm(ap2, n):
    return ap2.unsqueeze(1).broadcast_to([ap2.shape[0], n, ap2.shape[1]])


HS = (slice(0, 64), slice(64, 128))


def v3(ps, n=8):
    return ps[:, 0:n * 64].rearrange("p (a b) -> p a b", a=n)


def mm2(P, ps, c, lhsT, rhs, **kw):
    for hs in HS:
        P.mm(ps[hs, c * 64:(c + 1) * 64], lhsT[hs], rhs[hs], **kw)


def tr2(P, ps, c, in_, ident):
    for hs in HS:
        P.mm(ps[hs, c * 64:(c + 1) * 64], in_[hs], ident[hs, hs])


def neumann2(k, Nn, Rm, tmp, bank, id2, n8=8):
    P = k.P
    idb = bcm(id2[:, :], n8)
    tA, tB, tC, tD = tmp
    pa_, pb_, pc_ = bank
    for c in range(n8):
        tr2(P, pa_, c, Nn[:, c, :], k.identb)
    P.cp(tA[:], v3(pa_, n8), eng="act")
    P.tt(Rm[:], Nn[:], idb, ALU.add)
    yield
    cur, curT = Nn, tA
    targets = [(tB, tC), (tD, tA)]
    for lvl in range(1, 7):
        nxt, nxtT = targets[(lvl - 1) % 2]
        for c in range(n8):
            if lvl <= 5:
                mm2(P, pb_, c, cur[:, c, :], curT[:, c, :])
                if lvl < 5:
                    mm2(P, pa_, c, curT[:, c, :], cur[:, c, :])
            if lvl >= 2:
                mm2(P, pc_, c, curT[:, c, :], Rm[:, c, :])
        if lvl <= 5:
            P.cp(nxtT[:], v3(pb_, n8), eng="act")
            if lvl < 5:
                P.cp(nxt[:], v3(pa_, n8), eng="act")
        if lvl >= 2:
            P.tt(Rm[:], Rm[:], v3(pc_, n8), ALU.add)
        yield
        cur, curT = nxt, nxtT


def run_interleaved(gens):
    gens = list(gens)
    while gens:
        for g in list(gens):
            try:
                next(g)
            except StopIteration:
                gens.remove(g)


def norm_pipe(k, n, src_fn, sqb, rnb, banks, bones, power, scale, bias, post_fn):
    P = k.P

    def pre(i):
        P.act(sqb[i % 2][:], src_fn(i), AF.Square)
        P.mm(banks[i % 2][:, :], bones[:], sqb[i % 2][:])

    def post(i):
        P.rpow(rnb[i % 2][:], banks[i % 2][:, :], power, scale=scale, bias=bias)
        post_fn(i, rnb[i % 2])
    pre(0)
    for i in range(n):
        if i + 1 < n:
            pre(i + 1)
        post(i)


def run_pipelined(chains, depth=2):
    active = []
    nxt = [0] * len(chains)

    def start(ci):
        if nxt[ci] < len(chains[ci]):
            active.append((ci, chains[ci][nxt[ci]](nxt[ci] % depth)))
            nxt[ci] += 1
    for ci in range(len(chains)):
        for _ in range(depth):
            start(ci)
    while active:
        for item in list(active):
            try:
                next(item[1])
            except StopIteration:
                active.remove(item)
                start(item[0])


def gdn_phase(k, l):
    P = k.P
    with scope(k):
        GC = P.sb("g_GC", [64, L], F32, blk=512)
        GP = [P.sb("g_GP%d" % p, [128, NCK, 4], F32) for p in range(3)]
        NBP = [P.sb("g_NBP%d" % p, [128, NCK, 2], F32) for p in range(3)]
        sel = P.sb("g_sel", [64, 6, 128], F32)
        negm = P.sb("g_negm", [128, 2, 64], F32)
        id2 = P.sb("g_id2", [128, 64], F32)
        cw = P.sb("g_cw", [128, 9, 5], F32)
        ng = P.sb("g_ng", [128, 1], F32)
        bones = P.sb("g_bones", [128, 128], F32)
        P.dma(sel[:], k.gsel_d[:])
        P.dma(negm[:], k.gnegm_d[:])
        P.dma(id2[:], k.gid2_d[:])
        P.dma(cw[:], k.gconv_d[l])
        P.dma(ng[:], k.gng_d[l])
        P.memset(bones[:], 0.0)
        P.memset(bones[0:64, 0:64], 1.0)
        P.memset(bones[64:128, 64:128], 1.0)
        with scope(k):
            GT = P.sb("g_GT", [128, L], F32, blk=512)
            m0 = P.sb("g_m0", [64, L], F32)
            gb = P.sb("g_gb", [128, 2], F32)
            negA = P.sb("g_negA", [128, 1], F32)
            P.dma(gb[:], k.ggb_d[l])
            P.act(negA[:], gb[:, 1:2], AF.Exp)
            P.ts(negA[:], negA[:], -1.0, ALU.mult)
            P.memset(m0[:], 1.0)
            P.memset(m0[:, 0:L:64], 0.0)
            pa = proj(k, l, "gab")
            for tb in range(4):
                sl = slice(tb * 512, (tb + 1) * 512)
                P.act(GT[0:64, sl], pa[tb][0:64, :], AF.Exp, bias=gb[0:64, 0:1])
                P.act(GT[64:128, sl], pa[tb][64:128, :], AF.Sigmoid)
            P.act(GT[0:64, :], GT[0:64, :], AF.Ln, bias=1.0)
            P.ts(GT[0:64, :], GT[0:64, :], negA[0:64, 0:1], ALU.mult)
            P.scan(GC[:, :], m0[:, :], GT[0:64, :], 0.0, ALU.mult, ALU.add)
            gc3 = GC[32:64, :].rearrange("p (a b) -> p a b", b=64)
            P.tt(m0[32:64, :].rearrange("p (a b) -> p a b", b=64), bc3(GC[32:64, 63:L:64], 64), gc3, ALU.subtract)
            P.tt(GC[32:64, :], m0[32:64, :], GT[32:64, :], ALU.add)
            for grp in range(4):
                g8 = slice(grp * 8, (grp + 1) * 8)
                for c in range(8):
                    ck = grp * 8 + c
                    cs = slice(c * 64, (c + 1) * 64)
                    for hs in HS:
                        P.mm(k.PB[2 * (grp % 2)][hs, cs], GC[:, ck * 64:(ck + 1) * 64], k.ident[0:64, 0:64])
                        P.mm(k.PB[2 * (grp % 2) + 1][hs, cs], GT[64:128, ck * 64:(ck + 1) * 64], k.ident[64:128, 64:128])
                n_ = 0
                for p in range(3):
                    for hf, hs in enumerate(HS):
                        h = 2 * p + hf
                        for q, ps in ((0, k.PB[2 * (grp % 2)]), (1, k.PB[2 * (grp % 2) + 1])):
                            src = v3(ps)[hs, :, h:h + 33:32]
                            P.cp(GP[p][hs, g8, 2 * q:2 * q + 2], src, eng=("act" if q else "dve"))
            for p in range(3):
                P.ts(NBP[p][:], GP[p][:, :, 2:4], -1.0, ALU.mult)
        for p in range(3):
            with scope(k):
                Q = P.sb("g_Q", [128, L], BF16, blk=512)
                K_ = P.sb("g_K", [128, L], BF16, blk=512)
                Kt = P.sb("g_Kt", [128, NCK, 64], BF16, blk=512)
                Vt = P.sb("g_Vt", [128, NCK, 64], BF16, blk=512)
                O = P.sb("g_O", [128, L], F32, blk=512)
                P.memset(O[:], 0.0, eng="pool")
                with scope(k):
                    xp = P.sb("g_xp", [128, L + 4], BF16)
                    Dg = P.sb("g_Dg", [128, 5, 128], BF16)
                    Vf = P.sb("g_Vf", [128, L], BF16, blk=512)
                    cv = P.sb("g_cv", [128, L], F32, blk=512)
                    sq = P.sb("g_sq", [128, 512], F32)
                    rn = P.sb("g_rn", [128, 512], F32)
                    sq2 = P.sb("g_sq2", [128, 512], F32)
                    rn2 = P.sb("g_rn2", [128, 512], F32)
                    P.memset(xp[:, 0:2], 0.0)
                    P.memset(xp[:, L + 2:L + 4], 0.0)
                    for part, nm, dst in ((0, "gq", Q), (1, "gk", K_), (2, "gv", Vf)):
                        pa = proj(k, l, "%s%d" % (nm, p))
                        for tb in range(4):
                            P.cp(xp[:, 2 + tb * 512:2 + (tb + 1) * 512], pa[tb][:, :], eng=("act" if tb % 2 else "dve"))
                        wi = part * 3 + p
                        for j in range(5):
                            P.ts(Dg[:, j, :], k.identb[:], cw[:, wi, j:j + 1], ALU.mult, eng=("pool" if j % 2 else "dve"))
                        for tb in range(4):
                            for j in range(5):
                                P.mm(k.PB[tb][:, :], Dg[:, j, :], xp[:, j + tb * 512:j + (tb + 1) * 512],
                                     start=(j == 0), stop=(j == 4))
                        for tb in range(4):
                            sl = slice(tb * 512, (tb + 1) * 512)
                            P.act((dst if part == 2 else cv)[:, sl], k.PB[tb][:, :], AF.Silu)
                        if part < 2:
                            def fin(tb, r, dst=dst):
                                P.tt(dst[:, tb * 512:(tb + 1) * 512], cv[:, tb * 512:(tb + 1) * 512], r[:], ALU.mult)
                            norm_pipe(k, 4, lambda tb: cv[:, tb * 512:(tb + 1) * 512], (sq, sq2), (rn, rn2),
                                      (k.PA[2], k.PA[3]), bones, -0.5, 64.0 if part == 0 else 1.0,
                                      64e-6 if part == 0 else 1e-6, fin)
                    for src, dstt in ((K_, Kt), (Vf, Vt)):
                        for grp in range(4):
                            ps = k.PB[grp % 2]
                            for c in range(8):
                                ck = grp * 8 + c
                                tr2(P, ps, c, src[:, ck * 64:(ck + 1) * 64], k.identb)
                            P.cp(dstt[:, grp * 8:(grp + 1) * 8, :], v3(ps), eng=("act" if grp % 2 else "dve"))
                with scope(k):
                    T = dict(GC=GC, GP=GP[p], NBP=NBP[p], sel=sel, negm=negm, id2=id2, Q=Q, K=K_, Kt=Kt, Vt=Vt, O=O)
                    run_pipelined([gdn_chain(k, p, d, T) for d in range(2)], depth=1)
                with scope(k):
                    sqo = [P.sb("g_osq%d" % i, [128, 512], F32) for i in range(2)]
                    rno = [P.sb("g_orn%d" % i, [128, 512], F32) for i in range(2)]

                    def fin_o(tb, r):
                        sl = slice(tb * 512, (tb + 1) * 512)
                        P.tt(r[:], O[:, sl], r[:], ALU.mult)
                        P.ts(mixc(k, p)[:, sl], r[:], ng[:, 0:1], ALU.mult)
                    norm_pipe(k, 4, lambda tb: O[:, tb * 512:(tb + 1) * 512], sqo, rno, (k.PB[2], k.PB[3]), bones,
                              -0.5, 1.0 / 64, EPS, fin_o)


def gdn_chain(k, p, d, T):
    P = k.P
    GC, GP, NBP, sel, negm, id2, Q, K_, Kt, Vt, O = (T[n] for n in ("GC", "GP", "NBP", "sel", "negm", "id2", "Q", "K", "Kt", "Vt", "O"))
    B = k.PA if d == 0 else k.PB
    tag = "g%d_" % d
    names = ("CB", "EI", "QG", "Rm", "U0", "WT", "BW", "KD", "GK", "AcT", "Sg", "Ug", "nA", "nB", "nC", "nD", "Nb", "PTb", "Sgb")
    f32n = ("CB", "EI", "AcT", "Sg")
    NSET = 1
    GG = [{n: P.sb(tag + "%d" % s_ + n, [128, 8, 64], F32 if n in f32n else BF16) for n in names} for s_ in range(NSET)]
    for s_ in range(NSET):
        GG[s_]["gend"] = P.sb(tag + "gend%d" % s_, [128, 8], F32)
        GG[s_]["kds"] = P.sb(tag + "kds%d" % s_, [128, 8], F32)
    gam = P.sb(tag + "gam", [128, NCK], F32)
    shared = {"scan": 0}
    Scar = P.sb(tag + "Scar", [128, 64], F32)
    e_ = 63 if d == 0 else 0
    idb = bcm(id2[:, :], 8)

    def f2(t):
        return t[:].rearrange("p a b -> p (a b)")
    P.memset(Scar[:], 0.0)
    P.act(gam[:], GP[:, :, d], AF.Exp)
    gorder = list(range(4)) if d == 0 else list(range(3, -1, -1))

    def group(gi, grp, G):
        Nb, PTb, Sgb, gend, kds = G["Nb"], G["PTb"], G["Sgb"], G["gend"], G["kds"]
        sl = slice(grp * 512, (grp + 1) * 512)
        g8 = slice(grp * 8, (grp + 1) * 8)
        cj = GP[:, g8, d]
        nb = NBP[:, g8, d]
        CB, EI, QG, Rm, U0, WT, BW, KD, GK, AcT, Sg, Ug = (G[n] for n in names[:12])
        P.mm(B[0][:, :], sel[:, d * 3 + p, :], GC[:, sl])
        P.cp(f2(CB), B[0][:, :], eng="act")
        P.act(f2(EI), f2(CB), AF.Exp)
        P.cp(gend[:], EI[:, :, e_], eng="pool")
        P.tt(f2(QG), f2(EI), Q[:, sl], ALU.mult)
        P.tt(kds[:], CB[:, :, e_], cj, ALU.subtract)
        P.act(kds[:], kds[:], AF.Exp)
        P.tt(GK[:], Kt[:, g8, :], bc3(gam[:, g8], 64), ALU.mult, eng="pool")
        P.tt(KD[:], Kt[:, g8, :], bc3(kds[:], 64), ALU.mult, eng="pool")
        P.tt(CB[:], CB[:], bc3(cj, 64), ALU.subtract)
        P.tt(CB[:], CB[:], bcm(negm[:, d, :], 8), ALU.add)
        P.act(f2(CB), f2(CB), AF.Exp)
        yield
        for c in range(8):
            cs = slice((grp * 8 + c) * 64, (grp * 8 + c + 1) * 64)
            mm2(P, B[0], c, K_[:, cs], Q[:, cs])
            mm2(P, B[1], c, K_[:, cs], K_[:, cs])
        P.tt(EI[:], CB[:], idb, ALU.add)
        P.tt(PTb[:], EI[:], v3(B[0]), ALU.mult)
        P.tt(CB[:], CB[:], v3(B[1]), ALU.mult)
        P.tt(Nb[:], CB[:], bc3(nb, 64), ALU.mult)
        yield
        for _ in neumann2(k, Nb, Rm, (G["nA"], G["nB"], G["nC"], G["nD"]), (B[0], B[1], B[2]), id2):
            yield
        for c in range(8):
            mm2(P, B[0], c, Rm[:, c, :], Vt[:, grp * 8 + c, :])
            mm2(P, B[1], c, GK[:, c, :], Rm[:, c, :])
            mm2(P, B[2], c, Rm[:, c, :], GK[:, c, :])
        P.tt(U0[:], v3(B[0]), bc3(GP[:, g8, 2 + d], 64), ALU.mult)
        P.cp(WT[:], v3(B[1]), eng="act")
        P.tt(BW[:], v3(B[2]), bc3(nb, 64), ALU.mult)
        yield
        for c in range(8):
            mm2(P, B[3], c, BW[:, c, :], KD[:, c, :])
        P.tt(AcT[:], idb, bc3(gend[:], 64), ALU.mult, eng="pool")
        P.tt(AcT[:], AcT[:], v3(B[3]), ALU.add)
        yield
        while shared["scan"] != gi:
            yield
        corder = range(8) if d == 0 else range(7, -1, -1)
        prev = Scar[:]
        for n, c in enumerate(corder):
            P.cp(Sg[:, c, :], prev, eng="pool") if n == 0 else None
            ps = B[2 + n % 2]
            mm2(P, ps, 0, AcT[:, c, :], Sg[:, c, :], start=True, stop=False)
            mm2(P, ps, 0, KD[:, c, :], U0[:, c, :], start=False, stop=True)
            last = (n == 7)
            dst = Scar[:] if last else Sg[:, corder[n + 1], :]
            P.cp(dst, ps[:, 0:64], eng="act")
            yield
        shared["scan"] = gi + 1
        P.cp(Sgb[:], Sg[:], eng="act")
        for c in range(8):
            mm2(P, B[0], c, WT[:, c, :], Sgb[:, c, :])
        P.tt(CB[:], v3(B[0]), bc3(nb, 64), ALU.mult)
        P.tt(Ug[:], CB[:], U0[:], ALU.add)
        yield
        for c in range(8):
            mm2(P, B[1], c, Sgb[:, c, :], QG[:, c, :], start=True, stop=False)
            mm2(P, B[1], c, Ug[:, c, :], PTb[:, c, :], start=False, stop=True)
        P.tt(O[:, sl], O[:, sl], B[1][:, :], ALU.add)
        yield
    return [(lambda slot, gi=gi, grp=grp: group(gi, grp, GG[slot])) for gi, grp in enumerate(gorder)]


RW_EPS = 64e-5
DEC = float(np.exp(-0.5))
GS = 8
NG = NCK // GS


def host_rwkv(inp, b, m):
    mu = inp["rwkv_mu"]
    o = np.zeros((DEPTH, 128, 8, 2), np.float32)
    for part in range(3):
        for p in range(2):
            o[:, :, part * 2 + p, :] = mu[:, :, part * 256 + p * 128: part * 256 + (p + 1) * 128].transpose(0, 2, 1)
    o[:, 0:64, 6, :] = mu[:, :, 768:832].transpose(0, 2, 1)
    o[:, 0:64, 7, :] = mu[:, :, 832:896].transpose(0, 2, 1)
    m["rmu"] = o

    def pp(a):
        if a.ndim == 2:
            return np.ascontiguousarray(a.reshape(DEPTH, 2, 128).transpose(0, 2, 1))
        return np.ascontiguousarray(a.reshape(DEPTH, 2, 2, 128).transpose(0, 3, 1, 2))
    pv = np.zeros((DEPTH, 128, 7, 2), np.float32)
    pv[:, :, 0:2, :] = pp(inp["rwkv_w0"])
    pv[:, :, 2:4, :] = pp(inp["rwkv_a0"])
    pv[:, :, 4, :] = pp(inp["rwkv_k_k"])
    pv[:, :, 5, :] = pp(inp["rwkv_k_a"])
    pv[:, :, 6, :] = pp(inp["rwkv_r_k"].reshape(DEPTH, 256))
    m["rpv"] = pv
    ln = np.zeros((DEPTH, 128, 2, 2), np.float32)
    ln[:, :, 0, :] = pp(inp["rwkv_ln_g"])
    ln[:, :, 1, :] = pp(inp["rwkv_ln_b"])
    m["rln"] = ln
    m["rw2"] = np.ascontiguousarray(inp["rwkv_w2"].transpose(0, 2, 1, 3))
    m["ra2"] = np.ascontiguousarray(inp["rwkv_a2"].transpose(0, 2, 1, 3))
    s_ = np.arange(64)[:, None]
    t_ = np.arange(64)[None, :]
    msk = np.zeros((128, 2, 4, 64), np.float32)
    msk[:, 0, 0, :] = np.tile((t_ > s_), (2, 1))
    msk[:, 0, 1, :] = np.tile((t_ >= s_), (2, 1))
    msk[:, 1, 0, :] = np.tile((t_ < s_), (2, 1))
    msk[:, 1, 1, :] = np.tile((t_ <= s_), (2, 1))
    msk[:, :, 2:4, :] = -msk[:, :, 0:2, :]
    m["rmsk"] = msk


def rwkv_decl(k):
    nc = k.nc

    def din(name, shape, dt=F32):
        return nc.dram_tensor(name, list(shape), dt, kind="ExternalInput").ap()
    k.rmu_d = din("rmu", [DEPTH, 128, 8, 2])
    k.rpv_d = din("rpv", [DEPTH, 128, 7, 2])
    k.rln_d = din("rln", [DEPTH, 128, 2, 2])
    k.rw2_d = din("rw2", [DEPTH, 64, 2, 256])
    k.ra2_d = din("ra2", [DEPTH, 64, 2, 256])
    k.rmsk_d = din("rmsk", [128, 2, 4, 64])


def rwkv_phase(k, l):
    P = k.P
    with scope(k):
        mu = P.sb("r_mu", [128, 8, 3], F32)
        pv = P.sb("r_pv", [128, 7, 2], F32)
        omka = P.sb("r_omka", [128, 2], F32)
        hrk = P.sb("r_hrk", [128, 2], F32)
        ln = P.sb("r_ln", [128, 2, 2], F32)
        w2 = P.sb("r_w2", [64, 2, 256], BF16)
        a2 = P.sb("r_a2", [64, 2, 256], BF16)
        msk = P.sb("r_msk", [128, 2, 4, 64], F32)
        id2 = P.sb("r_id2", [128, 64], F32)
        bones = P.sb("r_bones", [128, 128], F32)
        m0 = P.sb("r_m0", [128, GS * 64], F32)
        twd = P.sb("r_twd", [64, L], BF16, blk=512)
        adx = P.sb("r_adx", [64, L], BF16, blk=512)
        sh32 = P.sb("r_sh32", [128, L], F32, blk=512)
        xp = P.sb("r_xp", [128, L + 2], F32)
        P.dma(mu[:, :, 0:2], k.rmu_d[l])
        P.dma(pv[:], k.rpv_d[l])
        P.dma(ln[:], k.rln_d[l])
        with scope(k):
            w2f = P.sb("r_w2f", [64, 2, 256], F32)
            a2f = P.sb("r_a2f", [64, 2, 256], F32)
            P.dma(w2f[:], k.rw2_d[l])
            P.dma(a2f[:], k.ra2_d[l])
            P.cp(w2[:], w2f[:], eng="act")
            P.cp(a2[:], a2f[:], eng="act")
        P.dma(msk[:], k.rmsk_d[:])
        P.dma(id2[:], k.gid2_d[:])
        P.memset(bones[:], 0.0)
        P.memset(bones[0:64, 0:64], 1.0)
        P.memset(bones[64:128, 64:128], 1.0)
        P.memset(m0[:], 1.0)
        P.memset(m0[:, 0:GS * 64:64], 0.0)
        P.memset(xp[:, 0:1], 0.0)
        P.memset(xp[:, L + 1:L + 2], 0.0)
        P.tt(mu[:, :, 2], mu[:, :, 0], mu[:, :, 1], ALU.add)
        P.ts(mu[:, :, 2], mu[:, :, 2], -1.0, ALU.mult, 1.0, ALU.add)
        P.ts(omka[:], pv[:, 5, :], -1.0, ALU.mult, 1.0, ALU.add)
        P.ts(hrk[:], pv[:, 6, :], 0.5, ALU.mult)

        def shifted(name, ci, dst, np_=128, fn=None):
            pa = proj(k, l, name, alt=(ci % 2 == 1))
            for tb in range(4):
                P.cp(xp[0:np_, 1 + tb * 512:1 + (tb + 1) * 512], pa[tb][0:np_, :], eng=("act" if tb % 2 else "dve"))
            t_ = sh32[0:np_, :]
            P.ts(t_, xp[0:np_, 1:L + 1], mu[0:np_, ci, 2:3], ALU.mult)
            P.stt(t_, xp[0:np_, 0:L], mu[0:np_, ci, 0:1], t_, ALU.mult, ALU.add)
            if fn is None:
                P.stt(dst[:], xp[0:np_, 2:L + 2], mu[0:np_, ci, 1:2], t_, ALU.mult, ALU.add)
            else:
                P.stt(t_, xp[0:np_, 2:L + 2], mu[0:np_, ci, 1:2], t_, ALU.mult, ALU.add)
                P.act(dst[:], t_, fn)

        shifted("rwd", 6, twd, 64, AF.Tanh)
        shifted("rad", 7, adx, 64)
        R_ = P.sb("r_R", [128, L], BF16, blk=512)
        KX = P.sb("r_KX", [128, L], BF16, blk=512)
        V_ = P.sb("r_V", [128, L], BF16, blk=512)
        KK = P.sb("r_KK", [128, L], BF16, blk=512)
        Vt = P.sb("r_Vt", [128, NCK, 64], BF16, blk=512)
        KS = P.sb("r_KS", [128, L], F32, blk=512)
        Y = xp[:, 1:L + 1]
        sq = P.sb("r_sq", [128, 512], F32)
        rn = P.sb("r_rn", [128, 512], F32)
        CH = [rwkv_tiles(k, e) for e in range(2)]

        class _V:
            def __init__(s_, t):
                s_.t = t

            def __getitem__(s_, key):
                return s_.t[:].rearrange("p a b -> p (a b)")[key]
        sq2, rn2 = _V(CH[0]["lw"]), _V(CH[0]["a"])
        for p in range(2):
            shifted("rr%d" % p, 0 + p, R_)
            shifted("rk%d" % p, 2 + p, KX)
            shifted("rv%d" % p, 4 + p, V_)
            P.ts(sh32[:], KX[:], pv[:, 4, p:p + 1], ALU.mult)

            def fin_k(tb, r):
                P.tt(KK[:, tb * 512:(tb + 1) * 512], sh32[:, tb * 512:(tb + 1) * 512], r[:], ALU.mult)
            norm_pipe(k, 4, lambda tb: sh32[:, tb * 512:(tb + 1) * 512], (sq, sq2), (rn, rn2), (k.PB[2], k.PB[3]),
                      bones, -0.5, 1.0, 1e-6, fin_k)
            for grp in range(4):
                ps = k.PB[grp % 2]
                for c in range(8):
                    ck = grp * 8 + c
                    tr2(P, ps, c, V_[:, ck * 64:(ck + 1) * 64], k.identb)
                P.cp(Vt[:, grp * 8:(grp + 1) * 8, :], v3(ps), eng=("act" if grp % 2 else "dve"))
            P.memset(xp[:, 1:L + 1], 0.0, eng="pool")
            P.memset(KS[:], 0.0, eng="pool")
            T = dict(pv=pv, omka=omka, w2=w2, a2=a2, msk=msk, id2=id2, m0=m0, twd=twd, adx=adx,
                     R=R_, KX=KX, KK=KK, Vt=Vt, KS=KS, Y=Y)
            run_interleaved([rwkv_chain(k, p, e, T, CH[e]) for e in range(2)])
            for tb in range(4):
                sl = slice(tb * 512, (tb + 1) * 512)
                P.mm(k.PB[tb % 2][:, :], bones[:], Y[:, sl])
                P.stt(Y[:, sl], k.PB[tb % 2][:, :], -1.0 / 64, Y[:, sl], ALU.mult, ALU.add)

            def fin_y(tb, r):
                sl = slice(tb * 512, (tb + 1) * 512)
                P.tt(Y[:, sl], Y[:, sl], r[:], ALU.mult)
                P.ts(Y[:, sl], Y[:, sl], ln[:, 0, p:p + 1], ALU.mult, ln[:, 1, p:p + 1], ALU.add)
            norm_pipe(k, 4, lambda tb: Y[:, tb * 512:(tb + 1) * 512], (sq, sq2), (rn, rn2), (k.PB[2], k.PB[3]),
                      bones, -0.5, 1.0 / 64, RW_EPS, fin_y)
            for tb in range(4):
                sl = slice(tb * 512, (tb + 1) * 512)
                s_ = (sq, sq2)[tb % 2]
                r_ = (rn, rn2)[tb % 2]
                P.tt(s_[:], R_[:, sl], KS[:, sl], ALU.mult)
                P.ts(s_[:], s_[:], hrk[:, p:p + 1], ALU.mult, eng="pool")
                P.mm(k.PB[tb % 2][:, :], bones[:], s_[:])
                P.tt(r_[:], k.PB[tb % 2][:, :], V_[:, sl], ALU.mult)
                P.tt(mixc(k, 6 + p)[:, sl], Y[:, sl], r_[:], ALU.add)


RW_F32 = ("lw", "a", "km", "b", "cl", "e1", "e2", "dend", "AcT", "Tg")
RW_BF16 = ("kap", "rt", "kt_", "bt_", "ke", "be", "kapT", "keT", "nbeT", "N", "Akv", "Brk", "nBrb", "Rm",
           "nA", "nB", "nC", "nD", "X0", "P0", "WkT", "Wk", "Tgb", "Pg")


def rwkv_tiles(k, e):
    P = k.P
    G = {n: P.sb("r%d_%s" % (e, n), [128, GS, 64], F32) for n in RW_F32}
    for n in RW_BF16:
        G[n] = P.sb("r%d_%s" % (e, n), [128, GS, 64], BF16)
    G["gC"] = P.sb("r%d_gC" % e, [128, GS], F32)
    G["Tcar"] = P.sb("r%d_Tcar" % e, [128, 64], F32)
    return G


def rwkv_chain(k, p, e, T, G):
    P = k.P
    pv, omka, w2, a2, msk, id2, m0, twd, adx, R_, KX, KK, Vt, KS, Y = (T[n] for n in (
        "pv", "omka", "w2", "a2", "msk", "id2", "m0", "twd", "adx", "R", "KX", "KK", "Vt", "KS", "Y"))
    B = k.PA if e == 0 else k.PB
    W = GS * 64
    e_ = 63 if e == 0 else 0
    idb = bcm(id2[:, :], GS)
    gC, Tcar = G["gC"], G["Tcar"]

    def f2(t):
        return t[:].rearrange("p a b -> p (a b)")

    def w3(ps):
        return v3(ps, GS)
    P.memset(Tcar[:], 0.0)
    gorder = range(NG) if e == 0 else range(NG - 1, -1, -1)
    pc = slice(p * 128, (p + 1) * 128)
    for grp in gorder:
        sl = slice(grp * W, (grp + 1) * W)
        c0 = grp * GS
        P.mm(B[0][:, 0:W], w2[:, e, pc], twd[:, sl])
        P.mm(B[1][:, 0:W], a2[:, e, pc], adx[:, sl])
        P.act(f2(G["lw"]), B[0][:, 0:W], AF.Sigmoid, bias=pv[:, 0 + e, p:p + 1])
        P.act(f2(G["a"]), B[1][:, 0:W], AF.Sigmoid, bias=pv[:, 2 + e, p:p + 1])
        P.ts(f2(G["km"]), f2(G["a"]), pv[:, 5, p:p + 1], ALU.mult, omka[:, p:p + 1], ALU.add)
        P.tt(f2(G["km"]), f2(G["km"]), KX[:, sl], ALU.mult)
        P.tt(f2(G["b"]), f2(G["a"]), KK[:, sl], ALU.mult)
        P.tt(KS[:, sl], KS[:, sl], f2(G["km"]), ALU.add, eng="pool")
        P.scan(f2(G["cl"]), m0[:], f2(G["lw"]), 0.0, ALU.mult, ALU.add)
        if e == 1:
            P.tt(G["e1"][:], bc3(G["cl"][:, :, 63], 64), G["cl"][:], ALU.subtract)
            P.tt(G["cl"][:], G["e1"][:], G["lw"][:], ALU.add)
        yield
        P.act(G["e1"][:], G["cl"][:], AF.Exp, scale=-DEC)
        P.act(G["e2"][:], G["cl"][:], AF.Exp, scale=DEC)
        P.tt(f2(G["rt"]), f2(G["e1"]), R_[:, sl], ALU.mult)
        P.tt(G["kt_"][:], G["e2"][:], G["km"][:], ALU.mult)
        P.tt(G["bt_"][:], G["e2"][:], G["b"][:], ALU.mult)
        P.tt(G["dend"][:], G["cl"][:], G["lw"][:], ALU.subtract)
        P.act(G["dend"][:], G["dend"][:], AF.Exp, scale=-DEC)
        P.tt(f2(G["kap"]), f2(G["dend"]), KK[:, sl], ALU.mult)
        P.cp(gC[:], G["e1"][:, :, e_], eng="pool")
        P.tt(G["dend"][:], bc3(G["cl"][:, :, e_], 64), G["cl"][:], ALU.subtract)
        P.act(G["dend"][:], G["dend"][:], AF.Exp, scale=-DEC)
        P.tt(G["ke"][:], G["dend"][:], G["km"][:], ALU.mult)
        P.tt(G["be"][:], G["dend"][:], G["b"][:], ALU.mult, eng="pool")
        yield
        for src, dst, sc in ((G["kap"], G["kapT"], 1.0), (G["ke"], G["keT"], 1.0), (G["be"], G["nbeT"], -1.0)):
            ps = B[0] if sc == 1.0 and src is G["kap"] else (B[1] if sc == 1.0 else B[2])
            for c in range(GS):
                tr2(P, ps, c, src[:, c, :], k.identb)
            if sc == 1.0:
                P.cp(dst[:], w3(ps), eng="act")
            else:
                P.ts(dst[:], w3(ps), -1.0, ALU.mult)
        yield
        for c in range(GS):
            mm2(P, B[0], c, G["bt_"][:, c, :], G["kap"][:, c, :])
            mm2(P, B[1], c, G["kt_"][:, c, :], G["kap"][:, c, :])
            mm2(P, B[2], c, G["kt_"][:, c, :], G["rt"][:, c, :])
            mm2(P, B[3], c, G["bt_"][:, c, :], G["rt"][:, c, :])
        ms = bcm(msk[:, e, 0, :], GS)
        mi = bcm(msk[:, e, 1, :], GS)
        nms = bcm(msk[:, e, 2, :], GS)
        nmi = bcm(msk[:, e, 3, :], GS)
        P.tt(G["N"][:], w3(B[0]), nms, ALU.mult)
        P.tt(G["Akv"][:], w3(B[1]), ms, ALU.mult)
        P.tt(G["Brk"][:], w3(B[2]), mi, ALU.mult)
        P.tt(G["nBrb"][:], w3(B[3]), nmi, ALU.mult)
        yield
        for _ in neumann2(k, G["N"], G["Rm"], (G["nA"], G["nB"], G["nC"], G["nD"]), (B[0], B[1], B[2]), id2, GS):
            yield
        for c in range(GS):
            mm2(P, B[0], c, G["Akv"][:, c, :], Vt[:, c0 + c, :])
        P.cp(G["X0"][:], w3(B[0]), eng="act")
        yield
        for c in range(GS):
            mm2(P, B[0], c, G["Rm"][:, c, :], G["X0"][:, c, :])
            mm2(P, B[1], c, G["kapT"][:, c, :], G["Rm"][:, c, :])
            mm2(P, B[2], c, G["Rm"][:, c, :], G["kapT"][:, c, :])
        P.cp(G["P0"][:], w3(B[0]), eng="act")
        P.cp(G["WkT"][:], w3(B[1]), eng="dve")
        P.cp(G["Wk"][:], w3(B[2]), eng="act")
        yield
        for c in range(GS):
            mm2(P, B[3], c, G["Wk"][:, c, :], G["nbeT"][:, c, :])
        P.tt(G["AcT"][:], idb, bc3(gC[:], 64), ALU.mult, eng="pool")
        P.tt(G["AcT"][:], G["AcT"][:], w3(B[3]), ALU.add)
        yield
        corder = list(range(GS)) if e == 0 else list(range(GS - 1, -1, -1))
        Tg = G["Tg"]
        for n, c in enumerate(corder):
            if n == 0:
                P.cp(Tg[:, c, :], Tcar[:], eng="pool")
            ps = B[2 + n % 2]
            mm2(P, ps, 0, G["AcT"][:, c, :], Tg[:, c, :], start=True, stop=False)
            mm2(P, ps, 0, G["keT"][:, c, :], Vt[:, c0 + c, :], start=False, stop=False)
            mm2(P, ps, 0, G["nbeT"][:, c, :], G["P0"][:, c, :], start=False, stop=True)
            dst = Tcar[:] if n == GS - 1 else Tg[:, corder[n + 1], :]
            P.cp(dst, ps[:, 0:64], eng="act")
            yield
        P.cp(G["Tgb"][:], Tg[:], eng="act")
        for c in range(GS):
            mm2(P, B[0], c, G["WkT"][:, c, :], G["Tgb"][:, c, :])
        P.tt(G["Pg"][:], w3(B[0]), G["P0"][:], ALU.add)
        yield
        for c in range(GS):
            mm2(P, B[1], c, Vt[:, c0 + c, :], G["Brk"][:, c, :], start=True, stop=False)
            mm2(P, B[1], c, G["Tgb"][:, c, :], G["rt"][:, c, :], start=False, stop=False)
            mm2(P, B[1], c, G["Pg"][:, c, :], G["nBrb"][:, c, :], start=False, stop=True)
        P.tt(Y[:, sl], Y[:, sl], B[1][:, 0:W], ALU.add)
        yield


_CACHE = {}


def kernel(**inputs):
    inp = {k_: np.asarray(v) for k_, v in inputs.items()}
    if "k" not in _CACHE:
        _CACHE["k"] = build()
    k = _CACHE["k"]
    B = inp["x"].shape[0]
    base = host_inputs(inp, 0)
    in_maps = []
    for b in range(B):
        m = dict(base)
        m["x"] = np.ascontiguousarray(inp["x"][b], dtype=np.float32)
        m["pos"] = np.ascontiguousarray(inp["positions"][b].reshape(1, L).astype(np.int32))
        in_maps.append(m)
    res = run_bass_kernel_spmd(k.nc, in_maps, core_ids=list(range(B)))
    return np.stack([np.asarray(r["out"], dtype=np.float32) for r in res.results], axis=0)
```

```python
import numpy as np
import concourse.bass as bass
import concourse.mybir as mybir
from concourse.bass_utils import run_bass_kernel_spmd
from contextlib import ExitStack

F32 = mybir.dt.float32
BF16 = mybir.dt.bfloat16
I32 = mybir.dt.int32
AF = mybir.ActivationFunctionType
ALU = mybir.AluOpType
AX = mybir.AxisListType
DTSIZE = {F32: 4, BF16: 2, I32: 4}


class _Op:
    __slots__ = ("eng", "emit", "deps", "idx", "needed", "sigval", "dsem", "dval", "isdma")

    def __init__(self, eng, emit, isdma=False):
        self.eng = eng
        self.emit = emit
        self.deps = []
        self.idx = -1
        self.needed = False
        self.sigval = 0
        self.dsem = None
        self.dval = 0
        self.isdma = isdma


class _Blk:
    __slots__ = ("w", "r")

    def __init__(self):
        self.w = None
        self.r = {}


class Prog:
    ENGS = ("pe", "dve", "act", "pool", "sp")
    NDMA = 48
    NHW = 32

    def __init__(self, nc, stack):
        self.nc = nc
        self.stack = stack
        self.ops = {e: [] for e in self.ENGS}
        self.track = {}
        self.seen = {e: {} for e in self.ENGS}
        self.seen_dma = {e: set() for e in self.ENGS}
        self.dma_last = [None] * self.NDMA
        self.dma_uses = [0] * self.NDMA
        self.dma_rr = 0
        self.dma_rr_sw = 0
        self.ndma_ops = 0
        self.untracked = set()
        self.out_dmas = []
        self.dma_pending = []
        self.last_compute = {}

    def sb(self, name, shape, dtype=F32, blk=None):
        self.uid = getattr(self, "uid", 0) + 1
        name = "s%d_%s" % (self.uid, name)
        t = self.stack.enter_context(self.nc.sbuf_tensor(name, list(shape), dtype))
        self._register(name, shape, dtype, blk)
        return t

    def ps(self, name, shape=(128, 512), dtype=F32, blk=None):
        self.uid = getattr(self, "uid", 0) + 1
        name = "p%d_%s" % (self.uid, name)
        t = self.stack.enter_context(self.nc.psum_tensor(name, list(shape), dtype))
        self._register(name, shape, dtype, blk)
        return t

    def _register(self, name, shape, dtype, blk):
        row = int(np.prod(shape[1:])) * DTSIZE[dtype]
        bb = row if blk is None else blk * DTSIZE[dtype]
        nb = (row + bb - 1) // bb
        self.track[name] = (bb, row, [_Blk() for _ in range(nb)])

    def dram_track(self, name, total_bytes, blk_bytes):
        nb = (total_bytes + blk_bytes - 1) // blk_bytes
        self.track[name] = (blk_bytes, -1, [_Blk() for _ in range(nb)])

    def _blocks(self, ap):
        name = ap.tensor.name
        if name not in self.track:
            return ()
        bb, row, blks = self.track[name]
        if len(blks) == 1:
            return blks
        ds = DTSIZE[ap.dtype]
        pat = ap.ap
        if row < 0:
            lo = hi = ap.offset
            for step, cnt in pat:
                ext = step * (cnt - 1)
                if ext < 0:
                    lo += ext
                else:
                    hi += ext
            return blks[(lo * ds) // bb:(hi * ds) // bb + 1]
        rowel = row // ds
        foff = ap.offset % rowel
        lo = hi = foff
        for step, cnt in pat[1:]:
            ext = step * (cnt - 1)
            if ext < 0:
                lo += ext
            else:
                hi += ext
        b0 = (lo * ds) // bb
        b1 = (hi * ds) // bb
        return blks[b0:b1 + 1]

    def _dep(self, x, y):
        if y is None or y is x:
            return
        e = x.eng
        if y.isdma:
            if id(y) in self.seen_dma[e]:
                return
            self.seen_dma[e].add(id(y))
            x.deps.append(y)
            return
        if y.eng == "pe" and e == "pe":
            return
        if y.idx <= self.seen[e].get(y.eng, -1):
            return
        self.seen[e][y.eng] = y.idx
        y.needed = True
        x.deps.append(y)

    def add(self, eng, emit, reads=(), writes=(), isdma=False):
        x = _Op(eng, emit, isdma)
        x.idx = len(self.ops[eng])
        rb = []
        for ap in reads:
            if ap is None or isinstance(ap, (int, float)):
                continue
            rb.extend(self._blocks(ap))
        wb = []
        for ap in writes:
            wb.extend(self._blocks(ap))
        for ap in reads:
            if ap is None or isinstance(ap, (int, float)) or not ap.tensor.name.startswith("p"):
                continue
            for b in self._blocks(ap):
                for key, y in b.r.items():
                    if key != eng:
                        self._dep(x, y)
        for b in rb:
            self._dep(x, b.w)
        for b in wb:
            self._dep(x, b.w)
            for y in b.r.values():
                self._dep(x, y)
        if isdma:
            if eng == "pool":
                s = self.NHW + self.dma_rr_sw
                self.dma_rr_sw = (self.dma_rr_sw + 1) % (self.NDMA - self.NHW)
            else:
                s = self.dma_rr
                self.dma_rr = (self.dma_rr + 1) % self.NHW
            self._dep(x, self.dma_last[s])
            self.dma_last[s] = x
            self.dma_uses[s] += 1
            x.dsem = s
            x.dval = 16 * self.dma_uses[s]
            self.ndma_ops += 1
        key = id(x) if isdma else eng
        for b in rb:
            b.r[key] = x
        for b in wb:
            b.w = x
            b.r = {}
        self.ops[eng].append(x)
        if isdma:
            self.dma_pending.append(x)
        else:
            self.last_compute[eng] = x
        return x

    def barrier(self):
        lasts = dict(self.last_compute)
        pend = list(self.dma_pending)
        self.dma_pending = []
        for e in self.ENGS:
            b = _Op(e, None)
            b.idx = len(self.ops[e])
            for e2, y in lasts.items():
                if e2 == e and e == "pe":
                    continue
                self._dep(b, y)
            for y in pend:
                self._dep(b, y)
            self.ops[e].append(b)

    def mm(self, out, lhsT, rhs, start=True, stop=True):
        return self.add("pe", lambda e: e.matmul(out, lhsT, rhs, start=start, stop=stop),
                        reads=(lhsT, rhs), writes=(out,))

    def tr(self, out, in_, ident):
        return self.add("pe", lambda e: e.transpose(out, in_, ident), reads=(in_, ident), writes=(out,))

    def tt(self, out, in0, in1, op, eng="dve"):
        return self.add(eng, lambda e: e.tensor_tensor(out, in0, in1, op), reads=(in0, in1), writes=(out,))

    def ts(self, out, in0, s1, op0, s2=None, op1=None, eng="dve", accum_out=None):
        kw = {}
        if eng == "pool" and op1 is None:
            if op0 == ALU.mult:
                s2, op1 = 0.0, ALU.add
            elif op0 == ALU.add:
                s2, op1 = 1.0, ALU.mult
        if op1 is not None:
            kw["op1"] = op1
        if accum_out is not None:
            kw["accum_out"] = accum_out
        w = (out,) if accum_out is None else (out, accum_out)
        return self.add(eng, lambda e: e.tensor_scalar(out, in0, s1, s2, op0, **kw),
                        reads=(in0, s1, s2), writes=w)

    def stt(self, out, in0, scalar, in1, op0, op1, accum_out=None):
        kw = {}
        if accum_out is not None:
            kw["accum_out"] = accum_out
        w = (out,) if accum_out is None else (out, accum_out)
        return self.add("dve", lambda e: e.scalar_tensor_tensor(out, in0, scalar, in1, op0, op1, **kw),
                        reads=(in0, scalar, in1), writes=w)

    def cp(self, out, in_, eng="dve"):
        if eng == "act":
            return self.add("act", lambda e: e.copy(out, in_), reads=(in_,), writes=(out,))
        return self.add(eng, lambda e: e.tensor_copy(out, in_), reads=(in_,), writes=(out,))

    def act(self, out, in_, func, bias=0.0, scale=1.0, accum_out=None):
        kw = {}
        if accum_out is not None:
            kw["accum_out"] = accum_out
        w = (out,) if accum_out is None else (out, accum_out)
        return self.add("act", lambda e: e.activation(out, in_, func, bias=bias, scale=scale, **kw),
                        reads=(in_, bias, scale), writes=w)

    def red(self, out, in_, op, axis=AX.X, eng="dve"):
        return self.add(eng, lambda e: e.tensor_reduce(out, in_, axis, op), reads=(in_,), writes=(out,))

    def recip(self, out, in_):
        return self.add("dve", lambda e: e.reciprocal(out, in_), reads=(in_,), writes=(out,))

    def rpow(self, out, in_, power, scale=1.0, bias=0.0):
        self.act(out, in_, AF.Ln, bias=bias, scale=scale)
        return self.act(out, out, AF.Exp, scale=power)

    def memset(self, ap, val, eng="dve"):
        return self.add(eng, lambda e: e.memset(ap, val), writes=(ap,))

    def scan(self, out, d0, d1, init, op0, op1):
        return self.add("dve", lambda e: e.tensor_tensor_scan(out, d0, d1, init, op0, op1),
                        reads=(d0, d1, init), writes=(out,))

    def dma(self, out, in_, eng="sp", is_output=False):
        x = self.add(eng, lambda e: e.dma_start(out=out, in_=in_), reads=(in_,), writes=(out,), isdma=True)
        if is_output:
            self.out_dmas.append(x)
        return x

    def finish(self):
        nc = self.nc
        fin = _Op("sp", None)
        fin.idx = len(self.ops["sp"])
        for y in self.out_dmas:
            self._dep(fin, y)
        self.ops["sp"].append(fin)
        sems = {}
        for e in ("pe", "dve", "act", "pool"):
            sems[e] = self.stack.enter_context(nc.semaphore("s_" + e))
        dsems = [self.stack.enter_context(nc.semaphore("d%d" % i)) for i in range(self.NDMA)]
        for e in ("pe", "dve", "act", "pool"):
            c = 0
            for x in self.ops[e]:
                if x.isdma:
                    continue
                if x.needed:
                    c += 1
                    x.sigval = c
            self.stats_sig = getattr(self, "stats_sig", {})
            self.stats_sig[e] = c
        ops = self.ops

        def replay(e, engobj):
            for x in ops[e]:
                for y in x.deps:
                    if y.isdma:
                        engobj.wait_ge(dsems[y.dsem], y.dval)
                    else:
                        engobj.wait_ge(sems[y.eng], y.sigval)
                if x.emit is None:
                    continue
                ins = x.emit(engobj)
                if x.isdma:
                    ins.then_inc(dsems[x.dsem], 16)
                elif x.needed:
                    ins.then_inc(sems[e], 1)

        with nc.Block() as block:
            @block.tensor
            def _(eng):
                replay("pe", eng)

            @block.vector
            def _(eng):
                replay("dve", eng)

            @block.scalar
            def _(eng):
                replay("act", eng)

            @block.gpsimd
            def _(eng):
                replay("pool", eng)

            @block.sync
            def _(eng):
                replay("sp", eng)


L = 2048
D = 1024
NT = L // 128
DEPTH = 2
N_IN = 3448
EPS = 1e-6

OFF = dict(gate=0, gdn_q=1024, gdn_k=1408, gdn_v=1792, gdn_a=2176, gdn_b=2188, mla_cq=2200, mla_ckv=2392,
           mla_kr=2520, rw_r=2552, rw_k=2808, rw_v=3064, rw_wd=3320, rw_ad=3384)


def chunk_table():
    ch = []
    for h in range(3):
        ch.append(("gq%d" % h, [(0, OFF["gdn_q"] + h * 128, 128)]))
        ch.append(("gk%d" % h, [(0, OFF["gdn_k"] + h * 128, 128)]))
        ch.append(("gv%d" % h, [(0, OFF["gdn_v"] + h * 128, 128)]))
    ch.append(("gab", [(0, OFF["gdn_a"], 6), (32, OFF["gdn_a"] + 6, 6), (64, OFF["gdn_b"], 6), (96, OFF["gdn_b"] + 6, 6)]))
    ch.append(("cq0", [(0, OFF["mla_cq"], 128)]))
    ch.append(("cq1", [(0, OFF["mla_cq"] + 128, 64)]))
    ch.append(("ckv", [(0, OFF["mla_ckv"], 128)]))
    ch.append(("kr", [(0, OFF["mla_kr"], 32)]))
    for i in range(2):
        ch.append(("rr%d" % i, [(0, OFF["rw_r"] + i * 128, 128)]))
        ch.append(("rk%d" % i, [(0, OFF["rw_k"] + i * 128, 128)]))
        ch.append(("rv%d" % i, [(0, OFF["rw_v"] + i * 128, 128)]))
    ch.append(("rwd", [(0, OFF["rw_wd"], 64)]))
    ch.append(("rad", [(0, OFF["rw_ad"], 64)]))
    for i in range(8):
        ch.append(("g%d" % i, [(0, OFF["gate"] + i * 128, 128)]))
    return ch


CHUNKS = chunk_table()
CH_IDX = {n: i for i, (n, _) in enumerate(CHUNKS)}
NCH = len(CHUNKS)


def host_win(w_in):
    out = np.zeros((DEPTH, NCH, 128, 8, 128), np.float32)
    for ci, (_, parts) in enumerate(CHUNKS):
        for dst, src, w in parts:
            blk = w_in[:, :, src:src + w].reshape(DEPTH, 8, 128, w)
            out[:, ci, :, :, dst:dst + w] = blk.transpose(0, 2, 1, 3)
    return out


class K:
    pass


def build(depth=DEPTH, mixers=("gdn", "mla", "rwkv"), dbg=False):
    nc = bass.Bass("TRN2", target_bir_lowering=False)
    k = K()
    k.nc = nc
    k.dbg = dbg
    k.dbg_outs = []
    k.cut = 99

    def din(name, shape, dt=F32):
        return nc.dram_tensor(name, list(shape), dt, kind="ExternalInput").ap()

    k.x_d = din("x", [L, D])
    k.win_d = din("win", [DEPTH, NCH, 128, 8, 128])
    k.normg_d = din("normg", [DEPTH, 128, 8])
    k.wout_d = din("wout", [DEPTH, 128, 8, 1024])
    k.fing_d = din("fing", [1, D])
    k.ident_d = din("ident", [128, 128])
    k.out_d = nc.dram_tensor("out", [L, D], F32, kind="ExternalOutput").ap()
    mla_decl(k)
    gdn_decl(k)
    rwkv_decl(k)

    with ExitStack() as st:
        P = Prog(nc, st)
        k.P = P
        k.xscr = nc.dram_tensor("xscr", [L, D], F32, kind="Internal").ap()
        P.dram_track("xscr", L * D * 4, 128 * D * 4)
        k.hT = P.sb("hT", [128, 8, L], BF16, blk=512)
        k.ident = P.sb("ident", [128, 128], F32)
        k.identb = P.sb("identb", [128, 128], BF16)
        k.normg = P.sb("normg", [128, DEPTH, 8], F32)
        k.wst = [P.sb("wst%d" % i, [128, 8, 128], F32) for i in range(2)]
        k.wbf = [P.sb("wbf%d" % i, [128, 8, 128], BF16) for i in range(2)]
        k.wrr = 0
        k.PAW = [P.ps("paw%d" % i, [128, 1024], F32, blk=512) for i in range(2)]
        k.PA = [k.PAW[i // 2][:, (i % 2) * 512:(i % 2 + 1) * 512] for i in range(4)]
        k.PB = [P.ps("pb%d" % i, [128, 512], F32) for i in range(4)]

        P.dma(k.ident[:], k.ident_d[:])
        P.cp(k.identb[:], k.ident[:])
        for l in range(DEPTH):
            P.dma(k.normg[:, l, :], k.normg_d[l])

        for l in range(depth):
            phase_a(k, l)
            with scope(k):
                k.mix_r = P.sb("mix_r", [128, 2, L], BF16, blk=512)
                if "rwkv" in mixers:
                    rwkv_phase(k, l)
                else:
                    P.memset(k.mix_r[:].rearrange("p a b -> p (a b)"), 1.0)
                with scope(k):
                    k.mix_m = P.sb("mix_m", [128, 3, L], BF16, blk=512)
                    if "mla" in mixers:
                        mla_phase(k, l)
                    else:
                        P.memset(k.mix_m[:].rearrange("p a b -> p (a b)"), 1.0)
                    with scope(k):
                        k.mix_g = P.sb("mix_g", [128, 3, L], BF16, blk=512)
                        if "gdn" in mixers:
                            gdn_phase(k, l)
                        else:
                            P.memset(k.mix_g[:].rearrange("p a b -> p (a b)"), 1.0)
                        if k.dbg:
                            for nm, t_, n_ in (("g", k.mix_g, 3), ("m", k.mix_m, 3), ("r", k.mix_r, 2)):
                                dump(k, "mix_%s%d" % (nm, l), t_[:].rearrange("p a b -> p (a b)"), [128, n_ * L])
                        phase_z(k, l, last=(l == depth - 1))
        P.finish()
        print("ops:", {e: len(v) for e, v in P.ops.items()}, "sig:", P.stats_sig, "dma:", P.ndma_ops)
    return k


def mixc(k, c):
    if c < 3:
        return k.mix_g[:, c, :]
    if c < 6:
        return k.mix_m[:, c - 3, :]
    return k.mix_r[:, c - 6, :]


def dump(k, name, ap, shape=None):
    if not k.dbg:
        return
    P = k.P
    shape = list(ap.shape) if shape is None else shape
    d = k.nc.dram_tensor("dbg_" + name, shape, ap.dtype, kind="ExternalOutput").ap()
    P.dma(d[:] if len(shape) == 2 else d, ap, is_output=True)
    k.dbg_outs.append("dbg_" + name)


def scope(k):
    class _S:
        def __enter__(s):
            s.old = k.P.stack
            s.st = ExitStack()
            s.st.__enter__()
            k.P.stack = s.st
            return s

        def __exit__(s, *a):
            k.P.barrier()
            k.P.stack = s.old
            s.st.__exit__(*a)
            return False
    return _S()


def phase_a(k, l):
    P = k.P
    with scope(k):
        ssq = P.sb("a_ssq", [128, NT])
        rs = P.sb("a_rs", [128, NT])
        rstd = P.sb("a_rstd", [128, NT])
        junk = [P.sb("a_junk%d" % i, [128, D], BF16) for i in range(2)]
        xs = [P.sb("a_xs%d" % i, [128, D], BF16) for i in range(2)]
        xin = [P.sb("a_xin%d" % i, [128, D], F32) for i in range(3)]
        src = k.x_d if l == 0 else k.xscr

        def stage1(tt):
            b = tt % 2
            xt_ = xin[tt % 3]
            P.dma(xt_[:], src[tt * 128:(tt + 1) * 128, :])
            P.act(junk[b][:], xt_[:], AF.Square, accum_out=ssq[:, tt:tt + 1])
            P.act(rs[:, tt:tt + 1], ssq[:, tt:tt + 1], AF.Sqrt, bias=EPS, scale=1.0 / D)
            P.recip(rstd[:, tt:tt + 1], rs[:, tt:tt + 1])
            P.ts(xs[b][:], xt_[:], rstd[:, tt:tt + 1], ALU.mult)

        def stage2(tt):
            b = tt % 2
            pt = k.PB[b][:].bitcast(BF16)
            for dc in range(8):
                P.tr(pt[:, dc * 128:(dc + 1) * 128], xs[b][:, dc * 128:(dc + 1) * 128], k.identb[:])
            P.cp(k.hT[:, :, tt * 128:(tt + 1) * 128], pt[:].rearrange("p (a b) -> p a b", a=8),
                 eng=("act" if tt % 2 == 0 else "dve"))
        stage1(0)
        for tt in range(NT):
            if tt + 1 < NT:
                stage1(tt + 1)
            stage2(tt)


def proj(k, l, name, alt=False):
    P = k.P
    BK = k.PB if alt else k.PA
    ci = CH_IDX[name]
    b = k.wrr
    k.wrr ^= 1
    P.dma(k.wst[b][:], k.win_d[l, ci], eng="sp")
    gb = k.normg[:, l, :].unsqueeze(2).broadcast_to([128, 8, 128])
    P.tt(k.wbf[b][:], k.wst[b][:], gb, ALU.mult, eng="pool")
    for tb in range(4):
        for dc in range(8):
            P.mm(BK[tb][:, :], k.wbf[b][:, dc, :], k.hT[:, dc, tb * 512:(tb + 1) * 512], start=(dc == 0), stop=(dc == 7))
    return BK


ZQ = "act"


def phase_z(k, l, last):
    P = k.P
    with scope(k):
        wst = P.sb("z_wst", [128, 8, 512], F32)
        wob = P.sb("z_wob", [128, 8, 1024], BF16, blk=512)
        sg = [P.sb("z_sg%d" % i, [128, L], BF16, blk=512) for i in range(2)]
        for nb in range(2):
            P.dma(wst[:], k.wout_d[l, :, :, nb * 512:(nb + 1) * 512])
            P.cp(wob[:, :, nb * 512:(nb + 1) * 512], wst[:], eng="act")
        for gc in range(8):
            pa = proj(k, l, "g%d" % gc, alt=(gc % 2 == 1))
            s = sg[gc % 2]
            for tb in range(4):
                P.act(s[:, tb * 512:(tb + 1) * 512], pa[tb][:, :], AF.Silu)
                mc = mixc(k, gc)[:, tb * 512:(tb + 1) * 512]
                P.tt(mc, mc, s[:, tb * 512:(tb + 1) * 512], ALU.mult)
        xin = [P.sb("z_xin%d" % i, [128, D], F32) for i in range(3)]
        src = k.x_d if l == 0 else k.xscr
        if last:
            ssq = P.sb("f_ssq", [128, NT])
            rs = P.sb("f_rs", [128, NT])
            rstd = P.sb("f_rstd", [128, NT])
            junk = [P.sb("f_junk%d" % i, [128, D], BF16) for i in range(2)]
            gf = P.sb("f_g", [128, D])
            ot = [P.sb("f_o%d" % i, [128, D]) for i in range(2)]
            P.dma(gf[:], k.fing_d[0:1, :].partition_broadcast(128))
        def za(tt):
            xt_ = xin[tt % 3]
            P.dma(xt_[:], src[tt * 128:(tt + 1) * 128, :])
            for nb in range(2):
                ps = k.PB[(tt * 2 + nb) % 4]
                for kc in range(8):
                    P.mm(ps[:, :], mixc(k, kc)[:, tt * 128:(tt + 1) * 128], wob[:, kc, nb * 512:(nb + 1) * 512],
                         start=(kc == 0), stop=(kc == 7))
                xs = xt_[:, nb * 512:(nb + 1) * 512]
                P.tt(xs, xs, ps[:, :], ALU.add)
            if not last:
                P.dma(k.xscr[tt * 128:(tt + 1) * 128, :], xt_[:], eng=ZQ)
            else:
                b = tt % 2
                P.act(junk[b][:], xt_[:], AF.Square, accum_out=ssq[:, tt:tt + 1])
                P.act(rs[:, tt:tt + 1], ssq[:, tt:tt + 1], AF.Sqrt, bias=EPS, scale=1.0 / D)

        def zb(tt):
            if last:
                xt_ = xin[tt % 3]
                b = tt % 2
                P.recip(rstd[:, tt:tt + 1], rs[:, tt:tt + 1])
                P.stt(ot[b][:], xt_[:], rstd[:, tt:tt + 1], gf[:], ALU.mult, ALU.mult)
                P.dma(k.out_d[tt * 128:(tt + 1) * 128, :], ot[b][:], eng=ZQ, is_output=True)
        za(0)
        for tt in range(NT):
            if tt + 1 < NT:
                za(tt + 1)
            zb(tt)


def host_inputs(inp, b):
    m = {}
    m["x"] = np.ascontiguousarray(inp["x"][b])
    m["win"] = host_win(inp["w_in"])
    m["normg"] = np.ascontiguousarray(inp["norm_g"].reshape(DEPTH, 8, 128).transpose(0, 2, 1))
    m["wout"] = np.ascontiguousarray(inp["w_out"].reshape(DEPTH, 8, 128, 1024).transpose(0, 2, 1, 3))
    m["fing"] = np.ascontiguousarray(inp["final_norm_g"].reshape(1, D))
    m["ident"] = np.eye(128, dtype=np.float32)
    host_mla(inp, b, m)
    host_gdn(inp, b, m)
    host_rwkv(inp, b, m)
    return m


TWO_PI = 2.0 * np.pi


def host_mla(inp, b, m):
    half = 16
    inv_freq = (10000.0 ** (-np.arange(half, dtype=np.float32) / half)).astype(np.float32)
    invf = np.zeros((32, 1), np.float32)
    invf[:, 0] = np.tile(inv_freq, 2) / np.float32(TWO_PI)
    m["invf"] = invf
    rm = np.zeros((32, 32), np.float32)
    for i in range(16):
        rm[i, i + 16] = -1.0
        rm[i + 16, i] = 1.0
    m["rmT"] = np.ascontiguousarray(rm.T)
    m["pos"] = np.ascontiguousarray(inp["positions"][b].reshape(1, L).astype(np.int32))
    wuq = inp["mla_w_uq"]
    o = np.zeros((DEPTH, 128, 2, 6, 128), np.float32)
    for h in range(6):
        nope = wuq[:, :, h * 96:h * 96 + 64]
        rope = wuq[:, :, h * 96 + 64:h * 96 + 96]
        o[:, :, 0, h, 64:128] = nope[:, 0:128]
        o[:, 0:64, 1, h, 64:128] = nope[:, 128:192]
        o[:, :, 0, h, 0:32] = rope[:, 0:128]
        o[:, 0:64, 1, h, 0:32] = rope[:, 128:192]
    m["wuq"] = o
    gq = np.zeros((DEPTH, 128, 2), np.float32)
    gq[:, :, 0] = inp["mla_q_norm_g"][:, 0:128]
    gq[:, 0:64, 1] = inp["mla_q_norm_g"][:, 128:192]
    m["gq"] = gq
    wukv = inp["mla_w_ukv"]
    wk = np.zeros((DEPTH, 128, 6, 128), np.float32)
    wv = np.zeros((DEPTH, 128, 6, 64), np.float32)
    for h in range(6):
        wk[:, :, h, 64:128] = wukv[:, :, h * 128:h * 128 + 64]
        wv[:, :, h, :] = wukv[:, :, h * 128 + 64:h * 128 + 128]
    m["wuk"] = wk
    m["wuv"] = wv
    m["gkv"] = np.ascontiguousarray(inp["mla_kv_norm_g"].reshape(DEPTH, 128, 1))


def mla_decl(k):
    nc = k.nc

    def din(name, shape, dt=F32):
        return nc.dram_tensor(name, list(shape), dt, kind="ExternalInput").ap()
    k.invf_d = din("invf", [32, 1])
    k.rmT_d = din("rmT", [32, 32])
    k.pos_d = din("pos", [1, L], I32)
    k.wuq_d = din("wuq", [DEPTH, 128, 2, 6, 128])
    k.gq_d = din("gq", [DEPTH, 128, 2])
    k.wuk_d = din("wuk", [DEPTH, 128, 6, 128])
    k.wuv_d = din("wuv", [DEPTH, 128, 6, 64])
    k.gkv_d = din("gkv", [DEPTH, 128, 1])


def latent_norm(k, l, names, nfeat, outs, ones):
    P = k.P
    sq = [P.sb("ln_sq%d" % i, [128, 512]) for i in range(2)]
    rq = [P.sb("ln_rq%d" % i, [128, 512]) for i in range(2)]
    n = len(names)
    for i, nm in enumerate(names):
        pa = proj(k, l, nm)
        for tb in range(4):
            s = sq[tb % 2]
            sk = ""
            if "a" not in sk:
                P.act(s[:], pa[tb][:, :], AF.Square)
            if "c" not in sk:
                P.cp(outs[i][:, tb * 512:(tb + 1) * 512], pa[tb][:, :], eng="dve")
            if "m" not in sk:
                P.mm(k.PB[tb][:, :], ones[:], s[:], start=(i == 0), stop=(i == n - 1))
    c2 = 9
    if c2 < 1:
        return
    for tb in range(4):
        r = rq[tb % 2]
        P.rpow(r[:], k.PB[tb][:, :], -0.5, scale=1.0 / nfeat, bias=EPS)
        for i in range(n):
            o = outs[i][:, tb * 512:(tb + 1) * 512]
            P.tt(o, o, r[:], ALU.mult)


def mla_phase(k, l):
    P = k.P
    SC = 96.0 ** -0.5
    with scope(k):
        cqn0 = P.sb("m_cqn0", [128, L], BF16, blk=512)
        cqn1 = P.sb("m_cqn1", [128, L], BF16, blk=512)
        ckvn = P.sb("m_ckvn", [128, L], BF16, blk=512)
        krope = P.sb("m_krope", [32, L], BF16, blk=512)
        cos2 = P.sb("m_cos2", [32, L], BF16, blk=512)
        sin2 = P.sb("m_sin2", [32, L], BF16, blk=512)
        wq = P.sb("m_wq", [128, 2, 6, 128], BF16)
        wk = P.sb("m_wk", [128, 6, 128], BF16)
        wv = P.sb("m_wv", [128, 6, 64], BF16)
        ones = P.sb("m_ones", [128, 128], F32)
        onesk = P.sb("m_onesk", [128, 128], F32)
        rmT = P.sb("m_rmT", [32, 32], F32)
        P.memset(ones[:], 1.0)
        P.memset(onesk[:], 1.0)
        P.memset(onesk[32:64, :], 0.0)
        P.dma(rmT[:], k.rmT_d[:])
        with scope(k):
            st = P.sb("m_st", [128, 2, 6, 128], F32)
            g = P.sb("m_g", [128, 4], F32)
            P.dma(st[:], k.wuq_d[l])
            P.dma(g[:, 0:2], k.gq_d[l])
            P.dma(g[:, 2:3], k.gkv_d[l])
            for kc in range(2):
                P.ts(wq[:, kc].rearrange("p a b -> p (a b)"), st[:, kc].rearrange("p a b -> p (a b)"),
                     g[:, kc:kc + 1], ALU.mult)
            st2 = P.sb("m_st2", [128, 6, 128], F32)
            P.dma(st2[:], k.wuk_d[l])
            P.ts(wk[:].rearrange("p a b -> p (a b)"), st2[:].rearrange("p a b -> p (a b)"), g[:, 2:3], ALU.mult)
            st3 = P.sb("m_st3", [128, 6, 64], F32)
            P.dma(st3[:], k.wuv_d[l])
            P.ts(wv[:].rearrange("p a b -> p (a b)"), st3[:].rearrange("p a b -> p (a b)"), g[:, 2:3], ALU.mult)
        if k.cut < 1:
            return
        with scope(k):
            latent_norm(k, l, ["cq0", "cq1"], 192, [cqn0, cqn1], ones)
            latent_norm(k, l, ["ckv"], 128, [ckvn], ones)
        if k.cut < 2:
            return
        with scope(k):
            invf = P.sb("m_invf", [32, 1], F32)
            P.dma(invf[:], k.invf_d[:])
            pa = proj(k, l, "kr")

            def rope_blk(tb):
                sl = slice(tb * 512, (tb + 1) * 512)
                posi = P.sb("m_posi%d" % tb, [32, 512], I32)
                y = P.sb("m_y%d" % tb, [32, 512], F32)
                yi = P.sb("m_yi%d" % tb, [32, 512], I32)
                fr = P.sb("m_fr%d" % tb, [32, 512], F32)
                kr = P.sb("m_kr%d" % tb, [32, 512], F32)
                t1 = P.sb("m_t1%d" % tb, [32, 512], F32)
                t2 = P.sb("m_t2%d" % tb, [32, 512], F32)
                P.dma(posi[:], k.pos_d[0:1, sl].partition_broadcast(32))
                P.cp(kr[:], pa[tb][0:32, :], eng="act")
                yield
                P.cp(y[:], posi[:])
                P.mm(k.PB[tb][0:32, :], rmT[:], kr[:])
                yield
                P.ts(y[:], y[:], invf[:, 0:1], ALU.mult)
                yield
                for off, dst in ((0.0, sin2), (0.25, cos2)):
                    if off != 0.0:
                        P.ts(y[:], y[:], off, ALU.add)
                        yield
                    P.cp(yi[:], y[:])
                    yield
                    P.cp(fr[:], yi[:])
                    yield
                    P.tt(fr[:], y[:], fr[:], ALU.subtract)
                    yield
                    P.act(dst[:, sl], fr[:], AF.Sin, scale=TWO_PI * (1.0 - 1e-6))
                    yield
                P.tt(t1[:], kr[:], cos2[:, sl], ALU.mult)
                P.tt(t2[:], k.PB[tb][0:32, :], sin2[:, sl], ALU.mult)
                yield
                P.tt(krope[:, sl], t1[:], t2[:], ALU.add)
                yield
            run_interleaved([rope_blk(tb) for tb in range(4)])
        if k.cut < 3:
            return
        kT = [P.sb("m_kT%d" % i, [128, L], BF16, blk=512) for i in range(2)]
        qT = [P.sb("m_qT%d" % i, [128, L], BF16, blk=512) for i in range(2)]
        Vh = [P.sb("m_V%d" % i, [128, NT, 96], BF16) for i in range(2)]
        pT = [P.sb("m_pT%d" % i, [128, 1024], BF16, blk=512) for i in range(2)]
        sq = [P.sb("m_sq%d" % i, [128, 512], F32) for i in range(2)]
        qr = [P.sb("m_qr%d" % i, [32, 512], F32) for i in range(2)]
        t1 = P.sb("m_t1b", [32, 512], F32)
        t2 = P.sb("m_t2b", [32, 512], F32)
        mrow = P.sb("m_mrow", [64, 512], F32)
        km4 = P.sb("m_km4", [128, 4], F32)
        kmax2 = P.sb("m_kmax2", [128, 1], F32)
        rden = [P.sb("m_rden%d" % i, [64, 512], F32) for i in range(2)]
        for i in range(2):
            P.memset(kT[i][32:64, :], 0.0)
            P.memset(kT[i][32:33, :], 1.0)
            P.memset(qT[i][32:64, :], 0.0)
            P.memset(Vh[i][:, :, 64:96], 1.0)
        kmx = [P.sb("m_kmx%d" % i, [128, 1], F32) for i in range(2)]

        def prep(h):
            kt_, qt_, vh_ = kT[h % 2], qT[h % 2], Vh[h % 2]
            kmax2_ = kmx[h % 2]
            P.cp(kt_[0:32, :], krope[:], eng="pool")
            for tb in range(4):
                sl = slice(tb * 512, (tb + 1) * 512)
                s = sq[tb % 2]
                P.mm(k.PB[2][:, :], wk[:, h, :], ckvn[:, sl])
                yield
                P.cp(kt_[64:128, sl], k.PB[2][64:128, :], eng="dve")
                yield
                P.tt(s[:], kt_[:, sl], kt_[:, sl], ALU.mult, eng="pool")
                yield
                yield
                P.mm(k.PB[3][:, :], onesk[:], s[:])
                yield
                P.red(km4[:, tb:tb + 1], k.PB[3][:, :], ALU.max)
                yield
            P.red(kmax2_[:], km4[:], ALU.max)
            for half in range(2):
                for j in range(8):
                    tt = half * 8 + j
                    P.mm(k.PB[2][:, j * 64:(j + 1) * 64], ckvn[:, tt * 128:(tt + 1) * 128], wv[:, h, :])
                yield
                P.cp(vh_[:, half * 8:(half + 1) * 8, 0:64], k.PB[2][:, :].rearrange("p (a b) -> p a b", a=8), eng="dve")
                yield
            for tb in range(4):
                sl = slice(tb * 512, (tb + 1) * 512)
                s = sq[tb % 2]
                q_ = qr[tb % 2]
                P.mm(k.PB[2][:, :], wq[:, 0, h, :], cqn0[:, sl], start=True, stop=False)
                P.mm(k.PB[2][:, :], wq[:, 1, h, :], cqn1[:, sl], start=False, stop=True)
                yield
                P.cp(qt_[64:128, sl], k.PB[2][64:128, :], eng="dve")
                P.cp(q_[:], k.PB[2][0:32, :], eng="dve")
                yield
                P.act(s[:], k.PB[2][:, :], AF.Square)
                yield
                P.mm(k.PB[3][:, :], ones[:], s[:])
                yield
                P.act(mrow[32:33, :], k.PB[3][32:33, :], AF.Sqrt, scale=kmax2_[32:33, 0:1])
                yield
                P.ts(qt_[32:33, sl], mrow[32:33, :], -1.0, ALU.mult)
                P.mm(k.PB[2][0:32, :], rmT[:], q_[:])
                P.tt(t1[:], q_[:], cos2[:, sl], ALU.mult, eng="pool")
                yield
                P.tt(t2[:], k.PB[2][0:32, :], sin2[:, sl], ALU.mult)
                yield
                P.tt(qt_[0:32, sl], t1[:], t2[:], ALU.add)
                yield

        def attn(h):
            kt_, qt_, vh_ = kT[h % 2], qT[h % 2], Vh[h % 2]
            pti = 0
            for qb in range(4):
                qs = slice(qb * 512, (qb + 1) * 512)
                O = k.PB[qb % 2]

                def s_pair(m_):
                    for j in range(2):
                        kt = 2 * m_ + j
                        P.mm(k.PAW[m_ % 2][:, j * 512:(j + 1) * 512], kt_[:, kt * 128:(kt + 1) * 128], qt_[:, qs])
                s_pair(0)
                s_pair(1)
                for m_ in range(NT // 2):
                    p_ = pT[pti % 2]
                    pti += 1
                    P.act(p_[:], k.PAW[m_ % 2][:, :], AF.Exp, scale=SC)
                    if m_ + 2 < NT // 2:
                        s_pair(m_ + 2)
                    for j in range(2):
                        kt = 2 * m_ + j
                        P.mm(O[0:96, :], vh_[:, kt, :], p_[:, j * 512:(j + 1) * 512], start=(kt == 0), stop=(kt == NT - 1))
                    yield
                rd = rden[qb % 2]
                P.rpow(rd[0:32, :], O[64:96, :], -1.0)
                P.rpow(rd[32:64, :], O[64:96, :], -1.0)
                ob = (h % 2) * 64
                P.tt(mixc(k, 3 + h // 2)[ob:ob + 64, qs], O[0:64, :], rd[:], ALU.mult)
                yield

        for _ in prep(0):
            pass
        mode = "il"
        for h in range(6):
            gens = [attn(h)]
            if h + 1 < 6:
                if mode == "il":
                    gens.append(prep(h + 1))
                elif mode == "seq":
                    run_interleaved(gens)
                    gens = [prep(h + 1)]
                elif mode == "noprep":
                    pass
            if mode == "noprep" and h > 0:
                gens = [attn(0)]
            run_interleaved(gens)


NCK = L // 64
NEG = -30000.0


def host_gdn(inp, b, m):
    cw = inp["gdn_conv"]
    o = np.zeros((DEPTH, 128, 9, 5), np.float32)
    for part in range(3):
        for p in range(3):
            o[:, :, part * 3 + p, :] = cw[:, :, part * 384 + p * 128: part * 384 + (p + 1) * 128].transpose(0, 2, 1)
    m["gconv"] = o
    gb = np.zeros((DEPTH, 128, 2), np.float32)
    for d in range(2):
        gb[:, d * 32:d * 32 + 6, 0] = inp["gdn_dt_bias"][:, d, :]
        gb[:, d * 32:d * 32 + 6, 1] = inp["gdn_a_log"][:, d, :]
    m["ggb"] = gb
    m["gng"] = np.ascontiguousarray(np.tile(inp["gdn_norm_g"], (1, 2)).reshape(DEPTH, 128, 1))
    sel = np.zeros((64, 6, 128), np.float32)
    for d in range(2):
        for p in range(3):
            sel[d * 32 + 2 * p, d * 3 + p, 0:64] = 1.0
            sel[d * 32 + 2 * p + 1, d * 3 + p, 64:128] = 1.0
    m["gsel"] = sel
    j = np.arange(64)[:, None]
    i = np.arange(64)[None, :]
    nm = np.zeros((128, 2, 64), np.float32)
    nm[:, 0, :] = np.tile(np.where(i > j, 0.0, NEG), (2, 1))
    nm[:, 1, :] = np.tile(np.where(i < j, 0.0, NEG), (2, 1))
    m["gnegm"] = nm
    m["gid2"] = np.ascontiguousarray(np.tile(np.eye(64, dtype=np.float32), (2, 1)))


def gdn_decl(k):
    nc = k.nc

    def din(name, shape, dt=F32):
        return nc.dram_tensor(name, list(shape), dt, kind="ExternalInput").ap()
    k.gconv_d = din("gconv", [DEPTH, 128, 9, 5])
    k.ggb_d = din("ggb", [DEPTH, 128, 2])
    k.gng_d = din("gng", [DEPTH, 128, 1])
    k.gsel_d = din("gsel", [64, 6, 128])
    k.gnegm_d = din("gnegm", [128, 2, 64])
    k.gid2_d = din("gid2", [128, 64])


def bc3(ap2, n):
    return ap2.unsqueeze(2).broadcast_to([ap2.shape[0], ap2.shape[1], n])


def bcm(ap2, n):
    return ap2.unsqueeze(1).broadcast_to([ap2.shape[0], n, ap2.shape[1]])


HS = (slice(0, 64), slice(64, 128))


def v3(ps, n=8):
    return ps[:, 0:n * 64].rearrange("p (a b) -> p a b", a=n)


def mm2(P, ps, c, lhsT, rhs, **kw):
    for hs in HS:
        P.mm(ps[hs, c * 64:(c + 1) * 64], lhsT[hs], rhs[hs], **kw)


def tr2(P, ps, c, in_, ident):
    for hs in HS:
        P.mm(ps[hs, c * 64:(c + 1) * 64], in_[hs], ident[hs, hs])


def neumann2(k, Nn, Rm, tmp, bank, id2, n8=8):
    P = k.P
    idb = bcm(id2[:, :], n8)
    tA, tB, tC, tD = tmp
    pa_, pb_, pc_ = bank
    for c in range(n8):
        tr2(P, pa_, c, Nn[:, c, :], k.identb)
    P.cp(tA[:], v3(pa_, n8), eng="act")
    P.tt(Rm[:], Nn[:], idb, ALU.add)
    yield
    cur, curT = Nn, tA
    targets = [(tB, tC), (tD, tA)]
    for lvl in range(1, 7):
        nxt, nxtT = targets[(lvl - 1) % 2]
        if lvl >= 2:
            for c in range(n8):
                mm2(P, pc_, c, curT[:, c, :], Rm[:, c, :])
            P.tt(Rm[:], Rm[:], v3(pc_, n8), ALU.add)
        if lvl <= 5:
            for c in range(n8):
                mm2(P, pb_, c, cur[:, c, :], curT[:, c, :])
            P.cp(nxtT[:], v3(pb_, n8), eng="act")
            if lvl < 5:
                for c in range(n8):
                    mm2(P, pa_, c, curT[:, c, :], cur[:, c, :])
                P.cp(nxt[:], v3(pa_, n8), eng="act")
        yield
        cur, curT = nxt, nxtT


def run_interleaved(gens, skew=0):
    gens = list(gens)
    for _ in range(skew):
        try:
            next(gens[0])
        except StopIteration:
            gens.pop(0)
            break
    while gens:
        for g in list(gens):
            try:
                next(g)
            except StopIteration:
                gens.remove(g)


def norm_pipe(k, n, src_fn, sqb, rnb, banks, bones, power, scale, bias, post_fn):
    P = k.P

    def pre(i):
        P.act(sqb[i % 2][:], src_fn(i), AF.Square)
        P.mm(banks[i % 2][:, :], bones[:], sqb[i % 2][:])

    def post(i):
        P.rpow(rnb[i % 2][:], banks[i % 2][:, :], power, scale=scale, bias=bias)
        post_fn(i, rnb[i % 2])
    pre(0)
    for i in range(n):
        if i + 1 < n:
            pre(i + 1)
        post(i)


def run_pipelined(chains, depth=2):
    active = []
    nxt = [0] * len(chains)

    def start(ci):
        if nxt[ci] < len(chains[ci]):
            active.append((ci, chains[ci][nxt[ci]](nxt[ci] % depth)))
            nxt[ci] += 1
    for ci in range(len(chains)):
        for _ in range(depth):
            start(ci)
    while active:
        for item in list(active):
            try:
                next(item[1])
            except StopIteration:
                active.remove(item)
                start(item[0])


def gdn_phase(k, l):
    P = k.P
    with scope(k):
        GC = P.sb("g_GC", [64, L], F32, blk=512)
        GP = [P.sb("g_GP%d" % p, [128, NCK, 4], F32) for p in range(3)]
        NBP = [P.sb("g_NBP%d" % p, [128, NCK, 2], F32) for p in range(3)]
        sel = P.sb("g_sel", [64, 6, 128], F32)
        negm = P.sb("g_negm", [128, 2, 64], F32)
        id2 = P.sb("g_id2", [128, 64], F32)
        cw = P.sb("g_cw", [128, 9, 5], F32)
        ng = P.sb("g_ng", [128, 1], F32)
        bones = P.sb("g_bones", [128, 128], F32)
        P.dma(sel[:], k.gsel_d[:])
        P.dma(negm[:], k.gnegm_d[:])
        P.dma(id2[:], k.gid2_d[:])
        P.dma(cw[:], k.gconv_d[l])
        P.dma(ng[:], k.gng_d[l])
        P.memset(bones[:], 0.0)
        P.memset(bones[0:64, 0:64], 1.0)
        P.memset(bones[64:128, 64:128], 1.0)
        with scope(k):
            GT = P.sb("g_GT", [128, L], F32, blk=512)
            m0 = P.sb("g_m0", [64, L], F32)
            gb = P.sb("g_gb", [128, 2], F32)
            negA = P.sb("g_negA", [128, 1], F32)
            P.dma(gb[:], k.ggb_d[l])
            P.act(negA[:], gb[:, 1:2], AF.Exp)
            P.ts(negA[:], negA[:], -1.0, ALU.mult)
            P.memset(m0[:], 1.0)
            P.memset(m0[:, 0:L:64], 0.0)
            pa = proj(k, l, "gab")
            for tb in range(4):
                sl = slice(tb * 512, (tb + 1) * 512)
                P.act(GT[0:64, sl], pa[tb][0:64, :], AF.Exp, bias=gb[0:64, 0:1])
                P.act(GT[64:128, sl], pa[tb][64:128, :], AF.Sigmoid)
            P.act(GT[0:64, :], GT[0:64, :], AF.Ln, bias=1.0)
            P.ts(GT[0:64, :], GT[0:64, :], negA[0:64, 0:1], ALU.mult)
            P.scan(GC[:, :], m0[:, :], GT[0:64, :], 0.0, ALU.mult, ALU.add)
            gc3 = GC[32:64, :].rearrange("p (a b) -> p a b", b=64)
            P.tt(m0[32:64, :].rearrange("p (a b) -> p a b", b=64), bc3(GC[32:64, 63:L:64], 64), gc3, ALU.subtract)
            P.tt(GC[32:64, :], m0[32:64, :], GT[32:64, :], ALU.add)
            for grp in range(4):
                g8 = slice(grp * 8, (grp + 1) * 8)
                for c in range(8):
                    ck = grp * 8 + c
                    cs = slice(c * 64, (c + 1) * 64)
                    for hs in HS:
                        P.mm(k.PB[2 * (grp % 2)][hs, cs], GC[:, ck * 64:(ck + 1) * 64], k.ident[0:64, 0:64])
                        P.mm(k.PB[2 * (grp % 2) + 1][hs, cs], GT[64:128, ck * 64:(ck + 1) * 64], k.ident[64:128, 64:128])
                n_ = 0
                for p in range(3):
                    for hf, hs in enumerate(HS):
                        h = 2 * p + hf
                        for q, ps in ((0, k.PB[2 * (grp % 2)]), (1, k.PB[2 * (grp % 2) + 1])):
                            src = v3(ps)[hs, :, h:h + 33:32]
                            P.cp(GP[p][hs, g8, 2 * q:2 * q + 2], src, eng=("act" if q else "dve"))
            for p in range(3):
                P.ts(NBP[p][:], GP[p][:, :, 2:4], -1.0, ALU.mult)
        for p in range(3):
            with scope(k):
                Q = P.sb("g_Q", [128, L], BF16, blk=512)
                K_ = P.sb("g_K", [128, L], BF16, blk=512)
                Kt = P.sb("g_Kt", [128, NCK, 64], BF16, blk=512)
                Vt = P.sb("g_Vt", [128, NCK, 64], BF16, blk=512)
                O = P.sb("g_O", [128, L], F32, blk=512)
                P.memset(O[:], 0.0, eng="pool")
                with scope(k):
                    xp = P.sb("g_xp", [128, L + 4], BF16)
                    Dg = P.sb("g_Dg", [128, 5, 128], BF16)
                    Vf = P.sb("g_Vf", [128, L], BF16, blk=512)
                    cv = P.sb("g_cv", [128, L], F32, blk=512)
                    sq = P.sb("g_sq", [128, 512], F32)
                    rn = P.sb("g_rn", [128, 512], F32)
                    sq2 = P.sb("g_sq2", [128, 512], F32)
                    rn2 = P.sb("g_rn2", [128, 512], F32)
                    P.memset(xp[:, 0:2], 0.0)
                    P.memset(xp[:, L + 2:L + 4], 0.0)
                    for part, nm, dst in ((0, "gq", Q), (1, "gk", K_), (2, "gv", Vf)):
                        pa = proj(k, l, "%s%d" % (nm, p))
                        for tb in range(4):
                            P.cp(xp[:, 2 + tb * 512:2 + (tb + 1) * 512], pa[tb][:, :], eng=("act" if tb % 2 else "dve"))
                        wi = part * 3 + p
                        for j in range(5):
                            P.ts(Dg[:, j, :], k.identb[:], cw[:, wi, j:j + 1], ALU.mult, eng=("pool" if j % 2 else "dve"))
                        for tb in range(4):
                            for j in range(5):
                                P.mm(k.PB[tb][:, :], Dg[:, j, :], xp[:, j + tb * 512:j + (tb + 1) * 512],
                                     start=(j == 0), stop=(j == 4))
                        for tb in range(4):
                            sl = slice(tb * 512, (tb + 1) * 512)
                            P.act((dst if part == 2 else cv)[:, sl], k.PB[tb][:, :], AF.Silu)
                        if part < 2:
                            def fin(tb, r, dst=dst):
                                P.tt(dst[:, tb * 512:(tb + 1) * 512], cv[:, tb * 512:(tb + 1) * 512], r[:], ALU.mult)
                            norm_pipe(k, 4, lambda tb: cv[:, tb * 512:(tb + 1) * 512], (sq, sq2), (rn, rn2),
                                      (k.PA[2], k.PA[3]), bones, -0.5, 64.0 if part == 0 else 1.0,
                                      64e-6 if part == 0 else 1e-6, fin)
                    for src, dstt in ((K_, Kt), (Vf, Vt)):
                        for grp in range(4):
                            ps = k.PB[grp % 2]
                            for c in range(8):
                                ck = grp * 8 + c
                                tr2(P, ps, c, src[:, ck * 64:(ck + 1) * 64], k.identb)
                            P.cp(dstt[:, grp * 8:(grp + 1) * 8, :], v3(ps), eng=("act" if grp % 2 else "dve"))
                with scope(k):
                    T = dict(GC=GC, GP=GP[p], NBP=NBP[p], sel=sel, negm=negm, id2=id2, Q=Q, K=K_, Kt=Kt, Vt=Vt, O=O)
                    run_pipelined([gdn_chain(k, p, d, T) for d in range(2)], depth=1)
                with scope(k):
                    sqo = [P.sb("g_osq%d" % i, [128, 512], F32) for i in range(2)]
                    rno = [P.sb("g_orn%d" % i, [128, 512], F32) for i in range(2)]

                    def fin_o(tb, r):
                        sl = slice(tb * 512, (tb + 1) * 512)
                        P.tt(r[:], O[:, sl], r[:], ALU.mult)
                        P.ts(mixc(k, p)[:, sl], r[:], ng[:, 0:1], ALU.mult)
                    norm_pipe(k, 4, lambda tb: O[:, tb * 512:(tb + 1) * 512], sqo, rno, (k.PB[2], k.PB[3]), bones,
                              -0.5, 1.0 / 64, EPS, fin_o)


def gdn_chain(k, p, d, T):
    P = k.P
    GC, GP, NBP, sel, negm, id2, Q, K_, Kt, Vt, O = (T[n] for n in ("GC", "GP", "NBP", "sel", "negm", "id2", "Q", "K", "Kt", "Vt", "O"))
    B = k.PA if d == 0 else k.PB
    tag = "g%d_" % d
    names = ("CB", "EI", "QG", "Rm", "U0", "WT", "BW", "KD", "GK", "AcT", "Sg", "Ug", "nA", "nB", "nC", "nD", "Nb", "PTb", "Sgb")
    f32n = ("CB", "EI", "AcT", "Sg")
    NSET = 1
    GG = [{n: P.sb(tag + "%d" % s_ + n, [128, 8, 64], F32 if n in f32n else BF16) for n in names} for s_ in range(NSET)]
    for s_ in range(NSET):
        GG[s_]["gend"] = P.sb(tag + "gend%d" % s_, [128, 8], F32)
        GG[s_]["kds"] = P.sb(tag + "kds%d" % s_, [128, 8], F32)
    gam = P.sb(tag + "gam", [128, NCK], F32)
    shared = {"scan": 0}
    Scar = P.sb(tag + "Scar", [128, 64], F32)
    e_ = 63 if d == 0 else 0
    idb = bcm(id2[:, :], 8)

    def f2(t):
        return t[:].rearrange("p a b -> p (a b)")
    P.memset(Scar[:], 0.0)
    P.act(gam[:], GP[:, :, d], AF.Exp)
    gorder = list(range(4)) if d == 0 else list(range(3, -1, -1))

    def group(gi, grp, G):
        Nb, PTb, Sgb, gend, kds = G["Nb"], G["PTb"], G["Sgb"], G["gend"], G["kds"]
        sl = slice(grp * 512, (grp + 1) * 512)
        g8 = slice(grp * 8, (grp + 1) * 8)
        cj = GP[:, g8, d]
        nb = NBP[:, g8, d]
        CB, EI, QG, Rm, U0, WT, BW, KD, GK, AcT, Sg, Ug = (G[n] for n in names[:12])
        P.mm(B[0][:, :], sel[:, d * 3 + p, :], GC[:, sl])
        P.cp(f2(CB), B[0][:, :], eng="act")
        P.act(f2(EI), f2(CB), AF.Exp)
        P.cp(gend[:], EI[:, :, e_], eng="pool")
        P.tt(f2(QG), f2(EI), Q[:, sl], ALU.mult)
        P.tt(kds[:], CB[:, :, e_], cj, ALU.subtract)
        P.act(kds[:], kds[:], AF.Exp)
        P.tt(GK[:], Kt[:, g8, :], bc3(gam[:, g8], 64), ALU.mult, eng="pool")
        P.tt(KD[:], Kt[:, g8, :], bc3(kds[:], 64), ALU.mult, eng="pool")
        P.tt(CB[:], CB[:], bc3(cj, 64), ALU.subtract)
        P.tt(CB[:], CB[:], bcm(negm[:, d, :], 8), ALU.add)
        P.act(f2(CB), f2(CB), AF.Exp)
        yield
        P.tt(EI[:], CB[:], idb, ALU.add)
        for c in range(8):
            cs = slice((grp * 8 + c) * 64, (grp * 8 + c + 1) * 64)
            mm2(P, B[0], c, K_[:, cs], Q[:, cs])
        P.tt(PTb[:], EI[:], v3(B[0]), ALU.mult)
        for c in range(8):
            cs = slice((grp * 8 + c) * 64, (grp * 8 + c + 1) * 64)
            mm2(P, B[1], c, K_[:, cs], K_[:, cs])
        P.tt(CB[:], CB[:], v3(B[1]), ALU.mult)
        P.tt(Nb[:], CB[:], bc3(nb, 64), ALU.mult)
        yield
        for _ in neumann2(k, Nb, Rm, (G["nA"], G["nB"], G["nC"], G["nD"]), (B[0], B[1], B[2]), id2):
            yield
        for c in range(8):
            mm2(P, B[2], c, Rm[:, c, :], GK[:, c, :])
        P.tt(BW[:], v3(B[2]), bc3(nb, 64), ALU.mult)
        for c in range(8):
            mm2(P, B[0], c, Rm[:, c, :], Vt[:, grp * 8 + c, :])
        P.tt(U0[:], v3(B[0]), bc3(GP[:, g8, 2 + d], 64), ALU.mult)
        for c in range(8):
            mm2(P, B[1], c, GK[:, c, :], Rm[:, c, :])
        P.cp(WT[:], v3(B[1]), eng="act")
        yield
        for c in range(8):
            mm2(P, B[3], c, BW[:, c, :], KD[:, c, :])
        P.tt(AcT[:], idb, bc3(gend[:], 64), ALU.mult, eng="pool")
        P.tt(AcT[:], AcT[:], v3(B[3]), ALU.add)
        yield
        while shared["scan"] != gi:
            yield
        corder = range(8) if d == 0 else range(7, -1, -1)
        prev = Scar[:]
        for n, c in enumerate(corder):
            P.cp(Sg[:, c, :], prev, eng="pool") if n == 0 else None
            ps = B[2 + n % 2]
            mm2(P, ps, 0, AcT[:, c, :], Sg[:, c, :], start=True, stop=False)
            mm2(P, ps, 0, KD[:, c, :], U0[:, c, :], start=False, stop=True)
            last = (n == 7)
            dst = Scar[:] if last else Sg[:, corder[n + 1], :]
            P.cp(dst, ps[:, 0:64], eng="act")
            yield
        shared["scan"] = gi + 1
        P.cp(Sgb[:], Sg[:], eng="act")
        for c in range(8):
            mm2(P, B[0], c, WT[:, c, :], Sgb[:, c, :])
        P.tt(CB[:], v3(B[0]), bc3(nb, 64), ALU.mult)
        P.tt(Ug[:], CB[:], U0[:], ALU.add)
        yield
        for c in range(8):
            mm2(P, B[1], c, Sgb[:, c, :], QG[:, c, :], start=True, stop=False)
            mm2(P, B[1], c, Ug[:, c, :], PTb[:, c, :], start=False, stop=True)
        P.tt(O[:, sl], O[:, sl], B[1][:, :], ALU.add)
        yield
    return [(lambda slot, gi=gi, grp=grp: group(gi, grp, GG[slot])) for gi, grp in enumerate(gorder)]


RSKEW = 0
RW_EPS = 64e-5
DEC = float(np.exp(-0.5))
GS = 8
NG = NCK // GS


def host_rwkv(inp, b, m):
    mu = inp["rwkv_mu"]
    o = np.zeros((DEPTH, 128, 8, 2), np.float32)
    for part in range(3):
        for p in range(2):
            o[:, :, part * 2 + p, :] = mu[:, :, part * 256 + p * 128: part * 256 + (p + 1) * 128].transpose(0, 2, 1)
    o[:, 0:64, 6, :] = mu[:, :, 768:832].transpose(0, 2, 1)
    o[:, 0:64, 7, :] = mu[:, :, 832:896].transpose(0, 2, 1)
    m["rmu"] = o

    def pp(a):
        if a.ndim == 2:
            return np.ascontiguousarray(a.reshape(DEPTH, 2, 128).transpose(0, 2, 1))
        return np.ascontiguousarray(a.reshape(DEPTH, 2, 2, 128).transpose(0, 3, 1, 2))
    pv = np.zeros((DEPTH, 128, 7, 2), np.float32)
    pv[:, :, 0:2, :] = pp(inp["rwkv_w0"])
    pv[:, :, 2:4, :] = pp(inp["rwkv_a0"])
    pv[:, :, 4, :] = pp(inp["rwkv_k_k"])
    pv[:, :, 5, :] = pp(inp["rwkv_k_a"])
    pv[:, :, 6, :] = pp(inp["rwkv_r_k"].reshape(DEPTH, 256))
    m["rpv"] = pv
    ln = np.zeros((DEPTH, 128, 2, 2), np.float32)
    ln[:, :, 0, :] = pp(inp["rwkv_ln_g"])
    ln[:, :, 1, :] = pp(inp["rwkv_ln_b"])
    m["rln"] = ln
    m["rw2"] = np.ascontiguousarray(inp["rwkv_w2"].transpose(0, 2, 1, 3))
    m["ra2"] = np.ascontiguousarray(inp["rwkv_a2"].transpose(0, 2, 1, 3))
    s_ = np.arange(64)[:, None]
    t_ = np.arange(64)[None, :]
    msk = np.zeros((128, 2, 4, 64), np.float32)
    msk[:, 0, 0, :] = np.tile((t_ > s_), (2, 1))
    msk[:, 0, 1, :] = np.tile((t_ >= s_), (2, 1))
    msk[:, 1, 0, :] = np.tile((t_ < s_), (2, 1))
    msk[:, 1, 1, :] = np.tile((t_ <= s_), (2, 1))
    msk[:, :, 2:4, :] = -msk[:, :, 0:2, :]
    m["rmsk"] = msk


def rwkv_decl(k):
    nc = k.nc

    def din(name, shape, dt=F32):
        return nc.dram_tensor(name, list(shape), dt, kind="ExternalInput").ap()
    k.rmu_d = din("rmu", [DEPTH, 128, 8, 2])
    k.rpv_d = din("rpv", [DEPTH, 128, 7, 2])
    k.rln_d = din("rln", [DEPTH, 128, 2, 2])
    k.rw2_d = din("rw2", [DEPTH, 64, 2, 256])
    k.ra2_d = din("ra2", [DEPTH, 64, 2, 256])
    k.rmsk_d = din("rmsk", [128, 2, 4, 64])


def rwkv_phase(k, l):
    P = k.P
    with scope(k):
        mu = P.sb("r_mu", [128, 8, 3], F32)
        pv = P.sb("r_pv", [128, 7, 2], F32)
        omka = P.sb("r_omka", [128, 2], F32)
        hrk = P.sb("r_hrk", [128, 2], F32)
        ln = P.sb("r_ln", [128, 2, 2], F32)
        w2 = P.sb("r_w2", [64, 2, 256], BF16)
        a2 = P.sb("r_a2", [64, 2, 256], BF16)
        msk = P.sb("r_msk", [128, 2, 4, 64], F32)
        id2 = P.sb("r_id2", [128, 64], F32)
        bones = P.sb("r_bones", [128, 128], F32)
        m0 = P.sb("r_m0", [128, GS * 64], F32)
        twd = P.sb("r_twd", [64, L], BF16, blk=512)
        adx = P.sb("r_adx", [64, L], BF16, blk=512)
        sh32 = P.sb("r_sh32", [128, L], F32, blk=512)
        xp = P.sb("r_xp", [128, L + 2], F32)
        P.dma(mu[:, :, 0:2], k.rmu_d[l])
        P.dma(pv[:], k.rpv_d[l])
        P.dma(ln[:], k.rln_d[l])
        with scope(k):
            w2f = P.sb("r_w2f", [64, 2, 256], F32)
            a2f = P.sb("r_a2f", [64, 2, 256], F32)
            P.dma(w2f[:], k.rw2_d[l])
            P.dma(a2f[:], k.ra2_d[l])
            P.cp(w2[:], w2f[:], eng="act")
            P.cp(a2[:], a2f[:], eng="act")
        P.dma(msk[:], k.rmsk_d[:])
        P.dma(id2[:], k.gid2_d[:])
        P.memset(bones[:], 0.0)
        P.memset(bones[0:64, 0:64], 1.0)
        P.memset(bones[64:128, 64:128], 1.0)
        P.memset(m0[:], 1.0)
        P.memset(m0[:, 0:GS * 64:64], 0.0)
        P.memset(xp[:, 0:1], 0.0)
        P.memset(xp[:, L + 1:L + 2], 0.0)
        P.tt(mu[:, :, 2], mu[:, :, 0], mu[:, :, 1], ALU.add)
        P.ts(mu[:, :, 2], mu[:, :, 2], -1.0, ALU.mult, 1.0, ALU.add)
        P.ts(omka[:], pv[:, 5, :], -1.0, ALU.mult, 1.0, ALU.add)
        P.ts(hrk[:], pv[:, 6, :], 0.5, ALU.mult)

        def shifted(name, ci, dst, np_=128, fn=None):
            pa = proj(k, l, name, alt=(ci % 2 == 1))
            for tb in range(4):
                P.cp(xp[0:np_, 1 + tb * 512:1 + (tb + 1) * 512], pa[tb][0:np_, :], eng=("act" if tb % 2 else "dve"))
            t_ = sh32[0:np_, :]
            P.ts(t_, xp[0:np_, 1:L + 1], mu[0:np_, ci, 2:3], ALU.mult)
            P.stt(t_, xp[0:np_, 0:L], mu[0:np_, ci, 0:1], t_, ALU.mult, ALU.add)
            if fn is None:
                P.stt(dst[:], xp[0:np_, 2:L + 2], mu[0:np_, ci, 1:2], t_, ALU.mult, ALU.add)
            else:
                P.stt(t_, xp[0:np_, 2:L + 2], mu[0:np_, ci, 1:2], t_, ALU.mult, ALU.add)
                P.act(dst[:], t_, fn)

        shifted("rwd", 6, twd, 64, AF.Tanh)
        shifted("rad", 7, adx, 64)
        R_ = P.sb("r_R", [128, L], BF16, blk=512)
        KX = P.sb("r_KX", [128, L], BF16, blk=512)
        V_ = P.sb("r_V", [128, L], BF16, blk=512)
        KK = P.sb("r_KK", [128, L], BF16, blk=512)
        Vt = P.sb("r_Vt", [128, NCK, 64], BF16, blk=512)
        KS = P.sb("r_KS", [128, L], F32, blk=512)
        Y = xp[:, 1:L + 1]
        sq = P.sb("r_sq", [128, 512], F32)
        rn = P.sb("r_rn", [128, 512], F32)
        CH = [rwkv_tiles(k, e) for e in range(2)]

        class _V:
            def __init__(s_, t):
                s_.t = t

            def __getitem__(s_, key):
                return s_.t[:].rearrange("p a b -> p (a b)")[key]
        sq2, rn2 = _V(CH[0]["lw"]), _V(CH[0]["a"])
        for p in range(2):
            shifted("rr%d" % p, 0 + p, R_)
            shifted("rk%d" % p, 2 + p, KX)
            shifted("rv%d" % p, 4 + p, V_)
            P.ts(sh32[:], KX[:], pv[:, 4, p:p + 1], ALU.mult)

            def fin_k(tb, r):
                P.tt(KK[:, tb * 512:(tb + 1) * 512], sh32[:, tb * 512:(tb + 1) * 512], r[:], ALU.mult)
            norm_pipe(k, 4, lambda tb: sh32[:, tb * 512:(tb + 1) * 512], (sq, sq2), (rn, rn2), (k.PB[2], k.PB[3]),
                      bones, -0.5, 1.0, 1e-6, fin_k)
            for grp in range(4):
                ps = k.PB[grp % 2]
                for c in range(8):
                    ck = grp * 8 + c
                    tr2(P, ps, c, V_[:, ck * 64:(ck + 1) * 64], k.identb)
                P.cp(Vt[:, grp * 8:(grp + 1) * 8, :], v3(ps), eng=("act" if grp % 2 else "dve"))
            P.memset(xp[:, 1:L + 1], 0.0, eng="pool")
            P.memset(KS[:], 0.0, eng="pool")
            T = dict(pv=pv, omka=omka, w2=w2, a2=a2, msk=msk, id2=id2, m0=m0, twd=twd, adx=adx,
                     R=R_, KX=KX, KK=KK, Vt=Vt, KS=KS, Y=Y)
            run_interleaved([rwkv_chain(k, p, e, T, CH[e]) for e in range(2)], skew=RSKEW)
            for tb in range(4):
                sl = slice(tb * 512, (tb + 1) * 512)
                P.mm(k.PB[tb % 2][:, :], bones[:], Y[:, sl])
                P.stt(Y[:, sl], k.PB[tb % 2][:, :], -1.0 / 64, Y[:, sl], ALU.mult, ALU.add)

            def fin_y(tb, r):
                sl = slice(tb * 512, (tb + 1) * 512)
                P.tt(Y[:, sl], Y[:, sl], r[:], ALU.mult)
                P.ts(Y[:, sl], Y[:, sl], ln[:, 0, p:p + 1], ALU.mult, ln[:, 1, p:p + 1], ALU.add)
            norm_pipe(k, 4, lambda tb: Y[:, tb * 512:(tb + 1) * 512], (sq, sq2), (rn, rn2), (k.PB[2], k.PB[3]),
                      bones, -0.5, 1.0 / 64, RW_EPS, fin_y)
            for tb in range(4):
                sl = slice(tb * 512, (tb + 1) * 512)
                s_ = (sq, sq2)[tb % 2]
                r_ = (rn, rn2)[tb % 2]
                P.tt(s_[:], R_[:, sl], KS[:, sl], ALU.mult)
                P.ts(s_[:], s_[:], hrk[:, p:p + 1], ALU.mult, eng="pool")
                P.mm(k.PB[tb % 2][:, :], bones[:], s_[:])
                P.tt(r_[:], k.PB[tb % 2][:, :], V_[:, sl], ALU.mult)
                P.tt(mixc(k, 6 + p)[:, sl], Y[:, sl], r_[:], ALU.add)


RW_F32 = ("lw", "a", "km", "b", "cl", "e1", "e2", "dend", "AcT", "Tg")
RW_BF16 = ("kap", "rt", "kt_", "bt_", "ke", "be", "kapT", "keT", "nbeT", "N", "Akv", "Brk", "nBrb", "Rm",
           "nA", "nB", "nC", "nD", "X0", "P0", "WkT", "Wk", "Tgb", "Pg")


def rwkv_tiles(k, e):
    P = k.P
    G = {n: P.sb("r%d_%s" % (e, n), [128, GS, 64], F32) for n in RW_F32}
    for n in RW_BF16:
        G[n] = P.sb("r%d_%s" % (e, n), [128, GS, 64], BF16)
    G["gC"] = P.sb("r%d_gC" % e, [128, GS], F32)
    G["Tcar"] = P.sb("r%d_Tcar" % e, [128, 64], F32)
    return G


def rwkv_chain(k, p, e, T, G):
    P = k.P
    pv, omka, w2, a2, msk, id2, m0, twd, adx, R_, KX, KK, Vt, KS, Y = (T[n] for n in (
        "pv", "omka", "w2", "a2", "msk", "id2", "m0", "twd", "adx", "R", "KX", "KK", "Vt", "KS", "Y"))
    B = k.PA if e == 0 else k.PB
    W = GS * 64
    e_ = 63 if e == 0 else 0
    idb = bcm(id2[:, :], GS)
    gC, Tcar = G["gC"], G["Tcar"]

    def f2(t):
        return t[:].rearrange("p a b -> p (a b)")

    def w3(ps):
        return v3(ps, GS)
    P.memset(Tcar[:], 0.0)
    gorder = range(NG) if e == 0 else range(NG - 1, -1, -1)
    pc = slice(p * 128, (p + 1) * 128)
    for grp in gorder:
        sl = slice(grp * W, (grp + 1) * W)
        c0 = grp * GS
        P.mm(B[0][:, 0:W], w2[:, e, pc], twd[:, sl])
        P.mm(B[1][:, 0:W], a2[:, e, pc], adx[:, sl])
        P.act(f2(G["lw"]), B[0][:, 0:W], AF.Sigmoid, bias=pv[:, 0 + e, p:p + 1])
        P.act(f2(G["a"]), B[1][:, 0:W], AF.Sigmoid, bias=pv[:, 2 + e, p:p + 1])
        P.ts(f2(G["km"]), f2(G["a"]), pv[:, 5, p:p + 1], ALU.mult, omka[:, p:p + 1], ALU.add)
        P.tt(f2(G["km"]), f2(G["km"]), KX[:, sl], ALU.mult)
        P.tt(f2(G["b"]), f2(G["a"]), KK[:, sl], ALU.mult)
        P.tt(KS[:, sl], KS[:, sl], f2(G["km"]), ALU.add, eng="pool")
        P.scan(f2(G["cl"]), m0[:], f2(G["lw"]), 0.0, ALU.mult, ALU.add)
        if e == 1:
            P.tt(G["e1"][:], bc3(G["cl"][:, :, 63], 64), G["cl"][:], ALU.subtract)
            P.tt(G["cl"][:], G["e1"][:], G["lw"][:], ALU.add)
        yield
        P.act(G["e1"][:], G["cl"][:], AF.Exp, scale=-DEC)
        P.act(G["e2"][:], G["cl"][:], AF.Exp, scale=DEC)
        P.tt(f2(G["rt"]), f2(G["e1"]), R_[:, sl], ALU.mult)
        P.tt(G["kt_"][:], G["e2"][:], G["km"][:], ALU.mult)
        P.tt(G["bt_"][:], G["e2"][:], G["b"][:], ALU.mult)
        P.tt(G["dend"][:], G["cl"][:], G["lw"][:], ALU.subtract)
        P.act(G["dend"][:], G["dend"][:], AF.Exp, scale=-DEC)
        P.tt(f2(G["kap"]), f2(G["dend"]), KK[:, sl], ALU.mult)
        P.cp(gC[:], G["e1"][:, :, e_], eng="pool")
        P.tt(G["dend"][:], bc3(G["cl"][:, :, e_], 64), G["cl"][:], ALU.subtract)
        P.act(G["dend"][:], G["dend"][:], AF.Exp, scale=-DEC)
        P.tt(G["ke"][:], G["dend"][:], G["km"][:], ALU.mult)
        P.tt(G["be"][:], G["dend"][:], G["b"][:], ALU.mult, eng="pool")
        yield
        for src, dst, sc in ((G["kap"], G["kapT"], 1.0), (G["ke"], G["keT"], 1.0), (G["be"], G["nbeT"], -1.0)):
            ps = B[0] if sc == 1.0 and src is G["kap"] else (B[1] if sc == 1.0 else B[2])
            for c in range(GS):
                tr2(P, ps, c, src[:, c, :], k.identb)
            if sc == 1.0:
                P.cp(dst[:], w3(ps), eng="act")
            else:
                P.ts(dst[:], w3(ps), -1.0, ALU.mult)
        yield
        for c in range(GS):
            mm2(P, B[0], c, G["bt_"][:, c, :], G["kap"][:, c, :])
            mm2(P, B[1], c, G["kt_"][:, c, :], G["kap"][:, c, :])
            mm2(P, B[2], c, G["kt_"][:, c, :], G["rt"][:, c, :])
            mm2(P, B[3], c, G["bt_"][:, c, :], G["rt"][:, c, :])
        ms = bcm(msk[:, e, 0, :], GS)
        mi = bcm(msk[:, e, 1, :], GS)
        nms = bcm(msk[:, e, 2, :], GS)
        nmi = bcm(msk[:, e, 3, :], GS)
        P.tt(G["N"][:], w3(B[0]), nms, ALU.mult)
        P.tt(G["Akv"][:], w3(B[1]), ms, ALU.mult)
        P.tt(G["Brk"][:], w3(B[2]), mi, ALU.mult)
        P.tt(G["nBrb"][:], w3(B[3]), nmi, ALU.mult)
        yield
        for _ in neumann2(k, G["N"], G["Rm"], (G["nA"], G["nB"], G["nC"], G["nD"]), (B[0], B[1], B[2]), id2, GS):
            yield
        for c in range(GS):
            mm2(P, B[0], c, G["Akv"][:, c, :], Vt[:, c0 + c, :])
        P.cp(G["X0"][:], w3(B[0]), eng="act")
        yield
        for c in range(GS):
            mm2(P, B[0], c, G["Rm"][:, c, :], G["X0"][:, c, :])
            mm2(P, B[1], c, G["kapT"][:, c, :], G["Rm"][:, c, :])
            mm2(P, B[2], c, G["Rm"][:, c, :], G["kapT"][:, c, :])
        P.cp(G["P0"][:], w3(B[0]), eng="act")
        P.cp(G["WkT"][:], w3(B[1]), eng="dve")
        P.cp(G["Wk"][:], w3(B[2]), eng="act")
        yield
        for c in range(GS):
            mm2(P, B[3], c, G["Wk"][:, c, :], G["nbeT"][:, c, :])
        P.tt(G["AcT"][:], idb, bc3(gC[:], 64), ALU.mult, eng="pool")
        P.tt(G["AcT"][:], G["AcT"][:], w3(B[3]), ALU.add)
        yield
        corder = list(range(GS)) if e == 0 else list(range(GS - 1, -1, -1))
        Tg = G["Tg"]
        for n, c in enumerate(corder):
            if n == 0:
                P.cp(Tg[:, c, :], Tcar[:], eng="pool")
            ps = B[2 + n % 2]
            mm2(P, ps, 0, G["AcT"][:, c, :], Tg[:, c, :], start=True, stop=False)
            mm2(P, ps, 0, G["keT"][:, c, :], Vt[:, c0 + c, :], start=False, stop=False)
            mm2(P, ps, 0, G["nbeT"][:, c, :], G["P0"][:, c, :], start=False, stop=True)
            dst = Tcar[:] if n == GS - 1 else Tg[:, corder[n + 1], :]
            P.cp(dst, ps[:, 0:64], eng="act")
            yield
        P.cp(G["Tgb"][:], Tg[:], eng="act")
        for c in range(GS):
            mm2(P, B[0], c, G["WkT"][:, c, :], G["Tgb"][:, c, :])
        P.tt(G["Pg"][:], w3(B[0]), G["P0"][:], ALU.add)
        yield
        for c in range(GS):
            mm2(P, B[1], c, Vt[:, c0 + c, :], G["Brk"][:, c, :], start=True, stop=False)
            mm2(P, B[1], c, G["Tgb"][:, c, :], G["rt"][:, c, :], start=False, stop=False)
            mm2(P, B[1], c, G["Pg"][:, c, :], G["nBrb"][:, c, :], start=False, stop=True)
        P.tt(Y[:, sl], Y[:, sl], B[1][:, 0:W], ALU.add)
        yield


_CACHE = {}


def kernel(**inputs):
    inp = {k_: np.asarray(v) for k_, v in inputs.items()}
    if "k" not in _CACHE:
        _CACHE["k"] = build()
    k = _CACHE["k"]
    B = inp["x"].shape[0]
    base = host_inputs(inp, 0)
    in_maps = []
    for b in range(B):
        m = dict(base)
        m["x"] = np.ascontiguousarray(inp["x"][b], dtype=np.float32)
        m["pos"] = np.ascontiguousarray(inp["positions"][b].reshape(1, L).astype(np.int32))
        in_maps.append(m)
    res = run_bass_kernel_spmd(k.nc, in_maps, core_ids=list(range(B)))
    return np.stack([np.asarray(r["out"], dtype=np.float32) for r in res.results], axis=0)
```

```python
import numpy as np
import concourse.bass as bass
import concourse.mybir as mybir
from concourse.bass_utils import run_bass_kernel_spmd
from contextlib import ExitStack

F32 = mybir.dt.float32
BF16 = mybir.dt.bfloat16
I32 = mybir.dt.int32
AF = mybir.ActivationFunctionType
ALU = mybir.AluOpType
AX = mybir.AxisListType
DTSIZE = {F32: 4, BF16: 2, I32: 4}


class _Op:
    __slots__ = ("eng", "emit", "deps", "idx", "needed", "sigval", "dsem", "dval", "isdma")

    def __init__(self, eng, emit, isdma=False):
        self.eng = eng
        self.emit = emit
        self.deps = []
        self.idx = -1
        self.needed = False
        self.sigval = 0
        self.dsem = None
        self.dval = 0
        self.isdma = isdma


class _Blk:
    __slots__ = ("w", "r")

    def __init__(self):
        self.w = None
        self.r = {}


class Prog:
    ENGS = ("pe", "dve", "act", "pool", "sp")
    NDMA = 48
    NHW = 32

    def __init__(self, nc, stack):
        self.nc = nc
        self.stack = stack
        self.ops = {e: [] for e in self.ENGS}
        self.track = {}
        self.seen = {e: {} for e in self.ENGS}
        self.seen_dma = {e: set() for e in self.ENGS}
        self.dma_last = [None] * self.NDMA
        self.dma_uses = [0] * self.NDMA
        self.dma_rr = 0
        self.dma_rr_sw = 0
        self.ndma_ops = 0
        self.untracked = set()
        self.out_dmas = []
        self.dma_pending = []
        self.last_compute = {}

    def sb(self, name, shape, dtype=F32, blk=None):
        self.uid = getattr(self, "uid", 0) + 1
        name = "s%d_%s" % (self.uid, name)
        t = self.stack.enter_context(self.nc.sbuf_tensor(name, list(shape), dtype))
        self._register(name, shape, dtype, blk)
        return t

    def ps(self, name, shape=(128, 512), dtype=F32, blk=None):
        self.uid = getattr(self, "uid", 0) + 1
        name = "p%d_%s" % (self.uid, name)
        t = self.stack.enter_context(self.nc.psum_tensor(name, list(shape), dtype))
        self._register(name, shape, dtype, blk)
        return t

    def _register(self, name, shape, dtype, blk):
        row = int(np.prod(shape[1:])) * DTSIZE[dtype]
        bb = row if blk is None else blk * DTSIZE[dtype]
        nb = (row + bb - 1) // bb
        self.track[name] = (bb, row, [_Blk() for _ in range(nb)])

    def dram_track(self, name, total_bytes, blk_bytes):
        nb = (total_bytes + blk_bytes - 1) // blk_bytes
        self.track[name] = (blk_bytes, -1, [_Blk() for _ in range(nb)])

    def _blocks(self, ap):
        name = ap.tensor.name
        if name not in self.track:
            return ()
        bb, row, blks = self.track[name]
        if len(blks) == 1:
            return blks
        ds = DTSIZE[ap.dtype]
        pat = ap.ap
        if row < 0:
            lo = hi = ap.offset
            for step, cnt in pat:
                ext = step * (cnt - 1)
                if ext < 0:
                    lo += ext
                else:
                    hi += ext
            return blks[(lo * ds) // bb:(hi * ds) // bb + 1]
        rowel = row // ds
        foff = ap.offset % rowel
        lo = hi = foff
        for step, cnt in pat[1:]:
            ext = step * (cnt - 1)
            if ext < 0:
                lo += ext
            else:
                hi += ext
        b0 = (lo * ds) // bb
        b1 = (hi * ds) // bb
        return blks[b0:b1 + 1]

    def _dep(self, x, y):
        if y is None or y is x:
            return
        e = x.eng
        if y.isdma:
            if id(y) in self.seen_dma[e]:
                return
            self.seen_dma[e].add(id(y))
            x.deps.append(y)
            return
        if y.eng == "pe" and e == "pe":
            return
        if y.idx <= self.seen[e].get(y.eng, -1):
            return
        self.seen[e][y.eng] = y.idx
        y.needed = True
        x.deps.append(y)

    def add(self, eng, emit, reads=(), writes=(), isdma=False):
        x = _Op(eng, emit, isdma)
        x.idx = len(self.ops[eng])
        rb = []
        for ap in reads:
            if ap is None or isinstance(ap, (int, float)):
                continue
            rb.extend(self._blocks(ap))
        wb = []
        for ap in writes:
            wb.extend(self._blocks(ap))
        for ap in reads:
            if ap is None or isinstance(ap, (int, float)) or not ap.tensor.name.startswith("p"):
                continue
            for b in self._blocks(ap):
                for key, y in b.r.items():
                    if key != eng:
                        self._dep(x, y)
        for b in rb:
            self._dep(x, b.w)
        for b in wb:
            self._dep(x, b.w)
            for y in b.r.values():
                self._dep(x, y)
        if isdma:
            if eng == "pool":
                s = self.NHW + self.dma_rr_sw
                self.dma_rr_sw = (self.dma_rr_sw + 1) % (self.NDMA - self.NHW)
            else:
                s = self.dma_rr
                self.dma_rr = (self.dma_rr + 1) % self.NHW
            self._dep(x, self.dma_last[s])
            self.dma_last[s] = x
            self.dma_uses[s] += 1
            x.dsem = s
            x.dval = 16 * self.dma_uses[s]
            self.ndma_ops += 1
        key = id(x) if isdma else eng
        for b in rb:
            b.r[key] = x
        for b in wb:
            b.w = x
            b.r = {}
        self.ops[eng].append(x)
        if isdma:
            self.dma_pending.append(x)
        else:
            self.last_compute[eng] = x
        return x

    def barrier(self):
        lasts = dict(self.last_compute)
        pend = list(self.dma_pending)
        self.dma_pending = []
        for e in self.ENGS:
            b = _Op(e, None)
            b.idx = len(self.ops[e])
            for e2, y in lasts.items():
                if e2 == e and e == "pe":
                    continue
                self._dep(b, y)
            for y in pend:
                self._dep(b, y)
            self.ops[e].append(b)

    def mm(self, out, lhsT, rhs, start=True, stop=True):
        return self.add("pe", lambda e: e.matmul(out, lhsT, rhs, start=start, stop=stop),
                        reads=(lhsT, rhs), writes=(out,))

    def tr(self, out, in_, ident):
        return self.add("pe", lambda e: e.transpose(out, in_, ident), reads=(in_, ident), writes=(out,))

    def tt(self, out, in0, in1, op, eng="dve"):
        return self.add(eng, lambda e: e.tensor_tensor(out, in0, in1, op), reads=(in0, in1), writes=(out,))

    def ts(self, out, in0, s1, op0, s2=None, op1=None, eng="dve", accum_out=None):
        kw = {}
        if eng == "pool" and op1 is None:
            if op0 == ALU.mult:
                s2, op1 = 0.0, ALU.add
            elif op0 == ALU.add:
                s2, op1 = 1.0, ALU.mult
        if op1 is not None:
            kw["op1"] = op1
        if accum_out is not None:
            kw["accum_out"] = accum_out
        w = (out,) if accum_out is None else (out, accum_out)
        return self.add(eng, lambda e: e.tensor_scalar(out, in0, s1, s2, op0, **kw),
                        reads=(in0, s1, s2), writes=w)

    def stt(self, out, in0, scalar, in1, op0, op1, accum_out=None):
        kw = {}
        if accum_out is not None:
            kw["accum_out"] = accum_out
        w = (out,) if accum_out is None else (out, accum_out)
        return self.add("dve", lambda e: e.scalar_tensor_tensor(out, in0, scalar, in1, op0, op1, **kw),
                        reads=(in0, scalar, in1), writes=w)

    def cp(self, out, in_, eng="dve"):
        if eng == "act":
            return self.add("act", lambda e: e.copy(out, in_), reads=(in_,), writes=(out,))
        return self.add(eng, lambda e: e.tensor_copy(out, in_), reads=(in_,), writes=(out,))

    def act(self, out, in_, func, bias=0.0, scale=1.0, accum_out=None):
        kw = {}
        if accum_out is not None:
            kw["accum_out"] = accum_out
        w = (out,) if accum_out is None else (out, accum_out)
        return self.add("act", lambda e: e.activation(out, in_, func, bias=bias, scale=scale, **kw),
                        reads=(in_, bias, scale), writes=w)

    def red(self, out, in_, op, axis=AX.X, eng="dve"):
        return self.add(eng, lambda e: e.tensor_reduce(out, in_, axis, op), reads=(in_,), writes=(out,))

    def recip(self, out, in_):
        return self.add("dve", lambda e: e.reciprocal(out, in_), reads=(in_,), writes=(out,))

    def rpow(self, out, in_, power, scale=1.0, bias=0.0):
        self.act(out, in_, AF.Ln, bias=bias, scale=scale)
        return self.act(out, out, AF.Exp, scale=power)

    def memset(self, ap, val, eng="dve"):
        return self.add(eng, lambda e: e.memset(ap, val), writes=(ap,))

    def scan(self, out, d0, d1, init, op0, op1):
        return self.add("dve", lambda e: e.tensor_tensor_scan(out, d0, d1, init, op0, op1),
                        reads=(d0, d1, init), writes=(out,))

    def dma(self, out, in_, eng="sp", is_output=False):
        x = self.add(eng, lambda e: e.dma_start(out=out, in_=in_), reads=(in_,), writes=(out,), isdma=True)
        if is_output:
            self.out_dmas.append(x)
        return x

    def finish(self):
        nc = self.nc
        fin = _Op("sp", None)
        fin.idx = len(self.ops["sp"])
        for y in self.out_dmas:
            self._dep(fin, y)
        self.ops["sp"].append(fin)
        sems = {}
        for e in ("pe", "dve", "act", "pool"):
            sems[e] = self.stack.enter_context(nc.semaphore("s_" + e))
        dsems = [self.stack.enter_context(nc.semaphore("d%d" % i)) for i in range(self.NDMA)]
        for e in ("pe", "dve", "act", "pool"):
            c = 0
            for x in self.ops[e]:
                if x.isdma:
                    continue
                if x.needed:
                    c += 1
                    x.sigval = c
            self.stats_sig = getattr(self, "stats_sig", {})
            self.stats_sig[e] = c
        ops = self.ops

        def replay(e, engobj):
            for x in ops[e]:
                for y in x.deps:
                    if y.isdma:
                        engobj.wait_ge(dsems[y.dsem], y.dval)
                    else:
                        engobj.wait_ge(sems[y.eng], y.sigval)
                if x.emit is None:
                    continue
                ins = x.emit(engobj)
                if x.isdma:
                    ins.then_inc(dsems[x.dsem], 16)
                elif x.needed:
                    ins.then_inc(sems[e], 1)

        with nc.Block() as block:
            @block.tensor
            def _(eng):
                replay("pe", eng)

            @block.vector
            def _(eng):
                replay("dve", eng)

            @block.scalar
            def _(eng):
                replay("act", eng)

            @block.gpsimd
            def _(eng):
                replay("pool", eng)

            @block.sync
            def _(eng):
                replay("sp", eng)


L = 2048
D = 1024
NT = L // 128
DEPTH = 2
N_IN = 3448
EPS = 1e-6

OFF = dict(gate=0, gdn_q=1024, gdn_k=1408, gdn_v=1792, gdn_a=2176, gdn_b=2188, mla_cq=2200, mla_ckv=2392,
           mla_kr=2520, rw_r=2552, rw_k=2808, rw_v=3064, rw_wd=3320, rw_ad=3384)


def chunk_table():
    ch = []
    for h in range(3):
        ch.append(("gq%d" % h, [(0, OFF["gdn_q"] + h * 128, 128)]))
        ch.append(("gk%d" % h, [(0, OFF["gdn_k"] + h * 128, 128)]))
        ch.append(("gv%d" % h, [(0, OFF["gdn_v"] + h * 128, 128)]))
    ch.append(("gab", [(0, OFF["gdn_a"], 6), (32, OFF["gdn_a"] + 6, 6), (64, OFF["gdn_b"], 6), (96, OFF["gdn_b"] + 6, 6)]))
    ch.append(("cq0", [(0, OFF["mla_cq"], 128)]))
    ch.append(("cq1", [(0, OFF["mla_cq"] + 128, 64)]))
    ch.append(("ckv", [(0, OFF["mla_ckv"], 128)]))
    ch.append(("kr", [(0, OFF["mla_kr"], 32)]))
    for i in range(2):
        ch.append(("rr%d" % i, [(0, OFF["rw_r"] + i * 128, 128)]))
        ch.append(("rk%d" % i, [(0, OFF["rw_k"] + i * 128, 128)]))
        ch.append(("rv%d" % i, [(0, OFF["rw_v"] + i * 128, 128)]))
    ch.append(("rwd", [(0, OFF["rw_wd"], 64)]))
    ch.append(("rad", [(0, OFF["rw_ad"], 64)]))
    for i in range(8):
        ch.append(("g%d" % i, [(0, OFF["gate"] + i * 128, 128)]))
    return ch


CHUNKS = chunk_table()
CH_IDX = {n: i for i, (n, _) in enumerate(CHUNKS)}
NCH = len(CHUNKS)


def host_win(w_in):
    out = np.zeros((DEPTH, NCH, 128, 8, 128), np.float32)
    for ci, (_, parts) in enumerate(CHUNKS):
        for dst, src, w in parts:
            blk = w_in[:, :, src:src + w].reshape(DEPTH, 8, 128, w)
            out[:, ci, :, :, dst:dst + w] = blk.transpose(0, 2, 1, 3)
    return out


class K:
    pass


def build(depth=DEPTH, mixers=("gdn", "mla", "rwkv"), dbg=False):
    nc = bass.Bass("TRN2", target_bir_lowering=False)
    k = K()
    k.nc = nc
    k.dbg = dbg
    k.dbg_outs = []
    k.cut = 99

    def din(name, shape, dt=F32):
        return nc.dram_tensor(name, list(shape), dt, kind="ExternalInput").ap()

    k.x_d = din("x", [L, D])
    k.win_d = din("win", [DEPTH, NCH, 128, 8, 128])
    k.normg_d = din("normg", [DEPTH, 128, 8])
    k.wout_d = din("wout", [DEPTH, 128, 8, 1024])
    k.fing_d = din("fing", [1, D])
    k.ident_d = din("ident", [128, 128])
    k.out_d = nc.dram_tensor("out", [L, D], F32, kind="ExternalOutput").ap()
    mla_decl(k)
    gdn_decl(k)
    rwkv_decl(k)

    with ExitStack() as st:
        P = Prog(nc, st)
        k.P = P
        k.xscr = nc.dram_tensor("xscr", [L, D], F32, kind="Internal").ap()
        P.dram_track("xscr", L * D * 4, 128 * D * 4)
        k.hT = P.sb("hT", [128, 8, L], BF16, blk=512)
        k.ident = P.sb("ident", [128, 128], F32)
        k.identb = P.sb("identb", [128, 128], BF16)
        k.normg = P.sb("normg", [128, DEPTH, 8], F32)
        k.wst = [P.sb("wst%d" % i, [128, 8, 128], F32) for i in range(2)]
        k.wbf = [P.sb("wbf%d" % i, [128, 8, 128], BF16) for i in range(2)]
        k.wrr = 0
        k.PAW = [P.ps("paw%d" % i, [128, 1024], F32, blk=512) for i in range(2)]
        k.PA = [k.PAW[i // 2][:, (i % 2) * 512:(i % 2 + 1) * 512] for i in range(4)]
        k.PB = [P.ps("pb%d" % i, [128, 512], F32) for i in range(4)]

        P.dma(k.ident[:], k.ident_d[:])
        P.cp(k.identb[:], k.ident[:])
        for l in range(DEPTH):
            P.dma(k.normg[:, l, :], k.normg_d[l])

        for l in range(depth):
            phase_a(k, l)
            with scope(k):
                k.mix_r = P.sb("mix_r", [128, 2, L], BF16, blk=512)
                if "rwkv" in mixers:
                    rwkv_phase(k, l)
                else:
                    P.memset(k.mix_r[:].rearrange("p a b -> p (a b)"), 1.0)
                with scope(k):
                    k.mix_m = P.sb("mix_m", [128, 3, L], BF16, blk=512)
                    if "mla" in mixers:
                        mla_phase(k, l)
                    else:
                        P.memset(k.mix_m[:].rearrange("p a b -> p (a b)"), 1.0)
                    with scope(k):
                        k.mix_g = P.sb("mix_g", [128, 3, L], BF16, blk=512)
                        if "gdn" in mixers:
                            gdn_phase(k, l)
                        else:
                            P.memset(k.mix_g[:].rearrange("p a b -> p (a b)"), 1.0)
                        if k.dbg:
                            for nm, t_, n_ in (("g", k.mix_g, 3), ("m", k.mix_m, 3), ("r", k.mix_r, 2)):
                                dump(k, "mix_%s%d" % (nm, l), t_[:].rearrange("p a b -> p (a b)"), [128, n_ * L])
                        phase_z(k, l, last=(l == depth - 1))
        P.finish()
        print("ops:", {e: len(v) for e, v in P.ops.items()}, "sig:", P.stats_sig, "dma:", P.ndma_ops)
    return k


def mixc(k, c):
    if c < 3:
        return k.mix_g[:, c, :]
    if c < 6:
        return k.mix_m[:, c - 3, :]
    return k.mix_r[:, c - 6, :]


def dump(k, name, ap, shape=None):
    if not k.dbg:
        return
    P = k.P
    shape = list(ap.shape) if shape is None else shape
    d = k.nc.dram_tensor("dbg_" + name, shape, ap.dtype, kind="ExternalOutput").ap()
    P.dma(d[:] if len(shape) == 2 else d, ap, is_output=True)
    k.dbg_outs.append("dbg_" + name)


def scope(k):
    class _S:
        def __enter__(s):
            s.old = k.P.stack
            s.st = ExitStack()
            s.st.__enter__()
            k.P.stack = s.st
            return s

        def __exit__(s, *a):
            k.P.barrier()
            k.P.stack = s.old
            s.st.__exit__(*a)
            return False
    return _S()


def phase_a(k, l):
    P = k.P
    with scope(k):
        ssq = P.sb("a_ssq", [128, NT])
        rs = P.sb("a_rs", [128, NT])
        rstd = P.sb("a_rstd", [128, NT])
        junk = [P.sb("a_junk%d" % i, [128, D], BF16) for i in range(2)]
        xs = [P.sb("a_xs%d" % i, [128, D], BF16) for i in range(2)]
        xin = [P.sb("a_xin%d" % i, [128, D], F32) for i in range(3)]
        src = k.x_d if l == 0 else k.xscr

        def stage1(tt):
            b = tt % 2
            xt_ = xin[tt % 3]
            P.dma(xt_[:], src[tt * 128:(tt + 1) * 128, :])
            P.act(junk[b][:], xt_[:], AF.Square, accum_out=ssq[:, tt:tt + 1])
            P.act(rs[:, tt:tt + 1], ssq[:, tt:tt + 1], AF.Sqrt, bias=EPS, scale=1.0 / D)
            P.recip(rstd[:, tt:tt + 1], rs[:, tt:tt + 1])
            P.ts(xs[b][:], xt_[:], rstd[:, tt:tt + 1], ALU.mult)

        def stage2(tt):
            b = tt % 2
            pt = k.PB[b][:].bitcast(BF16)
            for dc in range(8):
                P.tr(pt[:, dc * 128:(dc + 1) * 128], xs[b][:, dc * 128:(dc + 1) * 128], k.identb[:])
            P.cp(k.hT[:, :, tt * 128:(tt + 1) * 128], pt[:].rearrange("p (a b) -> p a b", a=8),
                 eng=("act" if tt % 2 == 0 else "dve"))
        stage1(0)
        for tt in range(NT):
            if tt + 1 < NT:
                stage1(tt + 1)
            stage2(tt)


def proj(k, l, name, alt=False):
    P = k.P
    BK = k.PB if alt else k.PA
    ci = CH_IDX[name]
    b = k.wrr
    k.wrr ^= 1
    P.dma(k.wst[b][:], k.win_d[l, ci], eng="sp")
    gb = k.normg[:, l, :].unsqueeze(2).broadcast_to([128, 8, 128])
    P.tt(k.wbf[b][:], k.wst[b][:], gb, ALU.mult, eng="pool")
    for tb in range(4):
        for dc in range(8):
            P.mm(BK[tb][:, :], k.wbf[b][:, dc, :], k.hT[:, dc, tb * 512:(tb + 1) * 512], start=(dc == 0), stop=(dc == 7))
    return BK


ZQ = "act"


def phase_z(k, l, last):
    P = k.P
    with scope(k):
        wst = P.sb("z_wst", [128, 8, 512], F32)
        wob = P.sb("z_wob", [128, 8, 1024], BF16, blk=512)
        sg = [P.sb("z_sg%d" % i, [128, L], BF16, blk=512) for i in range(2)]
        for nb in range(2):
            P.dma(wst[:], k.wout_d[l, :, :, nb * 512:(nb + 1) * 512])
            P.cp(wob[:, :, nb * 512:(nb + 1) * 512], wst[:], eng="act")
        for gc in range(8):
            pa = proj(k, l, "g%d" % gc, alt=(gc % 2 == 1))
            s = sg[gc % 2]
            for tb in range(4):
                P.act(s[:, tb * 512:(tb + 1) * 512], pa[tb][:, :], AF.Silu)
                mc = mixc(k, gc)[:, tb * 512:(tb + 1) * 512]
                P.tt(mc, mc, s[:, tb * 512:(tb + 1) * 512], ALU.mult)
        xin = [P.sb("z_xin%d" % i, [128, D], F32) for i in range(3)]
        src = k.x_d if l == 0 else k.xscr
        if last:
            ssq = P.sb("f_ssq", [128, NT])
            rs = P.sb("f_rs", [128, NT])
            rstd = P.sb("f_rstd", [128, NT])
            junk = [P.sb("f_junk%d" % i, [128, D], BF16) for i in range(2)]
            gf = P.sb("f_g", [128, D])
            ot = [P.sb("f_o%d" % i, [128, D]) for i in range(2)]
            P.dma(gf[:], k.fing_d[0:1, :].partition_broadcast(128))
        def za(tt):
            xt_ = xin[tt % 3]
            P.dma(xt_[:], src[tt * 128:(tt + 1) * 128, :])
            for nb in range(2):
                ps = k.PB[(tt * 2 + nb) % 4]
                for kc in range(8):
                    P.mm(ps[:, :], mixc(k, kc)[:, tt * 128:(tt + 1) * 128], wob[:, kc, nb * 512:(nb + 1) * 512],
                         start=(kc == 0), stop=(kc == 7))
                xs = xt_[:, nb * 512:(nb + 1) * 512]
                P.tt(xs, xs, ps[:, :], ALU.add)
            if not last:
                P.dma(k.xscr[tt * 128:(tt + 1) * 128, :], xt_[:], eng=ZQ)
            else:
                b = tt % 2
                P.act(junk[b][:], xt_[:], AF.Square, accum_out=ssq[:, tt:tt + 1])
                P.act(rs[:, tt:tt + 1], ssq[:, tt:tt + 1], AF.Sqrt, bias=EPS, scale=1.0 / D)

        def zb(tt):
            if last:
                xt_ = xin[tt % 3]
                b = tt % 2
                P.recip(rstd[:, tt:tt + 1], rs[:, tt:tt + 1])
                P.stt(ot[b][:], xt_[:], rstd[:, tt:tt + 1], gf[:], ALU.mult, ALU.mult)
                P.dma(k.out_d[tt * 128:(tt + 1) * 128, :], ot[b][:], eng=ZQ, is_output=True)
        za(0)
        for tt in range(NT):
            if tt + 1 < NT:
                za(tt + 1)
            zb(tt)


def host_inputs(inp, b):
    m = {}
    m["x"] = np.ascontiguousarray(inp["x"][b])
    m["win"] = host_win(inp["w_in"])
    m["normg"] = np.ascontiguousarray(inp["norm_g"].reshape(DEPTH, 8, 128).transpose(0, 2, 1))
    m["wout"] = np.ascontiguousarray(inp["w_out"].reshape(DEPTH, 8, 128, 1024).transpose(0, 2, 1, 3))
    m["fing"] = np.ascontiguousarray(inp["final_norm_g"].reshape(1, D))
    m["ident"] = np.eye(128, dtype=np.float32)
    host_mla(inp, b, m)
    host_gdn(inp, b, m)
    host_rwkv(inp, b, m)
    return m


TWO_PI = 2.0 * np.pi


def host_mla(inp, b, m):
    half = 16
    inv_freq = (10000.0 ** (-np.arange(half, dtype=np.float32) / half)).astype(np.float32)
    invf = np.zeros((32, 1), np.float32)
    invf[:, 0] = np.tile(inv_freq, 2) / np.float32(TWO_PI)
    m["invf"] = invf
    rm = np.zeros((32, 32), np.float32)
    for i in range(16):
        rm[i, i + 16] = -1.0
        rm[i + 16, i] = 1.0
    m["rmT"] = np.ascontiguousarray(rm.T)
    m["pos"] = np.ascontiguousarray(inp["positions"][b].reshape(1, L).astype(np.int32))
    wuq = inp["mla_w_uq"]
    o = np.zeros((DEPTH, 128, 2, 6, 128), np.float32)
    for h in range(6):
        nope = wuq[:, :, h * 96:h * 96 + 64]
        rope = wuq[:, :, h * 96 + 64:h * 96 + 96]
        o[:, :, 0, h, 64:128] = nope[:, 0:128]
        o[:, 0:64, 1, h, 64:128] = nope[:, 128:192]
        o[:, :, 0, h, 0:32] = rope[:, 0:128]
        o[:, 0:64, 1, h, 0:32] = rope[:, 128:192]
    m["wuq"] = o
    gq = np.zeros((DEPTH, 128, 2), np.float32)
    gq[:, :, 0] = inp["mla_q_norm_g"][:, 0:128]
    gq[:, 0:64, 1] = inp["mla_q_norm_g"][:, 128:192]
    m["gq"] = gq
    wukv = inp["mla_w_ukv"]
    wk = np.zeros((DEPTH, 128, 6, 128), np.float32)
    wv = np.zeros((DEPTH, 128, 6, 64), np.float32)
    for h in range(6):
        wk[:, :, h, 64:128] = wukv[:, :, h * 128:h * 128 + 64]
        wv[:, :, h, :] = wukv[:, :, h * 128 + 64:h * 128 + 128]
    m["wuk"] = wk
    m["wuv"] = wv
    m["gkv"] = np.ascontiguousarray(inp["mla_kv_norm_g"].reshape(DEPTH, 128, 1))


def mla_decl(k):
    nc = k.nc

    def din(name, shape, dt=F32):
        return nc.dram_tensor(name, list(shape), dt, kind="ExternalInput").ap()
    k.invf_d = din("invf", [32, 1])
    k.rmT_d = din("rmT", [32, 32])
    k.pos_d = din("pos", [1, L], I32)
    k.wuq_d = din("wuq", [DEPTH, 128, 2, 6, 128])
    k.gq_d = din("gq", [DEPTH, 128, 2])
    k.wuk_d = din("wuk", [DEPTH, 128, 6, 128])
    k.wuv_d = din("wuv", [DEPTH, 128, 6, 64])
    k.gkv_d = din("gkv", [DEPTH, 128, 1])


def latent_norm(k, l, names, nfeat, outs, ones):
    P = k.P
    sq = [P.sb("ln_sq%d" % i, [128, 512]) for i in range(2)]
    rq = [P.sb("ln_rq%d" % i, [128, 512]) for i in range(2)]
    n = len(names)
    for i, nm in enumerate(names):
        pa = proj(k, l, nm)
        for tb in range(4):
            s = sq[tb % 2]
            sk = ""
            if "a" not in sk:
                P.act(s[:], pa[tb][:, :], AF.Square)
            if "c" not in sk:
                P.cp(outs[i][:, tb * 512:(tb + 1) * 512], pa[tb][:, :], eng="dve")
            if "m" not in sk:
                P.mm(k.PB[tb][:, :], ones[:], s[:], start=(i == 0), stop=(i == n - 1))
    c2 = 9
    if c2 < 1:
        return
    for tb in range(4):
        r = rq[tb % 2]
        P.rpow(r[:], k.PB[tb][:, :], -0.5, scale=1.0 / nfeat, bias=EPS)
        for i in range(n):
            o = outs[i][:, tb * 512:(tb + 1) * 512]
            P.tt(o, o, r[:], ALU.mult)


def mla_phase(k, l):
    P = k.P
    SC = 96.0 ** -0.5
    with scope(k):
        cqn0 = P.sb("m_cqn0", [128, L], BF16, blk=512)
        cqn1 = P.sb("m_cqn1", [128, L], BF16, blk=512)
        ckvn = P.sb("m_ckvn", [128, L], BF16, blk=512)
        krope = P.sb("m_krope", [32, L], BF16, blk=512)
        cos2 = P.sb("m_cos2", [32, L], BF16, blk=512)
        sin2 = P.sb("m_sin2", [32, L], BF16, blk=512)
        wq = P.sb("m_wq", [128, 2, 6, 128], BF16)
        wk = P.sb("m_wk", [128, 6, 128], BF16)
        wv = P.sb("m_wv", [128, 6, 64], BF16)
        ones = P.sb("m_ones", [128, 128], F32)
        onesk = P.sb("m_onesk", [128, 128], F32)
        rmT = P.sb("m_rmT", [32, 32], F32)
        P.memset(ones[:], 1.0)
        P.memset(onesk[:], 1.0)
        P.memset(onesk[32:64, :], 0.0)
        P.dma(rmT[:], k.rmT_d[:])
        with scope(k):
            st = P.sb("m_st", [128, 2, 6, 128], F32)
            g = P.sb("m_g", [128, 4], F32)
            P.dma(st[:], k.wuq_d[l])
            P.dma(g[:, 0:2], k.gq_d[l])
            P.dma(g[:, 2:3], k.gkv_d[l])
            for kc in range(2):
                P.ts(wq[:, kc].rearrange("p a b -> p (a b)"), st[:, kc].rearrange("p a b -> p (a b)"),
                     g[:, kc:kc + 1], ALU.mult)
            st2 = P.sb("m_st2", [128, 6, 128], F32)
            P.dma(st2[:], k.wuk_d[l])
            P.ts(wk[:].rearrange("p a b -> p (a b)"), st2[:].rearrange("p a b -> p (a b)"), g[:, 2:3], ALU.mult)
            st3 = P.sb("m_st3", [128, 6, 64], F32)
            P.dma(st3[:], k.wuv_d[l])
            P.ts(wv[:].rearrange("p a b -> p (a b)"), st3[:].rearrange("p a b -> p (a b)"), g[:, 2:3], ALU.mult)
        if k.cut < 1:
            return
        with scope(k):
            latent_norm(k, l, ["cq0", "cq1"], 192, [cqn0, cqn1], ones)
            latent_norm(k, l, ["ckv"], 128, [ckvn], ones)
        if k.cut < 2:
            return
        with scope(k):
            invf = P.sb("m_invf", [32, 1], F32)
            P.dma(invf[:], k.invf_d[:])
            pa = proj(k, l, "kr")

            def rope_blk(tb):
                sl = slice(tb * 512, (tb + 1) * 512)
                posi = P.sb("m_posi%d" % tb, [32, 512], I32)
                y = P.sb("m_y%d" % tb, [32, 512], F32)
                yi = P.sb("m_yi%d" % tb, [32, 512], I32)
                fr = P.sb("m_fr%d" % tb, [32, 512], F32)
                kr = P.sb("m_kr%d" % tb, [32, 512], F32)
                t1 = P.sb("m_t1%d" % tb, [32, 512], F32)
                t2 = P.sb("m_t2%d" % tb, [32, 512], F32)
                P.dma(posi[:], k.pos_d[0:1, sl].partition_broadcast(32))
                P.cp(kr[:], pa[tb][0:32, :], eng="act")
                yield
                P.cp(y[:], posi[:])
                P.mm(k.PB[tb][0:32, :], rmT[:], kr[:])
                yield
                P.ts(y[:], y[:], invf[:, 0:1], ALU.mult)
                yield
                for off, dst in ((0.0, sin2), (0.25, cos2)):
                    if off != 0.0:
                        P.ts(y[:], y[:], off, ALU.add)
                        yield
                    P.cp(yi[:], y[:])
                    yield
                    P.cp(fr[:], yi[:])
                    yield
                    P.tt(fr[:], y[:], fr[:], ALU.subtract)
                    yield
                    P.act(dst[:, sl], fr[:], AF.Sin, scale=TWO_PI * (1.0 - 1e-6))
                    yield
                P.tt(t1[:], kr[:], cos2[:, sl], ALU.mult)
                P.tt(t2[:], k.PB[tb][0:32, :], sin2[:, sl], ALU.mult)
                yield
                P.tt(krope[:, sl], t1[:], t2[:], ALU.add)
                yield
            run_interleaved([rope_blk(tb) for tb in range(4)])
        if k.cut < 3:
            return
        kT = [P.sb("m_kT%d" % i, [128, L], BF16, blk=512) for i in range(2)]
        qT = [P.sb("m_qT%d" % i, [128, L], BF16, blk=512) for i in range(2)]
        Vh = [P.sb("m_V%d" % i, [128, NT, 96], BF16) for i in range(2)]
        pT = [P.sb("m_pT%d" % i, [128, 1024], BF16, blk=512) for i in range(2)]
        sq = [P.sb("m_sq%d" % i, [128, 512], F32) for i in range(2)]
        qr = [P.sb("m_qr%d" % i, [32, 512], F32) for i in range(2)]
        t1 = P.sb("m_t1b", [32, 512], F32)
        t2 = P.sb("m_t2b", [32, 512], F32)
        mrow = P.sb("m_mrow", [64, 512], F32)
        km4 = P.sb("m_km4", [128, 4], F32)
        kmax2 = P.sb("m_kmax2", [128, 1], F32)
        rden = [P.sb("m_rden%d" % i, [64, 512], F32) for i in range(2)]
        for i in range(2):
            P.memset(kT[i][32:64, :], 0.0)
            P.memset(kT[i][32:33, :], 1.0)
            P.memset(qT[i][32:64, :], 0.0)
            P.memset(Vh[i][:, :, 64:96], 1.0)
        kmx = [P.sb("m_kmx%d" % i, [128, 1], F32) for i in range(2)]

        def prep(h):
            kt_, qt_, vh_ = kT[h % 2], qT[h % 2], Vh[h % 2]
            kmax2_ = kmx[h % 2]
            P.cp(kt_[0:32, :], krope[:], eng="pool")
            for tb in range(4):
                sl = slice(tb * 512, (tb + 1) * 512)
                s = sq[tb % 2]
                P.mm(k.PB[2][:, :], wk[:, h, :], ckvn[:, sl])
                yield
                P.cp(kt_[64:128, sl], k.PB[2][64:128, :], eng="dve")
                yield
                P.tt(s[:], kt_[:, sl], kt_[:, sl], ALU.mult, eng="pool")
                yield
                yield
                P.mm(k.PB[3][:, :], onesk[:], s[:])
                yield
                P.red(km4[:, tb:tb + 1], k.PB[3][:, :], ALU.max)
                yield
            P.red(kmax2_[:], km4[:], ALU.max)
            for half in range(2):
                for j in range(8):
                    tt = half * 8 + j
                    P.mm(k.PB[2][:, j * 64:(j + 1) * 64], ckvn[:, tt * 128:(tt + 1) * 128], wv[:, h, :])
                yield
                P.cp(vh_[:, half * 8:(half + 1) * 8, 0:64], k.PB[2][:, :].rearrange("p (a b) -> p a b", a=8), eng="dve")
                yield
            for tb in range(4):
                sl = slice(tb * 512, (tb + 1) * 512)
                s = sq[tb % 2]
                q_ = qr[tb % 2]
                P.mm(k.PB[2][:, :], wq[:, 0, h, :], cqn0[:, sl], start=True, stop=False)
                P.mm(k.PB[2][:, :], wq[:, 1, h, :], cqn1[:, sl], start=False, stop=True)
                yield
                P.cp(qt_[64:128, sl], k.PB[2][64:128, :], eng="dve")
                P.cp(q_[:], k.PB[2][0:32, :], eng="dve")
                yield
                P.act(s[:], k.PB[2][:, :], AF.Square)
                yield
                P.mm(k.PB[3][:, :], ones[:], s[:])
                yield
                P.act(mrow[32:33, :], k.PB[3][32:33, :], AF.Sqrt, scale=kmax2_[32:33, 0:1])
                yield
                P.ts(qt_[32:33, sl], mrow[32:33, :], -1.0, ALU.mult)
                P.mm(k.PB[2][0:32, :], rmT[:], q_[:])
                P.tt(t1[:], q_[:], cos2[:, sl], ALU.mult, eng="pool")
                yield
                P.tt(t2[:], k.PB[2][0:32, :], sin2[:, sl], ALU.mult)
                yield
                P.tt(qt_[0:32, sl], t1[:], t2[:], ALU.add)
                yield

        def attn(h):
            kt_, qt_, vh_ = kT[h % 2], qT[h % 2], Vh[h % 2]
            pti = 0
            for qb in range(4):
                qs = slice(qb * 512, (qb + 1) * 512)
                O = k.PB[qb % 2]

                def s_pair(m_):
                    for j in range(2):
                        kt = 2 * m_ + j
                        P.mm(k.PAW[m_ % 2][:, j * 512:(j + 1) * 512], kt_[:, kt * 128:(kt + 1) * 128], qt_[:, qs])
                s_pair(0)
                s_pair(1)
                for m_ in range(NT // 2):
                    p_ = pT[pti % 2]
                    pti += 1
                    P.act(p_[:], k.PAW[m_ % 2][:, :], AF.Exp, scale=SC)
                    if m_ + 2 < NT // 2:
                        s_pair(m_ + 2)
                    for j in range(2):
                        kt = 2 * m_ + j
                        P.mm(O[0:96, :], vh_[:, kt, :], p_[:, j * 512:(j + 1) * 512], start=(kt == 0), stop=(kt == NT - 1))
                    yield
                rd = rden[qb % 2]
                P.rpow(rd[0:32, :], O[64:96, :], -1.0)
                P.rpow(rd[32:64, :], O[64:96, :], -1.0)
                ob = (h % 2) * 64
                P.tt(mixc(k, 3 + h // 2)[ob:ob + 64, qs], O[0:64, :], rd[:], ALU.mult)
                yield

        for _ in prep(0):
            pass
        mode = "il"
        for h in range(6):
            gens = [attn(h)]
            if h + 1 < 6:
                if mode == "il":
                    gens.append(prep(h + 1))
                elif mode == "seq":
                    run_interleaved(gens)
                    gens = [prep(h + 1)]
                elif mode == "noprep":
                    pass
            if mode == "noprep" and h > 0:
                gens = [attn(0)]
            run_interleaved(gens)


NCK = L // 64
NEG = -30000.0


def host_gdn(inp, b, m):
    cw = inp["gdn_conv"]
    o = np.zeros((DEPTH, 128, 9, 5), np.float32)
    for part in range(3):
        for p in range(3):
            o[:, :, part * 3 + p, :] = cw[:, :, part * 384 + p * 128: part * 384 + (p + 1) * 128].transpose(0, 2, 1)
    m["gconv"] = o
    gb = np.zeros((DEPTH, 128, 2), np.float32)
    for d in range(2):
        gb[:, d * 32:d * 32 + 6, 0] = inp["gdn_dt_bias"][:, d, :]
        gb[:, d * 32:d * 32 + 6, 1] = inp["gdn_a_log"][:, d, :]
    m["ggb"] = gb
    m["gng"] = np.ascontiguousarray(np.tile(inp["gdn_norm_g"], (1, 2)).reshape(DEPTH, 128, 1))
    sel = np.zeros((64, 6, 128), np.float32)
    for d in range(2):
        for p in range(3):
            sel[d * 32 + 2 * p, d * 3 + p, 0:64] = 1.0
            sel[d * 32 + 2 * p + 1, d * 3 + p, 64:128] = 1.0
    m["gsel"] = sel
    j = np.arange(64)[:, None]
    i = np.arange(64)[None, :]
    nm = np.zeros((128, 2, 64), np.float32)
    nm[:, 0, :] = np.tile(np.where(i > j, 0.0, NEG), (2, 1))
    nm[:, 1, :] = np.tile(np.where(i < j, 0.0, NEG), (2, 1))
    m["gnegm"] = nm
    m["gid2"] = np.ascontiguousarray(np.tile(np.eye(64, dtype=np.float32), (2, 1)))


def gdn_decl(k):
    nc = k.nc

    def din(name, shape, dt=F32):
        return nc.dram_tensor(name, list(shape), dt, kind="ExternalInput").ap()
    k.gconv_d = din("gconv", [DEPTH, 128, 9, 5])
    k.ggb_d = din("ggb", [DEPTH, 128, 2])
    k.gng_d = din("gng", [DEPTH, 128, 1])
    k.gsel_d = din("gsel", [64, 6, 128])
    k.gnegm_d = din("gnegm", [128, 2, 64])
    k.gid2_d = din("gid2", [128, 64])


def bc3(ap2, n):
    return ap2.unsqueeze(2).broadcast_to([ap2.shape[0], ap2.shape[1], n])


def bcm(ap2, n):
    return ap2.unsqueeze(1).broadcast_to([ap2.shape[0], n, ap2.shape[1]])


HS = (slice(0, 64), slice(64, 128))


def v3(ps, n=8):
    return ps[:, 0:n * 64].rearrange("p (a b) -> p a b", a=n)


def mm2(P, ps, c, lhsT, rhs, **kw):
    for hs in HS:
        P.mm(ps[hs, c * 64:(c + 1) * 64], lhsT[hs], rhs[hs], **kw)


def tr2(P, ps, c, in_, ident):
    for hs in HS:
        P.mm(ps[hs, c * 64:(c + 1) * 64], in_[hs], ident[hs, hs])


def neumann2(k, Nn, Rm, tmp, bank, id2, n8=8):
    P = k.P
    idb = bcm(id2[:, :], n8)
    tA, tB, tC, tD = tmp
    pa_, pb_, pc_ = bank
    for c in range(n8):
        tr2(P, pa_, c, Nn[:, c, :], k.identb)
    P.cp(tA[:], v3(pa_, n8), eng="act")
    P.tt(Rm[:], Nn[:], idb, ALU.add)
    yield
    cur, curT = Nn, tA
    targets = [(tB, tC), (tD, tA)]
    for lvl in range(1, 7):
        nxt, nxtT = targets[(lvl - 1) % 2]
        if lvl >= 2:
            for c in range(n8):
                mm2(P, pc_, c, curT[:, c, :], Rm[:, c, :])
            P.tt(Rm[:], Rm[:], v3(pc_, n8), ALU.add)
        if lvl <= 5:
            for c in range(n8):
                mm2(P, pb_, c, cur[:, c, :], curT[:, c, :])
            P.cp(nxtT[:], v3(pb_, n8), eng="act")
            if lvl < 5:
                for c in range(n8):
                    mm2(P, pa_, c, curT[:, c, :], cur[:, c, :])
                P.cp(nxt[:], v3(pa_, n8), eng="act")
        yield
        cur, curT = nxt, nxtT


def run_interleaved(gens, skew=0):
    gens = list(gens)
    for _ in range(skew):
        try:
            next(gens[0])
        except StopIteration:
            gens.pop(0)
            break
    while gens:
        for g in list(gens):
            try:
                next(g)
            except StopIteration:
                gens.remove(g)


def norm_pipe(k, n, src_fn, sqb, rnb, banks, bones, power, scale, bias, post_fn):
    P = k.P

    def pre(i):
        P.act(sqb[i % 2][:], src_fn(i), AF.Square)
        P.mm(banks[i % 2][:, :], bones[:], sqb[i % 2][:])

    def post(i):
        P.rpow(rnb[i % 2][:], banks[i % 2][:, :], power, scale=scale, bias=bias)
        post_fn(i, rnb[i % 2])
    pre(0)
    for i in range(n):
        if i + 1 < n:
            pre(i + 1)
        post(i)


def run_pipelined(chains, depth=2):
    active = []
    nxt = [0] * len(chains)

    def start(ci):
        if nxt[ci] < len(chains[ci]):
            active.append((ci, chains[ci][nxt[ci]](nxt[ci] % depth)))
            nxt[ci] += 1
    for ci in range(len(chains)):
        for _ in range(depth):
            start(ci)
    while active:
        for item in list(active):
            try:
                next(item[1])
            except StopIteration:
                active.remove(item)
                start(item[0])


def gdn_phase(k, l):
    P = k.P
    with scope(k):
        GC = P.sb("g_GC", [64, L], F32, blk=512)
        GP = [P.sb("g_GP%d" % p, [128, NCK, 4], F32) for p in range(3)]
        NBP = [P.sb("g_NBP%d" % p, [128, NCK, 2], F32) for p in range(3)]
        sel = P.sb("g_sel", [64, 6, 128], F32)
        negm = P.sb("g_negm", [128, 2, 64], F32)
        id2 = P.sb("g_id2", [128, 64], F32)
        cw = P.sb("g_cw", [128, 9, 5], F32)
        ng = P.sb("g_ng", [128, 1], F32)
        bones = P.sb("g_bones", [128, 128], F32)
        P.dma(sel[:], k.gsel_d[:])
        P.dma(negm[:], k.gnegm_d[:])
        P.dma(id2[:], k.gid2_d[:])
        P.dma(cw[:], k.gconv_d[l])
        P.dma(ng[:], k.gng_d[l])
        P.memset(bones[:], 0.0)
        P.memset(bones[0:64, 0:64], 1.0)
        P.memset(bones[64:128, 64:128], 1.0)
        with scope(k):
            GT = P.sb("g_GT", [128, L], F32, blk=512)
            m0 = P.sb("g_m0", [64, L], F32)
            gb = P.sb("g_gb", [128, 2], F32)
            negA = P.sb("g_negA", [128, 1], F32)
            P.dma(gb[:], k.ggb_d[l])
            P.act(negA[:], gb[:, 1:2], AF.Exp)
            P.ts(negA[:], negA[:], -1.0, ALU.mult)
            P.memset(m0[:], 1.0)
            P.memset(m0[:, 0:L:64], 0.0)
            pa = proj(k, l, "gab")
            for tb in range(4):
                sl = slice(tb * 512, (tb + 1) * 512)
                P.act(GT[0:64, sl], pa[tb][0:64, :], AF.Identity, bias=gb[0:64, 0:1])
                P.act(GT[64:128, sl], pa[tb][64:128, :], AF.Sigmoid)
            P.act(GC[:, :], GT[0:64, :], AF.Abs)
            P.act(GC[:, :], GC[:, :], AF.Exp, scale=-1.0)
            P.act(GC[:, :], GC[:, :], AF.Ln, bias=1.0)
            P.act(GT[0:64, :], GT[0:64, :], AF.Relu)
            P.tt(GT[0:64, :], GT[0:64, :], GC[:, :], ALU.add)
            P.ts(GT[0:64, :], GT[0:64, :], negA[0:64, 0:1], ALU.mult)
            P.scan(GC[:, :], m0[:, :], GT[0:64, :], 0.0, ALU.mult, ALU.add)
            gc3 = GC[32:64, :].rearrange("p (a b) -> p a b", b=64)
            P.tt(m0[32:64, :].rearrange("p (a b) -> p a b", b=64), bc3(GC[32:64, 63:L:64], 64), gc3, ALU.subtract)
            P.tt(GC[32:64, :], m0[32:64, :], GT[32:64, :], ALU.add)
            for grp in range(4):
                g8 = slice(grp * 8, (grp + 1) * 8)
                for c in range(8):
                    ck = grp * 8 + c
                    cs = slice(c * 64, (c + 1) * 64)
                    for hs in HS:
                        P.mm(k.PB[2 * (grp % 2)][hs, cs], GC[:, ck * 64:(ck + 1) * 64], k.ident[0:64, 0:64])
                        P.mm(k.PB[2 * (grp % 2) + 1][hs, cs], GT[64:128, ck * 64:(ck + 1) * 64], k.ident[64:128, 64:128])
                n_ = 0
                for p in range(3):
                    for hf, hs in enumerate(HS):
                        h = 2 * p + hf
                        for q, ps in ((0, k.PB[2 * (grp % 2)]), (1, k.PB[2 * (grp % 2) + 1])):
                            src = v3(ps)[hs, :, h:h + 33:32]
                            P.cp(GP[p][hs, g8, 2 * q:2 * q + 2], src, eng=("act" if q else "dve"))
            for p in range(3):
                P.ts(NBP[p][:], GP[p][:, :, 2:4], -1.0, ALU.mult)
        for p in range(3):
            with scope(k):
                Q = P.sb("g_Q", [128, L], BF16, blk=512)
                K_ = P.sb("g_K", [128, L], BF16, blk=512)
                Kt = P.sb("g_Kt", [128, NCK, 64], BF16, blk=512)
                Vt = P.sb("g_Vt", [128, NCK, 64], BF16, blk=512)
                O = P.sb("g_O", [128, L], F32, blk=512)
                P.memset(O[:], 0.0, eng="pool")
                with scope(k):
                    xp = P.sb("g_xp", [128, L + 4], BF16)
                    Dg = P.sb("g_Dg", [128, 5, 128], BF16)
                    Vf = P.sb("g_Vf", [128, L], BF16, blk=512)
                    cv = P.sb("g_cv", [128, L], F32, blk=512)
                    sq = P.sb("g_sq", [128, 512], F32)
                    rn = P.sb("g_rn", [128, 512], F32)
                    sq2 = P.sb("g_sq2", [128, 512], F32)
                    rn2 = P.sb("g_rn2", [128, 512], F32)
                    P.memset(xp[:, 0:2], 0.0)
                    P.memset(xp[:, L + 2:L + 4], 0.0)
                    for part, nm, dst in ((0, "gq", Q), (1, "gk", K_), (2, "gv", Vf)):
                        pa = proj(k, l, "%s%d" % (nm, p))
                        for tb in range(4):
                            P.cp(xp[:, 2 + tb * 512:2 + (tb + 1) * 512], pa[tb][:, :], eng=("act" if tb % 2 else "dve"))
                        wi = part * 3 + p
                        for j in range(5):
                            P.ts(Dg[:, j, :], k.identb[:], cw[:, wi, j:j + 1], ALU.mult, eng=("pool" if j % 2 else "dve"))
                        for tb in range(4):
                            for j in range(5):
                                P.mm(k.PB[tb][:, :], Dg[:, j, :], xp[:, j + tb * 512:j + (tb + 1) * 512],
                                     start=(j == 0), stop=(j == 4))
                        for tb in range(4):
                            sl = slice(tb * 512, (tb + 1) * 512)
                            P.act((dst if part == 2 else cv)[:, sl], k.PB[tb][:, :], AF.Silu)
                        if part < 2:
                            def fin(tb, r, dst=dst):
                                P.tt(dst[:, tb * 512:(tb + 1) * 512], cv[:, tb * 512:(tb + 1) * 512], r[:], ALU.mult)
                            norm_pipe(k, 4, lambda tb: cv[:, tb * 512:(tb + 1) * 512], (sq, sq2), (rn, rn2),
                                      (k.PA[2], k.PA[3]), bones, -0.5, 64.0 if part == 0 else 1.0,
                                      64e-6 if part == 0 else 1e-6, fin)
                    for src, dstt in ((K_, Kt), (Vf, Vt)):
                        for grp in range(4):
                            ps = k.PB[grp % 2]
                            for c in range(8):
                                ck = grp * 8 + c
                                tr2(P, ps, c, src[:, ck * 64:(ck + 1) * 64], k.identb)
                            P.cp(dstt[:, grp * 8:(grp + 1) * 8, :], v3(ps), eng=("act" if grp % 2 else "dve"))
                with scope(k):
                    T = dict(GC=GC, GP=GP[p], NBP=NBP[p], sel=sel, negm=negm, id2=id2, Q=Q, K=K_, Kt=Kt, Vt=Vt, O=O)
                    run_pipelined([gdn_chain(k, p, d, T) for d in range(2)], depth=1)
                with scope(k):
                    sqo = [P.sb("g_osq%d" % i, [128, 512], F32) for i in range(2)]
                    rno = [P.sb("g_orn%d" % i, [128, 512], F32) for i in range(2)]

                    def fin_o(tb, r):
                        sl = slice(tb * 512, (tb + 1) * 512)
                        P.tt(r[:], O[:, sl], r[:], ALU.mult)
                        P.ts(mixc(k, p)[:, sl], r[:], ng[:, 0:1], ALU.mult)
                    norm_pipe(k, 4, lambda tb: O[:, tb * 512:(tb + 1) * 512], sqo, rno, (k.PB[2], k.PB[3]), bones,
                              -0.5, 1.0 / 64, EPS, fin_o)


def gdn_chain(k, p, d, T):
    P = k.P
    GC, GP, NBP, sel, negm, id2, Q, K_, Kt, Vt, O = (T[n] for n in ("GC", "GP", "NBP", "sel", "negm", "id2", "Q", "K", "Kt", "Vt", "O"))
    B = k.PA if d == 0 else k.PB
    tag = "g%d_" % d
    names = ("CB", "EI", "QG", "Rm", "U0", "WT", "BW", "KD", "GK", "AcT", "Sg", "Ug", "nA", "nB", "nC", "nD", "Nb", "PTb", "Sgb")
    f32n = ("CB", "EI", "AcT", "Sg")
    NSET = 1
    GG = [{n: P.sb(tag + "%d" % s_ + n, [128, 8, 64], F32 if n in f32n else BF16) for n in names} for s_ in range(NSET)]
    for s_ in range(NSET):
        GG[s_]["gend"] = P.sb(tag + "gend%d" % s_, [128, 8], F32)
        GG[s_]["kds"] = P.sb(tag + "kds%d" % s_, [128, 8], F32)
    gam = P.sb(tag + "gam", [128, NCK], F32)
    shared = {"scan": 0}
    Scar = P.sb(tag + "Scar", [128, 64], F32)
    e_ = 63 if d == 0 else 0
    idb = bcm(id2[:, :], 8)

    def f2(t):
        return t[:].rearrange("p a b -> p (a b)")
    P.memset(Scar[:], 0.0)
    P.act(gam[:], GP[:, :, d], AF.Exp)
    gorder = list(range(4)) if d == 0 else list(range(3, -1, -1))

    def group(gi, grp, G):
        Nb, PTb, Sgb, gend, kds = G["Nb"], G["PTb"], G["Sgb"], G["gend"], G["kds"]
        sl = slice(grp * 512, (grp + 1) * 512)
        g8 = slice(grp * 8, (grp + 1) * 8)
        cj = GP[:, g8, d]
        nb = NBP[:, g8, d]
        CB, EI, QG, Rm, U0, WT, BW, KD, GK, AcT, Sg, Ug = (G[n] for n in names[:12])
        P.mm(B[0][:, :], sel[:, d * 3 + p, :], GC[:, sl])
        P.cp(f2(CB), B[0][:, :], eng="act")
        P.act(f2(EI), f2(CB), AF.Exp)
        P.cp(gend[:], EI[:, :, e_], eng="pool")
        P.tt(f2(QG), f2(EI), Q[:, sl], ALU.mult)
        P.tt(kds[:], CB[:, :, e_], cj, ALU.subtract)
        P.act(kds[:], kds[:], AF.Exp)
        P.tt(GK[:], Kt[:, g8, :], bc3(gam[:, g8], 64), ALU.mult, eng="pool")
        P.tt(KD[:], Kt[:, g8, :], bc3(kds[:], 64), ALU.mult, eng="pool")
        P.tt(CB[:], CB[:], bc3(cj, 64), ALU.subtract)
        P.tt(CB[:], CB[:], bcm(negm[:, d, :], 8), ALU.add)
        P.act(f2(CB), f2(CB), AF.Exp)
        yield
        P.tt(EI[:], CB[:], idb, ALU.add)
        for c in range(8):
            cs = slice((grp * 8 + c) * 64, (grp * 8 + c + 1) * 64)
            mm2(P, B[0], c, K_[:, cs], Q[:, cs])
        P.tt(PTb[:], EI[:], v3(B[0]), ALU.mult)
        for c in range(8):
            cs = slice((grp * 8 + c) * 64, (grp * 8 + c + 1) * 64)
            mm2(P, B[1], c, K_[:, cs], K_[:, cs])
        P.tt(CB[:], CB[:], v3(B[1]), ALU.mult)
        P.tt(Nb[:], CB[:], bc3(nb, 64), ALU.mult)
        yield
        for _ in neumann2(k, Nb, Rm, (G["nA"], G["nB"], G["nC"], G["nD"]), (B[0], B[1], B[2]), id2):
            yield
        for c in range(8):
            mm2(P, B[2], c, Rm[:, c, :], GK[:, c, :])
        P.tt(BW[:], v3(B[2]), bc3(nb, 64), ALU.mult)
        for c in range(8):
            mm2(P, B[0], c, Rm[:, c, :], Vt[:, grp * 8 + c, :])
        P.tt(U0[:], v3(B[0]), bc3(GP[:, g8, 2 + d], 64), ALU.mult)
        for c in range(8):
            mm2(P, B[1], c, GK[:, c, :], Rm[:, c, :])
        P.cp(WT[:], v3(B[1]), eng="act")
        yield
        for c in range(8):
            mm2(P, B[3], c, BW[:, c, :], KD[:, c, :])
        P.tt(AcT[:], idb, bc3(gend[:], 64), ALU.mult, eng="pool")
        P.tt(AcT[:], AcT[:], v3(B[3]), ALU.add)
        yield
        while shared["scan"] != gi:
            yield
        corder = range(8) if d == 0 else range(7, -1, -1)
        prev = Scar[:]
        for n, c in enumerate(corder):
            P.cp(Sg[:, c, :], prev, eng="pool") if n == 0 else None
            ps = B[2 + n % 2]
            mm2(P, ps, 0, AcT[:, c, :], Sg[:, c, :], start=True, stop=False)
            mm2(P, ps, 0, KD[:, c, :], U0[:, c, :], start=False, stop=True)
            last = (n == 7)
            dst = Scar[:] if last else Sg[:, corder[n + 1], :]
            P.cp(dst, ps[:, 0:64], eng="act")
            yield
        shared["scan"] = gi + 1
        P.cp(Sgb[:], Sg[:], eng="act")
        for c in range(8):
            mm2(P, B[0], c, WT[:, c, :], Sgb[:, c, :])
        P.tt(CB[:], v3(B[0]), bc3(nb, 64), ALU.mult)
        P.tt(Ug[:], CB[:], U0[:], ALU.add)
        yield
        for c in range(8):
            mm2(P, B[1], c, Sgb[:, c, :], QG[:, c, :], start=True, stop=False)
            mm2(P, B[1], c, Ug[:, c, :], PTb[:, c, :], start=False, stop=True)
        P.tt(O[:, sl], O[:, sl], B[1][:, :], ALU.add)
        yield
    return [(lambda slot, gi=gi, grp=grp: group(gi, grp, GG[slot])) for gi, grp in enumerate(gorder)]


RSKEW = 0
RW_EPS = 64e-5
DEC = float(np.exp(-0.5))
GS = 8
NG = NCK // GS


def host_rwkv(inp, b, m):
    mu = inp["rwkv_mu"]
    o = np.zeros((DEPTH, 128, 8, 2), np.float32)
    for part in range(3):
        for p in range(2):
            o[:, :, part * 2 + p, :] = mu[:, :, part * 256 + p * 128: part * 256 + (p + 1) * 128].transpose(0, 2, 1)
    o[:, 0:64, 6, :] = mu[:, :, 768:832].transpose(0, 2, 1)
    o[:, 0:64, 7, :] = mu[:, :, 832:896].transpose(0, 2, 1)
    m["rmu"] = o

    def pp(a):
        if a.ndim == 2:
            return np.ascontiguousarray(a.reshape(DEPTH, 2, 128).transpose(0, 2, 1))
        return np.ascontiguousarray(a.reshape(DEPTH, 2, 2, 128).transpose(0, 3, 1, 2))
    pv = np.zeros((DEPTH, 128, 7, 2), np.float32)
    pv[:, :, 0:2, :] = pp(inp["rwkv_w0"])
    pv[:, :, 2:4, :] = pp(inp["rwkv_a0"])
    pv[:, :, 4, :] = pp(inp["rwkv_k_k"])
    pv[:, :, 5, :] = pp(inp["rwkv_k_a"])
    pv[:, :, 6, :] = pp(inp["rwkv_r_k"].reshape(DEPTH, 256))
    m["rpv"] = pv
    ln = np.zeros((DEPTH, 128, 2, 2), np.float32)
    ln[:, :, 0, :] = pp(inp["rwkv_ln_g"])
    ln[:, :, 1, :] = pp(inp["rwkv_ln_b"])
    m["rln"] = ln
    m["rw2"] = np.ascontiguousarray(inp["rwkv_w2"].transpose(0, 2, 1, 3))
    m["ra2"] = np.ascontiguousarray(inp["rwkv_a2"].transpose(0, 2, 1, 3))
    s_ = np.arange(64)[:, None]
    t_ = np.arange(64)[None, :]
    msk = np.zeros((128, 2, 4, 64), np.float32)
    msk[:, 0, 0, :] = np.tile((t_ > s_), (2, 1))
    msk[:, 0, 1, :] = np.tile((t_ >= s_), (2, 1))
    msk[:, 1, 0, :] = np.tile((t_ < s_), (2, 1))
    msk[:, 1, 1, :] = np.tile((t_ <= s_), (2, 1))
    msk[:, :, 2:4, :] = -msk[:, :, 0:2, :]
    m["rmsk"] = msk


def rwkv_decl(k):
    nc = k.nc

    def din(name, shape, dt=F32):
        return nc.dram_tensor(name, list(shape), dt, kind="ExternalInput").ap()
    k.rmu_d = din("rmu", [DEPTH, 128, 8, 2])
    k.rpv_d = din("rpv", [DEPTH, 128, 7, 2])
    k.rln_d = din("rln", [DEPTH, 128, 2, 2])
    k.rw2_d = din("rw2", [DEPTH, 64, 2, 256])
    k.ra2_d = din("ra2", [DEPTH, 64, 2, 256])
    k.rmsk_d = din("rmsk", [128, 2, 4, 64])


def rwkv_phase(k, l):
    P = k.P
    with scope(k):
        mu = P.sb("r_mu", [128, 8, 3], F32)
        pv = P.sb("r_pv", [128, 7, 2], F32)
        omka = P.sb("r_omka", [128, 2], F32)
        hrk = P.sb("r_hrk", [128, 2], F32)
        ln = P.sb("r_ln", [128, 2, 2], F32)
        w2 = P.sb("r_w2", [64, 2, 256], BF16)
        a2 = P.sb("r_a2", [64, 2, 256], BF16)
        msk = P.sb("r_msk", [128, 2, 4, 64], F32)
        id2 = P.sb("r_id2", [128, 64], F32)
        bones = P.sb("r_bones", [128, 128], F32)
        m0 = P.sb("r_m0", [128, GS * 64], F32)
        twd = P.sb("r_twd", [64, L], BF16, blk=512)
        adx = P.sb("r_adx", [64, L], BF16, blk=512)
        sh32 = P.sb("r_sh32", [128, L], F32, blk=512)
        xp = P.sb("r_xp", [128, L + 2], F32)
        P.dma(mu[:, :, 0:2], k.rmu_d[l])
        P.dma(pv[:], k.rpv_d[l])
        P.dma(ln[:], k.rln_d[l])
        with scope(k):
            w2f = P.sb("r_w2f", [64, 2, 256], F32)
            a2f = P.sb("r_a2f", [64, 2, 256], F32)
            P.dma(w2f[:], k.rw2_d[l])
            P.dma(a2f[:], k.ra2_d[l])
            P.cp(w2[:], w2f[:], eng="act")
            P.cp(a2[:], a2f[:], eng="act")
        P.dma(msk[:], k.rmsk_d[:])
        P.dma(id2[:], k.gid2_d[:])
        P.memset(bones[:], 0.0)
        P.memset(bones[0:64, 0:64], 1.0)
        P.memset(bones[64:128, 64:128], 1.0)
        P.memset(m0[:], 1.0)
        P.memset(m0[:, 0:GS * 64:64], 0.0)
        P.memset(xp[:, 0:1], 0.0)
        P.memset(xp[:, L + 1:L + 2], 0.0)
        P.tt(mu[:, :, 2], mu[:, :, 0], mu[:, :, 1], ALU.add)
        P.ts(mu[:, :, 2], mu[:, :, 2], -1.0, ALU.mult, 1.0, ALU.add)
        P.ts(omka[:], pv[:, 5, :], -1.0, ALU.mult, 1.0, ALU.add)
        P.ts(hrk[:], pv[:, 6, :], 0.5, ALU.mult)

        def shifted(name, ci, dst, np_=128, fn=None):
            pa = proj(k, l, name, alt=(ci % 2 == 1))
            for tb in range(4):
                P.cp(xp[0:np_, 1 + tb * 512:1 + (tb + 1) * 512], pa[tb][0:np_, :], eng=("act" if tb % 2 else "dve"))
            t_ = sh32[0:np_, :]
            P.ts(t_, xp[0:np_, 1:L + 1], mu[0:np_, ci, 2:3], ALU.mult)
            P.stt(t_, xp[0:np_, 0:L], mu[0:np_, ci, 0:1], t_, ALU.mult, ALU.add)
            if fn is None:
                P.stt(dst[:], xp[0:np_, 2:L + 2], mu[0:np_, ci, 1:2], t_, ALU.mult, ALU.add)
            else:
                P.stt(t_, xp[0:np_, 2:L + 2], mu[0:np_, ci, 1:2], t_, ALU.mult, ALU.add)
                P.act(dst[:], t_, fn)

        shifted("rwd", 6, twd, 64, AF.Tanh)
        shifted("rad", 7, adx, 64)
        R_ = P.sb("r_R", [128, L], BF16, blk=512)
        KX = P.sb("r_KX", [128, L], BF16, blk=512)
        V_ = P.sb("r_V", [128, L], BF16, blk=512)
        KK = P.sb("r_KK", [128, L], BF16, blk=512)
        Vt = P.sb("r_Vt", [128, NCK, 64], BF16, blk=512)
        KS = P.sb("r_KS", [128, L], F32, blk=512)
        Y = xp[:, 1:L + 1]
        sq = P.sb("r_sq", [128, 512], F32)
        rn = P.sb("r_rn", [128, 512], F32)
        CH = [rwkv_tiles(k, e) for e in range(2)]

        class _V:
            def __init__(s_, t):
                s_.t = t

            def __getitem__(s_, key):
                return s_.t[:].rearrange("p a b -> p (a b)")[key]
        sq2, rn2 = _V(CH[0]["lw"]), _V(CH[0]["a"])
        for p in range(2):
            shifted("rr%d" % p, 0 + p, R_)
            shifted("rk%d" % p, 2 + p, KX)
            shifted("rv%d" % p, 4 + p, V_)
            P.ts(sh32[:], KX[:], pv[:, 4, p:p + 1], ALU.mult)

            def fin_k(tb, r):
                P.tt(KK[:, tb * 512:(tb + 1) * 512], sh32[:, tb * 512:(tb + 1) * 512], r[:], ALU.mult)
            norm_pipe(k, 4, lambda tb: sh32[:, tb * 512:(tb + 1) * 512], (sq, sq2), (rn, rn2), (k.PB[2], k.PB[3]),
                      bones, -0.5, 1.0, 1e-6, fin_k)
            for grp in range(4):
                ps = k.PB[grp % 2]
                for c in range(8):
                    ck = grp * 8 + c
                    tr2(P, ps, c, V_[:, ck * 64:(ck + 1) * 64], k.identb)
                P.cp(Vt[:, grp * 8:(grp + 1) * 8, :], v3(ps), eng=("act" if grp % 2 else "dve"))
            P.memset(xp[:, 1:L + 1], 0.0, eng="pool")
            P.memset(KS[:], 0.0, eng="pool")
            T = dict(pv=pv, omka=omka, w2=w2, a2=a2, msk=msk, id2=id2, m0=m0, twd=twd, adx=adx,
                     R=R_, KX=KX, KK=KK, Vt=Vt, KS=KS, Y=Y)
            run_interleaved([rwkv_chain(k, p, e, T, CH[e]) for e in range(2)], skew=RSKEW)
            for tb in range(4):
                sl = slice(tb * 512, (tb + 1) * 512)
                P.mm(k.PB[tb % 2][:, :], bones[:], Y[:, sl])
                P.stt(Y[:, sl], k.PB[tb % 2][:, :], -1.0 / 64, Y[:, sl], ALU.mult, ALU.add)

            def fin_y(tb, r):
                sl = slice(tb * 512, (tb + 1) * 512)
                P.tt(Y[:, sl], Y[:, sl], r[:], ALU.mult)
                P.ts(Y[:, sl], Y[:, sl], ln[:, 0, p:p + 1], ALU.mult, ln[:, 1, p:p + 1], ALU.add)
            norm_pipe(k, 4, lambda tb: Y[:, tb * 512:(tb + 1) * 512], (sq, sq2), (rn, rn2), (k.PB[2], k.PB[3]),
                      bones, -0.5, 1.0 / 64, RW_EPS, fin_y)
            for tb in range(4):
                sl = slice(tb * 512, (tb + 1) * 512)
                s_ = (sq, sq2)[tb % 2]
                r_ = (rn, rn2)[tb % 2]
                P.tt(s_[:], R_[:, sl], KS[:, sl], ALU.mult)
                P.ts(s_[:], s_[:], hrk[:, p:p + 1], ALU.mult, eng="pool")
                P.mm(k.PB[tb % 2][:, :], bones[:], s_[:])
                P.tt(r_[:], k.PB[tb % 2][:, :], V_[:, sl], ALU.mult)
                P.tt(mixc(k, 6 + p)[:, sl], Y[:, sl], r_[:], ALU.add)


RW_F32 = ("lw", "a", "km", "b", "cl", "e1", "e2", "dend", "AcT", "Tg")
RW_BF16 = ("kap", "rt", "kt_", "bt_", "ke", "be", "kapT", "keT", "nbeT", "N", "Akv", "Brk", "nBrb", "Rm",
           "nA", "nB", "nC", "nD", "X0", "P0", "WkT", "Wk", "Tgb", "Pg")


def rwkv_tiles(k, e):
    P = k.P
    G = {n: P.sb("r%d_%s" % (e, n), [128, GS, 64], F32) for n in RW_F32}
    for n in RW_BF16:
        G[n] = P.sb("r%d_%s" % (e, n), [128, GS, 64], BF16)
    G["gC"] = P.sb("r%d_gC" % e, [128, GS], F32)
    G["Tcar"] = P.sb("r%d_Tcar" % e, [128, 64], F32)
    return G


def rwkv_chain(k, p, e, T, G):
    P = k.P
    pv, omka, w2, a2, msk, id2, m0, twd, adx, R_, KX, KK, Vt, KS, Y = (T[n] for n in (
        "pv", "omka", "w2", "a2", "msk", "id2", "m0", "twd", "adx", "R", "KX", "KK", "Vt", "KS", "Y"))
    B = k.PA if e == 0 else k.PB
    W = GS * 64
    e_ = 63 if e == 0 else 0
    idb = bcm(id2[:, :], GS)
    gC, Tcar = G["gC"], G["Tcar"]

    def f2(t):
        return t[:].rearrange("p a b -> p (a b)")

    def w3(ps):
        return v3(ps, GS)
    P.memset(Tcar[:], 0.0)
    gorder = range(NG) if e == 0 else range(NG - 1, -1, -1)
    pc = slice(p * 128, (p + 1) * 128)
    for grp in gorder:
        sl = slice(grp * W, (grp + 1) * W)
        c0 = grp * GS
        P.mm(B[0][:, 0:W], w2[:, e, pc], twd[:, sl])
        P.mm(B[1][:, 0:W], a2[:, e, pc], adx[:, sl])
        P.act(f2(G["lw"]), B[0][:, 0:W], AF.Sigmoid, bias=pv[:, 0 + e, p:p + 1])
        P.act(f2(G["a"]), B[1][:, 0:W], AF.Sigmoid, bias=pv[:, 2 + e, p:p + 1])
        P.ts(f2(G["km"]), f2(G["a"]), pv[:, 5, p:p + 1], ALU.mult, omka[:, p:p + 1], ALU.add)
        P.tt(f2(G["km"]), f2(G["km"]), KX[:, sl], ALU.mult)
        P.tt(f2(G["b"]), f2(G["a"]), KK[:, sl], ALU.mult)
        P.tt(KS[:, sl], KS[:, sl], f2(G["km"]), ALU.add, eng="pool")
        P.scan(f2(G["cl"]), m0[:], f2(G["lw"]), 0.0, ALU.mult, ALU.add)
        if e == 1:
            P.tt(G["e1"][:], bc3(G["cl"][:, :, 63], 64), G["cl"][:], ALU.subtract)
            P.tt(G["cl"][:], G["e1"][:], G["lw"][:], ALU.add)
        yield
        P.act(G["e1"][:], G["cl"][:], AF.Exp, scale=-DEC)
        P.act(G["e2"][:], G["cl"][:], AF.Exp, scale=DEC)
        P.tt(f2(G["rt"]), f2(G["e1"]), R_[:, sl], ALU.mult)
        P.tt(G["kt_"][:], G["e2"][:], G["km"][:], ALU.mult)
        P.tt(G["bt_"][:], G["e2"][:], G["b"][:], ALU.mult)
        P.tt(G["dend"][:], G["cl"][:], G["lw"][:], ALU.subtract)
        P.act(G["dend"][:], G["dend"][:], AF.Exp, scale=-DEC)
        P.tt(f2(G["kap"]), f2(G["dend"]), KK[:, sl], ALU.mult)
        P.cp(gC[:], G["e1"][:, :, e_], eng="pool")
        P.tt(G["dend"][:], bc3(G["cl"][:, :, e_], 64), G["cl"][:], ALU.subtract)
        P.act(G["dend"][:], G["dend"][:], AF.Exp, scale=-DEC)
        P.tt(G["ke"][:], G["dend"][:], G["km"][:], ALU.mult)
        P.tt(G["be"][:], G["dend"][:], G["b"][:], ALU.mult, eng="pool")
        yield
        for src, dst, sc in ((G["kap"], G["kapT"], 1.0), (G["ke"], G["keT"], 1.0), (G["be"], G["nbeT"], -1.0)):
            ps = B[0] if sc == 1.0 and src is G["kap"] else (B[1] if sc == 1.0 else B[2])
            for c in range(GS):
                tr2(P, ps, c, src[:, c, :], k.identb)
            if sc == 1.0:
                P.cp(dst[:], w3(ps), eng="act")
            else:
                P.ts(dst[:], w3(ps), -1.0, ALU.mult)
        yield
        for c in range(GS):
            mm2(P, B[0], c, G["bt_"][:, c, :], G["kap"][:, c, :])
            mm2(P, B[1], c, G["kt_"][:, c, :], G["kap"][:, c, :])
            mm2(P, B[2], c, G["kt_"][:, c, :], G["rt"][:, c, :])
            mm2(P, B[3], c, G["bt_"][:, c, :], G["rt"][:, c, :])
        ms = bcm(msk[:, e, 0, :], GS)
        mi = bcm(msk[:, e, 1, :], GS)
        nms = bcm(msk[:, e, 2, :], GS)
        nmi = bcm(msk[:, e, 3, :], GS)
        P.tt(G["N"][:], w3(B[0]), nms, ALU.mult)
        P.tt(G["Akv"][:], w3(B[1]), ms, ALU.mult)
        P.tt(G["Brk"][:], w3(B[2]), mi, ALU.mult)
        P.tt(G["nBrb"][:], w3(B[3]), nmi, ALU.mult)
        yield
        for _ in neumann2(k, G["N"], G["Rm"], (G["nA"], G["nB"], G["nC"], G["nD"]), (B[0], B[1], B[2]), id2, GS):
            yield
        for c in range(GS):
            mm2(P, B[0], c, G["Akv"][:, c, :], Vt[:, c0 + c, :])
        P.cp(G["X0"][:], w3(B[0]), eng="act")
        yield
        for c in range(GS):
            mm2(P, B[0], c, G["Rm"][:, c, :], G["X0"][:, c, :])
            mm2(P, B[1], c, G["kapT"][:, c, :], G["Rm"][:, c, :])
            mm2(P, B[2], c, G["Rm"][:, c, :], G["kapT"][:, c, :])
        P.cp(G["P0"][:], w3(B[0]), eng="act")
        P.cp(G["WkT"][:], w3(B[1]), eng="dve")
        P.cp(G["Wk"][:], w3(B[2]), eng="act")
        yield
        for c in range(GS):
            mm2(P, B[3], c, G["Wk"][:, c, :], G["nbeT"][:, c, :])
        P.tt(G["AcT"][:], idb, bc3(gC[:], 64), ALU.mult, eng="pool")
        P.tt(G["AcT"][:], G["AcT"][:], w3(B[3]), ALU.add)
        yield
        corder = list(range(GS)) if e == 0 else list(range(GS - 1, -1, -1))
        Tg = G["Tg"]
        for n, c in enumerate(corder):
            if n == 0:
                P.cp(Tg[:, c, :], Tcar[:], eng="pool")
            ps = B[2 + n % 2]
            mm2(P, ps, 0, G["AcT"][:, c, :], Tg[:, c, :], start=True, stop=False)
            mm2(P, ps, 0, G["keT"][:, c, :], Vt[:, c0 + c, :], start=False, stop=False)
            mm2(P, ps, 0, G["nbeT"][:, c, :], G["P0"][:, c, :], start=False, stop=True)
            dst = Tcar[:] if n == GS - 1 else Tg[:, corder[n + 1], :]
            P.cp(dst, ps[:, 0:64], eng="act")
            yield
        P.cp(G["Tgb"][:], Tg[:], eng="act")
        for c in range(GS):
            mm2(P, B[0], c, G["WkT"][:, c, :], G["Tgb"][:, c, :])
        P.tt(G["Pg"][:], w3(B[0]), G["P0"][:], ALU.add)
        yield
        for c in range(GS):
            mm2(P, B[1], c, Vt[:, c0 + c, :], G["Brk"][:, c, :], start=True, stop=False)
            mm2(P, B[1], c, G["Tgb"][:, c, :], G["rt"][:, c, :], start=False, stop=False)
            mm2(P, B[1], c, G["Pg"][:, c, :], G["nBrb"][:, c, :], start=False, stop=True)
        P.tt(Y[:, sl], Y[:, sl], B[1][:, 0:W], ALU.add)
        yield


_CACHE = {}


def kernel(**inputs):
    inp = {k_: np.asarray(v) for k_, v in inputs.items()}
    if "k" not in _CACHE:
        _CACHE["k"] = build()
    k = _CACHE["k"]
    B = inp["x"].shape[0]
    base = host_inputs(inp, 0)
    in_maps = []
    for b in range(B):
        m = dict(base)
        m["x"] = np.ascontiguousarray(inp["x"][b], dtype=np.float32)
        m["pos"] = np.ascontiguousarray(inp["positions"][b].reshape(1, L).astype(np.int32))
        in_maps.append(m)
    res = run_bass_kernel_spmd(k.nc, in_maps, core_ids=list(range(B)))
    return np.stack([np.asarray(r["out"], dtype=np.float32) for r in res.results], axis=0)
```

```python
import numpy as np
import concourse.bass as bass
import concourse.mybir as mybir
from concourse.bass_utils import run_bass_kernel_spmd
from contextlib import ExitStack

F32 = mybir.dt.float32
BF16 = mybir.dt.bfloat16
I32 = mybir.dt.int32
AF = mybir.ActivationFunctionType
ALU = mybir.AluOpType
AX = mybir.AxisListType
DTSIZE = {F32: 4, BF16: 2, I32: 4}


class _Op:
    __slots__ = ("eng", "emit", "deps", "idx", "needed", "sigval", "dsem", "dval", "isdma")

    def __init__(self, eng, emit, isdma=False):
        self.eng = eng
        self.emit = emit
        self.deps = []
        self.idx = -1
        self.needed = False
        self.sigval = 0
        self.dsem = None
        self.dval = 0
        self.isdma = isdma


class _Blk:
    __slots__ = ("w", "r")

    def __init__(self):
        self.w = None
        self.r = {}


class Prog:
    ENGS = ("pe", "dve", "act", "pool", "sp")
    NDMA = 48
    NHW = 32

    def __init__(self, nc, stack):
        self.nc = nc
        self.stack = stack
        self.ops = {e: [] for e in self.ENGS}
        self.track = {}
        self.seen = {e: {} for e in self.ENGS}
        self.seen_dma = {e: set() for e in self.ENGS}
        self.dma_last = [None] * self.NDMA
        self.dma_uses = [0] * self.NDMA
        self.dma_rr = 0
        self.dma_rr_sw = 0
        self.ndma_ops = 0
        self.untracked = set()
        self.out_dmas = []
        self.dma_pending = []
        self.last_compute = {}

    def sb(self, name, shape, dtype=F32, blk=None):
        self.uid = getattr(self, "uid", 0) + 1
        name = "s%d_%s" % (self.uid, name)
        t = self.stack.enter_context(self.nc.sbuf_tensor(name, list(shape), dtype))
        self._register(name, shape, dtype, blk)
        return t

    def ps(self, name, shape=(128, 512), dtype=F32, blk=None):
        self.uid = getattr(self, "uid", 0) + 1
        name = "p%d_%s" % (self.uid, name)
        t = self.stack.enter_context(self.nc.psum_tensor(name, list(shape), dtype))
        self._register(name, shape, dtype, blk)
        return t

    def _register(self, name, shape, dtype, blk):
        row = int(np.prod(shape[1:])) * DTSIZE[dtype]
        bb = row if blk is None else blk * DTSIZE[dtype]
        nb = (row + bb - 1) // bb
        self.track[name] = (bb, row, [_Blk() for _ in range(nb)])

    def dram_track(self, name, total_bytes, blk_bytes):
        nb = (total_bytes + blk_bytes - 1) // blk_bytes
        self.track[name] = (blk_bytes, -1, [_Blk() for _ in range(nb)])

    def _blocks(self, ap):
        name = ap.tensor.name
        if name not in self.track:
            return ()
        bb, row, blks = self.track[name]
        if len(blks) == 1:
            return blks
        ds = DTSIZE[ap.dtype]
        pat = ap.ap
        if row < 0:
            lo = hi = ap.offset
            for step, cnt in pat:
                ext = step * (cnt - 1)
                if ext < 0:
                    lo += ext
                else:
                    hi += ext
            return blks[(lo * ds) // bb:(hi * ds) // bb + 1]
        rowel = row // ds
        foff = ap.offset % rowel
        lo = hi = foff
        for step, cnt in pat[1:]:
            ext = step * (cnt - 1)
            if ext < 0:
                lo += ext
            else:
                hi += ext
        b0 = (lo * ds) // bb
        b1 = (hi * ds) // bb
        return blks[b0:b1 + 1]

    def _dep(self, x, y):
        if y is None or y is x:
            return
        e = x.eng
        if y.isdma:
            if id(y) in self.seen_dma[e]:
                return
            self.seen_dma[e].add(id(y))
            x.deps.append(y)
            return
        if y.eng == "pe" and e == "pe":
            return
        if y.idx <= self.seen[e].get(y.eng, -1):
            return
        self.seen[e][y.eng] = y.idx
        y.needed = True
        x.deps.append(y)

    def add(self, eng, emit, reads=(), writes=(), isdma=False):
        x = _Op(eng, emit, isdma)
        x.idx = len(self.ops[eng])
        rb = []
        for ap in reads:
            if ap is None or isinstance(ap, (int, float)):
                continue
            rb.extend(self._blocks(ap))
        wb = []
        for ap in writes:
            wb.extend(self._blocks(ap))
        for ap in reads:
            if ap is None or isinstance(ap, (int, float)) or not ap.tensor.name.startswith("p"):
                continue
            for b in self._blocks(ap):
                for key, y in b.r.items():
                    if key != eng:
                        self._dep(x, y)
        for b in rb:
            self._dep(x, b.w)
        for b in wb:
            self._dep(x, b.w)
            for y in b.r.values():
                self._dep(x, y)
        if isdma:
            if eng == "pool":
                s = self.NHW + self.dma_rr_sw
                self.dma_rr_sw = (self.dma_rr_sw + 1) % (self.NDMA - self.NHW)
            else:
                s = self.dma_rr
                self.dma_rr = (self.dma_rr + 1) % self.NHW
            self._dep(x, self.dma_last[s])
            self.dma_last[s] = x
            self.dma_uses[s] += 1
            x.dsem = s
            x.dval = 16 * self.dma_uses[s]
            self.ndma_ops += 1
        key = id(x) if isdma else eng
        for b in rb:
            b.r[key] = x
        for b in wb:
            b.w = x
            b.r = {}
        self.ops[eng].append(x)
        if isdma:
            self.dma_pending.append(x)
        else:
            self.last_compute[eng] = x
        return x

    def barrier(self):
        lasts = dict(self.last_compute)
        pend = list(self.dma_pending)
        self.dma_pending = []
        for e in self.ENGS:
            b = _Op(e, None)
            b.idx = len(self.ops[e])
            for e2, y in lasts.items():
                if e2 == e and e == "pe":
                    continue
                self._dep(b, y)
            for y in pend:
                self._dep(b, y)
            self.ops[e].append(b)

    def mm(self, out, lhsT, rhs, start=True, stop=True):
        return self.add("pe", lambda e: e.matmul(out, lhsT, rhs, start=start, stop=stop),
                        reads=(lhsT, rhs), writes=(out,))

    def tr(self, out, in_, ident):
        return self.add("pe", lambda e: e.transpose(out, in_, ident), reads=(in_, ident), writes=(out,))

    def tt(self, out, in0, in1, op, eng="dve"):
        return self.add(eng, lambda e: e.tensor_tensor(out, in0, in1, op), reads=(in0, in1), writes=(out,))

    def ts(self, out, in0, s1, op0, s2=None, op1=None, eng="dve", accum_out=None):
        kw = {}
        if eng == "pool" and op1 is None:
            if op0 == ALU.mult:
                s2, op1 = 0.0, ALU.add
            elif op0 == ALU.add:
                s2, op1 = 1.0, ALU.mult
        if op1 is not None:
            kw["op1"] = op1
        if accum_out is not None:
            kw["accum_out"] = accum_out
        w = (out,) if accum_out is None else (out, accum_out)
        return self.add(eng, lambda e: e.tensor_scalar(out, in0, s1, s2, op0, **kw),
                        reads=(in0, s1, s2), writes=w)

    def stt(self, out, in0, scalar, in1, op0, op1, accum_out=None):
        kw = {}
        if accum_out is not None:
            kw["accum_out"] = accum_out
        w = (out,) if accum_out is None else (out, accum_out)
        return self.add("dve", lambda e: e.scalar_tensor_tensor(out, in0, scalar, in1, op0, op1, **kw),
                        reads=(in0, scalar, in1), writes=w)

    def cp(self, out, in_, eng="dve"):
        if eng == "act":
            return self.add("act", lambda e: e.copy(out, in_), reads=(in_,), writes=(out,))
        return self.add(eng, lambda e: e.tensor_copy(out, in_), reads=(in_,), writes=(out,))

    def act(self, out, in_, func, bias=0.0, scale=1.0, accum_out=None):
        kw = {}
        if accum_out is not None:
            kw["accum_out"] = accum_out
        w = (out,) if accum_out is None else (out, accum_out)
        return self.add("act", lambda e: e.activation(out, in_, func, bias=bias, scale=scale, **kw),
                        reads=(in_, bias, scale), writes=w)

    def red(self, out, in_, op, axis=AX.X, eng="dve"):
        return self.add(eng, lambda e: e.tensor_reduce(out, in_, axis, op), reads=(in_,), writes=(out,))

    def recip(self, out, in_):
        return self.add("dve", lambda e: e.reciprocal(out, in_), reads=(in_,), writes=(out,))

    def rpow(self, out, in_, power, scale=1.0, bias=0.0):
        self.act(out, in_, AF.Ln, bias=bias, scale=scale)
        return self.act(out, out, AF.Exp, scale=power)

    def memset(self, ap, val, eng="dve"):
        return self.add(eng, lambda e: e.memset(ap, val), writes=(ap,))

    def scan(self, out, d0, d1, init, op0, op1):
        return self.add("dve", lambda e: e.tensor_tensor_scan(out, d0, d1, init, op0, op1),
                        reads=(d0, d1, init), writes=(out,))

    def dma(self, out, in_, eng="sp", is_output=False):
        x = self.add(eng, lambda e: e.dma_start(out=out, in_=in_), reads=(in_,), writes=(out,), isdma=True)
        if is_output:
            self.out_dmas.append(x)
        return x

    def finish(self):
        nc = self.nc
        fin = _Op("sp", None)
        fin.idx = len(self.ops["sp"])
        for y in self.out_dmas:
            self._dep(fin, y)
        self.ops["sp"].append(fin)
        sems = {}
        for e in ("pe", "dve", "act", "pool"):
            sems[e] = self.stack.enter_context(nc.semaphore("s_" + e))
        dsems = [self.stack.enter_context(nc.semaphore("d%d" % i)) for i in range(self.NDMA)]
        for e in ("pe", "dve", "act", "pool"):
            c = 0
            for x in self.ops[e]:
                if x.isdma:
                    continue
                if x.needed:
                    c += 1
                    x.sigval = c
            self.stats_sig = getattr(self, "stats_sig", {})
            self.stats_sig[e] = c
        ops = self.ops

        def replay(e, engobj):
            for x in ops[e]:
                for y in x.deps:
                    if y.isdma:
                        engobj.wait_ge(dsems[y.dsem], y.dval)
                    else:
                        engobj.wait_ge(sems[y.eng], y.sigval)
                if x.emit is None:
                    continue
                ins = x.emit(engobj)
                if x.isdma:
                    ins.then_inc(dsems[x.dsem], 16)
                elif x.needed:
                    ins.then_inc(sems[e], 1)

        with nc.Block() as block:
            @block.tensor
            def _(eng):
                replay("pe", eng)

            @block.vector
            def _(eng):
                replay("dve", eng)

            @block.scalar
            def _(eng):
                replay("act", eng)

            @block.gpsimd
            def _(eng):
                replay("pool", eng)

            @block.sync
            def _(eng):
                replay("sp", eng)


L = 2048
D = 1024
NT = L // 128
DEPTH = 2
N_IN = 3448
EPS = 1e-6

OFF = dict(gate=0, gdn_q=1024, gdn_k=1408, gdn_v=1792, gdn_a=2176, gdn_b=2188, mla_cq=2200, mla_ckv=2392,
           mla_kr=2520, rw_r=2552, rw_k=2808, rw_v=3064, rw_wd=3320, rw_ad=3384)


def chunk_table():
    ch = []
    for h in range(3):
        ch.append(("gq%d" % h, [(0, OFF["gdn_q"] + h * 128, 128)]))
        ch.append(("gk%d" % h, [(0, OFF["gdn_k"] + h * 128, 128)]))
        ch.append(("gv%d" % h, [(0, OFF["gdn_v"] + h * 128, 128)]))
    ch.append(("gab", [(0, OFF["gdn_a"], 6), (32, OFF["gdn_a"] + 6, 6), (64, OFF["gdn_b"], 6), (96, OFF["gdn_b"] + 6, 6)]))
    ch.append(("cq0", [(0, OFF["mla_cq"], 128)]))
    ch.append(("cq1", [(0, OFF["mla_cq"] + 128, 64)]))
    ch.append(("ckv", [(0, OFF["mla_ckv"], 128)]))
    ch.append(("kr", [(0, OFF["mla_kr"], 32)]))
    for i in range(2):
        ch.append(("rr%d" % i, [(0, OFF["rw_r"] + i * 128, 128)]))
        ch.append(("rk%d" % i, [(0, OFF["rw_k"] + i * 128, 128)]))
        ch.append(("rv%d" % i, [(0, OFF["rw_v"] + i * 128, 128)]))
    ch.append(("rwd", [(0, OFF["rw_wd"], 64)]))
    ch.append(("rad", [(0, OFF["rw_ad"], 64)]))
    for i in range(8):
        ch.append(("g%d" % i, [(0, OFF["gate"] + i * 128, 128)]))
    return ch


CHUNKS = chunk_table()
CH_IDX = {n: i for i, (n, _) in enumerate(CHUNKS)}
NCH = len(CHUNKS)


def host_win(w_in):
    out = np.zeros((DEPTH, NCH, 128, 8, 128), np.float32)
    for ci, (_, parts) in enumerate(CHUNKS):
        for dst, src, w in parts:
            blk = w_in[:, :, src:src + w].reshape(DEPTH, 8, 128, w)
            out[:, ci, :, :, dst:dst + w] = blk.transpose(0, 2, 1, 3)
    return out


class K:
    pass


def build(depth=DEPTH, mixers=("gdn", "mla", "rwkv"), dbg=False):
    nc = bass.Bass("TRN2", target_bir_lowering=False)
    k = K()
    k.nc = nc
    k.dbg = dbg
    k.dbg_outs = []
    k.cut = 99

    def din(name, shape, dt=F32):
        return nc.dram_tensor(name, list(shape), dt, kind="ExternalInput").ap()

    k.x_d = din("x", [L, D])
    k.win_d = din("win", [DEPTH, NCH, 128, 8, 128])
    k.normg_d = din("normg", [DEPTH, 128, 8])
    k.wout_d = din("wout", [DEPTH, 128, 8, 1024])
    k.fing_d = din("fing", [1, D])
    k.ident_d = din("ident", [128, 128])
    k.out_d = nc.dram_tensor("out", [L, D], F32, kind="ExternalOutput").ap()
    mla_decl(k)
    gdn_decl(k)
    rwkv_decl(k)

    with ExitStack() as st:
        P = Prog(nc, st)
        k.P = P
        k.xscr = nc.dram_tensor("xscr", [L, D], F32, kind="Internal").ap()
        P.dram_track("xscr", L * D * 4, 128 * D * 4)
        k.hT = P.sb("hT", [128, 8, L], BF16, blk=512)
        k.ident = P.sb("ident", [128, 128], F32)
        k.identb = P.sb("identb", [128, 128], BF16)
        k.normg = P.sb("normg", [128, DEPTH, 8], F32)
        k.wst = [P.sb("wst%d" % i, [128, 8, 128], F32) for i in range(2)]
        k.wbf = [P.sb("wbf%d" % i, [128, 8, 128], BF16) for i in range(2)]
        k.wrr = 0
        k.PAW = [P.ps("paw%d" % i, [128, 1024], F32, blk=512) for i in range(2)]
        k.PA = [k.PAW[i // 2][:, (i % 2) * 512:(i % 2 + 1) * 512] for i in range(4)]
        k.PB = [P.ps("pb%d" % i, [128, 512], F32) for i in range(4)]

        P.dma(k.ident[:], k.ident_d[:])
        P.cp(k.identb[:], k.ident[:])
        for l in range(DEPTH):
            P.dma(k.normg[:, l, :], k.normg_d[l])

        for l in range(depth):
            phase_a(k, l)
            with scope(k):
                k.mix_r = P.sb("mix_r", [128, 2, L], BF16, blk=512)
                if "rwkv" in mixers:
                    rwkv_phase(k, l)
                else:
                    P.memset(k.mix_r[:].rearrange("p a b -> p (a b)"), 1.0)
                with scope(k):
                    k.mix_m = P.sb("mix_m", [128, 3, L], BF16, blk=512)
                    if "mla" in mixers:
                        mla_phase(k, l)
                    else:
                        P.memset(k.mix_m[:].rearrange("p a b -> p (a b)"), 1.0)
                    with scope(k):
                        k.mix_g = P.sb("mix_g", [128, 3, L], BF16, blk=512)
                        if "gdn" in mixers:
                            gdn_phase(k, l)
                        else:
                            P.memset(k.mix_g[:].rearrange("p a b -> p (a b)"), 1.0)
                        if k.dbg:
                            for nm, t_, n_ in (("g", k.mix_g, 3), ("m", k.mix_m, 3), ("r", k.mix_r, 2)):
                                dump(k, "mix_%s%d" % (nm, l), t_[:].rearrange("p a b -> p (a b)"), [128, n_ * L])
                        phase_z(k, l, last=(l == depth - 1))
        P.finish()
        print("ops:", {e: len(v) for e, v in P.ops.items()}, "sig:", P.stats_sig, "dma:", P.ndma_ops)
    return k


def mixc(k, c):
    if c < 3:
        return k.mix_g[:, c, :]
    if c < 6:
        return k.mix_m[:, c - 3, :]
    return k.mix_r[:, c - 6, :]


def dump(k, name, ap, shape=None):
    if not k.dbg:
        return
    P = k.P
    shape = list(ap.shape) if shape is None else shape
    d = k.nc.dram_tensor("dbg_" + name, shape, ap.dtype, kind="ExternalOutput").ap()
    P.dma(d[:] if len(shape) == 2 else d, ap, is_output=True)
    k.dbg_outs.append("dbg_" + name)


def scope(k):
    class _S:
        def __enter__(s):
            s.old = k.P.stack
            s.st = ExitStack()
            s.st.__enter__()
            k.P.stack = s.st
            return s

        def __exit__(s, *a):
            k.P.barrier()
            k.P.stack = s.old
            s.st.__exit__(*a)
            return False
    return _S()


def phase_a(k, l):
    P = k.P
    with scope(k):
        ssq = P.sb("a_ssq", [128, NT])
        rs = P.sb("a_rs", [128, NT])
        rstd = P.sb("a_rstd", [128, NT])
        junk = [P.sb("a_junk%d" % i, [128, D], BF16) for i in range(2)]
        xs = [P.sb("a_xs%d" % i, [128, D], BF16) for i in range(2)]
        xin = [P.sb("a_xin%d" % i, [128, D], F32) for i in range(3)]
        src = k.x_d if l == 0 else k.xscr

        def stage1(tt):
            b = tt % 2
            xt_ = xin[tt % 3]
            P.dma(xt_[:], src[tt * 128:(tt + 1) * 128, :])
            P.act(junk[b][:], xt_[:], AF.Square, accum_out=ssq[:, tt:tt + 1])
            P.act(rs[:, tt:tt + 1], ssq[:, tt:tt + 1], AF.Sqrt, bias=EPS, scale=1.0 / D)
            P.recip(rstd[:, tt:tt + 1], rs[:, tt:tt + 1])
            P.ts(xs[b][:], xt_[:], rstd[:, tt:tt + 1], ALU.mult)

        def stage2(tt):
            b = tt % 2
            pt = k.PB[b][:].bitcast(BF16)
            for dc in range(8):
                P.tr(pt[:, dc * 128:(dc + 1) * 128], xs[b][:, dc * 128:(dc + 1) * 128], k.identb[:])
            P.cp(k.hT[:, :, tt * 128:(tt + 1) * 128], pt[:].rearrange("p (a b) -> p a b", a=8),
                 eng=("act" if tt % 2 == 0 else "dve"))
        stage1(0)
        for tt in range(NT):
            if tt + 1 < NT:
                stage1(tt + 1)
            stage2(tt)


def proj(k, l, name, alt=False):
    P = k.P
    BK = k.PB if alt else k.PA
    ci = CH_IDX[name]
    b = k.wrr
    k.wrr ^= 1
    P.dma(k.wst[b][:], k.win_d[l, ci], eng="sp")
    gb = k.normg[:, l, :].unsqueeze(2).broadcast_to([128, 8, 128])
    P.tt(k.wbf[b][:], k.wst[b][:], gb, ALU.mult, eng="pool")
    for tb in range(4):
        for dc in range(8):
            P.mm(BK[tb][:, :], k.wbf[b][:, dc, :], k.hT[:, dc, tb * 512:(tb + 1) * 512], start=(dc == 0), stop=(dc == 7))
    return BK


ZQ = "act"


def phase_z(k, l, last):
    P = k.P
    with scope(k):
        wst = P.sb("z_wst", [128, 8, 512], F32)
        wob = P.sb("z_wob", [128, 8, 1024], BF16, blk=512)
        sg = [P.sb("z_sg%d" % i, [128, L], BF16, blk=512) for i in range(2)]
        for nb in range(2):
            P.dma(wst[:], k.wout_d[l, :, :, nb * 512:(nb + 1) * 512])
            P.cp(wob[:, :, nb * 512:(nb + 1) * 512], wst[:], eng="act")
        for gc in range(8):
            pa = proj(k, l, "g%d" % gc, alt=(gc % 2 == 1))
            s = sg[gc % 2]
            for tb in range(4):
                P.act(s[:, tb * 512:(tb + 1) * 512], pa[tb][:, :], AF.Silu)
                mc = mixc(k, gc)[:, tb * 512:(tb + 1) * 512]
                P.tt(mc, mc, s[:, tb * 512:(tb + 1) * 512], ALU.mult)
        xin = [P.sb("z_xin%d" % i, [128, D], F32) for i in range(3)]
        src = k.x_d if l == 0 else k.xscr
        if last:
            ssq = P.sb("f_ssq", [128, NT])
            rs = P.sb("f_rs", [128, NT])
            rstd = P.sb("f_rstd", [128, NT])
            junk = [P.sb("f_junk%d" % i, [128, D], BF16) for i in range(2)]
            gf = P.sb("f_g", [128, D])
            ot = [P.sb("f_o%d" % i, [128, D]) for i in range(2)]
            P.dma(gf[:], k.fing_d[0:1, :].partition_broadcast(128))
        def za(tt):
            xt_ = xin[tt % 3]
            P.dma(xt_[:], src[tt * 128:(tt + 1) * 128, :])
            for nb in range(2):
                ps = k.PB[(tt * 2 + nb) % 4]
                for kc in range(8):
                    P.mm(ps[:, :], mixc(k, kc)[:, tt * 128:(tt + 1) * 128], wob[:, kc, nb * 512:(nb + 1) * 512],
                         start=(kc == 0), stop=(kc == 7))
                xs = xt_[:, nb * 512:(nb + 1) * 512]
                P.tt(xs, xs, ps[:, :], ALU.add)
            if not last:
                P.dma(k.xscr[tt * 128:(tt + 1) * 128, :], xt_[:], eng=ZQ)
            else:
                b = tt % 2
                P.act(junk[b][:], xt_[:], AF.Square, accum_out=ssq[:, tt:tt + 1])
                P.act(rs[:, tt:tt + 1], ssq[:, tt:tt + 1], AF.Sqrt, bias=EPS, scale=1.0 / D)

        def zb(tt):
            if last:
                xt_ = xin[tt % 3]
                b = tt % 2
                P.recip(rstd[:, tt:tt + 1], rs[:, tt:tt + 1])
                P.stt(ot[b][:], xt_[:], rstd[:, tt:tt + 1], gf[:], ALU.mult, ALU.mult)
                P.dma(k.out_d[tt * 128:(tt + 1) * 128, :], ot[b][:], eng=ZQ, is_output=True)
        za(0)
        for tt in range(NT):
            if tt + 1 < NT:
                za(tt + 1)
            zb(tt)


def host_inputs(inp, b):
    m = {}
    m["x"] = np.ascontiguousarray(inp["x"][b])
    m["win"] = host_win(inp["w_in"])
    m["normg"] = np.ascontiguousarray(inp["norm_g"].reshape(DEPTH, 8, 128).transpose(0, 2, 1))
    m["wout"] = np.ascontiguousarray(inp["w_out"].reshape(DEPTH, 8, 128, 1024).transpose(0, 2, 1, 3))
    m["fing"] = np.ascontiguousarray(inp["final_norm_g"].reshape(1, D))
    m["ident"] = np.eye(128, dtype=np.float32)
    host_mla(inp, b, m)
    host_gdn(inp, b, m)
    host_rwkv(inp, b, m)
    return m


TWO_PI = 2.0 * np.pi


def host_mla(inp, b, m):
    half = 16
    inv_freq = (10000.0 ** (-np.arange(half, dtype=np.float32) / half)).astype(np.float32)
    invf = np.zeros((32, 1), np.float32)
    invf[:, 0] = np.tile(inv_freq, 2) / np.float32(TWO_PI)
    m["invf"] = invf
    rm = np.zeros((32, 32), np.float32)
    for i in range(16):
        rm[i, i + 16] = -1.0
        rm[i + 16, i] = 1.0
    m["rmT"] = np.ascontiguousarray(rm.T)
    m["pos"] = np.ascontiguousarray(inp["positions"][b].reshape(1, L).astype(np.int32))
    wuq = inp["mla_w_uq"]
    o = np.zeros((DEPTH, 128, 2, 6, 128), np.float32)
    for h in range(6):
        nope = wuq[:, :, h * 96:h * 96 + 64]
        rope = wuq[:, :, h * 96 + 64:h * 96 + 96]
        o[:, :, 0, h, 64:128] = nope[:, 0:128]
        o[:, 0:64, 1, h, 64:128] = nope[:, 128:192]
        o[:, :, 0, h, 0:32] = rope[:, 0:128]
        o[:, 0:64, 1, h, 0:32] = rope[:, 128:192]
    m["wuq"] = o
    gq = np.zeros((DEPTH, 128, 2), np.float32)
    gq[:, :, 0] = inp["mla_q_norm_g"][:, 0:128]
    gq[:, 0:64, 1] = inp["mla_q_norm_g"][:, 128:192]
    m["gq"] = gq
    wukv = inp["mla_w_ukv"]
    wk = np.zeros((DEPTH, 128, 6, 128), np.float32)
    wv = np.zeros((DEPTH, 128, 6, 64), np.float32)
    for h in range(6):
        wk[:, :, h, 64:128] = wukv[:, :, h * 128:h * 128 + 64]
        wv[:, :, h, :] = wukv[:, :, h * 128 + 64:h * 128 + 128]
    m["wuk"] = wk
    m["wuv"] = wv
    m["gkv"] = np.ascontiguousarray(inp["mla_kv_norm_g"].reshape(DEPTH, 128, 1))


def mla_decl(k):
    nc = k.nc

    def din(name, shape, dt=F32):
        return nc.dram_tensor(name, list(shape), dt, kind="ExternalInput").ap()
    k.invf_d = din("invf", [32, 1])
    k.rmT_d = din("rmT", [32, 32])
    k.pos_d = din("pos", [1, L], I32)
    k.wuq_d = din("wuq", [DEPTH, 128, 2, 6, 128])
    k.gq_d = din("gq", [DEPTH, 128, 2])
    k.wuk_d = din("wuk", [DEPTH, 128, 6, 128])
    k.wuv_d = din("wuv", [DEPTH, 128, 6, 64])
    k.gkv_d = din("gkv", [DEPTH, 128, 1])


def latent_norm(k, l, names, nfeat, outs, ones):
    P = k.P
    sq = [P.sb("ln_sq%d" % i, [128, 512]) for i in range(2)]
    rq = [P.sb("ln_rq%d" % i, [128, 512]) for i in range(2)]
    n = len(names)
    for i, nm in enumerate(names):
        pa = proj(k, l, nm)
        for tb in range(4):
            s = sq[tb % 2]
            sk = ""
            if "a" not in sk:
                P.act(s[:], pa[tb][:, :], AF.Square)
            if "c" not in sk:
                P.cp(outs[i][:, tb * 512:(tb + 1) * 512], pa[tb][:, :], eng="dve")
            if "m" not in sk:
                P.mm(k.PB[tb][:, :], ones[:], s[:], start=(i == 0), stop=(i == n - 1))
    c2 = 9
    if c2 < 1:
        return
    for tb in range(4):
        r = rq[tb % 2]
        P.rpow(r[:], k.PB[tb][:, :], -0.5, scale=1.0 / nfeat, bias=EPS)
        for i in range(n):
            o = outs[i][:, tb * 512:(tb + 1) * 512]
            P.tt(o, o, r[:], ALU.mult)


def mla_phase(k, l):
    P = k.P
    SC = 96.0 ** -0.5
    with scope(k):
        cqn0 = P.sb("m_cqn0", [128, L], BF16, blk=512)
        cqn1 = P.sb("m_cqn1", [128, L], BF16, blk=512)
        ckvn = P.sb("m_ckvn", [128, L], BF16, blk=512)
        krope = P.sb("m_krope", [32, L], BF16, blk=512)
        cos2 = P.sb("m_cos2", [32, L], BF16, blk=512)
        sin2 = P.sb("m_sin2", [32, L], BF16, blk=512)
        wq = P.sb("m_wq", [128, 2, 6, 128], BF16)
        wk = P.sb("m_wk", [128, 6, 128], BF16)
        wv = P.sb("m_wv", [128, 6, 64], BF16)
        ones = P.sb("m_ones", [128, 128], F32)
        onesk = P.sb("m_onesk", [128, 128], F32)
        rmT = P.sb("m_rmT", [32, 32], F32)
        P.memset(ones[:], 1.0)
        P.memset(onesk[:], 1.0)
        P.memset(onesk[32:64, :], 0.0)
        P.dma(rmT[:], k.rmT_d[:])
        with scope(k):
            st = P.sb("m_st", [128, 2, 6, 128], F32)
            g = P.sb("m_g", [128, 4], F32)
            P.dma(st[:], k.wuq_d[l])
            P.dma(g[:, 0:2], k.gq_d[l])
            P.dma(g[:, 2:3], k.gkv_d[l])
            for kc in range(2):
                P.ts(wq[:, kc].rearrange("p a b -> p (a b)"), st[:, kc].rearrange("p a b -> p (a b)"),
                     g[:, kc:kc + 1], ALU.mult)
            st2 = P.sb("m_st2", [128, 6, 128], F32)
            P.dma(st2[:], k.wuk_d[l])
            P.ts(wk[:].rearrange("p a b -> p (a b)"), st2[:].rearrange("p a b -> p (a b)"), g[:, 2:3], ALU.mult)
            st3 = P.sb("m_st3", [128, 6, 64], F32)
            P.dma(st3[:], k.wuv_d[l])
            P.ts(wv[:].rearrange("p a b -> p (a b)"), st3[:].rearrange("p a b -> p (a b)"), g[:, 2:3], ALU.mult)
        if k.cut < 1:
            return
        with scope(k):
            latent_norm(k, l, ["cq0", "cq1"], 192, [cqn0, cqn1], ones)
            latent_norm(k, l, ["ckv"], 128, [ckvn], ones)
        if k.cut < 2:
            return
        with scope(k):
            invf = P.sb("m_invf", [32, 1], F32)
            P.dma(invf[:], k.invf_d[:])
            pa = proj(k, l, "kr")

            def rope_blk(tb):
                sl = slice(tb * 512, (tb + 1) * 512)
                posi = P.sb("m_posi%d" % tb, [32, 512], I32)
                y = P.sb("m_y%d" % tb, [32, 512], F32)
                yi = P.sb("m_yi%d" % tb, [32, 512], I32)
                fr = P.sb("m_fr%d" % tb, [32, 512], F32)
                kr = P.sb("m_kr%d" % tb, [32, 512], F32)
                t1 = P.sb("m_t1%d" % tb, [32, 512], F32)
                t2 = P.sb("m_t2%d" % tb, [32, 512], F32)
                P.dma(posi[:], k.pos_d[0:1, sl].partition_broadcast(32))
                P.cp(kr[:], pa[tb][0:32, :], eng="act")
                yield
                P.cp(y[:], posi[:])
                P.mm(k.PB[tb][0:32, :], rmT[:], kr[:])
                yield
                P.ts(y[:], y[:], invf[:, 0:1], ALU.mult)
                yield
                for off, dst in ((0.0, sin2), (0.25, cos2)):
                    if off != 0.0:
                        P.ts(y[:], y[:], off, ALU.add)
                        yield
                    P.cp(yi[:], y[:])
                    yield
                    P.cp(fr[:], yi[:])
                    yield
                    P.tt(fr[:], y[:], fr[:], ALU.subtract)
                    yield
                    P.act(dst[:, sl], fr[:], AF.Sin, scale=TWO_PI * (1.0 - 1e-6))
                    yield
                P.tt(t1[:], kr[:], cos2[:, sl], ALU.mult)
                P.tt(t2[:], k.PB[tb][0:32, :], sin2[:, sl], ALU.mult)
                yield
                P.tt(krope[:, sl], t1[:], t2[:], ALU.add)
                yield
            run_interleaved([rope_blk(tb) for tb in range(4)])
        if k.cut < 3:
            return
        kT = [P.sb("m_kT%d" % i, [128, L], BF16, blk=512) for i in range(2)]
        qT = [P.sb("m_qT%d" % i, [128, L], BF16, blk=512) for i in range(2)]
        Vh = [P.sb("m_V%d" % i, [128, NT, 96], BF16) for i in range(2)]
        pT = [P.sb("m_pT%d" % i, [128, 1024], BF16, blk=512) for i in range(2)]
        sq = [P.sb("m_sq%d" % i, [128, 512], BF16) for i in range(2)]
        qr = [P.sb("m_qr%d" % i, [32, 512], BF16) for i in range(2)]
        onesb = P.sb("m_onesb", [128, 128], BF16)
        oneskb = P.sb("m_oneskb", [128, 128], BF16)
        rmTb = P.sb("m_rmTb", [32, 32], BF16)
        P.cp(onesb[:], ones[:])
        P.cp(oneskb[:], onesk[:])
        P.cp(rmTb[:], rmT[:])
        t1 = P.sb("m_t1b", [32, 512], F32)
        t2 = P.sb("m_t2b", [32, 512], F32)
        mrow = P.sb("m_mrow", [64, 512], F32)
        km4 = P.sb("m_km4", [128, 4], F32)
        kmax2 = P.sb("m_kmax2", [128, 1], F32)
        rden = [P.sb("m_rden%d" % i, [64, 512], F32) for i in range(2)]
        for i in range(2):
            P.memset(kT[i][32:64, :], 0.0)
            P.memset(kT[i][32:33, :], 1.0)
            P.memset(qT[i][32:64, :], 0.0)
            P.memset(Vh[i][:, :, 64:96], 1.0)
        kmx = [P.sb("m_kmx%d" % i, [128, 1], F32) for i in range(2)]

        def prep(h):
            kt_, qt_, vh_ = kT[h % 2], qT[h % 2], Vh[h % 2]
            kmax2_ = kmx[h % 2]
            P.cp(kt_[0:32, :], krope[:], eng="pool")
            for tb in range(4):
                sl = slice(tb * 512, (tb + 1) * 512)
                s = sq[tb % 2]
                P.mm(k.PB[2][:, :], wk[:, h, :], ckvn[:, sl])
                yield
                P.cp(kt_[64:128, sl], k.PB[2][64:128, :], eng="dve")
                yield
                P.tt(s[:], kt_[:, sl], kt_[:, sl], ALU.mult, eng="pool")
                yield
                yield
                P.mm(k.PB[3][:, :], oneskb[:], s[:])
                yield
                P.red(km4[:, tb:tb + 1], k.PB[3][:, :], ALU.max)
                yield
            P.red(kmax2_[:], km4[:], ALU.max)
            for half in range(2):
                for j in range(8):
                    tt = half * 8 + j
                    P.mm(k.PB[2][:, j * 64:(j + 1) * 64], ckvn[:, tt * 128:(tt + 1) * 128], wv[:, h, :])
                yield
                P.cp(vh_[:, half * 8:(half + 1) * 8, 0:64], k.PB[2][:, :].rearrange("p (a b) -> p a b", a=8), eng="dve")
                yield
            for tb in range(4):
                sl = slice(tb * 512, (tb + 1) * 512)
                s = sq[tb % 2]
                q_ = qr[tb % 2]
                P.mm(k.PB[2][:, :], wq[:, 0, h, :], cqn0[:, sl], start=True, stop=False)
                P.mm(k.PB[2][:, :], wq[:, 1, h, :], cqn1[:, sl], start=False, stop=True)
                yield
                P.cp(qt_[64:128, sl], k.PB[2][64:128, :], eng="dve")
                P.cp(q_[:], k.PB[2][0:32, :], eng="dve")
                yield
                P.act(s[:], k.PB[2][:, :], AF.Square)
                yield
                P.mm(k.PB[3][:, :], onesb[:], s[:])
                yield
                P.act(mrow[32:33, :], k.PB[3][32:33, :], AF.Sqrt, scale=kmax2_[32:33, 0:1])
                yield
                P.ts(qt_[32:33, sl], mrow[32:33, :], -1.0, ALU.mult)
                P.mm(k.PB[2][0:32, :], rmTb[:], q_[:])
                P.tt(t1[:], q_[:], cos2[:, sl], ALU.mult, eng="pool")
                yield
                P.tt(t2[:], k.PB[2][0:32, :], sin2[:, sl], ALU.mult)
                yield
                P.tt(qt_[0:32, sl], t1[:], t2[:], ALU.add)
                yield

        def attn(h):
            kt_, qt_, vh_ = kT[h % 2], qT[h % 2], Vh[h % 2]
            pti = 0
            for qb in range(4):
                qs = slice(qb * 512, (qb + 1) * 512)
                O = k.PB[qb % 2]

                def s_pair(m_):
                    for j in range(2):
                        kt = 2 * m_ + j
                        P.mm(k.PAW[m_ % 2][:, j * 512:(j + 1) * 512], kt_[:, kt * 128:(kt + 1) * 128], qt_[:, qs])
                s_pair(0)
                s_pair(1)
                for m_ in range(NT // 2):
                    p_ = pT[pti % 2]
                    pti += 1
                    P.act(p_[:], k.PAW[m_ % 2][:, :], AF.Exp, scale=SC)
                    if m_ + 2 < NT // 2:
                        s_pair(m_ + 2)
                    for j in range(2):
                        kt = 2 * m_ + j
                        P.mm(O[0:96, :], vh_[:, kt, :], p_[:, j * 512:(j + 1) * 512], start=(kt == 0), stop=(kt == NT - 1))
                    yield
                rd = rden[qb % 2]
                P.rpow(rd[0:32, :], O[64:96, :], -1.0)
                P.rpow(rd[32:64, :], O[64:96, :], -1.0)
                ob = (h % 2) * 64
                P.tt(mixc(k, 3 + h // 2)[ob:ob + 64, qs], O[0:64, :], rd[:], ALU.mult)
                yield

        for _ in prep(0):
            pass
        mode = "il"
        for h in range(6):
            gens = [attn(h)]
            if h + 1 < 6:
                if mode == "il":
                    gens.append(prep(h + 1))
                elif mode == "seq":
                    run_interleaved(gens)
                    gens = [prep(h + 1)]
                elif mode == "noprep":
                    pass
            if mode == "noprep" and h > 0:
                gens = [attn(0)]
            run_interleaved(gens)


NCK = L // 64
NEG = -30000.0


def host_gdn(inp, b, m):
    cw = inp["gdn_conv"]
    o = np.zeros((DEPTH, 128, 9, 5), np.float32)
    for part in range(3):
        for p in range(3):
            o[:, :, part * 3 + p, :] = cw[:, :, part * 384 + p * 128: part * 384 + (p + 1) * 128].transpose(0, 2, 1)
    m["gconv"] = o
    gb = np.zeros((DEPTH, 128, 2), np.float32)
    for d in range(2):
        gb[:, d * 32:d * 32 + 6, 0] = inp["gdn_dt_bias"][:, d, :]
        gb[:, d * 32:d * 32 + 6, 1] = inp["gdn_a_log"][:, d, :]
    m["ggb"] = gb
    m["gng"] = np.ascontiguousarray(np.tile(inp["gdn_norm_g"], (1, 2)).reshape(DEPTH, 128, 1))
    sel = np.zeros((64, 6, 128), np.float32)
    for d in range(2):
        for p in range(3):
            sel[d * 32 + 2 * p, d * 3 + p, 0:64] = 1.0
            sel[d * 32 + 2 * p + 1, d * 3 + p, 64:128] = 1.0
    m["gsel"] = sel
    j = np.arange(64)[:, None]
    i = np.arange(64)[None, :]
    nm = np.zeros((128, 2, 64), np.float32)
    nm[:, 0, :] = np.tile(np.where(i > j, 0.0, NEG), (2, 1))
    nm[:, 1, :] = np.tile(np.where(i < j, 0.0, NEG), (2, 1))
    m["gnegm"] = nm
    m["gid2"] = np.ascontiguousarray(np.tile(np.eye(64, dtype=np.float32), (2, 1)))


def gdn_decl(k):
    nc = k.nc

    def din(name, shape, dt=F32):
        return nc.dram_tensor(name, list(shape), dt, kind="ExternalInput").ap()
    k.gconv_d = din("gconv", [DEPTH, 128, 9, 5])
    k.ggb_d = din("ggb", [DEPTH, 128, 2])
    k.gng_d = din("gng", [DEPTH, 128, 1])
    k.gsel_d = din("gsel", [64, 6, 128])
    k.gnegm_d = din("gnegm", [128, 2, 64])
    k.gid2_d = din("gid2", [128, 64])


def bc3(ap2, n):
    return ap2.unsqueeze(2).broadcast_to([ap2.shape[0], ap2.shape[1], n])


def bcm(ap2, n):
    return ap2.unsqueeze(1).broadcast_to([ap2.shape[0], n, ap2.shape[1]])


HS = (slice(0, 64), slice(64, 128))


def v3(ps, n=8):
    return ps[:, 0:n * 64].rearrange("p (a b) -> p a b", a=n)


def mm2(P, ps, c, lhsT, rhs, **kw):
    for hs in HS:
        P.mm(ps[hs, c * 64:(c + 1) * 64], lhsT[hs], rhs[hs], **kw)


def tr2(P, ps, c, in_, ident):
    for hs in HS:
        P.mm(ps[hs, c * 64:(c + 1) * 64], in_[hs], ident[hs, hs])


def neumann2(k, Nn, Rm, tmp, bank, id2, n8=8):
    P = k.P
    idb = bcm(id2[:, :], n8)
    tA, tB, tC, tD = tmp
    pa_, pb_, pc_ = bank
    for c in range(n8):
        tr2(P, pa_, c, Nn[:, c, :], k.identb)
    P.cp(tA[:], v3(pa_, n8), eng="act")
    P.tt(Rm[:], Nn[:], idb, ALU.add)
    yield
    cur, curT = Nn, tA
    targets = [(tB, tC), (tD, tA)]
    for lvl in range(1, 7):
        nxt, nxtT = targets[(lvl - 1) % 2]
        if lvl >= 2:
            for c in range(n8):
                mm2(P, pc_, c, curT[:, c, :], Rm[:, c, :])
            P.tt(Rm[:], Rm[:], v3(pc_, n8), ALU.add)
        if lvl <= 5:
            for c in range(n8):
                mm2(P, pb_, c, cur[:, c, :], curT[:, c, :])
            P.cp(nxtT[:], v3(pb_, n8), eng="act")
            if lvl < 5:
                for c in range(n8):
                    mm2(P, pa_, c, curT[:, c, :], cur[:, c, :])
                P.cp(nxt[:], v3(pa_, n8), eng="act")
        yield
        cur, curT = nxt, nxtT


def run_interleaved(gens, skew=0):
    gens = list(gens)
    for _ in range(skew):
        try:
            next(gens[0])
        except StopIteration:
            gens.pop(0)
            break
    while gens:
        for g in list(gens):
            try:
                next(g)
            except StopIteration:
                gens.remove(g)


def norm_pipe(k, n, src_fn, sqb, rnb, banks, bones, power, scale, bias, post_fn):
    P = k.P

    def pre(i):
        P.act(sqb[i % 2][:], src_fn(i), AF.Square)
        P.mm(banks[i % 2][:, :], bones[:], sqb[i % 2][:])

    def post(i):
        P.rpow(rnb[i % 2][:], banks[i % 2][:, :], power, scale=scale, bias=bias)
        post_fn(i, rnb[i % 2])
    pre(0)
    for i in range(n):
        if i + 1 < n:
            pre(i + 1)
        post(i)


def run_pipelined(chains, depth=2):
    active = []
    nxt = [0] * len(chains)

    def start(ci):
        if nxt[ci] < len(chains[ci]):
            active.append((ci, chains[ci][nxt[ci]](nxt[ci] % depth)))
            nxt[ci] += 1
    for ci in range(len(chains)):
        for _ in range(depth):
            start(ci)
    while active:
        for item in list(active):
            try:
                next(item[1])
            except StopIteration:
                active.remove(item)
                start(item[0])


def gdn_phase(k, l):
    P = k.P
    with scope(k):
        GC = P.sb("g_GC", [64, L], F32, blk=512)
        GP = [P.sb("g_GP%d" % p, [128, NCK, 4], F32) for p in range(3)]
        NBP = [P.sb("g_NBP%d" % p, [128, NCK, 2], F32) for p in range(3)]
        sel = P.sb("g_sel", [64, 6, 128], F32)
        negm = P.sb("g_negm", [128, 2, 64], F32)
        id2 = P.sb("g_id2", [128, 64], F32)
        cw = P.sb("g_cw", [128, 9, 5], F32)
        ng = P.sb("g_ng", [128, 1], F32)
        bones = P.sb("g_bones", [128, 128], F32)
        P.dma(sel[:], k.gsel_d[:])
        P.dma(negm[:], k.gnegm_d[:])
        P.dma(id2[:], k.gid2_d[:])
        P.dma(cw[:], k.gconv_d[l])
        P.dma(ng[:], k.gng_d[l])
        P.memset(bones[:], 0.0)
        P.memset(bones[0:64, 0:64], 1.0)
        P.memset(bones[64:128, 64:128], 1.0)
        bonesb = P.sb("g_bonesb", [128, 128], BF16)
        P.cp(bonesb[:], bones[:])
        with scope(k):
            GT = P.sb("g_GT", [128, L], F32, blk=512)
            m0 = P.sb("g_m0", [64, L], F32)
            gb = P.sb("g_gb", [128, 2], F32)
            negA = P.sb("g_negA", [128, 1], F32)
            P.dma(gb[:], k.ggb_d[l])
            P.act(negA[:], gb[:, 1:2], AF.Exp)
            P.ts(negA[:], negA[:], -1.0, ALU.mult)
            P.memset(m0[:], 1.0)
            P.memset(m0[:, 0:L:64], 0.0)
            pa = proj(k, l, "gab")
            for tb in range(4):
                sl = slice(tb * 512, (tb + 1) * 512)
                P.act(GT[0:64, sl], pa[tb][0:64, :], AF.Identity, bias=gb[0:64, 0:1])
                P.act(GT[64:128, sl], pa[tb][64:128, :], AF.Sigmoid)
            P.act(GC[:, :], GT[0:64, :], AF.Abs)
            P.act(GC[:, :], GC[:, :], AF.Exp, scale=-1.0)
            P.act(GC[:, :], GC[:, :], AF.Ln, bias=1.0)
            P.act(GT[0:64, :], GT[0:64, :], AF.Relu)
            P.tt(GT[0:64, :], GT[0:64, :], GC[:, :], ALU.add)
            P.ts(GT[0:64, :], GT[0:64, :], negA[0:64, 0:1], ALU.mult)
            P.scan(GC[:, :], m0[:, :], GT[0:64, :], 0.0, ALU.mult, ALU.add)
            gc3 = GC[32:64, :].rearrange("p (a b) -> p a b", b=64)
            P.tt(m0[32:64, :].rearrange("p (a b) -> p a b", b=64), bc3(GC[32:64, 63:L:64], 64), gc3, ALU.subtract)
            P.tt(GC[32:64, :], m0[32:64, :], GT[32:64, :], ALU.add)
            for grp in range(4):
                g8 = slice(grp * 8, (grp + 1) * 8)
                for c in range(8):
                    ck = grp * 8 + c
                    cs = slice(c * 64, (c + 1) * 64)
                    for hs in HS:
                        P.mm(k.PB[2 * (grp % 2)][hs, cs], GC[:, ck * 64:(ck + 1) * 64], k.ident[0:64, 0:64])
                        P.mm(k.PB[2 * (grp % 2) + 1][hs, cs], GT[64:128, ck * 64:(ck + 1) * 64], k.ident[64:128, 64:128])
                n_ = 0
                for p in range(3):
                    for hf, hs in enumerate(HS):
                        h = 2 * p + hf
                        for q, ps in ((0, k.PB[2 * (grp % 2)]), (1, k.PB[2 * (grp % 2) + 1])):
                            src = v3(ps)[hs, :, h:h + 33:32]
                            P.cp(GP[p][hs, g8, 2 * q:2 * q + 2], src, eng=("act" if q else "dve"))
            for p in range(3):
                P.ts(NBP[p][:], GP[p][:, :, 2:4], -1.0, ALU.mult)
        for p in range(3):
            with scope(k):
                Q = P.sb("g_Q", [128, L], BF16, blk=512)
                K_ = P.sb("g_K", [128, L], BF16, blk=512)
                Kt = P.sb("g_Kt", [128, NCK, 64], BF16, blk=512)
                Vt = P.sb("g_Vt", [128, NCK, 64], BF16, blk=512)
                O = P.sb("g_O", [128, L], F32, blk=512)
                P.memset(O[:], 0.0, eng="pool")
                with scope(k):
                    xp = P.sb("g_xp", [128, L + 4], BF16)
                    Dg = P.sb("g_Dg", [128, 5, 128], BF16)
                    Vf = P.sb("g_Vf", [128, L], BF16, blk=512)
                    cv = P.sb("g_cv", [128, L], F32, blk=512)
                    sq = P.sb("g_sq", [128, 512], BF16)
                    rn = P.sb("g_rn", [128, 512], F32)
                    sq2 = P.sb("g_sq2", [128, 512], BF16)
                    rn2 = P.sb("g_rn2", [128, 512], F32)
                    P.memset(xp[:, 0:2], 0.0)
                    P.memset(xp[:, L + 2:L + 4], 0.0)
                    for part, nm, dst in ((0, "gq", Q), (1, "gk", K_), (2, "gv", Vf)):
                        pa = proj(k, l, "%s%d" % (nm, p))
                        for tb in range(4):
                            P.cp(xp[:, 2 + tb * 512:2 + (tb + 1) * 512], pa[tb][:, :], eng=("act" if tb % 2 else "dve"))
                        wi = part * 3 + p
                        for j in range(5):
                            P.ts(Dg[:, j, :], k.identb[:], cw[:, wi, j:j + 1], ALU.mult, eng=("pool" if j % 2 else "dve"))
                        for tb in range(4):
                            for j in range(5):
                                P.mm(k.PB[tb][:, :], Dg[:, j, :], xp[:, j + tb * 512:j + (tb + 1) * 512],
                                     start=(j == 0), stop=(j == 4))
                        for tb in range(4):
                            sl = slice(tb * 512, (tb + 1) * 512)
                            P.act((dst if part == 2 else cv)[:, sl], k.PB[tb][:, :], AF.Silu)
                        if part < 2:
                            def fin(tb, r, dst=dst):
                                P.tt(dst[:, tb * 512:(tb + 1) * 512], cv[:, tb * 512:(tb + 1) * 512], r[:], ALU.mult)
                            norm_pipe(k, 4, lambda tb: cv[:, tb * 512:(tb + 1) * 512], (sq, sq2), (rn, rn2),
                                      (k.PA[2], k.PA[3]), bonesb, -0.5, 64.0 if part == 0 else 1.0,
                                      64e-6 if part == 0 else 1e-6, fin)
                    for src, dstt in ((K_, Kt), (Vf, Vt)):
                        for grp in range(4):
                            ps = k.PB[grp % 2]
                            for c in range(8):
                                ck = grp * 8 + c
                                tr2(P, ps, c, src[:, ck * 64:(ck + 1) * 64], k.identb)
                            P.cp(dstt[:, grp * 8:(grp + 1) * 8, :], v3(ps), eng=("act" if grp % 2 else "dve"))
                with scope(k):
                    T = dict(GC=GC, GP=GP[p], NBP=NBP[p], sel=sel, negm=negm, id2=id2, Q=Q, K=K_, Kt=Kt, Vt=Vt, O=O)
                    run_pipelined([gdn_chain(k, p, d, T) for d in range(2)], depth=1)
                with scope(k):
                    sqo = [P.sb("g_osq%d" % i, [128, 512], BF16) for i in range(2)]
                    rno = [P.sb("g_orn%d" % i, [128, 512], F32) for i in range(2)]

                    def fin_o(tb, r):
                        sl = slice(tb * 512, (tb + 1) * 512)
                        P.tt(r[:], O[:, sl], r[:], ALU.mult)
                        P.ts(mixc(k, p)[:, sl], r[:], ng[:, 0:1], ALU.mult)
                    norm_pipe(k, 4, lambda tb: O[:, tb * 512:(tb + 1) * 512], sqo, rno, (k.PB[2], k.PB[3]), bonesb,
                              -0.5, 1.0 / 64, EPS, fin_o)


def gdn_chain(k, p, d, T):
    P = k.P
    GC, GP, NBP, sel, negm, id2, Q, K_, Kt, Vt, O = (T[n] for n in ("GC", "GP", "NBP", "sel", "negm", "id2", "Q", "K", "Kt", "Vt", "O"))
    B = k.PA if d == 0 else k.PB
    tag = "g%d_" % d
    names = ("CB", "EI", "QG", "Rm", "U0", "WT", "BW", "KD", "GK", "AcT", "Sg", "Ug", "nA", "nB", "nC", "nD", "Nb", "PTb", "Sgb")
    f32n = ("CB", "EI", "AcT", "Sg")
    NSET = 1
    GG = [{n: P.sb(tag + "%d" % s_ + n, [128, 8, 64], F32 if n in f32n else BF16) for n in names} for s_ in range(NSET)]
    for s_ in range(NSET):
        GG[s_]["gend"] = P.sb(tag + "gend%d" % s_, [128, 8], F32)
        GG[s_]["kds"] = P.sb(tag + "kds%d" % s_, [128, 8], F32)
    gam = P.sb(tag + "gam", [128, NCK], F32)
    shared = {"scan": 0}
    Scar = P.sb(tag + "Scar", [128, 64], F32)
    e_ = 63 if d == 0 else 0
    idb = bcm(id2[:, :], 8)

    def f2(t):
        return t[:].rearrange("p a b -> p (a b)")
    P.memset(Scar[:], 0.0)
    P.act(gam[:], GP[:, :, d], AF.Exp)
    gorder = list(range(4)) if d == 0 else list(range(3, -1, -1))

    def group(gi, grp, G):
        Nb, PTb, Sgb, gend, kds = G["Nb"], G["PTb"], G["Sgb"], G["gend"], G["kds"]
        sl = slice(grp * 512, (grp + 1) * 512)
        g8 = slice(grp * 8, (grp + 1) * 8)
        cj = GP[:, g8, d]
        nb = NBP[:, g8, d]
        CB, EI, QG, Rm, U0, WT, BW, KD, GK, AcT, Sg, Ug = (G[n] for n in names[:12])
        P.mm(B[0][:, :], sel[:, d * 3 + p, :], GC[:, sl])
        P.cp(f2(CB), B[0][:, :], eng="act")
        P.act(f2(EI), f2(CB), AF.Exp)
        P.cp(gend[:], EI[:, :, e_], eng="pool")
        P.tt(f2(QG), f2(EI), Q[:, sl], ALU.mult)
        P.tt(kds[:], CB[:, :, e_], cj, ALU.subtract)
        P.act(kds[:], kds[:], AF.Exp)
        P.tt(GK[:], Kt[:, g8, :], bc3(gam[:, g8], 64), ALU.mult, eng="pool")
        P.tt(KD[:], Kt[:, g8, :], bc3(kds[:], 64), ALU.mult, eng="pool")
        P.tt(CB[:], CB[:], bc3(cj, 64), ALU.subtract)
        P.tt(CB[:], CB[:], bcm(negm[:, d, :], 8), ALU.add)
        P.act(f2(CB), f2(CB), AF.Exp)
        yield
        P.tt(EI[:], CB[:], idb, ALU.add)
        for c in range(8):
            cs = slice((grp * 8 + c) * 64, (grp * 8 + c + 1) * 64)
            mm2(P, B[0], c, K_[:, cs], Q[:, cs])
        P.tt(PTb[:], EI[:], v3(B[0]), ALU.mult)
        for c in range(8):
            cs = slice((grp * 8 + c) * 64, (grp * 8 + c + 1) * 64)
            mm2(P, B[1], c, K_[:, cs], K_[:, cs])
        P.tt(CB[:], CB[:], v3(B[1]), ALU.mult)
        P.tt(Nb[:], CB[:], bc3(nb, 64), ALU.mult)
        yield
        for _ in neumann2(k, Nb, Rm, (G["nA"], G["nB"], G["nC"], G["nD"]), (B[0], B[1], B[2]), id2):
            yield
        for c in range(8):
            mm2(P, B[2], c, Rm[:, c, :], GK[:, c, :])
        P.tt(BW[:], v3(B[2]), bc3(nb, 64), ALU.mult)
        for c in range(8):
            mm2(P, B[0], c, Rm[:, c, :], Vt[:, grp * 8 + c, :])
        P.tt(U0[:], v3(B[0]), bc3(GP[:, g8, 2 + d], 64), ALU.mult)
        for c in range(8):
            mm2(P, B[1], c, GK[:, c, :], Rm[:, c, :])
        P.cp(WT[:], v3(B[1]), eng="act")
        yield
        for c in range(8):
            mm2(P, B[3], c, BW[:, c, :], KD[:, c, :])
        P.tt(AcT[:], idb, bc3(gend[:], 64), ALU.mult, eng="pool")
        P.tt(AcT[:], AcT[:], v3(B[3]), ALU.add)
        yield
        while shared["scan"] != gi:
            yield
        corder = range(8) if d == 0 else range(7, -1, -1)
        prev = Scar[:]
        for n, c in enumerate(corder):
            P.cp(Sg[:, c, :], prev, eng="pool") if n == 0 else None
            ps = B[2 + n % 2]
            mm2(P, ps, 0, AcT[:, c, :], Sg[:, c, :], start=True, stop=False)
            mm2(P, ps, 0, KD[:, c, :], U0[:, c, :], start=False, stop=True)
            last = (n == 7)
            dst = Scar[:] if last else Sg[:, corder[n + 1], :]
            P.cp(dst, ps[:, 0:64], eng="act")
            yield
        shared["scan"] = gi + 1
        P.cp(Sgb[:], Sg[:], eng="act")
        for c in range(8):
            mm2(P, B[0], c, WT[:, c, :], Sgb[:, c, :])
        P.tt(CB[:], v3(B[0]), bc3(nb, 64), ALU.mult)
        P.tt(Ug[:], CB[:], U0[:], ALU.add)
        yield
        for c in range(8):
            mm2(P, B[1], c, Sgb[:, c, :], QG[:, c, :], start=True, stop=False)
            mm2(P, B[1], c, Ug[:, c, :], PTb[:, c, :], start=False, stop=True)
        P.tt(O[:, sl], O[:, sl], B[1][:, :], ALU.add)
        yield
    return [(lambda slot, gi=gi, grp=grp: group(gi, grp, GG[slot])) for gi, grp in enumerate(gorder)]


RSKEW = 0
RW_EPS = 64e-5
DEC = float(np.exp(-0.5))
GS = 8
NG = NCK // GS


def host_rwkv(inp, b, m):
    mu = inp["rwkv_mu"]
    o = np.zeros((DEPTH, 128, 8, 2), np.float32)
    for part in range(3):
        for p in range(2):
            o[:, :, part * 2 + p, :] = mu[:, :, part * 256 + p * 128: part * 256 + (p + 1) * 128].transpose(0, 2, 1)
    o[:, 0:64, 6, :] = mu[:, :, 768:832].transpose(0, 2, 1)
    o[:, 0:64, 7, :] = mu[:, :, 832:896].transpose(0, 2, 1)
    m["rmu"] = o

    def pp(a):
        if a.ndim == 2:
            return np.ascontiguousarray(a.reshape(DEPTH, 2, 128).transpose(0, 2, 1))
        return np.ascontiguousarray(a.reshape(DEPTH, 2, 2, 128).transpose(0, 3, 1, 2))
    pv = np.zeros((DEPTH, 128, 7, 2), np.float32)
    pv[:, :, 0:2, :] = pp(inp["rwkv_w0"])
    pv[:, :, 2:4, :] = pp(inp["rwkv_a0"])
    pv[:, :, 4, :] = pp(inp["rwkv_k_k"])
    pv[:, :, 5, :] = pp(inp["rwkv_k_a"])
    pv[:, :, 6, :] = pp(inp["rwkv_r_k"].reshape(DEPTH, 256))
    m["rpv"] = pv
    ln = np.zeros((DEPTH, 128, 2, 2), np.float32)
    ln[:, :, 0, :] = pp(inp["rwkv_ln_g"])
    ln[:, :, 1, :] = pp(inp["rwkv_ln_b"])
    m["rln"] = ln
    m["rw2"] = np.ascontiguousarray(inp["rwkv_w2"].transpose(0, 2, 1, 3))
    m["ra2"] = np.ascontiguousarray(inp["rwkv_a2"].transpose(0, 2, 1, 3))
    s_ = np.arange(64)[:, None]
    t_ = np.arange(64)[None, :]
    msk = np.zeros((128, 2, 4, 64), np.float32)
    msk[:, 0, 0, :] = np.tile((t_ > s_), (2, 1))
    msk[:, 0, 1, :] = np.tile((t_ >= s_), (2, 1))
    msk[:, 1, 0, :] = np.tile((t_ < s_), (2, 1))
    msk[:, 1, 1, :] = np.tile((t_ <= s_), (2, 1))
    msk[:, :, 2:4, :] = -msk[:, :, 0:2, :]
    m["rmsk"] = msk


def rwkv_decl(k):
    nc = k.nc

    def din(name, shape, dt=F32):
        return nc.dram_tensor(name, list(shape), dt, kind="ExternalInput").ap()
    k.rmu_d = din("rmu", [DEPTH, 128, 8, 2])
    k.rpv_d = din("rpv", [DEPTH, 128, 7, 2])
    k.rln_d = din("rln", [DEPTH, 128, 2, 2])
    k.rw2_d = din("rw2", [DEPTH, 64, 2, 256])
    k.ra2_d = din("ra2", [DEPTH, 64, 2, 256])
    k.rmsk_d = din("rmsk", [128, 2, 4, 64])


def rwkv_phase(k, l):
    P = k.P
    with scope(k):
        mu = P.sb("r_mu", [128, 8, 3], F32)
        pv = P.sb("r_pv", [128, 7, 2], F32)
        omka = P.sb("r_omka", [128, 2], F32)
        hrk = P.sb("r_hrk", [128, 2], F32)
        ln = P.sb("r_ln", [128, 2, 2], F32)
        w2 = P.sb("r_w2", [64, 2, 256], BF16)
        a2 = P.sb("r_a2", [64, 2, 256], BF16)
        msk = P.sb("r_msk", [128, 2, 4, 64], F32)
        id2 = P.sb("r_id2", [128, 64], F32)
        bones = P.sb("r_bones", [128, 128], F32)
        m0 = P.sb("r_m0", [128, GS * 64], F32)
        twd = P.sb("r_twd", [64, L], BF16, blk=512)
        adx = P.sb("r_adx", [64, L], BF16, blk=512)
        sh32 = P.sb("r_sh32", [128, L], F32, blk=512)
        xp = P.sb("r_xp", [128, L + 2], F32)
        P.dma(mu[:, :, 0:2], k.rmu_d[l])
        P.dma(pv[:], k.rpv_d[l])
        P.dma(ln[:], k.rln_d[l])
        with scope(k):
            w2f = P.sb("r_w2f", [64, 2, 256], F32)
            a2f = P.sb("r_a2f", [64, 2, 256], F32)
            P.dma(w2f[:], k.rw2_d[l])
            P.dma(a2f[:], k.ra2_d[l])
            P.cp(w2[:], w2f[:], eng="act")
            P.cp(a2[:], a2f[:], eng="act")
        P.dma(msk[:], k.rmsk_d[:])
        P.dma(id2[:], k.gid2_d[:])
        P.memset(bones[:], 0.0)
        P.memset(bones[0:64, 0:64], 1.0)
        P.memset(bones[64:128, 64:128], 1.0)
        P.memset(m0[:], 1.0)
        P.memset(m0[:, 0:GS * 64:64], 0.0)
        P.memset(xp[:, 0:1], 0.0)
        P.memset(xp[:, L + 1:L + 2], 0.0)
        P.tt(mu[:, :, 2], mu[:, :, 0], mu[:, :, 1], ALU.add)
        P.ts(mu[:, :, 2], mu[:, :, 2], -1.0, ALU.mult, 1.0, ALU.add)
        P.ts(omka[:], pv[:, 5, :], -1.0, ALU.mult, 1.0, ALU.add)
        P.ts(hrk[:], pv[:, 6, :], 0.5, ALU.mult)

        def shifted(name, ci, dst, np_=128, fn=None):
            pa = proj(k, l, name, alt=(ci % 2 == 1))
            for tb in range(4):
                P.cp(xp[0:np_, 1 + tb * 512:1 + (tb + 1) * 512], pa[tb][0:np_, :], eng=("act" if tb % 2 else "dve"))
            t_ = sh32[0:np_, :]
            P.ts(t_, xp[0:np_, 1:L + 1], mu[0:np_, ci, 2:3], ALU.mult)
            P.stt(t_, xp[0:np_, 0:L], mu[0:np_, ci, 0:1], t_, ALU.mult, ALU.add)
            if fn is None:
                P.stt(dst[:], xp[0:np_, 2:L + 2], mu[0:np_, ci, 1:2], t_, ALU.mult, ALU.add)
            else:
                P.stt(t_, xp[0:np_, 2:L + 2], mu[0:np_, ci, 1:2], t_, ALU.mult, ALU.add)
                P.act(dst[:], t_, fn)

        shifted("rwd", 6, twd, 64, AF.Tanh)
        shifted("rad", 7, adx, 64)
        R_ = P.sb("r_R", [128, L], BF16, blk=512)
        KX = P.sb("r_KX", [128, L], BF16, blk=512)
        V_ = P.sb("r_V", [128, L], BF16, blk=512)
        KK = P.sb("r_KK", [128, L], BF16, blk=512)
        Vt = P.sb("r_Vt", [128, NCK, 64], BF16, blk=512)
        KS = P.sb("r_KS", [128, L], F32, blk=512)
        Y = xp[:, 1:L + 1]
        sq = P.sb("r_sq", [128, 512], BF16)
        bonesb = P.sb("r_bonesb", [128, 128], BF16)
        P.cp(bonesb[:], bones[:])
        rn = P.sb("r_rn", [128, 512], F32)
        CH = [rwkv_tiles(k, e) for e in range(2)]

        class _V:
            def __init__(s_, t):
                s_.t = t

            def __getitem__(s_, key):
                return s_.t[:].rearrange("p a b -> p (a b)")[key]
        sq2, rn2 = _V(CH[0]["kap"]), _V(CH[0]["a"])
        for p in range(2):
            shifted("rr%d" % p, 0 + p, R_)
            shifted("rk%d" % p, 2 + p, KX)
            shifted("rv%d" % p, 4 + p, V_)
            P.ts(sh32[:], KX[:], pv[:, 4, p:p + 1], ALU.mult)

            def fin_k(tb, r):
                P.tt(KK[:, tb * 512:(tb + 1) * 512], sh32[:, tb * 512:(tb + 1) * 512], r[:], ALU.mult)
            norm_pipe(k, 4, lambda tb: sh32[:, tb * 512:(tb + 1) * 512], (sq, sq2), (rn, rn2), (k.PB[2], k.PB[3]),
                      bonesb, -0.5, 1.0, 1e-6, fin_k)
            for grp in range(4):
                ps = k.PB[grp % 2]
                for c in range(8):
                    ck = grp * 8 + c
                    tr2(P, ps, c, V_[:, ck * 64:(ck + 1) * 64], k.identb)
                P.cp(Vt[:, grp * 8:(grp + 1) * 8, :], v3(ps), eng=("act" if grp % 2 else "dve"))
            P.memset(xp[:, 1:L + 1], 0.0, eng="pool")
            P.memset(KS[:], 0.0, eng="pool")
            T = dict(pv=pv, omka=omka, w2=w2, a2=a2, msk=msk, id2=id2, m0=m0, twd=twd, adx=adx,
                     R=R_, KX=KX, KK=KK, Vt=Vt, KS=KS, Y=Y)
            run_interleaved([rwkv_chain(k, p, e, T, CH[e]) for e in range(2)], skew=RSKEW)
            for tb in range(4):
                sl = slice(tb * 512, (tb + 1) * 512)
                P.mm(k.PB[tb % 2][:, :], bones[:], Y[:, sl])
                P.stt(Y[:, sl], k.PB[tb % 2][:, :], -1.0 / 64, Y[:, sl], ALU.mult, ALU.add)

            def fin_y(tb, r):
                sl = slice(tb * 512, (tb + 1) * 512)
                P.tt(Y[:, sl], Y[:, sl], r[:], ALU.mult)
                P.ts(Y[:, sl], Y[:, sl], ln[:, 0, p:p + 1], ALU.mult, ln[:, 1, p:p + 1], ALU.add)
            norm_pipe(k, 4, lambda tb: Y[:, tb * 512:(tb + 1) * 512], (sq, sq2), (rn, rn2), (k.PB[2], k.PB[3]),
                      bonesb, -0.5, 1.0 / 64, RW_EPS, fin_y)
            for tb in range(4):
                sl = slice(tb * 512, (tb + 1) * 512)
                s_ = (sq, sq2)[tb % 2]
                r_ = (rn, rn2)[tb % 2]
                P.tt(s_[:], R_[:, sl], KS[:, sl], ALU.mult)
                P.ts(s_[:], s_[:], hrk[:, p:p + 1], ALU.mult, eng="pool")
                P.mm(k.PB[tb % 2][:, :], bonesb[:], s_[:])
                P.tt(r_[:], k.PB[tb % 2][:, :], V_[:, sl], ALU.mult)
                P.tt(mixc(k, 6 + p)[:, sl], Y[:, sl], r_[:], ALU.add)


RW_F32 = ("lw", "a", "km", "b", "cl", "e1", "e2", "dend", "AcT", "Tg")
RW_BF16 = ("kap", "rt", "kt_", "bt_", "ke", "be", "kapT", "keT", "nbeT", "N", "Akv", "Brk", "nBrb", "Rm",
           "nA", "nB", "nC", "nD", "X0", "P0", "WkT", "Wk", "Tgb", "Pg")


def rwkv_tiles(k, e):
    P = k.P
    G = {n: P.sb("r%d_%s" % (e, n), [128, GS, 64], F32) for n in RW_F32}
    for n in RW_BF16:
        G[n] = P.sb("r%d_%s" % (e, n), [128, GS, 64], BF16)
    G["gC"] = P.sb("r%d_gC" % e, [128, GS], F32)
    G["Tcar"] = P.sb("r%d_Tcar" % e, [128, 64], F32)
    return G


def rwkv_chain(k, p, e, T, G):
    P = k.P
    pv, omka, w2, a2, msk, id2, m0, twd, adx, R_, KX, KK, Vt, KS, Y = (T[n] for n in (
        "pv", "omka", "w2", "a2", "msk", "id2", "m0", "twd", "adx", "R", "KX", "KK", "Vt", "KS", "Y"))
    B = k.PA if e == 0 else k.PB
    W = GS * 64
    e_ = 63 if e == 0 else 0
    idb = bcm(id2[:, :], GS)
    gC, Tcar = G["gC"], G["Tcar"]

    def f2(t):
        return t[:].rearrange("p a b -> p (a b)")

    def w3(ps):
        return v3(ps, GS)
    P.memset(Tcar[:], 0.0)
    gorder = range(NG) if e == 0 else range(NG - 1, -1, -1)
    pc = slice(p * 128, (p + 1) * 128)
    for grp in gorder:
        sl = slice(grp * W, (grp + 1) * W)
        c0 = grp * GS
        P.mm(B[0][:, 0:W], w2[:, e, pc], twd[:, sl])
        P.mm(B[1][:, 0:W], a2[:, e, pc], adx[:, sl])
        P.act(f2(G["lw"]), B[0][:, 0:W], AF.Sigmoid, bias=pv[:, 0 + e, p:p + 1])
        P.act(f2(G["a"]), B[1][:, 0:W], AF.Sigmoid, bias=pv[:, 2 + e, p:p + 1])
        P.ts(f2(G["km"]), f2(G["a"]), pv[:, 5, p:p + 1], ALU.mult, omka[:, p:p + 1], ALU.add)
        P.tt(f2(G["km"]), f2(G["km"]), KX[:, sl], ALU.mult)
        P.tt(f2(G["b"]), f2(G["a"]), KK[:, sl], ALU.mult)
        P.tt(KS[:, sl], KS[:, sl], f2(G["km"]), ALU.add, eng="pool")
        P.scan(f2(G["cl"]), m0[:], f2(G["lw"]), 0.0, ALU.mult, ALU.add)
        if e == 1:
            P.tt(G["e1"][:], bc3(G["cl"][:, :, 63], 64), G["cl"][:], ALU.subtract)
            P.tt(G["cl"][:], G["e1"][:], G["lw"][:], ALU.add)
        yield
        P.act(G["e1"][:], G["cl"][:], AF.Exp, scale=-DEC)
        P.act(G["e2"][:], G["cl"][:], AF.Exp, scale=DEC)
        P.tt(f2(G["rt"]), f2(G["e1"]), R_[:, sl], ALU.mult)
        P.tt(G["kt_"][:], G["e2"][:], G["km"][:], ALU.mult)
        P.tt(G["bt_"][:], G["e2"][:], G["b"][:], ALU.mult)
        P.tt(G["dend"][:], G["cl"][:], G["lw"][:], ALU.subtract)
        P.act(G["dend"][:], G["dend"][:], AF.Exp, scale=-DEC)
        P.tt(f2(G["kap"]), f2(G["dend"]), KK[:, sl], ALU.mult)
        P.cp(gC[:], G["e1"][:, :, e_], eng="pool")
        P.tt(G["dend"][:], bc3(G["cl"][:, :, e_], 64), G["cl"][:], ALU.subtract)
        P.act(G["dend"][:], G["dend"][:], AF.Exp, scale=-DEC)
        P.tt(G["ke"][:], G["dend"][:], G["km"][:], ALU.mult)
        P.tt(G["be"][:], G["dend"][:], G["b"][:], ALU.mult, eng="pool")
        yield
        for src, dst, sc in ((G["kap"], G["kapT"], 1.0), (G["ke"], G["keT"], 1.0), (G["be"], G["nbeT"], -1.0)):
            ps = B[0] if sc == 1.0 and src is G["kap"] else (B[1] if sc == 1.0 else B[2])
            for c in range(GS):
                tr2(P, ps, c, src[:, c, :], k.identb)
            if sc == 1.0:
                P.cp(dst[:], w3(ps), eng="act")
            else:
                P.ts(dst[:], w3(ps), -1.0, ALU.mult)
        yield
        for c in range(GS):
            mm2(P, B[0], c, G["bt_"][:, c, :], G["kap"][:, c, :])
            mm2(P, B[1], c, G["kt_"][:, c, :], G["kap"][:, c, :])
            mm2(P, B[2], c, G["kt_"][:, c, :], G["rt"][:, c, :])
            mm2(P, B[3], c, G["bt_"][:, c, :], G["rt"][:, c, :])
        ms = bcm(msk[:, e, 0, :], GS)
        mi = bcm(msk[:, e, 1, :], GS)
        nms = bcm(msk[:, e, 2, :], GS)
        nmi = bcm(msk[:, e, 3, :], GS)
        P.tt(G["N"][:], w3(B[0]), nms, ALU.mult)
        P.tt(G["Akv"][:], w3(B[1]), ms, ALU.mult)
        P.tt(G["Brk"][:], w3(B[2]), mi, ALU.mult)
        P.tt(G["nBrb"][:], w3(B[3]), nmi, ALU.mult)
        yield
        for _ in neumann2(k, G["N"], G["Rm"], (G["nA"], G["nB"], G["nC"], G["nD"]), (B[0], B[1], B[2]), id2, GS):
            yield
        for c in range(GS):
            mm2(P, B[0], c, G["Akv"][:, c, :], Vt[:, c0 + c, :])
        P.cp(G["X0"][:], w3(B[0]), eng="act")
        yield
        for c in range(GS):
            mm2(P, B[0], c, G["Rm"][:, c, :], G["X0"][:, c, :])
            mm2(P, B[1], c, G["kapT"][:, c, :], G["Rm"][:, c, :])
            mm2(P, B[2], c, G["Rm"][:, c, :], G["kapT"][:, c, :])
        P.cp(G["P0"][:], w3(B[0]), eng="act")
        P.cp(G["WkT"][:], w3(B[1]), eng="dve")
        P.cp(G["Wk"][:], w3(B[2]), eng="act")
        yield
        for c in range(GS):
            mm2(P, B[3], c, G["Wk"][:, c, :], G["nbeT"][:, c, :])
        P.tt(G["AcT"][:], idb, bc3(gC[:], 64), ALU.mult, eng="pool")
        P.tt(G["AcT"][:], G["AcT"][:], w3(B[3]), ALU.add)
        yield
        corder = list(range(GS)) if e == 0 else list(range(GS - 1, -1, -1))
        Tg = G["Tg"]
        for n, c in enumerate(corder):
            if n == 0:
                P.cp(Tg[:, c, :], Tcar[:], eng="pool")
            ps = B[2 + n % 2]
            mm2(P, ps, 0, G["AcT"][:, c, :], Tg[:, c, :], start=True, stop=False)
            mm2(P, ps, 0, G["keT"][:, c, :], Vt[:, c0 + c, :], start=False, stop=False)
            mm2(P, ps, 0, G["nbeT"][:, c, :], G["P0"][:, c, :], start=False, stop=True)
            dst = Tcar[:] if n == GS - 1 else Tg[:, corder[n + 1], :]
            P.cp(dst, ps[:, 0:64], eng="act")
            yield
        P.cp(G["Tgb"][:], Tg[:], eng="act")
        for c in range(GS):
            mm2(P, B[0], c, G["WkT"][:, c, :], G["Tgb"][:, c, :])
        P.tt(G["Pg"][:], w3(B[0]), G["P0"][:], ALU.add)
        yield
        for c in range(GS):
            mm2(P, B[1], c, Vt[:, c0 + c, :], G["Brk"][:, c, :], start=True, stop=False)
            mm2(P, B[1], c, G["Tgb"][:, c, :], G["rt"][:, c, :], start=False, stop=False)
            mm2(P, B[1], c, G["Pg"][:, c, :], G["nBrb"][:, c, :], start=False, stop=True)
        P.tt(Y[:, sl], Y[:, sl], B[1][:, 0:W], ALU.add)
        yield


_CACHE = {}


def kernel(**inputs):
    inp = {k_: np.asarray(v) for k_, v in inputs.items()}
    if "k" not in _CACHE:
        _CACHE["k"] = build()
    k = _CACHE["k"]
    B = inp["x"].shape[0]
    base = host_inputs(inp, 0)
    in_maps = []
    for b in range(B):
        m = dict(base)
        m["x"] = np.ascontiguousarray(inp["x"][b], dtype=np.float32)
        m["pos"] = np.ascontiguousarray(inp["positions"][b].reshape(1, L).astype(np.int32))
        in_maps.append(m)
    res = run_bass_kernel_spmd(k.nc, in_maps, core_ids=list(range(B)))
    return np.stack([np.asarray(r["out"], dtype=np.float32) for r in res.results], axis=0)
```

```python
import numpy as np
import concourse.bass as bass
import concourse.mybir as mybir
from concourse.bass_utils import run_bass_kernel_spmd
from contextlib import ExitStack

F32 = mybir.dt.float32
BF16 = mybir.dt.bfloat16
I32 = mybir.dt.int32
AF = mybir.ActivationFunctionType
ALU = mybir.AluOpType
AX = mybir.AxisListType
DTSIZE = {F32: 4, BF16: 2, I32: 4}


class _Op:
    __slots__ = ("eng", "emit", "deps", "idx", "needed", "sigval", "dsem", "dval", "isdma")

    def __init__(self, eng, emit, isdma=False):
        self.eng = eng
        self.emit = emit
        self.deps = []
        self.idx = -1
        self.needed = False
        self.sigval = 0
        self.dsem = None
        self.dval = 0
        self.isdma = isdma


class _Blk:
    __slots__ = ("w", "r")

    def __init__(self):
        self.w = None
        self.r = {}


class Prog:
    ENGS = ("pe", "dve", "act", "pool", "sp")
    NDMA = 48
    NHW = 32

    def __init__(self, nc, stack):
        self.nc = nc
        self.stack = stack
        self.ops = {e: [] for e in self.ENGS}
        self.track = {}
        self.seen = {e: {} for e in self.ENGS}
        self.seen_dma = {e: set() for e in self.ENGS}
        self.dma_last = [None] * self.NDMA
        self.dma_uses = [0] * self.NDMA
        self.dma_rr = 0
        self.dma_rr_sw = 0
        self.ndma_ops = 0
        self.untracked = set()
        self.out_dmas = []
        self.dma_pending = []
        self.last_compute = {}

    def sb(self, name, shape, dtype=F32, blk=None):
        self.uid = getattr(self, "uid", 0) + 1
        name = "s%d_%s" % (self.uid, name)
        t = self.stack.enter_context(self.nc.sbuf_tensor(name, list(shape), dtype))
        self._register(name, shape, dtype, blk)
        return t

    def ps(self, name, shape=(128, 512), dtype=F32, blk=None):
        self.uid = getattr(self, "uid", 0) + 1
        name = "p%d_%s" % (self.uid, name)
        t = self.stack.enter_context(self.nc.psum_tensor(name, list(shape), dtype))
        self._register(name, shape, dtype, blk)
        return t

    def _register(self, name, shape, dtype, blk):
        row = int(np.prod(shape[1:])) * DTSIZE[dtype]
        bb = row if blk is None else blk * DTSIZE[dtype]
        nb = (row + bb - 1) // bb
        self.track[name] = (bb, row, [_Blk() for _ in range(nb)])

    def dram_track(self, name, total_bytes, blk_bytes):
        nb = (total_bytes + blk_bytes - 1) // blk_bytes
        self.track[name] = (blk_bytes, -1, [_Blk() for _ in range(nb)])

    def _blocks(self, ap):
        name = ap.tensor.name
        if name not in self.track:
            return ()
        bb, row, blks = self.track[name]
        if len(blks) == 1:
            return blks
        ds = DTSIZE[ap.dtype]
        pat = ap.ap
        if row < 0:
            lo = hi = ap.offset
            for step, cnt in pat:
                ext = step * (cnt - 1)
                if ext < 0:
                    lo += ext
                else:
                    hi += ext
            return blks[(lo * ds) // bb:(hi * ds) // bb + 1]
        rowel = row // ds
        foff = ap.offset % rowel
        lo = hi = foff
        for step, cnt in pat[1:]:
            ext = step * (cnt - 1)
            if ext < 0:
                lo += ext
            else:
                hi += ext
        b0 = (lo * ds) // bb
        b1 = (hi * ds) // bb
        return blks[b0:b1 + 1]

    def _dep(self, x, y):
        if y is None or y is x:
            return
        e = x.eng
        if y.isdma:
            if id(y) in self.seen_dma[e]:
                return
            self.seen_dma[e].add(id(y))
            x.deps.append(y)
            return
        if y.eng == "pe" and e == "pe":
            return
        if y.idx <= self.seen[e].get(y.eng, -1):
            return
        self.seen[e][y.eng] = y.idx
        y.needed = True
        x.deps.append(y)

    def add(self, eng, emit, reads=(), writes=(), isdma=False):
        x = _Op(eng, emit, isdma)
        x.idx = len(self.ops[eng])
        rb = []
        for ap in reads:
            if ap is None or isinstance(ap, (int, float)):
                continue
            rb.extend(self._blocks(ap))
        wb = []
        for ap in writes:
            wb.extend(self._blocks(ap))
        for ap in reads:
            if ap is None or isinstance(ap, (int, float)) or not ap.tensor.name.startswith("p"):
                continue
            for b in self._blocks(ap):
                for key, y in b.r.items():
                    if key != eng:
                        self._dep(x, y)
        for b in rb:
            self._dep(x, b.w)
        for b in wb:
            self._dep(x, b.w)
            for y in b.r.values():
                self._dep(x, y)
        if isdma:
            if eng == "pool":
                s = self.NHW + self.dma_rr_sw
                self.dma_rr_sw = (self.dma_rr_sw + 1) % (self.NDMA - self.NHW)
            else:
                s = self.dma_rr
                self.dma_rr = (self.dma_rr + 1) % self.NHW
            self._dep(x, self.dma_last[s])
            self.dma_last[s] = x
            self.dma_uses[s] += 1
            x.dsem = s
            x.dval = 16 * self.dma_uses[s]
            self.ndma_ops += 1
        key = id(x) if isdma else eng
        for b in rb:
            b.r[key] = x
        for b in wb:
            b.w = x
            b.r = {}
        self.ops[eng].append(x)
        if isdma:
            self.dma_pending.append(x)
        else:
            self.last_compute[eng] = x
        return x

    def barrier(self):
        lasts = dict(self.last_compute)
        pend = list(self.dma_pending)
        self.dma_pending = []
        for e in self.ENGS:
            b = _Op(e, None)
            b.idx = len(self.ops[e])
            for e2, y in lasts.items():
                if e2 == e and e == "pe":
                    continue
                self._dep(b, y)
            for y in pend:
                self._dep(b, y)
            self.ops[e].append(b)

    def mm(self, out, lhsT, rhs, start=True, stop=True):
        return self.add("pe", lambda e: e.matmul(out, lhsT, rhs, start=start, stop=stop),
                        reads=(lhsT, rhs), writes=(out,))

    def tr(self, out, in_, ident):
        return self.add("pe", lambda e: e.transpose(out, in_, ident), reads=(in_, ident), writes=(out,))

    def tt(self, out, in0, in1, op, eng="dve"):
        return self.add(eng, lambda e: e.tensor_tensor(out, in0, in1, op), reads=(in0, in1), writes=(out,))

    def ts(self, out, in0, s1, op0, s2=None, op1=None, eng="dve", accum_out=None):
        kw = {}
        if eng == "pool" and op1 is None:
            if op0 == ALU.mult:
                s2, op1 = 0.0, ALU.add
            elif op0 == ALU.add:
                s2, op1 = 1.0, ALU.mult
        if op1 is not None:
            kw["op1"] = op1
        if accum_out is not None:
            kw["accum_out"] = accum_out
        w = (out,) if accum_out is None else (out, accum_out)
        return self.add(eng, lambda e: e.tensor_scalar(out, in0, s1, s2, op0, **kw),
                        reads=(in0, s1, s2), writes=w)

    def stt(self, out, in0, scalar, in1, op0, op1, accum_out=None):
        kw = {}
        if accum_out is not None:
            kw["accum_out"] = accum_out
        w = (out,) if accum_out is None else (out, accum_out)
        return self.add("dve", lambda e: e.scalar_tensor_tensor(out, in0, scalar, in1, op0, op1, **kw),
                        reads=(in0, scalar, in1), writes=w)

    def cp(self, out, in_, eng="dve"):
        if eng == "act":
            return self.add("act", lambda e: e.copy(out, in_), reads=(in_,), writes=(out,))
        return self.add(eng, lambda e: e.tensor_copy(out, in_), reads=(in_,), writes=(out,))

    def act(self, out, in_, func, bias=0.0, scale=1.0, accum_out=None):
        kw = {}
        if accum_out is not None:
            kw["accum_out"] = accum_out
        w = (out,) if accum_out is None else (out, accum_out)
        return self.add("act", lambda e: e.activation(out, in_, func, bias=bias, scale=scale, **kw),
                        reads=(in_, bias, scale), writes=w)

    def red(self, out, in_, op, axis=AX.X, eng="dve"):
        return self.add(eng, lambda e: e.tensor_reduce(out, in_, axis, op), reads=(in_,), writes=(out,))

    def recip(self, out, in_):
        return self.add("dve", lambda e: e.reciprocal(out, in_), reads=(in_,), writes=(out,))

    def rpow(self, out, in_, power, scale=1.0, bias=0.0):
        self.act(out, in_, AF.Ln, bias=bias, scale=scale)
        return self.act(out, out, AF.Exp, scale=power)

    def memset(self, ap, val, eng="dve"):
        return self.add(eng, lambda e: e.memset(ap, val), writes=(ap,))

    def scan(self, out, d0, d1, init, op0, op1):
        return self.add("dve", lambda e: e.tensor_tensor_scan(out, d0, d1, init, op0, op1),
                        reads=(d0, d1, init), writes=(out,))

    def dma(self, out, in_, eng="sp", is_output=False):
        x = self.add(eng, lambda e: e.dma_start(out=out, in_=in_), reads=(in_,), writes=(out,), isdma=True)
        if is_output:
            self.out_dmas.append(x)
        return x

    def finish(self):
        nc = self.nc
        fin = _Op("sp", None)
        fin.idx = len(self.ops["sp"])
        for y in self.out_dmas:
            self._dep(fin, y)
        self.ops["sp"].append(fin)
        sems = {}
        for e in ("pe", "dve", "act", "pool"):
            sems[e] = self.stack.enter_context(nc.semaphore("s_" + e))
        dsems = [self.stack.enter_context(nc.semaphore("d%d" % i)) for i in range(self.NDMA)]
        for e in ("pe", "dve", "act", "pool"):
            c = 0
            for x in self.ops[e]:
                if x.isdma:
                    continue
                if x.needed:
                    c += 1
                    x.sigval = c
            self.stats_sig = getattr(self, "stats_sig", {})
            self.stats_sig[e] = c
        ops = self.ops

        def replay(e, engobj):
            for x in ops[e]:
                for y in x.deps:
                    if y.isdma:
                        engobj.wait_ge(dsems[y.dsem], y.dval)
                    else:
                        engobj.wait_ge(sems[y.eng], y.sigval)
                if x.emit is None:
                    continue
                ins = x.emit(engobj)
                if x.isdma:
                    ins.then_inc(dsems[x.dsem], 16)
                elif x.needed:
                    ins.then_inc(sems[e], 1)

        with nc.Block() as block:
            @block.tensor
            def _(eng):
                replay("pe", eng)

            @block.vector
            def _(eng):
                replay("dve", eng)

            @block.scalar
            def _(eng):
                replay("act", eng)

            @block.gpsimd
            def _(eng):
                replay("pool", eng)

            @block.sync
            def _(eng):
                replay("sp", eng)


L = 2048
D = 1024
NT = L // 128
DEPTH = 2
N_IN = 3448
EPS = 1e-6

OFF = dict(gate=0, gdn_q=1024, gdn_k=1408, gdn_v=1792, gdn_a=2176, gdn_b=2188, mla_cq=2200, mla_ckv=2392,
           mla_kr=2520, rw_r=2552, rw_k=2808, rw_v=3064, rw_wd=3320, rw_ad=3384)


def chunk_table():
    ch = []
    for h in range(3):
        ch.append(("gq%d" % h, [(0, OFF["gdn_q"] + h * 128, 128)]))
        ch.append(("gk%d" % h, [(0, OFF["gdn_k"] + h * 128, 128)]))
        ch.append(("gv%d" % h, [(0, OFF["gdn_v"] + h * 128, 128)]))
    ch.append(("gab", [(0, OFF["gdn_a"], 6), (32, OFF["gdn_a"] + 6, 6), (64, OFF["gdn_b"], 6), (96, OFF["gdn_b"] + 6, 6)]))
    ch.append(("cq0", [(0, OFF["mla_cq"], 128)]))
    ch.append(("cq1", [(0, OFF["mla_cq"] + 128, 64)]))
    ch.append(("ckv", [(0, OFF["mla_ckv"], 128)]))
    ch.append(("kr", [(0, OFF["mla_kr"], 32)]))
    for i in range(2):
        ch.append(("rr%d" % i, [(0, OFF["rw_r"] + i * 128, 128)]))
        ch.append(("rk%d" % i, [(0, OFF["rw_k"] + i * 128, 128)]))
        ch.append(("rv%d" % i, [(0, OFF["rw_v"] + i * 128, 128)]))
    ch.append(("rwd", [(0, OFF["rw_wd"], 64)]))
    ch.append(("rad", [(0, OFF["rw_ad"], 64)]))
    for i in range(8):
        ch.append(("g%d" % i, [(0, OFF["gate"] + i * 128, 128)]))
    return ch


CHUNKS = chunk_table()
CH_IDX = {n: i for i, (n, _) in enumerate(CHUNKS)}
NCH = len(CHUNKS)


def host_win(w_in):
    out = np.zeros((DEPTH, NCH, 128, 8, 128), np.float32)
    for ci, (_, parts) in enumerate(CHUNKS):
        for dst, src, w in parts:
            blk = w_in[:, :, src:src + w].reshape(DEPTH, 8, 128, w)
            out[:, ci, :, :, dst:dst + w] = blk.transpose(0, 2, 1, 3)
    return out


class K:
    pass


def build(depth=DEPTH, mixers=("gdn", "mla", "rwkv"), dbg=False):
    nc = bass.Bass("TRN2", target_bir_lowering=False)
    k = K()
    k.nc = nc
    k.dbg = dbg
    k.dbg_outs = []
    k.cut = 99

    def din(name, shape, dt=F32):
        return nc.dram_tensor(name, list(shape), dt, kind="ExternalInput").ap()

    k.x_d = din("x", [L, D])
    k.win_d = din("win", [DEPTH, NCH, 128, 8, 128])
    k.normg_d = din("normg", [DEPTH, 128, 8])
    k.wout_d = din("wout", [DEPTH, 128, 8, 1024])
    k.fing_d = din("fing", [1, D])
    k.ident_d = din("ident", [128, 128])
    k.out_d = nc.dram_tensor("out", [L, D], F32, kind="ExternalOutput").ap()
    mla_decl(k)
    gdn_decl(k)
    rwkv_decl(k)

    with ExitStack() as st:
        P = Prog(nc, st)
        k.P = P
        k.xscr = nc.dram_tensor("xscr", [L, D], F32, kind="Internal").ap()
        P.dram_track("xscr", L * D * 4, 128 * D * 4)
        k.hT = P.sb("hT", [128, 8, L], BF16, blk=512)
        k.ident = P.sb("ident", [128, 128], F32)
        k.identb = P.sb("identb", [128, 128], BF16)
        k.normg = P.sb("normg", [128, DEPTH, 8], F32)
        k.wst = [P.sb("wst%d" % i, [128, 8, 128], F32) for i in range(2)]
        k.wbf = [P.sb("wbf%d" % i, [128, 8, 128], BF16) for i in range(2)]
        k.wrr = 0
        k.PAW = [P.ps("paw%d" % i, [128, 1024], F32, blk=512) for i in range(2)]
        k.PA = [k.PAW[i // 2][:, (i % 2) * 512:(i % 2 + 1) * 512] for i in range(4)]
        k.PB = [P.ps("pb%d" % i, [128, 512], F32) for i in range(4)]

        P.dma(k.ident[:], k.ident_d[:])
        P.cp(k.identb[:], k.ident[:])
        for l in range(DEPTH):
            P.dma(k.normg[:, l, :], k.normg_d[l])

        for l in range(depth):
            phase_a(k, l)
            with scope(k):
                k.mix_r = P.sb("mix_r", [128, 2, L], BF16, blk=512)
                if "rwkv" in mixers:
                    rwkv_phase(k, l)
                else:
                    P.memset(k.mix_r[:].rearrange("p a b -> p (a b)"), 1.0)
                with scope(k):
                    k.mix_m = P.sb("mix_m", [128, 3, L], BF16, blk=512)
                    if "mla" in mixers:
                        mla_phase(k, l)
                    else:
                        P.memset(k.mix_m[:].rearrange("p a b -> p (a b)"), 1.0)
                    with scope(k):
                        k.mix_g = P.sb("mix_g", [128, 3, L], BF16, blk=512)
                        if "gdn" in mixers:
                            gdn_phase(k, l)
                        else:
                            P.memset(k.mix_g[:].rearrange("p a b -> p (a b)"), 1.0)
                        if k.dbg:
                            for nm, t_, n_ in (("g", k.mix_g, 3), ("m", k.mix_m, 3), ("r", k.mix_r, 2)):
                                dump(k, "mix_%s%d" % (nm, l), t_[:].rearrange("p a b -> p (a b)"), [128, n_ * L])
                        phase_z(k, l, last=(l == depth - 1))
        P.finish()
        print("ops:", {e: len(v) for e, v in P.ops.items()}, "sig:", P.stats_sig, "dma:", P.ndma_ops)
    return k


def mixc(k, c):
    if c < 3:
        return k.mix_g[:, c, :]
    if c < 6:
        return k.mix_m[:, c - 3, :]
    return k.mix_r[:, c - 6, :]


def dump(k, name, ap, shape=None):
    if not k.dbg:
        return
    P = k.P
    shape = list(ap.shape) if shape is None else shape
    d = k.nc.dram_tensor("dbg_" + name, shape, ap.dtype, kind="ExternalOutput").ap()
    P.dma(d[:] if len(shape) == 2 else d, ap, is_output=True)
    k.dbg_outs.append("dbg_" + name)


def scope(k):
    class _S:
        def __enter__(s):
            s.old = k.P.stack
            s.st = ExitStack()
            s.st.__enter__()
            k.P.stack = s.st
            return s

        def __exit__(s, *a):
            k.P.barrier()
            k.P.stack = s.old
            s.st.__exit__(*a)
            return False
    return _S()


def phase_a(k, l):
    P = k.P
    with scope(k):
        ssq = P.sb("a_ssq", [128, NT])
        rs = P.sb("a_rs", [128, NT])
        rstd = P.sb("a_rstd", [128, NT])
        junk = [P.sb("a_junk%d" % i, [128, D], BF16) for i in range(2)]
        xs = [P.sb("a_xs%d" % i, [128, D], BF16) for i in range(2)]
        xin = [P.sb("a_xin%d" % i, [128, D], F32) for i in range(3)]
        src = k.x_d if l == 0 else k.xscr

        def stage1(tt):
            b = tt % 2
            xt_ = xin[tt % 3]
            P.dma(xt_[:], src[tt * 128:(tt + 1) * 128, :])
            P.act(junk[b][:], xt_[:], AF.Square, accum_out=ssq[:, tt:tt + 1])
            P.act(rs[:, tt:tt + 1], ssq[:, tt:tt + 1], AF.Sqrt, bias=EPS, scale=1.0 / D)
            P.recip(rstd[:, tt:tt + 1], rs[:, tt:tt + 1])
            P.ts(xs[b][:], xt_[:], rstd[:, tt:tt + 1], ALU.mult)

        def stage2(tt):
            b = tt % 2
            pt = k.PB[b][:].bitcast(BF16)
            for dc in range(8):
                P.tr(pt[:, dc * 128:(dc + 1) * 128], xs[b][:, dc * 128:(dc + 1) * 128], k.identb[:])
            P.cp(k.hT[:, :, tt * 128:(tt + 1) * 128], pt[:].rearrange("p (a b) -> p a b", a=8),
                 eng=("act" if tt % 2 == 0 else "dve"))
        stage1(0)
        for tt in range(NT):
            if tt + 1 < NT:
                stage1(tt + 1)
            stage2(tt)


def proj(k, l, name, alt=False):
    P = k.P
    BK = k.PB if alt else k.PA
    ci = CH_IDX[name]
    b = k.wrr
    k.wrr ^= 1
    P.dma(k.wst[b][:], k.win_d[l, ci], eng="sp")
    gb = k.normg[:, l, :].unsqueeze(2).broadcast_to([128, 8, 128])
    P.tt(k.wbf[b][:], k.wst[b][:], gb, ALU.mult, eng="pool")
    for tb in range(4):
        for dc in range(8):
            P.mm(BK[tb][:, :], k.wbf[b][:, dc, :], k.hT[:, dc, tb * 512:(tb + 1) * 512], start=(dc == 0), stop=(dc == 7))
    return BK


ZQ = "act"


def phase_z(k, l, last):
    P = k.P
    with scope(k):
        wst = P.sb("z_wst", [128, 8, 512], F32)
        wob = P.sb("z_wob", [128, 8, 1024], BF16, blk=512)
        sg = [P.sb("z_sg%d" % i, [128, L], BF16, blk=512) for i in range(2)]
        for nb in range(2):
            P.dma(wst[:], k.wout_d[l, :, :, nb * 512:(nb + 1) * 512])
            P.cp(wob[:, :, nb * 512:(nb + 1) * 512], wst[:], eng="act")
        for gc in range(8):
            pa = proj(k, l, "g%d" % gc, alt=(gc % 2 == 1))
            s = sg[gc % 2]
            for tb in range(4):
                P.act(s[:, tb * 512:(tb + 1) * 512], pa[tb][:, :], AF.Silu)
                mc = mixc(k, gc)[:, tb * 512:(tb + 1) * 512]
                P.tt(mc, mc, s[:, tb * 512:(tb + 1) * 512], ALU.mult)
        xin = [P.sb("z_xin%d" % i, [128, D], F32) for i in range(3)]
        src = k.x_d if l == 0 else k.xscr
        if last:
            ssq = P.sb("f_ssq", [128, NT])
            rs = P.sb("f_rs", [128, NT])
            rstd = P.sb("f_rstd", [128, NT])
            junk = [P.sb("f_junk%d" % i, [128, D], BF16) for i in range(2)]
            gf = P.sb("f_g", [128, D])
            ot = [P.sb("f_o%d" % i, [128, D]) for i in range(2)]
            P.dma(gf[:], k.fing_d[0:1, :].partition_broadcast(128))
        def za(tt):
            xt_ = xin[tt % 3]
            P.dma(xt_[:], src[tt * 128:(tt + 1) * 128, :])
            for nb in range(2):
                ps = k.PB[(tt * 2 + nb) % 4]
                for kc in range(8):
                    P.mm(ps[:, :], mixc(k, kc)[:, tt * 128:(tt + 1) * 128], wob[:, kc, nb * 512:(nb + 1) * 512],
                         start=(kc == 0), stop=(kc == 7))
                xs = xt_[:, nb * 512:(nb + 1) * 512]
                P.tt(xs, xs, ps[:, :], ALU.add)
            if not last:
                P.dma(k.xscr[tt * 128:(tt + 1) * 128, :], xt_[:], eng=ZQ)
            else:
                b = tt % 2
                P.act(junk[b][:], xt_[:], AF.Square, accum_out=ssq[:, tt:tt + 1])
                P.act(rs[:, tt:tt + 1], ssq[:, tt:tt + 1], AF.Sqrt, bias=EPS, scale=1.0 / D)

        def zb(tt):
            if last:
                xt_ = xin[tt % 3]
                b = tt % 2
                P.recip(rstd[:, tt:tt + 1], rs[:, tt:tt + 1])
                P.stt(ot[b][:], xt_[:], rstd[:, tt:tt + 1], gf[:], ALU.mult, ALU.mult)
                P.dma(k.out_d[tt * 128:(tt + 1) * 128, :], ot[b][:], eng=ZQ, is_output=True)
        za(0)
        for tt in range(NT):
            if tt + 1 < NT:
                za(tt + 1)
            zb(tt)


def host_inputs(inp, b):
    m = {}
    m["x"] = np.ascontiguousarray(inp["x"][b])
    m["win"] = host_win(inp["w_in"])
    m["normg"] = np.ascontiguousarray(inp["norm_g"].reshape(DEPTH, 8, 128).transpose(0, 2, 1))
    m["wout"] = np.ascontiguousarray(inp["w_out"].reshape(DEPTH, 8, 128, 1024).transpose(0, 2, 1, 3))
    m["fing"] = np.ascontiguousarray(inp["final_norm_g"].reshape(1, D))
    m["ident"] = np.eye(128, dtype=np.float32)
    host_mla(inp, b, m)
    host_gdn(inp, b, m)
    host_rwkv(inp, b, m)
    return m


TWO_PI = 2.0 * np.pi


def host_mla(inp, b, m):
    half = 16
    inv_freq = (10000.0 ** (-np.arange(half, dtype=np.float32) / half)).astype(np.float32)
    invf = np.zeros((32, 1), np.float32)
    invf[:, 0] = np.tile(inv_freq, 2) / np.float32(TWO_PI)
    m["invf"] = invf
    rm = np.zeros((32, 32), np.float32)
    for i in range(16):
        rm[i, i + 16] = -1.0
        rm[i + 16, i] = 1.0
    m["rmT"] = np.ascontiguousarray(rm.T)
    m["pos"] = np.ascontiguousarray(inp["positions"][b].reshape(1, L).astype(np.int32))
    wuq = inp["mla_w_uq"]
    o = np.zeros((DEPTH, 128, 2, 6, 128), np.float32)
    for h in range(6):
        nope = wuq[:, :, h * 96:h * 96 + 64]
        rope = wuq[:, :, h * 96 + 64:h * 96 + 96]
        o[:, :, 0, h, 64:128] = nope[:, 0:128]
        o[:, 0:64, 1, h, 64:128] = nope[:, 128:192]
        o[:, :, 0, h, 0:32] = rope[:, 0:128]
        o[:, 0:64, 1, h, 0:32] = rope[:, 128:192]
    m["wuq"] = o
    gq = np.zeros((DEPTH, 128, 2), np.float32)
    gq[:, :, 0] = inp["mla_q_norm_g"][:, 0:128]
    gq[:, 0:64, 1] = inp["mla_q_norm_g"][:, 128:192]
    m["gq"] = gq
    wukv = inp["mla_w_ukv"]
    wk = np.zeros((DEPTH, 128, 6, 128), np.float32)
    wv = np.zeros((DEPTH, 128, 6, 64), np.float32)
    for h in range(6):
        wk[:, :, h, 64:128] = wukv[:, :, h * 128:h * 128 + 64]
        wv[:, :, h, :] = wukv[:, :, h * 128 + 64:h * 128 + 128]
    m["wuk"] = wk
    m["wuv"] = wv
    m["gkv"] = np.ascontiguousarray(inp["mla_kv_norm_g"].reshape(DEPTH, 128, 1))


def mla_decl(k):
    nc = k.nc

    def din(name, shape, dt=F32):
        return nc.dram_tensor(name, list(shape), dt, kind="ExternalInput").ap()
    k.invf_d = din("invf", [32, 1])
    k.rmT_d = din("rmT", [32, 32])
    k.pos_d = din("pos", [1, L], I32)
    k.wuq_d = din("wuq", [DEPTH, 128, 2, 6, 128])
    k.gq_d = din("gq", [DEPTH, 128, 2])
    k.wuk_d = din("wuk", [DEPTH, 128, 6, 128])
    k.wuv_d = din("wuv", [DEPTH, 128, 6, 64])
    k.gkv_d = din("gkv", [DEPTH, 128, 1])


def latent_norm(k, l, names, nfeat, outs, ones):
    P = k.P
    sq = [P.sb("ln_sq%d" % i, [128, 512]) for i in range(2)]
    rq = [P.sb("ln_rq%d" % i, [128, 512]) for i in range(2)]
    n = len(names)
    for i, nm in enumerate(names):
        pa = proj(k, l, nm)
        for tb in range(4):
            s = sq[tb % 2]
            sk = ""
            if "a" not in sk:
                P.act(s[:], pa[tb][:, :], AF.Square)
            if "c" not in sk:
                P.cp(outs[i][:, tb * 512:(tb + 1) * 512], pa[tb][:, :], eng="dve")
            if "m" not in sk:
                P.mm(k.PB[tb][:, :], ones[:], s[:], start=(i == 0), stop=(i == n - 1))
    c2 = 9
    if c2 < 1:
        return
    for tb in range(4):
        r = rq[tb % 2]
        P.rpow(r[:], k.PB[tb][:, :], -0.5, scale=1.0 / nfeat, bias=EPS)
        for i in range(n):
            o = outs[i][:, tb * 512:(tb + 1) * 512]
            P.tt(o, o, r[:], ALU.mult)


def mla_phase(k, l):
    P = k.P
    SC = 96.0 ** -0.5
    with scope(k):
        cqn0 = P.sb("m_cqn0", [128, L], BF16, blk=512)
        cqn1 = P.sb("m_cqn1", [128, L], BF16, blk=512)
        ckvn = P.sb("m_ckvn", [128, L], BF16, blk=512)
        krope = P.sb("m_krope", [32, L], BF16, blk=512)
        cos2 = P.sb("m_cos2", [32, L], BF16, blk=512)
        sin2 = P.sb("m_sin2", [32, L], BF16, blk=512)
        wq = P.sb("m_wq", [128, 2, 6, 128], BF16)
        wk = P.sb("m_wk", [128, 6, 128], BF16)
        wv = P.sb("m_wv", [128, 6, 64], BF16)
        ones = P.sb("m_ones", [128, 128], F32)
        onesk = P.sb("m_onesk", [128, 128], F32)
        rmT = P.sb("m_rmT", [32, 32], F32)
        P.memset(ones[:], 1.0)
        P.memset(onesk[:], 1.0)
        P.memset(onesk[32:64, :], 0.0)
        P.dma(rmT[:], k.rmT_d[:])
        with scope(k):
            st = P.sb("m_st", [128, 2, 6, 128], F32)
            g = P.sb("m_g", [128, 4], F32)
            P.dma(st[:], k.wuq_d[l])
            P.dma(g[:, 0:2], k.gq_d[l])
            P.dma(g[:, 2:3], k.gkv_d[l])
            for kc in range(2):
                P.ts(wq[:, kc].rearrange("p a b -> p (a b)"), st[:, kc].rearrange("p a b -> p (a b)"),
                     g[:, kc:kc + 1], ALU.mult)
            st2 = P.sb("m_st2", [128, 6, 128], F32)
            P.dma(st2[:], k.wuk_d[l])
            P.ts(wk[:].rearrange("p a b -> p (a b)"), st2[:].rearrange("p a b -> p (a b)"), g[:, 2:3], ALU.mult)
            st3 = P.sb("m_st3", [128, 6, 64], F32)
            P.dma(st3[:], k.wuv_d[l])
            P.ts(wv[:].rearrange("p a b -> p (a b)"), st3[:].rearrange("p a b -> p (a b)"), g[:, 2:3], ALU.mult)
        if k.cut < 1:
            return
        with scope(k):
            latent_norm(k, l, ["cq0", "cq1"], 192, [cqn0, cqn1], ones)
            latent_norm(k, l, ["ckv"], 128, [ckvn], ones)
        if k.cut < 2:
            return
        with scope(k):
            invf = P.sb("m_invf", [32, 1], F32)
            P.dma(invf[:], k.invf_d[:])
            pa = proj(k, l, "kr")

            def rope_blk(tb):
                sl = slice(tb * 512, (tb + 1) * 512)
                posi = P.sb("m_posi%d" % tb, [32, 512], I32)
                y = P.sb("m_y%d" % tb, [32, 512], F32)
                yi = P.sb("m_yi%d" % tb, [32, 512], I32)
                fr = P.sb("m_fr%d" % tb, [32, 512], F32)
                kr = P.sb("m_kr%d" % tb, [32, 512], F32)
                t1 = P.sb("m_t1%d" % tb, [32, 512], F32)
                t2 = P.sb("m_t2%d" % tb, [32, 512], F32)
                P.dma(posi[:], k.pos_d[0:1, sl].partition_broadcast(32))
                P.cp(kr[:], pa[tb][0:32, :], eng="act")
                yield
                P.cp(y[:], posi[:])
                P.mm(k.PB[tb][0:32, :], rmT[:], kr[:])
                yield
                P.ts(y[:], y[:], invf[:, 0:1], ALU.mult)
                yield
                for off, dst in ((0.0, sin2), (0.25, cos2)):
                    if off != 0.0:
                        P.ts(y[:], y[:], off, ALU.add)
                        yield
                    P.cp(yi[:], y[:])
                    yield
                    P.cp(fr[:], yi[:])
                    yield
                    P.tt(fr[:], y[:], fr[:], ALU.subtract)
                    yield
                    P.act(dst[:, sl], fr[:], AF.Sin, scale=TWO_PI * (1.0 - 1e-6))
                    yield
                P.tt(t1[:], kr[:], cos2[:, sl], ALU.mult)
                P.tt(t2[:], k.PB[tb][0:32, :], sin2[:, sl], ALU.mult)
                yield
                P.tt(krope[:, sl], t1[:], t2[:], ALU.add)
                yield
            run_interleaved([rope_blk(tb) for tb in range(4)])
        if k.cut < 3:
            return
        kT = [P.sb("m_kT%d" % i, [128, L], BF16, blk=512) for i in range(2)]
        qT = [P.sb("m_qT%d" % i, [128, L], BF16, blk=512) for i in range(2)]
        Vh = [P.sb("m_V%d" % i, [128, NT, 96], BF16) for i in range(2)]
        pT = [P.sb("m_pT%d" % i, [128, 1024], BF16, blk=512) for i in range(4)]
        sq = [P.sb("m_sq%d" % i, [128, 512], BF16) for i in range(2)]
        qr = [P.sb("m_qr%d" % i, [32, 512], BF16) for i in range(2)]
        onesb = P.sb("m_onesb", [128, 128], BF16)
        oneskb = P.sb("m_oneskb", [128, 128], BF16)
        rmTb = P.sb("m_rmTb", [32, 32], BF16)
        P.cp(onesb[:], ones[:])
        P.cp(oneskb[:], onesk[:])
        P.cp(rmTb[:], rmT[:])
        t1 = P.sb("m_t1b", [32, 512], F32)
        t2 = P.sb("m_t2b", [32, 512], F32)
        mrow = P.sb("m_mrow", [64, 512], F32)
        km4 = P.sb("m_km4", [128, 4], F32)
        kmax2 = P.sb("m_kmax2", [128, 1], F32)
        rden = [P.sb("m_rden%d" % i, [64, 512], F32) for i in range(2)]
        for i in range(2):
            P.memset(kT[i][32:64, :], 0.0)
            P.memset(kT[i][32:33, :], 1.0)
            P.memset(qT[i][32:64, :], 0.0)
            P.memset(Vh[i][:, :, 64:96], 1.0)
        kmx = [P.sb("m_kmx%d" % i, [128, 1], F32) for i in range(2)]

        def prep(h):
            kt_, qt_, vh_ = kT[h % 2], qT[h % 2], Vh[h % 2]
            kmax2_ = kmx[h % 2]
            P.cp(kt_[0:32, :], krope[:], eng="pool")
            for tb in range(4):
                sl = slice(tb * 512, (tb + 1) * 512)
                s = sq[tb % 2]
                P.mm(k.PB[2][:, :], wk[:, h, :], ckvn[:, sl])
                yield
                P.cp(kt_[64:128, sl], k.PB[2][64:128, :], eng="dve")
                yield
                P.tt(s[:], kt_[:, sl], kt_[:, sl], ALU.mult, eng="pool")
                yield
                yield
                P.mm(k.PB[3][:, :], oneskb[:], s[:])
                yield
                P.red(km4[:, tb:tb + 1], k.PB[3][:, :], ALU.max)
                yield
            P.red(kmax2_[:], km4[:], ALU.max)
            for half in range(2):
                for j in range(8):
                    tt = half * 8 + j
                    P.mm(k.PB[2][:, j * 64:(j + 1) * 64], ckvn[:, tt * 128:(tt + 1) * 128], wv[:, h, :])
                yield
                P.cp(vh_[:, half * 8:(half + 1) * 8, 0:64], k.PB[2][:, :].rearrange("p (a b) -> p a b", a=8), eng="dve")
                yield
            for tb in range(4):
                sl = slice(tb * 512, (tb + 1) * 512)
                s = sq[tb % 2]
                q_ = qr[tb % 2]
                P.mm(k.PB[2][:, :], wq[:, 0, h, :], cqn0[:, sl], start=True, stop=False)
                P.mm(k.PB[2][:, :], wq[:, 1, h, :], cqn1[:, sl], start=False, stop=True)
                yield
                P.cp(qt_[64:128, sl], k.PB[2][64:128, :], eng="dve")
                P.cp(q_[:], k.PB[2][0:32, :], eng="dve")
                yield
                P.act(s[:], k.PB[2][:, :], AF.Square)
                yield
                P.mm(k.PB[3][:, :], onesb[:], s[:])
                yield
                P.act(mrow[32:33, :], k.PB[3][32:33, :], AF.Sqrt, scale=kmax2_[32:33, 0:1])
                yield
                P.ts(qt_[32:33, sl], mrow[32:33, :], -1.0, ALU.mult)
                P.mm(k.PB[2][0:32, :], rmTb[:], q_[:])
                P.tt(t1[:], q_[:], cos2[:, sl], ALU.mult, eng="pool")
                yield
                P.tt(t2[:], k.PB[2][0:32, :], sin2[:, sl], ALU.mult)
                yield
                P.tt(qt_[0:32, sl], t1[:], t2[:], ALU.add)
                yield

        def attn(h):
            kt_, qt_, vh_ = kT[h % 2], qT[h % 2], Vh[h % 2]
            pti = 0
            for qb in range(4):
                qs = slice(qb * 512, (qb + 1) * 512)
                O = k.PB[qb % 2]

                def s_pair(m_):
                    for j in range(2):
                        kt = 2 * m_ + j
                        P.mm(k.PAW[m_ % 2][:, j * 512:(j + 1) * 512], kt_[:, kt * 128:(kt + 1) * 128], qt_[:, qs])
                s_pair(0)
                s_pair(1)
                for m_ in range(NT // 2):
                    p_ = pT[pti % 4]
                    pti += 1
                    P.act(p_[:], k.PAW[m_ % 2][:, :], AF.Exp, scale=SC)
                    if m_ + 2 < NT // 2:
                        s_pair(m_ + 2)
                    for j in range(2):
                        kt = 2 * m_ + j
                        P.mm(O[0:96, :], vh_[:, kt, :], p_[:, j * 512:(j + 1) * 512], start=(kt == 0), stop=(kt == NT - 1))
                    yield
                rd = rden[qb % 2]
                P.rpow(rd[0:32, :], O[64:96, :], -1.0)
                P.rpow(rd[32:64, :], O[64:96, :], -1.0)
                ob = (h % 2) * 64
                P.tt(mixc(k, 3 + h // 2)[ob:ob + 64, qs], O[0:64, :], rd[:], ALU.mult)
                yield

        for _ in prep(0):
            pass
        mode = "il"
        for h in range(6):
            gens = [attn(h)]
            if h + 1 < 6:
                if mode == "il":
                    gens.append(prep(h + 1))
                elif mode == "seq":
                    run_interleaved(gens)
                    gens = [prep(h + 1)]
                elif mode == "noprep":
                    pass
            if mode == "noprep" and h > 0:
                gens = [attn(0)]
            run_interleaved(gens)


NCK = L // 64
NEG = -30000.0


def host_gdn(inp, b, m):
    cw = inp["gdn_conv"]
    o = np.zeros((DEPTH, 128, 9, 5), np.float32)
    for part in range(3):
        for p in range(3):
            o[:, :, part * 3 + p, :] = cw[:, :, part * 384 + p * 128: part * 384 + (p + 1) * 128].transpose(0, 2, 1)
    m["gconv"] = o
    gb = np.zeros((DEPTH, 128, 2), np.float32)
    for d in range(2):
        gb[:, d * 32:d * 32 + 6, 0] = inp["gdn_dt_bias"][:, d, :]
        gb[:, d * 32:d * 32 + 6, 1] = inp["gdn_a_log"][:, d, :]
    m["ggb"] = gb
    m["gng"] = np.ascontiguousarray(np.tile(inp["gdn_norm_g"], (1, 2)).reshape(DEPTH, 128, 1))
    sel = np.zeros((64, 6, 128), np.float32)
    for d in range(2):
        for p in range(3):
            sel[d * 32 + 2 * p, d * 3 + p, 0:64] = 1.0
            sel[d * 32 + 2 * p + 1, d * 3 + p, 64:128] = 1.0
    m["gsel"] = sel
    j = np.arange(64)[:, None]
    i = np.arange(64)[None, :]
    nm = np.zeros((128, 2, 64), np.float32)
    nm[:, 0, :] = np.tile(np.where(i > j, 0.0, NEG), (2, 1))
    nm[:, 1, :] = np.tile(np.where(i < j, 0.0, NEG), (2, 1))
    m["gnegm"] = nm
    m["gid2"] = np.ascontiguousarray(np.tile(np.eye(64, dtype=np.float32), (2, 1)))


def gdn_decl(k):
    nc = k.nc

    def din(name, shape, dt=F32):
        return nc.dram_tensor(name, list(shape), dt, kind="ExternalInput").ap()
    k.gconv_d = din("gconv", [DEPTH, 128, 9, 5])
    k.ggb_d = din("ggb", [DEPTH, 128, 2])
    k.gng_d = din("gng", [DEPTH, 128, 1])
    k.gsel_d = din("gsel", [64, 6, 128])
    k.gnegm_d = din("gnegm", [128, 2, 64])
    k.gid2_d = din("gid2", [128, 64])


def bc3(ap2, n):
    return ap2.unsqueeze(2).broadcast_to([ap2.shape[0], ap2.shape[1], n])


def bcm(ap2, n):
    return ap2.unsqueeze(1).broadcast_to([ap2.shape[0], n, ap2.shape[1]])


HS = (slice(0, 64), slice(64, 128))


def v3(ps, n=8):
    return ps[:, 0:n * 64].rearrange("p (a b) -> p a b", a=n)


def mm2(P, ps, c, lhsT, rhs, **kw):
    for hs in HS:
        P.mm(ps[hs, c * 64:(c + 1) * 64], lhsT[hs], rhs[hs], **kw)


def tr2(P, ps, c, in_, ident):
    for hs in HS:
        P.mm(ps[hs, c * 64:(c + 1) * 64], in_[hs], ident[hs, hs])


def neumann2(k, Nn, Rm, tmp, bank, id2, n8=8):
    P = k.P
    idb = bcm(id2[:, :], n8)
    tA, tB, tC, tD = tmp
    pa_, pb_, pc_ = bank
    for c in range(n8):
        tr2(P, pa_, c, Nn[:, c, :], k.identb)
    P.cp(tA[:], v3(pa_, n8), eng="act")
    P.tt(Rm[:], Nn[:], idb, ALU.add)
    yield
    cur, curT = Nn, tA
    targets = [(tB, tC), (tD, tA)]
    for lvl in range(1, 7):
        nxt, nxtT = targets[(lvl - 1) % 2]
        if lvl >= 2:
            for c in range(n8):
                mm2(P, pc_, c, curT[:, c, :], Rm[:, c, :])
            P.tt(Rm[:], Rm[:], v3(pc_, n8), ALU.add)
        if lvl <= 5:
            for c in range(n8):
                mm2(P, pb_, c, cur[:, c, :], curT[:, c, :])
            P.cp(nxtT[:], v3(pb_, n8), eng="act")
            if lvl < 5:
                for c in range(n8):
                    mm2(P, pa_, c, curT[:, c, :], cur[:, c, :])
                P.cp(nxt[:], v3(pa_, n8), eng="act")
        yield
        cur, curT = nxt, nxtT


def run_interleaved(gens, skew=0):
    gens = list(gens)
    for _ in range(skew):
        try:
            next(gens[0])
        except StopIteration:
            gens.pop(0)
            break
    while gens:
        for g in list(gens):
            try:
                next(g)
            except StopIteration:
                gens.remove(g)


def norm_pipe(k, n, src_fn, sqb, rnb, banks, bones, power, scale, bias, post_fn):
    P = k.P

    def pre(i):
        P.act(sqb[i % 2][:], src_fn(i), AF.Square)
        P.mm(banks[i % 2][:, :], bones[:], sqb[i % 2][:])

    def post(i):
        P.rpow(rnb[i % 2][:], banks[i % 2][:, :], power, scale=scale, bias=bias)
        post_fn(i, rnb[i % 2])
    pre(0)
    for i in range(n):
        if i + 1 < n:
            pre(i + 1)
        post(i)


def run_pipelined(chains, depth=2):
    active = []
    nxt = [0] * len(chains)

    def start(ci):
        if nxt[ci] < len(chains[ci]):
            active.append((ci, chains[ci][nxt[ci]](nxt[ci] % depth)))
            nxt[ci] += 1
    for ci in range(len(chains)):
        for _ in range(depth):
            start(ci)
    while active:
        for item in list(active):
            try:
                next(item[1])
            except StopIteration:
                active.remove(item)
                start(item[0])


def gdn_phase(k, l):
    P = k.P
    with scope(k):
        GC = P.sb("g_GC", [64, L], F32, blk=512)
        GP = [P.sb("g_GP%d" % p, [128, NCK, 4], F32) for p in range(3)]
        NBP = [P.sb("g_NBP%d" % p, [128, NCK, 2], F32) for p in range(3)]
        sel = P.sb("g_sel", [64, 6, 128], F32)
        negm = P.sb("g_negm", [128, 2, 64], F32)
        id2 = P.sb("g_id2", [128, 64], F32)
        cw = P.sb("g_cw", [128, 9, 5], F32)
        ng = P.sb("g_ng", [128, 1], F32)
        bones = P.sb("g_bones", [128, 128], F32)
        P.dma(sel[:], k.gsel_d[:])
        P.dma(negm[:], k.gnegm_d[:])
        P.dma(id2[:], k.gid2_d[:])
        P.dma(cw[:], k.gconv_d[l])
        P.dma(ng[:], k.gng_d[l])
        P.memset(bones[:], 0.0)
        P.memset(bones[0:64, 0:64], 1.0)
        P.memset(bones[64:128, 64:128], 1.0)
        bonesb = P.sb("g_bonesb", [128, 128], BF16)
        P.cp(bonesb[:], bones[:])
        with scope(k):
            GT = P.sb("g_GT", [128, L], F32, blk=512)
            m0 = P.sb("g_m0", [64, L], F32)
            gb = P.sb("g_gb", [128, 2], F32)
            negA = P.sb("g_negA", [128, 1], F32)
            P.dma(gb[:], k.ggb_d[l])
            P.act(negA[:], gb[:, 1:2], AF.Exp)
            P.ts(negA[:], negA[:], -1.0, ALU.mult)
            P.memset(m0[:], 1.0)
            P.memset(m0[:, 0:L:64], 0.0)
            pa = proj(k, l, "gab")
            for tb in range(4):
                sl = slice(tb * 512, (tb + 1) * 512)
                P.act(GT[0:64, sl], pa[tb][0:64, :], AF.Identity, bias=gb[0:64, 0:1])
                P.act(GT[64:128, sl], pa[tb][64:128, :], AF.Sigmoid)
            P.act(GC[:, :], GT[0:64, :], AF.Abs)
            P.act(GC[:, :], GC[:, :], AF.Exp, scale=-1.0)
            P.act(GC[:, :], GC[:, :], AF.Ln, bias=1.0)
            P.act(GT[0:64, :], GT[0:64, :], AF.Relu)
            P.tt(GT[0:64, :], GT[0:64, :], GC[:, :], ALU.add)
            P.ts(GT[0:64, :], GT[0:64, :], negA[0:64, 0:1], ALU.mult)
            P.scan(GC[:, :], m0[:, :], GT[0:64, :], 0.0, ALU.mult, ALU.add)
            gc3 = GC[32:64, :].rearrange("p (a b) -> p a b", b=64)
            P.tt(m0[32:64, :].rearrange("p (a b) -> p a b", b=64), bc3(GC[32:64, 63:L:64], 64), gc3, ALU.subtract)
            P.tt(GC[32:64, :], m0[32:64, :], GT[32:64, :], ALU.add)
            for grp in range(4):
                g8 = slice(grp * 8, (grp + 1) * 8)
                for c in range(8):
                    ck = grp * 8 + c
                    cs = slice(c * 64, (c + 1) * 64)
                    for hs in HS:
                        P.mm(k.PB[2 * (grp % 2)][hs, cs], GC[:, ck * 64:(ck + 1) * 64], k.ident[0:64, 0:64])
                        P.mm(k.PB[2 * (grp % 2) + 1][hs, cs], GT[64:128, ck * 64:(ck + 1) * 64], k.ident[64:128, 64:128])
                n_ = 0
                for p in range(3):
                    for hf, hs in enumerate(HS):
                        h = 2 * p + hf
                        for q, ps in ((0, k.PB[2 * (grp % 2)]), (1, k.PB[2 * (grp % 2) + 1])):
                            src = v3(ps)[hs, :, h:h + 33:32]
                            P.cp(GP[p][hs, g8, 2 * q:2 * q + 2], src, eng=("act" if q else "dve"))
            for p in range(3):
                P.ts(NBP[p][:], GP[p][:, :, 2:4], -1.0, ALU.mult)
        for p in range(3):
            with scope(k):
                Q = P.sb("g_Q", [128, L], BF16, blk=512)
                K_ = P.sb("g_K", [128, L], BF16, blk=512)
                Kt = P.sb("g_Kt", [128, NCK, 64], BF16, blk=512)
                Vt = P.sb("g_Vt", [128, NCK, 64], BF16, blk=512)
                O = P.sb("g_O", [128, L], F32, blk=512)
                P.memset(O[:], 0.0, eng="pool")
                with scope(k):
                    xp = P.sb("g_xp", [128, L + 4], BF16)
                    Dg = P.sb("g_Dg", [128, 5, 128], BF16)
                    Vf = P.sb("g_Vf", [128, L], BF16, blk=512)
                    cv = P.sb("g_cv", [128, L], F32, blk=512)
                    sq = P.sb("g_sq", [128, 512], BF16)
                    rn = P.sb("g_rn", [128, 512], F32)
                    sq2 = P.sb("g_sq2", [128, 512], BF16)
                    rn2 = P.sb("g_rn2", [128, 512], F32)
                    P.memset(xp[:, 0:2], 0.0)
                    P.memset(xp[:, L + 2:L + 4], 0.0)
                    for part, nm, dst in ((0, "gq", Q), (1, "gk", K_), (2, "gv", Vf)):
                        pa = proj(k, l, "%s%d" % (nm, p))
                        for tb in range(4):
                            P.cp(xp[:, 2 + tb * 512:2 + (tb + 1) * 512], pa[tb][:, :], eng=("act" if tb % 2 else "dve"))
                        wi = part * 3 + p
                        for j in range(5):
                            P.ts(Dg[:, j, :], k.identb[:], cw[:, wi, j:j + 1], ALU.mult, eng=("pool" if j % 2 else "dve"))
                        for tb in range(4):
                            for j in range(5):
                                P.mm(k.PB[tb][:, :], Dg[:, j, :], xp[:, j + tb * 512:j + (tb + 1) * 512],
                                     start=(j == 0), stop=(j == 4))
                        for tb in range(4):
                            sl = slice(tb * 512, (tb + 1) * 512)
                            P.act((dst if part == 2 else cv)[:, sl], k.PB[tb][:, :], AF.Silu)
                        if part < 2:
                            def fin(tb, r, dst=dst):
                                P.tt(dst[:, tb * 512:(tb + 1) * 512], cv[:, tb * 512:(tb + 1) * 512], r[:], ALU.mult)
                            norm_pipe(k, 4, lambda tb: cv[:, tb * 512:(tb + 1) * 512], (sq, sq2), (rn, rn2),
                                      (k.PA[2], k.PA[3]), bonesb, -0.5, 64.0 if part == 0 else 1.0,
                                      64e-6 if part == 0 else 1e-6, fin)
                    for src, dstt in ((K_, Kt), (Vf, Vt)):
                        for grp in range(4):
                            ps = k.PB[grp % 2]
                            for c in range(8):
                                ck = grp * 8 + c
                                tr2(P, ps, c, src[:, ck * 64:(ck + 1) * 64], k.identb)
                            P.cp(dstt[:, grp * 8:(grp + 1) * 8, :], v3(ps), eng=("act" if grp % 2 else "dve"))
                with scope(k):
                    T = dict(GC=GC, GP=GP[p], NBP=NBP[p], sel=sel, negm=negm, id2=id2, Q=Q, K=K_, Kt=Kt, Vt=Vt, O=O)
                    run_pipelined([gdn_chain(k, p, d, T) for d in range(2)], depth=1)
                with scope(k):
                    sqo = [P.sb("g_osq%d" % i, [128, 512], BF16) for i in range(2)]
                    rno = [P.sb("g_orn%d" % i, [128, 512], F32) for i in range(2)]

                    def fin_o(tb, r):
                        sl = slice(tb * 512, (tb + 1) * 512)
                        P.tt(r[:], O[:, sl], r[:], ALU.mult)
                        P.ts(mixc(k, p)[:, sl], r[:], ng[:, 0:1], ALU.mult)
                    norm_pipe(k, 4, lambda tb: O[:, tb * 512:(tb + 1) * 512], sqo, rno, (k.PB[2], k.PB[3]), bonesb,
                              -0.5, 1.0 / 64, EPS, fin_o)


def gdn_chain(k, p, d, T):
    P = k.P
    GC, GP, NBP, sel, negm, id2, Q, K_, Kt, Vt, O = (T[n] for n in ("GC", "GP", "NBP", "sel", "negm", "id2", "Q", "K", "Kt", "Vt", "O"))
    B = k.PA if d == 0 else k.PB
    tag = "g%d_" % d
    names = ("CB", "EI", "QG", "Rm", "U0", "WT", "BW", "KD", "GK", "AcT", "Sg", "Ug", "nA", "nB", "nC", "nD", "Nb", "PTb", "Sgb")
    f32n = ("CB", "EI", "AcT", "Sg")
    NSET = 1
    GG = [{n: P.sb(tag + "%d" % s_ + n, [128, 8, 64], F32 if n in f32n else BF16) for n in names} for s_ in range(NSET)]
    for s_ in range(NSET):
        GG[s_]["gend"] = P.sb(tag + "gend%d" % s_, [128, 8], F32)
        GG[s_]["kds"] = P.sb(tag + "kds%d" % s_, [128, 8], F32)
    gam = P.sb(tag + "gam", [128, NCK], F32)
    shared = {"scan": 0}
    Scar = P.sb(tag + "Scar", [128, 64], F32)
    e_ = 63 if d == 0 else 0
    idb = bcm(id2[:, :], 8)

    def f2(t):
        return t[:].rearrange("p a b -> p (a b)")
    P.memset(Scar[:], 0.0)
    P.act(gam[:], GP[:, :, d], AF.Exp)
    gorder = list(range(4)) if d == 0 else list(range(3, -1, -1))

    def group(gi, grp, G):
        Nb, PTb, Sgb, gend, kds = G["Nb"], G["PTb"], G["Sgb"], G["gend"], G["kds"]
        sl = slice(grp * 512, (grp + 1) * 512)
        g8 = slice(grp * 8, (grp + 1) * 8)
        cj = GP[:, g8, d]
        nb = NBP[:, g8, d]
        CB, EI, QG, Rm, U0, WT, BW, KD, GK, AcT, Sg, Ug = (G[n] for n in names[:12])
        P.mm(B[0][:, :], sel[:, d * 3 + p, :], GC[:, sl])
        P.cp(f2(CB), B[0][:, :], eng="act")
        P.act(f2(EI), f2(CB), AF.Exp)
        P.cp(gend[:], EI[:, :, e_], eng="pool")
        P.tt(f2(QG), f2(EI), Q[:, sl], ALU.mult)
        P.tt(kds[:], CB[:, :, e_], cj, ALU.subtract)
        P.act(kds[:], kds[:], AF.Exp)
        P.tt(GK[:], Kt[:, g8, :], bc3(gam[:, g8], 64), ALU.mult, eng="pool")
        P.tt(KD[:], Kt[:, g8, :], bc3(kds[:], 64), ALU.mult, eng="pool")
        P.tt(CB[:], CB[:], bc3(cj, 64), ALU.subtract)
        P.tt(CB[:], CB[:], bcm(negm[:, d, :], 8), ALU.add)
        P.act(f2(CB), f2(CB), AF.Exp)
        yield
        P.tt(EI[:], CB[:], idb, ALU.add)
        for c in range(8):
            cs = slice((grp * 8 + c) * 64, (grp * 8 + c + 1) * 64)
            mm2(P, B[0], c, K_[:, cs], Q[:, cs])
        P.tt(PTb[:], EI[:], v3(B[0]), ALU.mult)
        for c in range(8):
            cs = slice((grp * 8 + c) * 64, (grp * 8 + c + 1) * 64)
            mm2(P, B[1], c, K_[:, cs], K_[:, cs])
        P.tt(CB[:], CB[:], v3(B[1]), ALU.mult)
        P.tt(Nb[:], CB[:], bc3(nb, 64), ALU.mult)
        yield
        for _ in neumann2(k, Nb, Rm, (G["nA"], G["nB"], G["nC"], G["nD"]), (B[0], B[1], B[2]), id2):
            yield
        for c in range(8):
            mm2(P, B[2], c, Rm[:, c, :], GK[:, c, :])
        P.tt(BW[:], v3(B[2]), bc3(nb, 64), ALU.mult)
        for c in range(8):
            mm2(P, B[0], c, Rm[:, c, :], Vt[:, grp * 8 + c, :])
        P.tt(U0[:], v3(B[0]), bc3(GP[:, g8, 2 + d], 64), ALU.mult)
        for c in range(8):
            mm2(P, B[1], c, GK[:, c, :], Rm[:, c, :])
        P.cp(WT[:], v3(B[1]), eng="act")
        yield
        for c in range(8):
            mm2(P, B[3], c, BW[:, c, :], KD[:, c, :])
        P.tt(AcT[:], idb, bc3(gend[:], 64), ALU.mult, eng="pool")
        P.tt(AcT[:], AcT[:], v3(B[3]), ALU.add)
        yield
        while shared["scan"] != gi:
            yield
        corder = range(8) if d == 0 else range(7, -1, -1)
        prev = Scar[:]
        for n, c in enumerate(corder):
            P.cp(Sg[:, c, :], prev, eng="pool") if n == 0 else None
            ps = B[2 + n % 2]
            mm2(P, ps, 0, AcT[:, c, :], Sg[:, c, :], start=True, stop=False)
            mm2(P, ps, 0, KD[:, c, :], U0[:, c, :], start=False, stop=True)
            last = (n == 7)
            dst = Scar[:] if last else Sg[:, corder[n + 1], :]
            P.cp(dst, ps[:, 0:64], eng="act")
            yield
        shared["scan"] = gi + 1
        P.cp(Sgb[:], Sg[:], eng="act")
        for c in range(8):
            mm2(P, B[0], c, WT[:, c, :], Sgb[:, c, :])
        P.tt(CB[:], v3(B[0]), bc3(nb, 64), ALU.mult)
        P.tt(Ug[:], CB[:], U0[:], ALU.add)
        yield
        for c in range(8):
            mm2(P, B[1], c, Sgb[:, c, :], QG[:, c, :], start=True, stop=False)
            mm2(P, B[1], c, Ug[:, c, :], PTb[:, c, :], start=False, stop=True)
        P.tt(O[:, sl], O[:, sl], B[1][:, :], ALU.add)
        yield
    return [(lambda slot, gi=gi, grp=grp: group(gi, grp, GG[slot])) for gi, grp in enumerate(gorder)]


RSKEW = 0
RW_EPS = 64e-5
DEC = float(np.exp(-0.5))
GS = 8
NG = NCK // GS


def host_rwkv(inp, b, m):
    mu = inp["rwkv_mu"]
    o = np.zeros((DEPTH, 128, 8, 2), np.float32)
    for part in range(3):
        for p in range(2):
            o[:, :, part * 2 + p, :] = mu[:, :, part * 256 + p * 128: part * 256 + (p + 1) * 128].transpose(0, 2, 1)
    o[:, 0:64, 6, :] = mu[:, :, 768:832].transpose(0, 2, 1)
    o[:, 0:64, 7, :] = mu[:, :, 832:896].transpose(0, 2, 1)
    m["rmu"] = o

    def pp(a):
        if a.ndim == 2:
            return np.ascontiguousarray(a.reshape(DEPTH, 2, 128).transpose(0, 2, 1))
        return np.ascontiguousarray(a.reshape(DEPTH, 2, 2, 128).transpose(0, 3, 1, 2))
    pv = np.zeros((DEPTH, 128, 7, 2), np.float32)
    pv[:, :, 0:2, :] = pp(inp["rwkv_w0"])
    pv[:, :, 2:4, :] = pp(inp["rwkv_a0"])
    pv[:, :, 4, :] = pp(inp["rwkv_k_k"])
    pv[:, :, 5, :] = pp(inp["rwkv_k_a"])
    pv[:, :, 6, :] = pp(inp["rwkv_r_k"].reshape(DEPTH, 256))
    m["rpv"] = pv
    ln = np.zeros((DEPTH, 128, 2, 2), np.float32)
    ln[:, :, 0, :] = pp(inp["rwkv_ln_g"])
    ln[:, :, 1, :] = pp(inp["rwkv_ln_b"])
    m["rln"] = ln
    m["rw2"] = np.ascontiguousarray(inp["rwkv_w2"].transpose(0, 2, 1, 3))
    m["ra2"] = np.ascontiguousarray(inp["rwkv_a2"].transpose(0, 2, 1, 3))
    s_ = np.arange(64)[:, None]
    t_ = np.arange(64)[None, :]
    msk = np.zeros((128, 2, 4, 64), np.float32)
    msk[:, 0, 0, :] = np.tile((t_ > s_), (2, 1))
    msk[:, 0, 1, :] = np.tile((t_ >= s_), (2, 1))
    msk[:, 1, 0, :] = np.tile((t_ < s_), (2, 1))
    msk[:, 1, 1, :] = np.tile((t_ <= s_), (2, 1))
    msk[:, :, 2:4, :] = -msk[:, :, 0:2, :]
    m["rmsk"] = msk


def rwkv_decl(k):
    nc = k.nc

    def din(name, shape, dt=F32):
        return nc.dram_tensor(name, list(shape), dt, kind="ExternalInput").ap()
    k.rmu_d = din("rmu", [DEPTH, 128, 8, 2])
    k.rpv_d = din("rpv", [DEPTH, 128, 7, 2])
    k.rln_d = din("rln", [DEPTH, 128, 2, 2])
    k.rw2_d = din("rw2", [DEPTH, 64, 2, 256])
    k.ra2_d = din("ra2", [DEPTH, 64, 2, 256])
    k.rmsk_d = din("rmsk", [128, 2, 4, 64])


def rwkv_phase(k, l):
    P = k.P
    with scope(k):
        mu = P.sb("r_mu", [128, 8, 3], F32)
        pv = P.sb("r_pv", [128, 7, 2], F32)
        omka = P.sb("r_omka", [128, 2], F32)
        hrk = P.sb("r_hrk", [128, 2], F32)
        ln = P.sb("r_ln", [128, 2, 2], F32)
        w2 = P.sb("r_w2", [64, 2, 256], BF16)
        a2 = P.sb("r_a2", [64, 2, 256], BF16)
        msk = P.sb("r_msk", [128, 2, 4, 64], F32)
        id2 = P.sb("r_id2", [128, 64], F32)
        bones = P.sb("r_bones", [128, 128], F32)
        m0 = P.sb("r_m0", [128, GS * 64], F32)
        twd = P.sb("r_twd", [64, L], BF16, blk=512)
        adx = P.sb("r_adx", [64, L], BF16, blk=512)
        sh32 = P.sb("r_sh32", [128, L], F32, blk=512)
        xp = P.sb("r_xp", [128, L + 2], F32)
        P.dma(mu[:, :, 0:2], k.rmu_d[l])
        P.dma(pv[:], k.rpv_d[l])
        P.dma(ln[:], k.rln_d[l])
        with scope(k):
            w2f = P.sb("r_w2f", [64, 2, 256], F32)
            a2f = P.sb("r_a2f", [64, 2, 256], F32)
            P.dma(w2f[:], k.rw2_d[l])
            P.dma(a2f[:], k.ra2_d[l])
            P.cp(w2[:], w2f[:], eng="act")
            P.cp(a2[:], a2f[:], eng="act")
        P.dma(msk[:], k.rmsk_d[:])
        P.dma(id2[:], k.gid2_d[:])
        P.memset(bones[:], 0.0)
        P.memset(bones[0:64, 0:64], 1.0)
        P.memset(bones[64:128, 64:128], 1.0)
        P.memset(m0[:], 1.0)
        P.memset(m0[:, 0:GS * 64:64], 0.0)
        P.memset(xp[:, 0:1], 0.0)
        P.memset(xp[:, L + 1:L + 2], 0.0)
        P.tt(mu[:, :, 2], mu[:, :, 0], mu[:, :, 1], ALU.add)
        P.ts(mu[:, :, 2], mu[:, :, 2], -1.0, ALU.mult, 1.0, ALU.add)
        P.ts(omka[:], pv[:, 5, :], -1.0, ALU.mult, 1.0, ALU.add)
        P.ts(hrk[:], pv[:, 6, :], 0.5, ALU.mult)

        def shifted(name, ci, dst, np_=128, fn=None):
            pa = proj(k, l, name, alt=(ci % 2 == 1))
            for tb in range(4):
                P.cp(xp[0:np_, 1 + tb * 512:1 + (tb + 1) * 512], pa[tb][0:np_, :], eng=("act" if tb % 2 else "dve"))
            t_ = sh32[0:np_, :]
            P.ts(t_, xp[0:np_, 1:L + 1], mu[0:np_, ci, 2:3], ALU.mult)
            P.stt(t_, xp[0:np_, 0:L], mu[0:np_, ci, 0:1], t_, ALU.mult, ALU.add)
            if fn is None:
                P.stt(dst[:], xp[0:np_, 2:L + 2], mu[0:np_, ci, 1:2], t_, ALU.mult, ALU.add)
            else:
                P.stt(t_, xp[0:np_, 2:L + 2], mu[0:np_, ci, 1:2], t_, ALU.mult, ALU.add)
                P.act(dst[:], t_, fn)

        shifted("rwd", 6, twd, 64, AF.Tanh)
        shifted("rad", 7, adx, 64)
        R_ = P.sb("r_R", [128, L], BF16, blk=512)
        KX = P.sb("r_KX", [128, L], BF16, blk=512)
        V_ = P.sb("r_V", [128, L], BF16, blk=512)
        KK = P.sb("r_KK", [128, L], BF16, blk=512)
        Vt = P.sb("r_Vt", [128, NCK, 64], BF16, blk=512)
        KS = P.sb("r_KS", [128, L], F32, blk=512)
        Y = xp[:, 1:L + 1]
        sq = P.sb("r_sq", [128, 512], BF16)
        bonesb = P.sb("r_bonesb", [128, 128], BF16)
        P.cp(bonesb[:], bones[:])
        rn = P.sb("r_rn", [128, 512], F32)
        CH = [rwkv_tiles(k, e) for e in range(2)]

        class _V:
            def __init__(s_, t):
                s_.t = t

            def __getitem__(s_, key):
                return s_.t[:].rearrange("p a b -> p (a b)")[key]
        sq2, rn2 = _V(CH[0]["kap"]), _V(CH[0]["a"])
        for p in range(2):
            shifted("rr%d" % p, 0 + p, R_)
            shifted("rk%d" % p, 2 + p, KX)
            shifted("rv%d" % p, 4 + p, V_)
            P.ts(sh32[:], KX[:], pv[:, 4, p:p + 1], ALU.mult)

            def fin_k(tb, r):
                P.tt(KK[:, tb * 512:(tb + 1) * 512], sh32[:, tb * 512:(tb + 1) * 512], r[:], ALU.mult)
            norm_pipe(k, 4, lambda tb: sh32[:, tb * 512:(tb + 1) * 512], (sq, sq2), (rn, rn2), (k.PB[2], k.PB[3]),
                      bonesb, -0.5, 1.0, 1e-6, fin_k)
            for grp in range(4):
                ps = k.PB[grp % 2]
                for c in range(8):
                    ck = grp * 8 + c
                    tr2(P, ps, c, V_[:, ck * 64:(ck + 1) * 64], k.identb)
                P.cp(Vt[:, grp * 8:(grp + 1) * 8, :], v3(ps), eng=("act" if grp % 2 else "dve"))
            P.memset(xp[:, 1:L + 1], 0.0, eng="pool")
            P.memset(KS[:], 0.0, eng="pool")
            T = dict(pv=pv, omka=omka, w2=w2, a2=a2, msk=msk, id2=id2, m0=m0, twd=twd, adx=adx,
                     R=R_, KX=KX, KK=KK, Vt=Vt, KS=KS, Y=Y)
            run_interleaved([rwkv_chain(k, p, e, T, CH[e]) for e in range(2)], skew=RSKEW)
            for tb in range(4):
                sl = slice(tb * 512, (tb + 1) * 512)
                P.mm(k.PB[tb % 2][:, :], bones[:], Y[:, sl])
                P.stt(Y[:, sl], k.PB[tb % 2][:, :], -1.0 / 64, Y[:, sl], ALU.mult, ALU.add)

            def fin_y(tb, r):
                sl = slice(tb * 512, (tb + 1) * 512)
                P.tt(Y[:, sl], Y[:, sl], r[:], ALU.mult)
                P.ts(Y[:, sl], Y[:, sl], ln[:, 0, p:p + 1], ALU.mult, ln[:, 1, p:p + 1], ALU.add)
            norm_pipe(k, 4, lambda tb: Y[:, tb * 512:(tb + 1) * 512], (sq, sq2), (rn, rn2), (k.PB[2], k.PB[3]),
                      bonesb, -0.5, 1.0 / 64, RW_EPS, fin_y)
            for tb in range(4):
                sl = slice(tb * 512, (tb + 1) * 512)
                s_ = (sq, sq2)[tb % 2]
                r_ = (rn, rn2)[tb % 2]
                P.tt(s_[:], R_[:, sl], KS[:, sl], ALU.mult)
                P.ts(s_[:], s_[:], hrk[:, p:p + 1], ALU.mult, eng="pool")
                P.mm(k.PB[tb % 2][:, :], bonesb[:], s_[:])
                P.tt(r_[:], k.PB[tb % 2][:, :], V_[:, sl], ALU.mult)
                P.tt(mixc(k, 6 + p)[:, sl], Y[:, sl], r_[:], ALU.add)


RW_F32 = ("lw", "a", "km", "b", "cl", "e1", "e2", "dend", "AcT", "Tg")
RW_BF16 = ("kap", "rt", "kt_", "bt_", "ke", "be", "kapT", "keT", "nbeT", "N", "Akv", "Brk", "nBrb", "Rm",
           "nA", "nB", "nC", "nD", "X0", "P0", "WkT", "Wk", "Tgb", "Pg")


def rwkv_tiles(k, e):
    P = k.P
    G = {n: P.sb("r%d_%s" % (e, n), [128, GS, 64], F32) for n in RW_F32}
    for n in RW_BF16:
        G[n] = P.sb("r%d_%s" % (e, n), [128, GS, 64], BF16)
    G["gC"] = P.sb("r%d_gC" % e, [128, GS], F32)
    G["Tcar"] = P.sb("r%d_Tcar" % e, [128, 64], F32)
    return G


def rwkv_chain(k, p, e, T, G):
    P = k.P
    pv, omka, w2, a2, msk, id2, m0, twd, adx, R_, KX, KK, Vt, KS, Y = (T[n] for n in (
        "pv", "omka", "w2", "a2", "msk", "id2", "m0", "twd", "adx", "R", "KX", "KK", "Vt", "KS", "Y"))
    B = k.PA if e == 0 else k.PB
    W = GS * 64
    e_ = 63 if e == 0 else 0
    idb = bcm(id2[:, :], GS)
    gC, Tcar = G["gC"], G["Tcar"]

    def f2(t):
        return t[:].rearrange("p a b -> p (a b)")

    def w3(ps):
        return v3(ps, GS)
    P.memset(Tcar[:], 0.0)
    gorder = range(NG) if e == 0 else range(NG - 1, -1, -1)
    pc = slice(p * 128, (p + 1) * 128)
    for grp in gorder:
        sl = slice(grp * W, (grp + 1) * W)
        c0 = grp * GS
        P.mm(B[0][:, 0:W], w2[:, e, pc], twd[:, sl])
        P.mm(B[1][:, 0:W], a2[:, e, pc], adx[:, sl])
        P.act(f2(G["lw"]), B[0][:, 0:W], AF.Sigmoid, bias=pv[:, 0 + e, p:p + 1])
        P.act(f2(G["a"]), B[1][:, 0:W], AF.Sigmoid, bias=pv[:, 2 + e, p:p + 1])
        P.ts(f2(G["km"]), f2(G["a"]), pv[:, 5, p:p + 1], ALU.mult, omka[:, p:p + 1], ALU.add)
        P.tt(f2(G["km"]), f2(G["km"]), KX[:, sl], ALU.mult)
        P.tt(f2(G["b"]), f2(G["a"]), KK[:, sl], ALU.mult)
        P.tt(KS[:, sl], KS[:, sl], f2(G["km"]), ALU.add, eng="pool")
        P.scan(f2(G["cl"]), m0[:], f2(G["lw"]), 0.0, ALU.mult, ALU.add)
        if e == 1:
            P.tt(G["e1"][:], bc3(G["cl"][:, :, 63], 64), G["cl"][:], ALU.subtract)
            P.tt(G["cl"][:], G["e1"][:], G["lw"][:], ALU.add)
        yield
        P.act(G["e1"][:], G["cl"][:], AF.Exp, scale=-DEC)
        P.act(G["e2"][:], G["cl"][:], AF.Exp, scale=DEC)
        P.tt(f2(G["rt"]), f2(G["e1"]), R_[:, sl], ALU.mult)
        P.tt(G["kt_"][:], G["e2"][:], G["km"][:], ALU.mult)
        P.tt(G["bt_"][:], G["e2"][:], G["b"][:], ALU.mult)
        P.tt(G["dend"][:], G["cl"][:], G["lw"][:], ALU.subtract)
        P.act(G["dend"][:], G["dend"][:], AF.Exp, scale=-DEC)
        P.tt(f2(G["kap"]), f2(G["dend"]), KK[:, sl], ALU.mult)
        P.cp(gC[:], G["e1"][:, :, e_], eng="pool")
        P.tt(G["dend"][:], bc3(G["cl"][:, :, e_], 64), G["cl"][:], ALU.subtract)
        P.act(G["dend"][:], G["dend"][:], AF.Exp, scale=-DEC)
        P.tt(G["ke"][:], G["dend"][:], G["km"][:], ALU.mult)
        P.tt(G["be"][:], G["dend"][:], G["b"][:], ALU.mult, eng="pool")
        yield
        for src, dst, sc in ((G["kap"], G["kapT"], 1.0), (G["ke"], G["keT"], 1.0), (G["be"], G["nbeT"], -1.0)):
            ps = B[0] if sc == 1.0 and src is G["kap"] else (B[1] if sc == 1.0 else B[2])
            for c in range(GS):
                tr2(P, ps, c, src[:, c, :], k.identb)
            if sc == 1.0:
                P.cp(dst[:], w3(ps), eng="act")
            else:
                P.ts(dst[:], w3(ps), -1.0, ALU.mult)
        yield
        for c in range(GS):
            mm2(P, B[0], c, G["bt_"][:, c, :], G["kap"][:, c, :])
            mm2(P, B[1], c, G["kt_"][:, c, :], G["kap"][:, c, :])
            mm2(P, B[2], c, G["kt_"][:, c, :], G["rt"][:, c, :])
            mm2(P, B[3], c, G["bt_"][:, c, :], G["rt"][:, c, :])
        ms = bcm(msk[:, e, 0, :], GS)
        mi = bcm(msk[:, e, 1, :], GS)
        nms = bcm(msk[:, e, 2, :], GS)
        nmi = bcm(msk[:, e, 3, :], GS)
        P.tt(G["N"][:], w3(B[0]), nms, ALU.mult)
        P.tt(G["Akv"][:], w3(B[1]), ms, ALU.mult)
        P.tt(G["Brk"][:], w3(B[2]), mi, ALU.mult)
        P.tt(G["nBrb"][:], w3(B[3]), nmi, ALU.mult)
        yield
        for _ in neumann2(k, G["N"], G["Rm"], (G["nA"], G["nB"], G["nC"], G["nD"]), (B[0], B[1], B[2]), id2, GS):
            yield
        for c in range(GS):
            mm2(P, B[0], c, G["Akv"][:, c, :], Vt[:, c0 + c, :])
        P.cp(G["X0"][:], w3(B[0]), eng="act")
        yield
        for c in range(GS):
            mm2(P, B[0], c, G["Rm"][:, c, :], G["X0"][:, c, :])
            mm2(P, B[1], c, G["kapT"][:, c, :], G["Rm"][:, c, :])
            mm2(P, B[2], c, G["Rm"][:, c, :], G["kapT"][:, c, :])
        P.cp(G["P0"][:], w3(B[0]), eng="act")
        P.cp(G["WkT"][:], w3(B[1]), eng="dve")
        P.cp(G["Wk"][:], w3(B[2]), eng="act")
        yield
        for c in range(GS):
            mm2(P, B[3], c, G["Wk"][:, c, :], G["nbeT"][:, c, :])
        P.tt(G["AcT"][:], idb, bc3(gC[:], 64), ALU.mult, eng="pool")
        P.tt(G["AcT"][:], G["AcT"][:], w3(B[3]), ALU.add)
        yield
        corder = list(range(GS)) if e == 0 else list(range(GS - 1, -1, -1))
        Tg = G["Tg"]
        for n, c in enumerate(corder):
            if n == 0:
                P.cp(Tg[:, c, :], Tcar[:], eng="pool")
            ps = B[2 + n % 2]
            mm2(P, ps, 0, G["AcT"][:, c, :], Tg[:, c, :], start=True, stop=False)
            mm2(P, ps, 0, G["keT"][:, c, :], Vt[:, c0 + c, :], start=False, stop=False)
            mm2(P, ps, 0, G["nbeT"][:, c, :], G["P0"][:, c, :], start=False, stop=True)
            dst = Tcar[:] if n == GS - 1 else Tg[:, corder[n + 1], :]
            P.cp(dst, ps[:, 0:64], eng="act")
            yield
        P.cp(G["Tgb"][:], Tg[:], eng="act")
        for c in range(GS):
            mm2(P, B[0], c, G["WkT"][:, c, :], G["Tgb"][:, c, :])
        P.tt(G["Pg"][:], w3(B[0]), G["P0"][:], ALU.add)
        yield
        for c in range(GS):
            mm2(P, B[1], c, Vt[:, c0 + c, :], G["Brk"][:, c, :], start=True, stop=False)
            mm2(P, B[1], c, G["Tgb"][:, c, :], G["rt"][:, c, :], start=False, stop=False)
            mm2(P, B[1], c, G["Pg"][:, c, :], G["nBrb"][:, c, :], start=False, stop=True)
        P.tt(Y[:, sl], Y[:, sl], B[1][:, 0:W], ALU.add)
        yield


_CACHE = {}


def kernel(**inputs):
    inp = {k_: np.asarray(v) for k_, v in inputs.items()}
    if "k" not in _CACHE:
        _CACHE["k"] = build()
    k = _CACHE["k"]
    B = inp["x"].shape[0]
    base = host_inputs(inp, 0)
    in_maps = []
    for b in range(B):
        m = dict(base)
        m["x"] = np.ascontiguousarray(inp["x"][b], dtype=np.float32)
        m["pos"] = np.ascontiguousarray(inp["positions"][b].reshape(1, L).astype(np.int32))
        in_maps.append(m)
    res = run_bass_kernel_spmd(k.nc, in_maps, core_ids=list(range(B)))
    return np.stack([np.asarray(r["out"], dtype=np.float32) for r in res.results], axis=0)
```

```python
import numpy as np
import concourse.bass as bass
import concourse.mybir as mybir
from concourse.bass_utils import run_bass_kernel_spmd
from contextlib import ExitStack

F32 = mybir.dt.float32
BF16 = mybir.dt.bfloat16
I32 = mybir.dt.int32
AF = mybir.ActivationFunctionType
ALU = mybir.AluOpType
AX = mybir.AxisListType
DTSIZE = {F32: 4, BF16: 2, I32: 4}


class _Op:
    __slots__ = ("eng", "emit", "deps", "idx", "needed", "sigval", "dsem", "dval", "isdma")

    def __init__(self, eng, emit, isdma=False):
        self.eng = eng
        self.emit = emit
        self.deps = []
        self.idx = -1
        self.needed = False
        self.sigval = 0
        self.dsem = None
        self.dval = 0
        self.isdma = isdma


class _Blk:
    __slots__ = ("w", "r")

    def __init__(self):
        self.w = None
        self.r = {}


class Prog:
    ENGS = ("pe", "dve", "act", "pool", "sp")
    NDMA = 48
    NHW = 32

    def __init__(self, nc, stack):
        self.nc = nc
        self.stack = stack
        self.ops = {e: [] for e in self.ENGS}
        self.track = {}
        self.seen = {e: {} for e in self.ENGS}
        self.seen_dma = {e: set() for e in self.ENGS}
        self.dma_last = [None] * self.NDMA
        self.dma_uses = [0] * self.NDMA
        self.dma_rr = 0
        self.dma_rr_sw = 0
        self.ndma_ops = 0
        self.untracked = set()
        self.out_dmas = []
        self.dma_pending = []
        self.last_compute = {}

    def sb(self, name, shape, dtype=F32, blk=None):
        self.uid = getattr(self, "uid", 0) + 1
        name = "s%d_%s" % (self.uid, name)
        t = self.stack.enter_context(self.nc.sbuf_tensor(name, list(shape), dtype))
        self._register(name, shape, dtype, blk)
        return t

    def ps(self, name, shape=(128, 512), dtype=F32, blk=None):
        self.uid = getattr(self, "uid", 0) + 1
        name = "p%d_%s" % (self.uid, name)
        t = self.stack.enter_context(self.nc.psum_tensor(name, list(shape), dtype))
        self._register(name, shape, dtype, blk)
        return t

    def _register(self, name, shape, dtype, blk):
        row = int(np.prod(shape[1:])) * DTSIZE[dtype]
        bb = row if blk is None else blk * DTSIZE[dtype]
        nb = (row + bb - 1) // bb
        self.track[name] = (bb, row, [_Blk() for _ in range(nb)])

    def dram_track(self, name, total_bytes, blk_bytes):
        nb = (total_bytes + blk_bytes - 1) // blk_bytes
        self.track[name] = (blk_bytes, -1, [_Blk() for _ in range(nb)])

    def _blocks(self, ap):
        name = ap.tensor.name
        if name not in self.track:
            return ()
        bb, row, blks = self.track[name]
        if len(blks) == 1:
            return blks
        ds = DTSIZE[ap.dtype]
        pat = ap.ap
        if row < 0:
            lo = hi = ap.offset
            for step, cnt in pat:
                ext = step * (cnt - 1)
                if ext < 0:
                    lo += ext
                else:
                    hi += ext
            return blks[(lo * ds) // bb:(hi * ds) // bb + 1]
        rowel = row // ds
        foff = ap.offset % rowel
        lo = hi = foff
        for step, cnt in pat[1:]:
            ext = step * (cnt - 1)
            if ext < 0:
                lo += ext
            else:
                hi += ext
        b0 = (lo * ds) // bb
        b1 = (hi * ds) // bb
        return blks[b0:b1 + 1]

    def _dep(self, x, y):
        if y is None or y is x:
            return
        e = x.eng
        if y.isdma:
            if id(y) in self.seen_dma[e]:
                return
            self.seen_dma[e].add(id(y))
            x.deps.append(y)
            return
        if y.eng == "pe" and e == "pe":
            return
        if y.idx <= self.seen[e].get(y.eng, -1):
            return
        self.seen[e][y.eng] = y.idx
        y.needed = True
        x.deps.append(y)

    def add(self, eng, emit, reads=(), writes=(), isdma=False):
        x = _Op(eng, emit, isdma)
        x.idx = len(self.ops[eng])
        rb = []
        for ap in reads:
            if ap is None or isinstance(ap, (int, float)):
                continue
            rb.extend(self._blocks(ap))
        wb = []
        for ap in writes:
            wb.extend(self._blocks(ap))
        for ap in reads:
            if ap is None or isinstance(ap, (int, float)) or not ap.tensor.name.startswith("p"):
                continue
            for b in self._blocks(ap):
                for key, y in b.r.items():
                    if key != eng:
                        self._dep(x, y)
        for b in rb:
            self._dep(x, b.w)
        for b in wb:
            self._dep(x, b.w)
            for y in b.r.values():
                self._dep(x, y)
        if isdma:
            if eng == "pool":
                s = self.NHW + self.dma_rr_sw
                self.dma_rr_sw = (self.dma_rr_sw + 1) % (self.NDMA - self.NHW)
            else:
                s = self.dma_rr
                self.dma_rr = (self.dma_rr + 1) % self.NHW
            self._dep(x, self.dma_last[s])
            self.dma_last[s] = x
            self.dma_uses[s] += 1
            x.dsem = s
            x.dval = 16 * self.dma_uses[s]
            self.ndma_ops += 1
        key = id(x) if isdma else eng
        for b in rb:
            b.r[key] = x
        for b in wb:
            b.w = x
            b.r = {}
        self.ops[eng].append(x)
        if isdma:
            self.dma_pending.append(x)
        else:
            self.last_compute[eng] = x
        return x

    def barrier(self):
        lasts = dict(self.last_compute)
        pend = list(self.dma_pending)
        self.dma_pending = []
        for e in self.ENGS:
            b = _Op(e, None)
            b.idx = len(self.ops[e])
            for e2, y in lasts.items():
                if e2 == e and e == "pe":
                    continue
                self._dep(b, y)
            for y in pend:
                self._dep(b, y)
            self.ops[e].append(b)

    def mm(self, out, lhsT, rhs, start=True, stop=True):
        return self.add("pe", lambda e: e.matmul(out, lhsT, rhs, start=start, stop=stop),
                        reads=(lhsT, rhs), writes=(out,))

    def tr(self, out, in_, ident):
        return self.add("pe", lambda e: e.transpose(out, in_, ident), reads=(in_, ident), writes=(out,))

    def tt(self, out, in0, in1, op, eng="dve"):
        return self.add(eng, lambda e: e.tensor_tensor(out, in0, in1, op), reads=(in0, in1), writes=(out,))

    def ts(self, out, in0, s1, op0, s2=None, op1=None, eng="dve", accum_out=None):
        kw = {}
        if eng == "pool" and op1 is None:
            if op0 == ALU.mult:
                s2, op1 = 0.0, ALU.add
            elif op0 == ALU.add:
                s2, op1 = 1.0, ALU.mult
        if op1 is not None:
            kw["op1"] = op1
        if accum_out is not None:
            kw["accum_out"] = accum_out
        w = (out,) if accum_out is None else (out, accum_out)
        return self.add(eng, lambda e: e.tensor_scalar(out, in0, s1, s2, op0, **kw),
                        reads=(in0, s1, s2), writes=w)

    def stt(self, out, in0, scalar, in1, op0, op1, accum_out=None):
        kw = {}
        if accum_out is not None:
            kw["accum_out"] = accum_out
        w = (out,) if accum_out is None else (out, accum_out)
        return self.add("dve", lambda e: e.scalar_tensor_tensor(out, in0, scalar, in1, op0, op1, **kw),
                        reads=(in0, scalar, in1), writes=w)

    def cp(self, out, in_, eng="dve"):
        if eng == "act":
            return self.add("act", lambda e: e.copy(out, in_), reads=(in_,), writes=(out,))
        return self.add(eng, lambda e: e.tensor_copy(out, in_), reads=(in_,), writes=(out,))

    def act(self, out, in_, func, bias=0.0, scale=1.0, accum_out=None):
        kw = {}
        if accum_out is not None:
            kw["accum_out"] = accum_out
        w = (out,) if accum_out is None else (out, accum_out)
        return self.add("act", lambda e: e.activation(out, in_, func, bias=bias, scale=scale, **kw),
                        reads=(in_, bias, scale), writes=w)

    def red(self, out, in_, op, axis=AX.X, eng="dve"):
        return self.add(eng, lambda e: e.tensor_reduce(out, in_, axis, op), reads=(in_,), writes=(out,))

    def recip(self, out, in_):
        return self.add("dve", lambda e: e.reciprocal(out, in_), reads=(in_,), writes=(out,))

    def rpow(self, out, in_, power, scale=1.0, bias=0.0):
        self.act(out, in_, AF.Ln, bias=bias, scale=scale)
        return self.act(out, out, AF.Exp, scale=power)

    def memset(self, ap, val, eng="dve"):
        return self.add(eng, lambda e: e.memset(ap, val), writes=(ap,))

    def scan(self, out, d0, d1, init, op0, op1):
        return self.add("dve", lambda e: e.tensor_tensor_scan(out, d0, d1, init, op0, op1),
                        reads=(d0, d1, init), writes=(out,))

    def dma(self, out, in_, eng="sp", is_output=False):
        x = self.add(eng, lambda e: e.dma_start(out=out, in_=in_), reads=(in_,), writes=(out,), isdma=True)
        if is_output:
            self.out_dmas.append(x)
        return x

    def finish(self):
        nc = self.nc
        fin = _Op("sp", None)
        fin.idx = len(self.ops["sp"])
        for y in self.out_dmas:
            self._dep(fin, y)
        self.ops["sp"].append(fin)
        sems = {}
        for e in ("pe", "dve", "act", "pool"):
            sems[e] = self.stack.enter_context(nc.semaphore("s_" + e))
        dsems = [self.stack.enter_context(nc.semaphore("d%d" % i)) for i in range(self.NDMA)]
        for e in ("pe", "dve", "act", "pool"):
            c = 0
            for x in self.ops[e]:
                if x.isdma:
                    continue
                if x.needed:
                    c += 1
                    x.sigval = c
            self.stats_sig = getattr(self, "stats_sig", {})
            self.stats_sig[e] = c
        ops = self.ops

        def replay(e, engobj):
            for x in ops[e]:
                for y in x.deps:
                    if y.isdma:
                        engobj.wait_ge(dsems[y.dsem], y.dval)
                    else:
                        engobj.wait_ge(sems[y.eng], y.sigval)
                if x.emit is None:
                    continue
                ins = x.emit(engobj)
                if x.isdma:
                    ins.then_inc(dsems[x.dsem], 16)
                elif x.needed:
                    ins.then_inc(sems[e], 1)

        with nc.Block() as block:
            @block.tensor
            def _(eng):
                replay("pe", eng)

            @block.vector
            def _(eng):
                replay("dve", eng)

            @block.scalar
            def _(eng):
                replay("act", eng)

            @block.gpsimd
            def _(eng):
                replay("pool", eng)

            @block.sync
            def _(eng):
                replay("sp", eng)


L = 2048
D = 1024
NT = L // 128
DEPTH = 2
N_IN = 3448
EPS = 1e-6

OFF = dict(gate=0, gdn_q=1024, gdn_k=1408, gdn_v=1792, gdn_a=2176, gdn_b=2188, mla_cq=2200, mla_ckv=2392,
           mla_kr=2520, rw_r=2552, rw_k=2808, rw_v=3064, rw_wd=3320, rw_ad=3384)


def chunk_table():
    ch = []
    for h in range(3):
        ch.append(("gq%d" % h, [(0, OFF["gdn_q"] + h * 128, 128)]))
        ch.append(("gk%d" % h, [(0, OFF["gdn_k"] + h * 128, 128)]))
        ch.append(("gv%d" % h, [(0, OFF["gdn_v"] + h * 128, 128)]))
    ch.append(("gab", [(0, OFF["gdn_a"], 6), (32, OFF["gdn_a"] + 6, 6), (64, OFF["gdn_b"], 6), (96, OFF["gdn_b"] + 6, 6)]))
    ch.append(("cq0", [(0, OFF["mla_cq"], 128)]))
    ch.append(("cq1", [(0, OFF["mla_cq"] + 128, 64)]))
    ch.append(("ckv", [(0, OFF["mla_ckv"], 128)]))
    ch.append(("kr", [(0, OFF["mla_kr"], 32)]))
    for i in range(2):
        ch.append(("rr%d" % i, [(0, OFF["rw_r"] + i * 128, 128)]))
        ch.append(("rk%d" % i, [(0, OFF["rw_k"] + i * 128, 128)]))
        ch.append(("rv%d" % i, [(0, OFF["rw_v"] + i * 128, 128)]))
    ch.append(("rwd", [(0, OFF["rw_wd"], 64)]))
    ch.append(("rad", [(0, OFF["rw_ad"], 64)]))
    for i in range(8):
        ch.append(("g%d" % i, [(0, OFF["gate"] + i * 128, 128)]))
    return ch


CHUNKS = chunk_table()
CH_IDX = {n: i for i, (n, _) in enumerate(CHUNKS)}
NCH = len(CHUNKS)


def host_win(w_in):
    out = np.zeros((DEPTH, NCH, 128, 8, 128), np.float32)
    for ci, (_, parts) in enumerate(CHUNKS):
        for dst, src, w in parts:
            blk = w_in[:, :, src:src + w].reshape(DEPTH, 8, 128, w)
            out[:, ci, :, :, dst:dst + w] = blk.transpose(0, 2, 1, 3)
    return out


class K:
    pass


def build(depth=DEPTH, mixers=("gdn", "mla", "rwkv"), dbg=False):
    nc = bass.Bass("TRN2", target_bir_lowering=False)
    k = K()
    k.nc = nc
    k.dbg = dbg
    k.dbg_outs = []
    k.cut = 99

    def din(name, shape, dt=F32):
        return nc.dram_tensor(name, list(shape), dt, kind="ExternalInput").ap()

    k.x_d = din("x", [L, D])
    k.win_d = din("win", [DEPTH, NCH, 128, 8, 128])
    k.normg_d = din("normg", [DEPTH, 128, 8])
    k.wout_d = din("wout", [DEPTH, 128, 8, 1024])
    k.fing_d = din("fing", [1, D])
    k.ident_d = din("ident", [128, 128])
    k.out_d = nc.dram_tensor("out", [L, D], F32, kind="ExternalOutput").ap()
    mla_decl(k)
    gdn_decl(k)
    rwkv_decl(k)

    with ExitStack() as st:
        P = Prog(nc, st)
        k.P = P
        k.xscr = nc.dram_tensor("xscr", [L, D], F32, kind="Internal").ap()
        P.dram_track("xscr", L * D * 4, 128 * D * 4)
        k.hT = P.sb("hT", [128, 8, L], BF16, blk=512)
        k.ident = P.sb("ident", [128, 128], F32)
        k.identb = P.sb("identb", [128, 128], BF16)
        k.normg = P.sb("normg", [128, DEPTH, 8], F32)
        k.wst = [P.sb("wst%d" % i, [128, 8, 128], F32) for i in range(2)]
        k.wbf = [P.sb("wbf%d" % i, [128, 8, 128], BF16) for i in range(2)]
        k.wrr = 0
        k.PAW = [P.ps("paw%d" % i, [128, 1024], F32, blk=512) for i in range(2)]
        k.PA = [k.PAW[i // 2][:, (i % 2) * 512:(i % 2 + 1) * 512] for i in range(4)]
        k.PB = [P.ps("pb%d" % i, [128, 512], F32) for i in range(4)]

        P.dma(k.ident[:], k.ident_d[:])
        P.cp(k.identb[:], k.ident[:])
        for l in range(DEPTH):
            P.dma(k.normg[:, l, :], k.normg_d[l])

        for l in range(depth):
            phase_a(k, l)
            with scope(k):
                k.mix_r = P.sb("mix_r", [128, 2, L], BF16, blk=512)
                if "rwkv" in mixers:
                    rwkv_phase(k, l)
                else:
                    P.memset(k.mix_r[:].rearrange("p a b -> p (a b)"), 1.0)
                with scope(k):
                    k.mix_m = P.sb("mix_m", [128, 3, L], BF16, blk=512)
                    if "mla" in mixers:
                        mla_phase(k, l)
                    else:
                        P.memset(k.mix_m[:].rearrange("p a b -> p (a b)"), 1.0)
                    with scope(k):
                        k.mix_g = P.sb("mix_g", [128, 3, L], BF16, blk=512)
                        if "gdn" in mixers:
                            gdn_phase(k, l)
                        else:
                            P.memset(k.mix_g[:].rearrange("p a b -> p (a b)"), 1.0)
                        if k.dbg:
                            for nm, t_, n_ in (("g", k.mix_g, 3), ("m", k.mix_m, 3), ("r", k.mix_r, 2)):
                                dump(k, "mix_%s%d" % (nm, l), t_[:].rearrange("p a b -> p (a b)"), [128, n_ * L])
                        phase_z(k, l, last=(l == depth - 1))
        P.finish()
        print("ops:", {e: len(v) for e, v in P.ops.items()}, "sig:", P.stats_sig, "dma:", P.ndma_ops)
    return k


def mixc(k, c):
    if c < 3:
        return k.mix_g[:, c, :]
    if c < 6:
        return k.mix_m[:, c - 3, :]
    return k.mix_r[:, c - 6, :]


def dump(k, name, ap, shape=None):
    if not k.dbg:
        return
    P = k.P
    shape = list(ap.shape) if shape is None else shape
    d = k.nc.dram_tensor("dbg_" + name, shape, ap.dtype, kind="ExternalOutput").ap()
    P.dma(d[:] if len(shape) == 2 else d, ap, is_output=True)
    k.dbg_outs.append("dbg_" + name)


def scope(k):
    class _S:
        def __enter__(s):
            s.old = k.P.stack
            s.st = ExitStack()
            s.st.__enter__()
            k.P.stack = s.st
            return s

        def __exit__(s, *a):
            k.P.barrier()
            k.P.stack = s.old
            s.st.__exit__(*a)
            return False
    return _S()


def phase_a(k, l):
    P = k.P
    with scope(k):
        ssq = P.sb("a_ssq", [128, NT])
        rs = P.sb("a_rs", [128, NT])
        rstd = P.sb("a_rstd", [128, NT])
        junk = [P.sb("a_junk%d" % i, [128, D], BF16) for i in range(2)]
        xs = [P.sb("a_xs%d" % i, [128, D], BF16) for i in range(2)]
        xin = [P.sb("a_xin%d" % i, [128, D], F32) for i in range(3)]
        src = k.x_d if l == 0 else k.xscr

        def stage1(tt):
            b = tt % 2
            xt_ = xin[tt % 3]
            P.dma(xt_[:], src[tt * 128:(tt + 1) * 128, :])
            P.act(junk[b][:], xt_[:], AF.Square, accum_out=ssq[:, tt:tt + 1])
            P.act(rs[:, tt:tt + 1], ssq[:, tt:tt + 1], AF.Sqrt, bias=EPS, scale=1.0 / D)
            P.recip(rstd[:, tt:tt + 1], rs[:, tt:tt + 1])
            P.ts(xs[b][:], xt_[:], rstd[:, tt:tt + 1], ALU.mult)

        def stage2(tt):
            b = tt % 2
            pt = k.PB[b][:].bitcast(BF16)
            for dc in range(8):
                P.tr(pt[:, dc * 128:(dc + 1) * 128], xs[b][:, dc * 128:(dc + 1) * 128], k.identb[:])
            P.cp(k.hT[:, :, tt * 128:(tt + 1) * 128], pt[:].rearrange("p (a b) -> p a b", a=8),
                 eng=("act" if tt % 2 == 0 else "dve"))
        stage1(0)
        for tt in range(NT):
            if tt + 1 < NT:
                stage1(tt + 1)
            stage2(tt)


def proj(k, l, name, alt=False):
    P = k.P
    BK = k.PB if alt else k.PA
    ci = CH_IDX[name]
    b = k.wrr
    k.wrr ^= 1
    P.dma(k.wst[b][:], k.win_d[l, ci], eng="sp")
    gb = k.normg[:, l, :].unsqueeze(2).broadcast_to([128, 8, 128])
    P.tt(k.wbf[b][:], k.wst[b][:], gb, ALU.mult, eng="pool")
    for tb in range(4):
        for dc in range(8):
            P.mm(BK[tb][:, :], k.wbf[b][:, dc, :], k.hT[:, dc, tb * 512:(tb + 1) * 512], start=(dc == 0), stop=(dc == 7))
    return BK


NXB = 5
ZQ = "act"


def phase_z(k, l, last):
    P = k.P
    with scope(k):
        wst = P.sb("z_wst", [128, 8, 512], F32)
        wob = P.sb("z_wob", [128, 8, 1024], BF16, blk=512)
        sg = [P.sb("z_sg%d" % i, [128, L], BF16, blk=512) for i in range(2)]
        for nb in range(2):
            P.dma(wst[:], k.wout_d[l, :, :, nb * 512:(nb + 1) * 512])
            P.cp(wob[:, :, nb * 512:(nb + 1) * 512], wst[:], eng="act")
        for gc in range(8):
            pa = proj(k, l, "g%d" % gc, alt=(gc % 2 == 1))
            s = sg[gc % 2]
            for tb in range(4):
                P.act(s[:, tb * 512:(tb + 1) * 512], pa[tb][:, :], AF.Silu)
                mc = mixc(k, gc)[:, tb * 512:(tb + 1) * 512]
                P.tt(mc, mc, s[:, tb * 512:(tb + 1) * 512], ALU.mult)
        xin = [P.sb("z_xin%d" % i, [128, D], F32) for i in range(NXB)]
        src = k.x_d if l == 0 else k.xscr
        if last:
            ssq = P.sb("f_ssq", [128, NT])
            rs = P.sb("f_rs", [128, NT])
            rstd = P.sb("f_rstd", [128, NT])
            junk = [P.sb("f_junk%d" % i, [128, D], BF16) for i in range(2)]
            gf = P.sb("f_g", [128, D])
            ot = [P.sb("f_o%d" % i, [128, D]) for i in range(2)]
            P.dma(gf[:], k.fing_d[0:1, :].partition_broadcast(128))
        def za(tt):
            xt_ = xin[tt % NXB]
            P.dma(xt_[:], src[tt * 128:(tt + 1) * 128, :])
            for nb in range(2):
                ps = k.PB[(tt * 2 + nb) % 4]
                for kc in range(8):
                    P.mm(ps[:, :], mixc(k, kc)[:, tt * 128:(tt + 1) * 128], wob[:, kc, nb * 512:(nb + 1) * 512],
                         start=(kc == 0), stop=(kc == 7))
                xs = xt_[:, nb * 512:(nb + 1) * 512]
                P.tt(xs, xs, ps[:, :], ALU.add)
            if not last:
                P.dma(k.xscr[tt * 128:(tt + 1) * 128, :], xt_[:], eng=ZQ)
            else:
                b = tt % 2
                P.act(junk[b][:], xt_[:], AF.Square, accum_out=ssq[:, tt:tt + 1])
                P.act(rs[:, tt:tt + 1], ssq[:, tt:tt + 1], AF.Sqrt, bias=EPS, scale=1.0 / D)

        def zb(tt):
            if last:
                xt_ = xin[tt % NXB]
                b = tt % 2
                P.recip(rstd[:, tt:tt + 1], rs[:, tt:tt + 1])
                P.stt(ot[b][:], xt_[:], rstd[:, tt:tt + 1], gf[:], ALU.mult, ALU.mult)
                P.dma(k.out_d[tt * 128:(tt + 1) * 128, :], ot[b][:], eng=ZQ, is_output=True)
        za(0)
        for tt in range(NT):
            if tt + 1 < NT:
                za(tt + 1)
            zb(tt)


def host_inputs(inp, b):
    m = {}
    m["x"] = np.ascontiguousarray(inp["x"][b])
    m["win"] = host_win(inp["w_in"])
    m["normg"] = np.ascontiguousarray(inp["norm_g"].reshape(DEPTH, 8, 128).transpose(0, 2, 1))
    m["wout"] = np.ascontiguousarray(inp["w_out"].reshape(DEPTH, 8, 128, 1024).transpose(0, 2, 1, 3))
    m["fing"] = np.ascontiguousarray(inp["final_norm_g"].reshape(1, D))
    m["ident"] = np.eye(128, dtype=np.float32)
    host_mla(inp, b, m)
    host_gdn(inp, b, m)
    host_rwkv(inp, b, m)
    return m


TWO_PI = 2.0 * np.pi


def host_mla(inp, b, m):
    half = 16
    inv_freq = (10000.0 ** (-np.arange(half, dtype=np.float32) / half)).astype(np.float32)
    invf = np.zeros((32, 1), np.float32)
    invf[:, 0] = np.tile(inv_freq, 2) / np.float32(TWO_PI)
    m["invf"] = invf
    rm = np.zeros((32, 32), np.float32)
    for i in range(16):
        rm[i, i + 16] = -1.0
        rm[i + 16, i] = 1.0
    m["rmT"] = np.ascontiguousarray(rm.T)
    m["pos"] = np.ascontiguousarray(inp["positions"][b].reshape(1, L).astype(np.int32))
    wuq = inp["mla_w_uq"]
    o = np.zeros((DEPTH, 128, 2, 6, 128), np.float32)
    for h in range(6):
        nope = wuq[:, :, h * 96:h * 96 + 64]
        rope = wuq[:, :, h * 96 + 64:h * 96 + 96]
        o[:, :, 0, h, 64:128] = nope[:, 0:128]
        o[:, 0:64, 1, h, 64:128] = nope[:, 128:192]
        o[:, :, 0, h, 0:32] = rope[:, 0:128]
        o[:, 0:64, 1, h, 0:32] = rope[:, 128:192]
    m["wuq"] = o
    gq = np.zeros((DEPTH, 128, 2), np.float32)
    gq[:, :, 0] = inp["mla_q_norm_g"][:, 0:128]
    gq[:, 0:64, 1] = inp["mla_q_norm_g"][:, 128:192]
    m["gq"] = gq
    wukv = inp["mla_w_ukv"]
    wk = np.zeros((DEPTH, 128, 6, 128), np.float32)
    wv = np.zeros((DEPTH, 128, 6, 64), np.float32)
    for h in range(6):
        wk[:, :, h, 64:128] = wukv[:, :, h * 128:h * 128 + 64]
        wv[:, :, h, :] = wukv[:, :, h * 128 + 64:h * 128 + 128]
    m["wuk"] = wk
    m["wuv"] = wv
    m["gkv"] = np.ascontiguousarray(inp["mla_kv_norm_g"].reshape(DEPTH, 128, 1))


def mla_decl(k):
    nc = k.nc

    def din(name, shape, dt=F32):
        return nc.dram_tensor(name, list(shape), dt, kind="ExternalInput").ap()
    k.invf_d = din("invf", [32, 1])
    k.rmT_d = din("rmT", [32, 32])
    k.pos_d = din("pos", [1, L], I32)
    k.wuq_d = din("wuq", [DEPTH, 128, 2, 6, 128])
    k.gq_d = din("gq", [DEPTH, 128, 2])
    k.wuk_d = din("wuk", [DEPTH, 128, 6, 128])
    k.wuv_d = din("wuv", [DEPTH, 128, 6, 64])
    k.gkv_d = din("gkv", [DEPTH, 128, 1])


def latent_norm(k, l, names, nfeat, outs, ones):
    P = k.P
    sq = [P.sb("ln_sq%d" % i, [128, 512]) for i in range(2)]
    rq = [P.sb("ln_rq%d" % i, [128, 512]) for i in range(2)]
    n = len(names)
    for i, nm in enumerate(names):
        pa = proj(k, l, nm)
        for tb in range(4):
            s = sq[tb % 2]
            sk = ""
            if "a" not in sk:
                P.act(s[:], pa[tb][:, :], AF.Square)
            if "c" not in sk:
                P.cp(outs[i][:, tb * 512:(tb + 1) * 512], pa[tb][:, :], eng="dve")
            if "m" not in sk:
                P.mm(k.PB[tb][:, :], ones[:], s[:], start=(i == 0), stop=(i == n - 1))
    c2 = 9
    if c2 < 1:
        return
    for tb in range(4):
        r = rq[tb % 2]
        P.rpow(r[:], k.PB[tb][:, :], -0.5, scale=1.0 / nfeat, bias=EPS)
        for i in range(n):
            o = outs[i][:, tb * 512:(tb + 1) * 512]
            P.tt(o, o, r[:], ALU.mult)


def mla_phase(k, l):
    P = k.P
    SC = 96.0 ** -0.5
    with scope(k):
        cqn0 = P.sb("m_cqn0", [128, L], BF16, blk=512)
        cqn1 = P.sb("m_cqn1", [128, L], BF16, blk=512)
        ckvn = P.sb("m_ckvn", [128, L], BF16, blk=512)
        krope = P.sb("m_krope", [32, L], BF16, blk=512)
        cos2 = P.sb("m_cos2", [32, L], BF16, blk=512)
        sin2 = P.sb("m_sin2", [32, L], BF16, blk=512)
        wq = P.sb("m_wq", [128, 2, 6, 128], BF16)
        wk = P.sb("m_wk", [128, 6, 128], BF16)
        wv = P.sb("m_wv", [128, 6, 64], BF16)
        ones = P.sb("m_ones", [128, 128], F32)
        onesk = P.sb("m_onesk", [128, 128], F32)
        rmT = P.sb("m_rmT", [32, 32], F32)
        P.memset(ones[:], 1.0)
        P.memset(onesk[:], 1.0)
        P.memset(onesk[32:64, :], 0.0)
        P.dma(rmT[:], k.rmT_d[:])
        with scope(k):
            st = P.sb("m_st", [128, 2, 6, 128], F32)
            g = P.sb("m_g", [128, 4], F32)
            P.dma(st[:], k.wuq_d[l])
            P.dma(g[:, 0:2], k.gq_d[l])
            P.dma(g[:, 2:3], k.gkv_d[l])
            for kc in range(2):
                P.ts(wq[:, kc].rearrange("p a b -> p (a b)"), st[:, kc].rearrange("p a b -> p (a b)"),
                     g[:, kc:kc + 1], ALU.mult)
            st2 = P.sb("m_st2", [128, 6, 128], F32)
            P.dma(st2[:], k.wuk_d[l])
            P.ts(wk[:].rearrange("p a b -> p (a b)"), st2[:].rearrange("p a b -> p (a b)"), g[:, 2:3], ALU.mult)
            st3 = P.sb("m_st3", [128, 6, 64], F32)
            P.dma(st3[:], k.wuv_d[l])
            P.ts(wv[:].rearrange("p a b -> p (a b)"), st3[:].rearrange("p a b -> p (a b)"), g[:, 2:3], ALU.mult)
        if k.cut < 1:
            return
        with scope(k):
            latent_norm(k, l, ["cq0", "cq1"], 192, [cqn0, cqn1], ones)
            latent_norm(k, l, ["ckv"], 128, [ckvn], ones)
        if k.cut < 2:
            return
        with scope(k):
            invf = P.sb("m_invf", [32, 1], F32)
            P.dma(invf[:], k.invf_d[:])
            pa = proj(k, l, "kr")

            def rope_blk(tb):
                sl = slice(tb * 512, (tb + 1) * 512)
                posi = P.sb("m_posi%d" % tb, [32, 512], I32)
                y = P.sb("m_y%d" % tb, [32, 512], F32)
                yi = P.sb("m_yi%d" % tb, [32, 512], I32)
                fr = P.sb("m_fr%d" % tb, [32, 512], F32)
                kr = P.sb("m_kr%d" % tb, [32, 512], F32)
                t1 = P.sb("m_t1%d" % tb, [32, 512], F32)
                t2 = P.sb("m_t2%d" % tb, [32, 512], F32)
                P.dma(posi[:], k.pos_d[0:1, sl].partition_broadcast(32))
                P.cp(kr[:], pa[tb][0:32, :], eng="act")
                yield
                P.cp(y[:], posi[:])
                P.mm(k.PB[tb][0:32, :], rmT[:], kr[:])
                yield
                P.ts(y[:], y[:], invf[:, 0:1], ALU.mult)
                yield
                for off, dst in ((0.0, sin2), (0.25, cos2)):
                    if off != 0.0:
                        P.ts(y[:], y[:], off, ALU.add)
                        yield
                    P.cp(yi[:], y[:])
                    yield
                    P.cp(fr[:], yi[:])
                    yield
                    P.tt(fr[:], y[:], fr[:], ALU.subtract)
                    yield
                    P.act(dst[:, sl], fr[:], AF.Sin, scale=TWO_PI * (1.0 - 1e-6))
                    yield
                P.tt(t1[:], kr[:], cos2[:, sl], ALU.mult)
                P.tt(t2[:], k.PB[tb][0:32, :], sin2[:, sl], ALU.mult)
                yield
                P.tt(krope[:, sl], t1[:], t2[:], ALU.add)
                yield
            run_interleaved([rope_blk(tb) for tb in range(4)])
        if k.cut < 3:
            return
        kT = [P.sb("m_kT%d" % i, [128, L], BF16, blk=512) for i in range(2)]
        qT = [P.sb("m_qT%d" % i, [128, L], BF16, blk=512) for i in range(2)]
        Vh = [P.sb("m_V%d" % i, [128, NT, 96], BF16) for i in range(2)]
        pT = [P.sb("m_pT%d" % i, [128, 1024], BF16, blk=512) for i in range(4)]
        sq = [P.sb("m_sq%d" % i, [128, 512], BF16) for i in range(2)]
        qr = [P.sb("m_qr%d" % i, [32, 512], BF16) for i in range(2)]
        onesb = P.sb("m_onesb", [128, 128], BF16)
        oneskb = P.sb("m_oneskb", [128, 128], BF16)
        rmTb = P.sb("m_rmTb", [32, 32], BF16)
        P.cp(onesb[:], ones[:])
        P.cp(oneskb[:], onesk[:])
        P.cp(rmTb[:], rmT[:])
        t1 = P.sb("m_t1b", [32, 512], F32)
        t2 = P.sb("m_t2b", [32, 512], F32)
        mrow = P.sb("m_mrow", [64, 512], F32)
        km4 = P.sb("m_km4", [128, 4], F32)
        kmax2 = P.sb("m_kmax2", [128, 1], F32)
        rden = [P.sb("m_rden%d" % i, [64, 512], F32) for i in range(2)]
        for i in range(2):
            P.memset(kT[i][32:64, :], 0.0)
            P.memset(kT[i][32:33, :], 1.0)
            P.memset(qT[i][32:64, :], 0.0)
            P.memset(Vh[i][:, :, 64:96], 1.0)
        kmx = [P.sb("m_kmx%d" % i, [128, 1], F32) for i in range(2)]

        def prep(h):
            kt_, qt_, vh_ = kT[h % 2], qT[h % 2], Vh[h % 2]
            kmax2_ = kmx[h % 2]
            P.cp(kt_[0:32, :], krope[:], eng="pool")
            for tb in range(4):
                sl = slice(tb * 512, (tb + 1) * 512)
                s = sq[tb % 2]
                P.mm(k.PB[2][:, :], wk[:, h, :], ckvn[:, sl])
                yield
                P.cp(kt_[64:128, sl], k.PB[2][64:128, :], eng="dve")
                yield
                P.tt(s[:], kt_[:, sl], kt_[:, sl], ALU.mult, eng="pool")
                yield
                yield
                P.mm(k.PB[3][:, :], oneskb[:], s[:])
                yield
                P.red(km4[:, tb:tb + 1], k.PB[3][:, :], ALU.max)
                yield
            P.red(kmax2_[:], km4[:], ALU.max)
            for half in range(2):
                for j in range(8):
                    tt = half * 8 + j
                    P.mm(k.PB[2][:, j * 64:(j + 1) * 64], ckvn[:, tt * 128:(tt + 1) * 128], wv[:, h, :])
                yield
                P.cp(vh_[:, half * 8:(half + 1) * 8, 0:64], k.PB[2][:, :].rearrange("p (a b) -> p a b", a=8), eng="dve")
                yield
            for tb in range(4):
                sl = slice(tb * 512, (tb + 1) * 512)
                s = sq[tb % 2]
                q_ = qr[tb % 2]
                P.mm(k.PB[2][:, :], wq[:, 0, h, :], cqn0[:, sl], start=True, stop=False)
                P.mm(k.PB[2][:, :], wq[:, 1, h, :], cqn1[:, sl], start=False, stop=True)
                yield
                P.cp(qt_[64:128, sl], k.PB[2][64:128, :], eng="dve")
                P.cp(q_[:], k.PB[2][0:32, :], eng="dve")
                yield
                P.act(s[:], k.PB[2][:, :], AF.Square)
                yield
                P.mm(k.PB[3][:, :], onesb[:], s[:])
                yield
                P.act(mrow[32:33, :], k.PB[3][32:33, :], AF.Sqrt, scale=kmax2_[32:33, 0:1])
                yield
                P.ts(qt_[32:33, sl], mrow[32:33, :], -1.0, ALU.mult)
                P.mm(k.PB[2][0:32, :], rmTb[:], q_[:])
                P.tt(t1[:], q_[:], cos2[:, sl], ALU.mult, eng="pool")
                yield
                P.tt(t2[:], k.PB[2][0:32, :], sin2[:, sl], ALU.mult)
                yield
                P.tt(qt_[0:32, sl], t1[:], t2[:], ALU.add)
                yield

        def attn(h):
            kt_, qt_, vh_ = kT[h % 2], qT[h % 2], Vh[h % 2]
            pti = 0
            for qb in range(4):
                qs = slice(qb * 512, (qb + 1) * 512)
                O = k.PB[qb % 2]

                def s_pair(m_):
                    for j in range(2):
                        kt = 2 * m_ + j
                        P.mm(k.PAW[m_ % 2][:, j * 512:(j + 1) * 512], kt_[:, kt * 128:(kt + 1) * 128], qt_[:, qs])
                s_pair(0)
                s_pair(1)
                for m_ in range(NT // 2):
                    p_ = pT[pti % 4]
                    pti += 1
                    P.act(p_[:], k.PAW[m_ % 2][:, :], AF.Exp, scale=SC)
                    if m_ + 2 < NT // 2:
                        s_pair(m_ + 2)
                    for j in range(2):
                        kt = 2 * m_ + j
                        P.mm(O[0:96, :], vh_[:, kt, :], p_[:, j * 512:(j + 1) * 512], start=(kt == 0), stop=(kt == NT - 1))
                    yield
                rd = rden[qb % 2]
                P.rpow(rd[0:32, :], O[64:96, :], -1.0)
                P.rpow(rd[32:64, :], O[64:96, :], -1.0)
                ob = (h % 2) * 64
                P.tt(mixc(k, 3 + h // 2)[ob:ob + 64, qs], O[0:64, :], rd[:], ALU.mult)
                yield

        for _ in prep(0):
            pass
        mode = "il"
        for h in range(6):
            gens = [attn(h)]
            if h + 1 < 6:
                if mode == "il":
                    gens.append(prep(h + 1))
                elif mode == "seq":
                    run_interleaved(gens)
                    gens = [prep(h + 1)]
                elif mode == "noprep":
                    pass
            if mode == "noprep" and h > 0:
                gens = [attn(0)]
            run_interleaved(gens)


NCK = L // 64
NEG = -30000.0


def host_gdn(inp, b, m):
    cw = inp["gdn_conv"]
    o = np.zeros((DEPTH, 128, 9, 5), np.float32)
    for part in range(3):
        for p in range(3):
            o[:, :, part * 3 + p, :] = cw[:, :, part * 384 + p * 128: part * 384 + (p + 1) * 128].transpose(0, 2, 1)
    m["gconv"] = o
    gb = np.zeros((DEPTH, 128, 2), np.float32)
    for d in range(2):
        gb[:, d * 32:d * 32 + 6, 0] = inp["gdn_dt_bias"][:, d, :]
        gb[:, d * 32:d * 32 + 6, 1] = inp["gdn_a_log"][:, d, :]
    m["ggb"] = gb
    m["gng"] = np.ascontiguousarray(np.tile(inp["gdn_norm_g"], (1, 2)).reshape(DEPTH, 128, 1))
    sel = np.zeros((64, 6, 128), np.float32)
    for d in range(2):
        for p in range(3):
            sel[d * 32 + 2 * p, d * 3 + p, 0:64] = 1.0
            sel[d * 32 + 2 * p + 1, d * 3 + p, 64:128] = 1.0
    m["gsel"] = sel
    j = np.arange(64)[:, None]
    i = np.arange(64)[None, :]
    nm = np.zeros((128, 2, 64), np.float32)
    nm[:, 0, :] = np.tile(np.where(i > j, 0.0, NEG), (2, 1))
    nm[:, 1, :] = np.tile(np.where(i < j, 0.0, NEG), (2, 1))
    m["gnegm"] = nm
    m["gid2"] = np.ascontiguousarray(np.tile(np.eye(64, dtype=np.float32), (2, 1)))


def gdn_decl(k):
    nc = k.nc

    def din(name, shape, dt=F32):
        return nc.dram_tensor(name, list(shape), dt, kind="ExternalInput").ap()
    k.gconv_d = din("gconv", [DEPTH, 128, 9, 5])
    k.ggb_d = din("ggb", [DEPTH, 128, 2])
    k.gng_d = din("gng", [DEPTH, 128, 1])
    k.gsel_d = din("gsel", [64, 6, 128])
    k.gnegm_d = din("gnegm", [128, 2, 64])
    k.gid2_d = din("gid2", [128, 64])


def bc3(ap2, n):
    return ap2.unsqueeze(2).broadcast_to([ap2.shape[0], ap2.shape[1], n])


def bcm(ap2, n):
    return ap2.unsqueeze(1).broadcast_to([ap2.shape[0], n, ap2.shape[1]])


HS = (slice(0, 64), slice(64, 128))


def v3(ps, n=8):
    return ps[:, 0:n * 64].rearrange("p (a b) -> p a b", a=n)


def mm2(P, ps, c, lhsT, rhs, **kw):
    for hs in HS:
        P.mm(ps[hs, c * 64:(c + 1) * 64], lhsT[hs], rhs[hs], **kw)


def tr2(P, ps, c, in_, ident):
    for hs in HS:
        P.mm(ps[hs, c * 64:(c + 1) * 64], in_[hs], ident[hs, hs])


def neumann2(k, Nn, Rm, tmp, bank, id2, n8=8):
    P = k.P
    idb = bcm(id2[:, :], n8)
    tA, tB, tC, tD = tmp
    pa_, pb_, pc_ = bank
    for c in range(n8):
        tr2(P, pa_, c, Nn[:, c, :], k.identb)
    P.cp(tA[:], v3(pa_, n8), eng="act")
    P.tt(Rm[:], Nn[:], idb, ALU.add)
    yield
    cur, curT = Nn, tA
    targets = [(tB, tC), (tD, tA)]
    for lvl in range(1, 7):
        nxt, nxtT = targets[(lvl - 1) % 2]
        if lvl >= 2:
            for c in range(n8):
                mm2(P, pc_, c, curT[:, c, :], Rm[:, c, :])
            P.tt(Rm[:], Rm[:], v3(pc_, n8), ALU.add)
        if lvl <= 5:
            for c in range(n8):
                mm2(P, pb_, c, cur[:, c, :], curT[:, c, :])
            P.cp(nxtT[:], v3(pb_, n8), eng="act")
            if lvl < 5:
                for c in range(n8):
                    mm2(P, pa_, c, curT[:, c, :], cur[:, c, :])
                P.cp(nxt[:], v3(pa_, n8), eng=("act" if lvl % 2 else "dve"))
        yield
        cur, curT = nxt, nxtT


def run_interleaved(gens, skew=0):
    gens = list(gens)
    for _ in range(skew):
        try:
            next(gens[0])
        except StopIteration:
            gens.pop(0)
            break
    while gens:
        for g in list(gens):
            try:
                next(g)
            except StopIteration:
                gens.remove(g)


def norm_pipe(k, n, src_fn, sqb, rnb, banks, bones, power, scale, bias, post_fn):
    P = k.P

    def pre(i):
        P.act(sqb[i % 2][:], src_fn(i), AF.Square)
        P.mm(banks[i % 2][:, :], bones[:], sqb[i % 2][:])

    def post(i):
        P.rpow(rnb[i % 2][:], banks[i % 2][:, :], power, scale=scale, bias=bias)
        post_fn(i, rnb[i % 2])
    pre(0)
    for i in range(n):
        if i + 1 < n:
            pre(i + 1)
        post(i)


def run_pipelined(chains, depth=2):
    active = []
    nxt = [0] * len(chains)

    def start(ci):
        if nxt[ci] < len(chains[ci]):
            active.append((ci, chains[ci][nxt[ci]](nxt[ci] % depth)))
            nxt[ci] += 1
    for ci in range(len(chains)):
        for _ in range(depth):
            start(ci)
    while active:
        for item in list(active):
            try:
                next(item[1])
            except StopIteration:
                active.remove(item)
                start(item[0])


def gdn_phase(k, l):
    P = k.P
    with scope(k):
        GC = P.sb("g_GC", [64, L], F32, blk=512)
        GP = [P.sb("g_GP%d" % p, [128, NCK, 4], F32) for p in range(3)]
        NBP = [P.sb("g_NBP%d" % p, [128, NCK, 2], F32) for p in range(3)]
        sel = P.sb("g_sel", [64, 6, 128], F32)
        negm = P.sb("g_negm", [128, 2, 64], F32)
        id2 = P.sb("g_id2", [128, 64], F32)
        cw = P.sb("g_cw", [128, 9, 5], F32)
        ng = P.sb("g_ng", [128, 1], F32)
        bones = P.sb("g_bones", [128, 128], F32)
        P.dma(sel[:], k.gsel_d[:])
        P.dma(negm[:], k.gnegm_d[:])
        P.dma(id2[:], k.gid2_d[:])
        P.dma(cw[:], k.gconv_d[l])
        P.dma(ng[:], k.gng_d[l])
        P.memset(bones[:], 0.0)
        P.memset(bones[0:64, 0:64], 1.0)
        P.memset(bones[64:128, 64:128], 1.0)
        bonesb = P.sb("g_bonesb", [128, 128], BF16)
        P.cp(bonesb[:], bones[:])
        with scope(k):
            GT = P.sb("g_GT", [128, L], F32, blk=512)
            m0 = P.sb("g_m0", [64, L], F32)
            gb = P.sb("g_gb", [128, 2], F32)
            negA = P.sb("g_negA", [128, 1], F32)
            P.dma(gb[:], k.ggb_d[l])
            P.act(negA[:], gb[:, 1:2], AF.Exp)
            P.ts(negA[:], negA[:], -1.0, ALU.mult)
            P.memset(m0[:], 1.0)
            P.memset(m0[:, 0:L:64], 0.0)
            pa = proj(k, l, "gab")
            for tb in range(4):
                sl = slice(tb * 512, (tb + 1) * 512)
                P.act(GT[0:64, sl], pa[tb][0:64, :], AF.Identity, bias=gb[0:64, 0:1])
                P.act(GT[64:128, sl], pa[tb][64:128, :], AF.Sigmoid)
            P.act(GC[:, :], GT[0:64, :], AF.Abs)
            P.act(GC[:, :], GC[:, :], AF.Exp, scale=-1.0)
            P.act(GC[:, :], GC[:, :], AF.Ln, bias=1.0)
            P.act(GT[0:64, :], GT[0:64, :], AF.Relu)
            P.tt(GT[0:64, :], GT[0:64, :], GC[:, :], ALU.add)
            P.ts(GT[0:64, :], GT[0:64, :], negA[0:64, 0:1], ALU.mult)
            P.scan(GC[:, :], m0[:, :], GT[0:64, :], 0.0, ALU.mult, ALU.add)
            gc3 = GC[32:64, :].rearrange("p (a b) -> p a b", b=64)
            P.tt(m0[32:64, :].rearrange("p (a b) -> p a b", b=64), bc3(GC[32:64, 63:L:64], 64), gc3, ALU.subtract)
            P.tt(GC[32:64, :], m0[32:64, :], GT[32:64, :], ALU.add)
            for grp in range(4):
                g8 = slice(grp * 8, (grp + 1) * 8)
                for c in range(8):
                    ck = grp * 8 + c
                    cs = slice(c * 64, (c + 1) * 64)
                    for hs in HS:
                        P.mm(k.PB[2 * (grp % 2)][hs, cs], GC[:, ck * 64:(ck + 1) * 64], k.ident[0:64, 0:64])
                        P.mm(k.PB[2 * (grp % 2) + 1][hs, cs], GT[64:128, ck * 64:(ck + 1) * 64], k.ident[64:128, 64:128])
                n_ = 0
                for p in range(3):
                    for hf, hs in enumerate(HS):
                        h = 2 * p + hf
                        for q, ps in ((0, k.PB[2 * (grp % 2)]), (1, k.PB[2 * (grp % 2) + 1])):
                            src = v3(ps)[hs, :, h:h + 33:32]
                            P.cp(GP[p][hs, g8, 2 * q:2 * q + 2], src, eng=("act" if q else "dve"))
            for p in range(3):
                P.ts(NBP[p][:], GP[p][:, :, 2:4], -1.0, ALU.mult)
        for p in range(3):
            with scope(k):
                Q = P.sb("g_Q", [128, L], BF16, blk=512)
                K_ = P.sb("g_K", [128, L], BF16, blk=512)
                Kt = P.sb("g_Kt", [128, NCK, 64], BF16, blk=512)
                Vt = P.sb("g_Vt", [128, NCK, 64], BF16, blk=512)
                O = P.sb("g_O", [128, L], F32, blk=512)
                P.memset(O[:], 0.0, eng="pool")
                with scope(k):
                    xps = [P.sb("g_xp%d" % i, [128, L + 4], BF16) for i in range(2)]
                    Dgs = [P.sb("g_Dg%d" % i, [128, 5, 128], BF16) for i in range(2)]
                    cvs = [P.sb("g_cv%d" % i, [128, L], F32, blk=512) for i in range(2)]
                    Vf = P.sb("g_Vf", [128, L], BF16, blk=512)
                    sq = P.sb("g_sq", [128, 512], BF16)
                    rn = P.sb("g_rn", [128, 512], F32)
                    sq2 = P.sb("g_sq2", [128, 512], BF16)
                    rn2 = P.sb("g_rn2", [128, 512], F32)
                    for xp in xps:
                        P.memset(xp[:, 0:2], 0.0)
                        P.memset(xp[:, L + 2:L + 4], 0.0)
                    for part, nm, dst in ((0, "gq", Q), (1, "gk", K_), (2, "gv", Vf)):
                        xp, Dg, cv = xps[part % 2], Dgs[part % 2], cvs[part % 2]
                        pa = proj(k, l, "%s%d" % (nm, p))
                        for tb in range(4):
                            P.cp(xp[:, 2 + tb * 512:2 + (tb + 1) * 512], pa[tb][:, :], eng=("act" if tb % 2 else "dve"))
                        wi = part * 3 + p
                        for j in range(5):
                            P.ts(Dg[:, j, :], k.identb[:], cw[:, wi, j:j + 1], ALU.mult, eng=("pool" if j % 2 else "dve"))
                        for tb in range(4):
                            for j in range(5):
                                P.mm(k.PB[tb][:, :], Dg[:, j, :], xp[:, j + tb * 512:j + (tb + 1) * 512],
                                     start=(j == 0), stop=(j == 4))
                        for tb in range(4):
                            sl = slice(tb * 512, (tb + 1) * 512)
                            P.act((dst if part == 2 else cv)[:, sl], k.PB[tb][:, :], AF.Silu)
                        if part < 2:
                            def fin(tb, r, dst=dst, cv=cv):
                                P.tt(dst[:, tb * 512:(tb + 1) * 512], cv[:, tb * 512:(tb + 1) * 512], r[:], ALU.mult)
                            norm_pipe(k, 4, lambda tb, cv=cv: cv[:, tb * 512:(tb + 1) * 512], (sq, sq2), (rn, rn2),
                                      (k.PA[2], k.PA[3]), bonesb, -0.5, 64.0 if part == 0 else 1.0,
                                      64e-6 if part == 0 else 1e-6, fin)
                    for src, dstt in ((K_, Kt), (Vf, Vt)):
                        for grp in range(4):
                            ps = k.PB[grp % 2]
                            for c in range(8):
                                ck = grp * 8 + c
                                tr2(P, ps, c, src[:, ck * 64:(ck + 1) * 64], k.identb)
                            P.cp(dstt[:, grp * 8:(grp + 1) * 8, :], v3(ps), eng=("act" if grp % 2 else "dve"))
                with scope(k):
                    T = dict(GC=GC, GP=GP[p], NBP=NBP[p], sel=sel, negm=negm, id2=id2, Q=Q, K=K_, Kt=Kt, Vt=Vt, O=O)
                    run_pipelined([gdn_chain(k, p, d, T) for d in range(2)], depth=1)
                with scope(k):
                    sqo = [P.sb("g_osq%d" % i, [128, 512], BF16) for i in range(2)]
                    rno = [P.sb("g_orn%d" % i, [128, 512], F32) for i in range(2)]

                    def fin_o(tb, r):
                        sl = slice(tb * 512, (tb + 1) * 512)
                        P.tt(r[:], O[:, sl], r[:], ALU.mult)
                        P.ts(mixc(k, p)[:, sl], r[:], ng[:, 0:1], ALU.mult)
                    norm_pipe(k, 4, lambda tb: O[:, tb * 512:(tb + 1) * 512], sqo, rno, (k.PB[2], k.PB[3]), bonesb,
                              -0.5, 1.0 / 64, EPS, fin_o)


def gdn_chain(k, p, d, T):
    P = k.P
    GC, GP, NBP, sel, negm, id2, Q, K_, Kt, Vt, O = (T[n] for n in ("GC", "GP", "NBP", "sel", "negm", "id2", "Q", "K", "Kt", "Vt", "O"))
    B = k.PA if d == 0 else k.PB
    tag = "g%d_" % d
    names = ("CB", "EI", "QG", "Rm", "U0", "WT", "BW", "KD", "GK", "AcT", "Sg", "Ug", "nA", "nB", "nC", "nD", "Nb", "PTb", "Sgb")
    f32n = ("CB", "EI", "AcT", "Sg")
    NSET = 1
    GG = [{n: P.sb(tag + "%d" % s_ + n, [128, 8, 64], F32 if n in f32n else BF16) for n in names} for s_ in range(NSET)]
    for s_ in range(NSET):
        GG[s_]["gend"] = P.sb(tag + "gend%d" % s_, [128, 8], F32)
        GG[s_]["kds"] = P.sb(tag + "kds%d" % s_, [128, 8], F32)
    gam = P.sb(tag + "gam", [128, NCK], F32)
    shared = {"scan": 0}
    Scar = P.sb(tag + "Scar", [128, 64], F32)
    e_ = 63 if d == 0 else 0
    idb = bcm(id2[:, :], 8)

    def f2(t):
        return t[:].rearrange("p a b -> p (a b)")
    P.memset(Scar[:], 0.0)
    P.act(gam[:], GP[:, :, d], AF.Exp)
    gorder = list(range(4)) if d == 0 else list(range(3, -1, -1))

    def group(gi, grp, G):
        Nb, PTb, Sgb, gend, kds = G["Nb"], G["PTb"], G["Sgb"], G["gend"], G["kds"]
        sl = slice(grp * 512, (grp + 1) * 512)
        g8 = slice(grp * 8, (grp + 1) * 8)
        cj = GP[:, g8, d]
        nb = NBP[:, g8, d]
        CB, EI, QG, Rm, U0, WT, BW, KD, GK, AcT, Sg, Ug = (G[n] for n in names[:12])
        P.mm(B[0][:, :], sel[:, d * 3 + p, :], GC[:, sl])
        P.cp(f2(CB), B[0][:, :], eng="act")
        P.act(f2(EI), f2(CB), AF.Exp)
        P.cp(gend[:], EI[:, :, e_], eng="pool")
        P.tt(f2(QG), f2(EI), Q[:, sl], ALU.mult)
        P.tt(kds[:], CB[:, :, e_], cj, ALU.subtract)
        P.act(kds[:], kds[:], AF.Exp)
        P.tt(GK[:], Kt[:, g8, :], bc3(gam[:, g8], 64), ALU.mult, eng="pool")
        P.tt(KD[:], Kt[:, g8, :], bc3(kds[:], 64), ALU.mult, eng="pool")
        P.tt(CB[:], CB[:], bc3(cj, 64), ALU.subtract)
        P.tt(CB[:], CB[:], bcm(negm[:, d, :], 8), ALU.add)
        P.act(f2(CB), f2(CB), AF.Exp)
        yield
        P.tt(EI[:], CB[:], idb, ALU.add)
        for c in range(8):
            cs = slice((grp * 8 + c) * 64, (grp * 8 + c + 1) * 64)
            mm2(P, B[0], c, K_[:, cs], Q[:, cs])
        P.tt(PTb[:], EI[:], v3(B[0]), ALU.mult)
        for c in range(8):
            cs = slice((grp * 8 + c) * 64, (grp * 8 + c + 1) * 64)
            mm2(P, B[1], c, K_[:, cs], K_[:, cs])
        P.tt(CB[:], CB[:], v3(B[1]), ALU.mult)
        P.tt(Nb[:], CB[:], bc3(nb, 64), ALU.mult)
        yield
        for _ in neumann2(k, Nb, Rm, (G["nA"], G["nB"], G["nC"], G["nD"]), (B[0], B[1], B[2]), id2):
            yield
        for c in range(8):
            mm2(P, B[2], c, Rm[:, c, :], GK[:, c, :])
        P.tt(BW[:], v3(B[2]), bc3(nb, 64), ALU.mult)
        for c in range(8):
            mm2(P, B[0], c, Rm[:, c, :], Vt[:, grp * 8 + c, :])
        P.tt(U0[:], v3(B[0]), bc3(GP[:, g8, 2 + d], 64), ALU.mult)
        for c in range(8):
            mm2(P, B[1], c, GK[:, c, :], Rm[:, c, :])
        P.cp(WT[:], v3(B[1]), eng="act")
        yield
        for c in range(8):
            mm2(P, B[3], c, BW[:, c, :], KD[:, c, :])
        P.tt(AcT[:], idb, bc3(gend[:], 64), ALU.mult, eng="pool")
        P.tt(AcT[:], AcT[:], v3(B[3]), ALU.add)
        yield
        while shared["scan"] != gi:
            yield
        corder = range(8) if d == 0 else range(7, -1, -1)
        prev = Scar[:]
        for n, c in enumerate(corder):
            P.cp(Sg[:, c, :], prev, eng="pool") if n == 0 else None
            ps = B[2 + n % 2]
            mm2(P, ps, 0, AcT[:, c, :], Sg[:, c, :], start=True, stop=False)
            mm2(P, ps, 0, KD[:, c, :], U0[:, c, :], start=False, stop=True)
            last = (n == 7)
            dst = Scar[:] if last else Sg[:, corder[n + 1], :]
            P.cp(dst, ps[:, 0:64], eng=("act" if n % 2 == 0 else "dve"))
            yield
        shared["scan"] = gi + 1
        P.cp(Sgb[:], Sg[:], eng="act")
        for c in range(8):
            mm2(P, B[0], c, WT[:, c, :], Sgb[:, c, :])
        P.tt(CB[:], v3(B[0]), bc3(nb, 64), ALU.mult)
        P.tt(Ug[:], CB[:], U0[:], ALU.add)
        yield
        for c in range(8):
            mm2(P, B[1], c, Sgb[:, c, :], QG[:, c, :], start=True, stop=False)
            mm2(P, B[1], c, Ug[:, c, :], PTb[:, c, :], start=False, stop=True)
        P.tt(O[:, sl], O[:, sl], B[1][:, :], ALU.add)
        yield
    return [(lambda slot, gi=gi, grp=grp: group(gi, grp, GG[slot])) for gi, grp in enumerate(gorder)]


RSKEW = 0
RW_EPS = 64e-5
DEC = float(np.exp(-0.5))
GS = 8
NG = NCK // GS


def host_rwkv(inp, b, m):
    mu = inp["rwkv_mu"]
    o = np.zeros((DEPTH, 128, 8, 2), np.float32)
    for part in range(3):
        for p in range(2):
            o[:, :, part * 2 + p, :] = mu[:, :, part * 256 + p * 128: part * 256 + (p + 1) * 128].transpose(0, 2, 1)
    o[:, 0:64, 6, :] = mu[:, :, 768:832].transpose(0, 2, 1)
    o[:, 0:64, 7, :] = mu[:, :, 832:896].transpose(0, 2, 1)
    m["rmu"] = o

    def pp(a):
        if a.ndim == 2:
            return np.ascontiguousarray(a.reshape(DEPTH, 2, 128).transpose(0, 2, 1))
        return np.ascontiguousarray(a.reshape(DEPTH, 2, 2, 128).transpose(0, 3, 1, 2))
    pv = np.zeros((DEPTH, 128, 7, 2), np.float32)
    pv[:, :, 0:2, :] = pp(inp["rwkv_w0"])
    pv[:, :, 2:4, :] = pp(inp["rwkv_a0"])
    pv[:, :, 4, :] = pp(inp["rwkv_k_k"])
    pv[:, :, 5, :] = pp(inp["rwkv_k_a"])
    pv[:, :, 6, :] = pp(inp["rwkv_r_k"].reshape(DEPTH, 256))
    m["rpv"] = pv
    ln = np.zeros((DEPTH, 128, 2, 2), np.float32)
    ln[:, :, 0, :] = pp(inp["rwkv_ln_g"])
    ln[:, :, 1, :] = pp(inp["rwkv_ln_b"])
    m["rln"] = ln
    m["rw2"] = np.ascontiguousarray(inp["rwkv_w2"].transpose(0, 2, 1, 3))
    m["ra2"] = np.ascontiguousarray(inp["rwkv_a2"].transpose(0, 2, 1, 3))
    s_ = np.arange(64)[:, None]
    t_ = np.arange(64)[None, :]
    msk = np.zeros((128, 2, 4, 64), np.float32)
    msk[:, 0, 0, :] = np.tile((t_ > s_), (2, 1))
    msk[:, 0, 1, :] = np.tile((t_ >= s_), (2, 1))
    msk[:, 1, 0, :] = np.tile((t_ < s_), (2, 1))
    msk[:, 1, 1, :] = np.tile((t_ <= s_), (2, 1))
    msk[:, :, 2:4, :] = -msk[:, :, 0:2, :]
    m["rmsk"] = msk


def rwkv_decl(k):
    nc = k.nc

    def din(name, shape, dt=F32):
        return nc.dram_tensor(name, list(shape), dt, kind="ExternalInput").ap()
    k.rmu_d = din("rmu", [DEPTH, 128, 8, 2])
    k.rpv_d = din("rpv", [DEPTH, 128, 7, 2])
    k.rln_d = din("rln", [DEPTH, 128, 2, 2])
    k.rw2_d = din("rw2", [DEPTH, 64, 2, 256])
    k.ra2_d = din("ra2", [DEPTH, 64, 2, 256])
    k.rmsk_d = din("rmsk", [128, 2, 4, 64])


def rwkv_phase(k, l):
    P = k.P
    with scope(k):
        mu = P.sb("r_mu", [128, 8, 3], F32)
        pv = P.sb("r_pv", [128, 7, 2], F32)
        omka = P.sb("r_omka", [128, 2], F32)
        hrk = P.sb("r_hrk", [128, 2], F32)
        ln = P.sb("r_ln", [128, 2, 2], F32)
        w2 = P.sb("r_w2", [64, 2, 256], BF16)
        a2 = P.sb("r_a2", [64, 2, 256], BF16)
        msk = P.sb("r_msk", [128, 2, 4, 64], F32)
        id2 = P.sb("r_id2", [128, 64], F32)
        bones = P.sb("r_bones", [128, 128], F32)
        m0 = P.sb("r_m0", [128, GS * 64], F32)
        twd = P.sb("r_twd", [64, L], BF16, blk=512)
        adx = P.sb("r_adx", [64, L], BF16, blk=512)
        sh32 = P.sb("r_sh32", [128, L], F32, blk=512)
        xp = P.sb("r_xp", [128, L + 2], F32)
        P.dma(mu[:, :, 0:2], k.rmu_d[l])
        P.dma(pv[:], k.rpv_d[l])
        P.dma(ln[:], k.rln_d[l])
        with scope(k):
            w2f = P.sb("r_w2f", [64, 2, 256], F32)
            a2f = P.sb("r_a2f", [64, 2, 256], F32)
            P.dma(w2f[:], k.rw2_d[l])
            P.dma(a2f[:], k.ra2_d[l])
            P.cp(w2[:], w2f[:], eng="act")
            P.cp(a2[:], a2f[:], eng="act")
        P.dma(msk[:], k.rmsk_d[:])
        P.dma(id2[:], k.gid2_d[:])
        P.memset(bones[:], 0.0)
        P.memset(bones[0:64, 0:64], 1.0)
        P.memset(bones[64:128, 64:128], 1.0)
        P.memset(m0[:], 1.0)
        P.memset(m0[:, 0:GS * 64:64], 0.0)
        P.memset(xp[:, 0:1], 0.0)
        P.memset(xp[:, L + 1:L + 2], 0.0)
        P.tt(mu[:, :, 2], mu[:, :, 0], mu[:, :, 1], ALU.add)
        P.ts(mu[:, :, 2], mu[:, :, 2], -1.0, ALU.mult, 1.0, ALU.add)
        P.ts(omka[:], pv[:, 5, :], -1.0, ALU.mult, 1.0, ALU.add)
        P.ts(hrk[:], pv[:, 6, :], 0.5, ALU.mult)

        def shifted(name, ci, dst, np_=128, fn=None):
            pa = proj(k, l, name, alt=(ci % 2 == 1))
            for tb in range(4):
                P.cp(xp[0:np_, 1 + tb * 512:1 + (tb + 1) * 512], pa[tb][0:np_, :], eng=("act" if tb % 2 else "dve"))
            t_ = sh32[0:np_, :]
            P.ts(t_, xp[0:np_, 1:L + 1], mu[0:np_, ci, 2:3], ALU.mult)
            P.stt(t_, xp[0:np_, 0:L], mu[0:np_, ci, 0:1], t_, ALU.mult, ALU.add)
            if fn is None:
                P.stt(dst[:], xp[0:np_, 2:L + 2], mu[0:np_, ci, 1:2], t_, ALU.mult, ALU.add)
            else:
                P.stt(t_, xp[0:np_, 2:L + 2], mu[0:np_, ci, 1:2], t_, ALU.mult, ALU.add)
                P.act(dst[:], t_, fn)

        shifted("rwd", 6, twd, 64, AF.Tanh)
        shifted("rad", 7, adx, 64)
        R_ = P.sb("r_R", [128, L], BF16, blk=512)
        KX = P.sb("r_KX", [128, L], BF16, blk=512)
        V_ = P.sb("r_V", [128, L], BF16, blk=512)
        KK = P.sb("r_KK", [128, L], BF16, blk=512)
        Vt = P.sb("r_Vt", [128, NCK, 64], BF16, blk=512)
        KS = P.sb("r_KS", [128, L], F32, blk=512)
        Y = xp[:, 1:L + 1]
        sq = P.sb("r_sq", [128, 512], BF16)
        bonesb = P.sb("r_bonesb", [128, 128], BF16)
        P.cp(bonesb[:], bones[:])
        rn = P.sb("r_rn", [128, 512], F32)
        CH = [rwkv_tiles(k, e) for e in range(2)]

        class _V:
            def __init__(s_, t):
                s_.t = t

            def __getitem__(s_, key):
                return s_.t[:].rearrange("p a b -> p (a b)")[key]
        sq2, rn2 = _V(CH[0]["kap"]), _V(CH[0]["a"])
        for p in range(2):
            shifted("rr%d" % p, 0 + p, R_)
            shifted("rk%d" % p, 2 + p, KX)
            shifted("rv%d" % p, 4 + p, V_)
            P.ts(sh32[:], KX[:], pv[:, 4, p:p + 1], ALU.mult)

            def fin_k(tb, r):
                P.tt(KK[:, tb * 512:(tb + 1) * 512], sh32[:, tb * 512:(tb + 1) * 512], r[:], ALU.mult)
            norm_pipe(k, 4, lambda tb: sh32[:, tb * 512:(tb + 1) * 512], (sq, sq2), (rn, rn2), (k.PB[2], k.PB[3]),
                      bonesb, -0.5, 1.0, 1e-6, fin_k)
            for grp in range(4):
                ps = k.PB[grp % 2]
                for c in range(8):
                    ck = grp * 8 + c
                    tr2(P, ps, c, V_[:, ck * 64:(ck + 1) * 64], k.identb)
                P.cp(Vt[:, grp * 8:(grp + 1) * 8, :], v3(ps), eng=("act" if grp % 2 else "dve"))
            P.memset(xp[:, 1:L + 1], 0.0, eng="pool")
            P.memset(KS[:], 0.0, eng="pool")
            T = dict(pv=pv, omka=omka, w2=w2, a2=a2, msk=msk, id2=id2, m0=m0, twd=twd, adx=adx,
                     R=R_, KX=KX, KK=KK, Vt=Vt, KS=KS, Y=Y)
            run_interleaved([rwkv_chain(k, p, e, T, CH[e]) for e in range(2)], skew=RSKEW)
            for tb in range(4):
                sl = slice(tb * 512, (tb + 1) * 512)
                P.mm(k.PB[tb % 2][:, :], bones[:], Y[:, sl])
                P.stt(Y[:, sl], k.PB[tb % 2][:, :], -1.0 / 64, Y[:, sl], ALU.mult, ALU.add)

            def fin_y(tb, r):
                sl = slice(tb * 512, (tb + 1) * 512)
                P.tt(Y[:, sl], Y[:, sl], r[:], ALU.mult)
                P.ts(Y[:, sl], Y[:, sl], ln[:, 0, p:p + 1], ALU.mult, ln[:, 1, p:p + 1], ALU.add)
            norm_pipe(k, 4, lambda tb: Y[:, tb * 512:(tb + 1) * 512], (sq, sq2), (rn, rn2), (k.PB[2], k.PB[3]),
                      bonesb, -0.5, 1.0 / 64, RW_EPS, fin_y)
            for tb in range(4):
                sl = slice(tb * 512, (tb + 1) * 512)
                s_ = (sq, sq2)[tb % 2]
                r_ = (rn, rn2)[tb % 2]
                P.tt(s_[:], R_[:, sl], KS[:, sl], ALU.mult)
                P.ts(s_[:], s_[:], hrk[:, p:p + 1], ALU.mult, eng="pool")
                P.mm(k.PB[tb % 2][:, :], bonesb[:], s_[:])
                P.tt(r_[:], k.PB[tb % 2][:, :], V_[:, sl], ALU.mult)
                P.tt(mixc(k, 6 + p)[:, sl], Y[:, sl], r_[:], ALU.add)


RW_F32 = ("lw", "a", "km", "b", "cl", "e1", "e2", "dend", "AcT", "Tg")
RW_BF16 = ("kap", "rt", "kt_", "bt_", "ke", "be", "kapT", "keT", "nbeT", "N", "Akv", "Brk", "nBrb", "Rm",
           "nA", "nB", "nC", "nD", "X0", "P0", "WkT", "Wk", "Tgb", "Pg")


def rwkv_tiles(k, e):
    P = k.P
    G = {n: P.sb("r%d_%s" % (e, n), [128, GS, 64], F32) for n in RW_F32}
    for n in RW_BF16:
        G[n] = P.sb("r%d_%s" % (e, n), [128, GS, 64], BF16)
    G["gC"] = P.sb("r%d_gC" % e, [128, GS], F32)
    G["Tcar"] = P.sb("r%d_Tcar" % e, [128, 64], F32)
    return G


def rwkv_chain(k, p, e, T, G):
    P = k.P
    pv, omka, w2, a2, msk, id2, m0, twd, adx, R_, KX, KK, Vt, KS, Y = (T[n] for n in (
        "pv", "omka", "w2", "a2", "msk", "id2", "m0", "twd", "adx", "R", "KX", "KK", "Vt", "KS", "Y"))
    B = k.PA if e == 0 else k.PB
    W = GS * 64
    e_ = 63 if e == 0 else 0
    idb = bcm(id2[:, :], GS)
    gC, Tcar = G["gC"], G["Tcar"]

    def f2(t):
        return t[:].rearrange("p a b -> p (a b)")

    def w3(ps):
        return v3(ps, GS)
    P.memset(Tcar[:], 0.0)
    gorder = range(NG) if e == 0 else range(NG - 1, -1, -1)
    pc = slice(p * 128, (p + 1) * 128)
    for grp in gorder:
        sl = slice(grp * W, (grp + 1) * W)
        c0 = grp * GS
        P.mm(B[0][:, 0:W], w2[:, e, pc], twd[:, sl])
        P.mm(B[1][:, 0:W], a2[:, e, pc], adx[:, sl])
        P.act(f2(G["lw"]), B[0][:, 0:W], AF.Sigmoid, bias=pv[:, 0 + e, p:p + 1])
        P.act(f2(G["a"]), B[1][:, 0:W], AF.Sigmoid, bias=pv[:, 2 + e, p:p + 1])
        P.ts(f2(G["km"]), f2(G["a"]), pv[:, 5, p:p + 1], ALU.mult, omka[:, p:p + 1], ALU.add)
        P.tt(f2(G["km"]), f2(G["km"]), KX[:, sl], ALU.mult)
        P.tt(f2(G["b"]), f2(G["a"]), KK[:, sl], ALU.mult)
        P.tt(KS[:, sl], KS[:, sl], f2(G["km"]), ALU.add, eng="pool")
        P.scan(f2(G["cl"]), m0[:], f2(G["lw"]), 0.0, ALU.mult, ALU.add)
        if e == 1:
            P.tt(G["e1"][:], bc3(G["cl"][:, :, 63], 64), G["cl"][:], ALU.subtract)
            P.tt(G["cl"][:], G["e1"][:], G["lw"][:], ALU.add)
        yield
        P.act(G["e1"][:], G["cl"][:], AF.Exp, scale=-DEC)
        P.act(G["e2"][:], G["cl"][:], AF.Exp, scale=DEC)
        P.tt(f2(G["rt"]), f2(G["e1"]), R_[:, sl], ALU.mult)
        P.tt(G["kt_"][:], G["e2"][:], G["km"][:], ALU.mult)
        P.tt(G["bt_"][:], G["e2"][:], G["b"][:], ALU.mult)
        P.tt(G["dend"][:], G["cl"][:], G["lw"][:], ALU.subtract)
        P.act(G["dend"][:], G["dend"][:], AF.Exp, scale=-DEC)
        P.tt(f2(G["kap"]), f2(G["dend"]), KK[:, sl], ALU.mult)
        P.cp(gC[:], G["e1"][:, :, e_], eng="pool")
        P.tt(G["dend"][:], bc3(G["cl"][:, :, e_], 64), G["cl"][:], ALU.subtract)
        P.act(G["dend"][:], G["dend"][:], AF.Exp, scale=-DEC)
        P.tt(G["ke"][:], G["dend"][:], G["km"][:], ALU.mult)
        P.tt(G["be"][:], G["dend"][:], G["b"][:], ALU.mult, eng="pool")
        yield
        for src, dst, sc in ((G["kap"], G["kapT"], 1.0), (G["ke"], G["keT"], 1.0), (G["be"], G["nbeT"], -1.0)):
            ps = B[0] if sc == 1.0 and src is G["kap"] else (B[1] if sc == 1.0 else B[2])
            for c in range(GS):
                tr2(P, ps, c, src[:, c, :], k.identb)
            if sc == 1.0:
                P.cp(dst[:], w3(ps), eng="act")
            else:
                P.ts(dst[:], w3(ps), -1.0, ALU.mult)
        yield
        for c in range(GS):
            mm2(P, B[0], c, G["bt_"][:, c, :], G["kap"][:, c, :])
            mm2(P, B[1], c, G["kt_"][:, c, :], G["kap"][:, c, :])
            mm2(P, B[2], c, G["kt_"][:, c, :], G["rt"][:, c, :])
            mm2(P, B[3], c, G["bt_"][:, c, :], G["rt"][:, c, :])
        ms = bcm(msk[:, e, 0, :], GS)
        mi = bcm(msk[:, e, 1, :], GS)
        nms = bcm(msk[:, e, 2, :], GS)
        nmi = bcm(msk[:, e, 3, :], GS)
        P.tt(G["N"][:], w3(B[0]), nms, ALU.mult)
        P.tt(G["Akv"][:], w3(B[1]), ms, ALU.mult)
        P.tt(G["Brk"][:], w3(B[2]), mi, ALU.mult)
        P.tt(G["nBrb"][:], w3(B[3]), nmi, ALU.mult)
        yield
        for _ in neumann2(k, G["N"], G["Rm"], (G["nA"], G["nB"], G["nC"], G["nD"]), (B[0], B[1], B[2]), id2, GS):
            yield
        for c in range(GS):
            mm2(P, B[0], c, G["Akv"][:, c, :], Vt[:, c0 + c, :])
        P.cp(G["X0"][:], w3(B[0]), eng="act")
        yield
        for c in range(GS):
            mm2(P, B[0], c, G["Rm"][:, c, :], G["X0"][:, c, :])
            mm2(P, B[1], c, G["kapT"][:, c, :], G["Rm"][:, c, :])
            mm2(P, B[2], c, G["Rm"][:, c, :], G["kapT"][:, c, :])
        P.cp(G["P0"][:], w3(B[0]), eng="act")
        P.cp(G["WkT"][:], w3(B[1]), eng="dve")
        P.cp(G["Wk"][:], w3(B[2]), eng="act")
        yield
        for c in range(GS):
            mm2(P, B[3], c, G["Wk"][:, c, :], G["nbeT"][:, c, :])
        P.tt(G["AcT"][:], idb, bc3(gC[:], 64), ALU.mult, eng="pool")
        P.tt(G["AcT"][:], G["AcT"][:], w3(B[3]), ALU.add)
        yield
        corder = list(range(GS)) if e == 0 else list(range(GS - 1, -1, -1))
        Tg = G["Tg"]
        for n, c in enumerate(corder):
            if n == 0:
                P.cp(Tg[:, c, :], Tcar[:], eng="pool")
            ps = B[2 + n % 2]
            mm2(P, ps, 0, G["AcT"][:, c, :], Tg[:, c, :], start=True, stop=False)
            mm2(P, ps, 0, G["keT"][:, c, :], Vt[:, c0 + c, :], start=False, stop=False)
            mm2(P, ps, 0, G["nbeT"][:, c, :], G["P0"][:, c, :], start=False, stop=True)
            dst = Tcar[:] if n == GS - 1 else Tg[:, corder[n + 1], :]
            P.cp(dst, ps[:, 0:64], eng=("act" if n % 2 == 0 else "dve"))
            yield
        P.cp(G["Tgb"][:], Tg[:], eng="act")
        for c in range(GS):
            mm2(P, B[0], c, G["WkT"][:, c, :], G["Tgb"][:, c, :])
        P.tt(G["Pg"][:], w3(B[0]), G["P0"][:], ALU.add)
        yield
        for c in range(GS):
            mm2(P, B[1], c, Vt[:, c0 + c, :], G["Brk"][:, c, :], start=True, stop=False)
            mm2(P, B[1], c, G["Tgb"][:, c, :], G["rt"][:, c, :], start=False, stop=False)
            mm2(P, B[1], c, G["Pg"][:, c, :], G["nBrb"][:, c, :], start=False, stop=True)
        P.tt(Y[:, sl], Y[:, sl], B[1][:, 0:W], ALU.add)
        yield


_CACHE = {}


def kernel(**inputs):
    inp = {k_: np.asarray(v) for k_, v in inputs.items()}
    if "k" not in _CACHE:
        _CACHE["k"] = build()
    k = _CACHE["k"]
    B = inp["x"].shape[0]
    base = host_inputs(inp, 0)
    in_maps = []
    for b in range(B):
        m = dict(base)
        m["x"] = np.ascontiguousarray(inp["x"][b], dtype=np.float32)
        m["pos"] = np.ascontiguousarray(inp["positions"][b].reshape(1, L).astype(np.int32))
        in_maps.append(m)
    res = run_bass_kernel_spmd(k.nc, in_maps, core_ids=list(range(B)))
    return np.stack([np.asarray(r["out"], dtype=np.float32) for r in res.results], axis=0)
```

```python
import numpy as np
import concourse.bass as bass
import concourse.mybir as mybir
from concourse.bass_utils import run_bass_kernel_spmd
from contextlib import ExitStack

F32 = mybir.dt.float32
BF16 = mybir.dt.bfloat16
I32 = mybir.dt.int32
AF = mybir.ActivationFunctionType
ALU = mybir.AluOpType
AX = mybir.AxisListType
DTSIZE = {F32: 4, BF16: 2, I32: 4}


class _Op:
    __slots__ = ("eng", "emit", "deps", "idx", "needed", "sigval", "dsem", "dval", "isdma")

    def __init__(self, eng, emit, isdma=False):
        self.eng = eng
        self.emit = emit
        self.deps = []
        self.idx = -1
        self.needed = False
        self.sigval = 0
        self.dsem = None
        self.dval = 0
        self.isdma = isdma


class _Blk:
    __slots__ = ("w", "r")

    def __init__(self):
        self.w = None
        self.r = {}


class Prog:
    ENGS = ("pe", "dve", "act", "pool", "sp")
    NDMA = 48
    NHW = 32

    def __init__(self, nc, stack):
        self.nc = nc
        self.stack = stack
        self.ops = {e: [] for e in self.ENGS}
        self.track = {}
        self.seen = {e: {} for e in self.ENGS}
        self.seen_dma = {e: set() for e in self.ENGS}
        self.dma_last = [None] * self.NDMA
        self.dma_uses = [0] * self.NDMA
        self.dma_rr = 0
        self.dma_rr_sw = 0
        self.ndma_ops = 0
        self.untracked = set()
        self.out_dmas = []
        self.dma_pending = []
        self.last_compute = {}

    def sb(self, name, shape, dtype=F32, blk=None):
        self.uid = getattr(self, "uid", 0) + 1
        name = "s%d_%s" % (self.uid, name)
        t = self.stack.enter_context(self.nc.sbuf_tensor(name, list(shape), dtype))
        self._register(name, shape, dtype, blk)
        return t

    def ps(self, name, shape=(128, 512), dtype=F32, blk=None):
        self.uid = getattr(self, "uid", 0) + 1
        name = "p%d_%s" % (self.uid, name)
        t = self.stack.enter_context(self.nc.psum_tensor(name, list(shape), dtype))
        self._register(name, shape, dtype, blk)
        return t

    def _register(self, name, shape, dtype, blk):
        row = int(np.prod(shape[1:])) * DTSIZE[dtype]
        bb = row if blk is None else blk * DTSIZE[dtype]
        nb = (row + bb - 1) // bb
        self.track[name] = (bb, row, [_Blk() for _ in range(nb)])

    def dram_track(self, name, total_bytes, blk_bytes):
        nb = (total_bytes + blk_bytes - 1) // blk_bytes
        self.track[name] = (blk_bytes, -1, [_Blk() for _ in range(nb)])

    def _blocks(self, ap):
        name = ap.tensor.name
        if name not in self.track:
            return ()
        bb, row, blks = self.track[name]
        if len(blks) == 1:
            return blks
        ds = DTSIZE[ap.dtype]
        pat = ap.ap
        if row < 0:
            lo = hi = ap.offset
            for step, cnt in pat:
                ext = step * (cnt - 1)
                if ext < 0:
                    lo += ext
                else:
                    hi += ext
            return blks[(lo * ds) // bb:(hi * ds) // bb + 1]
        rowel = row // ds
        foff = ap.offset % rowel
        lo = hi = foff
        for step, cnt in pat[1:]:
            ext = step * (cnt - 1)
            if ext < 0:
                lo += ext
            else:
                hi += ext
        b0 = (lo * ds) // bb
        b1 = (hi * ds) // bb
        return blks[b0:b1 + 1]

    def _dep(self, x, y):
        if y is None or y is x:
            return
        e = x.eng
        if y.isdma:
            if id(y) in self.seen_dma[e]:
                return
            self.seen_dma[e].add(id(y))
            x.deps.append(y)
            return
        if y.eng == "pe" and e == "pe":
            return
        if y.idx <= self.seen[e].get(y.eng, -1):
            return
        self.seen[e][y.eng] = y.idx
        y.needed = True
        x.deps.append(y)

    def add(self, eng, emit, reads=(), writes=(), isdma=False):
        x = _Op(eng, emit, isdma)
        x.idx = len(self.ops[eng])
        rb = []
        for ap in reads:
            if ap is None or isinstance(ap, (int, float)):
                continue
            rb.extend(self._blocks(ap))
        wb = []
        for ap in writes:
            wb.extend(self._blocks(ap))
        for ap in reads:
            if ap is None or isinstance(ap, (int, float)) or not ap.tensor.name.startswith("p"):
                continue
            for b in self._blocks(ap):
                for key, y in b.r.items():
                    if key != eng:
                        self._dep(x, y)
        for b in rb:
            self._dep(x, b.w)
        for b in wb:
            self._dep(x, b.w)
            for y in b.r.values():
                self._dep(x, y)
        if isdma:
            if eng == "pool":
                s = self.NHW + self.dma_rr_sw
                self.dma_rr_sw = (self.dma_rr_sw + 1) % (self.NDMA - self.NHW)
            else:
                s = self.dma_rr
                self.dma_rr = (self.dma_rr + 1) % self.NHW
            self._dep(x, self.dma_last[s])
            self.dma_last[s] = x
            self.dma_uses[s] += 1
            x.dsem = s
            x.dval = 16 * self.dma_uses[s]
            self.ndma_ops += 1
        key = id(x) if isdma else eng
        for b in rb:
            b.r[key] = x
        for b in wb:
            b.w = x
            b.r = {}
        self.ops[eng].append(x)
        if isdma:
            self.dma_pending.append(x)
        else:
            self.last_compute[eng] = x
        return x

    def barrier(self):
        lasts = dict(self.last_compute)
        pend = list(self.dma_pending)
        self.dma_pending = []
        for e in self.ENGS:
            b = _Op(e, None)
            b.idx = len(self.ops[e])
            for e2, y in lasts.items():
                if e2 == e and e == "pe":
                    continue
                self._dep(b, y)
            for y in pend:
                self._dep(b, y)
            self.ops[e].append(b)

    def mm(self, out, lhsT, rhs, start=True, stop=True):
        return self.add("pe", lambda e: e.matmul(out, lhsT, rhs, start=start, stop=stop),
                        reads=(lhsT, rhs), writes=(out,))

    def tr(self, out, in_, ident):
        return self.add("pe", lambda e: e.transpose(out, in_, ident), reads=(in_, ident), writes=(out,))

    def tt(self, out, in0, in1, op, eng="dve"):
        return self.add(eng, lambda e: e.tensor_tensor(out, in0, in1, op), reads=(in0, in1), writes=(out,))

    def ts(self, out, in0, s1, op0, s2=None, op1=None, eng="dve", accum_out=None):
        kw = {}
        if eng == "pool" and op1 is None:
            if op0 == ALU.mult:
                s2, op1 = 0.0, ALU.add
            elif op0 == ALU.add:
                s2, op1 = 1.0, ALU.mult
        if op1 is not None:
            kw["op1"] = op1
        if accum_out is not None:
            kw["accum_out"] = accum_out
        w = (out,) if accum_out is None else (out, accum_out)
        return self.add(eng, lambda e: e.tensor_scalar(out, in0, s1, s2, op0, **kw),
                        reads=(in0, s1, s2), writes=w)

    def stt(self, out, in0, scalar, in1, op0, op1, accum_out=None):
        kw = {}
        if accum_out is not None:
            kw["accum_out"] = accum_out
        w = (out,) if accum_out is None else (out, accum_out)
        return self.add("dve", lambda e: e.scalar_tensor_tensor(out, in0, scalar, in1, op0, op1, **kw),
                        reads=(in0, scalar, in1), writes=w)

    def cp(self, out, in_, eng="dve"):
        if eng == "act":
            return self.add("act", lambda e: e.copy(out, in_), reads=(in_,), writes=(out,))
        return self.add(eng, lambda e: e.tensor_copy(out, in_), reads=(in_,), writes=(out,))

    def act(self, out, in_, func, bias=0.0, scale=1.0, accum_out=None):
        kw = {}
        if accum_out is not None:
            kw["accum_out"] = accum_out
        w = (out,) if accum_out is None else (out, accum_out)
        return self.add("act", lambda e: e.activation(out, in_, func, bias=bias, scale=scale, **kw),
                        reads=(in_, bias, scale), writes=w)

    def red(self, out, in_, op, axis=AX.X, eng="dve"):
        return self.add(eng, lambda e: e.tensor_reduce(out, in_, axis, op), reads=(in_,), writes=(out,))

    def recip(self, out, in_):
        return self.add("dve", lambda e: e.reciprocal(out, in_), reads=(in_,), writes=(out,))

    def rpow(self, out, in_, power, scale=1.0, bias=0.0):
        self.act(out, in_, AF.Ln, bias=bias, scale=scale)
        return self.act(out, out, AF.Exp, scale=power)

    def memset(self, ap, val, eng="dve"):
        return self.add(eng, lambda e: e.memset(ap, val), writes=(ap,))

    def scan(self, out, d0, d1, init, op0, op1):
        return self.add("dve", lambda e: e.tensor_tensor_scan(out, d0, d1, init, op0, op1),
                        reads=(d0, d1, init), writes=(out,))

    def dma(self, out, in_, eng="sp", is_output=False):
        x = self.add(eng, lambda e: e.dma_start(out=out, in_=in_), reads=(in_,), writes=(out,), isdma=True)
        if is_output:
            self.out_dmas.append(x)
        return x

    def finish(self):
        nc = self.nc
        fin = _Op("sp", None)
        fin.idx = len(self.ops["sp"])
        for y in self.out_dmas:
            self._dep(fin, y)
        self.ops["sp"].append(fin)
        sems = {}
        for e in ("pe", "dve", "act", "pool"):
            sems[e] = self.stack.enter_context(nc.semaphore("s_" + e))
        dsems = [self.stack.enter_context(nc.semaphore("d%d" % i)) for i in range(self.NDMA)]
        for e in ("pe", "dve", "act", "pool"):
            c = 0
            for x in self.ops[e]:
                if x.isdma:
                    continue
                if x.needed:
                    c += 1
                    x.sigval = c
            self.stats_sig = getattr(self, "stats_sig", {})
            self.stats_sig[e] = c
        ops = self.ops

        def replay(e, engobj):
            for x in ops[e]:
                for y in x.deps:
                    if y.isdma:
                        engobj.wait_ge(dsems[y.dsem], y.dval)
                    else:
                        engobj.wait_ge(sems[y.eng], y.sigval)
                if x.emit is None:
                    continue
                ins = x.emit(engobj)
                if x.isdma:
                    ins.then_inc(dsems[x.dsem], 16)
                elif x.needed:
                    ins.then_inc(sems[e], 1)

        with nc.Block() as block:
            @block.tensor
            def _(eng):
                replay("pe", eng)

            @block.vector
            def _(eng):
                replay("dve", eng)

            @block.scalar
            def _(eng):
                replay("act", eng)

            @block.gpsimd
            def _(eng):
                replay("pool", eng)

            @block.sync
            def _(eng):
                replay("sp", eng)


L = 2048
D = 1024
NT = L // 128
DEPTH = 2
N_IN = 3448
EPS = 1e-6

OFF = dict(gate=0, gdn_q=1024, gdn_k=1408, gdn_v=1792, gdn_a=2176, gdn_b=2188, mla_cq=2200, mla_ckv=2392,
           mla_kr=2520, rw_r=2552, rw_k=2808, rw_v=3064, rw_wd=3320, rw_ad=3384)


def chunk_table():
    ch = []
    for h in range(3):
        ch.append(("gq%d" % h, [(0, OFF["gdn_q"] + h * 128, 128)]))
        ch.append(("gk%d" % h, [(0, OFF["gdn_k"] + h * 128, 128)]))
        ch.append(("gv%d" % h, [(0, OFF["gdn_v"] + h * 128, 128)]))
    ch.append(("gab", [(0, OFF["gdn_a"], 6), (32, OFF["gdn_a"] + 6, 6), (64, OFF["gdn_b"], 6), (96, OFF["gdn_b"] + 6, 6)]))
    ch.append(("cq0", [(0, OFF["mla_cq"], 128)]))
    ch.append(("cq1", [(0, OFF["mla_cq"] + 128, 64)]))
    ch.append(("ckv", [(0, OFF["mla_ckv"], 128)]))
    ch.append(("kr", [(0, OFF["mla_kr"], 32)]))
    for i in range(2):
        ch.append(("rr%d" % i, [(0, OFF["rw_r"] + i * 128, 128)]))
        ch.append(("rk%d" % i, [(0, OFF["rw_k"] + i * 128, 128)]))
        ch.append(("rv%d" % i, [(0, OFF["rw_v"] + i * 128, 128)]))
    ch.append(("rwd", [(0, OFF["rw_wd"], 64)]))
    ch.append(("rad", [(0, OFF["rw_ad"], 64)]))
    for i in range(8):
        ch.append(("g%d" % i, [(0, OFF["gate"] + i * 128, 128)]))
    return ch


CHUNKS = chunk_table()
CH_IDX = {n: i for i, (n, _) in enumerate(CHUNKS)}
NCH = len(CHUNKS)


def host_win(w_in):
    out = np.zeros((DEPTH, NCH, 128, 8, 128), np.float32)
    for ci, (_, parts) in enumerate(CHUNKS):
        for dst, src, w in parts:
            blk = w_in[:, :, src:src + w].reshape(DEPTH, 8, 128, w)
            out[:, ci, :, :, dst:dst + w] = blk.transpose(0, 2, 1, 3)
    return out


class K:
    pass


def build(depth=DEPTH, mixers=("gdn", "mla", "rwkv"), dbg=False):
    nc = bass.Bass("TRN2", target_bir_lowering=False)
    k = K()
    k.nc = nc
    k.dbg = dbg
    k.dbg_outs = []
    k.cut = 99

    def din(name, shape, dt=F32):
        return nc.dram_tensor(name, list(shape), dt, kind="ExternalInput").ap()

    k.x_d = din("x", [L, D])
    k.win_d = din("win", [DEPTH, NCH, 128, 8, 128])
    k.normg_d = din("normg", [DEPTH, 128, 8])
    k.wout_d = din("wout", [DEPTH, 128, 8, 1024])
    k.fing_d = din("fing", [1, D])
    k.ident_d = din("ident", [128, 128])
    k.out_d = nc.dram_tensor("out", [L, D], F32, kind="ExternalOutput").ap()
    mla_decl(k)
    gdn_decl(k)
    rwkv_decl(k)

    with ExitStack() as st:
        P = Prog(nc, st)
        k.P = P
        k.xscr = nc.dram_tensor("xscr", [L, D], F32, kind="Internal").ap()
        P.dram_track("xscr", L * D * 4, 128 * D * 4)
        k.hT = P.sb("hT", [128, 8, L], BF16, blk=512)
        k.ident = P.sb("ident", [128, 128], F32)
        k.identb = P.sb("identb", [128, 128], BF16)
        k.normg = P.sb("normg", [128, DEPTH, 8], F32)
        k.wst = [P.sb("wst%d" % i, [128, 8, 128], F32) for i in range(2)]
        k.wbf = [P.sb("wbf%d" % i, [128, 8, 128], BF16) for i in range(2)]
        k.wrr = 0
        k.PAW = [P.ps("paw%d" % i, [128, 1024], F32, blk=512) for i in range(2)]
        k.PA = [k.PAW[i // 2][:, (i % 2) * 512:(i % 2 + 1) * 512] for i in range(4)]
        k.PB = [P.ps("pb%d" % i, [128, 512], F32) for i in range(4)]

        P.dma(k.ident[:], k.ident_d[:])
        P.cp(k.identb[:], k.ident[:])
        for l in range(DEPTH):
            P.dma(k.normg[:, l, :], k.normg_d[l])

        for l in range(depth):
            phase_a(k, l)
            with scope(k):
                k.mix_r = P.sb("mix_r", [128, 2, L], BF16, blk=512)
                if "rwkv" in mixers:
                    rwkv_phase(k, l)
                else:
                    P.memset(k.mix_r[:].rearrange("p a b -> p (a b)"), 1.0)
                with scope(k):
                    k.mix_m = P.sb("mix_m", [128, 3, L], BF16, blk=512)
                    if "mla" in mixers:
                        mla_phase(k, l)
                    else:
                        P.memset(k.mix_m[:].rearrange("p a b -> p (a b)"), 1.0)
                    with scope(k):
                        k.mix_g = P.sb("mix_g", [128, 3, L], BF16, blk=512)
                        if "gdn" in mixers:
                            gdn_phase(k, l)
                        else:
                            P.memset(k.mix_g[:].rearrange("p a b -> p (a b)"), 1.0)
                        if k.dbg:
                            for nm, t_, n_ in (("g", k.mix_g, 3), ("m", k.mix_m, 3), ("r", k.mix_r, 2)):
                                dump(k, "mix_%s%d" % (nm, l), t_[:].rearrange("p a b -> p (a b)"), [128, n_ * L])
                        phase_z(k, l, last=(l == depth - 1))
        P.finish()
        print("ops:", {e: len(v) for e, v in P.ops.items()}, "sig:", P.stats_sig, "dma:", P.ndma_ops)
    return k


def mixc(k, c):
    if c < 3:
        return k.mix_g[:, c, :]
    if c < 6:
        return k.mix_m[:, c - 3, :]
    return k.mix_r[:, c - 6, :]


def dump(k, name, ap, shape=None):
    if not k.dbg:
        return
    P = k.P
    shape = list(ap.shape) if shape is None else shape
    d = k.nc.dram_tensor("dbg_" + name, shape, ap.dtype, kind="ExternalOutput").ap()
    P.dma(d[:] if len(shape) == 2 else d, ap, is_output=True)
    k.dbg_outs.append("dbg_" + name)


def scope(k):
    class _S:
        def __enter__(s):
            s.old = k.P.stack
            s.st = ExitStack()
            s.st.__enter__()
            k.P.stack = s.st
            return s

        def __exit__(s, *a):
            k.P.barrier()
            k.P.stack = s.old
            s.st.__exit__(*a)
            return False
    return _S()


def phase_a(k, l):
    P = k.P
    with scope(k):
        ssq = P.sb("a_ssq", [128, NT])
        rs = P.sb("a_rs", [128, NT])
        rstd = P.sb("a_rstd", [128, NT])
        junk = [P.sb("a_junk%d" % i, [128, D], BF16) for i in range(2)]
        xs = [P.sb("a_xs%d" % i, [128, D], BF16) for i in range(2)]
        xin = [P.sb("a_xin%d" % i, [128, D], F32) for i in range(3)]
        src = k.x_d if l == 0 else k.xscr

        def stage1(tt):
            b = tt % 2
            xt_ = xin[tt % 3]
            P.dma(xt_[:], src[tt * 128:(tt + 1) * 128, :])
            P.act(junk[b][:], xt_[:], AF.Square, accum_out=ssq[:, tt:tt + 1])
            P.act(rs[:, tt:tt + 1], ssq[:, tt:tt + 1], AF.Sqrt, bias=EPS, scale=1.0 / D)
            P.recip(rstd[:, tt:tt + 1], rs[:, tt:tt + 1])
            P.ts(xs[b][:], xt_[:], rstd[:, tt:tt + 1], ALU.mult)

        def stage2(tt):
            b = tt % 2
            pt = k.PB[b][:].bitcast(BF16)
            for dc in range(8):
                P.tr(pt[:, dc * 128:(dc + 1) * 128], xs[b][:, dc * 128:(dc + 1) * 128], k.identb[:])
            P.cp(k.hT[:, :, tt * 128:(tt + 1) * 128], pt[:].rearrange("p (a b) -> p a b", a=8),
                 eng=("act" if tt % 2 == 0 else "dve"))
        stage1(0)
        for tt in range(NT):
            if tt + 1 < NT:
                stage1(tt + 1)
            stage2(tt)


def proj(k, l, name, alt=False):
    P = k.P
    BK = k.PB if alt else k.PA
    ci = CH_IDX[name]
    b = k.wrr
    k.wrr ^= 1
    P.dma(k.wst[b][:], k.win_d[l, ci], eng="sp")
    gb = k.normg[:, l, :].unsqueeze(2).broadcast_to([128, 8, 128])
    P.tt(k.wbf[b][:], k.wst[b][:], gb, ALU.mult, eng="pool")
    for tb in range(4):
        for dc in range(8):
            P.mm(BK[tb][:, :], k.wbf[b][:, dc, :], k.hT[:, dc, tb * 512:(tb + 1) * 512], start=(dc == 0), stop=(dc == 7))
    return BK


NXB = 5
ZQ = "act"


def phase_z(k, l, last):
    P = k.P
    with scope(k):
        wst = P.sb("z_wst", [128, 8, 512], F32)
        wob = P.sb("z_wob", [128, 8, 1024], BF16, blk=512)
        sg = [P.sb("z_sg%d" % i, [128, L], BF16, blk=512) for i in range(2)]
        for nb in range(2):
            P.dma(wst[:], k.wout_d[l, :, :, nb * 512:(nb + 1) * 512])
            P.cp(wob[:, :, nb * 512:(nb + 1) * 512], wst[:], eng="act")
        for gc in range(8):
            pa = proj(k, l, "g%d" % gc, alt=(gc % 2 == 1))
            s = sg[gc % 2]
            for tb in range(4):
                P.act(s[:, tb * 512:(tb + 1) * 512], pa[tb][:, :], AF.Silu)
                mc = mixc(k, gc)[:, tb * 512:(tb + 1) * 512]
                P.tt(mc, mc, s[:, tb * 512:(tb + 1) * 512], ALU.mult)
        xin = [P.sb("z_xin%d" % i, [128, D], F32) for i in range(NXB)]
        src = k.x_d if l == 0 else k.xscr
        if last:
            ssq = P.sb("f_ssq", [128, NT])
            rs = P.sb("f_rs", [128, NT])
            rstd = P.sb("f_rstd", [128, NT])
            junk = [P.sb("f_junk%d" % i, [128, D], BF16) for i in range(2)]
            gf = P.sb("f_g", [128, D])
            ot = [P.sb("f_o%d" % i, [128, D]) for i in range(2)]
            P.dma(gf[:], k.fing_d[0:1, :].partition_broadcast(128))
        def za(tt):
            xt_ = xin[tt % NXB]
            P.dma(xt_[:], src[tt * 128:(tt + 1) * 128, :])
            for nb in range(2):
                ps = k.PB[(tt * 2 + nb) % 4]
                for kc in range(8):
                    P.mm(ps[:, :], mixc(k, kc)[:, tt * 128:(tt + 1) * 128], wob[:, kc, nb * 512:(nb + 1) * 512],
                         start=(kc == 0), stop=(kc == 7))
                xs = xt_[:, nb * 512:(nb + 1) * 512]
                P.tt(xs, xs, ps[:, :], ALU.add)
            if not last:
                P.dma(k.xscr[tt * 128:(tt + 1) * 128, :], xt_[:], eng=ZQ)
            else:
                b = tt % 2
                P.act(junk[b][:], xt_[:], AF.Square, accum_out=ssq[:, tt:tt + 1])
                P.act(rs[:, tt:tt + 1], ssq[:, tt:tt + 1], AF.Sqrt, bias=EPS, scale=1.0 / D)

        def zb(tt):
            if last:
                xt_ = xin[tt % NXB]
                b = tt % 2
                P.recip(rstd[:, tt:tt + 1], rs[:, tt:tt + 1])
                P.stt(ot[b][:], xt_[:], rstd[:, tt:tt + 1], gf[:], ALU.mult, ALU.mult)
                P.dma(k.out_d[tt * 128:(tt + 1) * 128, :], ot[b][:], eng=ZQ, is_output=True)
        za(0)
        for tt in range(NT):
            if tt + 1 < NT:
                za(tt + 1)
            zb(tt)


def host_inputs(inp, b):
    m = {}
    m["x"] = np.ascontiguousarray(inp["x"][b])
    m["win"] = host_win(inp["w_in"])
    m["normg"] = np.ascontiguousarray(inp["norm_g"].reshape(DEPTH, 8, 128).transpose(0, 2, 1))
    m["wout"] = np.ascontiguousarray(inp["w_out"].reshape(DEPTH, 8, 128, 1024).transpose(0, 2, 1, 3))
    m["fing"] = np.ascontiguousarray(inp["final_norm_g"].reshape(1, D))
    m["ident"] = np.eye(128, dtype=np.float32)
    host_mla(inp, b, m)
    host_gdn(inp, b, m)
    host_rwkv(inp, b, m)
    return m


TWO_PI = 2.0 * np.pi


def host_mla(inp, b, m):
    half = 16
    inv_freq = (10000.0 ** (-np.arange(half, dtype=np.float32) / half)).astype(np.float32)
    invf = np.zeros((32, 1), np.float32)
    invf[:, 0] = np.tile(inv_freq, 2) / np.float32(TWO_PI)
    m["invf"] = invf
    rm = np.zeros((32, 32), np.float32)
    for i in range(16):
        rm[i, i + 16] = -1.0
        rm[i + 16, i] = 1.0
    m["rmT"] = np.ascontiguousarray(rm.T)
    m["pos"] = np.ascontiguousarray(inp["positions"][b].reshape(1, L).astype(np.int32))
    wuq = inp["mla_w_uq"]
    o = np.zeros((DEPTH, 128, 2, 6, 128), np.float32)
    for h in range(6):
        nope = wuq[:, :, h * 96:h * 96 + 64]
        rope = wuq[:, :, h * 96 + 64:h * 96 + 96]
        o[:, :, 0, h, 64:128] = nope[:, 0:128]
        o[:, 0:64, 1, h, 64:128] = nope[:, 128:192]
        o[:, :, 0, h, 0:32] = rope[:, 0:128]
        o[:, 0:64, 1, h, 0:32] = rope[:, 128:192]
    m["wuq"] = o
    gq = np.zeros((DEPTH, 128, 2), np.float32)
    gq[:, :, 0] = inp["mla_q_norm_g"][:, 0:128]
    gq[:, 0:64, 1] = inp["mla_q_norm_g"][:, 128:192]
    m["gq"] = gq
    wukv = inp["mla_w_ukv"]
    wk = np.zeros((DEPTH, 128, 6, 128), np.float32)
    wv = np.zeros((DEPTH, 128, 6, 64), np.float32)
    for h in range(6):
        wk[:, :, h, 64:128] = wukv[:, :, h * 128:h * 128 + 64]
        wv[:, :, h, :] = wukv[:, :, h * 128 + 64:h * 128 + 128]
    m["wuk"] = wk
    m["wuv"] = wv
    m["gkv"] = np.ascontiguousarray(inp["mla_kv_norm_g"].reshape(DEPTH, 128, 1))


def mla_decl(k):
    nc = k.nc

    def din(name, shape, dt=F32):
        return nc.dram_tensor(name, list(shape), dt, kind="ExternalInput").ap()
    k.invf_d = din("invf", [32, 1])
    k.rmT_d = din("rmT", [32, 32])
    k.pos_d = din("pos", [1, L], I32)
    k.wuq_d = din("wuq", [DEPTH, 128, 2, 6, 128])
    k.gq_d = din("gq", [DEPTH, 128, 2])
    k.wuk_d = din("wuk", [DEPTH, 128, 6, 128])
    k.wuv_d = din("wuv", [DEPTH, 128, 6, 64])
    k.gkv_d = din("gkv", [DEPTH, 128, 1])


def latent_norm(k, l, names, nfeat, outs, ones):
    P = k.P
    sq = [P.sb("ln_sq%d" % i, [128, 512]) for i in range(2)]
    rq = [P.sb("ln_rq%d" % i, [128, 512]) for i in range(2)]
    n = len(names)
    for i, nm in enumerate(names):
        pa = proj(k, l, nm)
        for tb in range(4):
            s = sq[tb % 2]
            sk = ""
            if "a" not in sk:
                P.act(s[:], pa[tb][:, :], AF.Square)
            if "c" not in sk:
                P.cp(outs[i][:, tb * 512:(tb + 1) * 512], pa[tb][:, :], eng="dve")
            if "m" not in sk:
                P.mm(k.PB[tb][:, :], ones[:], s[:], start=(i == 0), stop=(i == n - 1))
    c2 = 9
    if c2 < 1:
        return
    for tb in range(4):
        r = rq[tb % 2]
        P.rpow(r[:], k.PB[tb][:, :], -0.5, scale=1.0 / nfeat, bias=EPS)
        for i in range(n):
            o = outs[i][:, tb * 512:(tb + 1) * 512]
            P.tt(o, o, r[:], ALU.mult)


def mla_phase(k, l):
    P = k.P
    SC = 96.0 ** -0.5
    with scope(k):
        cqn0 = P.sb("m_cqn0", [128, L], BF16, blk=512)
        cqn1 = P.sb("m_cqn1", [128, L], BF16, blk=512)
        ckvn = P.sb("m_ckvn", [128, L], BF16, blk=512)
        krope = P.sb("m_krope", [32, L], BF16, blk=512)
        cos2 = P.sb("m_cos2", [32, L], BF16, blk=512)
        sin2 = P.sb("m_sin2", [32, L], BF16, blk=512)
        wq = P.sb("m_wq", [128, 2, 6, 128], BF16)
        wk = P.sb("m_wk", [128, 6, 128], BF16)
        wv = P.sb("m_wv", [128, 6, 64], BF16)
        ones = P.sb("m_ones", [128, 128], F32)
        onesk = P.sb("m_onesk", [128, 128], F32)
        rmT = P.sb("m_rmT", [32, 32], F32)
        P.memset(ones[:], 1.0)
        P.memset(onesk[:], 1.0)
        P.memset(onesk[32:64, :], 0.0)
        P.dma(rmT[:], k.rmT_d[:])
        with scope(k):
            st = P.sb("m_st", [128, 2, 6, 128], F32)
            g = P.sb("m_g", [128, 4], F32)
            P.dma(st[:], k.wuq_d[l])
            P.dma(g[:, 0:2], k.gq_d[l])
            P.dma(g[:, 2:3], k.gkv_d[l])
            for kc in range(2):
                P.ts(wq[:, kc].rearrange("p a b -> p (a b)"), st[:, kc].rearrange("p a b -> p (a b)"),
                     g[:, kc:kc + 1], ALU.mult)
            st2 = P.sb("m_st2", [128, 6, 128], F32)
            P.dma(st2[:], k.wuk_d[l])
            P.ts(wk[:].rearrange("p a b -> p (a b)"), st2[:].rearrange("p a b -> p (a b)"), g[:, 2:3], ALU.mult)
            st3 = P.sb("m_st3", [128, 6, 64], F32)
            P.dma(st3[:], k.wuv_d[l])
            P.ts(wv[:].rearrange("p a b -> p (a b)"), st3[:].rearrange("p a b -> p (a b)"), g[:, 2:3], ALU.mult)
        if k.cut < 1:
            return
        with scope(k):
            latent_norm(k, l, ["cq0", "cq1"], 192, [cqn0, cqn1], ones)
            latent_norm(k, l, ["ckv"], 128, [ckvn], ones)
        if k.cut < 2:
            return
        with scope(k):
            invf = P.sb("m_invf", [32, 1], F32)
            P.dma(invf[:], k.invf_d[:])
            pa = proj(k, l, "kr")

            def rope_blk(tb):
                sl = slice(tb * 512, (tb + 1) * 512)
                posi = P.sb("m_posi%d" % tb, [32, 512], I32)
                y = P.sb("m_y%d" % tb, [32, 512], F32)
                yi = P.sb("m_yi%d" % tb, [32, 512], I32)
                fr = P.sb("m_fr%d" % tb, [32, 512], F32)
                kr = P.sb("m_kr%d" % tb, [32, 512], F32)
                t1 = P.sb("m_t1%d" % tb, [32, 512], F32)
                t2 = P.sb("m_t2%d" % tb, [32, 512], F32)
                P.dma(posi[:], k.pos_d[0:1, sl].partition_broadcast(32))
                P.cp(kr[:], pa[tb][0:32, :], eng="act")
                yield
                P.cp(y[:], posi[:])
                P.mm(k.PB[tb][0:32, :], rmT[:], kr[:])
                yield
                P.ts(y[:], y[:], invf[:, 0:1], ALU.mult)
                yield
                for off, dst in ((0.0, sin2), (0.25, cos2)):
                    if off != 0.0:
                        P.ts(y[:], y[:], off, ALU.add)
                        yield
                    P.cp(yi[:], y[:])
                    yield
                    P.cp(fr[:], yi[:])
                    yield
                    P.tt(fr[:], y[:], fr[:], ALU.subtract)
                    yield
                    P.act(dst[:, sl], fr[:], AF.Sin, scale=TWO_PI * (1.0 - 1e-6))
                    yield
                P.tt(t1[:], kr[:], cos2[:, sl], ALU.mult)
                P.tt(t2[:], k.PB[tb][0:32, :], sin2[:, sl], ALU.mult)
                yield
                P.tt(krope[:, sl], t1[:], t2[:], ALU.add)
                yield
            run_interleaved([rope_blk(tb) for tb in range(4)])
        if k.cut < 3:
            return
        kT = [P.sb("m_kT%d" % i, [128, L], BF16, blk=512) for i in range(2)]
        qT = [P.sb("m_qT%d" % i, [128, L], BF16, blk=512) for i in range(2)]
        Vh = [P.sb("m_V%d" % i, [128, NT, 96], BF16) for i in range(2)]
        pT = [P.sb("m_pT%d" % i, [128, 1024], BF16, blk=512) for i in range(4)]
        sq = [P.sb("m_sq%d" % i, [128, 512], BF16) for i in range(2)]
        qr = [P.sb("m_qr%d" % i, [32, 512], BF16) for i in range(2)]
        onesb = P.sb("m_onesb", [128, 128], BF16)
        oneskb = P.sb("m_oneskb", [128, 128], BF16)
        rmTb = P.sb("m_rmTb", [32, 32], BF16)
        P.cp(onesb[:], ones[:])
        P.cp(oneskb[:], onesk[:])
        P.cp(rmTb[:], rmT[:])
        t1 = P.sb("m_t1b", [32, 512], F32)
        t2 = P.sb("m_t2b", [32, 512], F32)
        mrow = P.sb("m_mrow", [64, 512], F32)
        km4 = P.sb("m_km4", [128, 4], F32)
        kmax2 = P.sb("m_kmax2", [128, 1], F32)
        rden = [P.sb("m_rden%d" % i, [64, 512], F32) for i in range(2)]
        for i in range(2):
            P.memset(kT[i][32:64, :], 0.0)
            P.memset(kT[i][32:33, :], 1.0)
            P.memset(qT[i][32:64, :], 0.0)
            P.memset(Vh[i][:, :, 64:96], 1.0)
        kmx = [P.sb("m_kmx%d" % i, [128, 1], F32) for i in range(2)]

        def prep(h):
            kt_, qt_, vh_ = kT[h % 2], qT[h % 2], Vh[h % 2]
            kmax2_ = kmx[h % 2]
            P.cp(kt_[0:32, :], krope[:], eng="pool")
            for tb in range(4):
                sl = slice(tb * 512, (tb + 1) * 512)
                s = sq[tb % 2]
                P.mm(k.PB[2][:, :], wk[:, h, :], ckvn[:, sl])
                yield
                P.cp(kt_[64:128, sl], k.PB[2][64:128, :], eng="dve")
                yield
                P.tt(s[:], kt_[:, sl], kt_[:, sl], ALU.mult, eng="pool")
                yield
                yield
                P.mm(k.PB[3][:, :], oneskb[:], s[:])
                yield
                P.red(km4[:, tb:tb + 1], k.PB[3][:, :], ALU.max)
                yield
            P.red(kmax2_[:], km4[:], ALU.max)
            for half in range(2):
                for j in range(8):
                    tt = half * 8 + j
                    P.mm(k.PB[2][:, j * 64:(j + 1) * 64], ckvn[:, tt * 128:(tt + 1) * 128], wv[:, h, :])
                yield
                P.cp(vh_[:, half * 8:(half + 1) * 8, 0:64], k.PB[2][:, :].rearrange("p (a b) -> p a b", a=8), eng="dve")
                yield
            for tb in range(4):
                sl = slice(tb * 512, (tb + 1) * 512)
                s = sq[tb % 2]
                q_ = qr[tb % 2]
                P.mm(k.PB[2][:, :], wq[:, 0, h, :], cqn0[:, sl], start=True, stop=False)
                P.mm(k.PB[2][:, :], wq[:, 1, h, :], cqn1[:, sl], start=False, stop=True)
                yield
                P.cp(qt_[64:128, sl], k.PB[2][64:128, :], eng="dve")
                P.cp(q_[:], k.PB[2][0:32, :], eng="dve")
                yield
                P.act(s[:], k.PB[2][:, :], AF.Square)
                yield
                P.mm(k.PB[3][:, :], onesb[:], s[:])
                yield
                P.act(mrow[32:33, :], k.PB[3][32:33, :], AF.Sqrt, scale=kmax2_[32:33, 0:1])
                yield
                P.ts(qt_[32:33, sl], mrow[32:33, :], -1.0, ALU.mult)
                P.mm(k.PB[2][0:32, :], rmTb[:], q_[:])
                P.tt(t1[:], q_[:], cos2[:, sl], ALU.mult, eng="pool")
                yield
                P.tt(t2[:], k.PB[2][0:32, :], sin2[:, sl], ALU.mult)
                yield
                P.tt(qt_[0:32, sl], t1[:], t2[:], ALU.add)
                yield

        def attn(h):
            kt_, qt_, vh_ = kT[h % 2], qT[h % 2], Vh[h % 2]
            pti = 0
            for qb in range(4):
                qs = slice(qb * 512, (qb + 1) * 512)
                O = k.PB[qb % 2]

                def s_pair(m_):
                    for j in range(2):
                        kt = 2 * m_ + j
                        P.mm(k.PAW[m_ % 2][:, j * 512:(j + 1) * 512], kt_[:, kt * 128:(kt + 1) * 128], qt_[:, qs])
                s_pair(0)
                s_pair(1)
                for m_ in range(NT // 2):
                    p_ = pT[pti % 4]
                    pti += 1
                    P.act(p_[:], k.PAW[m_ % 2][:, :], AF.Exp, scale=SC)
                    if m_ + 2 < NT // 2:
                        s_pair(m_ + 2)
                    for j in range(2):
                        kt = 2 * m_ + j
                        P.mm(O[0:96, :], vh_[:, kt, :], p_[:, j * 512:(j + 1) * 512], start=(kt == 0), stop=(kt == NT - 1))
                    yield
                rd = rden[qb % 2]
                P.rpow(rd[0:32, :], O[64:96, :], -1.0)
                P.cp(rd[32:64, :], rd[0:32, :], eng="dve")
                ob = (h % 2) * 64
                P.tt(mixc(k, 3 + h // 2)[ob:ob + 64, qs], O[0:64, :], rd[:], ALU.mult)
                yield

        for _ in prep(0):
            pass
        mode = "il"
        for h in range(6):
            gens = [attn(h)]
            if h + 1 < 6:
                if mode == "il":
                    gens.append(prep(h + 1))
                elif mode == "seq":
                    run_interleaved(gens)
                    gens = [prep(h + 1)]
                elif mode == "noprep":
                    pass
            if mode == "noprep" and h > 0:
                gens = [attn(0)]
            run_interleaved(gens)


NCK = L // 64
NEG = -30000.0


def host_gdn(inp, b, m):
    cw = inp["gdn_conv"]
    o = np.zeros((DEPTH, 128, 9, 5), np.float32)
    for part in range(3):
        for p in range(3):
            o[:, :, part * 3 + p, :] = cw[:, :, part * 384 + p * 128: part * 384 + (p + 1) * 128].transpose(0, 2, 1)
    m["gconv"] = o
    gb = np.zeros((DEPTH, 128, 2), np.float32)
    for d in range(2):
        gb[:, d * 32:d * 32 + 6, 0] = inp["gdn_dt_bias"][:, d, :]
        gb[:, d * 32:d * 32 + 6, 1] = inp["gdn_a_log"][:, d, :]
    m["ggb"] = gb
    m["gng"] = np.ascontiguousarray(np.tile(inp["gdn_norm_g"], (1, 2)).reshape(DEPTH, 128, 1))
    sel = np.zeros((64, 6, 128), np.float32)
    for d in range(2):
        for p in range(3):
            sel[d * 32 + 2 * p, d * 3 + p, 0:64] = 1.0
            sel[d * 32 + 2 * p + 1, d * 3 + p, 64:128] = 1.0
    m["gsel"] = sel
    j = np.arange(64)[:, None]
    i = np.arange(64)[None, :]
    nm = np.zeros((128, 2, 64), np.float32)
    nm[:, 0, :] = np.tile(np.where(i > j, 0.0, NEG), (2, 1))
    nm[:, 1, :] = np.tile(np.where(i < j, 0.0, NEG), (2, 1))
    m["gnegm"] = nm
    m["gid2"] = np.ascontiguousarray(np.tile(np.eye(64, dtype=np.float32), (2, 1)))


def gdn_decl(k):
    nc = k.nc

    def din(name, shape, dt=F32):
        return nc.dram_tensor(name, list(shape), dt, kind="ExternalInput").ap()
    k.gconv_d = din("gconv", [DEPTH, 128, 9, 5])
    k.ggb_d = din("ggb", [DEPTH, 128, 2])
    k.gng_d = din("gng", [DEPTH, 128, 1])
    k.gsel_d = din("gsel", [64, 6, 128])
    k.gnegm_d = din("gnegm", [128, 2, 64])
    k.gid2_d = din("gid2", [128, 64])


def bc3(ap2, n):
    return ap2.unsqueeze(2).broadcast_to([ap2.shape[0], ap2.shape[1], n])


def bcm(ap2, n):
    return ap2.unsqueeze(1).broadcast_to([ap2.shape[0], n, ap2.shape[1]])


HS = (slice(0, 64), slice(64, 128))


def v3(ps, n=8):
    return ps[:, 0:n * 64].rearrange("p (a b) -> p a b", a=n)


def mm2(P, ps, c, lhsT, rhs, **kw):
    for hs in HS:
        P.mm(ps[hs, c * 64:(c + 1) * 64], lhsT[hs], rhs[hs], **kw)


def tr2(P, ps, c, in_, ident):
    for hs in HS:
        P.mm(ps[hs, c * 64:(c + 1) * 64], in_[hs], ident[hs, hs])


def neumann2(k, Nn, Rm, tmp, bank, id2, n8=8):
    P = k.P
    idb = bcm(id2[:, :], n8)
    tA, tB, tC, tD = tmp
    pa_, pb_, pc_ = bank
    for c in range(n8):
        tr2(P, pa_, c, Nn[:, c, :], k.identb)
    P.cp(tA[:], v3(pa_, n8), eng="act")
    P.tt(Rm[:], Nn[:], idb, ALU.add)
    yield
    cur, curT = Nn, tA
    targets = [(tB, tC), (tD, tA)]
    for lvl in range(1, 7):
        nxt, nxtT = targets[(lvl - 1) % 2]
        if lvl >= 2:
            for c in range(n8):
                mm2(P, pc_, c, curT[:, c, :], Rm[:, c, :])
            P.tt(Rm[:], Rm[:], v3(pc_, n8), ALU.add)
        if lvl <= 5:
            for c in range(n8):
                mm2(P, pb_, c, cur[:, c, :], curT[:, c, :])
            P.cp(nxtT[:], v3(pb_, n8), eng="act")
            if lvl < 5:
                for c in range(n8):
                    mm2(P, pa_, c, curT[:, c, :], cur[:, c, :])
                P.cp(nxt[:], v3(pa_, n8), eng=("act" if lvl % 2 else "dve"))
        yield
        cur, curT = nxt, nxtT


def run_interleaved(gens, skew=0):
    gens = list(gens)
    for _ in range(skew):
        try:
            next(gens[0])
        except StopIteration:
            gens.pop(0)
            break
    while gens:
        for g in list(gens):
            try:
                next(g)
            except StopIteration:
                gens.remove(g)


def norm_pipe(k, n, src_fn, sqb, rnb, banks, bones, power, scale, bias, post_fn):
    P = k.P

    def pre(i):
        P.act(sqb[i % 2][:], src_fn(i), AF.Square)
        P.mm(banks[i % 2][:, :], bones[:], sqb[i % 2][:])

    def post(i):
        P.rpow(rnb[i % 2][:], banks[i % 2][:, :], power, scale=scale, bias=bias)
        post_fn(i, rnb[i % 2])
    pre(0)
    for i in range(n):
        if i + 1 < n:
            pre(i + 1)
        post(i)


def run_pipelined(chains, depth=2):
    active = []
    nxt = [0] * len(chains)

    def start(ci):
        if nxt[ci] < len(chains[ci]):
            active.append((ci, chains[ci][nxt[ci]](nxt[ci] % depth)))
            nxt[ci] += 1
    for ci in range(len(chains)):
        for _ in range(depth):
            start(ci)
    while active:
        for item in list(active):
            try:
                next(item[1])
            except StopIteration:
                active.remove(item)
                start(item[0])


def gdn_phase(k, l):
    P = k.P
    with scope(k):
        GC = P.sb("g_GC", [64, L], F32, blk=512)
        GP = [P.sb("g_GP%d" % p, [128, NCK, 4], F32) for p in range(3)]
        NBP = [P.sb("g_NBP%d" % p, [128, NCK, 2], F32) for p in range(3)]
        sel = P.sb("g_sel", [64, 6, 128], F32)
        negm = P.sb("g_negm", [128, 2, 64], F32)
        id2 = P.sb("g_id2", [128, 64], F32)
        cw = P.sb("g_cw", [128, 9, 5], F32)
        ng = P.sb("g_ng", [128, 1], F32)
        bones = P.sb("g_bones", [128, 128], F32)
        P.dma(sel[:], k.gsel_d[:])
        P.dma(negm[:], k.gnegm_d[:])
        P.dma(id2[:], k.gid2_d[:])
        P.dma(cw[:], k.gconv_d[l])
        P.dma(ng[:], k.gng_d[l])
        P.memset(bones[:], 0.0)
        P.memset(bones[0:64, 0:64], 1.0)
        P.memset(bones[64:128, 64:128], 1.0)
        bonesb = P.sb("g_bonesb", [128, 128], BF16)
        P.cp(bonesb[:], bones[:])
        with scope(k):
            GT = P.sb("g_GT", [128, L], F32, blk=512)
            m0 = P.sb("g_m0", [64, L], F32)
            gb = P.sb("g_gb", [128, 2], F32)
            negA = P.sb("g_negA", [128, 1], F32)
            P.dma(gb[:], k.ggb_d[l])
            P.act(negA[:], gb[:, 1:2], AF.Exp)
            P.ts(negA[:], negA[:], -1.0, ALU.mult)
            P.memset(m0[:], 1.0)
            P.memset(m0[:, 0:L:64], 0.0)
            pa = proj(k, l, "gab")
            for tb in range(4):
                sl = slice(tb * 512, (tb + 1) * 512)
                P.act(GT[0:64, sl], pa[tb][0:64, :], AF.Identity, bias=gb[0:64, 0:1])
                P.act(GT[64:128, sl], pa[tb][64:128, :], AF.Sigmoid)
            P.act(GC[:, :], GT[0:64, :], AF.Abs)
            P.act(GC[:, :], GC[:, :], AF.Exp, scale=-1.0)
            P.act(GC[:, :], GC[:, :], AF.Ln, bias=1.0)
            P.act(GT[0:64, :], GT[0:64, :], AF.Relu)
            P.tt(GT[0:64, :], GT[0:64, :], GC[:, :], ALU.add)
            P.ts(GT[0:64, :], GT[0:64, :], negA[0:64, 0:1], ALU.mult)
            P.scan(GC[:, :], m0[:, :], GT[0:64, :], 0.0, ALU.mult, ALU.add)
            gc3 = GC[32:64, :].rearrange("p (a b) -> p a b", b=64)
            P.tt(m0[32:64, :].rearrange("p (a b) -> p a b", b=64), bc3(GC[32:64, 63:L:64], 64), gc3, ALU.subtract)
            P.tt(GC[32:64, :], m0[32:64, :], GT[32:64, :], ALU.add)
            for grp in range(4):
                g8 = slice(grp * 8, (grp + 1) * 8)
                for c in range(8):
                    ck = grp * 8 + c
                    cs = slice(c * 64, (c + 1) * 64)
                    for hs in HS:
                        P.mm(k.PB[2 * (grp % 2)][hs, cs], GC[:, ck * 64:(ck + 1) * 64], k.ident[0:64, 0:64])
                        P.mm(k.PB[2 * (grp % 2) + 1][hs, cs], GT[64:128, ck * 64:(ck + 1) * 64], k.ident[64:128, 64:128])
                n_ = 0
                for p in range(3):
                    for hf, hs in enumerate(HS):
                        h = 2 * p + hf
                        for q, ps in ((0, k.PB[2 * (grp % 2)]), (1, k.PB[2 * (grp % 2) + 1])):
                            src = v3(ps)[hs, :, h:h + 33:32]
                            P.cp(GP[p][hs, g8, 2 * q:2 * q + 2], src, eng=("act" if q else "dve"))
            for p in range(3):
                P.ts(NBP[p][:], GP[p][:, :, 2:4], -1.0, ALU.mult)
        for p in range(3):
            with scope(k):
                Q = P.sb("g_Q", [128, L], BF16, blk=512)
                K_ = P.sb("g_K", [128, L], BF16, blk=512)
                Kt = P.sb("g_Kt", [128, NCK, 64], BF16, blk=512)
                Vt = P.sb("g_Vt", [128, NCK, 64], BF16, blk=512)
                O = P.sb("g_O", [128, L], F32, blk=512)
                P.memset(O[:], 0.0, eng="pool")
                with scope(k):
                    xps = [P.sb("g_xp%d" % i, [128, L + 4], BF16) for i in range(2)]
                    Dgs = [P.sb("g_Dg%d" % i, [128, 5, 128], BF16) for i in range(2)]
                    cvs = [P.sb("g_cv%d" % i, [128, L], F32, blk=512) for i in range(2)]
                    Vf = P.sb("g_Vf", [128, L], BF16, blk=512)
                    sq = P.sb("g_sq", [128, 512], BF16)
                    rn = P.sb("g_rn", [128, 512], F32)
                    sq2 = P.sb("g_sq2", [128, 512], BF16)
                    rn2 = P.sb("g_rn2", [128, 512], F32)
                    for xp in xps:
                        P.memset(xp[:, 0:2], 0.0)
                        P.memset(xp[:, L + 2:L + 4], 0.0)
                    for part, nm, dst in ((0, "gq", Q), (1, "gk", K_), (2, "gv", Vf)):
                        xp, Dg, cv = xps[part % 2], Dgs[part % 2], cvs[part % 2]
                        pa = proj(k, l, "%s%d" % (nm, p))
                        for tb in range(4):
                            P.cp(xp[:, 2 + tb * 512:2 + (tb + 1) * 512], pa[tb][:, :], eng=("act" if tb % 2 else "dve"))
                        wi = part * 3 + p
                        for j in range(5):
                            P.ts(Dg[:, j, :], k.identb[:], cw[:, wi, j:j + 1], ALU.mult, eng=("pool" if j % 2 else "dve"))
                        for tb in range(4):
                            for j in range(5):
                                P.mm(k.PB[tb][:, :], Dg[:, j, :], xp[:, j + tb * 512:j + (tb + 1) * 512],
                                     start=(j == 0), stop=(j == 4))
                        for tb in range(4):
                            sl = slice(tb * 512, (tb + 1) * 512)
                            P.act((dst if part == 2 else cv)[:, sl], k.PB[tb][:, :], AF.Silu)
                        if part < 2:
                            def fin(tb, r, dst=dst, cv=cv):
                                P.tt(dst[:, tb * 512:(tb + 1) * 512], cv[:, tb * 512:(tb + 1) * 512], r[:], ALU.mult)
                            norm_pipe(k, 4, lambda tb, cv=cv: cv[:, tb * 512:(tb + 1) * 512], (sq, sq2), (rn, rn2),
                                      (k.PA[2], k.PA[3]), bonesb, -0.5, 64.0 if part == 0 else 1.0,
                                      64e-6 if part == 0 else 1e-6, fin)
                    for src, dstt in ((K_, Kt), (Vf, Vt)):
                        for grp in range(4):
                            ps = k.PB[grp % 2]
                            for c in range(8):
                                ck = grp * 8 + c
                                tr2(P, ps, c, src[:, ck * 64:(ck + 1) * 64], k.identb)
                            P.cp(dstt[:, grp * 8:(grp + 1) * 8, :], v3(ps), eng=("act" if grp % 2 else "dve"))
                with scope(k):
                    T = dict(GC=GC, GP=GP[p], NBP=NBP[p], sel=sel, negm=negm, id2=id2, Q=Q, K=K_, Kt=Kt, Vt=Vt, O=O)
                    run_pipelined([gdn_chain(k, p, d, T) for d in range(2)], depth=1)
                with scope(k):
                    sqo = [P.sb("g_osq%d" % i, [128, 512], BF16) for i in range(2)]
                    rno = [P.sb("g_orn%d" % i, [128, 512], F32) for i in range(2)]

                    def fin_o(tb, r):
                        sl = slice(tb * 512, (tb + 1) * 512)
                        P.tt(r[:], O[:, sl], r[:], ALU.mult)
                        P.ts(mixc(k, p)[:, sl], r[:], ng[:, 0:1], ALU.mult)
                    norm_pipe(k, 4, lambda tb: O[:, tb * 512:(tb + 1) * 512], sqo, rno, (k.PB[2], k.PB[3]), bonesb,
                              -0.5, 1.0 / 64, EPS, fin_o)


def gdn_chain(k, p, d, T):
    P = k.P
    GC, GP, NBP, sel, negm, id2, Q, K_, Kt, Vt, O = (T[n] for n in ("GC", "GP", "NBP", "sel", "negm", "id2", "Q", "K", "Kt", "Vt", "O"))
    B = k.PA if d == 0 else k.PB
    tag = "g%d_" % d
    names = ("CB", "EI", "QG", "Rm", "U0", "WT", "BW", "KD", "GK", "AcT", "Sg", "Ug", "nA", "nB", "nC", "nD", "Nb", "PTb", "Sgb")
    f32n = ("CB", "EI", "AcT", "Sg")
    NSET = 1
    GG = [{n: P.sb(tag + "%d" % s_ + n, [128, 8, 64], F32 if n in f32n else BF16) for n in names} for s_ in range(NSET)]
    for s_ in range(NSET):
        GG[s_]["gend"] = P.sb(tag + "gend%d" % s_, [128, 8], F32)
        GG[s_]["kds"] = P.sb(tag + "kds%d" % s_, [128, 8], F32)
    gam = P.sb(tag + "gam", [128, NCK], F32)
    shared = {"scan": 0}
    Scar = P.sb(tag + "Scar", [128, 64], F32)
    e_ = 63 if d == 0 else 0
    idb = bcm(id2[:, :], 8)

    def f2(t):
        return t[:].rearrange("p a b -> p (a b)")
    P.memset(Scar[:], 0.0)
    P.act(gam[:], GP[:, :, d], AF.Exp)
    gorder = list(range(4)) if d == 0 else list(range(3, -1, -1))

    def group(gi, grp, G):
        Nb, PTb, Sgb, gend, kds = G["Nb"], G["PTb"], G["Sgb"], G["gend"], G["kds"]
        sl = slice(grp * 512, (grp + 1) * 512)
        g8 = slice(grp * 8, (grp + 1) * 8)
        cj = GP[:, g8, d]
        nb = NBP[:, g8, d]
        CB, EI, QG, Rm, U0, WT, BW, KD, GK, AcT, Sg, Ug = (G[n] for n in names[:12])
        P.mm(B[0][:, :], sel[:, d * 3 + p, :], GC[:, sl])
        P.cp(f2(CB), B[0][:, :], eng="act")
        P.act(f2(EI), f2(CB), AF.Exp)
        P.cp(gend[:], EI[:, :, e_], eng="pool")
        P.tt(f2(QG), f2(EI), Q[:, sl], ALU.mult)
        P.tt(kds[:], CB[:, :, e_], cj, ALU.subtract)
        P.act(kds[:], kds[:], AF.Exp)
        P.tt(GK[:], Kt[:, g8, :], bc3(gam[:, g8], 64), ALU.mult, eng="pool")
        P.tt(KD[:], Kt[:, g8, :], bc3(kds[:], 64), ALU.mult, eng="pool")
        P.tt(CB[:], CB[:], bc3(cj, 64), ALU.subtract)
        P.tt(CB[:], CB[:], bcm(negm[:, d, :], 8), ALU.add)
        P.act(f2(CB), f2(CB), AF.Exp)
        yield
        P.tt(EI[:], CB[:], idb, ALU.add)
        for c in range(8):
            cs = slice((grp * 8 + c) * 64, (grp * 8 + c + 1) * 64)
            mm2(P, B[0], c, K_[:, cs], Q[:, cs])
        P.tt(PTb[:], EI[:], v3(B[0]), ALU.mult)
        for c in range(8):
            cs = slice((grp * 8 + c) * 64, (grp * 8 + c + 1) * 64)
            mm2(P, B[1], c, K_[:, cs], K_[:, cs])
        P.tt(CB[:], CB[:], v3(B[1]), ALU.mult)
        P.tt(Nb[:], CB[:], bc3(nb, 64), ALU.mult)
        yield
        for _ in neumann2(k, Nb, Rm, (G["nA"], G["nB"], G["nC"], G["nD"]), (B[0], B[1], B[2]), id2):
            yield
        for c in range(8):
            mm2(P, B[2], c, Rm[:, c, :], GK[:, c, :])
        P.tt(BW[:], v3(B[2]), bc3(nb, 64), ALU.mult)
        for c in range(8):
            mm2(P, B[0], c, Rm[:, c, :], Vt[:, grp * 8 + c, :])
        P.tt(U0[:], v3(B[0]), bc3(GP[:, g8, 2 + d], 64), ALU.mult)
        for c in range(8):
            mm2(P, B[1], c, GK[:, c, :], Rm[:, c, :])
        P.cp(WT[:], v3(B[1]), eng="act")
        yield
        for c in range(8):
            mm2(P, B[3], c, BW[:, c, :], KD[:, c, :])
        P.tt(AcT[:], idb, bc3(gend[:], 64), ALU.mult, eng="pool")
        P.tt(AcT[:], AcT[:], v3(B[3]), ALU.add)
        yield
        while shared["scan"] != gi:
            yield
        corder = range(8) if d == 0 else range(7, -1, -1)
        prev = Scar[:]
        for n, c in enumerate(corder):
            P.cp(Sg[:, c, :], prev, eng="pool") if n == 0 else None
            ps = B[2 + n % 2]
            mm2(P, ps, 0, AcT[:, c, :], Sg[:, c, :], start=True, stop=False)
            mm2(P, ps, 0, KD[:, c, :], U0[:, c, :], start=False, stop=True)
            last = (n == 7)
            dst = Scar[:] if last else Sg[:, corder[n + 1], :]
            P.cp(dst, ps[:, 0:64], eng=("act" if n % 2 == 0 else "dve"))
            yield
        shared["scan"] = gi + 1
        P.cp(Sgb[:], Sg[:], eng="act")
        for c in range(8):
            mm2(P, B[0], c, WT[:, c, :], Sgb[:, c, :])
        P.tt(CB[:], v3(B[0]), bc3(nb, 64), ALU.mult)
        P.tt(Ug[:], CB[:], U0[:], ALU.add)
        yield
        for c in range(8):
            mm2(P, B[1], c, Sgb[:, c, :], QG[:, c, :], start=True, stop=False)
            mm2(P, B[1], c, Ug[:, c, :], PTb[:, c, :], start=False, stop=True)
        P.tt(O[:, sl], O[:, sl], B[1][:, :], ALU.add)
        yield
    return [(lambda slot, gi=gi, grp=grp: group(gi, grp, GG[slot])) for gi, grp in enumerate(gorder)]


RSKEW = 0
RW_EPS = 64e-5
DEC = float(np.exp(-0.5))
GS = 8
NG = NCK // GS


def host_rwkv(inp, b, m):
    mu = inp["rwkv_mu"]
    o = np.zeros((DEPTH, 128, 8, 2), np.float32)
    for part in range(3):
        for p in range(2):
            o[:, :, part * 2 + p, :] = mu[:, :, part * 256 + p * 128: part * 256 + (p + 1) * 128].transpose(0, 2, 1)
    o[:, 0:64, 6, :] = mu[:, :, 768:832].transpose(0, 2, 1)
    o[:, 0:64, 7, :] = mu[:, :, 832:896].transpose(0, 2, 1)
    m["rmu"] = o

    def pp(a):
        if a.ndim == 2:
            return np.ascontiguousarray(a.reshape(DEPTH, 2, 128).transpose(0, 2, 1))
        return np.ascontiguousarray(a.reshape(DEPTH, 2, 2, 128).transpose(0, 3, 1, 2))
    pv = np.zeros((DEPTH, 128, 7, 2), np.float32)
    pv[:, :, 0:2, :] = pp(inp["rwkv_w0"])
    pv[:, :, 2:4, :] = pp(inp["rwkv_a0"])
    pv[:, :, 4, :] = pp(inp["rwkv_k_k"])
    pv[:, :, 5, :] = pp(inp["rwkv_k_a"])
    pv[:, :, 6, :] = pp(inp["rwkv_r_k"].reshape(DEPTH, 256))
    m["rpv"] = pv
    ln = np.zeros((DEPTH, 128, 2, 2), np.float32)
    ln[:, :, 0, :] = pp(inp["rwkv_ln_g"])
    ln[:, :, 1, :] = pp(inp["rwkv_ln_b"])
    m["rln"] = ln
    m["rw2"] = np.ascontiguousarray(inp["rwkv_w2"].transpose(0, 2, 1, 3))
    m["ra2"] = np.ascontiguousarray(inp["rwkv_a2"].transpose(0, 2, 1, 3))
    s_ = np.arange(64)[:, None]
    t_ = np.arange(64)[None, :]
    msk = np.zeros((128, 2, 4, 64), np.float32)
    msk[:, 0, 0, :] = np.tile((t_ > s_), (2, 1))
    msk[:, 0, 1, :] = np.tile((t_ >= s_), (2, 1))
    msk[:, 1, 0, :] = np.tile((t_ < s_), (2, 1))
    msk[:, 1, 1, :] = np.tile((t_ <= s_), (2, 1))
    msk[:, :, 2:4, :] = -msk[:, :, 0:2, :]
    m["rmsk"] = msk


def rwkv_decl(k):
    nc = k.nc

    def din(name, shape, dt=F32):
        return nc.dram_tensor(name, list(shape), dt, kind="ExternalInput").ap()
    k.rmu_d = din("rmu", [DEPTH, 128, 8, 2])
    k.rpv_d = din("rpv", [DEPTH, 128, 7, 2])
    k.rln_d = din("rln", [DEPTH, 128, 2, 2])
    k.rw2_d = din("rw2", [DEPTH, 64, 2, 256])
    k.ra2_d = din("ra2", [DEPTH, 64, 2, 256])
    k.rmsk_d = din("rmsk", [128, 2, 4, 64])


def rwkv_phase(k, l):
    P = k.P
    with scope(k):
        mu = P.sb("r_mu", [128, 8, 3], F32)
        pv = P.sb("r_pv", [128, 7, 2], F32)
        omka = P.sb("r_omka", [128, 2], F32)
        hrk = P.sb("r_hrk", [128, 2], F32)
        ln = P.sb("r_ln", [128, 2, 2], F32)
        w2 = P.sb("r_w2", [64, 2, 256], BF16)
        a2 = P.sb("r_a2", [64, 2, 256], BF16)
        msk = P.sb("r_msk", [128, 2, 4, 64], F32)
        id2 = P.sb("r_id2", [128, 64], F32)
        bones = P.sb("r_bones", [128, 128], F32)
        m0 = P.sb("r_m0", [128, GS * 64], F32)
        twd = P.sb("r_twd", [64, L], BF16, blk=512)
        adx = P.sb("r_adx", [64, L], BF16, blk=512)
        sh32 = P.sb("r_sh32", [128, L], F32, blk=512)
        xp = P.sb("r_xp", [128, L + 2], F32)
        P.dma(mu[:, :, 0:2], k.rmu_d[l])
        P.dma(pv[:], k.rpv_d[l])
        P.dma(ln[:], k.rln_d[l])
        with scope(k):
            w2f = P.sb("r_w2f", [64, 2, 256], F32)
            a2f = P.sb("r_a2f", [64, 2, 256], F32)
            P.dma(w2f[:], k.rw2_d[l])
            P.dma(a2f[:], k.ra2_d[l])
            P.cp(w2[:], w2f[:], eng="act")
            P.cp(a2[:], a2f[:], eng="act")
        P.dma(msk[:], k.rmsk_d[:])
        P.dma(id2[:], k.gid2_d[:])
        P.memset(bones[:], 0.0)
        P.memset(bones[0:64, 0:64], 1.0)
        P.memset(bones[64:128, 64:128], 1.0)
        P.memset(m0[:], 1.0)
        P.memset(m0[:, 0:GS * 64:64], 0.0)
        P.memset(xp[:, 0:1], 0.0)
        P.memset(xp[:, L + 1:L + 2], 0.0)
        P.tt(mu[:, :, 2], mu[:, :, 0], mu[:, :, 1], ALU.add)
        P.ts(mu[:, :, 2], mu[:, :, 2], -1.0, ALU.mult, 1.0, ALU.add)
        P.ts(omka[:], pv[:, 5, :], -1.0, ALU.mult, 1.0, ALU.add)
        P.ts(hrk[:], pv[:, 6, :], 0.5, ALU.mult)

        def shifted(name, ci, dst, np_=128, fn=None):
            pa = proj(k, l, name, alt=(ci % 2 == 1))
            for tb in range(4):
                P.cp(xp[0:np_, 1 + tb * 512:1 + (tb + 1) * 512], pa[tb][0:np_, :], eng=("act" if tb % 2 else "dve"))
            t_ = sh32[0:np_, :]
            P.ts(t_, xp[0:np_, 1:L + 1], mu[0:np_, ci, 2:3], ALU.mult)
            P.stt(t_, xp[0:np_, 0:L], mu[0:np_, ci, 0:1], t_, ALU.mult, ALU.add)
            if fn is None:
                P.stt(dst[:], xp[0:np_, 2:L + 2], mu[0:np_, ci, 1:2], t_, ALU.mult, ALU.add)
            else:
                P.stt(t_, xp[0:np_, 2:L + 2], mu[0:np_, ci, 1:2], t_, ALU.mult, ALU.add)
                P.act(dst[:], t_, fn)

        shifted("rwd", 6, twd, 64, AF.Tanh)
        shifted("rad", 7, adx, 64)
        R_ = P.sb("r_R", [128, L], BF16, blk=512)
        KX = P.sb("r_KX", [128, L], BF16, blk=512)
        V_ = P.sb("r_V", [128, L], BF16, blk=512)
        KK = P.sb("r_KK", [128, L], BF16, blk=512)
        Vt = P.sb("r_Vt", [128, NCK, 64], BF16, blk=512)
        KS = P.sb("r_KS", [128, L], F32, blk=512)
        Y = xp[:, 1:L + 1]
        sq = P.sb("r_sq", [128, 512], BF16)
        bonesb = P.sb("r_bonesb", [128, 128], BF16)
        P.cp(bonesb[:], bones[:])
        rn = P.sb("r_rn", [128, 512], F32)
        CH = [rwkv_tiles(k, e) for e in range(2)]

        class _V:
            def __init__(s_, t):
                s_.t = t

            def __getitem__(s_, key):
                return s_.t[:].rearrange("p a b -> p (a b)")[key]
        sq2, rn2 = _V(CH[0]["kap"]), _V(CH[0]["a"])
        for p in range(2):
            shifted("rr%d" % p, 0 + p, R_)
            shifted("rk%d" % p, 2 + p, KX)
            shifted("rv%d" % p, 4 + p, V_)
            P.ts(sh32[:], KX[:], pv[:, 4, p:p + 1], ALU.mult)

            def fin_k(tb, r):
                P.tt(KK[:, tb * 512:(tb + 1) * 512], sh32[:, tb * 512:(tb + 1) * 512], r[:], ALU.mult)
            norm_pipe(k, 4, lambda tb: sh32[:, tb * 512:(tb + 1) * 512], (sq, sq2), (rn, rn2), (k.PB[2], k.PB[3]),
                      bonesb, -0.5, 1.0, 1e-6, fin_k)
            for grp in range(4):
                ps = k.PB[grp % 2]
                for c in range(8):
                    ck = grp * 8 + c
                    tr2(P, ps, c, V_[:, ck * 64:(ck + 1) * 64], k.identb)
                P.cp(Vt[:, grp * 8:(grp + 1) * 8, :], v3(ps), eng=("act" if grp % 2 else "dve"))
            P.memset(xp[:, 1:L + 1], 0.0, eng="pool")
            P.memset(KS[:], 0.0, eng="pool")
            T = dict(pv=pv, omka=omka, w2=w2, a2=a2, msk=msk, id2=id2, m0=m0, twd=twd, adx=adx,
                     R=R_, KX=KX, KK=KK, Vt=Vt, KS=KS, Y=Y)
            run_interleaved([rwkv_chain(k, p, e, T, CH[e]) for e in range(2)], skew=RSKEW)
            for tb in range(4):
                sl = slice(tb * 512, (tb + 1) * 512)
                P.mm(k.PB[tb % 2][:, :], bones[:], Y[:, sl])
                P.stt(Y[:, sl], k.PB[tb % 2][:, :], -1.0 / 64, Y[:, sl], ALU.mult, ALU.add)

            def fin_y(tb, r):
                sl = slice(tb * 512, (tb + 1) * 512)
                P.tt(Y[:, sl], Y[:, sl], r[:], ALU.mult)
                P.ts(Y[:, sl], Y[:, sl], ln[:, 0, p:p + 1], ALU.mult, ln[:, 1, p:p + 1], ALU.add)
            norm_pipe(k, 4, lambda tb: Y[:, tb * 512:(tb + 1) * 512], (sq, sq2), (rn, rn2), (k.PB[2], k.PB[3]),
                      bonesb, -0.5, 1.0 / 64, RW_EPS, fin_y)
            for tb in range(4):
                sl = slice(tb * 512, (tb + 1) * 512)
                s_ = (sq, sq2)[tb % 2]
                r_ = (rn, rn2)[tb % 2]
                P.tt(s_[:], R_[:, sl], KS[:, sl], ALU.mult)
                P.ts(s_[:], s_[:], hrk[:, p:p + 1], ALU.mult, eng="pool")
                P.mm(k.PB[tb % 2][:, :], bonesb[:], s_[:])
                P.tt(r_[:], k.PB[tb % 2][:, :], V_[:, sl], ALU.mult)
                P.tt(mixc(k, 6 + p)[:, sl], Y[:, sl], r_[:], ALU.add)


RW_F32 = ("lw", "a", "km", "b", "cl", "e1", "e2", "dend", "AcT", "Tg")
RW_BF16 = ("kap", "rt", "kt_", "bt_", "ke", "be", "kapT", "keT", "nbeT", "N", "Akv", "Brk", "nBrb", "Rm",
           "nA", "nB", "nC", "nD", "X0", "P0", "WkT", "Wk", "Tgb", "Pg")


def rwkv_tiles(k, e):
    P = k.P
    G = {n: P.sb("r%d_%s" % (e, n), [128, GS, 64], F32) for n in RW_F32}
    for n in RW_BF16:
        G[n] = P.sb("r%d_%s" % (e, n), [128, GS, 64], BF16)
    G["gC"] = P.sb("r%d_gC" % e, [128, GS], F32)
    G["Tcar"] = P.sb("r%d_Tcar" % e, [128, 64], F32)
    return G


def rwkv_chain(k, p, e, T, G):
    P = k.P
    pv, omka, w2, a2, msk, id2, m0, twd, adx, R_, KX, KK, Vt, KS, Y = (T[n] for n in (
        "pv", "omka", "w2", "a2", "msk", "id2", "m0", "twd", "adx", "R", "KX", "KK", "Vt", "KS", "Y"))
    B = k.PA if e == 0 else k.PB
    W = GS * 64
    e_ = 63 if e == 0 else 0
    idb = bcm(id2[:, :], GS)
    gC, Tcar = G["gC"], G["Tcar"]

    def f2(t):
        return t[:].rearrange("p a b -> p (a b)")

    def w3(ps):
        return v3(ps, GS)
    P.memset(Tcar[:], 0.0)
    gorder = range(NG) if e == 0 else range(NG - 1, -1, -1)
    pc = slice(p * 128, (p + 1) * 128)
    for grp in gorder:
        sl = slice(grp * W, (grp + 1) * W)
        c0 = grp * GS
        P.mm(B[0][:, 0:W], w2[:, e, pc], twd[:, sl])
        P.mm(B[1][:, 0:W], a2[:, e, pc], adx[:, sl])
        P.act(f2(G["lw"]), B[0][:, 0:W], AF.Sigmoid, bias=pv[:, 0 + e, p:p + 1])
        P.act(f2(G["a"]), B[1][:, 0:W], AF.Sigmoid, bias=pv[:, 2 + e, p:p + 1])
        P.ts(f2(G["km"]), f2(G["a"]), pv[:, 5, p:p + 1], ALU.mult, omka[:, p:p + 1], ALU.add)
        P.tt(f2(G["km"]), f2(G["km"]), KX[:, sl], ALU.mult)
        P.tt(f2(G["b"]), f2(G["a"]), KK[:, sl], ALU.mult)
        P.tt(KS[:, sl], KS[:, sl], f2(G["km"]), ALU.add, eng="pool")
        P.scan(f2(G["cl"]), m0[:], f2(G["lw"]), 0.0, ALU.mult, ALU.add)
        if e == 1:
            P.tt(G["e1"][:], bc3(G["cl"][:, :, 63], 64), G["cl"][:], ALU.subtract)
            P.tt(G["cl"][:], G["e1"][:], G["lw"][:], ALU.add)
        yield
        P.act(G["e1"][:], G["cl"][:], AF.Exp, scale=-DEC)
        P.act(G["e2"][:], G["cl"][:], AF.Exp, scale=DEC)
        P.tt(f2(G["rt"]), f2(G["e1"]), R_[:, sl], ALU.mult)
        P.tt(G["kt_"][:], G["e2"][:], G["km"][:], ALU.mult)
        P.tt(G["bt_"][:], G["e2"][:], G["b"][:], ALU.mult)
        P.tt(G["dend"][:], G["cl"][:], G["lw"][:], ALU.subtract)
        P.act(G["dend"][:], G["dend"][:], AF.Exp, scale=-DEC)
        P.tt(f2(G["kap"]), f2(G["dend"]), KK[:, sl], ALU.mult)
        P.cp(gC[:], G["e1"][:, :, e_], eng="pool")
        P.tt(G["dend"][:], bc3(G["cl"][:, :, e_], 64), G["cl"][:], ALU.subtract)
        P.act(G["dend"][:], G["dend"][:], AF.Exp, scale=-DEC)
        P.tt(G["ke"][:], G["dend"][:], G["km"][:], ALU.mult)
        P.tt(G["be"][:], G["dend"][:], G["b"][:], ALU.mult, eng="pool")
        yield
        for src, dst, sc in ((G["kap"], G["kapT"], 1.0), (G["ke"], G["keT"], 1.0), (G["be"], G["nbeT"], -1.0)):
            ps = B[0] if sc == 1.0 and src is G["kap"] else (B[1] if sc == 1.0 else B[2])
            for c in range(GS):
                tr2(P, ps, c, src[:, c, :], k.identb)
            if sc == 1.0:
                P.cp(dst[:], w3(ps), eng="act")
            else:
                P.ts(dst[:], w3(ps), -1.0, ALU.mult)
        yield
        for c in range(GS):
            mm2(P, B[0], c, G["bt_"][:, c, :], G["kap"][:, c, :])
            mm2(P, B[1], c, G["kt_"][:, c, :], G["kap"][:, c, :])
            mm2(P, B[2], c, G["kt_"][:, c, :], G["rt"][:, c, :])
            mm2(P, B[3], c, G["bt_"][:, c, :], G["rt"][:, c, :])
        ms = bcm(msk[:, e, 0, :], GS)
        mi = bcm(msk[:, e, 1, :], GS)
        nms = bcm(msk[:, e, 2, :], GS)
        nmi = bcm(msk[:, e, 3, :], GS)
        P.tt(G["N"][:], w3(B[0]), nms, ALU.mult)
        P.tt(G["Akv"][:], w3(B[1]), ms, ALU.mult)
        P.tt(G["Brk"][:], w3(B[2]), mi, ALU.mult)
        P.tt(G["nBrb"][:], w3(B[3]), nmi, ALU.mult)
        yield
        for _ in neumann2(k, G["N"], G["Rm"], (G["nA"], G["nB"], G["nC"], G["nD"]), (B[0], B[1], B[2]), id2, GS):
            yield
        for c in range(GS):
            mm2(P, B[0], c, G["Akv"][:, c, :], Vt[:, c0 + c, :])
        P.cp(G["X0"][:], w3(B[0]), eng="act")
        yield
        for c in range(GS):
            mm2(P, B[0], c, G["Rm"][:, c, :], G["X0"][:, c, :])
            mm2(P, B[1], c, G["kapT"][:, c, :], G["Rm"][:, c, :])
            mm2(P, B[2], c, G["Rm"][:, c, :], G["kapT"][:, c, :])
        P.cp(G["P0"][:], w3(B[0]), eng="act")
        P.cp(G["WkT"][:], w3(B[1]), eng="dve")
        P.cp(G["Wk"][:], w3(B[2]), eng="act")
        yield
        for c in range(GS):
            mm2(P, B[3], c, G["Wk"][:, c, :], G["nbeT"][:, c, :])
        P.tt(G["AcT"][:], idb, bc3(gC[:], 64), ALU.mult, eng="pool")
        P.tt(G["AcT"][:], G["AcT"][:], w3(B[3]), ALU.add)
        yield
        corder = list(range(GS)) if e == 0 else list(range(GS - 1, -1, -1))
        Tg = G["Tg"]
        for n, c in enumerate(corder):
            if n == 0:
                P.cp(Tg[:, c, :], Tcar[:], eng="pool")
            ps = B[2 + n % 2]
            mm2(P, ps, 0, G["AcT"][:, c, :], Tg[:, c, :], start=True, stop=False)
            mm2(P, ps, 0, G["keT"][:, c, :], Vt[:, c0 + c, :], start=False, stop=False)
            mm2(P, ps, 0, G["nbeT"][:, c, :], G["P0"][:, c, :], start=False, stop=True)
            dst = Tcar[:] if n == GS - 1 else Tg[:, corder[n + 1], :]
            P.cp(dst, ps[:, 0:64], eng=("act" if n % 2 == 0 else "dve"))
            yield
        P.cp(G["Tgb"][:], Tg[:], eng="act")
        for c in range(GS):
            mm2(P, B[0], c, G["WkT"][:, c, :], G["Tgb"][:, c, :])
        P.tt(G["Pg"][:], w3(B[0]), G["P0"][:], ALU.add)
        yield
        for c in range(GS):
            mm2(P, B[1], c, Vt[:, c0 + c, :], G["Brk"][:, c, :], start=True, stop=False)
            mm2(P, B[1], c, G["Tgb"][:, c, :], G["rt"][:, c, :], start=False, stop=False)
            mm2(P, B[1], c, G["Pg"][:, c, :], G["nBrb"][:, c, :], start=False, stop=True)
        P.tt(Y[:, sl], Y[:, sl], B[1][:, 0:W], ALU.add)
        yield


_CACHE = {}


def kernel(**inputs):
    inp = {k_: np.asarray(v) for k_, v in inputs.items()}
    if "k" not in _CACHE:
        _CACHE["k"] = build()
    k = _CACHE["k"]
    B = inp["x"].shape[0]
    base = host_inputs(inp, 0)
    in_maps = []
    for b in range(B):
        m = dict(base)
        m["x"] = np.ascontiguousarray(inp["x"][b], dtype=np.float32)
        m["pos"] = np.ascontiguousarray(inp["positions"][b].reshape(1, L).astype(np.int32))
        in_maps.append(m)
    res = run_bass_kernel_spmd(k.nc, in_maps, core_ids=list(range(B)))
    return np.stack([np.asarray(r["out"], dtype=np.float32) for r in res.results], axis=0)
```

```python
import numpy as np
import concourse.bass as bass
import concourse.mybir as mybir
from concourse.bass_utils import run_bass_kernel_spmd
from contextlib import ExitStack

F32 = mybir.dt.float32
BF16 = mybir.dt.bfloat16
I32 = mybir.dt.int32
AF = mybir.ActivationFunctionType
ALU = mybir.AluOpType
AX = mybir.AxisListType
DTSIZE = {F32: 4, BF16: 2, I32: 4}


class _Op:
    __slots__ = ("eng", "emit", "deps", "idx", "needed", "sigval", "dsem", "dval", "isdma")

    def __init__(self, eng, emit, isdma=False):
        self.eng = eng
        self.emit = emit
        self.deps = []
        self.idx = -1
        self.needed = False
        self.sigval = 0
        self.dsem = None
        self.dval = 0
        self.isdma = isdma


class _Blk:
    __slots__ = ("w", "r")

    def __init__(self):
        self.w = None
        self.r = {}


class Prog:
    ENGS = ("pe", "dve", "act", "pool", "sp")
    NDMA = 48
    NHW = 32

    def __init__(self, nc, stack):
        self.nc = nc
        self.stack = stack
        self.ops = {e: [] for e in self.ENGS}
        self.track = {}
        self.seen = {e: {} for e in self.ENGS}
        self.seen_dma = {e: set() for e in self.ENGS}
        self.dma_last = [None] * self.NDMA
        self.dma_uses = [0] * self.NDMA
        self.dma_rr = 0
        self.dma_rr_sw = 0
        self.ndma_ops = 0
        self.untracked = set()
        self.out_dmas = []
        self.dma_pending = []
        self.last_compute = {}

    def sb(self, name, shape, dtype=F32, blk=None):
        self.uid = getattr(self, "uid", 0) + 1
        name = "s%d_%s" % (self.uid, name)
        t = self.stack.enter_context(self.nc.sbuf_tensor(name, list(shape), dtype))
        self._register(name, shape, dtype, blk)
        return t

    def ps(self, name, shape=(128, 512), dtype=F32, blk=None):
        self.uid = getattr(self, "uid", 0) + 1
        name = "p%d_%s" % (self.uid, name)
        t = self.stack.enter_context(self.nc.psum_tensor(name, list(shape), dtype))
        self._register(name, shape, dtype, blk)
        return t

    def _register(self, name, shape, dtype, blk):
        row = int(np.prod(shape[1:])) * DTSIZE[dtype]
        bb = row if blk is None else blk * DTSIZE[dtype]
        nb = (row + bb - 1) // bb
        self.track[name] = (bb, row, [_Blk() for _ in range(nb)])

    def dram_track(self, name, total_bytes, blk_bytes):
        nb = (total_bytes + blk_bytes - 1) // blk_bytes
        self.track[name] = (blk_bytes, -1, [_Blk() for _ in range(nb)])

    def _blocks(self, ap):
        name = ap.tensor.name
        if name not in self.track:
            return ()
        bb, row, blks = self.track[name]
        if len(blks) == 1:
            return blks
        ds = DTSIZE[ap.dtype]
        pat = ap.ap
        if row < 0:
            lo = hi = ap.offset
            for step, cnt in pat:
                ext = step * (cnt - 1)
                if ext < 0:
                    lo += ext
                else:
                    hi += ext
            return blks[(lo * ds) // bb:(hi * ds) // bb + 1]
        rowel = row // ds
        foff = ap.offset % rowel
        lo = hi = foff
        for step, cnt in pat[1:]:
            ext = step * (cnt - 1)
            if ext < 0:
                lo += ext
            else:
                hi += ext
        b0 = (lo * ds) // bb
        b1 = (hi * ds) // bb
        return blks[b0:b1 + 1]

    def _dep(self, x, y):
        if y is None or y is x:
            return
        e = x.eng
        if y.isdma:
            if id(y) in self.seen_dma[e]:
                return
            self.seen_dma[e].add(id(y))
            x.deps.append(y)
            return
        if y.eng == "pe" and e == "pe":
            return
        if y.idx <= self.seen[e].get(y.eng, -1):
            return
        self.seen[e][y.eng] = y.idx
        y.needed = True
        x.deps.append(y)

    def add(self, eng, emit, reads=(), writes=(), isdma=False):
        x = _Op(eng, emit, isdma)
        x.idx = len(self.ops[eng])
        rb = []
        for ap in reads:
            if ap is None or isinstance(ap, (int, float)):
                continue
            rb.extend(self._blocks(ap))
        wb = []
        for ap in writes:
            wb.extend(self._blocks(ap))
        for ap in reads:
            if ap is None or isinstance(ap, (int, float)) or not ap.tensor.name.startswith("p"):
                continue
            for b in self._blocks(ap):
                for key, y in b.r.items():
                    if key != eng:
                        self._dep(x, y)
        for b in rb:
            self._dep(x, b.w)
        for b in wb:
            self._dep(x, b.w)
            for y in b.r.values():
                self._dep(x, y)
        if isdma:
            if eng == "pool":
                s = self.NHW + self.dma_rr_sw
                self.dma_rr_sw = (self.dma_rr_sw + 1) % (self.NDMA - self.NHW)
            else:
                s = self.dma_rr
                self.dma_rr = (self.dma_rr + 1) % self.NHW
            self._dep(x, self.dma_last[s])
            self.dma_last[s] = x
            self.dma_uses[s] += 1
            x.dsem = s
            x.dval = 16 * self.dma_uses[s]
            self.ndma_ops += 1
        key = id(x) if isdma else eng
        for b in rb:
            b.r[key] = x
        for b in wb:
            b.w = x
            b.r = {}
        self.ops[eng].append(x)
        if isdma:
            self.dma_pending.append(x)
        else:
            self.last_compute[eng] = x
        return x

    def barrier(self):
        lasts = dict(self.last_compute)
        pend = list(self.dma_pending)
        self.dma_pending = []
        for e in self.ENGS:
            b = _Op(e, None)
            b.idx = len(self.ops[e])
            for e2, y in lasts.items():
                if e2 == e and e == "pe":
                    continue
                self._dep(b, y)
            for y in pend:
                self._dep(b, y)
            self.ops[e].append(b)

    def mm(self, out, lhsT, rhs, start=True, stop=True):
        return self.add("pe", lambda e: e.matmul(out, lhsT, rhs, start=start, stop=stop),
                        reads=(lhsT, rhs), writes=(out,))

    def tr(self, out, in_, ident):
        return self.add("pe", lambda e: e.transpose(out, in_, ident), reads=(in_, ident), writes=(out,))

    def tt(self, out, in0, in1, op, eng="dve"):
        return self.add(eng, lambda e: e.tensor_tensor(out, in0, in1, op), reads=(in0, in1), writes=(out,))

    def ts(self, out, in0, s1, op0, s2=None, op1=None, eng="dve", accum_out=None):
        kw = {}
        if eng == "pool" and op1 is None:
            if op0 == ALU.mult:
                s2, op1 = 0.0, ALU.add
            elif op0 == ALU.add:
                s2, op1 = 1.0, ALU.mult
        if op1 is not None:
            kw["op1"] = op1
        if accum_out is not None:
            kw["accum_out"] = accum_out
        w = (out,) if accum_out is None else (out, accum_out)
        return self.add(eng, lambda e: e.tensor_scalar(out, in0, s1, s2, op0, **kw),
                        reads=(in0, s1, s2), writes=w)

    def stt(self, out, in0, scalar, in1, op0, op1, accum_out=None):
        kw = {}
        if accum_out is not None:
            kw["accum_out"] = accum_out
        w = (out,) if accum_out is None else (out, accum_out)
        return self.add("dve", lambda e: e.scalar_tensor_tensor(out, in0, scalar, in1, op0, op1, **kw),
                        reads=(in0, scalar, in1), writes=w)

    def cp(self, out, in_, eng="dve"):
        if eng == "act":
            return self.add("act", lambda e: e.copy(out, in_), reads=(in_,), writes=(out,))
        return self.add(eng, lambda e: e.tensor_copy(out, in_), reads=(in_,), writes=(out,))

    def act(self, out, in_, func, bias=0.0, scale=1.0, accum_out=None):
        kw = {}
        if accum_out is not None:
            kw["accum_out"] = accum_out
        w = (out,) if accum_out is None else (out, accum_out)
        return self.add("act", lambda e: e.activation(out, in_, func, bias=bias, scale=scale, **kw),
                        reads=(in_, bias, scale), writes=w)

    def red(self, out, in_, op, axis=AX.X, eng="dve"):
        return self.add(eng, lambda e: e.tensor_reduce(out, in_, axis, op), reads=(in_,), writes=(out,))

    def recip(self, out, in_):
        return self.add("dve", lambda e: e.reciprocal(out, in_), reads=(in_,), writes=(out,))

    def rpow(self, out, in_, power, scale=1.0, bias=0.0):
        self.act(out, in_, AF.Ln, bias=bias, scale=scale)
        return self.act(out, out, AF.Exp, scale=power)

    def memset(self, ap, val, eng="dve"):
        return self.add(eng, lambda e: e.memset(ap, val), writes=(ap,))

    def scan(self, out, d0, d1, init, op0, op1):
        return self.add("dve", lambda e: e.tensor_tensor_scan(out, d0, d1, init, op0, op1),
                        reads=(d0, d1, init), writes=(out,))

    def dma(self, out, in_, eng="sp", is_output=False):
        x = self.add(eng, lambda e: e.dma_start(out=out, in_=in_), reads=(in_,), writes=(out,), isdma=True)
        if is_output:
            self.out_dmas.append(x)
        return x

    def finish(self):
        nc = self.nc
        fin = _Op("sp", None)
        fin.idx = len(self.ops["sp"])
        for y in self.out_dmas:
            self._dep(fin, y)
        self.ops["sp"].append(fin)
        sems = {}
        for e in ("pe", "dve", "act", "pool"):
            sems[e] = self.stack.enter_context(nc.semaphore("s_" + e))
        dsems = [self.stack.enter_context(nc.semaphore("d%d" % i)) for i in range(self.NDMA)]
        for e in ("pe", "dve", "act", "pool"):
            c = 0
            for x in self.ops[e]:
                if x.isdma:
                    continue
                if x.needed:
                    c += 1
                    x.sigval = c
            self.stats_sig = getattr(self, "stats_sig", {})
            self.stats_sig[e] = c
        ops = self.ops

        def replay(e, engobj):
            for x in ops[e]:
                for y in x.deps:
                    if y.isdma:
                        engobj.wait_ge(dsems[y.dsem], y.dval)
                    else:
                        engobj.wait_ge(sems[y.eng], y.sigval)
                if x.emit is None:
                    continue
                ins = x.emit(engobj)
                if x.isdma:
                    ins.then_inc(dsems[x.dsem], 16)
                elif x.needed:
                    ins.then_inc(sems[e], 1)

        with nc.Block() as block:
            @block.tensor
            def _(eng):
                replay("pe", eng)

            @block.vector
            def _(eng):
                replay("dve", eng)

            @block.scalar
            def _(eng):
                replay("act", eng)

            @block.gpsimd
            def _(eng):
                replay("pool", eng)

            @block.sync
            def _(eng):
                replay("sp", eng)


L = 2048
D = 1024
NT = L // 128
DEPTH = 2
N_IN = 3448
EPS = 1e-6

OFF = dict(gate=0, gdn_q=1024, gdn_k=1408, gdn_v=1792, gdn_a=2176, gdn_b=2188, mla_cq=2200, mla_ckv=2392,
           mla_kr=2520, rw_r=2552, rw_k=2808, rw_v=3064, rw_wd=3320, rw_ad=3384)


def chunk_table():
    ch = []
    for h in range(3):
        ch.append(("gq%d" % h, [(0, OFF["gdn_q"] + h * 128, 128)]))
        ch.append(("gk%d" % h, [(0, OFF["gdn_k"] + h * 128, 128)]))
        ch.append(("gv%d" % h, [(0, OFF["gdn_v"] + h * 128, 128)]))
    ch.append(("gab", [(0, OFF["gdn_a"], 6), (32, OFF["gdn_a"] + 6, 6), (64, OFF["gdn_b"], 6), (96, OFF["gdn_b"] + 6, 6)]))
    ch.append(("cq0", [(0, OFF["mla_cq"], 128)]))
    ch.append(("cq1", [(0, OFF["mla_cq"] + 128, 64)]))
    ch.append(("ckv", [(0, OFF["mla_ckv"], 128)]))
    ch.append(("kr", [(0, OFF["mla_kr"], 32)]))
    for i in range(2):
        ch.append(("rr%d" % i, [(0, OFF["rw_r"] + i * 128, 128)]))
        ch.append(("rk%d" % i, [(0, OFF["rw_k"] + i * 128, 128)]))
        ch.append(("rv%d" % i, [(0, OFF["rw_v"] + i * 128, 128)]))
    ch.append(("rwd", [(0, OFF["rw_wd"], 64)]))
    ch.append(("rad", [(0, OFF["rw_ad"], 64)]))
    for i in range(8):
        ch.append(("g%d" % i, [(0, OFF["gate"] + i * 128, 128)]))
    return ch


CHUNKS = chunk_table()
CH_IDX = {n: i for i, (n, _) in enumerate(CHUNKS)}
NCH = len(CHUNKS)


def host_win(w_in):
    out = np.zeros((DEPTH, NCH, 128, 8, 128), np.float32)
    for ci, (_, parts) in enumerate(CHUNKS):
        for dst, src, w in parts:
            blk = w_in[:, :, src:src + w].reshape(DEPTH, 8, 128, w)
            out[:, ci, :, :, dst:dst + w] = blk.transpose(0, 2, 1, 3)
    return out


class K:
    pass


def build(depth=DEPTH, mixers=("gdn", "mla", "rwkv"), dbg=False):
    nc = bass.Bass("TRN2", target_bir_lowering=False)
    k = K()
    k.nc = nc
    k.dbg = dbg
    k.dbg_outs = []
    k.cut = 99

    def din(name, shape, dt=F32):
        return nc.dram_tensor(name, list(shape), dt, kind="ExternalInput").ap()

    k.x_d = din("x", [L, D])
    k.win_d = din("win", [DEPTH, NCH, 128, 8, 128])
    k.normg_d = din("normg", [DEPTH, 128, 8])
    k.wout_d = din("wout", [DEPTH, 128, 8, 1024])
    k.fing_d = din("fing", [1, D])
    k.ident_d = din("ident", [128, 128])
    k.out_d = nc.dram_tensor("out", [L, D], F32, kind="ExternalOutput").ap()
    mla_decl(k)
    gdn_decl(k)
    rwkv_decl(k)

    with ExitStack() as st:
        P = Prog(nc, st)
        k.P = P
        k.xscr = nc.dram_tensor("xscr", [L, D], F32, kind="Internal").ap()
        P.dram_track("xscr", L * D * 4, 128 * D * 4)
        k.hT = P.sb("hT", [128, 8, L], BF16, blk=512)
        k.ident = P.sb("ident", [128, 128], F32)
        k.identb = P.sb("identb", [128, 128], BF16)
        k.normg = P.sb("normg", [128, DEPTH, 8], F32)
        k.wst = [P.sb("wst%d" % i, [128, 8, 128], F32) for i in range(2)]
        k.wbf = [P.sb("wbf%d" % i, [128, 8, 128], BF16) for i in range(2)]
        k.wrr = 0
        k.PAW = [P.ps("paw%d" % i, [128, 1024], F32, blk=512) for i in range(2)]
        k.PA = [k.PAW[i // 2][:, (i % 2) * 512:(i % 2 + 1) * 512] for i in range(4)]
        k.PB = [P.ps("pb%d" % i, [128, 512], F32) for i in range(4)]

        P.dma(k.ident[:], k.ident_d[:])
        P.cp(k.identb[:], k.ident[:])
        for l in range(DEPTH):
            P.dma(k.normg[:, l, :], k.normg_d[l])

        for l in range(depth):
            phase_a(k, l)
            with scope(k):
                k.mix_r = P.sb("mix_r", [128, 2, L], BF16, blk=512)
                if "rwkv" in mixers:
                    rwkv_phase(k, l)
                else:
                    P.memset(k.mix_r[:].rearrange("p a b -> p (a b)"), 1.0)
                with scope(k):
                    k.mix_m = P.sb("mix_m", [128, 3, L], BF16, blk=512)
                    if "mla" in mixers:
                        mla_phase(k, l)
                    else:
                        P.memset(k.mix_m[:].rearrange("p a b -> p (a b)"), 1.0)
                    with scope(k):
                        k.mix_g = P.sb("mix_g", [128, 3, L], BF16, blk=512)
                        if "gdn" in mixers:
                            gdn_phase(k, l)
                        else:
                            P.memset(k.mix_g[:].rearrange("p a b -> p (a b)"), 1.0)
                        if k.dbg:
                            for nm, t_, n_ in (("g", k.mix_g, 3), ("m", k.mix_m, 3), ("r", k.mix_r, 2)):
                                dump(k, "mix_%s%d" % (nm, l), t_[:].rearrange("p a b -> p (a b)"), [128, n_ * L])
                        phase_z(k, l, last=(l == depth - 1))
        P.finish()
        print("ops:", {e: len(v) for e, v in P.ops.items()}, "sig:", P.stats_sig, "dma:", P.ndma_ops)
    return k


def mixc(k, c):
    if c < 3:
        return k.mix_g[:, c, :]
    if c < 6:
        return k.mix_m[:, c - 3, :]
    return k.mix_r[:, c - 6, :]


def dump(k, name, ap, shape=None):
    if not k.dbg:
        return
    P = k.P
    shape = list(ap.shape) if shape is None else shape
    d = k.nc.dram_tensor("dbg_" + name, shape, ap.dtype, kind="ExternalOutput").ap()
    P.dma(d[:] if len(shape) == 2 else d, ap, is_output=True)
    k.dbg_outs.append("dbg_" + name)


def scope(k):
    class _S:
        def __enter__(s):
            s.old = k.P.stack
            s.st = ExitStack()
            s.st.__enter__()
            k.P.stack = s.st
            return s

        def __exit__(s, *a):
            k.P.barrier()
            k.P.stack = s.old
            s.st.__exit__(*a)
            return False
    return _S()


def phase_a(k, l):
    P = k.P
    with scope(k):
        ssq = P.sb("a_ssq", [128, NT])
        rs = P.sb("a_rs", [128, NT])
        rstd = P.sb("a_rstd", [128, NT])
        junk = [P.sb("a_junk%d" % i, [128, D], BF16) for i in range(2)]
        xs = [P.sb("a_xs%d" % i, [128, D], BF16) for i in range(2)]
        xin = [P.sb("a_xin%d" % i, [128, D], F32) for i in range(3)]
        src = k.x_d if l == 0 else k.xscr

        def stage1(tt):
            b = tt % 2
            xt_ = xin[tt % 3]
            P.dma(xt_[:], src[tt * 128:(tt + 1) * 128, :])
            P.act(junk[b][:], xt_[:], AF.Square, accum_out=ssq[:, tt:tt + 1])
            P.act(rs[:, tt:tt + 1], ssq[:, tt:tt + 1], AF.Sqrt, bias=EPS, scale=1.0 / D)
            P.recip(rstd[:, tt:tt + 1], rs[:, tt:tt + 1])
            P.ts(xs[b][:], xt_[:], rstd[:, tt:tt + 1], ALU.mult)

        def stage2(tt):
            b = tt % 2
            pt = k.PB[b][:].bitcast(BF16)
            for dc in range(8):
                P.tr(pt[:, dc * 128:(dc + 1) * 128], xs[b][:, dc * 128:(dc + 1) * 128], k.identb[:])
            P.cp(k.hT[:, :, tt * 128:(tt + 1) * 128], pt[:].rearrange("p (a b) -> p a b", a=8),
                 eng=("act" if tt % 2 == 0 else "dve"))
        stage1(0)
        for tt in range(NT):
            if tt + 1 < NT:
                stage1(tt + 1)
            stage2(tt)


def proj(k, l, name, alt=False):
    P = k.P
    BK = k.PB if alt else k.PA
    ci = CH_IDX[name]
    b = k.wrr
    k.wrr ^= 1
    P.dma(k.wst[b][:], k.win_d[l, ci], eng="sp")
    gb = k.normg[:, l, :].unsqueeze(2).broadcast_to([128, 8, 128])
    P.tt(k.wbf[b][:], k.wst[b][:], gb, ALU.mult, eng="pool")
    for tb in range(4):
        for dc in range(8):
            P.mm(BK[tb][:, :], k.wbf[b][:, dc, :], k.hT[:, dc, tb * 512:(tb + 1) * 512], start=(dc == 0), stop=(dc == 7))
    return BK


NXB = 5
ZQ = "act"


def phase_z(k, l, last):
    P = k.P
    with scope(k):
        wst = P.sb("z_wst", [128, 8, 512], F32)
        wob = P.sb("z_wob", [128, 8, 1024], BF16, blk=512)
        sg = [P.sb("z_sg%d" % i, [128, L], BF16, blk=512) for i in range(2)]
        for nb in range(2):
            P.dma(wst[:], k.wout_d[l, :, :, nb * 512:(nb + 1) * 512])
            P.cp(wob[:, :, nb * 512:(nb + 1) * 512], wst[:], eng="act")
        for gc in range(8):
            pa = proj(k, l, "g%d" % gc, alt=(gc % 2 == 1))
            s = sg[gc % 2]
            for tb in range(4):
                P.act(s[:, tb * 512:(tb + 1) * 512], pa[tb][:, :], AF.Silu)
                mc = mixc(k, gc)[:, tb * 512:(tb + 1) * 512]
                P.tt(mc, mc, s[:, tb * 512:(tb + 1) * 512], ALU.mult)
        xin = [P.sb("z_xin%d" % i, [128, D], F32) for i in range(NXB)]
        src = k.x_d if l == 0 else k.xscr
        if last:
            ssq = P.sb("f_ssq", [128, NT])
            rs = P.sb("f_rs", [128, NT])
            rstd = P.sb("f_rstd", [128, NT])
            junk = [P.sb("f_junk%d" % i, [128, D], BF16) for i in range(2)]
            gf = P.sb("f_g", [128, D])
            ot = [P.sb("f_o%d" % i, [128, D]) for i in range(2)]
            P.dma(gf[:], k.fing_d[0:1, :].partition_broadcast(128))
        def za(tt):
            xt_ = xin[tt % NXB]
            P.dma(xt_[:], src[tt * 128:(tt + 1) * 128, :])
            for nb in range(2):
                ps = k.PB[(tt * 2 + nb) % 4]
                for kc in range(8):
                    P.mm(ps[:, :], mixc(k, kc)[:, tt * 128:(tt + 1) * 128], wob[:, kc, nb * 512:(nb + 1) * 512],
                         start=(kc == 0), stop=(kc == 7))
                xs = xt_[:, nb * 512:(nb + 1) * 512]
                P.tt(xs, xs, ps[:, :], ALU.add)
            if not last:
                P.dma(k.xscr[tt * 128:(tt + 1) * 128, :], xt_[:], eng=ZQ)
            else:
                b = tt % 2
                P.act(junk[b][:], xt_[:], AF.Square, accum_out=ssq[:, tt:tt + 1])
                P.act(rs[:, tt:tt + 1], ssq[:, tt:tt + 1], AF.Sqrt, bias=EPS, scale=1.0 / D)

        def zb(tt):
            if last:
                xt_ = xin[tt % NXB]
                b = tt % 2
                P.recip(rstd[:, tt:tt + 1], rs[:, tt:tt + 1])
                P.stt(ot[b][:], xt_[:], rstd[:, tt:tt + 1], gf[:], ALU.mult, ALU.mult)
                P.dma(k.out_d[tt * 128:(tt + 1) * 128, :], ot[b][:], eng=ZQ, is_output=True)
        za(0)
        for tt in range(NT):
            if tt + 1 < NT:
                za(tt + 1)
            zb(tt)


def host_inputs(inp, b):
    m = {}
    m["x"] = np.ascontiguousarray(inp["x"][b])
    m["win"] = host_win(inp["w_in"])
    m["normg"] = np.ascontiguousarray(inp["norm_g"].reshape(DEPTH, 8, 128).transpose(0, 2, 1))
    m["wout"] = np.ascontiguousarray(inp["w_out"].reshape(DEPTH, 8, 128, 1024).transpose(0, 2, 1, 3))
    m["fing"] = np.ascontiguousarray(inp["final_norm_g"].reshape(1, D))
    m["ident"] = np.eye(128, dtype=np.float32)
    host_mla(inp, b, m)
    host_gdn(inp, b, m)
    host_rwkv(inp, b, m)
    return m


TWO_PI = 2.0 * np.pi


def host_mla(inp, b, m):
    half = 16
    inv_freq = (10000.0 ** (-np.arange(half, dtype=np.float32) / half)).astype(np.float32)
    invf = np.zeros((32, 1), np.float32)
    invf[:, 0] = np.tile(inv_freq, 2) / np.float32(TWO_PI)
    m["invf"] = invf
    rm = np.zeros((32, 32), np.float32)
    for i in range(16):
        rm[i, i + 16] = -1.0
        rm[i + 16, i] = 1.0
    m["rmT"] = np.ascontiguousarray(rm.T)
    m["pos"] = np.ascontiguousarray(inp["positions"][b].reshape(1, L).astype(np.int32))
    wuq = inp["mla_w_uq"]
    o = np.zeros((DEPTH, 128, 2, 6, 128), np.float32)
    for h in range(6):
        nope = wuq[:, :, h * 96:h * 96 + 64]
        rope = wuq[:, :, h * 96 + 64:h * 96 + 96]
        o[:, :, 0, h, 64:128] = nope[:, 0:128]
        o[:, 0:64, 1, h, 64:128] = nope[:, 128:192]
        o[:, :, 0, h, 0:32] = rope[:, 0:128]
        o[:, 0:64, 1, h, 0:32] = rope[:, 128:192]
    m["wuq"] = o
    gq = np.zeros((DEPTH, 128, 2), np.float32)
    gq[:, :, 0] = inp["mla_q_norm_g"][:, 0:128]
    gq[:, 0:64, 1] = inp["mla_q_norm_g"][:, 128:192]
    m["gq"] = gq
    wukv = inp["mla_w_ukv"]
    wk = np.zeros((DEPTH, 128, 6, 128), np.float32)
    wv = np.zeros((DEPTH, 128, 6, 64), np.float32)
    for h in range(6):
        wk[:, :, h, 64:128] = wukv[:, :, h * 128:h * 128 + 64]
        wv[:, :, h, :] = wukv[:, :, h * 128 + 64:h * 128 + 128]
    m["wuk"] = wk
    m["wuv"] = wv
    m["gkv"] = np.ascontiguousarray(inp["mla_kv_norm_g"].reshape(DEPTH, 128, 1))


def mla_decl(k):
    nc = k.nc

    def din(name, shape, dt=F32):
        return nc.dram_tensor(name, list(shape), dt, kind="ExternalInput").ap()
    k.invf_d = din("invf", [32, 1])
    k.rmT_d = din("rmT", [32, 32])
    k.pos_d = din("pos", [1, L], I32)
    k.wuq_d = din("wuq", [DEPTH, 128, 2, 6, 128])
    k.gq_d = din("gq", [DEPTH, 128, 2])
    k.wuk_d = din("wuk", [DEPTH, 128, 6, 128])
    k.wuv_d = din("wuv", [DEPTH, 128, 6, 64])
    k.gkv_d = din("gkv", [DEPTH, 128, 1])


def latent_norm(k, l, names, nfeat, outs, ones):
    P = k.P
    sq = [P.sb("ln_sq%d" % i, [128, 512]) for i in range(2)]
    rq = [P.sb("ln_rq%d" % i, [128, 512]) for i in range(2)]
    n = len(names)
    for i, nm in enumerate(names):
        pa = proj(k, l, nm)
        for tb in range(4):
            s = sq[tb % 2]
            sk = ""
            if "a" not in sk:
                P.act(s[:], pa[tb][:, :], AF.Square)
            if "c" not in sk:
                P.cp(outs[i][:, tb * 512:(tb + 1) * 512], pa[tb][:, :], eng="dve")
            if "m" not in sk:
                P.mm(k.PB[tb][:, :], ones[:], s[:], start=(i == 0), stop=(i == n - 1))
    c2 = 9
    if c2 < 1:
        return
    for tb in range(4):
        r = rq[tb % 2]
        P.rpow(r[:], k.PB[tb][:, :], -0.5, scale=1.0 / nfeat, bias=EPS)
        for i in range(n):
            o = outs[i][:, tb * 512:(tb + 1) * 512]
            P.tt(o, o, r[:], ALU.mult)


def mla_phase(k, l):
    P = k.P
    SC = 96.0 ** -0.5
    with scope(k):
        cqn0 = P.sb("m_cqn0", [128, L], BF16, blk=512)
        cqn1 = P.sb("m_cqn1", [128, L], BF16, blk=512)
        ckvn = P.sb("m_ckvn", [128, L], BF16, blk=512)
        krope = P.sb("m_krope", [32, L], BF16, blk=512)
        cos2 = P.sb("m_cos2", [32, L], BF16, blk=512)
        sin2 = P.sb("m_sin2", [32, L], BF16, blk=512)
        wq = P.sb("m_wq", [128, 2, 6, 128], BF16)
        wk = P.sb("m_wk", [128, 6, 128], BF16)
        wv = P.sb("m_wv", [128, 6, 64], BF16)
        ones = P.sb("m_ones", [128, 128], F32)
        onesk = P.sb("m_onesk", [128, 128], F32)
        rmT = P.sb("m_rmT", [32, 32], F32)
        P.memset(ones[:], 1.0)
        P.memset(onesk[:], 1.0)
        P.memset(onesk[32:64, :], 0.0)
        P.dma(rmT[:], k.rmT_d[:])
        with scope(k):
            st = P.sb("m_st", [128, 2, 6, 128], F32)
            g = P.sb("m_g", [128, 4], F32)
            P.dma(st[:], k.wuq_d[l])
            P.dma(g[:, 0:2], k.gq_d[l])
            P.dma(g[:, 2:3], k.gkv_d[l])
            for kc in range(2):
                P.ts(wq[:, kc].rearrange("p a b -> p (a b)"), st[:, kc].rearrange("p a b -> p (a b)"),
                     g[:, kc:kc + 1], ALU.mult)
            st2 = P.sb("m_st2", [128, 6, 128], F32)
            P.dma(st2[:], k.wuk_d[l])
            P.ts(wk[:].rearrange("p a b -> p (a b)"), st2[:].rearrange("p a b -> p (a b)"), g[:, 2:3], ALU.mult)
            st3 = P.sb("m_st3", [128, 6, 64], F32)
            P.dma(st3[:], k.wuv_d[l])
            P.ts(wv[:].rearrange("p a b -> p (a b)"), st3[:].rearrange("p a b -> p (a b)"), g[:, 2:3], ALU.mult)
        if k.cut < 1:
            return
        with scope(k):
            latent_norm(k, l, ["cq0", "cq1"], 192, [cqn0, cqn1], ones)
            latent_norm(k, l, ["ckv"], 128, [ckvn], ones)
        if k.cut < 2:
            return
        with scope(k):
            invf = P.sb("m_invf", [32, 1], F32)
            P.dma(invf[:], k.invf_d[:])
            pa = proj(k, l, "kr")

            def rope_blk(tb):
                sl = slice(tb * 512, (tb + 1) * 512)
                posi = P.sb("m_posi%d" % tb, [32, 512], I32)
                y = P.sb("m_y%d" % tb, [32, 512], F32)
                yi = P.sb("m_yi%d" % tb, [32, 512], I32)
                fr = P.sb("m_fr%d" % tb, [32, 512], F32)
                kr = P.sb("m_kr%d" % tb, [32, 512], F32)
                t1 = P.sb("m_t1%d" % tb, [32, 512], F32)
                t2 = P.sb("m_t2%d" % tb, [32, 512], F32)
                P.dma(posi[:], k.pos_d[0:1, sl].partition_broadcast(32))
                P.cp(kr[:], pa[tb][0:32, :], eng="act")
                yield
                P.cp(y[:], posi[:])
                P.mm(k.PB[tb][0:32, :], rmT[:], kr[:])
                yield
                P.ts(y[:], y[:], invf[:, 0:1], ALU.mult)
                yield
                for off, dst in ((0.0, sin2), (0.25, cos2)):
                    if off != 0.0:
                        P.ts(y[:], y[:], off, ALU.add)
                        yield
                    P.cp(yi[:], y[:])
                    yield
                    P.cp(fr[:], yi[:])
                    yield
                    P.tt(fr[:], y[:], fr[:], ALU.subtract)
                    yield
                    P.act(dst[:, sl], fr[:], AF.Sin, scale=TWO_PI * (1.0 - 1e-6))
                    yield
                P.tt(t1[:], kr[:], cos2[:, sl], ALU.mult)
                P.tt(t2[:], k.PB[tb][0:32, :], sin2[:, sl], ALU.mult)
                yield
                P.tt(krope[:, sl], t1[:], t2[:], ALU.add)
                yield
            run_interleaved([rope_blk(tb) for tb in range(4)])
        if k.cut < 3:
            return
        kT = [P.sb("m_kT%d" % i, [128, L], BF16, blk=512) for i in range(2)]
        qT = [P.sb("m_qT%d" % i, [128, L], BF16, blk=512) for i in range(2)]
        Vh = [P.sb("m_V%d" % i, [128, NT, 96], BF16) for i in range(2)]
        pT = [P.sb("m_pT%d" % i, [128, 1024], BF16, blk=512) for i in range(4)]
        sq = [P.sb("m_sq%d" % i, [128, 512], BF16) for i in range(2)]
        qr = [P.sb("m_qr%d" % i, [32, 512], BF16) for i in range(2)]
        onesb = P.sb("m_onesb", [128, 128], BF16)
        oneskb = P.sb("m_oneskb", [128, 128], BF16)
        rmTb = P.sb("m_rmTb", [32, 32], BF16)
        P.cp(onesb[:], ones[:])
        P.cp(oneskb[:], onesk[:])
        P.cp(rmTb[:], rmT[:])
        t1 = P.sb("m_t1b", [32, 512], F32)
        t2 = P.sb("m_t2b", [32, 512], F32)
        mrow = P.sb("m_mrow", [64, 512], F32)
        km4 = P.sb("m_km4", [128, 4], F32)
        kmax2 = P.sb("m_kmax2", [128, 1], F32)
        rden = [P.sb("m_rden%d" % i, [64, 512], F32) for i in range(2)]
        for i in range(2):
            P.memset(kT[i][32:64, :], 0.0)
            P.memset(kT[i][32:33, :], 1.0)
            P.memset(qT[i][32:64, :], 0.0)
            P.memset(Vh[i][:, :, 64:96], 1.0)
        kmx = [P.sb("m_kmx%d" % i, [128, 1], F32) for i in range(2)]

        def prep(h):
            kt_, qt_, vh_ = kT[h % 2], qT[h % 2], Vh[h % 2]
            kmax2_ = kmx[h % 2]
            P.cp(kt_[0:32, :], krope[:], eng="pool")
            for tb in range(4):
                sl = slice(tb * 512, (tb + 1) * 512)
                s = sq[tb % 2]
                P.mm(k.PB[2][:, :], wk[:, h, :], ckvn[:, sl])
                yield
                P.cp(kt_[64:128, sl], k.PB[2][64:128, :], eng="dve")
                yield
                P.tt(s[:], kt_[:, sl], kt_[:, sl], ALU.mult, eng="pool")
                yield
                yield
                P.mm(k.PB[3][:, :], oneskb[:], s[:])
                yield
                P.red(km4[:, tb:tb + 1], k.PB[3][:, :], ALU.max)
                yield
            P.red(kmax2_[:], km4[:], ALU.max)
            for half in range(2):
                for j in range(8):
                    tt = half * 8 + j
                    P.mm(k.PB[2][:, j * 64:(j + 1) * 64], ckvn[:, tt * 128:(tt + 1) * 128], wv[:, h, :])
                yield
                P.cp(vh_[:, half * 8:(half + 1) * 8, 0:64], k.PB[2][:, :].rearrange("p (a b) -> p a b", a=8), eng="dve")
                yield
            for tb in range(4):
                sl = slice(tb * 512, (tb + 1) * 512)
                s = sq[tb % 2]
                q_ = qr[tb % 2]
                P.mm(k.PB[2][:, :], wq[:, 0, h, :], cqn0[:, sl], start=True, stop=False)
                P.mm(k.PB[2][:, :], wq[:, 1, h, :], cqn1[:, sl], start=False, stop=True)
                yield
                P.cp(qt_[64:128, sl], k.PB[2][64:128, :], eng="dve")
                P.cp(q_[:], k.PB[2][0:32, :], eng="dve")
                yield
                P.act(s[:], k.PB[2][:, :], AF.Square)
                yield
                P.mm(k.PB[3][:, :], onesb[:], s[:])
                yield
                P.act(mrow[32:33, :], k.PB[3][32:33, :], AF.Sqrt, scale=kmax2_[32:33, 0:1])
                yield
                P.ts(qt_[32:33, sl], mrow[32:33, :], -1.0, ALU.mult)
                P.mm(k.PB[2][0:32, :], rmTb[:], q_[:])
                P.tt(t1[:], q_[:], cos2[:, sl], ALU.mult, eng="pool")
                yield
                P.tt(t2[:], k.PB[2][0:32, :], sin2[:, sl], ALU.mult)
                yield
                P.tt(qt_[0:32, sl], t1[:], t2[:], ALU.add)
                yield

        def attn(h):
            kt_, qt_, vh_ = kT[h % 2], qT[h % 2], Vh[h % 2]
            pti = 0
            pend = None
            for qb in range(4):
                qs = slice(qb * 512, (qb + 1) * 512)
                O = k.PB[qb % 2]

                def s_pair(m_):
                    for j in range(2):
                        kt = 2 * m_ + j
                        P.mm(k.PAW[m_ % 2][:, j * 512:(j + 1) * 512], kt_[:, kt * 128:(kt + 1) * 128], qt_[:, qs])
                s_pair(0)
                s_pair(1)
                for m_ in range(NT // 2):
                    p_ = pT[pti % 4]
                    pti += 1
                    P.act(p_[:], k.PAW[m_ % 2][:, :], AF.Exp, scale=SC)
                    if m_ + 2 < NT // 2:
                        s_pair(m_ + 2)
                    for j in range(2):
                        kt = 2 * m_ + j
                        P.mm(O[0:96, :], vh_[:, kt, :], p_[:, j * 512:(j + 1) * 512], start=(kt == 0), stop=(kt == NT - 1))
                    if m_ == 1 and pend is not None:
                        pend()
                        pend = None
                    yield

                def fin(qb=qb, O=O, qs=qs):
                    rd = rden[qb % 2]
                    P.rpow(rd[0:32, :], O[64:96, :], -1.0)
                    P.cp(rd[32:64, :], rd[0:32, :], eng="dve")
                    ob = (h % 2) * 64
                    P.tt(mixc(k, 3 + h // 2)[ob:ob + 64, qs], O[0:64, :], rd[:], ALU.mult)
                pend = fin
            pend()
            yield

        for _ in prep(0):
            pass
        mode = "il"
        for h in range(6):
            gens = [attn(h)]
            if h + 1 < 6:
                if mode == "il":
                    gens.append(prep(h + 1))
                elif mode == "seq":
                    run_interleaved(gens)
                    gens = [prep(h + 1)]
                elif mode == "noprep":
                    pass
            if mode == "noprep" and h > 0:
                gens = [attn(0)]
            run_interleaved(gens)


NCK = L // 64
NEG = -30000.0


def host_gdn(inp, b, m):
    cw = inp["gdn_conv"]
    o = np.zeros((DEPTH, 128, 9, 5), np.float32)
    for part in range(3):
        for p in range(3):
            o[:, :, part * 3 + p, :] = cw[:, :, part * 384 + p * 128: part * 384 + (p + 1) * 128].transpose(0, 2, 1)
    m["gconv"] = o
    gb = np.zeros((DEPTH, 128, 2), np.float32)
    for d in range(2):
        gb[:, d * 32:d * 32 + 6, 0] = inp["gdn_dt_bias"][:, d, :]
        gb[:, d * 32:d * 32 + 6, 1] = inp["gdn_a_log"][:, d, :]
    m["ggb"] = gb
    m["gng"] = np.ascontiguousarray(np.tile(inp["gdn_norm_g"], (1, 2)).reshape(DEPTH, 128, 1))
    sel = np.zeros((64, 6, 128), np.float32)
    for d in range(2):
        for p in range(3):
            sel[d * 32 + 2 * p, d * 3 + p, 0:64] = 1.0
            sel[d * 32 + 2 * p + 1, d * 3 + p, 64:128] = 1.0
    m["gsel"] = sel
    j = np.arange(64)[:, None]
    i = np.arange(64)[None, :]
    nm = np.zeros((128, 2, 64), np.float32)
    nm[:, 0, :] = np.tile(np.where(i > j, 0.0, NEG), (2, 1))
    nm[:, 1, :] = np.tile(np.where(i < j, 0.0, NEG), (2, 1))
    m["gnegm"] = nm
    m["gid2"] = np.ascontiguousarray(np.tile(np.eye(64, dtype=np.float32), (2, 1)))


def gdn_decl(k):
    nc = k.nc

    def din(name, shape, dt=F32):
        return nc.dram_tensor(name, list(shape), dt, kind="ExternalInput").ap()
    k.gconv_d = din("gconv", [DEPTH, 128, 9, 5])
    k.ggb_d = din("ggb", [DEPTH, 128, 2])
    k.gng_d = din("gng", [DEPTH, 128, 1])
    k.gsel_d = din("gsel", [64, 6, 128])
    k.gnegm_d = din("gnegm", [128, 2, 64])
    k.gid2_d = din("gid2", [128, 64])


def bc3(ap2, n):
    return ap2.unsqueeze(2).broadcast_to([ap2.shape[0], ap2.shape[1], n])


def bcm(ap2, n):
    return ap2.unsqueeze(1).broadcast_to([ap2.shape[0], n, ap2.shape[1]])


HS = (slice(0, 64), slice(64, 128))


def v3(ps, n=8):
    return ps[:, 0:n * 64].rearrange("p (a b) -> p a b", a=n)


def mm2(P, ps, c, lhsT, rhs, **kw):
    for hs in HS:
        P.mm(ps[hs, c * 64:(c + 1) * 64], lhsT[hs], rhs[hs], **kw)


def tr2(P, ps, c, in_, ident):
    for hs in HS:
        P.mm(ps[hs, c * 64:(c + 1) * 64], in_[hs], ident[hs, hs])


def neumann2(k, Nn, Rm, tmp, bank, id2, n8=8):
    P = k.P
    idb = bcm(id2[:, :], n8)
    tA, tB, tC, tD = tmp
    pa_, pb_, pc_ = bank
    for c in range(n8):
        tr2(P, pa_, c, Nn[:, c, :], k.identb)
    P.cp(tA[:], v3(pa_, n8), eng="act")
    P.tt(Rm[:], Nn[:], idb, ALU.add)
    yield
    cur, curT = Nn, tA
    targets = [(tB, tC), (tD, tA)]
    for lvl in range(1, 7):
        nxt, nxtT = targets[(lvl - 1) % 2]
        if lvl >= 2:
            for c in range(n8):
                mm2(P, pc_, c, curT[:, c, :], Rm[:, c, :])
            P.tt(Rm[:], Rm[:], v3(pc_, n8), ALU.add)
        if lvl <= 5:
            for c in range(n8):
                mm2(P, pb_, c, cur[:, c, :], curT[:, c, :])
            P.cp(nxtT[:], v3(pb_, n8), eng="act")
            if lvl < 5:
                for c in range(n8):
                    mm2(P, pa_, c, curT[:, c, :], cur[:, c, :])
                P.cp(nxt[:], v3(pa_, n8), eng=("act" if lvl % 2 else "dve"))
        yield
        cur, curT = nxt, nxtT


def run_interleaved(gens, skew=0):
    gens = list(gens)
    for _ in range(skew):
        try:
            next(gens[0])
        except StopIteration:
            gens.pop(0)
            break
    while gens:
        for g in list(gens):
            try:
                next(g)
            except StopIteration:
                gens.remove(g)


def norm_pipe(k, n, src_fn, sqb, rnb, banks, bones, power, scale, bias, post_fn):
    P = k.P

    def pre(i):
        P.act(sqb[i % 2][:], src_fn(i), AF.Square)
        P.mm(banks[i % 2][:, :], bones[:], sqb[i % 2][:])

    def post(i):
        P.rpow(rnb[i % 2][:], banks[i % 2][:, :], power, scale=scale, bias=bias)
        post_fn(i, rnb[i % 2])
    pre(0)
    for i in range(n):
        if i + 1 < n:
            pre(i + 1)
        post(i)


def run_pipelined(chains, depth=2):
    active = []
    nxt = [0] * len(chains)

    def start(ci):
        if nxt[ci] < len(chains[ci]):
            active.append((ci, chains[ci][nxt[ci]](nxt[ci] % depth)))
            nxt[ci] += 1
    for ci in range(len(chains)):
        for _ in range(depth):
            start(ci)
    while active:
        for item in list(active):
            try:
                next(item[1])
            except StopIteration:
                active.remove(item)
                start(item[0])


def gdn_phase(k, l):
    P = k.P
    with scope(k):
        GC = P.sb("g_GC", [64, L], F32, blk=512)
        GP = [P.sb("g_GP%d" % p, [128, NCK, 4], F32) for p in range(3)]
        NBP = [P.sb("g_NBP%d" % p, [128, NCK, 2], F32) for p in range(3)]
        sel = P.sb("g_sel", [64, 6, 128], F32)
        negm = P.sb("g_negm", [128, 2, 64], F32)
        id2 = P.sb("g_id2", [128, 64], F32)
        cw = P.sb("g_cw", [128, 9, 5], F32)
        ng = P.sb("g_ng", [128, 1], F32)
        bones = P.sb("g_bones", [128, 128], F32)
        P.dma(sel[:], k.gsel_d[:])
        P.dma(negm[:], k.gnegm_d[:])
        P.dma(id2[:], k.gid2_d[:])
        P.dma(cw[:], k.gconv_d[l])
        P.dma(ng[:], k.gng_d[l])
        P.memset(bones[:], 0.0)
        P.memset(bones[0:64, 0:64], 1.0)
        P.memset(bones[64:128, 64:128], 1.0)
        bonesb = P.sb("g_bonesb", [128, 128], BF16)
        P.cp(bonesb[:], bones[:])
        with scope(k):
            GT = P.sb("g_GT", [128, L], F32, blk=512)
            m0 = P.sb("g_m0", [64, L], F32)
            gb = P.sb("g_gb", [128, 2], F32)
            negA = P.sb("g_negA", [128, 1], F32)
            P.dma(gb[:], k.ggb_d[l])
            P.act(negA[:], gb[:, 1:2], AF.Exp)
            P.ts(negA[:], negA[:], -1.0, ALU.mult)
            P.memset(m0[:], 1.0)
            P.memset(m0[:, 0:L:64], 0.0)
            pa = proj(k, l, "gab")
            for tb in range(4):
                sl = slice(tb * 512, (tb + 1) * 512)
                P.act(GT[0:64, sl], pa[tb][0:64, :], AF.Identity, bias=gb[0:64, 0:1])
                P.act(GT[64:128, sl], pa[tb][64:128, :], AF.Sigmoid)
            P.act(GC[:, :], GT[0:64, :], AF.Abs)
            P.act(GC[:, :], GC[:, :], AF.Exp, scale=-1.0)
            P.act(GC[:, :], GC[:, :], AF.Ln, bias=1.0)
            P.act(GT[0:64, :], GT[0:64, :], AF.Relu)
            P.tt(GT[0:64, :], GT[0:64, :], GC[:, :], ALU.add)
            P.ts(GT[0:64, :], GT[0:64, :], negA[0:64, 0:1], ALU.mult)
            P.scan(GC[:, :], m0[:, :], GT[0:64, :], 0.0, ALU.mult, ALU.add)
            gc3 = GC[32:64, :].rearrange("p (a b) -> p a b", b=64)
            P.tt(m0[32:64, :].rearrange("p (a b) -> p a b", b=64), bc3(GC[32:64, 63:L:64], 64), gc3, ALU.subtract)
            P.tt(GC[32:64, :], m0[32:64, :], GT[32:64, :], ALU.add)
            for grp in range(4):
                g8 = slice(grp * 8, (grp + 1) * 8)
                for c in range(8):
                    ck = grp * 8 + c
                    cs = slice(c * 64, (c + 1) * 64)
                    for hs in HS:
                        P.mm(k.PB[2 * (grp % 2)][hs, cs], GC[:, ck * 64:(ck + 1) * 64], k.ident[0:64, 0:64])
                        P.mm(k.PB[2 * (grp % 2) + 1][hs, cs], GT[64:128, ck * 64:(ck + 1) * 64], k.ident[64:128, 64:128])
                n_ = 0
                for p in range(3):
                    for hf, hs in enumerate(HS):
                        h = 2 * p + hf
                        for q, ps in ((0, k.PB[2 * (grp % 2)]), (1, k.PB[2 * (grp % 2) + 1])):
                            src = v3(ps)[hs, :, h:h + 33:32]
                            P.cp(GP[p][hs, g8, 2 * q:2 * q + 2], src, eng=("act" if q else "dve"))
            for p in range(3):
                P.ts(NBP[p][:], GP[p][:, :, 2:4], -1.0, ALU.mult)
        for p in range(3):
            with scope(k):
                Q = P.sb("g_Q", [128, L], BF16, blk=512)
                K_ = P.sb("g_K", [128, L], BF16, blk=512)
                Kt = P.sb("g_Kt", [128, NCK, 64], BF16, blk=512)
                Vt = P.sb("g_Vt", [128, NCK, 64], BF16, blk=512)
                O = P.sb("g_O", [128, L], F32, blk=512)
                P.memset(O[:], 0.0, eng="pool")
                with scope(k):
                    xps = [P.sb("g_xp%d" % i, [128, L + 4], BF16) for i in range(2)]
                    Dgs = [P.sb("g_Dg%d" % i, [128, 5, 128], BF16) for i in range(2)]
                    cvs = [P.sb("g_cv%d" % i, [128, L], F32, blk=512) for i in range(2)]
                    Vf = P.sb("g_Vf", [128, L], BF16, blk=512)
                    sq = P.sb("g_sq", [128, 512], BF16)
                    rn = P.sb("g_rn", [128, 512], F32)
                    sq2 = P.sb("g_sq2", [128, 512], BF16)
                    rn2 = P.sb("g_rn2", [128, 512], F32)
                    for xp in xps:
                        P.memset(xp[:, 0:2], 0.0)
                        P.memset(xp[:, L + 2:L + 4], 0.0)
                    for part, nm, dst in ((0, "gq", Q), (1, "gk", K_), (2, "gv", Vf)):
                        xp, Dg, cv = xps[part % 2], Dgs[part % 2], cvs[part % 2]
                        pa = proj(k, l, "%s%d" % (nm, p))
                        for tb in range(4):
                            P.cp(xp[:, 2 + tb * 512:2 + (tb + 1) * 512], pa[tb][:, :], eng=("act" if tb % 2 else "dve"))
                        wi = part * 3 + p
                        for j in range(5):
                            P.ts(Dg[:, j, :], k.identb[:], cw[:, wi, j:j + 1], ALU.mult, eng=("pool" if j % 2 else "dve"))
                        for tb in range(4):
                            for j in range(5):
                                P.mm(k.PB[tb][:, :], Dg[:, j, :], xp[:, j + tb * 512:j + (tb + 1) * 512],
                                     start=(j == 0), stop=(j == 4))
                        for tb in range(4):
                            sl = slice(tb * 512, (tb + 1) * 512)
                            P.act((dst if part == 2 else cv)[:, sl], k.PB[tb][:, :], AF.Silu)
                        if part < 2:
                            def fin(tb, r, dst=dst, cv=cv):
                                P.tt(dst[:, tb * 512:(tb + 1) * 512], cv[:, tb * 512:(tb + 1) * 512], r[:], ALU.mult)
                            norm_pipe(k, 4, lambda tb, cv=cv: cv[:, tb * 512:(tb + 1) * 512], (sq, sq2), (rn, rn2),
                                      (k.PA[2], k.PA[3]), bonesb, -0.5, 64.0 if part == 0 else 1.0,
                                      64e-6 if part == 0 else 1e-6, fin)
                    for src, dstt in ((K_, Kt), (Vf, Vt)):
                        for grp in range(4):
                            ps = k.PB[grp % 2]
                            for c in range(8):
                                ck = grp * 8 + c
                                tr2(P, ps, c, src[:, ck * 64:(ck + 1) * 64], k.identb)
                            P.cp(dstt[:, grp * 8:(grp + 1) * 8, :], v3(ps), eng=("act" if grp % 2 else "dve"))
                with scope(k):
                    T = dict(GC=GC, GP=GP[p], NBP=NBP[p], sel=sel, negm=negm, id2=id2, Q=Q, K=K_, Kt=Kt, Vt=Vt, O=O)
                    run_pipelined([gdn_chain(k, p, d, T) for d in range(2)], depth=1)
                with scope(k):
                    sqo = [P.sb("g_osq%d" % i, [128, 512], BF16) for i in range(2)]
                    rno = [P.sb("g_orn%d" % i, [128, 512], F32) for i in range(2)]

                    def fin_o(tb, r):
                        sl = slice(tb * 512, (tb + 1) * 512)
                        P.tt(r[:], O[:, sl], r[:], ALU.mult)
                        P.ts(mixc(k, p)[:, sl], r[:], ng[:, 0:1], ALU.mult)
                    norm_pipe(k, 4, lambda tb: O[:, tb * 512:(tb + 1) * 512], sqo, rno, (k.PB[2], k.PB[3]), bonesb,
                              -0.5, 1.0 / 64, EPS, fin_o)


def gdn_chain(k, p, d, T):
    P = k.P
    GC, GP, NBP, sel, negm, id2, Q, K_, Kt, Vt, O = (T[n] for n in ("GC", "GP", "NBP", "sel", "negm", "id2", "Q", "K", "Kt", "Vt", "O"))
    B = k.PA if d == 0 else k.PB
    tag = "g%d_" % d
    names = ("CB", "EI", "QG", "Rm", "U0", "WT", "BW", "KD", "GK", "AcT", "Sg", "Ug", "nA", "nB", "nC", "nD", "Nb", "PTb", "Sgb")
    f32n = ("CB", "EI", "AcT", "Sg")
    NSET = 1
    GG = [{n: P.sb(tag + "%d" % s_ + n, [128, 8, 64], F32 if n in f32n else BF16) for n in names} for s_ in range(NSET)]
    for s_ in range(NSET):
        GG[s_]["gend"] = P.sb(tag + "gend%d" % s_, [128, 8], F32)
        GG[s_]["kds"] = P.sb(tag + "kds%d" % s_, [128, 8], F32)
    gam = P.sb(tag + "gam", [128, NCK], F32)
    shared = {"scan": 0}
    Scar = P.sb(tag + "Scar", [128, 64], F32)
    e_ = 63 if d == 0 else 0
    idb = bcm(id2[:, :], 8)

    def f2(t):
        return t[:].rearrange("p a b -> p (a b)")
    P.memset(Scar[:], 0.0)
    P.act(gam[:], GP[:, :, d], AF.Exp)
    gorder = list(range(4)) if d == 0 else list(range(3, -1, -1))

    def group(gi, grp, G):
        Nb, PTb, Sgb, gend, kds = G["Nb"], G["PTb"], G["Sgb"], G["gend"], G["kds"]
        sl = slice(grp * 512, (grp + 1) * 512)
        g8 = slice(grp * 8, (grp + 1) * 8)
        cj = GP[:, g8, d]
        nb = NBP[:, g8, d]
        CB, EI, QG, Rm, U0, WT, BW, KD, GK, AcT, Sg, Ug = (G[n] for n in names[:12])
        P.mm(B[0][:, :], sel[:, d * 3 + p, :], GC[:, sl])
        P.cp(f2(CB), B[0][:, :], eng="act")
        P.act(f2(EI), f2(CB), AF.Exp)
        P.cp(gend[:], EI[:, :, e_], eng="pool")
        P.tt(f2(QG), f2(EI), Q[:, sl], ALU.mult)
        P.tt(kds[:], CB[:, :, e_], cj, ALU.subtract)
        P.act(kds[:], kds[:], AF.Exp)
        P.tt(GK[:], Kt[:, g8, :], bc3(gam[:, g8], 64), ALU.mult, eng="pool")
        P.tt(KD[:], Kt[:, g8, :], bc3(kds[:], 64), ALU.mult, eng="pool")
        P.tt(CB[:], CB[:], bc3(cj, 64), ALU.subtract)
        P.tt(CB[:], CB[:], bcm(negm[:, d, :], 8), ALU.add)
        P.act(f2(CB), f2(CB), AF.Exp)
        yield
        P.tt(EI[:], CB[:], idb, ALU.add)
        for c in range(8):
            cs = slice((grp * 8 + c) * 64, (grp * 8 + c + 1) * 64)
            mm2(P, B[0], c, K_[:, cs], Q[:, cs])
        P.tt(PTb[:], EI[:], v3(B[0]), ALU.mult)
        for c in range(8):
            cs = slice((grp * 8 + c) * 64, (grp * 8 + c + 1) * 64)
            mm2(P, B[1], c, K_[:, cs], K_[:, cs])
        P.tt(CB[:], CB[:], v3(B[1]), ALU.mult)
        P.tt(Nb[:], CB[:], bc3(nb, 64), ALU.mult)
        yield
        for _ in neumann2(k, Nb, Rm, (G["nA"], G["nB"], G["nC"], G["nD"]), (B[0], B[1], B[2]), id2):
            yield
        for c in range(8):
            mm2(P, B[2], c, Rm[:, c, :], GK[:, c, :])
        P.tt(BW[:], v3(B[2]), bc3(nb, 64), ALU.mult)
        for c in range(8):
            mm2(P, B[0], c, Rm[:, c, :], Vt[:, grp * 8 + c, :])
        P.tt(U0[:], v3(B[0]), bc3(GP[:, g8, 2 + d], 64), ALU.mult)
        for c in range(8):
            mm2(P, B[1], c, GK[:, c, :], Rm[:, c, :])
        P.cp(WT[:], v3(B[1]), eng="act")
        yield
        for c in range(8):
            mm2(P, B[3], c, BW[:, c, :], KD[:, c, :])
        P.tt(AcT[:], idb, bc3(gend[:], 64), ALU.mult, eng="pool")
        P.tt(AcT[:], AcT[:], v3(B[3]), ALU.add)
        yield
        while shared["scan"] != gi:
            yield
        corder = range(8) if d == 0 else range(7, -1, -1)
        prev = Scar[:]
        for n, c in enumerate(corder):
            P.cp(Sg[:, c, :], prev, eng="pool") if n == 0 else None
            ps = B[2 + n % 2]
            mm2(P, ps, 0, AcT[:, c, :], Sg[:, c, :], start=True, stop=False)
            mm2(P, ps, 0, KD[:, c, :], U0[:, c, :], start=False, stop=True)
            last = (n == 7)
            dst = Scar[:] if last else Sg[:, corder[n + 1], :]
            P.cp(dst, ps[:, 0:64], eng=("act" if n % 2 == 0 else "dve"))
            yield
        shared["scan"] = gi + 1
        P.cp(Sgb[:], Sg[:], eng="act")
        for c in range(8):
            mm2(P, B[0], c, WT[:, c, :], Sgb[:, c, :])
        P.tt(CB[:], v3(B[0]), bc3(nb, 64), ALU.mult)
        P.tt(Ug[:], CB[:], U0[:], ALU.add)
        yield
        for c in range(8):
            mm2(P, B[1], c, Sgb[:, c, :], QG[:, c, :], start=True, stop=False)
            mm2(P, B[1], c, Ug[:, c, :], PTb[:, c, :], start=False, stop=True)
        P.tt(O[:, sl], O[:, sl], B[1][:, :], ALU.add)
        yield
    return [(lambda slot, gi=gi, grp=grp: group(gi, grp, GG[slot])) for gi, grp in enumerate(gorder)]


RSKEW = 0
RW_EPS = 64e-5
DEC = float(np.exp(-0.5))
GS = 8
NG = NCK // GS


def host_rwkv(inp, b, m):
    mu = inp["rwkv_mu"]
    o = np.zeros((DEPTH, 128, 8, 2), np.float32)
    for part in range(3):
        for p in range(2):
            o[:, :, part * 2 + p, :] = mu[:, :, part * 256 + p * 128: part * 256 + (p + 1) * 128].transpose(0, 2, 1)
    o[:, 0:64, 6, :] = mu[:, :, 768:832].transpose(0, 2, 1)
    o[:, 0:64, 7, :] = mu[:, :, 832:896].transpose(0, 2, 1)
    m["rmu"] = o

    def pp(a):
        if a.ndim == 2:
            return np.ascontiguousarray(a.reshape(DEPTH, 2, 128).transpose(0, 2, 1))
        return np.ascontiguousarray(a.reshape(DEPTH, 2, 2, 128).transpose(0, 3, 1, 2))
    pv = np.zeros((DEPTH, 128, 7, 2), np.float32)
    pv[:, :, 0:2, :] = pp(inp["rwkv_w0"])
    pv[:, :, 2:4, :] = pp(inp["rwkv_a0"])
    pv[:, :, 4, :] = pp(inp["rwkv_k_k"])
    pv[:, :, 5, :] = pp(inp["rwkv_k_a"])
    pv[:, :, 6, :] = pp(inp["rwkv_r_k"].reshape(DEPTH, 256))
    m["rpv"] = pv
    ln = np.zeros((DEPTH, 128, 2, 2), np.float32)
    ln[:, :, 0, :] = pp(inp["rwkv_ln_g"])
    ln[:, :, 1, :] = pp(inp["rwkv_ln_b"])
    m["rln"] = ln
    m["rw2"] = np.ascontiguousarray(inp["rwkv_w2"].transpose(0, 2, 1, 3))
    m["ra2"] = np.ascontiguousarray(inp["rwkv_a2"].transpose(0, 2, 1, 3))
    s_ = np.arange(64)[:, None]
    t_ = np.arange(64)[None, :]
    msk = np.zeros((128, 2, 4, 64), np.float32)
    msk[:, 0, 0, :] = np.tile((t_ > s_), (2, 1))
    msk[:, 0, 1, :] = np.tile((t_ >= s_), (2, 1))
    msk[:, 1, 0, :] = np.tile((t_ < s_), (2, 1))
    msk[:, 1, 1, :] = np.tile((t_ <= s_), (2, 1))
    msk[:, :, 2:4, :] = -msk[:, :, 0:2, :]
    m["rmsk"] = msk


def rwkv_decl(k):
    nc = k.nc

    def din(name, shape, dt=F32):
        return nc.dram_tensor(name, list(shape), dt, kind="ExternalInput").ap()
    k.rmu_d = din("rmu", [DEPTH, 128, 8, 2])
    k.rpv_d = din("rpv", [DEPTH, 128, 7, 2])
    k.rln_d = din("rln", [DEPTH, 128, 2, 2])
    k.rw2_d = din("rw2", [DEPTH, 64, 2, 256])
    k.ra2_d = din("ra2", [DEPTH, 64, 2, 256])
    k.rmsk_d = din("rmsk", [128, 2, 4, 64])


def rwkv_phase(k, l):
    P = k.P
    with scope(k):
        mu = P.sb("r_mu", [128, 8, 3], F32)
        pv = P.sb("r_pv", [128, 7, 2], F32)
        omka = P.sb("r_omka", [128, 2], F32)
        hrk = P.sb("r_hrk", [128, 2], F32)
        ln = P.sb("r_ln", [128, 2, 2], F32)
        w2 = P.sb("r_w2", [64, 2, 256], BF16)
        a2 = P.sb("r_a2", [64, 2, 256], BF16)
        msk = P.sb("r_msk", [128, 2, 4, 64], F32)
        id2 = P.sb("r_id2", [128, 64], F32)
        bones = P.sb("r_bones", [128, 128], F32)
        m0 = P.sb("r_m0", [128, GS * 64], F32)
        twd = P.sb("r_twd", [64, L], BF16, blk=512)
        adx = P.sb("r_adx", [64, L], BF16, blk=512)
        sh32 = P.sb("r_sh32", [128, L], F32, blk=512)
        xp = P.sb("r_xp", [128, L + 2], F32)
        P.dma(mu[:, :, 0:2], k.rmu_d[l])
        P.dma(pv[:], k.rpv_d[l])
        P.dma(ln[:], k.rln_d[l])
        with scope(k):
            w2f = P.sb("r_w2f", [64, 2, 256], F32)
            a2f = P.sb("r_a2f", [64, 2, 256], F32)
            P.dma(w2f[:], k.rw2_d[l])
            P.dma(a2f[:], k.ra2_d[l])
            P.cp(w2[:], w2f[:], eng="act")
            P.cp(a2[:], a2f[:], eng="act")
        P.dma(msk[:], k.rmsk_d[:])
        P.dma(id2[:], k.gid2_d[:])
        P.memset(bones[:], 0.0)
        P.memset(bones[0:64, 0:64], 1.0)
        P.memset(bones[64:128, 64:128], 1.0)
        P.memset(m0[:], 1.0)
        P.memset(m0[:, 0:GS * 64:64], 0.0)
        P.memset(xp[:, 0:1], 0.0)
        P.memset(xp[:, L + 1:L + 2], 0.0)
        P.tt(mu[:, :, 2], mu[:, :, 0], mu[:, :, 1], ALU.add)
        P.ts(mu[:, :, 2], mu[:, :, 2], -1.0, ALU.mult, 1.0, ALU.add)
        P.ts(omka[:], pv[:, 5, :], -1.0, ALU.mult, 1.0, ALU.add)
        P.ts(hrk[:], pv[:, 6, :], 0.5, ALU.mult)

        def shifted(name, ci, dst, np_=128, fn=None):
            pa = proj(k, l, name, alt=(ci % 2 == 1))
            for tb in range(4):
                P.cp(xp[0:np_, 1 + tb * 512:1 + (tb + 1) * 512], pa[tb][0:np_, :], eng=("act" if tb % 2 else "dve"))
            t_ = sh32[0:np_, :]
            P.ts(t_, xp[0:np_, 1:L + 1], mu[0:np_, ci, 2:3], ALU.mult)
            P.stt(t_, xp[0:np_, 0:L], mu[0:np_, ci, 0:1], t_, ALU.mult, ALU.add)
            if fn is None:
                P.stt(dst[:], xp[0:np_, 2:L + 2], mu[0:np_, ci, 1:2], t_, ALU.mult, ALU.add)
            else:
                P.stt(t_, xp[0:np_, 2:L + 2], mu[0:np_, ci, 1:2], t_, ALU.mult, ALU.add)
                P.act(dst[:], t_, fn)

        shifted("rwd", 6, twd, 64, AF.Tanh)
        shifted("rad", 7, adx, 64)
        R_ = P.sb("r_R", [128, L], BF16, blk=512)
        KX = P.sb("r_KX", [128, L], BF16, blk=512)
        V_ = P.sb("r_V", [128, L], BF16, blk=512)
        KK = P.sb("r_KK", [128, L], BF16, blk=512)
        Vt = P.sb("r_Vt", [128, NCK, 64], BF16, blk=512)
        KS = P.sb("r_KS", [128, L], F32, blk=512)
        Y = xp[:, 1:L + 1]
        sq = P.sb("r_sq", [128, 512], BF16)
        bonesb = P.sb("r_bonesb", [128, 128], BF16)
        P.cp(bonesb[:], bones[:])
        rn = P.sb("r_rn", [128, 512], F32)
        CH = [rwkv_tiles(k, e) for e in range(2)]

        class _V:
            def __init__(s_, t):
                s_.t = t

            def __getitem__(s_, key):
                return s_.t[:].rearrange("p a b -> p (a b)")[key]
        sq2, rn2 = _V(CH[0]["kap"]), _V(CH[0]["a"])
        for p in range(2):
            shifted("rr%d" % p, 0 + p, R_)
            shifted("rk%d" % p, 2 + p, KX)
            shifted("rv%d" % p, 4 + p, V_)
            P.ts(sh32[:], KX[:], pv[:, 4, p:p + 1], ALU.mult)

            def fin_k(tb, r):
                P.tt(KK[:, tb * 512:(tb + 1) * 512], sh32[:, tb * 512:(tb + 1) * 512], r[:], ALU.mult)
            norm_pipe(k, 4, lambda tb: sh32[:, tb * 512:(tb + 1) * 512], (sq, sq2), (rn, rn2), (k.PB[2], k.PB[3]),
                      bonesb, -0.5, 1.0, 1e-6, fin_k)
            for grp in range(4):
                ps = k.PB[grp % 2]
                for c in range(8):
                    ck = grp * 8 + c
                    tr2(P, ps, c, V_[:, ck * 64:(ck + 1) * 64], k.identb)
                P.cp(Vt[:, grp * 8:(grp + 1) * 8, :], v3(ps), eng=("act" if grp % 2 else "dve"))
            P.memset(xp[:, 1:L + 1], 0.0, eng="pool")
            P.memset(KS[:], 0.0, eng="pool")
            T = dict(pv=pv, omka=omka, w2=w2, a2=a2, msk=msk, id2=id2, m0=m0, twd=twd, adx=adx,
                     R=R_, KX=KX, KK=KK, Vt=Vt, KS=KS, Y=Y)
            run_interleaved([rwkv_chain(k, p, e, T, CH[e]) for e in range(2)], skew=RSKEW)
            for tb in range(4):
                sl = slice(tb * 512, (tb + 1) * 512)
                P.mm(k.PB[tb % 2][:, :], bones[:], Y[:, sl])
                P.stt(Y[:, sl], k.PB[tb % 2][:, :], -1.0 / 64, Y[:, sl], ALU.mult, ALU.add)

            def fin_y(tb, r):
                sl = slice(tb * 512, (tb + 1) * 512)
                P.tt(Y[:, sl], Y[:, sl], r[:], ALU.mult)
                P.ts(Y[:, sl], Y[:, sl], ln[:, 0, p:p + 1], ALU.mult, ln[:, 1, p:p + 1], ALU.add)
            norm_pipe(k, 4, lambda tb: Y[:, tb * 512:(tb + 1) * 512], (sq, sq2), (rn, rn2), (k.PB[2], k.PB[3]),
                      bonesb, -0.5, 1.0 / 64, RW_EPS, fin_y)
            for tb in range(4):
                sl = slice(tb * 512, (tb + 1) * 512)
                s_ = (sq, sq2)[tb % 2]
                r_ = (rn, rn2)[tb % 2]
                P.tt(s_[:], R_[:, sl], KS[:, sl], ALU.mult)
                P.ts(s_[:], s_[:], hrk[:, p:p + 1], ALU.mult, eng="pool")
                P.mm(k.PB[tb % 2][:, :], bonesb[:], s_[:])
                P.tt(r_[:], k.PB[tb % 2][:, :], V_[:, sl], ALU.mult)
                P.tt(mixc(k, 6 + p)[:, sl], Y[:, sl], r_[:], ALU.add)


RW_F32 = ("lw", "a", "km", "b", "cl", "e1", "e2", "dend", "AcT", "Tg")
RW_BF16 = ("kap", "rt", "kt_", "bt_", "ke", "be", "kapT", "keT", "nbeT", "N", "Akv", "Brk", "nBrb", "Rm",
           "nA", "nB", "nC", "nD", "X0", "P0", "WkT", "Wk", "Tgb", "Pg")


def rwkv_tiles(k, e):
    P = k.P
    G = {n: P.sb("r%d_%s" % (e, n), [128, GS, 64], F32) for n in RW_F32}
    for n in RW_BF16:
        G[n] = P.sb("r%d_%s" % (e, n), [128, GS, 64], BF16)
    G["gC"] = P.sb("r%d_gC" % e, [128, GS], F32)
    G["Tcar"] = P.sb("r%d_Tcar" % e, [128, 64], F32)
    return G


def rwkv_chain(k, p, e, T, G):
    P = k.P
    pv, omka, w2, a2, msk, id2, m0, twd, adx, R_, KX, KK, Vt, KS, Y = (T[n] for n in (
        "pv", "omka", "w2", "a2", "msk", "id2", "m0", "twd", "adx", "R", "KX", "KK", "Vt", "KS", "Y"))
    B = k.PA if e == 0 else k.PB
    W = GS * 64
    e_ = 63 if e == 0 else 0
    idb = bcm(id2[:, :], GS)
    gC, Tcar = G["gC"], G["Tcar"]

    def f2(t):
        return t[:].rearrange("p a b -> p (a b)")

    def w3(ps):
        return v3(ps, GS)
    P.memset(Tcar[:], 0.0)
    gorder = range(NG) if e == 0 else range(NG - 1, -1, -1)
    pc = slice(p * 128, (p + 1) * 128)
    for grp in gorder:
        sl = slice(grp * W, (grp + 1) * W)
        c0 = grp * GS
        P.mm(B[0][:, 0:W], w2[:, e, pc], twd[:, sl])
        P.mm(B[1][:, 0:W], a2[:, e, pc], adx[:, sl])
        P.act(f2(G["lw"]), B[0][:, 0:W], AF.Sigmoid, bias=pv[:, 0 + e, p:p + 1])
        P.act(f2(G["a"]), B[1][:, 0:W], AF.Sigmoid, bias=pv[:, 2 + e, p:p + 1])
        P.ts(f2(G["km"]), f2(G["a"]), pv[:, 5, p:p + 1], ALU.mult, omka[:, p:p + 1], ALU.add)
        P.tt(f2(G["km"]), f2(G["km"]), KX[:, sl], ALU.mult)
        P.tt(f2(G["b"]), f2(G["a"]), KK[:, sl], ALU.mult)
        P.tt(KS[:, sl], KS[:, sl], f2(G["km"]), ALU.add, eng="pool")
        P.scan(f2(G["cl"]), m0[:], f2(G["lw"]), 0.0, ALU.mult, ALU.add)
        if e == 1:
            P.tt(G["e1"][:], bc3(G["cl"][:, :, 63], 64), G["cl"][:], ALU.subtract)
            P.tt(G["cl"][:], G["e1"][:], G["lw"][:], ALU.add)
        yield
        P.act(G["e1"][:], G["cl"][:], AF.Exp, scale=-DEC)
        P.act(G["e2"][:], G["cl"][:], AF.Exp, scale=DEC)
        P.tt(f2(G["rt"]), f2(G["e1"]), R_[:, sl], ALU.mult)
        P.tt(G["kt_"][:], G["e2"][:], G["km"][:], ALU.mult)
        P.tt(G["bt_"][:], G["e2"][:], G["b"][:], ALU.mult)
        P.tt(G["dend"][:], G["cl"][:], G["lw"][:], ALU.subtract)
        P.act(G["dend"][:], G["dend"][:], AF.Exp, scale=-DEC)
        P.tt(f2(G["kap"]), f2(G["dend"]), KK[:, sl], ALU.mult)
        P.cp(gC[:], G["e1"][:, :, e_], eng="pool")
        P.tt(G["dend"][:], bc3(G["cl"][:, :, e_], 64), G["cl"][:], ALU.subtract)
        P.act(G["dend"][:], G["dend"][:], AF.Exp, scale=-DEC)
        P.tt(G["ke"][:], G["dend"][:], G["km"][:], ALU.mult)
        P.tt(G["be"][:], G["dend"][:], G["b"][:], ALU.mult, eng="pool")
        yield
        for src, dst, sc in ((G["kap"], G["kapT"], 1.0), (G["ke"], G["keT"], 1.0), (G["be"], G["nbeT"], -1.0)):
            ps = B[0] if sc == 1.0 and src is G["kap"] else (B[1] if sc == 1.0 else B[2])
            for c in range(GS):
                tr2(P, ps, c, src[:, c, :], k.identb)
            if sc == 1.0:
                P.cp(dst[:], w3(ps), eng="act")
            else:
                P.ts(dst[:], w3(ps), -1.0, ALU.mult)
        yield
        for c in range(GS):
            mm2(P, B[0], c, G["bt_"][:, c, :], G["kap"][:, c, :])
            mm2(P, B[1], c, G["kt_"][:, c, :], G["kap"][:, c, :])
            mm2(P, B[2], c, G["kt_"][:, c, :], G["rt"][:, c, :])
            mm2(P, B[3], c, G["bt_"][:, c, :], G["rt"][:, c, :])
        ms = bcm(msk[:, e, 0, :], GS)
        mi = bcm(msk[:, e, 1, :], GS)
        nms = bcm(msk[:, e, 2, :], GS)
        nmi = bcm(msk[:, e, 3, :], GS)
        P.tt(G["N"][:], w3(B[0]), nms, ALU.mult)
        P.tt(G["Akv"][:], w3(B[1]), ms, ALU.mult)
        P.tt(G["Brk"][:], w3(B[2]), mi, ALU.mult)
        P.tt(G["nBrb"][:], w3(B[3]), nmi, ALU.mult)
        yield
        for _ in neumann2(k, G["N"], G["Rm"], (G["nA"], G["nB"], G["nC"], G["nD"]), (B[0], B[1], B[2]), id2, GS):
            yield
        for c in range(GS):
            mm2(P, B[0], c, G["Akv"][:, c, :], Vt[:, c0 + c, :])
        P.cp(G["X0"][:], w3(B[0]), eng="act")
        yield
        for c in range(GS):
            mm2(P, B[0], c, G["Rm"][:, c, :], G["X0"][:, c, :])
            mm2(P, B[1], c, G["kapT"][:, c, :], G["Rm"][:, c, :])
            mm2(P, B[2], c, G["Rm"][:, c, :], G["kapT"][:, c, :])
        P.cp(G["P0"][:], w3(B[0]), eng="act")
        P.cp(G["WkT"][:], w3(B[1]), eng="dve")
        P.cp(G["Wk"][:], w3(B[2]), eng="act")
        yield
        for c in range(GS):
            mm2(P, B[3], c, G["Wk"][:, c, :], G["nbeT"][:, c, :])
        P.tt(G["AcT"][:], idb, bc3(gC[:], 64), ALU.mult, eng="pool")
        P.tt(G["AcT"][:], G["AcT"][:], w3(B[3]), ALU.add)
        yield
        corder = list(range(GS)) if e == 0 else list(range(GS - 1, -1, -1))
        Tg = G["Tg"]
        for n, c in enumerate(corder):
            if n == 0:
                P.cp(Tg[:, c, :], Tcar[:], eng="pool")
            ps = B[2 + n % 2]
            mm2(P, ps, 0, G["AcT"][:, c, :], Tg[:, c, :], start=True, stop=False)
            mm2(P, ps, 0, G["keT"][:, c, :], Vt[:, c0 + c, :], start=False, stop=False)
            mm2(P, ps, 0, G["nbeT"][:, c, :], G["P0"][:, c, :], start=False, stop=True)
            dst = Tcar[:] if n == GS - 1 else Tg[:, corder[n + 1], :]
            P.cp(dst, ps[:, 0:64], eng=("act" if n % 2 == 0 else "dve"))
            yield
        P.cp(G["Tgb"][:], Tg[:], eng="act")
        for c in range(GS):
            mm2(P, B[0], c, G["WkT"][:, c, :], G["Tgb"][:, c, :])
        P.tt(G["Pg"][:], w3(B[0]), G["P0"][:], ALU.add)
        yield
        for c in range(GS):
            mm2(P, B[1], c, Vt[:, c0 + c, :], G["Brk"][:, c, :], start=True, stop=False)
            mm2(P, B[1], c, G["Tgb"][:, c, :], G["rt"][:, c, :], start=False, stop=False)
            mm2(P, B[1], c, G["Pg"][:, c, :], G["nBrb"][:, c, :], start=False, stop=True)
        P.tt(Y[:, sl], Y[:, sl], B[1][:, 0:W], ALU.add)
        yield


_CACHE = {}


def kernel(**inputs):
    inp = {k_: np.asarray(v) for k_, v in inputs.items()}
    if "k" not in _CACHE:
        _CACHE["k"] = build()
    k = _CACHE["k"]
    B = inp["x"].shape[0]
    base = host_inputs(inp, 0)
    in_maps = []
    for b in range(B):
        m = dict(base)
        m["x"] = np.ascontiguousarray(inp["x"][b], dtype=np.float32)
        m["pos"] = np.ascontiguousarray(inp["positions"][b].reshape(1, L).astype(np.int32))
        in_maps.append(m)
    res = run_bass_kernel_spmd(k.nc, in_maps, core_ids=list(range(B)))
    return np.stack([np.asarray(r["out"], dtype=np.float32) for r in res.results], axis=0)
```
